# Optimizing a Trainium2 kernel written in Bass

```python
import jax, jax.numpy as jnp
from jax import lax
import numpy as np

D_MODEL = 1024
BATCH = 2
SEQ = 16384
DEPTH = 4

HEAD_DIM = 64
N_ATTN_HEADS = 8
N_KV_GROUPS = 2
HEADS_PER_GROUP = N_ATTN_HEADS // N_KV_GROUPS
ATTN_WIDTH = N_ATTN_HEADS * HEAD_DIM
CONV_WIDTH = D_MODEL - ATTN_WIDTH
KV_WIDTH = N_KV_GROUPS * HEAD_DIM
N_BRANCH = 3
CMP_BLOCK = 32
CMP_STRIDE = 16
CMP_HIDDEN = 4 * HEAD_DIM
SEL_BLOCK = 64
SEL_TOPK = 16
WINDOW = 512
Q_BLOCK = 128
CONV_K = 3
D_FF = 4 * D_MODEL
IN_WIDTH = ATTN_WIDTH + 6 * KV_WIDTH + N_BRANCH * N_ATTN_HEADS + 3 * CONV_WIDTH
EPS = 1e-6
NEG = -1e30
FORCE_BONUS = 1e4

kernel_name = "nsa_shortconv_hymba_trunk"


def rms_norm(x, g):
    xf = x.astype(jnp.float32)
    y = xf * lax.rsqrt(jnp.mean(xf * xf, axis=-1, keepdims=True) + EPS)
    return (y * g.astype(jnp.float32)).astype(x.dtype)


def alibi_slopes():
    h = np.arange(1, N_ATTN_HEADS + 1, dtype=np.float32)
    s = np.power(np.float32(2.0), -8.0 * h / N_ATTN_HEADS).astype(np.float32)
    return jnp.asarray(s, dtype=jnp.float32).reshape(N_KV_GROUPS, HEADS_PER_GROUP)


def masked_softmax(s, mask):
    s = jnp.where(mask, s, NEG)
    m = jnp.max(s, axis=-1, keepdims=True)
    p = jnp.where(mask, jnp.exp(s - m), 0.0)
    return p / jnp.maximum(jnp.sum(p, axis=-1, keepdims=True), 1e-30)


def cmp_to_sel_matrix(n_cmp, n_sel):
    s0 = jnp.arange(n_cmp)[:, None] * CMP_STRIDE
    s1 = jnp.arange(n_sel)[None, :] * SEL_BLOCK
    shared = jnp.clip(jnp.minimum(s0 + CMP_BLOCK, s1 + SEL_BLOCK) - jnp.maximum(s0, s1), 0, None)
    return shared.astype(jnp.float32) / CMP_BLOCK


def compress(k_raw, pe, w1, b1, w2, b2):
    B, T = k_raw.shape[0], k_raw.shape[1]
    ch = k_raw.reshape(B, T // CMP_STRIDE, CMP_STRIDE, N_KV_GROUPS, HEAD_DIM)
    blk = jnp.concatenate([ch[:, :-1], ch[:, 1:]], axis=2)
    blk = blk + pe[None, None, :, None, :]
    blk = blk.transpose(0, 3, 1, 2, 4).reshape(B, N_KV_GROUPS, -1, CMP_BLOCK * HEAD_DIM)
    hdn = jax.nn.gelu(blk @ w1 + b1)
    return hdn @ w2 + b2


def nsa_attention(q, k_cmp, v_cmp, k_slc, v_slc, k_win, v_win, gates):
    B, T = q.shape[0], q.shape[1]
    G, R, hd = N_KV_GROUPS, HEADS_PER_GROUP, HEAD_DIM
    f32 = jnp.float32
    n_cmp = k_cmp.shape[2]
    n_sel = T // SEL_BLOCK
    n_top = min(SEL_TOPK, n_sel)
    n_qb = T // Q_BLOCK
    slopes = alibi_slopes()[None, :, :, None, None]
    sel_map = cmp_to_sel_matrix(n_cmp, n_sel)
    cmp_end = jnp.arange(n_cmp, dtype=jnp.int32) * CMP_STRIDE + (CMP_BLOCK - 1)
    sel_id = jnp.arange(n_sel, dtype=jnp.int32)
    sel_off = jnp.arange(SEL_BLOCK, dtype=jnp.int32)
    win_off = jnp.arange(Q_BLOCK + WINDOW, dtype=jnp.int32) - WINDOW
    ks_blk = k_slc.reshape(B, n_sel, SEL_BLOCK, G, hd).transpose(0, 3, 1, 2, 4)
    vs_blk = v_slc.reshape(B, n_sel, SEL_BLOCK, G, hd).transpose(0, 3, 1, 2, 4)
    kw_pad = jnp.pad(k_win, ((0, 0), (WINDOW, 0), (0, 0), (0, 0)))
    vw_pad = jnp.pad(v_win, ((0, 0), (WINDOW, 0), (0, 0), (0, 0)))
    q_blocks = (q * (hd ** -0.5)).reshape(B, n_qb, Q_BLOCK, G, R, hd).transpose(1, 0, 3, 4, 2, 5)
    g_blocks = gates.reshape(B, n_qb, Q_BLOCK, G, R, N_BRANCH).transpose(1, 0, 3, 4, 2, 5)
    b_ix = jnp.arange(B)[:, None, None, None]
    g_ix = jnp.arange(G)[None, :, None, None]

    def one_block(args):
        c, qb, gb = args
        q0 = c * Q_BLOCK
        t = q0 + jnp.arange(Q_BLOCK, dtype=jnp.int32)
        s = jnp.einsum('bgrqd,bgid->bgrqi', qb, k_cmp).astype(f32)
        dist = (t[:, None] - cmp_end[None, :]).astype(f32)
        p_cmp = masked_softmax(s - slopes * dist, dist >= 0)
        o_cmp = jnp.einsum('bgrqi,bgid->bgrqd', p_cmp.astype(v_cmp.dtype), v_cmp)
        imp = jnp.einsum('bgrqi,ij->bgqj', p_cmp, sel_map)
        jt = (t // SEL_BLOCK)[:, None]
        forced = (sel_id == 0) | (sel_id == jt) | (sel_id == jt - 1)
        valid = sel_id * SEL_BLOCK <= t[:, None]
        imp = jnp.where(valid, jnp.where(forced, imp + FORCE_BONUS, imp), NEG)
        _, idx = lax.top_k(imp, n_top)
        kg = ks_blk[b_ix, g_ix, idx].reshape(B, G, Q_BLOCK, n_top * SEL_BLOCK, hd)
        vg = vs_blk[b_ix, g_ix, idx].reshape(B, G, Q_BLOCK, n_top * SEL_BLOCK, hd)
        pos = (idx[..., None] * SEL_BLOCK + sel_off).reshape(B, G, Q_BLOCK, n_top * SEL_BLOCK)
        dist = (t[None, None, :, None] - pos).astype(f32)[:, :, None]
        s = jnp.einsum('bgrqd,bgqmd->bgrqm', qb, kg).astype(f32)
        p_slc = masked_softmax(s - slopes * dist, dist >= 0)
        o_slc = jnp.einsum('bgrqm,bgqmd->bgrqd', p_slc.astype(vg.dtype), vg)
        kw = lax.dynamic_slice_in_dim(kw_pad, q0, Q_BLOCK + WINDOW, axis=1)
        vw = lax.dynamic_slice_in_dim(vw_pad, q0, Q_BLOCK + WINDOW, axis=1)
        spos = q0 + win_off
        dist_i = t[:, None] - spos[None, :]
        mask = (spos[None, :] >= 0) & (dist_i >= 0) & (dist_i < WINDOW)
        s = jnp.einsum('bgrqd,bsgd->bgrqs', qb, kw).astype(f32)
        p_win = masked_softmax(s - slopes * dist_i.astype(f32), mask)
        o_win = jnp.einsum('bgrqs,bsgd->bgrqd', p_win.astype(vw.dtype), vw)
        o = gb[..., 0:1] * o_cmp + gb[..., 1:2] * o_slc + gb[..., 2:3] * o_win
        return o.astype(q.dtype)

    out = lax.map(one_block, (jnp.arange(n_qb, dtype=jnp.int32), q_blocks, g_blocks))
    return out.transpose(1, 0, 4, 2, 3, 5).reshape(B, T, N_ATTN_HEADS * hd)


def short_conv_mixer(h, c_gate, b_gate, conv_w):
    u = c_gate * h
    v = lax.conv_general_dilated(u, conv_w[:, None, :], window_strides=(1,),
                                 padding=((CONV_K - 1, 0),),
                                 dimension_numbers=('NWC', 'WIO', 'NWC'),
                                 feature_group_count=CONV_WIDTH)
    return b_gate * v


def setup_inputs(seed: int = 0) -> dict:
    key = jax.random.key(seed)
    ks = jax.random.split(key, 16)
    f32 = jnp.float32

    def nrm(k, shape, scale):
        return jax.random.normal(k, shape, f32) * scale

    return {
        'x': nrm(ks[0], (BATCH, SEQ, D_MODEL), 1.0),
        'g_mix_norm': 1.0 + nrm(ks[1], (DEPTH, D_MODEL), 0.02),
        'w_in': nrm(ks[2], (DEPTH, D_MODEL, IN_WIDTH), D_MODEL ** -0.5),
        'g_q': 1.0 + nrm(ks[3], (DEPTH, HEAD_DIM), 0.02),
        'g_k': 1.0 + nrm(ks[4], (DEPTH, N_BRANCH, HEAD_DIM), 0.02),
        'pe_cmp': nrm(ks[5], (DEPTH, 2, CMP_BLOCK, HEAD_DIM), 0.1),
        'w_cmp1': nrm(ks[6], (DEPTH, 2, CMP_BLOCK * HEAD_DIM, CMP_HIDDEN), (CMP_BLOCK * HEAD_DIM) ** -0.5),
        'b_cmp1': nrm(ks[7], (DEPTH, 2, CMP_HIDDEN), 0.01),
        'w_cmp2': nrm(ks[8], (DEPTH, 2, CMP_HIDDEN, HEAD_DIM), CMP_HIDDEN ** -0.5),
        'b_cmp2': nrm(ks[9], (DEPTH, 2, HEAD_DIM), 0.01),
        'conv_w': nrm(ks[10], (DEPTH, CONV_K, CONV_WIDTH), CONV_K ** -0.5),
        'g_out': 1.0 + nrm(ks[11], (DEPTH, D_MODEL), 0.02),
        'w_o': nrm(ks[12], (DEPTH, D_MODEL, D_MODEL), (2 * DEPTH * D_MODEL) ** -0.5),
        'g_ffn_norm': 1.0 + nrm(ks[13], (DEPTH, D_MODEL), 0.02),
        'w_up': nrm(ks[14], (DEPTH, D_MODEL, D_FF), D_MODEL ** -0.5),
        'w_down': nrm(ks[15], (DEPTH, D_FF, D_MODEL), (2 * DEPTH * D_FF) ** -0.5),
    }


def reference(x, g_mix_norm, w_in, g_q, g_k, pe_cmp, w_cmp1, b_cmp1, w_cmp2, b_cmp2,
              conv_w, g_out, w_o, g_ffn_norm, w_up, w_down):
    B, T = x.shape[0], x.shape[1]
    sizes = [ATTN_WIDTH] + [KV_WIDTH] * 6 + [N_BRANCH * N_ATTN_HEADS] + [CONV_WIDTH] * 3
    offs = [int(o) for o in np.cumsum(sizes)[:-1]]
    kv_shape = (B, T, N_KV_GROUPS, HEAD_DIM)
    for l in range(DEPTH):
        h = rms_norm(x, g_mix_norm[l])
        z = h @ w_in[l]
        q, kc, vc, ks_, vs_, kw_, vw_, gl, hc, cg, bg = jnp.split(z, offs, axis=-1)
        q = rms_norm(q.reshape(B, T, N_ATTN_HEADS, HEAD_DIM), g_q[l])
        k_cmp = rms_norm(compress(kc.reshape(kv_shape), pe_cmp[l, 0], w_cmp1[l, 0], b_cmp1[l, 0],
                                  w_cmp2[l, 0], b_cmp2[l, 0]), g_k[l, 0])
        v_cmp = compress(vc.reshape(kv_shape), pe_cmp[l, 1], w_cmp1[l, 1], b_cmp1[l, 1],
                         w_cmp2[l, 1], b_cmp2[l, 1])
        k_slc = rms_norm(ks_.reshape(kv_shape), g_k[l, 1])
        k_win = rms_norm(kw_.reshape(kv_shape), g_k[l, 2])
        gates = jax.nn.sigmoid(gl).reshape(B, T, N_ATTN_HEADS, N_BRANCH)
        attn_out = nsa_attention(q, k_cmp, v_cmp, k_slc, vs_.reshape(kv_shape),
                                 k_win, vw_.reshape(kv_shape), gates)
        conv_out = short_conv_mixer(hc, cg, bg, conv_w[l])
        mixed = jnp.concatenate([rms_norm(attn_out, g_out[l, :ATTN_WIDTH]),
                                 rms_norm(conv_out, g_out[l, ATTN_WIDTH:])], axis=-1)
        x = x + mixed @ w_o[l]
        h = rms_norm(x, g_ffn_norm[l])
        x = x + jnp.square(jax.nn.relu(h @ w_up[l])) @ w_down[l]
    return x
```

```python
import contextlib
import numpy as np
import ml_dtypes
import concourse.bass as bass
import concourse.mybir as mybir
from concourse.bass_utils import run_bass_kernel_spmd

F32 = mybir.dt.float32
BF16 = mybir.dt.bfloat16
AF = mybir.ActivationFunctionType
ALU = mybir.AluOpType
AX = mybir.AxisListType
NPBF = ml_dtypes.bfloat16

ENGS = ['tensor', 'vector', 'scalar', 'gpsimd', 'sync']
NCORES = 8
D = 1024
T = 16384
NB = 2
DEPTH = 4
INW = 2840
EPS = 1e-6
NEGM = -30000.0


def key_of(ap):
    return ap.tensor.name


class Prog:
    def __init__(self, nc):
        self.nc = nc
        self.q = {e: [] for e in ENGS}
        self.semcount = {}
        self.res = {}
        self.seen = {e: {} for e in ENGS}

    def _need(self, eng, reads, writes):
        need = {}

        def add(tok):
            if tok is None:
                return
            s, v = tok
            if eng == 'tensor' and s == 'e_tensor':
                return
            if need.get(s, 0) < v:
                need[s] = v
        for k in reads:
            st = self.res.get(k)
            if st:
                add(st['w'])
        for k in writes:
            st = self.res.get(k)
            if st:
                add(st['w'])
                for s, v in st['r'].items():
                    add((s, v))
        for s, v in need.items():
            if self.seen[eng].get(s, 0) < v:
                self.q[eng].append(('wait', s, v))
                self.seen[eng][s] = v

    def _update(self, reads, writes, tok):
        s, v = tok
        for k in reads:
            st = self.res.setdefault(k, {'w': None, 'r': {}})
            if st['r'].get(s, 0) < v:
                st['r'][s] = v
        for k in writes:
            self.res[k] = {'w': tok, 'r': {}}

    def op(self, eng, fn, reads=(), writes=()):
        reads = [r if isinstance(r, str) else key_of(r) for r in reads]
        writes = [r if isinstance(r, str) else key_of(r) for r in writes]
        self._need(eng, reads, writes)
        s = 'e_' + eng
        v = self.semcount.get(s, 0) + 1
        self.semcount[s] = v
        self.q[eng].append(('op', fn, s, 1))
        self._update(reads, writes, (s, v))

    def dma(self, eng, out, in_, reads=None, writes=None, sem=None, **kw):
        if reads is None:
            reads = [] if in_.tensor.name in self.dram_names else [in_]
        if writes is None:
            writes = [] if out.tensor.name in self.dram_names else [out]
        reads = [r if isinstance(r, str) else key_of(r) for r in reads]
        writes = [r if isinstance(r, str) else key_of(r) for r in writes]
        if sem is None:
            sem = writes[0] if writes else 'st_' + reads[0]
        self._need(eng, reads, writes)
        s = 'd_' + sem
        v = self.semcount.get(s, 0) + 16
        self.semcount[s] = v
        self.q[eng].append(('op', lambda e: e.dma_start(out=out, in_=in_, **kw), s, 16))
        self._update(reads, writes, (s, v))

    dram_names = set()

    def mm(self, out, lhsT, rhs, start=True, stop=True, reads=None, writes=None):
        self.op('tensor', lambda e: e.matmul(out, lhsT=lhsT, rhs=rhs, start=start, stop=stop),
                reads=reads if reads is not None else [lhsT, rhs],
                writes=writes if writes is not None else [out])

    def tr(self, out, in_, ident, reads=None, writes=None):
        self.op('tensor', lambda e: e.transpose(out, in_, ident),
                reads=reads if reads is not None else [in_, ident],
                writes=writes if writes is not None else [out])

    def act(self, out, in_, func, bias=None, scale=None, accum_out=None, reads=None, writes=None, eng='scalar'):
        kw = {}
        rd = [in_]
        if bias is not None:
            kw['bias'] = bias
            if not isinstance(bias, (int, float)):
                rd.append(bias)
        if scale is not None:
            kw['scale'] = scale
            if not isinstance(scale, (int, float)):
                rd.append(scale)
        wr = [out]
        if accum_out is not None:
            kw['accum_out'] = accum_out
            wr.append(accum_out)
        self.op('scalar', lambda e: e.activation(out=out, in_=in_, func=func, **kw),
                reads=reads if reads is not None else rd,
                writes=writes if writes is not None else wr)

    def tt(self, eng, out, in0, in1, op, reads=None, writes=None):
        self.op(eng, lambda e: e.tensor_tensor(out=out, in0=in0, in1=in1, op=op),
                reads=reads if reads is not None else [in0, in1],
                writes=writes if writes is not None else [out])

    def ts(self, eng, out, in0, s1, op0, s2=None, op1=None, reads=None, writes=None):
        rd = [in0]
        if not isinstance(s1, (int, float)):
            rd.append(s1)
        if s2 is not None and not isinstance(s2, (int, float)):
            rd.append(s2)
        if op1 is None:
            fn = lambda e: e.tensor_scalar(out=out, in0=in0, scalar1=s1, scalar2=None, op0=op0)
        else:
            fn = lambda e: e.tensor_scalar(out=out, in0=in0, scalar1=s1, scalar2=s2, op0=op0, op1=op1)
        self.op(eng, fn, reads=reads if reads is not None else rd,
                writes=writes if writes is not None else [out])

    def stt(self, out, in0, scalar, in1, op0, op1, reads=None, writes=None):
        rd = [in0, in1]
        if not isinstance(scalar, (int, float)):
            rd.append(scalar)
        self.op('vector', lambda e: e.scalar_tensor_tensor(out=out, in0=in0, scalar=scalar, in1=in1, op0=op0, op1=op1),
                reads=reads if reads is not None else rd,
                writes=writes if writes is not None else [out])

    def copy(self, eng, out, in_, reads=None, writes=None):
        if eng == 'scalar':
            fn = lambda e: e.activation(out=out, in_=in_, func=AF.Copy)
        else:
            fn = lambda e: e.tensor_copy(out=out, in_=in_)
        self.op(eng, fn, reads=reads if reads is not None else [in_],
                writes=writes if writes is not None else [out])

    def recip(self, out, in_):
        self.op('vector', lambda e: e.reciprocal(out=out, in_=in_), reads=[in_], writes=[out])

    def reduce(self, out, in_, op=ALU.add, axis=AX.X):
        self.op('vector', lambda e: e.tensor_reduce(out=out, in_=in_, axis=axis, op=op), reads=[in_], writes=[out])

    def memset(self, eng, ap, val):
        self.op(eng, lambda e: e.memset(ap, val), writes=[ap])

    def emit(self):
        nc = self.nc
        for s, v in self.semcount.items():
            if self.seen['sync'].get(s, 0) < v:
                self.q['sync'].append(('wait', s, v))
        with contextlib.ExitStack() as es:
            semh = {s: es.enter_context(nc.semaphore(s)) for s in self.semcount}
            block = es.enter_context(nc.Block())

            def mk(engname):
                def body(e):
                    for it in self.q[engname]:
                        if it[0] == 'wait':
                            e.wait_ge(semh[it[1]], it[2])
                        else:
                            ins = it[1](e)
                            ins.then_inc(semh[it[2]], it[3])
                return body
            for engname in ENGS:
                if self.q[engname]:
                    getattr(block, engname)(mk(engname))


def AP_(t, offset, dims):
    return bass.AP(t, offset, [list(d) for d in dims])


class Ctx:
    def __init__(self):
        self.nc = bass.Bass("TRN2", target_bir_lowering=False)
        self.es = contextlib.ExitStack()
        self.P = Prog(self.nc)
        self.P.dram_names = set()

    def din(self, name, shape, dt):
        self.P.dram_names.add(name)
        return self.nc.dram_tensor(name, list(shape), dt, kind="ExternalInput").ap()

    def dout(self, name, shape, dt):
        self.P.dram_names.add(name)
        return self.nc.dram_tensor(name, list(shape), dt, kind="ExternalOutput").ap()

    def sb(self, name, shape, dt):
        return self.es.enter_context(self.nc.sbuf_tensor(name, list(shape), dt))

    def ps(self, name, shape, dt):
        return self.es.enter_context(self.nc.psum_tensor(name, list(shape), dt))

    def finish(self):
        self.P.emit()
        self.es.close()
        return self.nc


A_NT = 34


def build_A():
    C = Ctx()
    P = C.P
    NTOK = A_NT * 128
    x = C.din("x", [NTOK, D], F32)
    w_in = C.din("w_in", [D, INW], F32)
    gmix = C.din("gmix", [128, 8], F32)
    gq = C.din("gq", [128, 512], F32)
    gk3 = C.din("gk3", [128, 3, 128], F32)
    cw = C.din("cw", [128, 4, 3], F32)
    peT = C.din("peT", [2, 64, 32], F32)
    w1 = C.din("w1", [2, 2048, 256], F32)
    b1b = C.din("b1b", [2, 128, 256], F32)
    w2 = C.din("w2", [2, 256, 64], F32)
    b2b = C.din("b2b", [128, 2, 64], F32)
    identb_d = C.din("identb", [128, 128], BF16)
    onesf_d = C.din("onesf", [128, 1], F32)

    qT_o = C.dout("qT", [512, 4096], BF16)
    ksT_o = C.dout("ksT", [128, 4096], BF16)
    kwT_o = C.dout("kwT", [128, 4096], BF16)
    vs_o = C.dout("vs", [4096, 128], BF16)
    vw_o = C.dout("vw", [4096, 128], BF16)
    gates_o = C.dout("gates", [4096, 24], F32)
    convT_o = C.dout("convT", [512, 4096], BF16)
    rconv_o = C.dout("rconv", [4096, 1], F32)
    kcmpT_o = C.dout("kcmpT", [128, 256], BF16)
    vcmp_o = C.dout("vcmp", [256, 128], BF16)

    Wbf = C.sb("Wbf", [128, 8, INW], BF16)
    wst = [C.sb("wst%d" % i, [128, INW // 2], F32) for i in range(2)]
    gmix_s = C.sb("gmix_s", [128, 8], F32)
    gq_s = C.sb("gq_s", [128, 512], F32)
    gk3_s = C.sb("gk3_s", [128, 3, 128], F32)
    cw_s = C.sb("cw_s", [128, 4, 3], F32)
    identb = C.sb("identb_s", [128, 128], BF16)
    onesf = C.sb("onesf_s", [128, 1], F32)
    b1b_s = C.sb("b1b_s", [128, 2, 256], F32)
    b2b_s = C.sb("b2b_s", [128, 2, 64], F32)
    peT_s = C.sb("peT_s", [64, 2, 32], F32)
    pebf = C.sb("pebf", [64, 2, 32, 128], BF16)
    W1bf = C.sb("W1bf", [128, 2, 32, 256], BF16)
    w1st = [C.sb("w1st%d" % i, [128, 8, 256], F32) for i in range(2)]
    W2bf = C.sb("W2bf", [128, 2, 2, 64], BF16)
    w2st = C.sb("w2st", [128, 2, 2, 64], F32)
    c1b = C.sb("c1b", [128, 2, 256], F32)

    xt = [C.sb("xt%d" % i, [128, D], F32) for i in range(2)]
    junk = C.sb("junk", [128, D], BF16)
    ss = C.sb("ss", [128, 1], F32)
    rstd = C.sb("rstd", [128, 1], F32)
    hn = C.sb("hn", [128, D], BF16)
    hT = C.sb("hT", [128, 8, 128], BF16)
    sq = C.sb("sq", [128, 1280], F32)
    ssg = C.sb("ssg", [128, 20], F32)
    rg = C.sb("rg", [128, 20], F32)
    qtmp = C.sb("qtmp", [128, 512], F32)
    ktmp = C.sb("ktmp", [128, 256], F32)
    nrm = C.sb("nrm", [128, 1024], BF16)
    trs = C.sb("trs", [128, 8, 128], BF16)
    vsw = C.sb("vsw", [128, 256], BF16)
    gts = C.sb("gts", [128, 24], F32)
    kcT_all = C.sb("kcT_all", [128, A_NT * 128], BF16)
    vcT_all = C.sb("vcT_all", [128, A_NT * 128], BF16)
    hcs = C.sb("hcs", [128, 4, 128], F32)
    uext = [C.sb("uext%d" % i, [128, 4, 130], F32) for i in range(2)]
    cv0 = C.sb("cv0", [128, 4, 128], F32)
    cv1 = C.sb("cv1", [128, 4, 128], F32)
    yv = C.sb("yv", [128, 4, 128], F32)
    ysq = C.sb("ysq", [128, 4, 128], F32)
    ybf = C.sb("ybf", [128, 4, 128], BF16)
    rc = C.sb("rc", [128, 1], F32)
    rc2 = C.sb("rc2", [128, 1], F32)
    hid = C.sb("hid", [128, 256], F32)
    g_x2 = C.sb("g_x2", [128, 256], F32)
    g_in = C.sb("g_in", [128, 256], F32)
    g_sg = C.sb("g_sg", [128, 256], F32)
    hbf = C.sb("hbf", [128, 256], BF16)
    hidT = C.sb("hidT", [128, 2, 128], BF16)
    co = C.sb("co", [128, 2, 2, 64], F32)
    cosq = C.sb("cosq", [128, 128], F32)
    css = C.sb("css", [128, 2], F32)
    crs = C.sb("crs", [128, 2], F32)
    kcn = C.sb("kcn", [128, 128], BF16)
    vcn = C.sb("vcn", [128, 128], BF16)
    kcnT = C.sb("kcnT", [128, 128], BF16)

    pT = C.ps("pT", [128, 8, 128], BF16)
    pz = [C.ps("pz%d" % i, [128, 512], F32) for i in range(3)]
    pc = [C.ps("pc%d" % i, [128, 4, 128], F32) for i in range(3)]
    px = C.ps("px", [128, 512], F32)

    P.dma('sync', gmix_s[:], gmix)
    P.dma('sync', gq_s[:], gq)
    P.dma('sync', gk3_s[:], gk3)
    P.dma('sync', cw_s[:], cw)
    P.dma('sync', identb[:], identb_d)
    P.dma('sync', onesf[:], onesf_d)
    P.dma('sync', b1b_s[:], b1b.rearrange("k p n -> p k n"))
    P.dma('sync', b2b_s[:], b2b)
    P.dma('sync', peT_s[:], peT.rearrange("k d j -> d k j"))
    P.dma('sync', w2st[:], w2.rearrange("k (c p) d -> p k c d", p=128))
    for kc in range(8):
        for hf in range(2):
            st = wst[(kc * 2 + hf) % 2]
            P.dma('sync' if hf == 0 else 'scalar', st[:], w_in[kc * 128:(kc + 1) * 128, hf * 1420:(hf + 1) * 1420])
            P.ts('vector' if hf == 0 else 'gpsimd', Wbf[:, kc, hf * 1420:(hf + 1) * 1420], st[:], gmix_s[:, kc:kc + 1], ALU.mult)

    for ti in range(A_NT):
        xb = xt[ti % 2]
        own = 1 <= ti <= 32
        col0 = (ti - 1) * 128
        P.dma('sync', xb[:], x[ti * 128:(ti + 1) * 128, :])
        P.act(junk[:], xb[:], AF.Square, accum_out=ss[:])
        P.act(rstd[:], ss[:], AF.Sqrt, bias=EPS, scale=1.0 / D)
        P.recip(rstd[:], rstd[:])
        P.act(hn[:], xb[:], AF.Copy, scale=rstd[:, 0:1])
        for kc in range(8):
            P.tr(pT[:, kc, :], hn[:, kc * 128:(kc + 1) * 128], identb[:])
        P.copy('vector', hT[:], pT[:])
        for bi, (c0, c1) in enumerate([(0, 512), (512, 1024), (1024, 1304)]):
            for kc in range(8):
                P.mm(pz[bi][:, 0:c1 - c0], hT[:, kc, :], Wbf[:, kc, c0:c1], start=(kc == 0), stop=(kc == 7))
        if ti <= 32:
            for ch in range(12):
                for kc in range(8):
                    P.mm(pc[ch // 4][:, ch % 4, :], Wbf[:, kc, 1304 + ch * 128:1304 + (ch + 1) * 128], hT[:, kc, :],
                         start=(kc == 0), stop=(kc == 7))
        P.act(sq[:, 0:512], pz[0][:], AF.Square)
        P.act(sq[:, 512:1024], pz[1][:], AF.Square)
        P.act(sq[:, 1024:1280], pz[2][:, 0:256], AF.Square)
        P.reduce(ssg[:], sq[:].rearrange("p (g d) -> p g d", d=64))
        P.act(rg[:, 0:8], ssg[:, 0:8], AF.Sqrt, bias=64 * EPS, scale=1.0)
        P.act(rg[:, 8:20], ssg[:, 8:20], AF.Sqrt, bias=EPS, scale=1.0 / 64)
        P.recip(rg[:], rg[:])
        if own:
            P.tt('vector', qtmp[:].rearrange("p (g d) -> p g d", d=64), pz[0][:].rearrange("p (g d) -> p g d", d=64),
                 AP_(rg, 0, [[20, 128], [1, 8], [0, 64]]), ALU.mult)
            P.tt('gpsimd', nrm[:, 0:512], qtmp[:], gq_s[:], ALU.mult)
            P.tt('vector', ktmp[:, 0:128].rearrange("p (g d) -> p g d", d=64), pz[1][:, 256:384].rearrange("p (g d) -> p g d", d=64),
                 AP_(rg, 12, [[20, 128], [1, 2], [0, 64]]), ALU.mult)
            P.tt('vector', ktmp[:, 128:256].rearrange("p (g d) -> p g d", d=64), pz[2][:, 0:128].rearrange("p (g d) -> p g d", d=64),
                 AP_(rg, 16, [[20, 128], [1, 2], [0, 64]]), ALU.mult)
            P.tt('gpsimd', nrm[:, 512:768], ktmp[:], gk3_s[:, 1:3, :].rearrange("p a d -> p (a d)"), ALU.mult)
            P.copy('scalar', vsw[:, 0:128], pz[1][:, 384:512])
            P.copy('scalar', vsw[:, 128:256], pz[2][:, 128:256])
            P.dma('scalar', vs_o[col0:col0 + 128, :], vsw[:, 0:128])
            P.dma('scalar', vw_o[col0:col0 + 128, :], vsw[:, 128:256])
            P.act(gts[:], pz[2][:, 256:280], AF.Sigmoid)
            P.dma('scalar', gates_o[col0:col0 + 128, :], gts[:])
        if ti >= 1:
            P.copy('vector', nrm[:, 768:1024], pz[1][:, 0:256])
        blks = (list(range(6)) if own else []) + ([6, 7] if ti >= 1 else [])
        if blks:
            for bk in blks:
                P.tr(pT[:, bk, :], nrm[:, bk * 128:(bk + 1) * 128], identb[:])
            if own:
                P.copy('vector', trs[:, 0:6, :], pT[:, 0:6, :])
                P.dma('sync', qT_o[:, col0:col0 + 128].rearrange("(b p) t -> p b t", p=128), trs[:, 0:4, :])
                P.dma('sync', ksT_o[:, col0:col0 + 128], trs[:, 4, :])
                P.dma('sync', kwT_o[:, col0:col0 + 128], trs[:, 5, :])
            P.copy('vector', kcT_all[:, ti * 128:(ti + 1) * 128], pT[:, 6, :])
            P.copy('vector', vcT_all[:, ti * 128:(ti + 1) * 128], pT[:, 7, :])
        if ti <= 32:
            ub = uext[ti % 2]
            ubp = uext[(ti + 1) % 2]
            if ti == 0:
                P.memset('gpsimd', ub[:, :, 0:2], 0.0)
            else:
                P.copy('gpsimd', ub[:, :, 0:2], ubp[:, :, 128:130])
            P.copy('scalar', hcs[:], pc[0][:])
            P.tt('vector', ub[:, :, 2:130], pc[1][:], hcs[:], ALU.mult)
            if own:
                P.tt('gpsimd', cv0[:], ub[:, :, 2:130], AP_(cw_s, 2, [[12, 128], [3, 4], [0, 128]]), ALU.mult)
                P.tt('gpsimd', cv1[:], ub[:, :, 1:129], AP_(cw_s, 1, [[12, 128], [3, 4], [0, 128]]), ALU.mult)
                P.tt('gpsimd', cv0[:], cv0[:], cv1[:], ALU.add)
                P.tt('gpsimd', cv1[:], ub[:, :, 0:128], AP_(cw_s, 0, [[12, 128], [3, 4], [0, 128]]), ALU.mult)
                P.tt('gpsimd', cv0[:], cv0[:], cv1[:], ALU.add)
                P.tt('vector', yv[:], pc[2][:], cv0[:], ALU.mult)
                P.copy('gpsimd', ybf[:], yv[:])
                P.dma('scalar', convT_o[:, col0:col0 + 128].rearrange("(b p) t -> p b t", p=128), ybf[:])
                P.act(ysq[:], yv[:], AF.Square)
                for ch in range(4):
                    P.mm(px[:, 0:1], ysq[:, ch, :], onesf[:], start=(ch == 0), stop=(ch == 3))
                P.act(rc[:], px[:, 0:1], AF.Sqrt, bias=EPS, scale=1.0 / 512)
                P.recip(rc2[:], rc[:])
                P.dma('scalar', rconv_o[col0:col0 + 128, :], rc2[:])

    P.copy('vector', W2bf[:], w2st[:])
    P.copy('vector', pebf[:], AP_(peT_s, 0, [[64, 64], [32, 2], [1, 32], [0, 128]]))
    n = 0
    for kv in range(2):
        for jq in range(4):
            st = w1st[n % 2]
            n += 1
            src = w1[kv, jq * 512:(jq + 1) * 512, :].rearrange("(j d) n -> d j n", d=64)
            P.dma('sync', st[0:64, :, :], src, writes=[st])
            P.dma('scalar', st[64:128, :, :], src, writes=[st], sem=st.name + "_b")
            P.copy('vector', W1bf[:, kv, jq * 8:(jq + 1) * 8, :], st[:])
    for kv in range(2):
        for j in range(32):
            P.mm(px[:, 0:256], pebf[:, kv, j, :], W1bf[0:64, kv, j, :], start=(j == 0), stop=(j == 31))
        P.tt('vector', c1b[:, kv, :], px[:, 0:256], b1b_s[:, kv, :], ALU.add)
    for ib in range(2):
        for kv in range(2):
            src_all = kcT_all if kv == 0 else vcT_all
            for g in range(2):
                base = 128 + 16 * 128 * ib
                for j in range(32):
                    lhs = AP_(src_all, 64 * g * (A_NT * 128) + base + j, [[A_NT * 128, 64], [16, 128]])
                    P.mm(px[:, 0:256], lhs, W1bf[64 * g:64 * g + 64, kv, j, :], start=(j == 0), stop=(j == 31),
                         reads=[src_all, W1bf])
                P.tt('vector', hid[:], px[:, 0:256], c1b[:, kv, :], ALU.add)
                P.act(g_x2[:], hid[:], AF.Square)
                P.ts('vector', g_x2[:], g_x2[:], 0.044715, ALU.mult, 1.0, ALU.add)
                P.tt('vector', g_in[:], g_x2[:], hid[:], ALU.mult)
                P.act(g_sg[:], g_in[:], AF.Sigmoid, scale=1.5957691216057308)
                P.tt('vector', hbf[:], hid[:], g_sg[:], ALU.mult)
                for c in range(2):
                    P.tr(pT[:, c, :], hbf[:, c * 128:(c + 1) * 128], identb[:])
                P.copy('vector', hidT[:], pT[:, 0:2, :])
                for c in range(2):
                    P.mm(pz[0][:, 0:64], hidT[:, c, :], W2bf[:, kv, c, :], start=(c == 0), stop=(c == 1))
                P.tt('vector', co[:, kv, g, :], pz[0][:, 0:64], b2b_s[:, kv, :], ALU.add)
        P.act(cosq[:], co[:, 0, :, :].rearrange("p g d -> p (g d)"), AF.Square)
        P.reduce(css[:], cosq[:].rearrange("p (g d) -> p g d", d=64))
        P.act(crs[:], css[:], AF.Sqrt, bias=EPS, scale=1.0 / 64)
        P.recip(crs[:], crs[:])
        P.tt('vector', cosq[:].rearrange("p (g d) -> p g d", d=64), co[:, 0, :, :], AP_(crs, 0, [[2, 128], [1, 2], [0, 64]]), ALU.mult)
        P.tt('vector', kcn[:], cosq[:], gk3_s[:, 0, :], ALU.mult)
        P.tr(pT[:, 0, :], kcn[:], identb[:])
        P.copy('vector', kcnT[:], pT[:, 0, :])
        P.dma('sync', kcmpT_o[:, ib * 128:(ib + 1) * 128], kcnT[:])
        P.copy('vector', vcn[:], co[:, 1, :, :].rearrange("p g d -> p (g d)"))
        P.dma('sync', vcmp_o[ib * 128:(ib + 1) * 128, :], vcn[:])
    return C.finish()


def key_of(ap):
    t = getattr(ap, 'tensor', ap)
    return t.name


IDENTB = np.eye(128, dtype=np.float32).astype(NPBF)
ONESF = np.ones((128, 1), np.float32)


def prep_A(l, xfull, p):
    maps = []
    cw = np.ascontiguousarray(p['conv_w'][l].reshape(3, 4, 128).transpose(2, 1, 0))
    common = {
        'w_in': np.ascontiguousarray(p['w_in'][l]),
        'gmix': np.ascontiguousarray(p['g_mix_norm'][l].reshape(8, 128).T),
        'gq': np.ascontiguousarray(np.tile(p['g_q'][l][None, :], (128, 8))),
        'gk3': np.ascontiguousarray(np.broadcast_to(np.tile(p['g_k'][l], (1, 2))[None], (128, 3, 128))),
        'cw': cw,
        'peT': np.ascontiguousarray(p['pe_cmp'][l].transpose(0, 2, 1)),
        'w1': np.ascontiguousarray(p['w_cmp1'][l]),
        'b1b': np.ascontiguousarray(np.broadcast_to(p['b_cmp1'][l][:, None, :], (2, 128, 256))),
        'w2': np.ascontiguousarray(p['w_cmp2'][l]),
        'b2b': np.ascontiguousarray(np.broadcast_to(p['b_cmp2'][l][None], (128, 2, 64))),
        'identb': IDENTB, 'onesf': ONESF,
    }
    for c in range(NCORES):
        b, s = divmod(c, 4)
        t0 = s * 4096
        xl = np.zeros((A_NT * 128, D), np.float32)
        lo, hi = t0 - 128, t0 + 4096 + 128
        slo, shi = max(lo, 0), min(hi, T)
        xl[slo - lo:shi - lo] = xfull[b, slo:shi]
        m = dict(common)
        m['x'] = xl
        maps.append(m)
    return maps


B_NM = 32


def build_B():
    C = Ctx()
    P = C.P
    ksT = C.din("ksT", [2, 68, T], BF16)
    vsx = C.din("vsx", [128, 128, 2, 65], BF16)
    kcT = C.din("kcT", [2, 68, 1024], BF16)
    vcx = C.din("vcx", [128, 8, 2, 321], BF16)
    qT = C.din("qT", [B_NM, 2, 68, 512], BF16)
    kwT = C.din("kwT", [B_NM, 2, 68, 640], BF16)
    vwx = C.din("vwx", [B_NM, 128, 5, 2, 65], BF16)
    gates = C.din("gates", [B_NM, 128, 24], F32)
    smask = C.din("smask", [128, 4, 512], BF16)
    cmask = C.din("cmask", [128, 5, 512], BF16)
    wmask = C.din("wmask", [128, 2, 5, 512], BF16)
    Eall = C.din("Eall", [128, 64, 128], BF16)
    Bsel = C.din("Bsel", [128, 512], F32)
    identb_d = C.din("identb", [128, 128], BF16)
    identf_d = C.din("identf", [128, 128], F32)
    attnT_o = C.dout("attnT", [B_NM, 128, 4, 128], BF16)
    rattn_o = C.dout("rattn", [B_NM, 128, 1], F32)

    ksT_s = [C.sb("ksT_s%d" % g, [68, T], BF16) for g in range(2)]
    vsx_s = C.sb("vsx_s", [128, 128, 2, 65], BF16)
    kcT_s = C.sb("kcT_s", [68, 2, 1024], BF16)
    vcx_s = C.sb("vcx_s", [128, 8, 2, 321], BF16)
    smask_s = C.sb("smask_s", [128, 4, 512], BF16)
    cmask_s = C.sb("cmask_s", [128, 5, 512], BF16)
    wmask_s = C.sb("wmask_s", [128, 2, 5, 512], BF16)
    Eall_s = C.sb("Eall_s", [128, 64, 128], BF16)
    Bsel_s = C.sb("Bsel_s", [128, 512], F32)
    identb = C.sb("identb_s", [128, 128], BF16)
    identf = C.sb("identf_s", [128, 128], F32)
    qT_s = [C.sb("qT_s%d" % i, [68, 2, 512], BF16) for i in range(2)]
    kwT_s = [C.sb("kwT_s%d" % i, [68, 2, 640], BF16) for i in range(2)]
    vwx_s = [C.sb("vwx_s%d" % i, [128, 5, 2, 65], BF16) for i in range(2)]
    gates_s = [C.sb("gates_s%d" % i, [128, 24], F32) for i in range(2)]
    NPB = 4
    Pb = [C.sb("Pb%d" % i, [128, 512], BF16) for i in range(NPB)]
    Pc = [C.sb("Pc%d" % g, [128, 8, 512], BF16) for g in range(2)]
    oc = C.sb("oc", [128, 4, 321], F32)
    den4 = C.sb("den4", [128, 4], F32)
    rd4 = C.sb("rd4", [128, 4], F32)
    coef4 = C.sb("coef4", [128, 4], F32)
    imp = [C.sb("imp%d" % g, [128, 256], F32) for g in range(2)]
    tmpi = C.sb("tmpi", [128, 256], F32)
    m8 = C.sb("m8", [128, 16], F32)
    thr = C.sb("thr", [128, 1], F32)
    nm = C.sb("nm", [128, 256], BF16)
    nmT = [C.sb("nmT%d" % g, [128, 2, 4, 128], BF16) for g in range(2)]
    osT = C.sb("osT", [65, 512], F32)
    otmp = C.sb("otmp", [128, 4, 64], F32)
    acc = C.sb("acc", [128, 8, 64], F32)
    accb = C.sb("accb", [128, 512], BF16)
    junk = C.sb("junk", [128, 512], BF16)
    ssq = C.sb("ssq", [128, 1], F32)
    rat = C.sb("rat", [128, 1], F32)
    rat2 = C.sb("rat2", [128, 1], F32)
    atT = C.sb("atT", [128, 4, 128], BF16)

    ps_s = [C.ps("ps_s%d" % i, [128, 512], F32) for i in range(3)]
    ps_o = [C.ps("ps_o%d" % i, [128, 512], F32) for i in range(2)]
    ps_acc = [C.ps("ps_acc%d" % i, [128, 512], F32) for i in range(2)]
    pm = C.ps("pm", [128, 512], F32)
    pm_b = pm[:].bitcast(BF16)

    P.dma('sync', identb[:], identb_d)
    P.dma('sync', identf[:], identf_d)
    P.dma('sync', kcT_s[:], kcT.rearrange("g p n -> p g n"))
    P.dma('sync', vcx_s[:], vcx)
    P.dma('sync', cmask_s[:], cmask)
    P.dma('sync', Bsel_s[:], Bsel)
    P.dma('scalar', wmask_s[:], wmask)
    P.dma('scalar', smask_s[:], smask)
    P.dma('scalar', Eall_s[:], Eall)
    for g in range(2):
        for hf in range(4):
            P.dma('sync' if (g * 4 + hf) % 2 == 0 else 'scalar', ksT_s[g][:, hf * 4096:(hf + 1) * 4096], ksT[g, :, hf * 4096:(hf + 1) * 4096],
                  sem="ksT%d_%d" % (g, hf))
    for hf in range(4):
        P.dma('gpsimd', vsx_s[:, hf * 32:(hf + 1) * 32], vsx[:, hf * 32:(hf + 1) * 32], sem="vsx_%d" % hf)

    state = {'s': 0, 'p': 0}

    def run_stream(tiles):
        LA = 2
        n = len(tiles)
        bufs = []
        for i in range(n + LA):
            if i < n:
                ps = ps_s[state['s'] % 3]
                pb = Pb[state['p'] % NPB]
                state['s'] += 1
                state['p'] += 1
                tiles[i][0](ps)
                P.act(pb[:], ps[:], AF.Exp)
                bufs.append(pb)
            if i - LA >= 0:
                tiles[i - LA][1](bufs[i - LA])

    def o_epilogue(psacc, g, br, first):
        P.copy('vector', osT[:], psacc[0:65, :])
        pmv = pm[:, 0:260].rearrange("p (r d) -> p r d", d=65)
        for r in range(4):
            P.tr(pmv[:, r, :], osT[0:65, r * 128:(r + 1) * 128], identf[0:65, 0:65])
        P.recip(rd4[:], pmv[:, :, 64])
        P.tt('vector', coef4[:], rd4[:], AP_(gs, 12 * g + br, [[24, 128], [3, 4]]), ALU.mult)
        dst = acc[:, 4 * g:4 * g + 4, :]
        if first:
            P.tt('vector', dst, pmv[:, :, 0:64], AP_(coef4, 0, [[4, 128], [1, 4], [0, 64]]), ALU.mult)
        else:
            P.tt('vector', otmp[:], pmv[:, :, 0:64], AP_(coef4, 0, [[4, 128], [1, 4], [0, 64]]), ALU.mult)
            P.tt('gpsimd', dst, dst, otmp[:], ALU.add)

    for m in range(B_NM):
        qs = qT_s[m % 2]
        kws = kwT_s[m % 2]
        vws = vwx_s[m % 2]
        gs = gates_s[m % 2]
        P.dma('sync', qs[:], qT[m].rearrange("g p n -> p g n"))
        P.dma('scalar', kws[:], kwT[m].rearrange("g p n -> p g n"))
        P.dma('sync', vws[:], vwx[m])
        P.dma('scalar', gs[:], gates[m])
        n_it = (32 * m + 30) // 128 + 1
        for g in range(2):
            for it in range(n_it):
                ps = ps_s[state['s'] % 3]
                state['s'] += 1
                delta = 128 * it - 32 * m
                masked = delta >= -128
                P.mm(ps[:], kcT_s[:, g, it * 128:(it + 1) * 128], qs[:, g, :], start=True, stop=not masked)
                if masked:
                    P.mm(ps[:], identb[:], cmask_s[:, (-delta) // 32, :], start=False, stop=True)
                P.act(Pc[g][:, it, :], ps[:], AF.Exp)
            for r in range(4):
                po = ps_o[r % 2]
                for it in range(n_it):
                    P.mm(po[:, 0:321], Pc[g][:, it, r * 128:(r + 1) * 128], vcx_s[:, it, g, :], start=(it == 0), stop=(it == n_it - 1))
                P.copy('scalar', oc[:, r, :], po[:, 0:321])
            P.ts('vector', den4[:], oc[:, :, 64], 1e-30, ALU.max)
            P.recip(rd4[:], den4[:])
            P.ts('vector', imp[g][:], oc[:, 0, 65:321], rd4[:, 0:1], ALU.mult)
            for r in range(1, 4):
                P.stt(imp[g][:], oc[:, r, 65:321], rd4[:, r:r + 1], imp[g][:], ALU.mult, ALU.add)
            P.tt('vector', coef4[:], rd4[:], AP_(gs, 12 * g + 0, [[24, 128], [3, 4]]), ALU.mult)
            P.tt('vector', acc[:, 4 * g:4 * g + 4, :], oc[:, :, 0:64], AP_(coef4, 0, [[4, 128], [1, 4], [0, 64]]), ALU.mult)
            P.tt('vector', imp[g][:], imp[g][:], Bsel_s[:, 256 - 8 * m:512 - 8 * m], ALU.add)
            P.ts('vector', imp[g][:, 0:1], imp[g][:, 0:1], 1e4, ALU.add)
            P.op('vector', lambda e, g=g: e.max(out=m8[:, 0:8], in_=imp[g][:]), reads=[imp[g]], writes=[m8])
            P.op('vector', lambda e, g=g: e.match_replace(out=tmpi[:], in_to_replace=m8[:, 0:8], in_values=imp[g][:], imm_value=-3.0e38),
                 reads=[imp[g], m8], writes=[tmpi])
            P.op('vector', lambda e: e.max(out=m8[:, 8:16], in_=tmpi[:]), reads=[tmpi], writes=[m8])
            P.ts('vector', thr[:], m8[:, 15:16], -1e29, ALU.max)
            P.ts('vector', nm[:], imp[g][:], thr[:, 0:1], ALU.is_lt, NEGM, ALU.mult)
            for hf in range(2):
                P.tr(pm_b[:, hf * 128:(hf + 1) * 128], nm[:, hf * 128:(hf + 1) * 128], identb[:])
            P.copy('vector', nmT[g][:], AP_(pm_b.tensor, pm_b.offset, [list(pm_b.ap[0]), [128, 2], [0, 4], [1, 128]]), reads=[pm])
        tiles = []
        for g in range(2):
            for w in range(5):
                def eS(ps, g=g, w=w):
                    has_mask = (m == 0) or (w in (0, 4))
                    P.mm(ps[:], kws[:, g, w * 128:(w + 1) * 128], qs[:, g, :], start=True, stop=not has_mask)
                    if has_mask:
                        P.mm(ps[:], identb[:], wmask_s[:, 0 if m == 0 else 1, w, :], start=False, stop=True)

                def ePV(pb, g=g, w=w):
                    P.mm(ps_acc[g][0:65, :], vws[:, w, g, :], pb[:], start=(w == 0), stop=(w == 4))
                tiles.append((eS, ePV))
        run_stream(tiles)
        for g in range(2):
            o_epilogue(ps_acc[g], g, 2, False)
        nkt = 4 * m + 4
        for g in range(2):
            tiles = []
            for kt in range(nkt):
                def eS(ps, g=g, kt=kt):
                    diag = kt >= 4 * m
                    P.mm(ps[:], ksT_s[g][:, kt * 128:(kt + 1) * 128], qs[:, g, :], start=True, stop=False)
                    P.mm(ps[:], Eall_s[:, kt % 64, :], nmT[g][:, kt // 64, :, :].rearrange("p a q -> p (a q)"), start=False, stop=not diag)
                    if diag:
                        P.mm(ps[:], identb[:], smask_s[:, kt - 4 * m, :], start=False, stop=True)

                def ePV(pb, g=g, kt=kt):
                    P.mm(ps_acc[g][0:65, :], vsx_s[:, kt, g, :], pb[:], start=(kt == 0), stop=(kt == nkt - 1))
                tiles.append((eS, ePV))
            run_stream(tiles)
            o_epilogue(ps_acc[g], g, 1, False)
        accf = acc[:].rearrange("p h d -> p (h d)")
        P.act(junk[:], accf, AF.Square, accum_out=ssq[:])
        P.act(rat[:], ssq[:], AF.Sqrt, bias=EPS, scale=1.0 / 512)
        P.recip(rat2[:], rat[:])
        P.dma('gpsimd', rattn_o[m], rat2[:])
        P.copy('gpsimd', accb[:], accf)
        for bk in range(4):
            P.tr(pm_b[:, bk * 128:(bk + 1) * 128], accb[:, bk * 128:(bk + 1) * 128], identb[:])
        P.copy('vector', atT[:].rearrange("p b q -> p (b q)"), pm_b[:, 0:512], reads=[pm])
        P.dma('gpsimd', attnT_o[m], atT[:])
    return C.finish()


def _bf(a):
    return np.ascontiguousarray(a).astype(NPBF)


def _selmap():
    i = np.arange(1024)[:, None] * 16
    j = np.arange(256)[None, :] * 64
    sh = np.clip(np.minimum(i + 32, j + 64) - np.maximum(i, j), 0, None).astype(np.float32) / 32.0
    sh[1023] = 0.0
    return sh


def _pos_rows(pos):
    pos = np.maximum(pos, 0)
    return np.stack([np.ones_like(pos), np.ones_like(pos), pos % 128, pos // 128]).astype(np.float32)


_BCONST = {}


def b_consts(s):
    if s in _BCONST:
        return _BCONST[s]
    ki = np.arange(128)[:, None]
    qi = np.arange(128)[None, :]
    tri_gt = np.where(ki > qi, NEGM, 0.0).astype(np.float32)
    tri_le = np.where(ki <= qi, NEGM, 0.0).astype(np.float32)
    full = np.full((128, 128), NEGM, np.float32)
    zero = np.zeros((128, 128), np.float32)
    smask = np.stack([zero if d < s else (tri_gt if d == s else full) for d in range(4)], axis=1)
    smask = np.tile(smask, (1, 1, 4))
    cm = []
    for v in range(5):
        ip = ki - 32 * v
        vis = (16 * ip + 31) <= (128 * s + qi)
        cm.append(np.where(vis, 0.0, NEGM).astype(np.float32))
    cmask = np.tile(np.stack(cm, axis=1), (1, 1, 4))
    wm = np.zeros((128, 2, 5, 128), np.float32)
    for var in range(2):
        for w in range(5):
            mk = zero
            if w == 0:
                mk = tri_le
            if w == 4:
                mk = tri_gt
            if var == 0 and w < 4 - s:
                mk = full
            wm[:, var, w, :] = mk
    wmask = np.tile(wm, (1, 1, 1, 4))
    E = np.zeros((128, 64, 128), np.float32)
    for kt in range(64):
        for k in range(128):
            E[2 * kt + k // 64, kt, k] = 1.0
    r = np.arange(512)[None, :] - 256 - 2 * s
    qq = np.arange(128)[:, None]
    Brel = np.zeros((128, 512), np.float32)
    Brel = np.where(r >= 2, np.float32(-1e30), Brel)
    Brel = np.where(r == 1, np.where(qq >= 64, np.float32(1e4), np.float32(-1e30)), Brel)
    Brel = np.where(r == 0, np.float32(1e4), Brel)
    Brel = np.where((r == -1) & (qq < 64), np.float32(1e4), Brel)
    out = {'smask': _bf(smask), 'cmask': _bf(cmask), 'wmask': _bf(wmask), 'Eall': _bf(E),
           'Bsel': np.ascontiguousarray(Brel.astype(np.float32)), 'identb': IDENTB,
           'identf': np.eye(128, dtype=np.float32)}
    _BCONST[s] = out
    return out


def prep_B(aout):
    maps = [None] * NCORES
    selmap = _selmap()
    for b in range(NB):
        cat = lambda k, ax: np.concatenate([np.asarray(aout[4 * b + s][k]) for s in range(4)], axis=ax)
        qT_f = cat('qT', 1)
        ksT_f = cat('ksT', 1)
        kwT_f = cat('kwT', 1)
        vs_f = cat('vs', 0)
        vw_f = cat('vw', 0)
        gates_f = cat('gates', 0)
        kcT_f = cat('kcmpT', 1).copy()
        vc_f = cat('vcmp', 0).copy()
        kcT_f[:, 1023] = 0
        vc_f[1023, :] = 0
        pos = np.arange(T)
        ksT_in = np.zeros((2, 68, T), NPBF)
        kcT_in = np.zeros((2, 68, 1024), NPBF)
        for g in range(2):
            ksT_in[g, 0:64] = ksT_f[64 * g:64 * g + 64]
            ksT_in[g, 64:68] = _pos_rows(pos)
            kcT_in[g, 0:64] = kcT_f[64 * g:64 * g + 64]
            kcT_in[g, 64:68] = _pos_rows(np.arange(1024) * 16 + 31)
        vsx_in = np.ones((128, 128, 2, 65), NPBF)
        vsx_in[:, :, :, 0:64] = vs_f.reshape(128, 128, 2, 64).transpose(1, 0, 2, 3)
        vcx_in = np.zeros((128, 8, 2, 321), NPBF)
        vcx_in[:, :, :, 0:64] = vc_f.reshape(8, 128, 2, 64).transpose(1, 0, 2, 3)
        vcx_in[:, :, :, 64] = 1
        vcx_in[127, 7, :, 64] = 0
        vcx_in[:, :, :, 65:321] = selmap.reshape(8, 128, 1, 256).transpose(1, 0, 2, 3)
        kw_pad = np.concatenate([np.zeros((128, 512), NPBF), kwT_f], axis=1)
        vw_pad = np.concatenate([np.zeros((512, 128), NPBF), vw_f], axis=0)
        for s in range(4):
            qT_in = np.zeros((B_NM, 2, 68, 512), NPBF)
            kwT_in = np.zeros((B_NM, 2, 68, 640), NPBF)
            vwx_in = np.ones((B_NM, 128, 5, 2, 65), NPBF)
            gates_in = np.zeros((B_NM, 128, 24), np.float32)
            qi = np.arange(128)
            for m in range(B_NM):
                c = 4 * m + s
                t0 = 128 * c
                for g in range(2):
                    for r in range(4):
                        h = 4 * g + r
                        sl = 2.0 ** (-(h + 1))
                        qT_in[m, g, 0:64, r * 128:(r + 1) * 128] = qT_f[h * 64:(h + 1) * 64, t0:t0 + 128]
                        qT_in[m, g, 64, r * 128:(r + 1) * 128] = -sl * qi
                        qT_in[m, g, 65, r * 128:(r + 1) * 128] = -sl * 128.0 * c
                        qT_in[m, g, 66, r * 128:(r + 1) * 128] = sl
                        qT_in[m, g, 67, r * 128:(r + 1) * 128] = sl * 128.0
                    kwT_in[m, g, 0:64] = kw_pad[64 * g:64 * g + 64, t0:t0 + 640]
                    kwT_in[m, g, 64:68] = _pos_rows(np.arange(t0 - 512, t0 + 128))
                vwx_in[m, :, :, :, 0:64] = vw_pad[t0:t0 + 640].reshape(5, 128, 2, 64).transpose(1, 0, 2, 3)
                gates_in[m] = gates_f[t0:t0 + 128]
            mp = {'ksT': ksT_in, 'vsx': vsx_in, 'kcT': kcT_in, 'vcx': vcx_in, 'qT': qT_in, 'kwT': kwT_in,
                  'vwx': vwx_in, 'gates': gates_in}
            mp.update(b_consts(s))
            maps[4 * b + s] = mp
    return maps


DFF = 4096


def build_C():
    C = Ctx()
    P = C.P
    x_d = C.din("x", [B_NM, 128, D], F32)
    attnT_d = C.din("attnT", [B_NM, 128, 4, 128], BF16)
    rattn_d = C.din("rattn", [B_NM, 128, 1], F32)
    convT_d = C.din("convT", [B_NM, 128, 4, 128], BF16)
    rconv_d = C.din("rconv", [B_NM, 128, 1], F32)
    w_o = C.din("w_o", [D, D], F32)
    gout = C.din("gout", [128, 8], F32)
    gffn = C.din("gffn", [128, D], F32)
    w_up = C.din("w_up", [D, DFF], F32)
    w_dn = C.din("w_dn", [DFF, D], F32)
    identb_d = C.din("identb", [128, 128], BF16)
    xo_d = C.dout("xo", [B_NM, 128, D], F32)

    Wo = C.sb("Wo", [128, 8, D], BF16)
    Wdn = C.sb("Wdn", [128, 32, D], BF16)
    wst = [C.sb("wst%d" % i, [128, 8, 512], F32) for i in range(2)]
    Wup = [C.sb("Wup%d" % i, [128, 8, 512], BF16) for i in range(2)]
    gout_s = C.sb("gout_s", [128, 8], F32)
    gffn_s = C.sb("gffn_s", [128, D], F32)
    identb = C.sb("identb_s", [128, 128], BF16)
    xt = [C.sb("xt%d" % i, [128, D], F32) for i in range(2)]
    x1 = [C.sb("x1_%d" % i, [128, D], F32) for i in range(4)]
    at_s = [C.sb("at_s%d" % i, [128, 4, 128], BF16) for i in range(2)]
    cv_s = [C.sb("cv_s%d" % i, [128, 4, 128], BF16) for i in range(2)]
    ra_s = [C.sb("ra_s%d" % i, [128, 1], F32) for i in range(2)]
    rc_s = [C.sb("rc_s%d" % i, [128, 1], F32) for i in range(2)]
    ss = C.sb("ss", [128, 1], F32)
    rstd = C.sb("rstd", [128, 1], F32)
    h2n = C.sb("h2n", [128, D], BF16)
    h2T = C.sb("h2T", [128, 8, 512], BF16)
    rl = [C.sb("rl%d" % i, [128, 512], F32) for i in range(2)]
    actT = C.sb("actT", [128, 32, 512], BF16)

    pa = [C.ps("pa%d" % i, [128, 512], F32) for i in range(2)]
    pcv = [C.ps("pcv%d" % i, [128, 512], F32) for i in range(2)]
    pu = [C.ps("pu%d" % i, [128, 512], F32) for i in range(2)]
    pT = C.ps("pT", [128, 8, 128], BF16)

    P.dma('sync', gout_s[:], gout)
    P.dma('sync', gffn_s[:], gffn)
    P.dma('sync', identb[:], identb_d)
    n = 0
    for kc in range(8):
        for hf in range(2):
            st = wst[n % 2]
            n += 1
            P.dma('sync' if hf == 0 else 'scalar', st[:, 0, :], w_o[kc * 128:(kc + 1) * 128, hf * 512:(hf + 1) * 512], writes=[st])
            P.ts('vector' if hf == 0 else 'gpsimd', Wo[:, kc, hf * 512:(hf + 1) * 512], st[:, 0, :], gout_s[:, kc:kc + 1], ALU.mult, reads=[st, gout_s])
    for f4 in range(8):
        st = wst[n % 2]
        n += 1
        stv = AP_(st, 0, [[4096, 128], [1024, 4], [1, 1024]])
        P.dma('sync' if f4 % 2 == 0 else 'scalar', stv, w_dn[f4 * 512:(f4 + 1) * 512, :].rearrange("(a p) n -> p a n", p=128), writes=[st])
        P.copy('vector' if f4 % 2 == 0 else 'gpsimd', Wdn[:, f4 * 4:(f4 + 1) * 4, :], stv, reads=[st])

    up_n = [n]

    for bt in range(B_NM // 4):
        for j in range(4):
            m = bt * 4 + j
            xb = xt[m % 2]
            ab, cb, rab, rcb = at_s[m % 2], cv_s[m % 2], ra_s[m % 2], rc_s[m % 2]
            P.dma('sync', xb[:], x_d[m])
            P.dma('scalar', ab[:], attnT_d[m])
            P.dma('scalar', cb[:], convT_d[m])
            P.dma('sync', rab[:], rattn_d[m])
            P.dma('sync', rcb[:], rconv_d[m])
            for nh in range(2):
                for kc in range(4):
                    P.mm(pa[nh][:], ab[:, kc, :], Wo[:, kc, nh * 512:(nh + 1) * 512], start=(kc == 0), stop=(kc == 3))
                for kc in range(4):
                    P.mm(pcv[nh][:], cb[:, kc, :], Wo[:, 4 + kc, nh * 512:(nh + 1) * 512], start=(kc == 0), stop=(kc == 3))
            x1b = x1[j]
            for nh in range(2):
                sl = slice(nh * 512, (nh + 1) * 512)
                P.stt(x1b[:, sl], pa[nh][:], rab[:, 0:1], xb[:, sl], ALU.mult, ALU.add)
                P.stt(x1b[:, sl], pcv[nh][:], rcb[:, 0:1], x1b[:, sl], ALU.mult, ALU.add)
            P.act(h2n[:], x1b[:], AF.Square, accum_out=ss[:])
            P.act(rstd[:], ss[:], AF.Sqrt, bias=EPS, scale=1.0 / D)
            P.recip(rstd[:], rstd[:])
            P.stt(h2n[:], x1b[:], rstd[:, 0:1], gffn_s[:], ALU.mult, ALU.mult)
            for kc in range(8):
                P.tr(pT[:, kc, :], h2n[:, kc * 128:(kc + 1) * 128], identb[:])
            P.copy('scalar', h2T[:, :, j * 128:(j + 1) * 128], pT[:])
        for u in range(8):
            st = wst[up_n[0] % 2]
            wb = Wup[up_n[0] % 2]
            up_n[0] += 1
            P.dma('sync' if u % 2 == 0 else 'scalar', st[:], w_up[:, u * 512:(u + 1) * 512].rearrange("(kc p) n -> p kc n", p=128))
            P.copy('gpsimd' if u % 2 == 0 else 'vector', wb[:], st[:])
            for fl in range(4):
                f = u * 4 + fl
                pp = pu[f % 2]
                for kc in range(8):
                    P.mm(pp[:], wb[:, kc, fl * 128:(fl + 1) * 128], h2T[:, kc, :], start=(kc == 0), stop=(kc == 7))
                rb = rl[f % 2]
                P.act(rb[:], pp[:], AF.Relu)
                P.tt('gpsimd' if f % 2 == 0 else 'vector', actT[:, f, :], rb[:], rb[:], ALU.mult)
        for j in range(4):
            m = bt * 4 + j
            ob = x1[j]
            for nh in range(2):
                pp = pa[nh] if j % 2 == 0 else pcv[nh]
                for f in range(32):
                    P.mm(pp[:], actT[:, f, j * 128:(j + 1) * 128], Wdn[:, f, nh * 512:(nh + 1) * 512], start=(f == 0), stop=(f == 31))
                P.tt('vector', ob[:, nh * 512:(nh + 1) * 512], pp[:], x1[j][:, nh * 512:(nh + 1) * 512], ALU.add)
            P.dma('sync', xo_d[m], ob[:])
    return C.finish()


def prep_C(l, xfull, aout, bout, p):
    maps = []
    common = {
        'w_o': np.ascontiguousarray(p['w_o'][l]),
        'gout': np.ascontiguousarray(p['g_out'][l].reshape(8, 128).T),
        'gffn': np.ascontiguousarray(np.tile(p['g_ffn_norm'][l][None, :], (128, 1))),
        'w_up': np.ascontiguousarray(p['w_up'][l]),
        'w_dn': np.ascontiguousarray(p['w_down'][l]),
        'identb': IDENTB,
    }
    for b in range(NB):
        convT_f = np.concatenate([np.asarray(aout[4 * b + s]['convT']) for s in range(4)], axis=1)
        rconv_f = np.concatenate([np.asarray(aout[4 * b + s]['rconv']) for s in range(4)], axis=0)
        cv = convT_f.reshape(4, 128, T // 128, 128)
        for s in range(4):
            cs = np.arange(B_NM) * 4 + s
            mp = dict(common)
            mp['x'] = np.ascontiguousarray(xfull[b].reshape(T // 128, 128, D)[cs])
            mp['convT'] = np.ascontiguousarray(cv[:, :, cs, :].transpose(2, 1, 0, 3))
            mp['rconv'] = np.ascontiguousarray(rconv_f.reshape(T // 128, 128, 1)[cs])
            mp['attnT'] = np.asarray(bout[4 * b + s]['attnT'])
            mp['rattn'] = np.asarray(bout[4 * b + s]['rattn'])
            maps.append(mp)
    return maps


_PROGS = {}


def _prog(name):
    if name not in _PROGS:
        _PROGS[name] = {'A': build_A, 'B': build_B, 'C': build_C}[name]()
    return _PROGS[name]


def _run(name, maps):
    res = run_bass_kernel_spmd(_prog(name), maps, core_ids=list(range(NCORES)))
    return res.results


def run_layers(inputs, nlayers=DEPTH, debug=None):
    p = {k: np.asarray(v) for k, v in inputs.items()}
    x = np.ascontiguousarray(p['x'], dtype=np.float32)
    for l in range(nlayers):
        aout = _run('A', prep_A(l, x, p))
        bout = _run('B', prep_B(aout))
        cout = _run('C', prep_C(l, x, aout, bout, p))
        xn = np.empty_like(x)
        for b in range(NB):
            xv = xn[b].reshape(T // 128, 128, D)
            for s in range(4):
                xv[np.arange(B_NM) * 4 + s] = np.asarray(cout[4 * b + s]['xo'])
        x = xn
        if debug is not None:
            debug.append((aout, bout, x))
    return x


def kernel(**inputs):
    return run_layers(inputs, DEPTH)
```

```python
import contextlib
import numpy as np
import ml_dtypes
import concourse.bass as bass
import concourse.mybir as mybir
from concourse.bass_utils import run_bass_kernel_spmd

F32 = mybir.dt.float32
BF16 = mybir.dt.bfloat16
AF = mybir.ActivationFunctionType
ALU = mybir.AluOpType
AX = mybir.AxisListType
NPBF = ml_dtypes.bfloat16

ENGS = ['tensor', 'vector', 'scalar', 'gpsimd', 'sync']
NCORES = 8
D = 1024
T = 16384
NB = 2
DEPTH = 4
INW = 2840
EPS = 1e-6
NEGM = -30000.0


def key_of(ap):
    t = getattr(ap, 'tensor', ap)
    return t.name.split('@')[0]


class Prog:
    def __init__(self, nc):
        self.nc = nc
        self.q = {e: [] for e in ENGS}
        self.semcount = {}
        self.res = {}
        self.seen = {e: {} for e in ENGS}

    grp = None

    @contextlib.contextmanager
    def group(self, sem):
        s = 'd_' + sem
        self.grp = {'sem': sem, 's': s, 'start': self.semcount.get(s, 0), 'keys': set(), 'pre': {}}
        try:
            yield
        finally:
            g, self.grp = self.grp, None
            v = self.semcount.get(s, 0)
            for k in g['keys']:
                self.res[k]['w'] = (s, v)

    def _need(self, eng, reads, writes):
        need = {}

        def add(tok):
            if tok is None:
                return
            s, v = tok
            if eng == 'tensor' and s == 'e_tensor':
                return
            if self.grp is not None and s == self.grp['s'] and v > self.grp['start']:
                return
            if need.get(s, 0) < v:
                need[s] = v
        for k in reads:
            st = self.res.get(k)
            if st:
                add(st['w'])
        for k in writes:
            sts = [self.res.get(k)]
            if self.grp is not None:
                if k not in self.grp['pre']:
                    st0 = self.res.get(k)
                    self.grp['pre'][k] = {'w': st0['w'], 'r': dict(st0['r'])} if st0 else None
                sts.append(self.grp['pre'][k])
            for st in sts:
                if st:
                    add(st['w'])
                    for s, v in st['r'].items():
                        add((s, v))
        for s, v in need.items():
            if self.seen[eng].get(s, 0) < v:
                self.q[eng].append(('wait', s, v))
                self.seen[eng][s] = v

    def _update(self, reads, writes, tok):
        s, v = tok
        for k in reads:
            st = self.res.setdefault(k, {'w': None, 'r': {}})
            if st['r'].get(s, 0) < v:
                st['r'][s] = v
        for k in writes:
            self.res[k] = {'w': tok, 'r': {}}

    def op(self, eng, fn, reads=(), writes=()):
        reads = [r if isinstance(r, str) else key_of(r) for r in reads]
        writes = [r if isinstance(r, str) else key_of(r) for r in writes]
        self._need(eng, reads, writes)
        s = 'e_' + eng
        v = self.semcount.get(s, 0) + 1
        self.semcount[s] = v
        self.q[eng].append(('op', fn, s, 1))
        self._update(reads, writes, (s, v))

    def dma(self, eng, out, in_, reads=None, writes=None, sem=None, **kw):
        if reads is None:
            reads = [] if in_.tensor.name in self.dram_names else [in_]
        if writes is None:
            writes = [] if out.tensor.name in self.dram_names else [out]
        assert reads or writes or sem
        reads = [r if isinstance(r, str) else key_of(r) for r in reads]
        writes = [r if isinstance(r, str) else key_of(r) for r in writes]
        if self.grp is not None:
            sem = self.grp['sem']
            self.grp['keys'].update(writes)
        if sem is None:
            sem = writes[0] if writes else 'st_' + reads[0]
        self._need(eng, reads, writes)
        s = 'd_' + sem
        v = self.semcount.get(s, 0) + 16
        self.semcount[s] = v
        self.q[eng].append(('op', lambda e: e.dma_start(out=out, in_=in_, **kw), s, 16))
        self._update(reads, writes, (s, v))

    dram_names = set()

    def mm(self, out, lhsT, rhs, start=True, stop=True, reads=None, writes=None):
        self.op('tensor', lambda e: e.matmul(out, lhsT=lhsT, rhs=rhs, start=start, stop=stop),
                reads=reads if reads is not None else [lhsT, rhs],
                writes=writes if writes is not None else [out])

    def tr(self, out, in_, ident, reads=None, writes=None):
        self.op('tensor', lambda e: e.transpose(out, in_, ident),
                reads=reads if reads is not None else [in_, ident],
                writes=writes if writes is not None else [out])

    def act(self, out, in_, func, bias=None, scale=None, accum_out=None, reads=None, writes=None, eng='scalar'):
        kw = {}
        rd = [in_]
        if bias is not None:
            kw['bias'] = bias
            if not isinstance(bias, (int, float)):
                rd.append(bias)
        if scale is not None:
            kw['scale'] = scale
            if not isinstance(scale, (int, float)):
                rd.append(scale)
        wr = [out]
        if accum_out is not None:
            kw['accum_out'] = accum_out
            wr.append(accum_out)
        self.op('scalar', lambda e: e.activation(out=out, in_=in_, func=func, **kw),
                reads=reads if reads is not None else rd,
                writes=writes if writes is not None else wr)

    def tt(self, eng, out, in0, in1, op, reads=None, writes=None):
        self.op(eng, lambda e: e.tensor_tensor(out=out, in0=in0, in1=in1, op=op),
                reads=reads if reads is not None else [in0, in1],
                writes=writes if writes is not None else [out])

    def ts(self, eng, out, in0, s1, op0, s2=None, op1=None, reads=None, writes=None):
        rd = [in0]
        if not isinstance(s1, (int, float)):
            rd.append(s1)
        if s2 is not None and not isinstance(s2, (int, float)):
            rd.append(s2)
        if op1 is None:
            fn = lambda e: e.tensor_scalar(out=out, in0=in0, scalar1=s1, scalar2=None, op0=op0)
        else:
            fn = lambda e: e.tensor_scalar(out=out, in0=in0, scalar1=s1, scalar2=s2, op0=op0, op1=op1)
        self.op(eng, fn, reads=reads if reads is not None else rd,
                writes=writes if writes is not None else [out])

    def stt(self, out, in0, scalar, in1, op0, op1, reads=None, writes=None):
        rd = [in0, in1]
        if not isinstance(scalar, (int, float)):
            rd.append(scalar)
        self.op('vector', lambda e: e.scalar_tensor_tensor(out=out, in0=in0, scalar=scalar, in1=in1, op0=op0, op1=op1),
                reads=reads if reads is not None else rd,
                writes=writes if writes is not None else [out])

    def copy(self, eng, out, in_, reads=None, writes=None):
        if eng == 'scalar':
            fn = lambda e: e.activation(out=out, in_=in_, func=AF.Copy)
        else:
            fn = lambda e: e.tensor_copy(out=out, in_=in_)
        self.op(eng, fn, reads=reads if reads is not None else [in_],
                writes=writes if writes is not None else [out])

    def recip(self, out, in_):
        self.op('vector', lambda e: e.reciprocal(out=out, in_=in_), reads=[in_], writes=[out])

    def reduce(self, out, in_, op=ALU.add, axis=AX.X):
        self.op('vector', lambda e: e.tensor_reduce(out=out, in_=in_, axis=axis, op=op), reads=[in_], writes=[out])

    def memset(self, eng, ap, val):
        self.op(eng, lambda e: e.memset(ap, val), writes=[ap])

    def barrier(self):
        for e in ENGS:
            for s, v in self.semcount.items():
                if self.seen[e].get(s, 0) < v:
                    self.q[e].append(('wait', s, v))
                    self.seen[e][s] = v
        self.res = {}

    def collectives(self, fns):
        self.barrier()
        s = 'c_cc'
        for fn in fns:
            v = self.semcount.get(s, 0) + 1
            self.semcount[s] = v
            self.q['gpsimd'].append(('op', fn, s, 1))
        self.barrier()

    def emit(self):
        nc = self.nc
        for s, v in self.semcount.items():
            if self.seen['sync'].get(s, 0) < v:
                self.q['sync'].append(('wait', s, v))
        with contextlib.ExitStack() as es:
            semh = {s: es.enter_context(nc.semaphore(s)) for s in self.semcount}
            block = es.enter_context(nc.Block())

            def mk(engname):
                def body(e):
                    for it in self.q[engname]:
                        if it[0] == 'wait':
                            e.wait_ge(semh[it[1]], it[2])
                        else:
                            ins = it[1](e)
                            ins.then_inc(semh[it[2]], it[3])
                return body
            for engname in ENGS:
                if self.q[engname]:
                    getattr(block, engname)(mk(engname))


def AP_(t, offset, dims):
    return bass.AP(t, offset, [list(d) for d in dims])


class Ctx:
    def __init__(self):
        self.nc = bass.Bass("TRN2", target_bir_lowering=False)
        self.es = contextlib.ExitStack()
        self.P = Prog(self.nc)
        self.P.dram_names = set()

    def din(self, name, shape, dt):
        self.P.dram_names.add(name)
        return self.nc.dram_tensor(name, list(shape), dt, kind="ExternalInput").ap()

    def dout(self, name, shape, dt):
        self.P.dram_names.add(name)
        return self.nc.dram_tensor(name, list(shape), dt, kind="ExternalOutput").ap()

    tag = None

    def dint(self, name, shape, dt):
        self.P.dram_names.add(name)
        return self.nc.dram_tensor(name, list(shape), dt)

    def sb(self, name, shape, dt):
        if self.tag:
            name = name + '@' + self.tag
        return self.es.enter_context(self.nc.sbuf_tensor(name, list(shape), dt))

    def ps(self, name, shape, dt):
        if self.tag:
            name = name + '@' + self.tag
        return self.es.enter_context(self.nc.psum_tensor(name, list(shape), dt))

    @contextlib.contextmanager
    def phase(self, tag):
        old, oldtag = self.es, self.tag
        self.es, self.tag = contextlib.ExitStack(), tag
        try:
            yield
        finally:
            self.P.barrier()
            self.es.close()
            self.es, self.tag = old, oldtag

    def finish(self):
        self.P.emit()
        self.es.close()
        return self.nc


B_NM = 32
DFF = 4096
PAYR = 6176
KS_R0, KW_R0, KC_R0, VC_R0, VS_R0, VW_R0 = 0, 1024, 2048, 3072, 4096, 5136
TP = T + 128


def emit_A(C, l, xsrc, W, G):
    P = C.P
    with C.phase("A%d" % l):
        Wbf = C.sb("Wbf", [128, 8, INW], BF16)
        wst = [C.sb("wst%d" % i, [128, INW // 2], F32) for i in range(2)]
        gmix_s = C.sb("gmix_s", [128, 8], F32)
        gq_s = C.sb("gq_s", [128, 512], F32)
        gk3_s = C.sb("gk3_s", [128, 3, 128], F32)
        cw_s = C.sb("cw_s", [128, 4, 3], F32)
        xt = [C.sb("xt%d" % i, [128, D], F32) for i in range(2)]
        junk = C.sb("junk", [128, D], BF16)
        ss = C.sb("ss", [128, 1], F32)
        rstd = C.sb("rstd", [128, 1], F32)
        hn = C.sb("hn", [128, D], BF16)
        hT = C.sb("hT", [128, 8, 128], BF16)
        sq = C.sb("sq", [128, 1280], F32)
        ssg = C.sb("ssg", [128, 20], F32)
        rg = C.sb("rg", [128, 20], F32)
        qtmp = C.sb("qtmp", [128, 512], F32)
        ktmp = C.sb("ktmp", [128, 256], F32)
        nrm = C.sb("nrm", [128, 1024], BF16)
        trs = C.sb("trs", [128, 8, 128], BF16)
        vsw = C.sb("vsw", [128, 2, 130], BF16)
        gts = C.sb("gts", [128, 24], F32)
        hcs = C.sb("hcs", [128, 4, 128], F32)
        ub = C.sb("ub", [128, 4, 130], F32)
        cv0 = C.sb("cv0", [128, 4, 128], F32)
        cv1 = C.sb("cv1", [128, 4, 128], F32)
        yv = C.sb("yv", [128, 4, 128], F32)
        ut = C.sb("ut", [128, 4, 2], F32)
        bg2 = C.sb("bg2", [128, 4, 2], F32)
        pT = C.ps("pT", [128, 8, 128], BF16)
        pz = [C.ps("pz%d" % i, [128, 512], F32) for i in range(3)]
        pc = [C.ps("pc%d" % i, [128, 4, 128], F32) for i in range(3)]
        identb = G['identb']

        with P.group('cst'):
            P.dma('sync', gmix_s[:], W['gmix'][l])
            P.dma('sync', gq_s[:], W['gq'][l])
            P.dma('sync', gk3_s[:], W['gk3'][l])
            P.dma('sync', cw_s[:], W['cw'][l])
        P.memset('gpsimd', vsw[:], 1.0)
        P.memset('gpsimd', ub[:], 0.0)
        for kc in range(8):
            for hf in range(2):
                st = wst[hf]
                P.dma('sync' if hf == 0 else 'scalar', st[:], W['w_in'][l, kc * 128:(kc + 1) * 128, hf * 1420:(hf + 1) * 1420])
                P.ts('vector' if hf == 0 else 'gpsimd', Wbf[:, kc, hf * 1420:(hf + 1) * 1420], st[:], gmix_s[:, kc:kc + 1], ALU.mult)

        pay, payu, qsc, gsc, ysc, bg2sc = G['pay'], G['payu'], G['qsc'], G['gsc'], G['ysc'], G['bg2sc']
        for m in range(B_NM):
            xb = xt[m % 2]
            P.dma('sync', xb[:], xsrc[m])
            P.act(junk[:], xb[:], AF.Square, accum_out=ss[:])
            P.act(rstd[:], ss[:], AF.Sqrt, bias=EPS, scale=1.0 / D)
            P.recip(rstd[:], rstd[:])
            P.act(hn[:], xb[:], AF.Copy, scale=rstd[:, 0:1])
            for kc in range(8):
                P.tr(pT[:, kc, :], hn[:, kc * 128:(kc + 1) * 128], identb[:])
            P.copy('vector', hT[:], pT[:])
            for bi, (c0, c1) in enumerate([(0, 512), (512, 1024), (1024, 1304)]):
                for kc in range(8):
                    P.mm(pz[bi][:, 0:c1 - c0], hT[:, kc, :], Wbf[:, kc, c0:c1], start=(kc == 0), stop=(kc == 7))
            for ch in range(12):
                for kc in range(8):
                    P.mm(pc[ch // 4][:, ch % 4, :], Wbf[:, kc, 1304 + ch * 128:1304 + (ch + 1) * 128], hT[:, kc, :],
                         start=(kc == 0), stop=(kc == 7))
            P.act(sq[:, 0:512], pz[0][:], AF.Square)
            P.act(sq[:, 512:1024], pz[1][:], AF.Square)
            P.act(sq[:, 1024:1280], pz[2][:, 0:256], AF.Square)
            P.reduce(ssg[:], sq[:].rearrange("p (g d) -> p g d", d=64))
            P.act(rg[:, 0:8], ssg[:, 0:8], AF.Sqrt, bias=64 * EPS, scale=1.0)
            P.act(rg[:, 8:20], ssg[:, 8:20], AF.Sqrt, bias=EPS, scale=1.0 / 64)
            P.recip(rg[:], rg[:])
            P.tt('vector', qtmp[:].rearrange("p (g d) -> p g d", d=64), pz[0][:].rearrange("p (g d) -> p g d", d=64),
                 AP_(rg, 0, [[20, 128], [1, 8], [0, 64]]), ALU.mult)
            P.tt('gpsimd', nrm[:, 0:512], qtmp[:], gq_s[:], ALU.mult)
            P.tt('vector', ktmp[:, 0:128].rearrange("p (g d) -> p g d", d=64), pz[1][:, 256:384].rearrange("p (g d) -> p g d", d=64),
                 AP_(rg, 12, [[20, 128], [1, 2], [0, 64]]), ALU.mult)
            P.tt('vector', ktmp[:, 128:256].rearrange("p (g d) -> p g d", d=64), pz[2][:, 0:128].rearrange("p (g d) -> p g d", d=64),
                 AP_(rg, 16, [[20, 128], [1, 2], [0, 64]]), ALU.mult)
            P.tt('gpsimd', nrm[:, 512:768], ktmp[:], gk3_s[:, 1:3, :].rearrange("p a d -> p (a d)"), ALU.mult)
            P.copy('vector', nrm[:, 768:1024], pz[1][:, 0:256])
            P.copy('scalar', AP_(vsw, 1, [[260, 128], [65, 2], [1, 64]]), pz[1][:, 384:512].rearrange("p (g d) -> p g d", d=64))
            P.copy('scalar', AP_(vsw, 131, [[260, 128], [65, 2], [1, 64]]), pz[2][:, 128:256].rearrange("p (g d) -> p g d", d=64))
            P.dma('scalar', AP_(pay['vs%d' % (m // 16)], (m % 16) * 128 * 130, [[130, 128], [1, 130]]), vsw[:, 0, :])
            P.dma('scalar', AP_(pay['vw%d' % (m // 16)], (m % 16) * 128 * 130, [[130, 128], [1, 130]]), vsw[:, 1, :])
            P.act(gts[:], pz[2][:, 256:280], AF.Sigmoid)
            P.dma('scalar', gsc[m], gts[:])
            for bk in range(8):
                P.tr(pT[:, bk, :], nrm[:, bk * 128:(bk + 1) * 128], identb[:])
            P.copy('vector', trs[:], pT[:])
            for e in range(2):
                for g in range(2):
                    P.dma('sync' if g == 0 else 'scalar', AP_(qsc, m * 65536 + g * 32768 + e * 128, [[512, 64], [256, 2], [1, 128]]),
                          trs[e * 64:(e + 1) * 64, 2 * g:2 * g + 2, :])
            for ci, cn in enumerate(('ks', 'kw', 'kc', 'vc')):
                P.dma('sync' if ci % 2 == 0 else 'scalar', AP_(pay[cn], m * 128, [[4096, 128], [1, 128]]), trs[:, 4 + ci, :])
            P.copy('scalar', hcs[:], pc[0][:])
            P.tt('vector', ub[:, :, 2:130], pc[1][:], hcs[:], ALU.mult)
            P.tt('gpsimd', cv0[:], ub[:, :, 2:130], AP_(cw_s, 2, [[12, 128], [3, 4], [0, 128]]), ALU.mult)
            P.tt('gpsimd', cv1[:], ub[:, :, 1:129], AP_(cw_s, 1, [[12, 128], [3, 4], [0, 128]]), ALU.mult)
            P.tt('gpsimd', cv0[:], cv0[:], cv1[:], ALU.add)
            P.tt('gpsimd', cv1[:], ub[:, :, 0:128], AP_(cw_s, 0, [[12, 128], [3, 4], [0, 128]]), ALU.mult)
            P.tt('gpsimd', cv0[:], cv0[:], cv1[:], ALU.add)
            P.tt('vector', yv[:], pc[2][:], cv0[:], ALU.mult)
            P.dma('scalar', ysc[m].rearrange("p (c t) -> p c t", c=4), yv[:])
            P.copy('gpsimd', ut[:], ub[:, :, 128:130])
            P.dma('sync', payu[m].rearrange("(p e) -> p e", e=8), ut[:].rearrange("p c e -> p (c e)"))
            P.copy('vector', bg2[:], pc[2][:, :, 0:2])
            P.dma('sync', bg2sc[m], bg2[:].rearrange("p c e -> p (c e)"))


def emit_Z(C, l, W, G):
    P = C.P
    with C.phase("Z%d" % l):
        kcT_all = C.sb("kcT_all", [128, TP], BF16)
        vcT_all = C.sb("vcT_all", [128, TP], BF16)
        W1bf = C.sb("W1bf", [128, 2, 32, 256], BF16)
        w1st = [C.sb("w1st%d" % i, [128, 8, 256], F32) for i in range(2)]
        W2bf = C.sb("W2bf", [128, 2, 2, 64], BF16)
        w2st = C.sb("w2st", [128, 2, 2, 64], F32)
        peT_s = C.sb("peT_s", [64, 2, 32], F32)
        pebf = C.sb("pebf", [64, 2, 32, 128], BF16)
        b1b_s = C.sb("b1b_s", [128, 2, 256], F32)
        b2b_s = C.sb("b2b_s", [128, 2, 64], F32)
        gkc = C.sb("gkc", [128, 128], F32)
        c1b = C.sb("c1b", [128, 2, 256], F32)
        hid = C.sb("hid", [128, 256], F32)
        g_x2 = C.sb("g_x2", [128, 256], F32)
        g_in = C.sb("g_in", [128, 256], F32)
        g_sg = C.sb("g_sg", [128, 256], F32)
        hbf = C.sb("hbf", [128, 256], BF16)
        hidT = C.sb("hidT", [128, 2, 128], BF16)
        co = C.sb("co", [128, 2, 2, 64], F32)
        cosq = C.sb("cosq", [128, 128], F32)
        css = C.sb("css", [128, 2], F32)
        crs = C.sb("crs", [128, 2], F32)
        kcn = C.sb("kcn", [128, 128], BF16)
        pT = C.ps("pT", [128, 8, 128], BF16)
        px = [C.ps("px%d" % i, [128, 512], F32) for i in range(2)]
        po = C.ps("po", [128, 512], F32)
        identb = G['identb']
        gath = G['gath']
        kcT_s, vcx_s = G['kcT_s'], G['vcx_s']

        with P.group('cst'):
            P.dma('sync', b1b_s[:], W['b1b'][l].rearrange("k p n -> p k n"))
            P.dma('sync', b2b_s[:], W['b2b'][l])
            P.dma('sync', peT_s[:], W['peT'][l].rearrange("k d j -> d k j"))
            P.dma('sync', w2st[:], W['w2'][l].rearrange("k (c p) d -> p k c d", p=128))
            P.dma('sync', gkc[:], W['gk3'][l, :, 0, :])
        P.memset('gpsimd', kcT_all[:, T:TP], 0.0)
        P.memset('gpsimd', vcT_all[:, T:TP], 0.0)
        with P.group('kvcl'):
            for r in range(4):
                for kv, dst in enumerate((kcT_all, vcT_all)):
                    P.dma('sync' if kv == 0 else 'scalar', AP_(dst, r * 128, [[TP, 128], [512, 32], [1, 128]]),
                          AP_(gath['kc' if kv == 0 else 'vc'], r * 1024 * 512, [[4096, 128], [128, 32], [1, 128]]),
                          writes=[dst])
        P.copy('vector', W2bf[:], w2st[:])
        P.copy('vector', pebf[:], AP_(peT_s, 0, [[64, 64], [32, 2], [1, 32], [0, 128]]))
        n = 0
        for kv in range(2):
            for jq in range(4):
                st = w1st[n % 2]
                n += 1
                src = W['w1'][l, kv, jq * 512:(jq + 1) * 512, :].rearrange("(j d) n -> d j n", d=64)
                with P.group("w1st%d" % ((n - 1) % 2)):
                    P.dma('sync', st[0:64, :, :], src, writes=[st])
                    P.dma('scalar', st[64:128, :, :], src, writes=[st])
                P.copy('vector' if jq % 2 == 0 else 'gpsimd', W1bf[:, kv, jq * 8:(jq + 1) * 8, :], st[:])
        for kv in range(2):
            for j in range(32):
                P.mm(px[0][:, 0:256], pebf[:, kv, j, :], W1bf[0:64, kv, j, :], start=(j == 0), stop=(j == 31))
            P.tt('vector', c1b[:, kv, :], px[0][:, 0:256], b1b_s[:, kv, :], ALU.add)
        n = 0
        for ib in range(8):
            for kv in range(2):
                src_all = kcT_all if kv == 0 else vcT_all
                for g in range(2):
                    pp = px[n % 2]
                    n += 1
                    base = 16 * 128 * ib
                    for j in range(32):
                        lhs = AP_(src_all, 64 * g * TP + base + j, [[TP, 64], [16, 128]])
                        P.mm(pp[:, 0:256], lhs, W1bf[64 * g:64 * g + 64, kv, j, :], start=(j == 0), stop=(j == 31),
                             reads=[src_all, W1bf])
                    P.tt('vector', hid[:], pp[:, 0:256], c1b[:, kv, :], ALU.add)
                    P.act(g_x2[:], hid[:], AF.Square)
                    P.ts('gpsimd', g_x2[:], g_x2[:], 0.044715, ALU.mult, 1.0, ALU.add)
                    P.tt('gpsimd', g_in[:], g_x2[:], hid[:], ALU.mult)
                    P.act(g_sg[:], g_in[:], AF.Sigmoid, scale=1.5957691216057308)
                    P.tt('vector', hbf[:], hid[:], g_sg[:], ALU.mult)
                    for c in range(2):
                        P.tr(pT[:, c, :], hbf[:, c * 128:(c + 1) * 128], identb[:])
                    P.copy('vector', hidT[:], pT[:, 0:2, :])
                    for c in range(2):
                        P.mm(po[:, 0:64], hidT[:, c, :], W2bf[:, kv, c, :], start=(c == 0), stop=(c == 1))
                    P.tt('vector', co[:, kv, g, :], po[:, 0:64], b2b_s[:, kv, :], ALU.add)
            P.act(cosq[:], co[:, 0, :, :].rearrange("p g d -> p (g d)"), AF.Square)
            P.reduce(css[:], cosq[:].rearrange("p (g d) -> p g d", d=64))
            P.act(crs[:], css[:], AF.Sqrt, bias=EPS, scale=1.0 / 64)
            P.recip(crs[:], crs[:])
            P.tt('vector', cosq[:].rearrange("p (g d) -> p g d", d=64), co[:, 0, :, :], AP_(crs, 0, [[2, 128], [1, 2], [0, 64]]), ALU.mult)
            P.tt('vector', kcn[:], cosq[:], gkc[:], ALU.mult)
            for g in range(2):
                P.tr(pT[0:64, 4 + g, :], kcn[:, 64 * g:64 * g + 64], identb[:])
            P.copy('vector', kcT_s[0:64, :, ib * 128:(ib + 1) * 128], pT[0:64, 4:6, :])
            P.copy('gpsimd', vcx_s[:, ib, :, 0:64], co[:, 1, :, :])
        if G.get('dbg'):
            P.dma('sync', G['dbg']['kc'], kcT_s[:])
            P.dma('sync', G['dbg']['vc'], vcx_s[:])


def emit_B(C, l, W, G):
    P = C.P
    with C.phase("B%d" % l):
        ksT_s = [C.sb("ksT_s%d" % g, [68, T], BF16) for g in range(2)]
        vsx_s = C.sb("vsx_s", [128, 128, 130], BF16)
        smask_s = C.sb("smask_s", [128, 4, 512], BF16)
        cmask_s = C.sb("cmask_s", [128, 5, 512], BF16)
        wmask_s = C.sb("wmask_s", [128, 8, 512], BF16)
        Eall_s = C.sb("Eall_s", [64, 32, 128], BF16)
        Bsel_s = C.sb("Bsel_s", [128, 512], F32)
        qT_s = [C.sb("qT_s%d" % i, [68, 2, 512], BF16) for i in range(2)]
        kwT_s = [C.sb("kwT_s%d" % i, [68, 2, 1024], BF16) for i in range(2)]
        vwx_s = [C.sb("vwx_s%d" % i, [128, 8, 130], BF16) for i in range(2)]
        gates_s = [C.sb("gates_s%d" % i, [128, 24], F32) for i in range(2)]
        NPB = 4
        Pb = [C.sb("Pb%d" % i, [128, 512], BF16) for i in range(NPB)]
        Pc = [C.sb("Pc%d" % g, [128, 8, 512], BF16) for g in range(2)]
        oc = C.sb("oc", [128, 4, 321], F32)
        den4 = C.sb("den4", [128, 4], F32)
        rd4 = C.sb("rd4", [128, 4], F32)
        coef4 = C.sb("coef4", [128, 4], F32)
        imp = [C.sb("imp%d" % g, [128, 256], F32) for g in range(2)]
        tmpi = C.sb("tmpi", [128, 256], F32)
        m8 = C.sb("m8", [128, 16], F32)
        thr = C.sb("thr", [128, 1], F32)
        nm = C.sb("nm", [128, 256], BF16)
        nmT = [C.sb("nmT%d" % g, [64, 4, 4, 128], BF16) for g in range(2)]
        osT = C.sb("osT", [65, 512], F32)
        otmp = C.sb("otmp", [128, 4, 64], F32)
        acc = C.sb("acc", [128, 8, 64], F32)
        accb = C.sb("accb", [128, 512], BF16)
        ssq = C.sb("ssq", [128, 1], F32)
        rat = C.sb("rat", [128, 1], F32)
        rat2 = C.sb("rat2", [128, 1], F32)
        atT = C.sb("atT", [128, 4, 128], BF16)
        ps_s = [C.ps("ps_s%d" % i, [128, 512], F32) for i in range(3)]
        ps_o = [C.ps("ps_o%d" % i, [128, 512], F32) for i in range(2)]
        ps_acc = [C.ps("ps_acc%d" % i, [128, 512], F32) for i in range(2)]
        pm = C.ps("pm", [128, 512], F32)
        pm_b = pm[:].bitcast(BF16)
        identb, identf = G['identb'], G['identf']
        kcT_s, vcx_s = G['kcT_s'], G['vcx_s']
        gath, qsc, gsc, atsc, rasc = G['gath'], G['qsc'], G['gsc'], G['atsc'], G['rasc']

        with P.group('cst'):
            P.dma('sync', cmask_s[:], W['cmask'])
            P.dma('sync', Bsel_s[:], W['Bsel'])
            P.dma('scalar', wmask_s[:], W['wmask'])
            P.dma('scalar', smask_s[:], W['smask'])
            P.dma('scalar', Eall_s[:], W['Eall'])
        with P.group('ksT'):
            for g in range(2):
                P.dma('gpsimd', ksT_s[g][64:68, :], W['kaug'], writes=[ksT_s[g]])
                for r in range(4):
                    P.dma('sync' if r % 2 == 0 else 'scalar', ksT_s[g][0:64, r * 4096:(r + 1) * 4096],
                          AP_(gath['ks'], r * 1024 * 512 + 64 * g * 4096, [[4096, 64], [1, 4096]]),
                          writes=[ksT_s[g]])
        with P.group('vsx'):
            for r in range(4):
                for hv in range(2):
                    P.dma('gpsimd', vsx_s[:, r * 32 + hv * 16:r * 32 + hv * 16 + 16, :],
                          AP_(gath['vs%d' % hv], r * 520 * 512, [[130, 128], [128 * 130, 16], [1, 130]]), writes=[vsx_s])

        if G.get('dbg'):
            P.dma('sync', G['dbg']['ks'], ksT_s[1][:])
            P.dma('sync', G['dbg']['vs'], vsx_s[:])
        state = {'s': 0, 'p': 0}

        def run_stream(tiles):
            LA = 2
            n = len(tiles)
            bufs = []
            for i in range(n + LA):
                if i < n:
                    ps = ps_s[state['s'] % 3]
                    pb = Pb[state['p'] % NPB]
                    state['s'] += 1
                    state['p'] += 1
                    tiles[i][0](ps)
                    P.act(pb[:], ps[:], AF.Exp)
                    bufs.append(pb)
                if i - LA >= 0:
                    tiles[i - LA][1](bufs[i - LA])

        def o_epilogue(psacc, g, br, gs):
            P.copy('vector', osT[:], psacc[0:65, :])
            pmv = pm[:, 0:260].rearrange("p (r d) -> p r d", d=65)
            for r in range(4):
                P.tr(pmv[:, r, :], osT[0:65, r * 128:(r + 1) * 128], identf[0:65, 0:65])
            P.recip(rd4[:], pmv[:, :, 0])
            P.tt('vector', coef4[:], rd4[:], AP_(gs, 12 * g + br, [[24, 128], [3, 4]]), ALU.mult)
            dst = acc[:, 4 * g:4 * g + 4, :]
            P.tt('vector', otmp[:], pmv[:, :, 1:65], AP_(coef4, 0, [[4, 128], [1, 4], [0, 64]]), ALU.mult)
            P.tt('gpsimd', dst, dst, otmp[:], ALU.add)

        for m in range(B_NM):
            qs = qT_s[m % 2]
            kws = kwT_s[m % 2]
            vws = vwx_s[m % 2]
            gs = gates_s[m % 2]
            h0 = 0 if m > 0 else 1
            with P.group("qT_s%d" % (m % 2)):
                P.dma('sync', qs[0:64, :, :], qsc[m].rearrange("g d n -> d g n"), writes=[qs])
                P.dma('scalar', qs[64:68, :, :], W['qaug'][m], writes=[qs])
            P.dma('scalar', gs[:], gsc[m])
            nh = 2 - h0
            c0 = 128 * (m - 1 + h0)
            with P.group("kws%d" % (m % 2)):
                for r in range(4):
                    P.dma('sync' if r % 2 == 0 else 'scalar',
                          AP_(kws, (r * 2 + h0) * 128, [[2048, 64], [1024, 2], [1, 128 * nh]]),
                          AP_(gath['kw'], r * 1024 * 512 + c0, [[4096, 64], [64 * 4096, 2], [1, 128 * nh]]),
                          writes=[kws])
            with P.group("vws%d" % (m % 2)):
                for r in range(4):
                    for hf in range(h0, 2):
                        mp = m - 1 + hf
                        P.dma('gpsimd', vws[:, r * 2 + hf, :],
                              AP_(gath['vw%d' % (mp // 16)], r * 520 * 512 + (mp % 16) * 128 * 130, [[130, 128], [1, 130]]),
                              writes=[vws])
            for g in range(2):
                P.copy('gpsimd', AP_(kws, 64 * 2048 + g * 1024 + h0 * 128, [[2048, 4], [256, 4], [1, 128 * (2 - h0)]]),
                       AP_(ksT_s[0], 64 * T + 128 * (m - 1 + h0), [[T, 4], [4096, 4], [1, 128 * (2 - h0)]]),
                       reads=[ksT_s[0]], writes=[kws])
            n_it = (32 * m + 30) // 128 + 1
            for g in range(2):
                for it in range(n_it):
                    ps = ps_s[state['s'] % 3]
                    state['s'] += 1
                    delta = 128 * it - 32 * m
                    masked = delta >= -128
                    P.mm(ps[:], kcT_s[:, g, it * 128:(it + 1) * 128], qs[:, g, :], start=True, stop=not masked)
                    if masked:
                        P.mm(ps[:], identb[:], cmask_s[:, (-delta) // 32, :], start=False, stop=True)
                    P.act(Pc[g][:, it, :], ps[:], AF.Exp)
                for r in range(4):
                    po = ps_o[r % 2]
                    for it in range(n_it):
                        P.mm(po[:, 0:321], Pc[g][:, it, r * 128:(r + 1) * 128], vcx_s[:, it, g, :], start=(it == 0), stop=(it == n_it - 1))
                    P.copy('scalar', oc[:, r, :], po[:, 0:321])
                P.ts('vector', den4[:], oc[:, :, 64], 1e-30, ALU.max)
                P.recip(rd4[:], den4[:])
                P.ts('vector', imp[g][:], oc[:, 0, 65:321], rd4[:, 0:1], ALU.mult)
                for r in range(1, 4):
                    P.stt(imp[g][:], oc[:, r, 65:321], rd4[:, r:r + 1], imp[g][:], ALU.mult, ALU.add)
                P.tt('vector', coef4[:], rd4[:], AP_(gs, 12 * g + 0, [[24, 128], [3, 4]]), ALU.mult)
                P.tt('vector', acc[:, 4 * g:4 * g + 4, :], oc[:, :, 0:64], AP_(coef4, 0, [[4, 128], [1, 4], [0, 64]]), ALU.mult)
                P.tt('vector', imp[g][:], imp[g][:], Bsel_s[:, 256 - 8 * m:512 - 8 * m], ALU.add)
                P.ts('vector', imp[g][:, 0:1], imp[g][:, 0:1], 1e4, ALU.add)
                P.op('vector', lambda e, g=g: e.max(out=m8[:, 0:8], in_=imp[g][:]), reads=[imp[g]], writes=[m8])
                P.op('vector', lambda e, g=g: e.match_replace(out=tmpi[:], in_to_replace=m8[:, 0:8], in_values=imp[g][:], imm_value=-3.0e38),
                     reads=[imp[g], m8], writes=[tmpi])
                P.op('vector', lambda e: e.max(out=m8[:, 8:16], in_=tmpi[:]), reads=[tmpi], writes=[m8])
                P.ts('vector', thr[:], m8[:, 15:16], -1e29, ALU.max)
                P.ts('vector', nm[:], imp[g][:], thr[:, 0:1], ALU.is_lt, NEGM, ALU.mult)
                for qt in range(4):
                    P.tr(pm_b[0:64, qt * 128:(qt + 1) * 128], nm[:, qt * 64:(qt + 1) * 64], identb[:])
                P.copy('vector', nmT[g][:], AP_(pm_b.tensor, pm_b.offset, [[pm_b.ap[0][0], 64], [128, 4], [0, 4], [1, 128]]), reads=[pm])
            tiles = []
            wt = [(r, hf) for hf in range(h0, 2) for r in range(4)]
            for g in range(2):
                for idx, (r, hf) in enumerate(wt):
                    def eS(ps, g=g, r=r, hf=hf):
                        P.mm(ps[:], kws[:, g, (r * 2 + hf) * 128:(r * 2 + hf + 1) * 128], qs[:, g, :], start=True, stop=False)
                        P.mm(ps[:], identb[:], wmask_s[:, hf * 4 + r, :], start=False, stop=True)

                    def ePV(pb, g=g, r=r, hf=hf, idx=idx):
                        P.mm(ps_acc[g][0:65, :], vws[:, r * 2 + hf, 65 * g:65 * g + 65], pb[:], start=(idx == 0), stop=(idx == len(wt) - 1))
                    tiles.append((eS, ePV))
            run_stream(tiles)
            for g in range(2):
                o_epilogue(ps_acc[g], g, 2, gs)
            nkt = 4 * m + 4
            for g in range(2):
                tiles = []
                for kt in range(nkt):
                    def eS(ps, g=g, kt=kt):
                        diag = kt >= 4 * m
                        col = ((kt % 4) * 32 + kt // 4) * 128
                        P.mm(ps[:], ksT_s[g][:, col:col + 128], qs[:, g, :], start=True, stop=False)
                        P.mm(ps[:], Eall_s[:, kt % 32, :], nmT[g][:, kt // 32, :, :].rearrange("p a q -> p (a q)"), start=False, stop=not diag)
                        if diag:
                            P.mm(ps[:], identb[:], smask_s[:, kt - 4 * m, :], start=False, stop=True)

                    def ePV(pb, g=g, kt=kt):
                        P.mm(ps_acc[g][0:65, :], vsx_s[:, (kt % 4) * 32 + kt // 4, 65 * g:65 * g + 65], pb[:], start=(kt == 0), stop=(kt == nkt - 1))
                    tiles.append((eS, ePV))
                run_stream(tiles)
                o_epilogue(ps_acc[g], g, 1, gs)
            accf = acc[:].rearrange("p h d -> p (h d)")
            P.act(accb[:], accf, AF.Square, accum_out=ssq[:])
            P.act(rat[:], ssq[:], AF.Sqrt, bias=EPS, scale=1.0 / 512)
            P.recip(rat2[:], rat[:])
            P.dma('gpsimd', rasc[m], rat2[:])
            P.copy('gpsimd', accb[:], accf)
            for bk in range(4):
                P.tr(pm_b[:, bk * 128:(bk + 1) * 128], accb[:, bk * 128:(bk + 1) * 128], identb[:])
            P.copy('vector', atT[:].rearrange("p b q -> p (b q)"), pm_b[:, 0:512], reads=[pm])
            P.dma('gpsimd', atsc[m], atT[:].rearrange("p b q -> p (b q)"))
            if G.get('dbg'):
                P.dma('gpsimd', G['dbg']['at'][m], atT[:].rearrange("p b q -> p (b q)"))
                P.dma('gpsimd', G['dbg']['ra'][m], rat2[:])


def emit_C(C, l, xsrc, xdst, W, G):
    P = C.P
    with C.phase("C%d" % l):
        Wo = C.sb("Wo", [128, 8, D], BF16)
        Wdn = C.sb("Wdn", [128, 32, D], BF16)
        wst = [C.sb("wst%d" % i, [128, 8, 512], F32) for i in range(2)]
        Wup = [C.sb("Wup%d" % i, [128, 8, 512], BF16) for i in range(2)]
        gout_s = C.sb("gout_s", [128, 8], F32)
        gffn_s = C.sb("gffn_s", [128, D], F32)
        cw_s = C.sb("cw_s", [128, 4, 3], F32)
        selw_s = C.sb("selw_s", [128, 4], F32)
        xt = [C.sb("xt%d" % i, [128, D], F32) for i in range(1)]
        x1 = [C.sb("x1_%d" % i, [128, D], F32) for i in range(4)]
        at_s = [C.sb("at_s%d" % i, [128, 4, 128], BF16) for i in range(2)]
        yv = [C.sb("yv%d" % i, [128, 4, 128], F32) for i in range(1)]
        bg2 = [C.sb("bg2_%d" % i, [128, 4, 2], F32) for i in range(2)]
        tl = [C.sb("tl%d" % i, [128, 4, 8], F32) for i in range(2)]
        halo = C.sb("halo", [128, 4, 2], F32)
        pt0 = C.sb("pt0", [128, 4], F32)
        pt1 = C.sb("pt1", [128, 4], F32)
        ysq = C.sb("ysq", [128, 4, 128], F32)
        ybf = C.sb("ybf", [128, 4, 128], BF16)
        rc = C.sb("rc", [128, 1], F32)
        rc2 = C.sb("rc2", [128, 1], F32)
        ra_s = [C.sb("ra_s%d" % i, [128, 1], F32) for i in range(2)]
        ss = C.sb("ss", [128, 1], F32)
        rstd = C.sb("rstd", [128, 1], F32)
        h2n = C.sb("h2n", [128, D], BF16)
        h2T = C.sb("h2T", [128, 8, 512], BF16)
        rl = [C.sb("rl%d" % i, [128, 512], F32) for i in range(1)]
        actT = C.sb("actT", [128, 32, 512], BF16)
        pa = [C.ps("pa%d" % i, [128, 512], F32) for i in range(2)]
        pcv = [C.ps("pcv%d" % i, [128, 512], F32) for i in range(2)]
        pu = [C.ps("pu%d" % i, [128, 512], F32) for i in range(2)]
        pT = C.ps("pT", [128, 8, 128], BF16)
        px = C.ps("px", [128, 512], F32)
        identb, onesf = G['identb'], G['onesf']
        gathu, ysc, bg2sc, atsc, rasc = G['gathu'], G['ysc'], G['bg2sc'], G['atsc'], G['rasc']

        with P.group('cst'):
            P.dma('sync', gout_s[:], W['gout'][l])
            P.dma('sync', gffn_s[:], W['gffn'][l])
            P.dma('sync', cw_s[:], W['cw'][l])
            P.dma('sync', selw_s[:], W['selw'])
        n = 0
        for kc in range(8):
            for hf in range(2):
                st = wst[n % 2]
                n += 1
                P.dma('sync' if hf == 0 else 'scalar', st[:, 0, :], W['w_o'][l, kc * 128:(kc + 1) * 128, hf * 512:(hf + 1) * 512], writes=[st])
                P.ts('vector' if hf == 0 else 'gpsimd', Wo[:, kc, hf * 512:(hf + 1) * 512], st[:, 0, :], gout_s[:, kc:kc + 1], ALU.mult, reads=[st, gout_s])
        for f4 in range(8):
            st = wst[n % 2]
            n += 1
            stv = AP_(st, 0, [[4096, 128], [1024, 4], [1, 1024]])
            P.dma('sync' if f4 % 2 == 0 else 'scalar', stv, W['w_dn'][l, f4 * 512:(f4 + 1) * 512, :].rearrange("(a p) n -> p a n", p=128), writes=[st])
            P.copy('vector' if f4 % 2 == 0 else 'gpsimd', Wdn[:, f4 * 4:(f4 + 1) * 4, :], stv, reads=[st])
        up_n = [n]

        for bt in range(B_NM // 4):
            for j in range(4):
                m = bt * 4 + j
                xb = xt[0]
                ab, rab, yb, bgb, tlb = at_s[m % 2], ra_s[m % 2], yv[0], bg2[m % 2], tl[m % 2]
                P.dma('sync', xb[:], xsrc[m])
                P.dma('scalar', ab[:], atsc[m].rearrange("p (b q) -> p b q", b=4))
                P.dma('sync', rab[:], rasc[m])
                P.dma('scalar', yb[:], ysc[m].rearrange("p (c t) -> p c t", c=4))
                P.dma('sync', bgb[:], bg2sc[m].rearrange("p (c e) -> p c e", e=2))
                if m == 0:
                    P.memset('gpsimd', tlb[:, 0, :], 0.0)
                with P.group("tl%d" % (m % 2)):
                    if m > 0:
                        P.dma('gpsimd', tlb[:, 0, :], gathu[3 * B_NM + m - 1].rearrange("(p e) -> p e", e=8), writes=[tlb])
                    for cd in range(1, 4):
                        P.dma('gpsimd', tlb[:, cd, :], gathu[(cd - 1) * B_NM + m].rearrange("(p e) -> p e", e=8), writes=[tlb])
                hf = halo[:].rearrange("p c e -> p (c e)")
                P.ts('vector', hf, tlb[:, 0, :], selw_s[:, 0:1], ALU.mult)
                for cd in range(1, 4):
                    P.stt(hf, tlb[:, cd, :], selw_s[:, cd:cd + 1], hf, ALU.mult, ALU.add)
                P.tt('vector', pt0[:], halo[:, :, 1], cw_s[:, :, 1], ALU.mult)
                P.tt('vector', pt1[:], halo[:, :, 0], cw_s[:, :, 0], ALU.mult)
                P.tt('vector', pt0[:], pt0[:], pt1[:], ALU.add)
                P.tt('vector', pt0[:], pt0[:], bgb[:, :, 0], ALU.mult)
                P.tt('vector', yb[:, :, 0], yb[:, :, 0], pt0[:], ALU.add)
                P.tt('vector', pt1[:], halo[:, :, 1], cw_s[:, :, 0], ALU.mult)
                P.tt('vector', pt1[:], pt1[:], bgb[:, :, 1], ALU.mult)
                P.tt('vector', yb[:, :, 1], yb[:, :, 1], pt1[:], ALU.add)
                P.act(ysq[:], yb[:], AF.Square)
                for ch in range(4):
                    P.mm(px[:, 0:1], ysq[:, ch, :], onesf[:], start=(ch == 0), stop=(ch == 3))
                P.act(rc[:], px[:, 0:1], AF.Sqrt, bias=EPS, scale=1.0 / 512)
                P.recip(rc2[:], rc[:])
                P.copy('gpsimd', ybf[:], yb[:])
                for nh in range(2):
                    for kc in range(4):
                        P.mm(pa[nh][:], ab[:, kc, :], Wo[:, kc, nh * 512:(nh + 1) * 512], start=(kc == 0), stop=(kc == 3))
                    for kc in range(4):
                        P.mm(pcv[nh][:], ybf[:, kc, :], Wo[:, 4 + kc, nh * 512:(nh + 1) * 512], start=(kc == 0), stop=(kc == 3))
                x1b = x1[j]
                for nh in range(2):
                    sl = slice(nh * 512, (nh + 1) * 512)
                    P.stt(x1b[:, sl], pa[nh][:], rab[:, 0:1], xb[:, sl], ALU.mult, ALU.add)
                    P.stt(x1b[:, sl], pcv[nh][:], rc2[:, 0:1], x1b[:, sl], ALU.mult, ALU.add)
                if G.get('dbg'):
                    P.dma('gpsimd', G['dbg']['x1'][m], x1b[:])
                    P.dma('gpsimd', G['dbg']['y'][m], ybf[:].rearrange("p c t -> p (c t)"))
                    P.dma('gpsimd', G['dbg']['rc'][m], rc2[:])
                P.act(h2n[:], x1b[:], AF.Square, accum_out=ss[:])
                P.act(rstd[:], ss[:], AF.Sqrt, bias=EPS, scale=1.0 / D)
                P.recip(rstd[:], rstd[:])
                P.stt(h2n[:], x1b[:], rstd[:, 0:1], gffn_s[:], ALU.mult, ALU.mult)
                for kc in range(8):
                    P.tr(pT[:, kc, :], h2n[:, kc * 128:(kc + 1) * 128], identb[:])
                P.copy('scalar', h2T[:, :, j * 128:(j + 1) * 128], pT[:])
            for u in range(8):
                st = wst[up_n[0] % 2]
                wb = Wup[up_n[0] % 2]
                up_n[0] += 1
                P.dma('sync' if u % 2 == 0 else 'scalar', st[:], W['w_up'][l, :, u * 512:(u + 1) * 512].rearrange("(kc p) n -> p kc n", p=128))
                P.copy('gpsimd' if u % 2 == 0 else 'vector', wb[:], st[:])
                for fl in range(4):
                    f = u * 4 + fl
                    pp = pu[f % 2]
                    for kc in range(8):
                        P.mm(pp[:], wb[:, kc, fl * 128:(fl + 1) * 128], h2T[:, kc, :], start=(kc == 0), stop=(kc == 7))
                    rb = rl[0]
                    P.act(rb[:], pp[:], AF.Relu)
                    P.tt('gpsimd' if f % 2 == 0 else 'vector', actT[:, f, :], rb[:], rb[:], ALU.mult)
            for j in range(4):
                m = bt * 4 + j
                ob = x1[j]
                for nh in range(2):
                    pp = pa[nh] if j % 2 == 0 else pcv[nh]
                    for f in range(32):
                        P.mm(pp[:], actT[:, f, j * 128:(j + 1) * 128], Wdn[:, f, nh * 512:(nh + 1) * 512], start=(f == 0), stop=(f == 31))
                    P.tt('vector', ob[:, nh * 512:(nh + 1) * 512], pp[:], x1[j][:, nh * 512:(nh + 1) * 512], ALU.add)
                P.dma('sync', xdst[m], ob[:])


def build_fused(nlayers=DEPTH, stages="AGZBC", dbg=False):
    C = Ctx()
    P = C.P
    W = {}
    W['x0'] = C.din("x0", [B_NM, 128, D], F32)
    W['w_in'] = C.din("w_in", [nlayers, D, INW], F32)
    W['gmix'] = C.din("gmix", [nlayers, 128, 8], F32)
    W['gq'] = C.din("gq", [nlayers, 128, 512], F32)
    W['gk3'] = C.din("gk3", [nlayers, 128, 3, 128], F32)
    W['cw'] = C.din("cw", [nlayers, 128, 4, 3], F32)
    W['peT'] = C.din("peT", [nlayers, 2, 64, 32], F32)
    W['w1'] = C.din("w1", [nlayers, 2, 2048, 256], F32)
    W['b1b'] = C.din("b1b", [nlayers, 2, 128, 256], F32)
    W['w2'] = C.din("w2", [nlayers, 2, 256, 64], F32)
    W['b2b'] = C.din("b2b", [nlayers, 128, 2, 64], F32)
    W['w_o'] = C.din("w_o", [nlayers, D, D], F32)
    W['gout'] = C.din("gout", [nlayers, 128, 8], F32)
    W['gffn'] = C.din("gffn", [nlayers, 128, D], F32)
    W['w_up'] = C.din("w_up", [nlayers, D, DFF], F32)
    W['w_dn'] = C.din("w_dn", [nlayers, DFF, D], F32)
    W['smask'] = C.din("smask", [128, 4, 512], BF16)
    W['cmask'] = C.din("cmask", [128, 5, 512], BF16)
    W['wmask'] = C.din("wmask", [128, 8, 512], BF16)
    W['Eall'] = C.din("Eall", [64, 32, 128], BF16)
    W['Bsel'] = C.din("Bsel", [128, 512], F32)
    W['kaug'] = C.din("kaug", [4, T], BF16)
    W['kcaug'] = C.din("kcaug", [4, 2, 1024], BF16)
    W['qaug'] = C.din("qaug", [B_NM, 4, 2, 512], BF16)
    W['vcxc'] = C.din("vcxc", [128, 8, 2, 257], BF16)
    W['selw'] = C.din("selw", [128, 4], F32)
    identb_d = C.din("identb", [128, 128], BF16)
    identf_d = C.din("identf", [128, 128], F32)
    onesf_d = C.din("onesf", [128, 1], F32)
    xo = C.dout("xo", [B_NM, 128, D], F32)

    G = {}
    COMPS = [('ks', 1024), ('kw', 1024), ('kc', 1024), ('vc', 1024), ('vs0', 520), ('vs1', 520), ('vw0', 520), ('vw1', 520)]
    G['pay'] = {cn: C.dint("pay_" + cn, [rows, 512], BF16) for cn, rows in COMPS}
    G['gath'] = {cn: C.dint("gath_" + cn, [4 * rows, 512], BF16) for cn, rows in COMPS}
    G['payu'] = C.dint("payu", [B_NM, 1024], F32).ap()
    G['gathu'] = C.dint("gathu", [4 * B_NM, 1024], F32).ap()
    xbuf = C.dint("xbuf", [B_NM, 128, D], F32).ap()
    G['qsc'] = C.dint("qsc", [B_NM, 2, 64, 512], BF16)
    G['gsc'] = C.dint("gsc", [B_NM, 128, 24], F32).ap()
    G['ysc'] = C.dint("ysc", [B_NM, 128, 512], F32).ap()
    G['bg2sc'] = C.dint("bg2sc", [B_NM, 128, 8], F32).ap()
    G['atsc'] = C.dint("atsc", [B_NM, 128, 512], BF16).ap()
    G['rasc'] = C.dint("rasc", [B_NM, 128, 1], F32).ap()
    qsc_ap = G['qsc'].ap()
    if dbg:
        G['dbg'] = {'at': C.dout("dbg_at", [B_NM, 128, 512], BF16), 'ra': C.dout("dbg_ra", [B_NM, 128, 1], F32),
                    'x1': C.dout("dbg_x1", [B_NM, 128, D], F32), 'y': C.dout("dbg_y", [B_NM, 128, 512], BF16),
                    'rc': C.dout("dbg_rc", [B_NM, 128, 1], F32),
                    'kc': C.dout("dbg_kc", [68, 2, 1024], BF16), 'vc': C.dout("dbg_vc", [128, 8, 2, 321], BF16),
                    'ks': C.dout("dbg_ks", [68, T], BF16), 'vs': C.dout("dbg_vs", [128, 128, 130], BF16)}

    G['identb'] = C.sb("identb_s", [128, 128], BF16)
    G['identf'] = C.sb("identf_s", [128, 128], F32)
    G['onesf'] = C.sb("onesf_s", [128, 1], F32)
    with P.group('cst'):
        P.dma('sync', G['identb'][:], identb_d)
        P.dma('sync', G['identf'][:], identf_d)
        P.dma('sync', G['onesf'][:], onesf_d)
    P.barrier()
    rg = [[0, 1, 2, 3], [4, 5, 6, 7]]
    def mk_cc(i_ap, o_ap):
        return lambda e: e.collective_compute("AllGather", ALU.bypass, replica_groups=rg, ins=[i_ap.opt()], outs=[o_ap.opt()])
    cc_fns = [mk_cc(G['pay'][cn].ap(), G['gath'][cn].ap()) for cn, _ in COMPS] + [mk_cc(G['payu'], G['gathu'])]
    for l in range(nlayers):
        xsrc = W['x0'] if l == 0 else xbuf
        xdst = xo if l == nlayers - 1 else xbuf
        if 'A' in stages:
            emit_A(C, l, xsrc, W, G)
        if 'G' in stages:
            P.collectives(cc_fns)
        with C.phase("ZB%d" % l):
            G['kcT_s'] = C.sb("kcT_s", [68, 2, 1024], BF16)
            G['vcx_s'] = C.sb("vcx_s", [128, 8, 2, 321], BF16)
            with P.group('cst2'):
                P.dma('sync', G['kcT_s'][64:68, :, :], W['kcaug'])
                P.dma('scalar', G['vcx_s'][:, :, :, 64:321], W['vcxc'])
            if 'Z' in stages:
                emit_Z(C, l, W, G)
            Gb = dict(G)
            Gb['qsc'] = qsc_ap
            if 'B' in stages:
                emit_B(C, l, W, Gb)
        if 'C' in stages:
            emit_C(C, l, xsrc, xdst, W, G)
    if 'C' not in stages:
        with C.phase("dbg"):
            xt = [C.sb("xt%d" % i, [128, D], F32) for i in range(2)]
            for m in range(B_NM):
                P.dma('sync', xt[m % 2][:], W['x0'][m])
                P.dma('sync', xo[m], xt[m % 2][:])
    return C.finish()


IDENTB = np.eye(128, dtype=np.float32).astype(NPBF)
ONESF = np.ones((128, 1), np.float32)


def _bf(a):
    return np.ascontiguousarray(a).astype(NPBF)


def _selmap():
    i = np.arange(1024)[:, None] * 16
    j = np.arange(256)[None, :] * 64
    sh = np.clip(np.minimum(i + 32, j + 64) - np.maximum(i, j), 0, None).astype(np.float32) / 32.0
    sh[1023] = 0.0
    return sh


def _pos_rows(pos):
    pos = np.maximum(pos, 0)
    return np.stack([np.ones_like(pos), np.ones_like(pos), pos % 128, pos // 128]).astype(np.float32)


def core_consts(s):
    ki = np.arange(128)[:, None]
    qi = np.arange(128)[None, :]
    tri_gt = np.where(ki > qi, NEGM, 0.0).astype(np.float32)
    tri_le = np.where(ki <= qi, NEGM, 0.0).astype(np.float32)
    full = np.full((128, 128), NEGM, np.float32)
    zero = np.zeros((128, 128), np.float32)
    smask = np.tile(np.stack([zero if d < s else (tri_gt if d == s else full) for d in range(4)], axis=1), (1, 1, 4))
    cm = []
    for v in range(5):
        ip = ki - 32 * v
        vis = (16 * ip + 31) <= (128 * s + qi)
        cm.append(np.where(vis, 0.0, NEGM).astype(np.float32))
    cmask = np.tile(np.stack(cm, axis=1), (1, 1, 4))
    wm = []
    for d in range(8):
        off = s + 4 - d
        if off == 0:
            wm.append(tri_gt)
        elif off == 4:
            wm.append(tri_le)
        elif 0 < off < 4:
            wm.append(zero)
        else:
            wm.append(full)
    wmask = np.tile(np.stack(wm, axis=1), (1, 1, 4))
    E = np.zeros((64, 32, 128), np.float32)
    for kt in range(32):
        for k in range(128):
            E[2 * kt + k // 64, kt, k] = 1.0
    r = np.arange(512)[None, :] - 256 - 2 * s
    qq = np.arange(128)[:, None]
    Brel = np.zeros((128, 512), np.float32)
    Brel = np.where(r >= 2, np.float32(-1e30), Brel)
    Brel = np.where(r == 1, np.where(qq >= 64, np.float32(1e4), np.float32(-1e30)), Brel)
    Brel = np.where(r == 0, np.float32(1e4), Brel)
    Brel = np.where((r == -1) & (qq < 64), np.float32(1e4), Brel)
    rr, mm, kk = np.meshgrid(np.arange(4), np.arange(32), np.arange(128), indexing='ij')
    kpos = (128 * (4 * mm + rr) + kk).reshape(-1)
    kaug = _pos_rows(kpos)
    kc = _pos_rows(np.arange(1024) * 16 + 31)
    kcaug = np.stack([kc, kc], axis=1)
    qaug = np.zeros((B_NM, 4, 2, 512), np.float32)
    qv = np.arange(128)
    for m in range(B_NM):
        c = 4 * m + s
        for g in range(2):
            for rh in range(4):
                sl = 2.0 ** (-(4 * g + rh + 1))
                cs = slice(rh * 128, (rh + 1) * 128)
                qaug[m, 0, g, cs] = -sl * qv
                qaug[m, 1, g, cs] = -sl * 128.0 * c
                qaug[m, 2, g, cs] = sl
                qaug[m, 3, g, cs] = sl * 128.0
    sm = _selmap().reshape(8, 128, 256).transpose(1, 0, 2)
    vcxc = np.zeros((128, 8, 2, 257), np.float32)
    vcxc[:, :, :, 0] = 1.0
    vcxc[127, 7, :, 0] = 0.0
    vcxc[:, :, :, 1:257] = sm[:, :, None, :]
    selw = np.zeros((128, 4), np.float32)
    selw[:, s] = 1.0
    return {'smask': _bf(smask), 'cmask': _bf(cmask), 'wmask': _bf(wmask), 'Eall': _bf(E),
            'Bsel': np.ascontiguousarray(Brel.astype(np.float32)), 'kaug': _bf(kaug), 'kcaug': _bf(kcaug),
            'qaug': _bf(qaug), 'vcxc': _bf(vcxc), 'selw': selw,
            'identb': IDENTB, 'identf': np.eye(128, dtype=np.float32), 'onesf': ONESF}


def prep_fused(p, L=DEPTH):
    p = {k: (v if k == 'x' else v[:L]) for k, v in p.items()}
    common = {
        'w_in': np.ascontiguousarray(p['w_in'], dtype=np.float32),
        'gmix': np.ascontiguousarray(p['g_mix_norm'].reshape(L, 8, 128).transpose(0, 2, 1)),
        'gq': np.ascontiguousarray(np.tile(p['g_q'][:, None, :], (1, 128, 8))),
        'gk3': np.ascontiguousarray(np.broadcast_to(np.tile(p['g_k'], (1, 1, 2))[:, None], (L, 128, 3, 128))),
        'cw': np.ascontiguousarray(p['conv_w'].reshape(L, 3, 4, 128).transpose(0, 3, 2, 1)),
        'peT': np.ascontiguousarray(p['pe_cmp'].transpose(0, 1, 3, 2)),
        'w1': np.ascontiguousarray(p['w_cmp1'], dtype=np.float32),
        'b1b': np.ascontiguousarray(np.broadcast_to(p['b_cmp1'][:, :, None, :], (L, 2, 128, 256))),
        'w2': np.ascontiguousarray(p['w_cmp2'], dtype=np.float32),
        'b2b': np.ascontiguousarray(np.broadcast_to(p['b_cmp2'][:, None], (L, 128, 2, 64))),
        'w_o': np.ascontiguousarray(p['w_o'], dtype=np.float32),
        'gout': np.ascontiguousarray(p['g_out'].reshape(L, 8, 128).transpose(0, 2, 1)),
        'gffn': np.ascontiguousarray(np.tile(p['g_ffn_norm'][:, None, :], (1, 128, 1))),
        'w_up': np.ascontiguousarray(p['w_up'], dtype=np.float32),
        'w_dn': np.ascontiguousarray(p['w_down'], dtype=np.float32),
    }
    maps = []
    x = np.ascontiguousarray(p['x'], dtype=np.float32)
    for b in range(NB):
        xv = x[b].reshape(T // 128, 128, D)
        for s in range(4):
            mp = dict(common)
            mp.update(core_consts(s))
            mp['x0'] = np.ascontiguousarray(xv[np.arange(B_NM) * 4 + s])
            maps.append(mp)
    return maps


_PROG = {}


def run_fused(inputs, nlayers=DEPTH):
    p = {k: np.asarray(v) for k, v in inputs.items()}
    if nlayers not in _PROG:
        _PROG[nlayers] = build_fused(nlayers)
    res = run_bass_kernel_spmd(_PROG[nlayers], prep_fused(p, nlayers), core_ids=list(range(NCORES)))
    out = np.empty((NB, T, D), np.float32)
    for b in range(NB):
        ov = out[b].reshape(T // 128, 128, D)
        for s in range(4):
            ov[np.arange(B_NM) * 4 + s] = np.asarray(res.results[4 * b + s]['xo'])
    return out


def kernel(**inputs):
    return run_fused(inputs, DEPTH)
```

```python
import contextlib
import numpy as np
import ml_dtypes
import concourse.bass as bass
import concourse.mybir as mybir
from concourse.bass_utils import run_bass_kernel_spmd

F32 = mybir.dt.float32
BF16 = mybir.dt.bfloat16
AF = mybir.ActivationFunctionType
ALU = mybir.AluOpType
AX = mybir.AxisListType
NPBF = ml_dtypes.bfloat16

ENGS = ['tensor', 'vector', 'scalar', 'gpsimd', 'sync']
NCORES = 8
D = 1024
T = 16384
NB = 2
DEPTH = 4
INW = 2840
EPS = 1e-6
NEGM = -30000.0


def key_of(ap):
    t = getattr(ap, 'tensor', ap)
    return t.name.split('@')[0]


class Prog:
    def __init__(self, nc):
        self.nc = nc
        self.q = {e: [] for e in ENGS}
        self.semcount = {}
        self.res = {}
        self.seen = {e: {} for e in ENGS}

    grp = None

    @contextlib.contextmanager
    def group(self, sem):
        s = 'd_' + sem
        self.grp = {'sem': sem, 's': s, 'start': self.semcount.get(s, 0), 'keys': set(), 'pre': {}}
        try:
            yield
        finally:
            g, self.grp = self.grp, None
            v = self.semcount.get(s, 0)
            for k in g['keys']:
                self.res[k]['w'] = (s, v)

    def _need(self, eng, reads, writes):
        need = {}

        def add(tok):
            if tok is None:
                return
            s, v = tok
            if eng == 'tensor' and s == 'e_tensor':
                return
            if self.grp is not None and s == self.grp['s'] and v > self.grp['start']:
                return
            if need.get(s, 0) < v:
                need[s] = v
        for k in reads:
            st = self.res.get(k)
            if st:
                add(st['w'])
        for k in writes:
            sts = [self.res.get(k)]
            if self.grp is not None:
                if k not in self.grp['pre']:
                    st0 = self.res.get(k)
                    self.grp['pre'][k] = {'w': st0['w'], 'r': dict(st0['r'])} if st0 else None
                sts.append(self.grp['pre'][k])
            for st in sts:
                if st:
                    add(st['w'])
                    for s, v in st['r'].items():
                        add((s, v))
        for s, v in need.items():
            if self.seen[eng].get(s, 0) < v:
                self.q[eng].append(('wait', s, v))
                self.seen[eng][s] = v

    def _update(self, reads, writes, tok):
        s, v = tok
        for k in reads:
            st = self.res.setdefault(k, {'w': None, 'r': {}})
            if st['r'].get(s, 0) < v:
                st['r'][s] = v
        for k in writes:
            self.res[k] = {'w': tok, 'r': {}}

    def op(self, eng, fn, reads=(), writes=()):
        reads = [r if isinstance(r, str) else key_of(r) for r in reads]
        writes = [r if isinstance(r, str) else key_of(r) for r in writes]
        self._need(eng, reads, writes)
        s = 'e_' + eng
        v = self.semcount.get(s, 0) + 1
        self.semcount[s] = v
        self.q[eng].append(('op', fn, s, 1))
        self._update(reads, writes, (s, v))

    def dma(self, eng, out, in_, reads=None, writes=None, sem=None, **kw):
        if reads is None:
            reads = [] if in_.tensor.name in self.dram_names else [in_]
        if writes is None:
            writes = [] if out.tensor.name in self.dram_names else [out]
        assert reads or writes or sem
        reads = [r if isinstance(r, str) else key_of(r) for r in reads]
        writes = [r if isinstance(r, str) else key_of(r) for r in writes]
        if self.grp is not None:
            sem = self.grp['sem']
            self.grp['keys'].update(writes)
        if sem is None:
            sem = writes[0] if writes else 'st_' + reads[0]
        self._need(eng, reads, writes)
        s = 'd_' + sem
        v = self.semcount.get(s, 0) + 16
        self.semcount[s] = v
        self.q[eng].append(('op', lambda e: e.dma_start(out=out, in_=in_, **kw), s, 16))
        self._update(reads, writes, (s, v))

    dram_names = set()

    def mm(self, out, lhsT, rhs, start=True, stop=True, reads=None, writes=None):
        self.op('tensor', lambda e: e.matmul(out, lhsT=lhsT, rhs=rhs, start=start, stop=stop),
                reads=reads if reads is not None else [lhsT, rhs],
                writes=writes if writes is not None else [out])

    def tr(self, out, in_, ident, reads=None, writes=None):
        self.op('tensor', lambda e: e.transpose(out, in_, ident),
                reads=reads if reads is not None else [in_, ident],
                writes=writes if writes is not None else [out])

    def act(self, out, in_, func, bias=None, scale=None, accum_out=None, reads=None, writes=None, eng='scalar'):
        kw = {}
        rd = [in_]
        if bias is not None:
            kw['bias'] = bias
            if not isinstance(bias, (int, float)):
                rd.append(bias)
        if scale is not None:
            kw['scale'] = scale
            if not isinstance(scale, (int, float)):
                rd.append(scale)
        wr = [out]
        if accum_out is not None:
            kw['accum_out'] = accum_out
            wr.append(accum_out)
        self.op('scalar', lambda e: e.activation(out=out, in_=in_, func=func, **kw),
                reads=reads if reads is not None else rd,
                writes=writes if writes is not None else wr)

    def tt(self, eng, out, in0, in1, op, reads=None, writes=None):
        self.op(eng, lambda e: e.tensor_tensor(out=out, in0=in0, in1=in1, op=op),
                reads=reads if reads is not None else [in0, in1],
                writes=writes if writes is not None else [out])

    def ts(self, eng, out, in0, s1, op0, s2=None, op1=None, reads=None, writes=None):
        rd = [in0]
        if not isinstance(s1, (int, float)):
            rd.append(s1)
        if s2 is not None and not isinstance(s2, (int, float)):
            rd.append(s2)
        if op1 is None:
            fn = lambda e: e.tensor_scalar(out=out, in0=in0, scalar1=s1, scalar2=None, op0=op0)
        else:
            fn = lambda e: e.tensor_scalar(out=out, in0=in0, scalar1=s1, scalar2=s2, op0=op0, op1=op1)
        self.op(eng, fn, reads=reads if reads is not None else rd,
                writes=writes if writes is not None else [out])

    def stt(self, out, in0, scalar, in1, op0, op1, reads=None, writes=None):
        rd = [in0, in1]
        if not isinstance(scalar, (int, float)):
            rd.append(scalar)
        self.op('vector', lambda e: e.scalar_tensor_tensor(out=out, in0=in0, scalar=scalar, in1=in1, op0=op0, op1=op1),
                reads=reads if reads is not None else rd,
                writes=writes if writes is not None else [out])

    def copy(self, eng, out, in_, reads=None, writes=None):
        if eng == 'scalar':
            fn = lambda e: e.activation(out=out, in_=in_, func=AF.Copy)
        else:
            fn = lambda e: e.tensor_copy(out=out, in_=in_)
        self.op(eng, fn, reads=reads if reads is not None else [in_],
                writes=writes if writes is not None else [out])

    def recip(self, out, in_):
        self.op('vector', lambda e: e.reciprocal(out=out, in_=in_), reads=[in_], writes=[out])

    def reduce(self, out, in_, op=ALU.add, axis=AX.X):
        self.op('vector', lambda e: e.tensor_reduce(out=out, in_=in_, axis=axis, op=op), reads=[in_], writes=[out])

    def memset(self, eng, ap, val):
        self.op(eng, lambda e: e.memset(ap, val), writes=[ap])

    def barrier(self):
        for e in ENGS:
            for s, v in self.semcount.items():
                if self.seen[e].get(s, 0) < v:
                    self.q[e].append(('wait', s, v))
                    self.seen[e][s] = v
        self.res = {}

    def collectives(self, fns):
        self.barrier()
        s = 'c_cc'
        for fn in fns:
            v = self.semcount.get(s, 0) + 1
            self.semcount[s] = v
            self.q['gpsimd'].append(('op', fn, s, 1))
        self.barrier()

    def emit(self):
        nc = self.nc
        for s, v in self.semcount.items():
            if self.seen['sync'].get(s, 0) < v:
                self.q['sync'].append(('wait', s, v))
        with contextlib.ExitStack() as es:
            semh = {s: es.enter_context(nc.semaphore(s)) for s in self.semcount}
            block = es.enter_context(nc.Block())

            def mk(engname):
                def body(e):
                    for it in self.q[engname]:
                        if it[0] == 'wait':
                            e.wait_ge(semh[it[1]], it[2])
                        else:
                            ins = it[1](e)
                            ins.then_inc(semh[it[2]], it[3])
                return body
            for engname in ENGS:
                if self.q[engname]:
                    getattr(block, engname)(mk(engname))


def AP_(t, offset, dims):
    return bass.AP(t, offset, [list(d) for d in dims])


class Ctx:
    def __init__(self):
        self.nc = bass.Bass("TRN2", target_bir_lowering=False)
        self.es = contextlib.ExitStack()
        self.P = Prog(self.nc)
        self.P.dram_names = set()

    def din(self, name, shape, dt):
        self.P.dram_names.add(name)
        return self.nc.dram_tensor(name, list(shape), dt, kind="ExternalInput").ap()

    def dout(self, name, shape, dt):
        self.P.dram_names.add(name)
        return self.nc.dram_tensor(name, list(shape), dt, kind="ExternalOutput").ap()

    tag = None

    def dint(self, name, shape, dt):
        self.P.dram_names.add(name)
        return self.nc.dram_tensor(name, list(shape), dt)

    def sb(self, name, shape, dt):
        if self.tag:
            name = name + '@' + self.tag
        return self.es.enter_context(self.nc.sbuf_tensor(name, list(shape), dt))

    def ps(self, name, shape, dt):
        if self.tag:
            name = name + '@' + self.tag
        return self.es.enter_context(self.nc.psum_tensor(name, list(shape), dt))

    @contextlib.contextmanager
    def phase(self, tag):
        old, oldtag = self.es, self.tag
        self.es, self.tag = contextlib.ExitStack(), tag
        try:
            yield
        finally:
            self.P.barrier()
            self.es.close()
            self.es, self.tag = old, oldtag

    def finish(self):
        self.P.emit()
        self.es.close()
        return self.nc


B_NM = 32
DFF = 4096
PAYR = 6176
KS_R0, KW_R0, KC_R0, VC_R0, VS_R0, VW_R0 = 0, 1024, 2048, 3072, 4096, 5136
TP = T + 128


def emit_A(C, l, xsrc, W, G):
    P = C.P
    with C.phase("A%d" % l):
        Wbf = C.sb("Wbf", [128, 8, INW], BF16)
        wst = [C.sb("wst%d" % i, [128, INW // 2], F32) for i in range(2)]
        gmix_s = C.sb("gmix_s", [128, 8], F32)
        gq_s = C.sb("gq_s", [128, 512], F32)
        gk3_s = C.sb("gk3_s", [128, 3, 128], F32)
        cw_s = C.sb("cw_s", [128, 4, 3], F32)
        xt = [C.sb("xt%d" % i, [128, D], F32) for i in range(2)]
        junk = C.sb("junk", [128, D], BF16)
        ss = C.sb("ss", [128, 1], F32)
        rstd = C.sb("rstd", [128, 1], F32)
        hn = C.sb("hn", [128, D], BF16)
        hT = C.sb("hT", [128, 8, 128], BF16)
        sq = C.sb("sq", [128, 1280], F32)
        ssg = C.sb("ssg", [128, 20], F32)
        rg = C.sb("rg", [128, 20], F32)
        qtmp = C.sb("qtmp", [128, 512], F32)
        ktmp = C.sb("ktmp", [128, 256], F32)
        nrm = C.sb("nrm", [128, 1024], BF16)
        trs = C.sb("trs", [128, 8, 128], BF16)
        vsw = C.sb("vsw", [128, 2, 130], BF16)
        gts = C.sb("gts", [128, 24], F32)
        hcs = C.sb("hcs", [128, 4, 128], F32)
        ub = C.sb("ub", [128, 4, 130], F32)
        cv0 = C.sb("cv0", [128, 4, 128], F32)
        cv1 = C.sb("cv1", [128, 4, 128], F32)
        yv = C.sb("yv", [128, 4, 128], F32)
        ut = C.sb("ut", [128, 4, 2], F32)
        bg2 = C.sb("bg2", [128, 4, 2], F32)
        pT = C.ps("pT", [128, 8, 128], BF16)
        pz = [C.ps("pz%d" % i, [128, 512], F32) for i in range(3)]
        pc = [C.ps("pc%d" % i, [128, 4, 128], F32) for i in range(3)]
        identb = G['identb']

        with P.group('cst'):
            P.dma('sync', gmix_s[:], W['gmix'][l])
            P.dma('sync', gq_s[:], W['gq'][l])
            P.dma('sync', gk3_s[:], W['gk3'][l])
            P.dma('sync', cw_s[:], W['cw'][l])
        P.memset('gpsimd', vsw[:], 1.0)
        P.memset('gpsimd', ub[:], 0.0)
        for kc in range(8):
            for hf in range(2):
                st = wst[hf]
                P.dma('sync' if hf == 0 else 'scalar', st[:], W['w_in'][l, kc * 128:(kc + 1) * 128, hf * 1420:(hf + 1) * 1420])
                P.ts('vector' if hf == 0 else 'gpsimd', Wbf[:, kc, hf * 1420:(hf + 1) * 1420], st[:], gmix_s[:, kc:kc + 1], ALU.mult)

        pay, payu, qsc, gsc, ysc, bg2sc = G['pay'], G['payu'], G['qsc'], G['gsc'], G['ysc'], G['bg2sc']
        for m in range(B_NM):
            xb = xt[m % 2]
            P.dma('sync', xb[:], xsrc[m])
            P.act(junk[:], xb[:], AF.Square, accum_out=ss[:])
            P.act(rstd[:], ss[:], AF.Sqrt, bias=EPS, scale=1.0 / D)
            P.recip(rstd[:], rstd[:])
            P.act(hn[:], xb[:], AF.Copy, scale=rstd[:, 0:1])
            for kc in range(8):
                P.tr(pT[:, kc, :], hn[:, kc * 128:(kc + 1) * 128], identb[:])
            P.copy('vector', hT[:], pT[:])
            for bi, (c0, c1) in enumerate([(0, 512), (512, 1024), (1024, 1304)]):
                for kc in range(8):
                    P.mm(pz[bi][:, 0:c1 - c0], hT[:, kc, :], Wbf[:, kc, c0:c1], start=(kc == 0), stop=(kc == 7))
            for ch in range(12):
                for kc in range(8):
                    P.mm(pc[ch // 4][:, ch % 4, :], Wbf[:, kc, 1304 + ch * 128:1304 + (ch + 1) * 128], hT[:, kc, :],
                         start=(kc == 0), stop=(kc == 7))
            P.act(sq[:, 0:512], pz[0][:], AF.Square)
            P.act(sq[:, 512:1024], pz[1][:], AF.Square)
            P.act(sq[:, 1024:1280], pz[2][:, 0:256], AF.Square)
            P.reduce(ssg[:], sq[:].rearrange("p (g d) -> p g d", d=64))
            P.act(rg[:, 0:8], ssg[:, 0:8], AF.Sqrt, bias=64 * EPS, scale=1.0)
            P.act(rg[:, 8:20], ssg[:, 8:20], AF.Sqrt, bias=EPS, scale=1.0 / 64)
            P.recip(rg[:], rg[:])
            P.tt('vector', qtmp[:].rearrange("p (g d) -> p g d", d=64), pz[0][:].rearrange("p (g d) -> p g d", d=64),
                 AP_(rg, 0, [[20, 128], [1, 8], [0, 64]]), ALU.mult)
            P.tt('gpsimd', nrm[:, 0:512], qtmp[:], gq_s[:], ALU.mult)
            P.tt('vector', ktmp[:, 0:128].rearrange("p (g d) -> p g d", d=64), pz[1][:, 256:384].rearrange("p (g d) -> p g d", d=64),
                 AP_(rg, 12, [[20, 128], [1, 2], [0, 64]]), ALU.mult)
            P.tt('vector', ktmp[:, 128:256].rearrange("p (g d) -> p g d", d=64), pz[2][:, 0:128].rearrange("p (g d) -> p g d", d=64),
                 AP_(rg, 16, [[20, 128], [1, 2], [0, 64]]), ALU.mult)
            P.tt('gpsimd', nrm[:, 512:768], ktmp[:], gk3_s[:, 1:3, :].rearrange("p a d -> p (a d)"), ALU.mult)
            P.copy('vector', nrm[:, 768:1024], pz[1][:, 0:256])
            P.copy('scalar', AP_(vsw, 1, [[260, 128], [65, 2], [1, 64]]), pz[1][:, 384:512].rearrange("p (g d) -> p g d", d=64))
            P.copy('scalar', AP_(vsw, 131, [[260, 128], [65, 2], [1, 64]]), pz[2][:, 128:256].rearrange("p (g d) -> p g d", d=64))
            P.dma('scalar', AP_(pay['vs%d' % (m // 16)], (m % 16) * 128 * 130, [[130, 128], [1, 130]]), vsw[:, 0, :])
            P.dma('scalar', AP_(pay['vw%d' % (m // 16)], (m % 16) * 128 * 130, [[130, 128], [1, 130]]), vsw[:, 1, :])
            P.act(gts[:], pz[2][:, 256:280], AF.Sigmoid)
            P.dma('scalar', gsc[m], gts[:])
            for bk in range(8):
                P.tr(pT[:, bk, :], nrm[:, bk * 128:(bk + 1) * 128], identb[:])
            P.copy('vector', trs[:], pT[:])
            for e in range(2):
                for g in range(2):
                    P.dma('sync' if g == 0 else 'scalar', AP_(qsc, m * 65536 + g * 32768 + e * 128, [[512, 64], [256, 2], [1, 128]]),
                          trs[e * 64:(e + 1) * 64, 2 * g:2 * g + 2, :])
            for ci, cn in enumerate(('ks', 'kw', 'kc', 'vc')):
                P.dma('sync' if ci % 2 == 0 else 'scalar', AP_(pay[cn], m * 128, [[4096, 128], [1, 128]]), trs[:, 4 + ci, :])
            P.copy('scalar', hcs[:], pc[0][:])
            P.tt('vector', ub[:, :, 2:130], pc[1][:], hcs[:], ALU.mult)
            P.tt('gpsimd', cv0[:], ub[:, :, 2:130], AP_(cw_s, 2, [[12, 128], [3, 4], [0, 128]]), ALU.mult)
            P.tt('gpsimd', cv1[:], ub[:, :, 1:129], AP_(cw_s, 1, [[12, 128], [3, 4], [0, 128]]), ALU.mult)
            P.tt('gpsimd', cv0[:], cv0[:], cv1[:], ALU.add)
            P.tt('gpsimd', cv1[:], ub[:, :, 0:128], AP_(cw_s, 0, [[12, 128], [3, 4], [0, 128]]), ALU.mult)
            P.tt('gpsimd', cv0[:], cv0[:], cv1[:], ALU.add)
            P.tt('vector', yv[:], pc[2][:], cv0[:], ALU.mult)
            P.dma('scalar', ysc[m].rearrange("p (c t) -> p c t", c=4), yv[:])
            P.copy('gpsimd', ut[:], ub[:, :, 128:130])
            P.dma('sync', payu[m].rearrange("(p e) -> p e", e=8), ut[:].rearrange("p c e -> p (c e)"))
            P.copy('vector', bg2[:], pc[2][:, :, 0:2])
            P.dma('sync', bg2sc[m], bg2[:].rearrange("p c e -> p (c e)"))


def emit_Z(C, l, W, G):
    P = C.P
    with C.phase("Z%d" % l):
        kcT_all = C.sb("kcT_all", [128, TP], BF16)
        vcT_all = C.sb("vcT_all", [128, TP], BF16)
        W1bf = C.sb("W1bf", [128, 2, 32, 256], BF16)
        w1st = [C.sb("w1st%d" % i, [128, 8, 256], F32) for i in range(2)]
        W2bf = C.sb("W2bf", [128, 2, 2, 64], BF16)
        w2st = C.sb("w2st", [128, 2, 2, 64], F32)
        peT_s = C.sb("peT_s", [64, 2, 32], F32)
        pebf = C.sb("pebf", [64, 2, 32, 128], BF16)
        b1b_s = C.sb("b1b_s", [128, 2, 256], F32)
        b2b_s = C.sb("b2b_s", [128, 2, 64], F32)
        gkc = C.sb("gkc", [128, 128], F32)
        c1b = C.sb("c1b", [128, 2, 256], F32)
        hid = C.sb("hid", [128, 256], F32)
        g_x2 = C.sb("g_x2", [128, 256], F32)
        g_in = C.sb("g_in", [128, 256], F32)
        g_sg = C.sb("g_sg", [128, 256], F32)
        hbf = C.sb("hbf", [128, 256], BF16)
        hidT = C.sb("hidT", [128, 2, 128], BF16)
        co = C.sb("co", [128, 2, 2, 64], F32)
        cosq = C.sb("cosq", [128, 128], F32)
        css = C.sb("css", [128, 2], F32)
        crs = C.sb("crs", [128, 2], F32)
        kcn = C.sb("kcn", [128, 128], BF16)
        pT = C.ps("pT", [128, 8, 128], BF16)
        px = [C.ps("px%d" % i, [128, 512], F32) for i in range(2)]
        po = C.ps("po", [128, 512], F32)
        identb = G['identb']
        gath = G['gath']
        kcT_s, vcx_s = G['kcT_s'], G['vcx_s']

        with P.group('cst'):
            P.dma('sync', b1b_s[:], W['b1b'][l].rearrange("k p n -> p k n"))
            P.dma('sync', b2b_s[:], W['b2b'][l])
            P.dma('sync', peT_s[:], W['peT'][l].rearrange("k d j -> d k j"))
            P.dma('sync', w2st[:], W['w2'][l].rearrange("k (c p) d -> p k c d", p=128))
            P.dma('sync', gkc[:], W['gk3'][l, :, 0, :])
        P.memset('gpsimd', kcT_all[:, T:TP], 0.0)
        P.memset('gpsimd', vcT_all[:, T:TP], 0.0)
        with P.group('kvcl'):
            for r in range(4):
                for kv, dst in enumerate((kcT_all, vcT_all)):
                    P.dma('sync' if kv == 0 else 'scalar', AP_(dst, r * 128, [[TP, 128], [512, 32], [1, 128]]),
                          AP_(gath['kc' if kv == 0 else 'vc'], r * 1024 * 512, [[4096, 128], [128, 32], [1, 128]]),
                          writes=[dst])
        P.copy('vector', W2bf[:], w2st[:])
        P.copy('vector', pebf[:], AP_(peT_s, 0, [[64, 64], [32, 2], [1, 32], [0, 128]]))
        n = 0
        for kv in range(2):
            for jq in range(4):
                st = w1st[n % 2]
                n += 1
                src = W['w1'][l, kv, jq * 512:(jq + 1) * 512, :].rearrange("(j d) n -> d j n", d=64)
                with P.group("w1st%d" % ((n - 1) % 2)):
                    P.dma('sync', st[0:64, :, :], src, writes=[st])
                    P.dma('scalar', st[64:128, :, :], src, writes=[st])
                P.copy('vector' if jq % 2 == 0 else 'gpsimd', W1bf[:, kv, jq * 8:(jq + 1) * 8, :], st[:])
        for kv in range(2):
            for j in range(32):
                P.mm(px[0][:, 0:256], pebf[:, kv, j, :], W1bf[0:64, kv, j, :], start=(j == 0), stop=(j == 31))
            P.tt('vector', c1b[:, kv, :], px[0][:, 0:256], b1b_s[:, kv, :], ALU.add)
        n = 0
        for ib in range(8):
            for kv in range(2):
                src_all = kcT_all if kv == 0 else vcT_all
                for g in range(2):
                    pp = px[n % 2]
                    n += 1
                    base = 16 * 128 * ib
                    for j in range(32):
                        lhs = AP_(src_all, 64 * g * TP + base + j, [[TP, 64], [16, 128]])
                        P.mm(pp[:, 0:256], lhs, W1bf[64 * g:64 * g + 64, kv, j, :], start=(j == 0), stop=(j == 31),
                             reads=[src_all, W1bf])
                    P.tt('vector', hid[:], pp[:, 0:256], c1b[:, kv, :], ALU.add)
                    P.act(g_x2[:], hid[:], AF.Square)
                    P.ts('gpsimd', g_x2[:], g_x2[:], 0.044715, ALU.mult, 1.0, ALU.add)
                    P.tt('gpsimd', g_in[:], g_x2[:], hid[:], ALU.mult)
                    P.act(g_sg[:], g_in[:], AF.Sigmoid, scale=1.5957691216057308)
                    P.tt('vector', hbf[:], hid[:], g_sg[:], ALU.mult)
                    for c in range(2):
                        P.tr(pT[:, c, :], hbf[:, c * 128:(c + 1) * 128], identb[:])
                    P.copy('vector', hidT[:], pT[:, 0:2, :])
                    for c in range(2):
                        P.mm(po[:, 0:64], hidT[:, c, :], W2bf[:, kv, c, :], start=(c == 0), stop=(c == 1))
                    P.tt('vector', co[:, kv, g, :], po[:, 0:64], b2b_s[:, kv, :], ALU.add)
            P.act(cosq[:], co[:, 0, :, :].rearrange("p g d -> p (g d)"), AF.Square)
            P.reduce(css[:], cosq[:].rearrange("p (g d) -> p g d", d=64))
            P.act(crs[:], css[:], AF.Sqrt, bias=EPS, scale=1.0 / 64)
            P.recip(crs[:], crs[:])
            P.tt('vector', cosq[:].rearrange("p (g d) -> p g d", d=64), co[:, 0, :, :], AP_(crs, 0, [[2, 128], [1, 2], [0, 64]]), ALU.mult)
            P.tt('vector', kcn[:], cosq[:], gkc[:], ALU.mult)
            for g in range(2):
                P.tr(pT[0:64, 4 + g, :], kcn[:, 64 * g:64 * g + 64], identb[:])
            P.copy('vector', kcT_s[0:64, :, ib * 128:(ib + 1) * 128], pT[0:64, 4:6, :])
            P.copy('gpsimd', vcx_s[:, ib, :, 0:64], co[:, 1, :, :])
        if G.get('dbg'):
            P.dma('sync', G['dbg']['kc'], kcT_s[:])
            P.dma('sync', G['dbg']['vc'], vcx_s[:])


def emit_B(C, l, W, G):
    P = C.P
    with C.phase("B%d" % l):
        ksT_s = [C.sb("ksT_s%d" % g, [68, T], BF16) for g in range(2)]
        vsx_s = C.sb("vsx_s", [128, 128, 130], BF16)
        smask_s = C.sb("smask_s", [128, 4, 512], BF16)
        cmask_s = C.sb("cmask_s", [128, 5, 512], BF16)
        wmask_s = C.sb("wmask_s", [128, 8, 128], BF16)
        Eall_s = C.sb("Eall_s", [64, 32, 128], BF16)
        Bsel_s = C.sb("Bsel_s", [128, 512], F32)
        qT_s = [C.sb("qT_s%d" % i, [68, 2, 512], BF16) for i in range(2)]
        kwT_s = [C.sb("kwT_s%d" % i, [68, 2, 1024], BF16) for i in range(2)]
        vwx_s = [C.sb("vwx_s%d" % i, [128, 8, 130], BF16) for i in range(2)]
        gates_s = [C.sb("gates_s%d" % i, [128, 24], F32) for i in range(2)]
        NPB = 4
        Pb = [C.sb("Pb%d" % i, [128, 512], BF16) for i in range(NPB)]
        Pc = [C.sb("Pc%d" % g, [128, 8, 512], BF16) for g in range(2)]
        oc = C.sb("oc", [128, 4, 321], F32)
        den4 = C.sb("den4", [128, 4], F32)
        rd4 = C.sb("rd4", [128, 4], F32)
        coef4 = C.sb("coef4", [128, 4], F32)
        imp = [C.sb("imp%d" % g, [128, 256], F32) for g in range(2)]
        tmpi = C.sb("tmpi", [128, 256], F32)
        m8 = C.sb("m8", [128, 16], F32)
        thr = C.sb("thr", [128, 1], F32)
        nm = C.sb("nm", [128, 256], BF16)
        nmT = [C.sb("nmT%d" % g, [64, 4, 128], BF16) for g in range(2)]
        osT = C.sb("osT", [65, 512], F32)
        otmp = C.sb("otmp", [128, 4, 64], F32)
        acc = C.sb("acc", [128, 8, 64], F32)
        accb = C.sb("accb", [128, 512], BF16)
        ssq = C.sb("ssq", [128, 1], F32)
        rat = C.sb("rat", [128, 1], F32)
        rat2 = C.sb("rat2", [128, 1], F32)
        atT = C.sb("atT", [128, 4, 128], BF16)
        ps_s = [C.ps("ps_s%d" % i, [128, 512], F32) for i in range(3)]
        ps_o = [C.ps("ps_o%d" % i, [128, 512], F32) for i in range(1)]
        ps_acc = [C.ps("ps_acc%d" % i, [128, 512], F32) for i in range(2)]
        pm = C.ps("pm", [128, 512], F32)
        pm_b = pm[:].bitcast(BF16)
        identb, identf = G['identb'], G['identf']
        kcT_s, vcx_s = G['kcT_s'], G['vcx_s']
        gath, qsc, gsc, atsc, rasc = G['gath'], G['qsc'], G['gsc'], G['atsc'], G['rasc']

        with P.group('cst'):
            P.dma('sync', cmask_s[:], W['cmask'])
            P.dma('sync', Bsel_s[:], W['Bsel'])
            P.dma('scalar', wmask_s[:], W['wmask'])
            P.dma('scalar', smask_s[:], W['smask'])
            P.dma('scalar', Eall_s[:], W['Eall'])
        with P.group('ksT'):
            for g in range(2):
                P.dma('gpsimd', ksT_s[g][64:68, :], W['kaug'], writes=[ksT_s[g]])
                for r in range(4):
                    P.dma('sync' if r % 2 == 0 else 'scalar', ksT_s[g][0:64, r * 4096:(r + 1) * 4096],
                          AP_(gath['ks'], r * 1024 * 512 + 64 * g * 4096, [[4096, 64], [1, 4096]]),
                          writes=[ksT_s[g]])
        with P.group('vsx'):
            for r in range(4):
                for hv in range(2):
                    P.dma('gpsimd', vsx_s[:, r * 32 + hv * 16:r * 32 + hv * 16 + 16, :],
                          AP_(gath['vs%d' % hv], r * 520 * 512, [[130, 128], [128 * 130, 16], [1, 130]]), writes=[vsx_s])

        if G.get('dbg'):
            P.dma('sync', G['dbg']['ks'], ksT_s[1][:])
            P.dma('sync', G['dbg']['vs'], vsx_s[:])
        state = {'s': 0, 'p': 0, 'x': 0}

        def run_stream(tiles):
            LA = 2
            n = len(tiles)
            bufs = []
            for i in range(n + LA):
                if i < n:
                    ps = ps_s[state['s'] % 3]
                    pb = Pb[state['p'] % NPB]
                    state['s'] += 1
                    state['p'] += 1
                    mk = tiles[i][0](ps)
                    P.act(pb[:], ps[:], AF.Exp)
                    if mk is not None:
                        map_, mkeys = mk
                        P.tt('vector', pb[:].rearrange("p (a q) -> p a q", a=4), pb[:].rearrange("p (a q) -> p a q", a=4), map_,
                             ALU.mult, reads=[pb] + mkeys, writes=[pb])
                    bufs.append(pb)
                if i - LA >= 0:
                    tiles[i - LA][1](bufs[i - LA])

        def o_epilogue(psacc, g, br, gs):
            P.copy('vector', osT[:], psacc[0:65, :])
            pmv = pm[:, 0:260].rearrange("p (r d) -> p r d", d=65)
            for r in range(4):
                P.tr(pmv[:, r, :], osT[0:65, r * 128:(r + 1) * 128], identf[0:65, 0:65])
            P.recip(rd4[:], pmv[:, :, 0])
            P.tt('vector', coef4[:], rd4[:], AP_(gs, 12 * g + br, [[24, 128], [3, 4]]), ALU.mult)
            dst = acc[:, 4 * g:4 * g + 4, :]
            P.tt('vector', otmp[:], pmv[:, :, 1:65], AP_(coef4, 0, [[4, 128], [1, 4], [0, 64]]), ALU.mult)
            P.tt('gpsimd', dst, dst, otmp[:], ALU.add)

        for m in range(B_NM):
            qs = qT_s[m % 2]
            kws = kwT_s[m % 2]
            vws = vwx_s[m % 2]
            gs = gates_s[m % 2]
            h0 = 0 if m > 0 else 1
            with P.group("qT_s%d" % (m % 2)):
                P.dma('sync', qs[0:64, :, :], qsc[m].rearrange("g d n -> d g n"), writes=[qs])
                P.dma('scalar', qs[64:68, :, :], W['qaug'][m], writes=[qs])
            P.dma('scalar', gs[:], gsc[m])
            nh = 2 - h0
            c0 = 128 * (m - 1 + h0)
            with P.group("kws%d" % (m % 2)):
                for r in range(4):
                    P.dma('sync' if r % 2 == 0 else 'scalar',
                          AP_(kws, (r * 2 + h0) * 128, [[2048, 64], [1024, 2], [1, 128 * nh]]),
                          AP_(gath['kw'], r * 1024 * 512 + c0, [[4096, 64], [64 * 4096, 2], [1, 128 * nh]]),
                          writes=[kws])
            with P.group("vws%d" % (m % 2)):
                for r in range(4):
                    for hf in range(h0, 2):
                        mp = m - 1 + hf
                        P.dma('gpsimd', vws[:, r * 2 + hf, :],
                              AP_(gath['vw%d' % (mp // 16)], r * 520 * 512 + (mp % 16) * 128 * 130, [[130, 128], [1, 130]]),
                              writes=[vws])
            for g in range(2):
                P.copy('gpsimd', AP_(kws, 64 * 2048 + g * 1024 + h0 * 128, [[2048, 4], [256, 4], [1, 128 * (2 - h0)]]),
                       AP_(ksT_s[0], 64 * T + 128 * (m - 1 + h0), [[T, 4], [4096, 4], [1, 128 * (2 - h0)]]),
                       reads=[ksT_s[0]], writes=[kws])
            n_it = (32 * m + 30) // 128 + 1
            for g in range(2):
                for it in range(n_it):
                    ps = ps_s[state['s'] % 3]
                    state['s'] += 1
                    delta = 128 * it - 32 * m
                    masked = delta >= -128
                    P.mm(ps[:], kcT_s[:, g, it * 128:(it + 1) * 128], qs[:, g, :], start=True, stop=not masked)
                    if masked:
                        P.mm(ps[:], identb[:], cmask_s[:, (-delta) // 32, :], start=False, stop=True)
                    P.act(Pc[g][:, it, :], ps[:], AF.Exp)
                for r in range(4):
                    po = ps_o[0]
                    for it in range(n_it):
                        P.mm(po[:, 0:321], Pc[g][:, it, r * 128:(r + 1) * 128], vcx_s[:, it, g, :], start=(it == 0), stop=(it == n_it - 1))
                    P.copy('scalar', oc[:, r, :], po[:, 0:321])
                P.ts('vector', den4[:], oc[:, :, 64], 1e-30, ALU.max)
                P.recip(rd4[:], den4[:])
                P.ts('vector', imp[g][:], oc[:, 0, 65:321], rd4[:, 0:1], ALU.mult)
                for r in range(1, 4):
                    P.stt(imp[g][:], oc[:, r, 65:321], rd4[:, r:r + 1], imp[g][:], ALU.mult, ALU.add)
                P.tt('vector', coef4[:], rd4[:], AP_(gs, 12 * g + 0, [[24, 128], [3, 4]]), ALU.mult)
                P.tt('vector', acc[:, 4 * g:4 * g + 4, :], oc[:, :, 0:64], AP_(coef4, 0, [[4, 128], [1, 4], [0, 64]]), ALU.mult)
                P.tt('vector', imp[g][:], imp[g][:], Bsel_s[:, 256 - 8 * m:512 - 8 * m], ALU.add)
                P.ts('vector', imp[g][:, 0:1], imp[g][:, 0:1], 1e4, ALU.add)
                P.op('vector', lambda e, g=g: e.max(out=m8[:, 0:8], in_=imp[g][:]), reads=[imp[g]], writes=[m8])
                P.op('vector', lambda e, g=g: e.match_replace(out=tmpi[:], in_to_replace=m8[:, 0:8], in_values=imp[g][:], imm_value=-3.0e38),
                     reads=[imp[g], m8], writes=[tmpi])
                P.op('vector', lambda e: e.max(out=m8[:, 8:16], in_=tmpi[:]), reads=[tmpi], writes=[m8])
                P.ts('vector', thr[:], m8[:, 15:16], -1e29, ALU.max)
                P.ts('vector', nm[:], imp[g][:], thr[:, 0:1], ALU.is_ge)
                for qt in range(4):
                    P.tr(pm_b[0:64, qt * 128:(qt + 1) * 128], nm[:, qt * 64:(qt + 1) * 64], identb[:])
                P.copy('vector', nmT[g][:].rearrange("p a q -> p (a q)"), pm_b[0:64, 0:512], reads=[pm])
            tiles = []
            wt = [(r, hf) for hf in range(h0, 2) for r in range(4)]
            for g in range(2):
                for idx, (r, hf) in enumerate(wt):
                    def eS(ps, g=g, r=r, hf=hf):
                        P.mm(ps[:], kws[:, g, (r * 2 + hf) * 128:(r * 2 + hf + 1) * 128], qs[:, g, :], start=True, stop=(hf == 0))
                        if hf == 1:
                            P.mm(ps[:], identb[:], smask_s[:, r, :], start=False, stop=True)
                            return None
                        return AP_(wmask_s, r * 128, [[1024, 128], [0, 4], [1, 128]]), [wmask_s]

                    def ePV(pb, g=g, r=r, hf=hf, idx=idx):
                        P.mm(ps_acc[g][0:65, :], vws[:, r * 2 + hf, 65 * g:65 * g + 65], pb[:], start=(idx == 0), stop=(idx == len(wt) - 1))
                    tiles.append((eS, ePV))
            run_stream(tiles)
            for g in range(2):
                o_epilogue(ps_acc[g], g, 2, gs)
            nkt = 4 * m + 4
            for g in range(2):
                tiles = []
                for kt in range(nkt):
                    def eS(ps, g=g, kt=kt):
                        diag = kt >= 4 * m
                        col = ((kt % 4) * 32 + kt // 4) * 128
                        P.mm(ps[:], ksT_s[g][:, col:col + 128], qs[:, g, :], start=True, stop=not diag)
                        if diag:
                            P.mm(ps[:], identb[:], smask_s[:, kt - 4 * m, :], start=False, stop=True)
                        bank = (ps_o[0], pm)[state['x'] % 2]
                        state['x'] += 1
                        P.mm(bank[:, 0:128], Eall_s[:, kt % 32, :], nmT[g][:, kt // 32, :], start=True, stop=True)
                        return AP_(bank, 0, [[512, 128], [0, 4], [1, 128]]), [bank]

                    def ePV(pb, g=g, kt=kt):
                        P.mm(ps_acc[g][0:65, :], vsx_s[:, (kt % 4) * 32 + kt // 4, 65 * g:65 * g + 65], pb[:], start=(kt == 0), stop=(kt == nkt - 1))
                    tiles.append((eS, ePV))
                run_stream(tiles)
                o_epilogue(ps_acc[g], g, 1, gs)
            accf = acc[:].rearrange("p h d -> p (h d)")
            P.act(accb[:], accf, AF.Square, accum_out=ssq[:])
            P.act(rat[:], ssq[:], AF.Sqrt, bias=EPS, scale=1.0 / 512)
            P.recip(rat2[:], rat[:])
            P.dma('gpsimd', rasc[m], rat2[:])
            P.copy('gpsimd', accb[:], accf)
            for bk in range(4):
                P.tr(pm_b[:, bk * 128:(bk + 1) * 128], accb[:, bk * 128:(bk + 1) * 128], identb[:])
            P.copy('vector', atT[:].rearrange("p b q -> p (b q)"), pm_b[:, 0:512], reads=[pm])
            P.dma('gpsimd', atsc[m], atT[:].rearrange("p b q -> p (b q)"))
            if G.get('dbg'):
                P.dma('gpsimd', G['dbg']['at'][m], atT[:].rearrange("p b q -> p (b q)"))
                P.dma('gpsimd', G['dbg']['ra'][m], rat2[:])


def emit_C(C, l, xsrc, xdst, W, G):
    P = C.P
    with C.phase("C%d" % l):
        Wo = C.sb("Wo", [128, 8, D], BF16)
        Wdn = C.sb("Wdn", [128, 32, D], BF16)
        wst = [C.sb("wst%d" % i, [128, 8, 512], F32) for i in range(2)]
        Wup = [C.sb("Wup%d" % i, [128, 8, 512], BF16) for i in range(2)]
        gout_s = C.sb("gout_s", [128, 8], F32)
        gffn_s = C.sb("gffn_s", [128, D], F32)
        cw_s = C.sb("cw_s", [128, 4, 3], F32)
        selw_s = C.sb("selw_s", [128, 4], F32)
        xt = [C.sb("xt%d" % i, [128, D], F32) for i in range(1)]
        x1 = [C.sb("x1_%d" % i, [128, D], F32) for i in range(4)]
        at_s = [C.sb("at_s%d" % i, [128, 4, 128], BF16) for i in range(2)]
        yv = [C.sb("yv%d" % i, [128, 4, 128], F32) for i in range(1)]
        bg2 = [C.sb("bg2_%d" % i, [128, 4, 2], F32) for i in range(2)]
        tl = [C.sb("tl%d" % i, [128, 4, 8], F32) for i in range(2)]
        halo = C.sb("halo", [128, 4, 2], F32)
        pt0 = C.sb("pt0", [128, 4], F32)
        pt1 = C.sb("pt1", [128, 4], F32)
        ysq = C.sb("ysq", [128, 4, 128], F32)
        ybf = C.sb("ybf", [128, 4, 128], BF16)
        rc = C.sb("rc", [128, 1], F32)
        rc2 = C.sb("rc2", [128, 1], F32)
        ra_s = [C.sb("ra_s%d" % i, [128, 1], F32) for i in range(2)]
        ss = C.sb("ss", [128, 1], F32)
        rstd = C.sb("rstd", [128, 1], F32)
        h2n = C.sb("h2n", [128, D], BF16)
        h2T = C.sb("h2T", [128, 8, 512], BF16)
        rl = [C.sb("rl%d" % i, [128, 512], F32) for i in range(1)]
        actT = C.sb("actT", [128, 32, 512], BF16)
        pa = [C.ps("pa%d" % i, [128, 512], F32) for i in range(2)]
        pcv = [C.ps("pcv%d" % i, [128, 512], F32) for i in range(2)]
        pu = [C.ps("pu%d" % i, [128, 512], F32) for i in range(2)]
        pT = C.ps("pT", [128, 8, 128], BF16)
        px = C.ps("px", [128, 512], F32)
        identb, onesf = G['identb'], G['onesf']
        gathu, ysc, bg2sc, atsc, rasc = G['gathu'], G['ysc'], G['bg2sc'], G['atsc'], G['rasc']

        with P.group('cst'):
            P.dma('sync', gout_s[:], W['gout'][l])
            P.dma('sync', gffn_s[:], W['gffn'][l])
            P.dma('sync', cw_s[:], W['cw'][l])
            P.dma('sync', selw_s[:], W['selw'])
        n = 0
        for kc in range(8):
            for hf in range(2):
                st = wst[n % 2]
                n += 1
                P.dma('sync' if hf == 0 else 'scalar', st[:, 0, :], W['w_o'][l, kc * 128:(kc + 1) * 128, hf * 512:(hf + 1) * 512], writes=[st])
                P.ts('vector' if hf == 0 else 'gpsimd', Wo[:, kc, hf * 512:(hf + 1) * 512], st[:, 0, :], gout_s[:, kc:kc + 1], ALU.mult, reads=[st, gout_s])
        for f4 in range(8):
            st = wst[n % 2]
            n += 1
            stv = AP_(st, 0, [[4096, 128], [1024, 4], [1, 1024]])
            P.dma('sync' if f4 % 2 == 0 else 'scalar', stv, W['w_dn'][l, f4 * 512:(f4 + 1) * 512, :].rearrange("(a p) n -> p a n", p=128), writes=[st])
            P.copy('vector' if f4 % 2 == 0 else 'gpsimd', Wdn[:, f4 * 4:(f4 + 1) * 4, :], stv, reads=[st])
        up_n = [n]

        for bt in range(B_NM // 4):
            for j in range(4):
                m = bt * 4 + j
                xb = xt[0]
                ab, rab, yb, bgb, tlb = at_s[m % 2], ra_s[m % 2], yv[0], bg2[m % 2], tl[m % 2]
                P.dma('sync', xb[:], xsrc[m])
                P.dma('scalar', ab[:], atsc[m].rearrange("p (b q) -> p b q", b=4))
                P.dma('sync', rab[:], rasc[m])
                P.dma('scalar', yb[:], ysc[m].rearrange("p (c t) -> p c t", c=4))
                P.dma('sync', bgb[:], bg2sc[m].rearrange("p (c e) -> p c e", e=2))
                if m == 0:
                    P.memset('gpsimd', tlb[:, 0, :], 0.0)
                with P.group("tl%d" % (m % 2)):
                    if m > 0:
                        P.dma('gpsimd', tlb[:, 0, :], gathu[3 * B_NM + m - 1].rearrange("(p e) -> p e", e=8), writes=[tlb])
                    for cd in range(1, 4):
                        P.dma('gpsimd', tlb[:, cd, :], gathu[(cd - 1) * B_NM + m].rearrange("(p e) -> p e", e=8), writes=[tlb])
                hf = halo[:].rearrange("p c e -> p (c e)")
                P.ts('vector', hf, tlb[:, 0, :], selw_s[:, 0:1], ALU.mult)
                for cd in range(1, 4):
                    P.stt(hf, tlb[:, cd, :], selw_s[:, cd:cd + 1], hf, ALU.mult, ALU.add)
                P.tt('vector', pt0[:], halo[:, :, 1], cw_s[:, :, 1], ALU.mult)
                P.tt('vector', pt1[:], halo[:, :, 0], cw_s[:, :, 0], ALU.mult)
                P.tt('vector', pt0[:], pt0[:], pt1[:], ALU.add)
                P.tt('vector', pt0[:], pt0[:], bgb[:, :, 0], ALU.mult)
                P.tt('vector', yb[:, :, 0], yb[:, :, 0], pt0[:], ALU.add)
                P.tt('vector', pt1[:], halo[:, :, 1], cw_s[:, :, 0], ALU.mult)
                P.tt('vector', pt1[:], pt1[:], bgb[:, :, 1], ALU.mult)
                P.tt('vector', yb[:, :, 1], yb[:, :, 1], pt1[:], ALU.add)
                P.act(ysq[:], yb[:], AF.Square)
                for ch in range(4):
                    P.mm(px[:, 0:1], ysq[:, ch, :], onesf[:], start=(ch == 0), stop=(ch == 3))
                P.act(rc[:], px[:, 0:1], AF.Sqrt, bias=EPS, scale=1.0 / 512)
                P.recip(rc2[:], rc[:])
                P.copy('gpsimd', ybf[:], yb[:])
                for nh in range(2):
                    for kc in range(4):
                        P.mm(pa[nh][:], ab[:, kc, :], Wo[:, kc, nh * 512:(nh + 1) * 512], start=(kc == 0), stop=(kc == 3))
                    for kc in range(4):
                        P.mm(pcv[nh][:], ybf[:, kc, :], Wo[:, 4 + kc, nh * 512:(nh + 1) * 512], start=(kc == 0), stop=(kc == 3))
                x1b = x1[j]
                for nh in range(2):
                    sl = slice(nh * 512, (nh + 1) * 512)
                    P.stt(x1b[:, sl], pa[nh][:], rab[:, 0:1], xb[:, sl], ALU.mult, ALU.add)
                    P.stt(x1b[:, sl], pcv[nh][:], rc2[:, 0:1], x1b[:, sl], ALU.mult, ALU.add)
                if G.get('dbg'):
                    P.dma('gpsimd', G['dbg']['x1'][m], x1b[:])
                    P.dma('gpsimd', G['dbg']['y'][m], ybf[:].rearrange("p c t -> p (c t)"))
                    P.dma('gpsimd', G['dbg']['rc'][m], rc2[:])
                P.act(h2n[:], x1b[:], AF.Square, accum_out=ss[:])
                P.act(rstd[:], ss[:], AF.Sqrt, bias=EPS, scale=1.0 / D)
                P.recip(rstd[:], rstd[:])
                P.stt(h2n[:], x1b[:], rstd[:, 0:1], gffn_s[:], ALU.mult, ALU.mult)
                for kc in range(8):
                    P.tr(pT[:, kc, :], h2n[:, kc * 128:(kc + 1) * 128], identb[:])
                P.copy('scalar', h2T[:, :, j * 128:(j + 1) * 128], pT[:])
            for u in range(8):
                st = wst[up_n[0] % 2]
                wb = Wup[up_n[0] % 2]
                up_n[0] += 1
                P.dma('sync' if u % 2 == 0 else 'scalar', st[:], W['w_up'][l, :, u * 512:(u + 1) * 512].rearrange("(kc p) n -> p kc n", p=128))
                P.copy('gpsimd' if u % 2 == 0 else 'vector', wb[:], st[:])
                for fl in range(4):
                    f = u * 4 + fl
                    pp = pu[f % 2]
                    for kc in range(8):
                        P.mm(pp[:], wb[:, kc, fl * 128:(fl + 1) * 128], h2T[:, kc, :], start=(kc == 0), stop=(kc == 7))
                    rb = rl[0]
                    P.act(rb[:], pp[:], AF.Relu)
                    P.tt('gpsimd' if f % 2 == 0 else 'vector', actT[:, f, :], rb[:], rb[:], ALU.mult)
            for j in range(4):
                m = bt * 4 + j
                ob = x1[j]
                for nh in range(2):
                    pp = pa[nh] if j % 2 == 0 else pcv[nh]
                    for f in range(32):
                        P.mm(pp[:], actT[:, f, j * 128:(j + 1) * 128], Wdn[:, f, nh * 512:(nh + 1) * 512], start=(f == 0), stop=(f == 31))
                    P.tt('vector', ob[:, nh * 512:(nh + 1) * 512], pp[:], x1[j][:, nh * 512:(nh + 1) * 512], ALU.add)
                P.dma('sync', xdst[m], ob[:])


def build_fused(nlayers=DEPTH, stages="AGZBC", dbg=False):
    C = Ctx()
    P = C.P
    W = {}
    W['x0'] = C.din("x0", [B_NM, 128, D], F32)
    W['w_in'] = C.din("w_in", [nlayers, D, INW], F32)
    W['gmix'] = C.din("gmix", [nlayers, 128, 8], F32)
    W['gq'] = C.din("gq", [nlayers, 128, 512], F32)
    W['gk3'] = C.din("gk3", [nlayers, 128, 3, 128], F32)
    W['cw'] = C.din("cw", [nlayers, 128, 4, 3], F32)
    W['peT'] = C.din("peT", [nlayers, 2, 64, 32], F32)
    W['w1'] = C.din("w1", [nlayers, 2, 2048, 256], F32)
    W['b1b'] = C.din("b1b", [nlayers, 2, 128, 256], F32)
    W['w2'] = C.din("w2", [nlayers, 2, 256, 64], F32)
    W['b2b'] = C.din("b2b", [nlayers, 128, 2, 64], F32)
    W['w_o'] = C.din("w_o", [nlayers, D, D], F32)
    W['gout'] = C.din("gout", [nlayers, 128, 8], F32)
    W['gffn'] = C.din("gffn", [nlayers, 128, D], F32)
    W['w_up'] = C.din("w_up", [nlayers, D, DFF], F32)
    W['w_dn'] = C.din("w_dn", [nlayers, DFF, D], F32)
    W['smask'] = C.din("smask", [128, 4, 512], BF16)
    W['cmask'] = C.din("cmask", [128, 5, 512], BF16)
    W['wmask'] = C.din("wmask", [128, 8, 128], BF16)
    W['Eall'] = C.din("Eall", [64, 32, 128], BF16)
    W['Bsel'] = C.din("Bsel", [128, 512], F32)
    W['kaug'] = C.din("kaug", [4, T], BF16)
    W['kcaug'] = C.din("kcaug", [4, 2, 1024], BF16)
    W['qaug'] = C.din("qaug", [B_NM, 4, 2, 512], BF16)
    W['vcxc'] = C.din("vcxc", [128, 8, 2, 257], BF16)
    W['selw'] = C.din("selw", [128, 4], F32)
    identb_d = C.din("identb", [128, 128], BF16)
    identf_d = C.din("identf", [128, 128], F32)
    onesf_d = C.din("onesf", [128, 1], F32)
    xo = C.dout("xo", [B_NM, 128, D], F32)

    G = {}
    COMPS = [('ks', 1024), ('kw', 1024), ('kc', 1024), ('vc', 1024), ('vs0', 520), ('vs1', 520), ('vw0', 520), ('vw1', 520)]
    G['pay'] = {cn: C.dint("pay_" + cn, [rows, 512], BF16) for cn, rows in COMPS}
    G['gath'] = {cn: C.dint("gath_" + cn, [4 * rows, 512], BF16) for cn, rows in COMPS}
    G['payu'] = C.dint("payu", [B_NM, 1024], F32).ap()
    G['gathu'] = C.dint("gathu", [4 * B_NM, 1024], F32).ap()
    xbuf = C.dint("xbuf", [B_NM, 128, D], F32).ap()
    G['qsc'] = C.dint("qsc", [B_NM, 2, 64, 512], BF16)
    G['gsc'] = C.dint("gsc", [B_NM, 128, 24], F32).ap()
    G['ysc'] = C.dint("ysc", [B_NM, 128, 512], F32).ap()
    G['bg2sc'] = C.dint("bg2sc", [B_NM, 128, 8], F32).ap()
    G['atsc'] = C.dint("atsc", [B_NM, 128, 512], BF16).ap()
    G['rasc'] = C.dint("rasc", [B_NM, 128, 1], F32).ap()
    qsc_ap = G['qsc'].ap()
    if dbg:
        G['dbg'] = {'at': C.dout("dbg_at", [B_NM, 128, 512], BF16), 'ra': C.dout("dbg_ra", [B_NM, 128, 1], F32),
                    'x1': C.dout("dbg_x1", [B_NM, 128, D], F32), 'y': C.dout("dbg_y", [B_NM, 128, 512], BF16),
                    'rc': C.dout("dbg_rc", [B_NM, 128, 1], F32),
                    'kc': C.dout("dbg_kc", [68, 2, 1024], BF16), 'vc': C.dout("dbg_vc", [128, 8, 2, 321], BF16),
                    'ks': C.dout("dbg_ks", [68, T], BF16), 'vs': C.dout("dbg_vs", [128, 128, 130], BF16)}

    G['identb'] = C.sb("identb_s", [128, 128], BF16)
    G['identf'] = C.sb("identf_s", [128, 128], F32)
    G['onesf'] = C.sb("onesf_s", [128, 1], F32)
    with P.group('cst'):
        P.dma('sync', G['identb'][:], identb_d)
        P.dma('sync', G['identf'][:], identf_d)
        P.dma('sync', G['onesf'][:], onesf_d)
    P.barrier()
    rg = [[0, 1, 2, 3], [4, 5, 6, 7]]
    def mk_cc(i_ap, o_ap):
        return lambda e: e.collective_compute("AllGather", ALU.bypass, replica_groups=rg, ins=[i_ap.opt()], outs=[o_ap.opt()])
    cc_fns = [mk_cc(G['pay'][cn].ap(), G['gath'][cn].ap()) for cn, _ in COMPS] + [mk_cc(G['payu'], G['gathu'])]
    for l in range(nlayers):
        xsrc = W['x0'] if l == 0 else xbuf
        xdst = xo if l == nlayers - 1 else xbuf
        if 'A' in stages:
            emit_A(C, l, xsrc, W, G)
        if 'G' in stages:
            P.collectives(cc_fns)
        with C.phase("ZB%d" % l):
            G['kcT_s'] = C.sb("kcT_s", [68, 2, 1024], BF16)
            G['vcx_s'] = C.sb("vcx_s", [128, 8, 2, 321], BF16)
            with P.group('cst2'):
                P.dma('sync', G['kcT_s'][64:68, :, :], W['kcaug'])
                P.dma('scalar', G['vcx_s'][:, :, :, 64:321], W['vcxc'])
            if 'Z' in stages:
                emit_Z(C, l, W, G)
            Gb = dict(G)
            Gb['qsc'] = qsc_ap
            if 'B' in stages:
                emit_B(C, l, W, Gb)
        if 'C' in stages:
            emit_C(C, l, xsrc, xdst, W, G)
    if 'C' not in stages:
        with C.phase("dbg"):
            xt = [C.sb("xt%d" % i, [128, D], F32) for i in range(2)]
            for m in range(B_NM):
                P.dma('sync', xt[m % 2][:], W['x0'][m])
                P.dma('sync', xo[m], xt[m % 2][:])
    return C.finish()


IDENTB = np.eye(128, dtype=np.float32).astype(NPBF)
ONESF = np.ones((128, 1), np.float32)


def _bf(a):
    return np.ascontiguousarray(a).astype(NPBF)


def _selmap():
    i = np.arange(1024)[:, None] * 16
    j = np.arange(256)[None, :] * 64
    sh = np.clip(np.minimum(i + 32, j + 64) - np.maximum(i, j), 0, None).astype(np.float32) / 32.0
    sh[1023] = 0.0
    return sh


def _pos_rows(pos):
    pos = np.maximum(pos, 0)
    return np.stack([np.ones_like(pos), np.ones_like(pos), pos % 128, pos // 128]).astype(np.float32)


def core_consts(s):
    ki = np.arange(128)[:, None]
    qi = np.arange(128)[None, :]
    tri_gt = np.where(ki > qi, NEGM, 0.0).astype(np.float32)
    tri_le = np.where(ki <= qi, NEGM, 0.0).astype(np.float32)
    full = np.full((128, 128), NEGM, np.float32)
    zero = np.zeros((128, 128), np.float32)
    smask = np.tile(np.stack([zero if d < s else (tri_gt if d == s else full) for d in range(4)], axis=1), (1, 1, 4))
    cm = []
    for v in range(5):
        ip = ki - 32 * v
        vis = (16 * ip + 31) <= (128 * s + qi)
        cm.append(np.where(vis, 0.0, NEGM).astype(np.float32))
    cmask = np.tile(np.stack(cm, axis=1), (1, 1, 4))
    wm = []
    for d in range(8):
        off = s + 4 - d
        if off == 0:
            wm.append((ki <= qi).astype(np.float32))
        elif off == 4:
            wm.append((ki > qi).astype(np.float32))
        elif 0 < off < 4:
            wm.append(np.ones((128, 128), np.float32))
        else:
            wm.append(zero)
    wmask = np.stack(wm, axis=1)
    E = np.zeros((64, 32, 128), np.float32)
    for kt in range(32):
        for k in range(128):
            E[2 * kt + k // 64, kt, k] = 1.0
    r = np.arange(512)[None, :] - 256 - 2 * s
    qq = np.arange(128)[:, None]
    Brel = np.zeros((128, 512), np.float32)
    Brel = np.where(r >= 2, np.float32(-1e30), Brel)
    Brel = np.where(r == 1, np.where(qq >= 64, np.float32(1e4), np.float32(-1e30)), Brel)
    Brel = np.where(r == 0, np.float32(1e4), Brel)
    Brel = np.where((r == -1) & (qq < 64), np.float32(1e4), Brel)
    rr, mm, kk = np.meshgrid(np.arange(4), np.arange(32), np.arange(128), indexing='ij')
    kpos = (128 * (4 * mm + rr) + kk).reshape(-1)
    kaug = _pos_rows(kpos)
    kc = _pos_rows(np.arange(1024) * 16 + 31)
    kcaug = np.stack([kc, kc], axis=1)
    qaug = np.zeros((B_NM, 4, 2, 512), np.float32)
    qv = np.arange(128)
    for m in range(B_NM):
        c = 4 * m + s
        for g in range(2):
            for rh in range(4):
                sl = 2.0 ** (-(4 * g + rh + 1))
                cs = slice(rh * 128, (rh + 1) * 128)
                qaug[m, 0, g, cs] = -sl * qv
                qaug[m, 1, g, cs] = -sl * 128.0 * c
                qaug[m, 2, g, cs] = sl
                qaug[m, 3, g, cs] = sl * 128.0
    sm = _selmap().reshape(8, 128, 256).transpose(1, 0, 2)
    vcxc = np.zeros((128, 8, 2, 257), np.float32)
    vcxc[:, :, :, 0] = 1.0
    vcxc[127, 7, :, 0] = 0.0
    vcxc[:, :, :, 1:257] = sm[:, :, None, :]
    selw = np.zeros((128, 4), np.float32)
    selw[:, s] = 1.0
    return {'smask': _bf(smask), 'cmask': _bf(cmask), 'wmask': _bf(wmask), 'Eall': _bf(E),
            'Bsel': np.ascontiguousarray(Brel.astype(np.float32)), 'kaug': _bf(kaug), 'kcaug': _bf(kcaug),
            'qaug': _bf(qaug), 'vcxc': _bf(vcxc), 'selw': selw,
            'identb': IDENTB, 'identf': np.eye(128, dtype=np.float32), 'onesf': ONESF}


def prep_fused(p, L=DEPTH):
    p = {k: (v if k == 'x' else v[:L]) for k, v in p.items()}
    common = {
        'w_in': np.ascontiguousarray(p['w_in'], dtype=np.float32),
        'gmix': np.ascontiguousarray(p['g_mix_norm'].reshape(L, 8, 128).transpose(0, 2, 1)),
        'gq': np.ascontiguousarray(np.tile(p['g_q'][:, None, :], (1, 128, 8))),
        'gk3': np.ascontiguousarray(np.broadcast_to(np.tile(p['g_k'], (1, 1, 2))[:, None], (L, 128, 3, 128))),
        'cw': np.ascontiguousarray(p['conv_w'].reshape(L, 3, 4, 128).transpose(0, 3, 2, 1)),
        'peT': np.ascontiguousarray(p['pe_cmp'].transpose(0, 1, 3, 2)),
        'w1': np.ascontiguousarray(p['w_cmp1'], dtype=np.float32),
        'b1b': np.ascontiguousarray(np.broadcast_to(p['b_cmp1'][:, :, None, :], (L, 2, 128, 256))),
        'w2': np.ascontiguousarray(p['w_cmp2'], dtype=np.float32),
        'b2b': np.ascontiguousarray(np.broadcast_to(p['b_cmp2'][:, None], (L, 128, 2, 64))),
        'w_o': np.ascontiguousarray(p['w_o'], dtype=np.float32),
        'gout': np.ascontiguousarray(p['g_out'].reshape(L, 8, 128).transpose(0, 2, 1)),
        'gffn': np.ascontiguousarray(np.tile(p['g_ffn_norm'][:, None, :], (1, 128, 1))),
        'w_up': np.ascontiguousarray(p['w_up'], dtype=np.float32),
        'w_dn': np.ascontiguousarray(p['w_down'], dtype=np.float32),
    }
    maps = []
    x = np.ascontiguousarray(p['x'], dtype=np.float32)
    for b in range(NB):
        xv = x[b].reshape(T // 128, 128, D)
        for s in range(4):
            mp = dict(common)
            mp.update(core_consts(s))
            mp['x0'] = np.ascontiguousarray(xv[np.arange(B_NM) * 4 + s])
            maps.append(mp)
    return maps


_PROG = {}


def run_fused(inputs, nlayers=DEPTH):
    p = {k: np.asarray(v) for k, v in inputs.items()}
    if nlayers not in _PROG:
        _PROG[nlayers] = build_fused(nlayers)
    res = run_bass_kernel_spmd(_PROG[nlayers], prep_fused(p, nlayers), core_ids=list(range(NCORES)))
    out = np.empty((NB, T, D), np.float32)
    for b in range(NB):
        ov = out[b].reshape(T // 128, 128, D)
        for s in range(4):
            ov[np.arange(B_NM) * 4 + s] = np.asarray(res.results[4 * b + s]['xo'])
    return out


def kernel(**inputs):
    return run_fused(inputs, DEPTH)
```

```python
import contextlib
import numpy as np
import ml_dtypes
import concourse.bass as bass
import concourse.mybir as mybir
from concourse.bass_utils import run_bass_kernel_spmd

F32 = mybir.dt.float32
BF16 = mybir.dt.bfloat16
AF = mybir.ActivationFunctionType
ALU = mybir.AluOpType
AX = mybir.AxisListType
NPBF = ml_dtypes.bfloat16

ENGS = ['tensor', 'vector', 'scalar', 'gpsimd', 'sync']
NCORES = 8
D = 1024
T = 16384
NB = 2
DEPTH = 4
INW = 2840
EPS = 1e-6
NEGM = -30000.0


def key_of(ap):
    t = getattr(ap, 'tensor', ap)
    return t.name.split('@')[0]


class Prog:
    def __init__(self, nc):
        self.nc = nc
        self.q = {e: [] for e in ENGS}
        self.semcount = {}
        self.res = {}
        self.seen = {e: {} for e in ENGS}

    grp = None

    @contextlib.contextmanager
    def group(self, sem):
        s = 'd_' + sem
        self.grp = {'sem': sem, 's': s, 'start': self.semcount.get(s, 0), 'keys': set(), 'pre': {}}
        try:
            yield
        finally:
            g, self.grp = self.grp, None
            v = self.semcount.get(s, 0)
            for k in g['keys']:
                self.res[k]['w'] = (s, v)

    def _need(self, eng, reads, writes):
        need = {}

        def add(tok):
            if tok is None:
                return
            s, v = tok
            if eng == 'tensor' and s == 'e_tensor':
                return
            if self.grp is not None and s == self.grp['s'] and v > self.grp['start']:
                return
            if need.get(s, 0) < v:
                need[s] = v
        for k in reads:
            st = self.res.get(k)
            if st:
                add(st['w'])
        for k in writes:
            sts = [self.res.get(k)]
            if self.grp is not None:
                if k not in self.grp['pre']:
                    st0 = self.res.get(k)
                    self.grp['pre'][k] = {'w': st0['w'], 'r': dict(st0['r'])} if st0 else None
                sts.append(self.grp['pre'][k])
            for st in sts:
                if st:
                    add(st['w'])
                    for s, v in st['r'].items():
                        add((s, v))
        for s, v in need.items():
            if self.seen[eng].get(s, 0) < v:
                self.q[eng].append(('wait', s, v))
                self.seen[eng][s] = v

    def _update(self, reads, writes, tok):
        s, v = tok
        for k in reads:
            st = self.res.setdefault(k, {'w': None, 'r': {}})
            if st['r'].get(s, 0) < v:
                st['r'][s] = v
        for k in writes:
            self.res[k] = {'w': tok, 'r': {}}

    def op(self, eng, fn, reads=(), writes=()):
        reads = [r if isinstance(r, str) else key_of(r) for r in reads]
        writes = [r if isinstance(r, str) else key_of(r) for r in writes]
        self._need(eng, reads, writes)
        s = 'e_' + eng
        v = self.semcount.get(s, 0) + 1
        self.semcount[s] = v
        self.q[eng].append(('op', fn, s, 1))
        self._update(reads, writes, (s, v))

    def dma(self, eng, out, in_, reads=None, writes=None, sem=None, **kw):
        if reads is None:
            reads = [] if in_.tensor.name in self.dram_names else [in_]
        if writes is None:
            writes = [] if out.tensor.name in self.dram_names else [out]
        assert reads or writes or sem
        reads = [r if isinstance(r, str) else key_of(r) for r in reads]
        writes = [r if isinstance(r, str) else key_of(r) for r in writes]
        if self.grp is not None:
            sem = self.grp['sem']
            self.grp['keys'].update(writes)
        if sem is None:
            sem = writes[0] if writes else 'st_' + reads[0]
        self._need(eng, reads, writes)
        s = 'd_' + sem
        v = self.semcount.get(s, 0) + 16
        self.semcount[s] = v
        self.q[eng].append(('op', lambda e: e.dma_start(out=out, in_=in_, **kw), s, 16))
        self._update(reads, writes, (s, v))

    dram_names = set()

    def mm(self, out, lhsT, rhs, start=True, stop=True, reads=None, writes=None):
        self.op('tensor', lambda e: e.matmul(out, lhsT=lhsT, rhs=rhs, start=start, stop=stop),
                reads=reads if reads is not None else [lhsT, rhs],
                writes=writes if writes is not None else [out])

    def tr(self, out, in_, ident, reads=None, writes=None):
        self.op('tensor', lambda e: e.transpose(out, in_, ident),
                reads=reads if reads is not None else [in_, ident],
                writes=writes if writes is not None else [out])

    def act(self, out, in_, func, bias=None, scale=None, accum_out=None, reads=None, writes=None, eng='scalar'):
        kw = {}
        rd = [in_]
        if bias is not None:
            kw['bias'] = bias
            if not isinstance(bias, (int, float)):
                rd.append(bias)
        if scale is not None:
            kw['scale'] = scale
            if not isinstance(scale, (int, float)):
                rd.append(scale)
        wr = [out]
        if accum_out is not None:
            kw['accum_out'] = accum_out
            wr.append(accum_out)
        self.op('scalar', lambda e: e.activation(out=out, in_=in_, func=func, **kw),
                reads=reads if reads is not None else rd,
                writes=writes if writes is not None else wr)

    def tt(self, eng, out, in0, in1, op, reads=None, writes=None):
        self.op(eng, lambda e: e.tensor_tensor(out=out, in0=in0, in1=in1, op=op),
                reads=reads if reads is not None else [in0, in1],
                writes=writes if writes is not None else [out])

    def ts(self, eng, out, in0, s1, op0, s2=None, op1=None, reads=None, writes=None):
        rd = [in0]
        if not isinstance(s1, (int, float)):
            rd.append(s1)
        if s2 is not None and not isinstance(s2, (int, float)):
            rd.append(s2)
        if op1 is None:
            fn = lambda e: e.tensor_scalar(out=out, in0=in0, scalar1=s1, scalar2=None, op0=op0)
        else:
            fn = lambda e: e.tensor_scalar(out=out, in0=in0, scalar1=s1, scalar2=s2, op0=op0, op1=op1)
        self.op(eng, fn, reads=reads if reads is not None else rd,
                writes=writes if writes is not None else [out])

    def stt(self, out, in0, scalar, in1, op0, op1, reads=None, writes=None):
        rd = [in0, in1]
        if not isinstance(scalar, (int, float)):
            rd.append(scalar)
        self.op('vector', lambda e: e.scalar_tensor_tensor(out=out, in0=in0, scalar=scalar, in1=in1, op0=op0, op1=op1),
                reads=reads if reads is not None else rd,
                writes=writes if writes is not None else [out])

    def copy(self, eng, out, in_, reads=None, writes=None):
        if eng == 'scalar':
            fn = lambda e: e.activation(out=out, in_=in_, func=AF.Copy)
        else:
            fn = lambda e: e.tensor_copy(out=out, in_=in_)
        self.op(eng, fn, reads=reads if reads is not None else [in_],
                writes=writes if writes is not None else [out])

    def recip(self, out, in_):
        self.op('vector', lambda e: e.reciprocal(out=out, in_=in_), reads=[in_], writes=[out])

    def reduce(self, out, in_, op=ALU.add, axis=AX.X):
        self.op('vector', lambda e: e.tensor_reduce(out=out, in_=in_, axis=axis, op=op), reads=[in_], writes=[out])

    def memset(self, eng, ap, val):
        self.op(eng, lambda e: e.memset(ap, val), writes=[ap])

    def barrier(self):
        for e in ENGS:
            for s, v in self.semcount.items():
                if self.seen[e].get(s, 0) < v:
                    self.q[e].append(('wait', s, v))
                    self.seen[e][s] = v
        self.res = {}

    def collectives(self, fns):
        self.barrier()
        s = 'c_cc'
        for fn in fns:
            v = self.semcount.get(s, 0) + 1
            self.semcount[s] = v
            self.q['gpsimd'].append(('op', fn, s, 1))
        self.barrier()

    def emit(self):
        nc = self.nc
        for s, v in self.semcount.items():
            if self.seen['sync'].get(s, 0) < v:
                self.q['sync'].append(('wait', s, v))
        with contextlib.ExitStack() as es:
            semh = {s: es.enter_context(nc.semaphore(s)) for s in self.semcount}
            block = es.enter_context(nc.Block())

            def mk(engname):
                def body(e):
                    for it in self.q[engname]:
                        if it[0] == 'wait':
                            e.wait_ge(semh[it[1]], it[2])
                        else:
                            ins = it[1](e)
                            ins.then_inc(semh[it[2]], it[3])
                return body
            for engname in ENGS:
                if self.q[engname]:
                    getattr(block, engname)(mk(engname))


def AP_(t, offset, dims):
    return bass.AP(t, offset, [list(d) for d in dims])


class Ctx:
    def __init__(self):
        self.nc = bass.Bass("TRN2", target_bir_lowering=False)
        self.es = contextlib.ExitStack()
        self.P = Prog(self.nc)
        self.P.dram_names = set()

    def din(self, name, shape, dt):
        self.P.dram_names.add(name)
        return self.nc.dram_tensor(name, list(shape), dt, kind="ExternalInput").ap()

    def dout(self, name, shape, dt):
        self.P.dram_names.add(name)
        return self.nc.dram_tensor(name, list(shape), dt, kind="ExternalOutput").ap()

    tag = None

    def dint(self, name, shape, dt):
        self.P.dram_names.add(name)
        return self.nc.dram_tensor(name, list(shape), dt)

    def sb(self, name, shape, dt):
        if self.tag:
            name = name + '@' + self.tag
        return self.es.enter_context(self.nc.sbuf_tensor(name, list(shape), dt))

    def ps(self, name, shape, dt):
        if self.tag:
            name = name + '@' + self.tag
        return self.es.enter_context(self.nc.psum_tensor(name, list(shape), dt))

    @contextlib.contextmanager
    def phase(self, tag):
        old, oldtag = self.es, self.tag
        self.es, self.tag = contextlib.ExitStack(), tag
        try:
            yield
        finally:
            self.P.barrier()
            self.es.close()
            self.es, self.tag = old, oldtag

    def finish(self):
        self.P.emit()
        self.es.close()
        return self.nc


B_NM = 32
DFF = 4096
PAYR = 6176
KS_R0, KW_R0, KC_R0, VC_R0, VS_R0, VW_R0 = 0, 1024, 2048, 3072, 4096, 5136
TP = T + 128


def emit_A(C, l, xsrc, W, G):
    P = C.P
    with C.phase("A%d" % l):
        Wbf = C.sb("Wbf", [128, 8, INW], BF16)
        wst = [C.sb("wst%d" % i, [128, INW // 2], F32) for i in range(2)]
        gmix_s = C.sb("gmix_s", [128, 8], F32)
        gq_s = C.sb("gq_s", [128, 512], F32)
        gk3_s = C.sb("gk3_s", [128, 3, 128], F32)
        cw_s = C.sb("cw_s", [128, 4, 3], F32)
        xt = [C.sb("xt%d" % i, [128, D], F32) for i in range(2)]
        junk = C.sb("junk", [128, D], BF16)
        ss = C.sb("ss", [128, 1], F32)
        rstd = C.sb("rstd", [128, 1], F32)
        hn = C.sb("hn", [128, D], BF16)
        hT = C.sb("hT", [128, 8, 128], BF16)
        sq = C.sb("sq", [128, 1280], F32)
        ssg = C.sb("ssg", [128, 20], F32)
        rg = C.sb("rg", [128, 20], F32)
        qtmp = C.sb("qtmp", [128, 512], F32)
        ktmp = C.sb("ktmp", [128, 256], F32)
        nrm = C.sb("nrm", [128, 1024], BF16)
        trs = C.sb("trs", [128, 8, 128], BF16)
        vsw = C.sb("vsw", [128, 2, 130], BF16)
        gts = C.sb("gts", [128, 24], F32)
        hcs = C.sb("hcs", [128, 4, 128], F32)
        ub = C.sb("ub", [128, 4, 130], F32)
        cv0 = C.sb("cv0", [128, 4, 128], F32)
        cv1 = C.sb("cv1", [128, 4, 128], F32)
        yv = C.sb("yv", [128, 4, 128], F32)
        ut = C.sb("ut", [128, 4, 2], F32)
        bg2 = C.sb("bg2", [128, 4, 2], F32)
        pT = C.ps("pT", [128, 8, 128], BF16)
        pz = [C.ps("pz%d" % i, [128, 512], F32) for i in range(3)]
        pc = [C.ps("pc%d" % i, [128, 4, 128], F32) for i in range(3)]
        identb = G['identb']

        with P.group('cst'):
            P.dma('sync', gmix_s[:], W['gmix'][l])
            P.dma('sync', gq_s[:], W['gq'][l])
            P.dma('sync', gk3_s[:], W['gk3'][l])
            P.dma('sync', cw_s[:], W['cw'][l])
        P.memset('gpsimd', vsw[:], 1.0)
        P.memset('gpsimd', ub[:], 0.0)
        for kc in range(8):
            for hf in range(2):
                st = wst[hf]
                P.dma('sync' if hf == 0 else 'scalar', st[:], W['w_in'][l, kc * 128:(kc + 1) * 128, hf * 1420:(hf + 1) * 1420])
                P.ts('vector' if hf == 0 else 'gpsimd', Wbf[:, kc, hf * 1420:(hf + 1) * 1420], st[:], gmix_s[:, kc:kc + 1], ALU.mult)

        pay, payu, qsc, gsc, ysc, bg2sc = G['pay'], G['payu'], G['qsc'], G['gsc'], G['ysc'], G['bg2sc']
        for m in range(B_NM):
            xb = xt[m % 2]
            P.dma('sync', xb[:], xsrc[m])
            P.act(junk[:], xb[:], AF.Square, accum_out=ss[:])
            P.act(rstd[:], ss[:], AF.Sqrt, bias=EPS, scale=1.0 / D)
            P.recip(rstd[:], rstd[:])
            P.act(hn[:], xb[:], AF.Copy, scale=rstd[:, 0:1])
            for kc in range(8):
                P.tr(pT[:, kc, :], hn[:, kc * 128:(kc + 1) * 128], identb[:])
            P.copy('vector', hT[:], pT[:])
            for bi, (c0, c1) in enumerate([(0, 512), (512, 1024), (1024, 1304)]):
                for kc in range(8):
                    P.mm(pz[bi][:, 0:c1 - c0], hT[:, kc, :], Wbf[:, kc, c0:c1], start=(kc == 0), stop=(kc == 7))
            for ch in range(12):
                for kc in range(8):
                    P.mm(pc[ch // 4][:, ch % 4, :], Wbf[:, kc, 1304 + ch * 128:1304 + (ch + 1) * 128], hT[:, kc, :],
                         start=(kc == 0), stop=(kc == 7))
            P.act(sq[:, 0:512], pz[0][:], AF.Square)
            P.act(sq[:, 512:1024], pz[1][:], AF.Square)
            P.act(sq[:, 1024:1280], pz[2][:, 0:256], AF.Square)
            P.reduce(ssg[:], sq[:].rearrange("p (g d) -> p g d", d=64))
            P.act(rg[:, 0:8], ssg[:, 0:8], AF.Sqrt, bias=64 * EPS, scale=1.0)
            P.act(rg[:, 8:20], ssg[:, 8:20], AF.Sqrt, bias=EPS, scale=1.0 / 64)
            P.recip(rg[:], rg[:])
            P.tt('vector', qtmp[:].rearrange("p (g d) -> p g d", d=64), pz[0][:].rearrange("p (g d) -> p g d", d=64),
                 AP_(rg, 0, [[20, 128], [1, 8], [0, 64]]), ALU.mult)
            P.tt('gpsimd', nrm[:, 0:512], qtmp[:], gq_s[:], ALU.mult)
            P.tt('vector', ktmp[:, 0:128].rearrange("p (g d) -> p g d", d=64), pz[1][:, 256:384].rearrange("p (g d) -> p g d", d=64),
                 AP_(rg, 12, [[20, 128], [1, 2], [0, 64]]), ALU.mult)
            P.tt('vector', ktmp[:, 128:256].rearrange("p (g d) -> p g d", d=64), pz[2][:, 0:128].rearrange("p (g d) -> p g d", d=64),
                 AP_(rg, 16, [[20, 128], [1, 2], [0, 64]]), ALU.mult)
            P.tt('gpsimd', nrm[:, 512:768], ktmp[:], gk3_s[:, 1:3, :].rearrange("p a d -> p (a d)"), ALU.mult)
            P.copy('vector', nrm[:, 768:1024], pz[1][:, 0:256])
            P.copy('scalar', AP_(vsw, 1, [[260, 128], [65, 2], [1, 64]]), pz[1][:, 384:512].rearrange("p (g d) -> p g d", d=64))
            P.copy('scalar', AP_(vsw, 131, [[260, 128], [65, 2], [1, 64]]), pz[2][:, 128:256].rearrange("p (g d) -> p g d", d=64))
            P.dma('scalar', AP_(pay['vs%d' % (m // 16)], (m % 16) * 128 * 130, [[130, 128], [1, 130]]), vsw[:, 0, :])
            P.dma('scalar', AP_(pay['vw%d' % (m // 16)], (m % 16) * 128 * 130, [[130, 128], [1, 130]]), vsw[:, 1, :])
            P.act(gts[:], pz[2][:, 256:280], AF.Sigmoid)
            P.dma('scalar', gsc[m], gts[:])
            for bk in range(8):
                P.tr(pT[:, bk, :], nrm[:, bk * 128:(bk + 1) * 128], identb[:])
            P.copy('vector', trs[:], pT[:])
            for e in range(2):
                for g in range(2):
                    P.dma('sync' if g == 0 else 'scalar', AP_(qsc, m * 65536 + g * 32768 + e * 128, [[512, 64], [256, 2], [1, 128]]),
                          trs[e * 64:(e + 1) * 64, 2 * g:2 * g + 2, :])
            for ci, cn in enumerate(('ks', 'kw', 'kc', 'vc')):
                P.dma('sync' if ci % 2 == 0 else 'scalar', AP_(pay[cn], m * 128, [[4096, 128], [1, 128]]), trs[:, 4 + ci, :])
            P.copy('scalar', hcs[:], pc[0][:])
            P.tt('vector', ub[:, :, 2:130], pc[1][:], hcs[:], ALU.mult)
            P.tt('gpsimd', cv0[:], ub[:, :, 2:130], AP_(cw_s, 2, [[12, 128], [3, 4], [0, 128]]), ALU.mult)
            P.tt('gpsimd', cv1[:], ub[:, :, 1:129], AP_(cw_s, 1, [[12, 128], [3, 4], [0, 128]]), ALU.mult)
            P.tt('gpsimd', cv0[:], cv0[:], cv1[:], ALU.add)
            P.tt('gpsimd', cv1[:], ub[:, :, 0:128], AP_(cw_s, 0, [[12, 128], [3, 4], [0, 128]]), ALU.mult)
            P.tt('gpsimd', cv0[:], cv0[:], cv1[:], ALU.add)
            P.tt('vector', yv[:], pc[2][:], cv0[:], ALU.mult)
            P.dma('scalar', ysc[m].rearrange("p (c t) -> p c t", c=4), yv[:])
            P.copy('gpsimd', ut[:], ub[:, :, 128:130])
            P.dma('sync', payu[m].rearrange("(p e) -> p e", e=8), ut[:].rearrange("p c e -> p (c e)"))
            P.copy('vector', bg2[:], pc[2][:, :, 0:2])
            P.dma('sync', bg2sc[m], bg2[:].rearrange("p c e -> p (c e)"))


def emit_Z(C, l, W, G):
    P = C.P
    with C.phase("Z%d" % l):
        kcT_all = C.sb("kcT_all", [128, TP], BF16)
        vcT_all = C.sb("vcT_all", [128, TP], BF16)
        W1bf = C.sb("W1bf", [128, 2, 32, 256], BF16)
        w1st = [C.sb("w1st%d" % i, [128, 8, 256], F32) for i in range(2)]
        W2bf = C.sb("W2bf", [128, 2, 2, 64], BF16)
        w2st = C.sb("w2st", [128, 2, 2, 64], F32)
        peT_s = C.sb("peT_s", [64, 2, 32], F32)
        pebf = C.sb("pebf", [64, 2, 32, 128], BF16)
        b1b_s = C.sb("b1b_s", [128, 2, 256], F32)
        b2b_s = C.sb("b2b_s", [128, 2, 64], F32)
        gkc = C.sb("gkc", [128, 128], F32)
        c1b = C.sb("c1b", [128, 2, 256], F32)
        hid = C.sb("hid", [128, 256], F32)
        g_x2 = C.sb("g_x2", [128, 256], F32)
        g_in = C.sb("g_in", [128, 256], F32)
        g_sg = C.sb("g_sg", [128, 256], F32)
        hbf = C.sb("hbf", [128, 256], BF16)
        hidT = C.sb("hidT", [128, 2, 128], BF16)
        co = C.sb("co", [128, 2, 2, 64], F32)
        cosq = C.sb("cosq", [128, 128], F32)
        css = C.sb("css", [128, 2], F32)
        crs = C.sb("crs", [128, 2], F32)
        kcn = C.sb("kcn", [128, 128], BF16)
        pT = C.ps("pT", [128, 8, 128], BF16)
        px = [C.ps("px%d" % i, [128, 512], F32) for i in range(2)]
        po = C.ps("po", [128, 512], F32)
        identb = G['identb']
        gath = G['gath']
        kcT_s, vcx_s = G['kcT_s'], G['vcx_s']

        with P.group('cst'):
            P.dma('sync', b1b_s[:], W['b1b'][l].rearrange("k p n -> p k n"))
            P.dma('sync', b2b_s[:], W['b2b'][l])
            P.dma('sync', peT_s[:], W['peT'][l].rearrange("k d j -> d k j"))
            P.dma('sync', w2st[:], W['w2'][l].rearrange("k (c p) d -> p k c d", p=128))
            P.dma('sync', gkc[:], W['gk3'][l, :, 0, :])
        P.memset('gpsimd', kcT_all[:, T:TP], 0.0)
        P.memset('gpsimd', vcT_all[:, T:TP], 0.0)
        with P.group('kvcl'):
            for r in range(4):
                for kv, dst in enumerate((kcT_all, vcT_all)):
                    P.dma('sync' if kv == 0 else 'scalar', AP_(dst, r * 128, [[TP, 128], [512, 32], [1, 128]]),
                          AP_(gath['kc' if kv == 0 else 'vc'], r * 1024 * 512, [[4096, 128], [128, 32], [1, 128]]),
                          writes=[dst])
        P.copy('vector', W2bf[:], w2st[:])
        P.copy('vector', pebf[:], AP_(peT_s, 0, [[64, 64], [32, 2], [1, 32], [0, 128]]))
        n = 0
        for kv in range(2):
            for jq in range(4):
                st = w1st[n % 2]
                n += 1
                src = W['w1'][l, kv, jq * 512:(jq + 1) * 512, :].rearrange("(j d) n -> d j n", d=64)
                with P.group("w1st%d" % ((n - 1) % 2)):
                    P.dma('sync', st[0:64, :, :], src, writes=[st])
                    P.dma('scalar', st[64:128, :, :], src, writes=[st])
                P.copy('vector' if jq % 2 == 0 else 'gpsimd', W1bf[:, kv, jq * 8:(jq + 1) * 8, :], st[:])
        for kv in range(2):
            for j in range(32):
                P.mm(px[0][:, 0:256], pebf[:, kv, j, :], W1bf[0:64, kv, j, :], start=(j == 0), stop=(j == 31))
            P.tt('vector', c1b[:, kv, :], px[0][:, 0:256], b1b_s[:, kv, :], ALU.add)
        n = 0
        for ib in range(8):
            for kv in range(2):
                src_all = kcT_all if kv == 0 else vcT_all
                for g in range(2):
                    pp = px[n % 2]
                    n += 1
                    base = 16 * 128 * ib
                    for j in range(32):
                        lhs = AP_(src_all, 64 * g * TP + base + j, [[TP, 64], [16, 128]])
                        P.mm(pp[:, 0:256], lhs, W1bf[64 * g:64 * g + 64, kv, j, :], start=(j == 0), stop=(j == 31),
                             reads=[src_all, W1bf])
                    P.tt('vector', hid[:], pp[:, 0:256], c1b[:, kv, :], ALU.add)
                    P.act(g_x2[:], hid[:], AF.Square)
                    P.ts('gpsimd', g_x2[:], g_x2[:], 0.044715, ALU.mult, 1.0, ALU.add)
                    P.tt('gpsimd', g_in[:], g_x2[:], hid[:], ALU.mult)
                    P.act(g_sg[:], g_in[:], AF.Sigmoid, scale=1.5957691216057308)
                    P.tt('vector', hbf[:], hid[:], g_sg[:], ALU.mult)
                    for c in range(2):
                        P.tr(pT[:, c, :], hbf[:, c * 128:(c + 1) * 128], identb[:])
                    P.copy('vector', hidT[:], pT[:, 0:2, :])
                    for c in range(2):
                        P.mm(po[:, 0:64], hidT[:, c, :], W2bf[:, kv, c, :], start=(c == 0), stop=(c == 1))
                    P.tt('vector', co[:, kv, g, :], po[:, 0:64], b2b_s[:, kv, :], ALU.add)
            P.act(cosq[:], co[:, 0, :, :].rearrange("p g d -> p (g d)"), AF.Square)
            P.reduce(css[:], cosq[:].rearrange("p (g d) -> p g d", d=64))
            P.act(crs[:], css[:], AF.Sqrt, bias=EPS, scale=1.0 / 64)
            P.recip(crs[:], crs[:])
            P.tt('vector', cosq[:].rearrange("p (g d) -> p g d", d=64), co[:, 0, :, :], AP_(crs, 0, [[2, 128], [1, 2], [0, 64]]), ALU.mult)
            P.tt('vector', kcn[:], cosq[:], gkc[:], ALU.mult)
            for g in range(2):
                P.tr(pT[0:64, 4 + g, :], kcn[:, 64 * g:64 * g + 64], identb[:])
            P.copy('vector', kcT_s[0:64, :, ib * 128:(ib + 1) * 128], pT[0:64, 4:6, :])
            P.copy('gpsimd', vcx_s[:, ib, :, 0:64], co[:, 1, :, :])
        if G.get('dbg'):
            P.dma('sync', G['dbg']['kc'], kcT_s[:])
            P.dma('sync', G['dbg']['vc'], vcx_s[:])


def emit_B(C, l, W, G):
    P = C.P
    with C.phase("B%d" % l):
        ksT_s = [C.sb("ksT_s%d" % g, [68, T], BF16) for g in range(2)]
        vsx_s = C.sb("vsx_s", [128, 128, 130], BF16)
        smask_s = C.sb("smask_s", [128, 4, 512], BF16)
        cmask_s = C.sb("cmask_s", [128, 5, 512], BF16)
        wmask_s = C.sb("wmask_s", [128, 8, 128], BF16)
        Eall_s = C.sb("Eall_s", [64, 32, 128], BF16)
        Bsel_s = C.sb("Bsel_s", [128, 512], F32)
        qT_s = [C.sb("qT_s%d" % i, [68, 2, 512], BF16) for i in range(2)]
        kwT_s = [C.sb("kwT_s%d" % i, [68, 2, 1024], BF16) for i in range(2)]
        vwx_s = [C.sb("vwx_s%d" % i, [128, 8, 130], BF16) for i in range(2)]
        gates_s = [C.sb("gates_s%d" % i, [128, 24], F32) for i in range(2)]
        NPB = 6
        Pb = [C.sb("Pb%d" % i, [128, 512], BF16) for i in range(NPB)]
        Pc = [C.sb("Pc%d" % g, [128, 8, 512], BF16) for g in range(2)]
        oc = C.sb("oc", [128, 4, 321], F32)
        den4 = C.sb("den4", [128, 4], F32)
        rd4 = C.sb("rd4", [128, 4], F32)
        coef4 = C.sb("coef4", [128, 4], F32)
        imp = [C.sb("imp%d" % g, [128, 256], F32) for g in range(2)]
        tmpi = C.sb("tmpi", [128, 256], F32)
        m8 = C.sb("m8", [128, 16], F32)
        thr = C.sb("thr", [128, 1], F32)
        nm = C.sb("nm", [128, 256], BF16)
        nmT = [C.sb("nmT%d" % g, [64, 4, 128], BF16) for g in range(2)]
        osT = C.sb("osT", [65, 512], F32)
        otmp = C.sb("otmp", [128, 4, 64], F32)
        acc = C.sb("acc", [128, 8, 64], F32)
        accb = C.sb("accb", [128, 512], BF16)
        ssq = C.sb("ssq", [128, 1], F32)
        rat = C.sb("rat", [128, 1], F32)
        rat2 = C.sb("rat2", [128, 1], F32)
        atT = C.sb("atT", [128, 4, 128], BF16)
        ps_s = [C.ps("ps_s%d" % i, [128, 512], F32) for i in range(4)]
        ps_o = [C.ps("ps_o%d" % i, [128, 512], F32) for i in range(1)]
        ps_acc = [C.ps("ps_acc%d" % i, [128, 512], F32) for i in range(2)]
        pm = C.ps("pm", [128, 512], F32)
        pm_b = pm[:].bitcast(BF16)
        identb, identf = G['identb'], G['identf']
        kcT_s, vcx_s = G['kcT_s'], G['vcx_s']
        gath, qsc, gsc, atsc, rasc = G['gath'], G['qsc'], G['gsc'], G['atsc'], G['rasc']

        with P.group('cst'):
            P.dma('sync', cmask_s[:], W['cmask'])
            P.dma('sync', Bsel_s[:], W['Bsel'])
            P.dma('scalar', wmask_s[:], W['wmask'])
            P.dma('scalar', smask_s[:], W['smask'])
            P.dma('scalar', Eall_s[:], W['Eall'])
        with P.group('ksT'):
            for g in range(2):
                P.dma('gpsimd', ksT_s[g][64:68, :], W['kaug'], writes=[ksT_s[g]])
                for r in range(4):
                    P.dma('sync' if r % 2 == 0 else 'scalar', ksT_s[g][0:64, r * 4096:(r + 1) * 4096],
                          AP_(gath['ks'], r * 1024 * 512 + 64 * g * 4096, [[4096, 64], [1, 4096]]),
                          writes=[ksT_s[g]])
        with P.group('vsx'):
            for r in range(4):
                for hv in range(2):
                    P.dma('gpsimd', vsx_s[:, r * 32 + hv * 16:r * 32 + hv * 16 + 16, :],
                          AP_(gath['vs%d' % hv], r * 520 * 512, [[130, 128], [128 * 130, 16], [1, 130]]), writes=[vsx_s])

        if G.get('dbg'):
            P.dma('sync', G['dbg']['ks'], ksT_s[1][:])
            P.dma('sync', G['dbg']['vs'], vsx_s[:])
        state = {'s': 0, 'p': 0, 'x': 0}

        LA = 4

        def run_stream(tiles):
            n = len(tiles)
            bufs = []
            for i in range(n + LA):
                if i < n:
                    ps = ps_s[state['s'] % 4]
                    pb = Pb[state['p'] % NPB]
                    state['s'] += 1
                    state['p'] += 1
                    tiles[i][0](ps)
                    P.act(pb[:], ps[:], AF.Exp)
                    bufs.append(pb)
                j = i - (LA - 2)
                if 0 <= j < n and tiles[j][1] is not None:
                    tiles[j][1](bufs[j])
                if i - LA >= 0:
                    tiles[i - LA][2](bufs[i - LA])

        def mask_mul(pb, map_, mkeys):
            P.tt('vector', pb[:].rearrange("p (a q) -> p a q", a=4), pb[:].rearrange("p (a q) -> p a q", a=4), map_,
                 ALU.mult, reads=[pb] + mkeys, writes=[pb])

        def o_epilogue(psacc, g, br, gs):
            P.copy('vector', osT[:], psacc[0:65, :])
            pmv = pm[:, 0:260].rearrange("p (r d) -> p r d", d=65)
            for r in range(4):
                P.tr(pmv[:, r, :], osT[0:65, r * 128:(r + 1) * 128], identf[0:65, 0:65])
            P.recip(rd4[:], pmv[:, :, 0])
            P.tt('vector', coef4[:], rd4[:], AP_(gs, 12 * g + br, [[24, 128], [3, 4]]), ALU.mult)
            dst = acc[:, 4 * g:4 * g + 4, :]
            P.tt('vector', otmp[:], pmv[:, :, 1:65], AP_(coef4, 0, [[4, 128], [1, 4], [0, 64]]), ALU.mult)
            P.tt('gpsimd', dst, dst, otmp[:], ALU.add)

        for m in range(B_NM):
            qs = qT_s[m % 2]
            kws = kwT_s[m % 2]
            vws = vwx_s[m % 2]
            gs = gates_s[m % 2]
            h0 = 0 if m > 0 else 1
            with P.group("qT_s%d" % (m % 2)):
                P.dma('sync', qs[0:64, :, :], qsc[m].rearrange("g d n -> d g n"), writes=[qs])
                P.dma('scalar', qs[64:68, :, :], W['qaug'][m], writes=[qs])
            P.dma('scalar', gs[:], gsc[m])
            nh = 2 - h0
            c0 = 128 * (m - 1 + h0)
            with P.group("kws%d" % (m % 2)):
                for r in range(4):
                    P.dma('sync' if r % 2 == 0 else 'scalar',
                          AP_(kws, (r * 2 + h0) * 128, [[2048, 64], [1024, 2], [1, 128 * nh]]),
                          AP_(gath['kw'], r * 1024 * 512 + c0, [[4096, 64], [64 * 4096, 2], [1, 128 * nh]]),
                          writes=[kws])
            with P.group("vws%d" % (m % 2)):
                for r in range(4):
                    for hf in range(h0, 2):
                        mp = m - 1 + hf
                        P.dma('gpsimd', vws[:, r * 2 + hf, :],
                              AP_(gath['vw%d' % (mp // 16)], r * 520 * 512 + (mp % 16) * 128 * 130, [[130, 128], [1, 130]]),
                              writes=[vws])
            for g in range(2):
                P.copy('gpsimd', AP_(kws, 64 * 2048 + g * 1024 + h0 * 128, [[2048, 4], [256, 4], [1, 128 * (2 - h0)]]),
                       AP_(ksT_s[0], 64 * T + 128 * (m - 1 + h0), [[T, 4], [4096, 4], [1, 128 * (2 - h0)]]),
                       reads=[ksT_s[0]], writes=[kws])
            n_it = (32 * m + 30) // 128 + 1
            for g in range(2):
                for it in range(n_it):
                    ps = ps_s[state['s'] % 4]
                    state['s'] += 1
                    delta = 128 * it - 32 * m
                    masked = delta >= -128
                    P.mm(ps[:], kcT_s[:, g, it * 128:(it + 1) * 128], qs[:, g, :], start=True, stop=not masked)
                    if masked:
                        P.mm(ps[:], identb[:], cmask_s[:, (-delta) // 32, :], start=False, stop=True)
                    P.act(Pc[g][:, it, :], ps[:], AF.Exp)
                for r in range(4):
                    po = ps_o[0]
                    for it in range(n_it):
                        P.mm(po[:, 0:321], Pc[g][:, it, r * 128:(r + 1) * 128], vcx_s[:, it, g, :], start=(it == 0), stop=(it == n_it - 1))
                    P.copy('scalar', oc[:, r, :], po[:, 0:321])
                P.ts('vector', den4[:], oc[:, :, 64], 1e-30, ALU.max)
                P.recip(rd4[:], den4[:])
                P.ts('vector', imp[g][:], oc[:, 0, 65:321], rd4[:, 0:1], ALU.mult)
                for r in range(1, 4):
                    P.stt(imp[g][:], oc[:, r, 65:321], rd4[:, r:r + 1], imp[g][:], ALU.mult, ALU.add)
                P.tt('vector', coef4[:], rd4[:], AP_(gs, 12 * g + 0, [[24, 128], [3, 4]]), ALU.mult)
                P.tt('vector', acc[:, 4 * g:4 * g + 4, :], oc[:, :, 0:64], AP_(coef4, 0, [[4, 128], [1, 4], [0, 64]]), ALU.mult)
                P.tt('vector', imp[g][:], imp[g][:], Bsel_s[:, 256 - 8 * m:512 - 8 * m], ALU.add)
                P.ts('vector', imp[g][:, 0:1], imp[g][:, 0:1], 1e4, ALU.add)
                P.op('vector', lambda e, g=g: e.max(out=m8[:, 0:8], in_=imp[g][:]), reads=[imp[g]], writes=[m8])
                P.op('vector', lambda e, g=g: e.match_replace(out=tmpi[:], in_to_replace=m8[:, 0:8], in_values=imp[g][:], imm_value=-3.0e38),
                     reads=[imp[g], m8], writes=[tmpi])
                P.op('vector', lambda e: e.max(out=m8[:, 8:16], in_=tmpi[:]), reads=[tmpi], writes=[m8])
                P.ts('vector', thr[:], m8[:, 15:16], -1e29, ALU.max)
                P.ts('vector', nm[:], imp[g][:], thr[:, 0:1], ALU.is_ge)
                for qt in range(4):
                    P.tr(pm_b[0:64, qt * 128:(qt + 1) * 128], nm[:, qt * 64:(qt + 1) * 64], identb[:])
                P.copy('vector', nmT[g][:].rearrange("p a q -> p (a q)"), pm_b[0:64, 0:512], reads=[pm])
            tiles = []
            wt = [(r, hf) for hf in range(h0, 2) for r in range(4)]
            for g in range(2):
                for idx, (r, hf) in enumerate(wt):
                    def eS(ps, g=g, r=r, hf=hf):
                        P.mm(ps[:], kws[:, g, (r * 2 + hf) * 128:(r * 2 + hf + 1) * 128], qs[:, g, :], start=True, stop=(hf == 0))
                        if hf == 1:
                            P.mm(ps[:], identb[:], smask_s[:, r, :], start=False, stop=True)

                    def eX(pb, r=r):
                        mask_mul(pb, AP_(wmask_s, r * 128, [[1024, 128], [0, 4], [1, 128]]), [wmask_s])

                    def ePV(pb, g=g, r=r, hf=hf, idx=idx):
                        P.mm(ps_acc[g][0:65, :], vws[:, r * 2 + hf, 65 * g:65 * g + 65], pb[:], start=(idx == 0), stop=(idx == len(wt) - 1))
                    tiles.append((eS, eX if hf == 0 else None, ePV))
            run_stream(tiles)
            for g in range(2):
                o_epilogue(ps_acc[g], g, 2, gs)
            nkt = 4 * m + 4
            for g in range(2):
                tiles = []
                for kt in range(nkt):
                    def eS(ps, g=g, kt=kt):
                        diag = kt >= 4 * m
                        col = ((kt % 4) * 32 + kt // 4) * 128
                        P.mm(ps[:], ksT_s[g][:, col:col + 128], qs[:, g, :], start=True, stop=not diag)
                        if diag:
                            P.mm(ps[:], identb[:], smask_s[:, kt - 4 * m, :], start=False, stop=True)

                    def eX(pb, g=g, kt=kt):
                        bank = (ps_o[0], pm)[state['x'] % 2]
                        state['x'] += 1
                        P.mm(bank[:, 0:128], Eall_s[:, kt % 32, :], nmT[g][:, kt // 32, :], start=True, stop=True)
                        mask_mul(pb, AP_(bank, 0, [[512, 128], [0, 4], [1, 128]]), [bank])

                    def ePV(pb, g=g, kt=kt):
                        P.mm(ps_acc[g][0:65, :], vsx_s[:, (kt % 4) * 32 + kt // 4, 65 * g:65 * g + 65], pb[:], start=(kt == 0), stop=(kt == nkt - 1))
                    tiles.append((eS, eX, ePV))
                run_stream(tiles)
                o_epilogue(ps_acc[g], g, 1, gs)
            accf = acc[:].rearrange("p h d -> p (h d)")
            P.act(accb[:], accf, AF.Square, accum_out=ssq[:])
            P.act(rat[:], ssq[:], AF.Sqrt, bias=EPS, scale=1.0 / 512)
            P.recip(rat2[:], rat[:])
            P.dma('gpsimd', rasc[m], rat2[:])
            P.copy('gpsimd', accb[:], accf)
            for bk in range(4):
                P.tr(pm_b[:, bk * 128:(bk + 1) * 128], accb[:, bk * 128:(bk + 1) * 128], identb[:])
            P.copy('vector', atT[:].rearrange("p b q -> p (b q)"), pm_b[:, 0:512], reads=[pm])
            P.dma('gpsimd', atsc[m], atT[:].rearrange("p b q -> p (b q)"))
            if G.get('dbg'):
                P.dma('gpsimd', G['dbg']['at'][m], atT[:].rearrange("p b q -> p (b q)"))
                P.dma('gpsimd', G['dbg']['ra'][m], rat2[:])


def emit_C(C, l, xsrc, xdst, W, G):
    P = C.P
    with C.phase("C%d" % l):
        Wo = C.sb("Wo", [128, 8, D], BF16)
        Wdn = C.sb("Wdn", [128, 32, D], BF16)
        wst = [C.sb("wst%d" % i, [128, 8, 512], F32) for i in range(2)]
        Wup = [C.sb("Wup%d" % i, [128, 8, 512], BF16) for i in range(2)]
        gout_s = C.sb("gout_s", [128, 8], F32)
        gffn_s = C.sb("gffn_s", [128, D], F32)
        cw_s = C.sb("cw_s", [128, 4, 3], F32)
        selw_s = C.sb("selw_s", [128, 4], F32)
        xt = [C.sb("xt%d" % i, [128, D], F32) for i in range(1)]
        x1 = [C.sb("x1_%d" % i, [128, D], F32) for i in range(4)]
        at_s = [C.sb("at_s%d" % i, [128, 4, 128], BF16) for i in range(2)]
        yv = [C.sb("yv%d" % i, [128, 4, 128], F32) for i in range(1)]
        bg2 = [C.sb("bg2_%d" % i, [128, 4, 2], F32) for i in range(2)]
        tl = [C.sb("tl%d" % i, [128, 4, 8], F32) for i in range(2)]
        halo = C.sb("halo", [128, 4, 2], F32)
        pt0 = C.sb("pt0", [128, 4], F32)
        pt1 = C.sb("pt1", [128, 4], F32)
        ysq = C.sb("ysq", [128, 4, 128], F32)
        ybf = C.sb("ybf", [128, 4, 128], BF16)
        rc = C.sb("rc", [128, 1], F32)
        rc2 = C.sb("rc2", [128, 1], F32)
        ra_s = [C.sb("ra_s%d" % i, [128, 1], F32) for i in range(2)]
        ss = C.sb("ss", [128, 1], F32)
        rstd = C.sb("rstd", [128, 1], F32)
        h2n = C.sb("h2n", [128, D], BF16)
        h2T = C.sb("h2T", [128, 8, 512], BF16)
        rl = [C.sb("rl%d" % i, [128, 512], F32) for i in range(1)]
        actT = C.sb("actT", [128, 32, 512], BF16)
        pa = [C.ps("pa%d" % i, [128, 512], F32) for i in range(2)]
        pcv = [C.ps("pcv%d" % i, [128, 512], F32) for i in range(2)]
        pu = [C.ps("pu%d" % i, [128, 512], F32) for i in range(2)]
        pT = C.ps("pT", [128, 8, 128], BF16)
        px = C.ps("px", [128, 512], F32)
        identb, onesf = G['identb'], G['onesf']
        gathu, ysc, bg2sc, atsc, rasc = G['gathu'], G['ysc'], G['bg2sc'], G['atsc'], G['rasc']

        with P.group('cst'):
            P.dma('sync', gout_s[:], W['gout'][l])
            P.dma('sync', gffn_s[:], W['gffn'][l])
            P.dma('sync', cw_s[:], W['cw'][l])
            P.dma('sync', selw_s[:], W['selw'])
        n = 0
        for kc in range(8):
            for hf in range(2):
                st = wst[n % 2]
                n += 1
                P.dma('sync' if hf == 0 else 'scalar', st[:, 0, :], W['w_o'][l, kc * 128:(kc + 1) * 128, hf * 512:(hf + 1) * 512], writes=[st])
                P.ts('vector' if hf == 0 else 'gpsimd', Wo[:, kc, hf * 512:(hf + 1) * 512], st[:, 0, :], gout_s[:, kc:kc + 1], ALU.mult, reads=[st, gout_s])
        for f4 in range(8):
            st = wst[n % 2]
            n += 1
            stv = AP_(st, 0, [[4096, 128], [1024, 4], [1, 1024]])
            P.dma('sync' if f4 % 2 == 0 else 'scalar', stv, W['w_dn'][l, f4 * 512:(f4 + 1) * 512, :].rearrange("(a p) n -> p a n", p=128), writes=[st])
            P.copy('vector' if f4 % 2 == 0 else 'gpsimd', Wdn[:, f4 * 4:(f4 + 1) * 4, :], stv, reads=[st])
        up_n = [n]

        for bt in range(B_NM // 4):
            for j in range(4):
                m = bt * 4 + j
                xb = xt[0]
                ab, rab, yb, bgb, tlb = at_s[m % 2], ra_s[m % 2], yv[0], bg2[m % 2], tl[m % 2]
                P.dma('sync', xb[:], xsrc[m])
                P.dma('scalar', ab[:], atsc[m].rearrange("p (b q) -> p b q", b=4))
                P.dma('sync', rab[:], rasc[m])
                P.dma('scalar', yb[:], ysc[m].rearrange("p (c t) -> p c t", c=4))
                P.dma('sync', bgb[:], bg2sc[m].rearrange("p (c e) -> p c e", e=2))
                if m == 0:
                    P.memset('gpsimd', tlb[:, 0, :], 0.0)
                with P.group("tl%d" % (m % 2)):
                    if m > 0:
                        P.dma('gpsimd', tlb[:, 0, :], gathu[3 * B_NM + m - 1].rearrange("(p e) -> p e", e=8), writes=[tlb])
                    for cd in range(1, 4):
                        P.dma('gpsimd', tlb[:, cd, :], gathu[(cd - 1) * B_NM + m].rearrange("(p e) -> p e", e=8), writes=[tlb])
                hf = halo[:].rearrange("p c e -> p (c e)")
                P.ts('vector', hf, tlb[:, 0, :], selw_s[:, 0:1], ALU.mult)
                for cd in range(1, 4):
                    P.stt(hf, tlb[:, cd, :], selw_s[:, cd:cd + 1], hf, ALU.mult, ALU.add)
                P.tt('vector', pt0[:], halo[:, :, 1], cw_s[:, :, 1], ALU.mult)
                P.tt('vector', pt1[:], halo[:, :, 0], cw_s[:, :, 0], ALU.mult)
                P.tt('vector', pt0[:], pt0[:], pt1[:], ALU.add)
                P.tt('vector', pt0[:], pt0[:], bgb[:, :, 0], ALU.mult)
                P.tt('vector', yb[:, :, 0], yb[:, :, 0], pt0[:], ALU.add)
                P.tt('vector', pt1[:], halo[:, :, 1], cw_s[:, :, 0], ALU.mult)
                P.tt('vector', pt1[:], pt1[:], bgb[:, :, 1], ALU.mult)
                P.tt('vector', yb[:, :, 1], yb[:, :, 1], pt1[:], ALU.add)
                P.act(ysq[:], yb[:], AF.Square)
                for ch in range(4):
                    P.mm(px[:, 0:1], ysq[:, ch, :], onesf[:], start=(ch == 0), stop=(ch == 3))
                P.act(rc[:], px[:, 0:1], AF.Sqrt, bias=EPS, scale=1.0 / 512)
                P.recip(rc2[:], rc[:])
                P.copy('gpsimd', ybf[:], yb[:])
                for nh in range(2):
                    for kc in range(4):
                        P.mm(pa[nh][:], ab[:, kc, :], Wo[:, kc, nh * 512:(nh + 1) * 512], start=(kc == 0), stop=(kc == 3))
                    for kc in range(4):
                        P.mm(pcv[nh][:], ybf[:, kc, :], Wo[:, 4 + kc, nh * 512:(nh + 1) * 512], start=(kc == 0), stop=(kc == 3))
                x1b = x1[j]
                for nh in range(2):
                    sl = slice(nh * 512, (nh + 1) * 512)
                    P.stt(x1b[:, sl], pa[nh][:], rab[:, 0:1], xb[:, sl], ALU.mult, ALU.add)
                    P.stt(x1b[:, sl], pcv[nh][:], rc2[:, 0:1], x1b[:, sl], ALU.mult, ALU.add)
                if G.get('dbg'):
                    P.dma('gpsimd', G['dbg']['x1'][m], x1b[:])
                    P.dma('gpsimd', G['dbg']['y'][m], ybf[:].rearrange("p c t -> p (c t)"))
                    P.dma('gpsimd', G['dbg']['rc'][m], rc2[:])
                P.act(h2n[:], x1b[:], AF.Square, accum_out=ss[:])
                P.act(rstd[:], ss[:], AF.Sqrt, bias=EPS, scale=1.0 / D)
                P.recip(rstd[:], rstd[:])
                P.stt(h2n[:], x1b[:], rstd[:, 0:1], gffn_s[:], ALU.mult, ALU.mult)
                for kc in range(8):
                    P.tr(pT[:, kc, :], h2n[:, kc * 128:(kc + 1) * 128], identb[:])
                P.copy('scalar', h2T[:, :, j * 128:(j + 1) * 128], pT[:])
            for u in range(8):
                st = wst[up_n[0] % 2]
                wb = Wup[up_n[0] % 2]
                up_n[0] += 1
                P.dma('sync' if u % 2 == 0 else 'scalar', st[:], W['w_up'][l, :, u * 512:(u + 1) * 512].rearrange("(kc p) n -> p kc n", p=128))
                P.copy('gpsimd' if u % 2 == 0 else 'vector', wb[:], st[:])
                for fl in range(4):
                    f = u * 4 + fl
                    pp = pu[f % 2]
                    for kc in range(8):
                        P.mm(pp[:], wb[:, kc, fl * 128:(fl + 1) * 128], h2T[:, kc, :], start=(kc == 0), stop=(kc == 7))
                    rb = rl[0]
                    P.act(rb[:], pp[:], AF.Relu)
                    P.tt('gpsimd' if f % 2 == 0 else 'vector', actT[:, f, :], rb[:], rb[:], ALU.mult)
            for j in range(4):
                m = bt * 4 + j
                ob = x1[j]
                for nh in range(2):
                    pp = pa[nh] if j % 2 == 0 else pcv[nh]
                    for f in range(32):
                        P.mm(pp[:], actT[:, f, j * 128:(j + 1) * 128], Wdn[:, f, nh * 512:(nh + 1) * 512], start=(f == 0), stop=(f == 31))
                    P.tt('vector', ob[:, nh * 512:(nh + 1) * 512], pp[:], x1[j][:, nh * 512:(nh + 1) * 512], ALU.add)
                P.dma('sync', xdst[m], ob[:])


def build_fused(nlayers=DEPTH, stages="AGZBC", dbg=False):
    C = Ctx()
    P = C.P
    W = {}
    W['x0'] = C.din("x0", [B_NM, 128, D], F32)
    W['w_in'] = C.din("w_in", [nlayers, D, INW], F32)
    W['gmix'] = C.din("gmix", [nlayers, 128, 8], F32)
    W['gq'] = C.din("gq", [nlayers, 128, 512], F32)
    W['gk3'] = C.din("gk3", [nlayers, 128, 3, 128], F32)
    W['cw'] = C.din("cw", [nlayers, 128, 4, 3], F32)
    W['peT'] = C.din("peT", [nlayers, 2, 64, 32], F32)
    W['w1'] = C.din("w1", [nlayers, 2, 2048, 256], F32)
    W['b1b'] = C.din("b1b", [nlayers, 2, 128, 256], F32)
    W['w2'] = C.din("w2", [nlayers, 2, 256, 64], F32)
    W['b2b'] = C.din("b2b", [nlayers, 128, 2, 64], F32)
    W['w_o'] = C.din("w_o", [nlayers, D, D], F32)
    W['gout'] = C.din("gout", [nlayers, 128, 8], F32)
    W['gffn'] = C.din("gffn", [nlayers, 128, D], F32)
    W['w_up'] = C.din("w_up", [nlayers, D, DFF], F32)
    W['w_dn'] = C.din("w_dn", [nlayers, DFF, D], F32)
    W['smask'] = C.din("smask", [128, 4, 512], BF16)
    W['cmask'] = C.din("cmask", [128, 5, 512], BF16)
    W['wmask'] = C.din("wmask", [128, 8, 128], BF16)
    W['Eall'] = C.din("Eall", [64, 32, 128], BF16)
    W['Bsel'] = C.din("Bsel", [128, 512], F32)
    W['kaug'] = C.din("kaug", [4, T], BF16)
    W['kcaug'] = C.din("kcaug", [4, 2, 1024], BF16)
    W['qaug'] = C.din("qaug", [B_NM, 4, 2, 512], BF16)
    W['vcxc'] = C.din("vcxc", [128, 8, 2, 257], BF16)
    W['selw'] = C.din("selw", [128, 4], F32)
    identb_d = C.din("identb", [128, 128], BF16)
    identf_d = C.din("identf", [128, 128], F32)
    onesf_d = C.din("onesf", [128, 1], F32)
    xo = C.dout("xo", [B_NM, 128, D], F32)

    G = {}
    COMPS = [('ks', 1024), ('kw', 1024), ('kc', 1024), ('vc', 1024), ('vs0', 520), ('vs1', 520), ('vw0', 520), ('vw1', 520)]
    G['pay'] = {cn: C.dint("pay_" + cn, [rows, 512], BF16) for cn, rows in COMPS}
    G['gath'] = {cn: C.dint("gath_" + cn, [4 * rows, 512], BF16) for cn, rows in COMPS}
    G['payu'] = C.dint("payu", [B_NM, 1024], F32).ap()
    G['gathu'] = C.dint("gathu", [4 * B_NM, 1024], F32).ap()
    xbuf = C.dint("xbuf", [B_NM, 128, D], F32).ap()
    G['qsc'] = C.dint("qsc", [B_NM, 2, 64, 512], BF16)
    G['gsc'] = C.dint("gsc", [B_NM, 128, 24], F32).ap()
    G['ysc'] = C.dint("ysc", [B_NM, 128, 512], F32).ap()
    G['bg2sc'] = C.dint("bg2sc", [B_NM, 128, 8], F32).ap()
    G['atsc'] = C.dint("atsc", [B_NM, 128, 512], BF16).ap()
    G['rasc'] = C.dint("rasc", [B_NM, 128, 1], F32).ap()
    qsc_ap = G['qsc'].ap()
    if dbg:
        G['dbg'] = {'at': C.dout("dbg_at", [B_NM, 128, 512], BF16), 'ra': C.dout("dbg_ra", [B_NM, 128, 1], F32),
                    'x1': C.dout("dbg_x1", [B_NM, 128, D], F32), 'y': C.dout("dbg_y", [B_NM, 128, 512], BF16),
                    'rc': C.dout("dbg_rc", [B_NM, 128, 1], F32),
                    'kc': C.dout("dbg_kc", [68, 2, 1024], BF16), 'vc': C.dout("dbg_vc", [128, 8, 2, 321], BF16),
                    'ks': C.dout("dbg_ks", [68, T], BF16), 'vs': C.dout("dbg_vs", [128, 128, 130], BF16)}

    G['identb'] = C.sb("identb_s", [128, 128], BF16)
    G['identf'] = C.sb("identf_s", [128, 128], F32)
    G['onesf'] = C.sb("onesf_s", [128, 1], F32)
    with P.group('cst'):
        P.dma('sync', G['identb'][:], identb_d)
        P.dma('sync', G['identf'][:], identf_d)
        P.dma('sync', G['onesf'][:], onesf_d)
    P.barrier()
    rg = [[0, 1, 2, 3], [4, 5, 6, 7]]
    def mk_cc(i_ap, o_ap):
        return lambda e: e.collective_compute("AllGather", ALU.bypass, replica_groups=rg, ins=[i_ap.opt()], outs=[o_ap.opt()])
    cc_fns = [mk_cc(G['pay'][cn].ap(), G['gath'][cn].ap()) for cn, _ in COMPS] + [mk_cc(G['payu'], G['gathu'])]
    for l in range(nlayers):
        xsrc = W['x0'] if l == 0 else xbuf
        xdst = xo if l == nlayers - 1 else xbuf
        if 'A' in stages:
            emit_A(C, l, xsrc, W, G)
        if 'G' in stages:
            P.collectives(cc_fns)
        with C.phase("ZB%d" % l):
            G['kcT_s'] = C.sb("kcT_s", [68, 2, 1024], BF16)
            G['vcx_s'] = C.sb("vcx_s", [128, 8, 2, 321], BF16)
            with P.group('cst2'):
                P.dma('sync', G['kcT_s'][64:68, :, :], W['kcaug'])
                P.dma('scalar', G['vcx_s'][:, :, :, 64:321], W['vcxc'])
            if 'Z' in stages:
                emit_Z(C, l, W, G)
            Gb = dict(G)
            Gb['qsc'] = qsc_ap
            if 'B' in stages:
                emit_B(C, l, W, Gb)
        if 'C' in stages:
            emit_C(C, l, xsrc, xdst, W, G)
    if 'C' not in stages:
        with C.phase("dbg"):
            xt = [C.sb("xt%d" % i, [128, D], F32) for i in range(2)]
            for m in range(B_NM):
                P.dma('sync', xt[m % 2][:], W['x0'][m])
                P.dma('sync', xo[m], xt[m % 2][:])
    return C.finish()


IDENTB = np.eye(128, dtype=np.float32).astype(NPBF)
ONESF = np.ones((128, 1), np.float32)


def _bf(a):
    return np.ascontiguousarray(a).astype(NPBF)


def _selmap():
    i = np.arange(1024)[:, None] * 16
    j = np.arange(256)[None, :] * 64
    sh = np.clip(np.minimum(i + 32, j + 64) - np.maximum(i, j), 0, None).astype(np.float32) / 32.0
    sh[1023] = 0.0
    return sh


def _pos_rows(pos):
    pos = np.maximum(pos, 0)
    return np.stack([np.ones_like(pos), np.ones_like(pos), pos % 128, pos // 128]).astype(np.float32)


def core_consts(s):
    ki = np.arange(128)[:, None]
    qi = np.arange(128)[None, :]
    tri_gt = np.where(ki > qi, NEGM, 0.0).astype(np.float32)
    tri_le = np.where(ki <= qi, NEGM, 0.0).astype(np.float32)
    full = np.full((128, 128), NEGM, np.float32)
    zero = np.zeros((128, 128), np.float32)
    smask = np.tile(np.stack([zero if d < s else (tri_gt if d == s else full) for d in range(4)], axis=1), (1, 1, 4))
    cm = []
    for v in range(5):
        ip = ki - 32 * v
        vis = (16 * ip + 31) <= (128 * s + qi)
        cm.append(np.where(vis, 0.0, NEGM).astype(np.float32))
    cmask = np.tile(np.stack(cm, axis=1), (1, 1, 4))
    wm = []
    for d in range(8):
        off = s + 4 - d
        if off == 0:
            wm.append((ki <= qi).astype(np.float32))
        elif off == 4:
            wm.append((ki > qi).astype(np.float32))
        elif 0 < off < 4:
            wm.append(np.ones((128, 128), np.float32))
        else:
            wm.append(zero)
    wmask = np.stack(wm, axis=1)
    E = np.zeros((64, 32, 128), np.float32)
    for kt in range(32):
        for k in range(128):
            E[2 * kt + k // 64, kt, k] = 1.0
    r = np.arange(512)[None, :] - 256 - 2 * s
    qq = np.arange(128)[:, None]
    Brel = np.zeros((128, 512), np.float32)
    Brel = np.where(r >= 2, np.float32(-1e30), Brel)
    Brel = np.where(r == 1, np.where(qq >= 64, np.float32(1e4), np.float32(-1e30)), Brel)
    Brel = np.where(r == 0, np.float32(1e4), Brel)
    Brel = np.where((r == -1) & (qq < 64), np.float32(1e4), Brel)
    rr, mm, kk = np.meshgrid(np.arange(4), np.arange(32), np.arange(128), indexing='ij')
    kpos = (128 * (4 * mm + rr) + kk).reshape(-1)
    kaug = _pos_rows(kpos)
    kc = _pos_rows(np.arange(1024) * 16 + 31)
    kcaug = np.stack([kc, kc], axis=1)
    qaug = np.zeros((B_NM, 4, 2, 512), np.float32)
    qv = np.arange(128)
    for m in range(B_NM):
        c = 4 * m + s
        for g in range(2):
            for rh in range(4):
                sl = 2.0 ** (-(4 * g + rh + 1))
                cs = slice(rh * 128, (rh + 1) * 128)
                qaug[m, 0, g, cs] = -sl * qv
                qaug[m, 1, g, cs] = -sl * 128.0 * c
                qaug[m, 2, g, cs] = sl
                qaug[m, 3, g, cs] = sl * 128.0
    sm = _selmap().reshape(8, 128, 256).transpose(1, 0, 2)
    vcxc = np.zeros((128, 8, 2, 257), np.float32)
    vcxc[:, :, :, 0] = 1.0
    vcxc[127, 7, :, 0] = 0.0
    vcxc[:, :, :, 1:257] = sm[:, :, None, :]
    selw = np.zeros((128, 4), np.float32)
    selw[:, s] = 1.0
    return {'smask': _bf(smask), 'cmask': _bf(cmask), 'wmask': _bf(wmask), 'Eall': _bf(E),
            'Bsel': np.ascontiguousarray(Brel.astype(np.float32)), 'kaug': _bf(kaug), 'kcaug': _bf(kcaug),
            'qaug': _bf(qaug), 'vcxc': _bf(vcxc), 'selw': selw,
            'identb': IDENTB, 'identf': np.eye(128, dtype=np.float32), 'onesf': ONESF}


def prep_fused(p, L=DEPTH):
    p = {k: (v if k == 'x' else v[:L]) for k, v in p.items()}
    common = {
        'w_in': np.ascontiguousarray(p['w_in'], dtype=np.float32),
        'gmix': np.ascontiguousarray(p['g_mix_norm'].reshape(L, 8, 128).transpose(0, 2, 1)),
        'gq': np.ascontiguousarray(np.tile(p['g_q'][:, None, :], (1, 128, 8))),
        'gk3': np.ascontiguousarray(np.broadcast_to(np.tile(p['g_k'], (1, 1, 2))[:, None], (L, 128, 3, 128))),
        'cw': np.ascontiguousarray(p['conv_w'].reshape(L, 3, 4, 128).transpose(0, 3, 2, 1)),
        'peT': np.ascontiguousarray(p['pe_cmp'].transpose(0, 1, 3, 2)),
        'w1': np.ascontiguousarray(p['w_cmp1'], dtype=np.float32),
        'b1b': np.ascontiguousarray(np.broadcast_to(p['b_cmp1'][:, :, None, :], (L, 2, 128, 256))),
        'w2': np.ascontiguousarray(p['w_cmp2'], dtype=np.float32),
        'b2b': np.ascontiguousarray(np.broadcast_to(p['b_cmp2'][:, None], (L, 128, 2, 64))),
        'w_o': np.ascontiguousarray(p['w_o'], dtype=np.float32),
        'gout': np.ascontiguousarray(p['g_out'].reshape(L, 8, 128).transpose(0, 2, 1)),
        'gffn': np.ascontiguousarray(np.tile(p['g_ffn_norm'][:, None, :], (1, 128, 1))),
        'w_up': np.ascontiguousarray(p['w_up'], dtype=np.float32),
        'w_dn': np.ascontiguousarray(p['w_down'], dtype=np.float32),
    }
    maps = []
    x = np.ascontiguousarray(p['x'], dtype=np.float32)
    for b in range(NB):
        xv = x[b].reshape(T // 128, 128, D)
        for s in range(4):
            mp = dict(common)
            mp.update(core_consts(s))
            mp['x0'] = np.ascontiguousarray(xv[np.arange(B_NM) * 4 + s])
            maps.append(mp)
    return maps


_PROG = {}


def run_fused(inputs, nlayers=DEPTH):
    p = {k: np.asarray(v) for k, v in inputs.items()}
    if nlayers not in _PROG:
        _PROG[nlayers] = build_fused(nlayers)
    res = run_bass_kernel_spmd(_PROG[nlayers], prep_fused(p, nlayers), core_ids=list(range(NCORES)))
    out = np.empty((NB, T, D), np.float32)
    for b in range(NB):
        ov = out[b].reshape(T // 128, 128, D)
        for s in range(4):
            ov[np.arange(B_NM) * 4 + s] = np.asarray(res.results[4 * b + s]['xo'])
    return out


def kernel(**inputs):
    return run_fused(inputs, DEPTH)
```

```python
import contextlib
import numpy as np
import ml_dtypes
import concourse.bass as bass
import concourse.mybir as mybir
from concourse.bass_utils import run_bass_kernel_spmd

F32 = mybir.dt.float32
BF16 = mybir.dt.bfloat16
AF = mybir.ActivationFunctionType
ALU = mybir.AluOpType
AX = mybir.AxisListType
NPBF = ml_dtypes.bfloat16

ENGS = ['tensor', 'vector', 'scalar', 'gpsimd', 'sync']
NCORES = 8
D = 1024
T = 16384
NB = 2
DEPTH = 4
INW = 2840
EPS = 1e-6
NEGM = -30000.0


def key_of(ap):
    t = getattr(ap, 'tensor', ap)
    return t.name.split('@')[0]


class Prog:
    def __init__(self, nc):
        self.nc = nc
        self.q = {e: [] for e in ENGS}
        self.semcount = {}
        self.res = {}
        self.seen = {e: {} for e in ENGS}

    grp = None

    @contextlib.contextmanager
    def group(self, sem):
        s = 'd_' + sem
        self.grp = {'sem': sem, 's': s, 'start': self.semcount.get(s, 0), 'keys': set(), 'pre': {}}
        try:
            yield
        finally:
            g, self.grp = self.grp, None
            v = self.semcount.get(s, 0)
            for k in g['keys']:
                self.res[k]['w'] = (s, v)

    def _need(self, eng, reads, writes):
        need = {}

        def add(tok):
            if tok is None:
                return
            s, v = tok
            if eng == 'tensor' and s == 'e_tensor':
                return
            if self.grp is not None and s == self.grp['s'] and v > self.grp['start']:
                return
            if need.get(s, 0) < v:
                need[s] = v
        for k in reads:
            st = self.res.get(k)
            if st:
                add(st['w'])
        for k in writes:
            sts = [self.res.get(k)]
            if self.grp is not None:
                if k not in self.grp['pre']:
                    st0 = self.res.get(k)
                    self.grp['pre'][k] = {'w': st0['w'], 'r': dict(st0['r'])} if st0 else None
                sts.append(self.grp['pre'][k])
            for st in sts:
                if st:
                    add(st['w'])
                    for s, v in st['r'].items():
                        add((s, v))
        for s, v in need.items():
            if self.seen[eng].get(s, 0) < v:
                self.q[eng].append(('wait', s, v))
                self.seen[eng][s] = v

    def _update(self, reads, writes, tok):
        s, v = tok
        for k in reads:
            st = self.res.setdefault(k, {'w': None, 'r': {}})
            if st['r'].get(s, 0) < v:
                st['r'][s] = v
        for k in writes:
            self.res[k] = {'w': tok, 'r': {}}

    def op(self, eng, fn, reads=(), writes=()):
        reads = [r if isinstance(r, str) else key_of(r) for r in reads]
        writes = [r if isinstance(r, str) else key_of(r) for r in writes]
        self._need(eng, reads, writes)
        s = 'e_' + eng
        v = self.semcount.get(s, 0) + 1
        self.semcount[s] = v
        self.q[eng].append(('op', fn, s, 1))
        self._update(reads, writes, (s, v))

    def dma(self, eng, out, in_, reads=None, writes=None, sem=None, **kw):
        if reads is None:
            reads = [] if in_.tensor.name in self.dram_names else [in_]
        if writes is None:
            writes = [] if out.tensor.name in self.dram_names else [out]
        assert reads or writes or sem
        reads = [r if isinstance(r, str) else key_of(r) for r in reads]
        writes = [r if isinstance(r, str) else key_of(r) for r in writes]
        if self.grp is not None:
            sem = self.grp['sem']
            self.grp['keys'].update(writes)
        if sem is None:
            sem = writes[0] if writes else 'st_' + reads[0]
        self._need(eng, reads, writes)
        s = 'd_' + sem
        v = self.semcount.get(s, 0) + 16
        self.semcount[s] = v
        self.q[eng].append(('op', lambda e: e.dma_start(out=out, in_=in_, **kw), s, 16))
        self._update(reads, writes, (s, v))

    dram_names = set()

    def mm(self, out, lhsT, rhs, start=True, stop=True, reads=None, writes=None):
        self.op('tensor', lambda e: e.matmul(out, lhsT=lhsT, rhs=rhs, start=start, stop=stop),
                reads=reads if reads is not None else [lhsT, rhs],
                writes=writes if writes is not None else [out])

    def tr(self, out, in_, ident, reads=None, writes=None):
        self.op('tensor', lambda e: e.transpose(out, in_, ident),
                reads=reads if reads is not None else [in_, ident],
                writes=writes if writes is not None else [out])

    def act(self, out, in_, func, bias=None, scale=None, accum_out=None, reads=None, writes=None, eng='scalar'):
        kw = {}
        rd = [in_]
        if bias is not None:
            kw['bias'] = bias
            if not isinstance(bias, (int, float)):
                rd.append(bias)
        if scale is not None:
            kw['scale'] = scale
            if not isinstance(scale, (int, float)):
                rd.append(scale)
        wr = [out]
        if accum_out is not None:
            kw['accum_out'] = accum_out
            wr.append(accum_out)
        self.op('scalar', lambda e: e.activation(out=out, in_=in_, func=func, **kw),
                reads=reads if reads is not None else rd,
                writes=writes if writes is not None else wr)

    def tt(self, eng, out, in0, in1, op, reads=None, writes=None):
        self.op(eng, lambda e: e.tensor_tensor(out=out, in0=in0, in1=in1, op=op),
                reads=reads if reads is not None else [in0, in1],
                writes=writes if writes is not None else [out])

    def ts(self, eng, out, in0, s1, op0, s2=None, op1=None, reads=None, writes=None):
        rd = [in0]
        if not isinstance(s1, (int, float)):
            rd.append(s1)
        if s2 is not None and not isinstance(s2, (int, float)):
            rd.append(s2)
        if op1 is None:
            fn = lambda e: e.tensor_scalar(out=out, in0=in0, scalar1=s1, scalar2=None, op0=op0)
        else:
            fn = lambda e: e.tensor_scalar(out=out, in0=in0, scalar1=s1, scalar2=s2, op0=op0, op1=op1)
        self.op(eng, fn, reads=reads if reads is not None else rd,
                writes=writes if writes is not None else [out])

    def stt(self, out, in0, scalar, in1, op0, op1, reads=None, writes=None):
        rd = [in0, in1]
        if not isinstance(scalar, (int, float)):
            rd.append(scalar)
        self.op('vector', lambda e: e.scalar_tensor_tensor(out=out, in0=in0, scalar=scalar, in1=in1, op0=op0, op1=op1),
                reads=reads if reads is not None else rd,
                writes=writes if writes is not None else [out])

    def copy(self, eng, out, in_, reads=None, writes=None):
        if eng == 'scalar':
            fn = lambda e: e.activation(out=out, in_=in_, func=AF.Copy)
        else:
            fn = lambda e: e.tensor_copy(out=out, in_=in_)
        self.op(eng, fn, reads=reads if reads is not None else [in_],
                writes=writes if writes is not None else [out])

    def recip(self, out, in_):
        self.op('vector', lambda e: e.reciprocal(out=out, in_=in_), reads=[in_], writes=[out])

    def reduce(self, out, in_, op=ALU.add, axis=AX.X):
        self.op('vector', lambda e: e.tensor_reduce(out=out, in_=in_, axis=axis, op=op), reads=[in_], writes=[out])

    def memset(self, eng, ap, val):
        self.op(eng, lambda e: e.memset(ap, val), writes=[ap])

    def barrier(self):
        for e in ENGS:
            for s, v in self.semcount.items():
                if self.seen[e].get(s, 0) < v:
                    self.q[e].append(('wait', s, v))
                    self.seen[e][s] = v
        self.res = {}

    def collectives(self, fns):
        self.barrier()
        s = 'c_cc'
        for fn in fns:
            v = self.semcount.get(s, 0) + 1
            self.semcount[s] = v
            self.q['gpsimd'].append(('op', fn, s, 1))
        self.barrier()

    def emit(self):
        nc = self.nc
        for s, v in self.semcount.items():
            if self.seen['sync'].get(s, 0) < v:
                self.q['sync'].append(('wait', s, v))
        with contextlib.ExitStack() as es:
            semh = {s: es.enter_context(nc.semaphore(s)) for s in self.semcount}
            block = es.enter_context(nc.Block())

            def mk(engname):
                def body(e):
                    for it in self.q[engname]:
                        if it[0] == 'wait':
                            e.wait_ge(semh[it[1]], it[2])
                        else:
                            ins = it[1](e)
                            ins.then_inc(semh[it[2]], it[3])
                return body
            for engname in ENGS:
                if self.q[engname]:
                    getattr(block, engname)(mk(engname))


def AP_(t, offset, dims):
    return bass.AP(t, offset, [list(d) for d in dims])


class Ctx:
    def __init__(self):
        self.nc = bass.Bass("TRN2", target_bir_lowering=False)
        self.es = contextlib.ExitStack()
        self.P = Prog(self.nc)
        self.P.dram_names = set()

    def din(self, name, shape, dt):
        self.P.dram_names.add(name)
        return self.nc.dram_tensor(name, list(shape), dt, kind="ExternalInput").ap()

    def dout(self, name, shape, dt):
        self.P.dram_names.add(name)
        return self.nc.dram_tensor(name, list(shape), dt, kind="ExternalOutput").ap()

    tag = None

    def dint(self, name, shape, dt):
        self.P.dram_names.add(name)
        return self.nc.dram_tensor(name, list(shape), dt)

    def sb(self, name, shape, dt):
        if self.tag:
            name = name + '@' + self.tag
        return self.es.enter_context(self.nc.sbuf_tensor(name, list(shape), dt))

    def ps(self, name, shape, dt):
        if self.tag:
            name = name + '@' + self.tag
        return self.es.enter_context(self.nc.psum_tensor(name, list(shape), dt))

    @contextlib.contextmanager
    def phase(self, tag):
        old, oldtag = self.es, self.tag
        self.es, self.tag = contextlib.ExitStack(), tag
        try:
            yield
        finally:
            self.P.barrier()
            self.es.close()
            self.es, self.tag = old, oldtag

    def finish(self):
        self.P.emit()
        self.es.close()
        return self.nc


B_NM = 32
DFF = 4096
PAYR = 6176
KS_R0, KW_R0, KC_R0, VC_R0, VS_R0, VW_R0 = 0, 1024, 2048, 3072, 4096, 5136
TP = T + 128


def emit_A(C, l, xsrc, W, G):
    P = C.P
    with C.phase("A%d" % l):
        Wbf = C.sb("Wbf", [128, 8, INW], BF16)
        wst = [C.sb("wst%d" % i, [128, INW // 2], F32) for i in range(2)]
        gmix_s = C.sb("gmix_s", [128, 8], F32)
        gq_s = C.sb("gq_s", [128, 512], F32)
        gk3_s = C.sb("gk3_s", [128, 3, 128], F32)
        cw_s = C.sb("cw_s", [128, 4, 3], F32)
        xt = [C.sb("xt%d" % i, [128, D], F32) for i in range(2)]
        junk = C.sb("junk", [128, D], BF16)
        ss = C.sb("ss", [128, 1], F32)
        rstd = C.sb("rstd", [128, 1], F32)
        hn = C.sb("hn", [128, D], BF16)
        hT = C.sb("hT", [128, 8, 128], BF16)
        sq = C.sb("sq", [128, 1280], F32)
        ssg = C.sb("ssg", [128, 20], F32)
        rg = C.sb("rg", [128, 20], F32)
        qtmp = C.sb("qtmp", [128, 512], F32)
        ktmp = C.sb("ktmp", [128, 256], F32)
        nrm = C.sb("nrm", [128, 1024], BF16)
        trs = C.sb("trs", [128, 8, 128], BF16)
        vsw = C.sb("vsw", [128, 2, 130], BF16)
        gts = C.sb("gts", [128, 24], F32)
        hcs = C.sb("hcs", [128, 4, 128], F32)
        ub = C.sb("ub", [128, 4, 130], F32)
        cv0 = C.sb("cv0", [128, 4, 128], F32)
        cv1 = C.sb("cv1", [128, 4, 128], F32)
        yv = C.sb("yv", [128, 4, 128], F32)
        ut = C.sb("ut", [128, 4, 2], F32)
        bg2 = C.sb("bg2", [128, 4, 2], F32)
        pT = C.ps("pT", [128, 8, 128], BF16)
        pz = [C.ps("pz%d" % i, [128, 512], F32) for i in range(3)]
        pc = [C.ps("pc%d" % i, [128, 4, 128], F32) for i in range(3)]
        identb = G['identb']

        with P.group('cst'):
            P.dma('sync', gmix_s[:], W['gmix'][l])
            P.dma('sync', gq_s[:], W['gq'][l])
            P.dma('sync', gk3_s[:], W['gk3'][l])
            P.dma('sync', cw_s[:], W['cw'][l])
        P.memset('gpsimd', vsw[:], 1.0)
        P.memset('gpsimd', ub[:], 0.0)
        for kc in range(8):
            for hf in range(2):
                st = wst[hf]
                P.dma('sync' if hf == 0 else 'scalar', st[:], W['w_in'][l, kc * 128:(kc + 1) * 128, hf * 1420:(hf + 1) * 1420])
                P.ts('vector' if hf == 0 else 'gpsimd', Wbf[:, kc, hf * 1420:(hf + 1) * 1420], st[:], gmix_s[:, kc:kc + 1], ALU.mult)

        pay, payu, qsc, gsc, ysc, bg2sc = G['pay'], G['payu'], G['qsc'], G['gsc'], G['ysc'], G['bg2sc']
        for m in range(B_NM):
            xb = xt[m % 2]
            P.dma('sync', xb[:], xsrc[m])
            P.act(junk[:], xb[:], AF.Square, accum_out=ss[:])
            P.act(rstd[:], ss[:], AF.Sqrt, bias=EPS, scale=1.0 / D)
            P.recip(rstd[:], rstd[:])
            P.act(hn[:], xb[:], AF.Copy, scale=rstd[:, 0:1])
            for kc in range(8):
                P.tr(pT[:, kc, :], hn[:, kc * 128:(kc + 1) * 128], identb[:])
            P.copy('vector', hT[:], pT[:])
            for bi, (c0, c1) in enumerate([(0, 512), (512, 1024), (1024, 1304)]):
                for kc in range(8):
                    P.mm(pz[bi][:, 0:c1 - c0], hT[:, kc, :], Wbf[:, kc, c0:c1], start=(kc == 0), stop=(kc == 7))
            for ch in range(12):
                for kc in range(8):
                    P.mm(pc[ch // 4][:, ch % 4, :], Wbf[:, kc, 1304 + ch * 128:1304 + (ch + 1) * 128], hT[:, kc, :],
                         start=(kc == 0), stop=(kc == 7))
            P.act(sq[:, 0:512], pz[0][:], AF.Square)
            P.act(sq[:, 512:1024], pz[1][:], AF.Square)
            P.act(sq[:, 1024:1280], pz[2][:, 0:256], AF.Square)
            P.reduce(ssg[:], sq[:].rearrange("p (g d) -> p g d", d=64))
            P.act(rg[:, 0:8], ssg[:, 0:8], AF.Sqrt, bias=64 * EPS, scale=1.0)
            P.act(rg[:, 8:20], ssg[:, 8:20], AF.Sqrt, bias=EPS, scale=1.0 / 64)
            P.recip(rg[:], rg[:])
            P.tt('vector', qtmp[:].rearrange("p (g d) -> p g d", d=64), pz[0][:].rearrange("p (g d) -> p g d", d=64),
                 AP_(rg, 0, [[20, 128], [1, 8], [0, 64]]), ALU.mult)
            P.tt('gpsimd', nrm[:, 0:512], qtmp[:], gq_s[:], ALU.mult)
            P.tt('vector', ktmp[:, 0:128].rearrange("p (g d) -> p g d", d=64), pz[1][:, 256:384].rearrange("p (g d) -> p g d", d=64),
                 AP_(rg, 12, [[20, 128], [1, 2], [0, 64]]), ALU.mult)
            P.tt('vector', ktmp[:, 128:256].rearrange("p (g d) -> p g d", d=64), pz[2][:, 0:128].rearrange("p (g d) -> p g d", d=64),
                 AP_(rg, 16, [[20, 128], [1, 2], [0, 64]]), ALU.mult)
            P.tt('gpsimd', nrm[:, 512:768], ktmp[:], gk3_s[:, 1:3, :].rearrange("p a d -> p (a d)"), ALU.mult)
            P.copy('vector', nrm[:, 768:1024], pz[1][:, 0:256])
            P.copy('scalar', AP_(vsw, 1, [[260, 128], [65, 2], [1, 64]]), pz[1][:, 384:512].rearrange("p (g d) -> p g d", d=64))
            P.copy('scalar', AP_(vsw, 131, [[260, 128], [65, 2], [1, 64]]), pz[2][:, 128:256].rearrange("p (g d) -> p g d", d=64))
            P.dma('scalar', AP_(pay['vs%d' % (m // 16)], (m % 16) * 128 * 130, [[130, 128], [1, 130]]), vsw[:, 0, :])
            P.dma('scalar', AP_(pay['vw%d' % (m // 16)], (m % 16) * 128 * 130, [[130, 128], [1, 130]]), vsw[:, 1, :])
            P.act(gts[:], pz[2][:, 256:280], AF.Sigmoid)
            P.dma('scalar', gsc[m], gts[:])
            for bk in range(8):
                P.tr(pT[:, bk, :], nrm[:, bk * 128:(bk + 1) * 128], identb[:])
            P.copy('vector', trs[:], pT[:])
            for e in range(2):
                for g in range(2):
                    P.dma('sync' if g == 0 else 'scalar', AP_(qsc, m * 65536 + g * 32768 + e * 128, [[512, 64], [256, 2], [1, 128]]),
                          trs[e * 64:(e + 1) * 64, 2 * g:2 * g + 2, :])
            for ci, cn in enumerate(('ks', 'kw', 'kc', 'vc')):
                P.dma('sync' if ci % 2 == 0 else 'scalar', AP_(pay[cn], m * 128, [[4096, 128], [1, 128]]), trs[:, 4 + ci, :])
            P.copy('scalar', hcs[:], pc[0][:])
            P.tt('vector', ub[:, :, 2:130], pc[1][:], hcs[:], ALU.mult)
            P.tt('gpsimd', cv0[:], ub[:, :, 2:130], AP_(cw_s, 2, [[12, 128], [3, 4], [0, 128]]), ALU.mult)
            P.tt('gpsimd', cv1[:], ub[:, :, 1:129], AP_(cw_s, 1, [[12, 128], [3, 4], [0, 128]]), ALU.mult)
            P.tt('gpsimd', cv0[:], cv0[:], cv1[:], ALU.add)
            P.tt('gpsimd', cv1[:], ub[:, :, 0:128], AP_(cw_s, 0, [[12, 128], [3, 4], [0, 128]]), ALU.mult)
            P.tt('gpsimd', cv0[:], cv0[:], cv1[:], ALU.add)
            P.tt('vector', yv[:], pc[2][:], cv0[:], ALU.mult)
            P.dma('scalar', ysc[m].rearrange("p (c t) -> p c t", c=4), yv[:])
            P.copy('gpsimd', ut[:], ub[:, :, 128:130])
            P.dma('sync', payu[m].rearrange("(p e) -> p e", e=8), ut[:].rearrange("p c e -> p (c e)"))
            P.copy('vector', bg2[:], pc[2][:, :, 0:2])
            P.dma('sync', bg2sc[m], bg2[:].rearrange("p c e -> p (c e)"))


def emit_Z(C, l, W, G):
    P = C.P
    with C.phase("Z%d" % l):
        kcT_all = C.sb("kcT_all", [128, TP], BF16)
        vcT_all = C.sb("vcT_all", [128, TP], BF16)
        W1bf = C.sb("W1bf", [128, 2, 32, 256], BF16)
        w1st = [C.sb("w1st%d" % i, [128, 8, 256], F32) for i in range(2)]
        W2bf = C.sb("W2bf", [128, 2, 2, 64], BF16)
        w2st = C.sb("w2st", [128, 2, 2, 64], F32)
        peT_s = C.sb("peT_s", [64, 2, 32], F32)
        pebf = C.sb("pebf", [64, 2, 32, 128], BF16)
        b1b_s = C.sb("b1b_s", [128, 2, 256], F32)
        b2b_s = C.sb("b2b_s", [128, 2, 64], F32)
        gkc = C.sb("gkc", [128, 128], F32)
        c1b = C.sb("c1b", [128, 2, 256], F32)
        hid = C.sb("hid", [128, 256], F32)
        g_x2 = C.sb("g_x2", [128, 256], F32)
        g_in = C.sb("g_in", [128, 256], F32)
        g_sg = C.sb("g_sg", [128, 256], F32)
        hbf = C.sb("hbf", [128, 256], BF16)
        hidT = C.sb("hidT", [128, 2, 128], BF16)
        co = C.sb("co", [128, 2, 2, 64], F32)
        cosq = C.sb("cosq", [128, 128], F32)
        css = C.sb("css", [128, 2], F32)
        crs = C.sb("crs", [128, 2], F32)
        kcn = C.sb("kcn", [128, 128], BF16)
        pT = C.ps("pT", [128, 8, 128], BF16)
        px = [C.ps("px%d" % i, [128, 512], F32) for i in range(2)]
        po = C.ps("po", [128, 512], F32)
        identb = G['identb']
        gath = G['gath']
        kcT_s, vcx_s = G['kcT_s'], G['vcx_s']

        with P.group('cst'):
            P.dma('sync', b1b_s[:], W['b1b'][l].rearrange("k p n -> p k n"))
            P.dma('sync', b2b_s[:], W['b2b'][l])
            P.dma('sync', peT_s[:], W['peT'][l].rearrange("k d j -> d k j"))
            P.dma('sync', w2st[:], W['w2'][l].rearrange("k (c p) d -> p k c d", p=128))
            P.dma('sync', gkc[:], W['gk3'][l, :, 0, :])
        P.memset('gpsimd', kcT_all[:, T:TP], 0.0)
        P.memset('gpsimd', vcT_all[:, T:TP], 0.0)
        with P.group('kvcl'):
            for r in range(4):
                for kv, dst in enumerate((kcT_all, vcT_all)):
                    P.dma('sync' if kv == 0 else 'scalar', AP_(dst, r * 128, [[TP, 128], [512, 32], [1, 128]]),
                          AP_(gath['kc' if kv == 0 else 'vc'], r * 1024 * 512, [[4096, 128], [128, 32], [1, 128]]),
                          writes=[dst])
        P.copy('vector', W2bf[:], w2st[:])
        P.copy('vector', pebf[:], AP_(peT_s, 0, [[64, 64], [32, 2], [1, 32], [0, 128]]))
        n = 0
        for kv in range(2):
            for jq in range(4):
                st = w1st[n % 2]
                n += 1
                src = W['w1'][l, kv, jq * 512:(jq + 1) * 512, :].rearrange("(j d) n -> d j n", d=64)
                with P.group("w1st%d" % ((n - 1) % 2)):
                    P.dma('sync', st[0:64, :, :], src, writes=[st])
                    P.dma('scalar', st[64:128, :, :], src, writes=[st])
                P.copy('vector' if jq % 2 == 0 else 'gpsimd', W1bf[:, kv, jq * 8:(jq + 1) * 8, :], st[:])
        for kv in range(2):
            for j in range(32):
                P.mm(px[0][:, 0:256], pebf[:, kv, j, :], W1bf[0:64, kv, j, :], start=(j == 0), stop=(j == 31))
            P.tt('vector', c1b[:, kv, :], px[0][:, 0:256], b1b_s[:, kv, :], ALU.add)
        n = 0
        for ib in range(8):
            for kv in range(2):
                src_all = kcT_all if kv == 0 else vcT_all
                for g in range(2):
                    pp = px[n % 2]
                    n += 1
                    base = 16 * 128 * ib
                    for j in range(32):
                        lhs = AP_(src_all, 64 * g * TP + base + j, [[TP, 64], [16, 128]])
                        P.mm(pp[:, 0:256], lhs, W1bf[64 * g:64 * g + 64, kv, j, :], start=(j == 0), stop=(j == 31),
                             reads=[src_all, W1bf])
                    P.tt('vector', hid[:], pp[:, 0:256], c1b[:, kv, :], ALU.add)
                    P.act(g_x2[:], hid[:], AF.Square)
                    P.ts('gpsimd', g_x2[:], g_x2[:], 0.044715, ALU.mult, 1.0, ALU.add)
                    P.tt('gpsimd', g_in[:], g_x2[:], hid[:], ALU.mult)
                    P.act(g_sg[:], g_in[:], AF.Sigmoid, scale=1.5957691216057308)
                    P.tt('vector', hbf[:], hid[:], g_sg[:], ALU.mult)
                    for c in range(2):
                        P.tr(pT[:, c, :], hbf[:, c * 128:(c + 1) * 128], identb[:])
                    P.copy('vector', hidT[:], pT[:, 0:2, :])
                    for c in range(2):
                        P.mm(po[:, 0:64], hidT[:, c, :], W2bf[:, kv, c, :], start=(c == 0), stop=(c == 1))
                    P.tt('vector', co[:, kv, g, :], po[:, 0:64], b2b_s[:, kv, :], ALU.add)
            P.act(cosq[:], co[:, 0, :, :].rearrange("p g d -> p (g d)"), AF.Square)
            P.reduce(css[:], cosq[:].rearrange("p (g d) -> p g d", d=64))
            P.act(crs[:], css[:], AF.Sqrt, bias=EPS, scale=1.0 / 64)
            P.recip(crs[:], crs[:])
            P.tt('vector', cosq[:].rearrange("p (g d) -> p g d", d=64), co[:, 0, :, :], AP_(crs, 0, [[2, 128], [1, 2], [0, 64]]), ALU.mult)
            P.tt('vector', kcn[:], cosq[:], gkc[:], ALU.mult)
            for g in range(2):
                P.tr(pT[0:64, 4 + g, :], kcn[:, 64 * g:64 * g + 64], identb[:])
            P.copy('vector', kcT_s[0:64, :, ib * 128:(ib + 1) * 128], pT[0:64, 4:6, :])
            P.copy('gpsimd', vcx_s[:, ib, :, 0:64], co[:, 1, :, :])
        if G.get('dbg'):
            P.dma('sync', G['dbg']['kc'], kcT_s[:])
            P.dma('sync', G['dbg']['vc'], vcx_s[:])


def emit_B(C, l, W, G):
    P = C.P
    with C.phase("B%d" % l):
        ksT_s = [C.sb("ksT_s%d" % g, [68, T], BF16) for g in range(2)]
        vsx_s = C.sb("vsx_s", [128, 128, 130], BF16)
        smask_s = C.sb("smask_s", [128, 4, 512], BF16)
        cmask_s = C.sb("cmask_s", [128, 5, 512], BF16)
        wmask_s = C.sb("wmask_s", [128, 8, 128], BF16)
        Eall_s = C.sb("Eall_s", [128, 64, 128], BF16)
        Bsel_s = C.sb("Bsel_s", [128, 512], F32)
        qT_s = [C.sb("qT_s%d" % i, [68, 2, 512], BF16) for i in range(2)]
        kwT_s = [C.sb("kwT_s%d" % i, [68, 2, 1024], BF16) for i in range(2)]
        vwx_s = [C.sb("vwx_s%d" % i, [128, 8, 130], BF16) for i in range(2)]
        gates_s = [C.sb("gates_s%d" % i, [128, 24], F32) for i in range(2)]
        NPB = 6
        Pb = [C.sb("Pb%d" % i, [128, 512], BF16) for i in range(NPB)]
        Pc = [C.sb("Pc%d" % g, [128, 8, 512], BF16) for g in range(2)]
        oc = C.sb("oc", [128, 4, 321], F32)
        den4 = C.sb("den4", [128, 4], F32)
        rd4 = C.sb("rd4", [128, 4], F32)
        coef4 = C.sb("coef4", [128, 4], F32)
        imp = [C.sb("imp%d" % g, [128, 256], F32) for g in range(2)]
        tmpi = C.sb("tmpi", [128, 256], F32)
        m8 = C.sb("m8", [128, 16], F32)
        thr = C.sb("thr", [128, 1], F32)
        nm = C.sb("nm", [128, 256], BF16)
        nmT = [C.sb("nmT%d" % g, [128, 2, 128], BF16) for g in range(2)]
        osT = C.sb("osT", [65, 512], F32)
        otmp = C.sb("otmp", [128, 4, 64], F32)
        acc = C.sb("acc", [128, 8, 64], F32)
        accb = C.sb("accb", [128, 512], BF16)
        ssq = C.sb("ssq", [128, 1], F32)
        rat = C.sb("rat", [128, 1], F32)
        rat2 = C.sb("rat2", [128, 1], F32)
        atT = C.sb("atT", [128, 4, 128], BF16)
        ps_s = [C.ps("ps_s%d" % i, [128, 512], F32) for i in range(4)]
        ps_o = [C.ps("ps_o%d" % i, [128, 512], F32) for i in range(1)]
        ps_acc = [C.ps("ps_acc%d" % i, [128, 512], F32) for i in range(2)]
        pm = C.ps("pm", [128, 512], F32)
        pm_b = pm[:].bitcast(BF16)
        identb, identf = G['identb'], G['identf']
        kcT_s, vcx_s = G['kcT_s'], G['vcx_s']
        gath, qsc, gsc, atsc, rasc = G['gath'], G['qsc'], G['gsc'], G['atsc'], G['rasc']

        with P.group('cst'):
            P.dma('sync', cmask_s[:], W['cmask'])
            P.dma('sync', Bsel_s[:], W['Bsel'])
            P.dma('scalar', wmask_s[:], W['wmask'])
            P.dma('scalar', smask_s[:], W['smask'])
            P.dma('scalar', Eall_s[:], W['Eall'])
        with P.group('ksT'):
            for g in range(2):
                P.dma('gpsimd', ksT_s[g][64:68, :], W['kaug'], writes=[ksT_s[g]])
                for r in range(4):
                    P.dma('sync' if r % 2 == 0 else 'scalar', ksT_s[g][0:64, r * 4096:(r + 1) * 4096],
                          AP_(gath['ks'], r * 1024 * 512 + 64 * g * 4096, [[4096, 64], [1, 4096]]),
                          writes=[ksT_s[g]])
        with P.group('vsx'):
            for r in range(4):
                for hv in range(2):
                    P.dma('gpsimd', vsx_s[:, r * 32 + hv * 16:r * 32 + hv * 16 + 16, :],
                          AP_(gath['vs%d' % hv], r * 520 * 512, [[130, 128], [128 * 130, 16], [1, 130]]), writes=[vsx_s])

        if G.get('dbg'):
            P.dma('sync', G['dbg']['ks'], ksT_s[1][:])
            P.dma('sync', G['dbg']['vs'], vsx_s[:])
        state = {'s': 0, 'p': 0, 'x': 0}

        LA = 4

        def run_stream(tiles):
            n = len(tiles)
            bufs = []
            for i in range(n + LA):
                if i < n:
                    ps = ps_s[state['s'] % 4]
                    pb = Pb[state['p'] % NPB]
                    state['s'] += 1
                    state['p'] += 1
                    tiles[i][0](ps)
                    P.act(pb[:], ps[:], AF.Exp)
                    bufs.append(pb)
                j = i - (LA - 2)
                if 0 <= j < n and tiles[j][1] is not None:
                    tiles[j][1](bufs[j])
                if i - LA >= 0:
                    tiles[i - LA][2](bufs[i - LA])

        def mask_mul(pb, map_, mkeys):
            P.tt('vector', pb[:].rearrange("p (a q) -> p a q", a=4), pb[:].rearrange("p (a q) -> p a q", a=4), map_,
                 ALU.mult, reads=[pb] + mkeys, writes=[pb])

        def o_epilogue(psacc, g, br, gs):
            P.copy('vector', osT[:], psacc[0:65, :])
            pmv = pm[:, 0:260].rearrange("p (r d) -> p r d", d=65)
            for r in range(4):
                P.tr(pmv[:, r, :], osT[0:65, r * 128:(r + 1) * 128], identf[0:65, 0:65])
            P.recip(rd4[:], pmv[:, :, 0])
            P.tt('vector', coef4[:], rd4[:], AP_(gs, 12 * g + br, [[24, 128], [3, 4]]), ALU.mult)
            dst = acc[:, 4 * g:4 * g + 4, :]
            P.tt('vector', otmp[:], pmv[:, :, 1:65], AP_(coef4, 0, [[4, 128], [1, 4], [0, 64]]), ALU.mult)
            P.tt('gpsimd', dst, dst, otmp[:], ALU.add)

        for m in range(B_NM):
            qs = qT_s[m % 2]
            kws = kwT_s[m % 2]
            vws = vwx_s[m % 2]
            gs = gates_s[m % 2]
            h0 = 0 if m > 0 else 1
            with P.group("qT_s%d" % (m % 2)):
                P.dma('sync', qs[0:64, :, :], qsc[m].rearrange("g d n -> d g n"), writes=[qs])
                P.dma('scalar', qs[64:68, :, :], W['qaug'][m], writes=[qs])
            P.dma('scalar', gs[:], gsc[m])
            nh = 2 - h0
            c0 = 128 * (m - 1 + h0)
            with P.group("kws%d" % (m % 2)):
                for r in range(4):
                    P.dma('sync' if r % 2 == 0 else 'scalar',
                          AP_(kws, (r * 2 + h0) * 128, [[2048, 64], [1024, 2], [1, 128 * nh]]),
                          AP_(gath['kw'], r * 1024 * 512 + c0, [[4096, 64], [64 * 4096, 2], [1, 128 * nh]]),
                          writes=[kws])
            with P.group("vws%d" % (m % 2)):
                for r in range(4):
                    for hf in range(h0, 2):
                        mp = m - 1 + hf
                        P.dma('gpsimd', vws[:, r * 2 + hf, :],
                              AP_(gath['vw%d' % (mp // 16)], r * 520 * 512 + (mp % 16) * 128 * 130, [[130, 128], [1, 130]]),
                              writes=[vws])
            for g in range(2):
                P.copy('gpsimd', AP_(kws, 64 * 2048 + g * 1024 + h0 * 128, [[2048, 4], [256, 4], [1, 128 * (2 - h0)]]),
                       AP_(ksT_s[0], 64 * T + 128 * (m - 1 + h0), [[T, 4], [4096, 4], [1, 128 * (2 - h0)]]),
                       reads=[ksT_s[0]], writes=[kws])
            n_it = (32 * m + 30) // 128 + 1
            for g in range(2):
                for it in range(n_it):
                    ps = ps_s[state['s'] % 4]
                    state['s'] += 1
                    delta = 128 * it - 32 * m
                    masked = delta >= -128
                    P.mm(ps[:], kcT_s[:, g, it * 128:(it + 1) * 128], qs[:, g, :], start=True, stop=not masked)
                    if masked:
                        P.mm(ps[:], identb[:], cmask_s[:, (-delta) // 32, :], start=False, stop=True)
                    P.act(Pc[g][:, it, :], ps[:], AF.Exp)
                for r in range(4):
                    po = ps_o[0]
                    for it in range(n_it):
                        P.mm(po[:, 0:321], Pc[g][:, it, r * 128:(r + 1) * 128], vcx_s[:, it, g, :], start=(it == 0), stop=(it == n_it - 1))
                    P.copy('scalar', oc[:, r, :], po[:, 0:321])
                P.ts('vector', den4[:], oc[:, :, 64], 1e-30, ALU.max)
                P.recip(rd4[:], den4[:])
                P.ts('vector', imp[g][:], oc[:, 0, 65:321], rd4[:, 0:1], ALU.mult)
                for r in range(1, 4):
                    P.stt(imp[g][:], oc[:, r, 65:321], rd4[:, r:r + 1], imp[g][:], ALU.mult, ALU.add)
                P.tt('vector', coef4[:], rd4[:], AP_(gs, 12 * g + 0, [[24, 128], [3, 4]]), ALU.mult)
                P.tt('vector', acc[:, 4 * g:4 * g + 4, :], oc[:, :, 0:64], AP_(coef4, 0, [[4, 128], [1, 4], [0, 64]]), ALU.mult)
                P.tt('vector', imp[g][:], imp[g][:], Bsel_s[:, 256 - 8 * m:512 - 8 * m], ALU.add)
                P.ts('vector', imp[g][:, 0:1], imp[g][:, 0:1], 1e4, ALU.add)
                P.op('vector', lambda e, g=g: e.max(out=m8[:, 0:8], in_=imp[g][:]), reads=[imp[g]], writes=[m8])
                P.op('vector', lambda e, g=g: e.match_replace(out=tmpi[:], in_to_replace=m8[:, 0:8], in_values=imp[g][:], imm_value=-3.0e38),
                     reads=[imp[g], m8], writes=[tmpi])
                P.op('vector', lambda e: e.max(out=m8[:, 8:16], in_=tmpi[:]), reads=[tmpi], writes=[m8])
                P.ts('vector', thr[:], m8[:, 15:16], -1e29, ALU.max)
                P.ts('vector', nm[:], imp[g][:], thr[:, 0:1], ALU.is_ge)
                for hb in range(2):
                    P.tr(pm_b[:, hb * 128:(hb + 1) * 128], nm[:, hb * 128:(hb + 1) * 128], identb[:])
                P.copy('vector', nmT[g][:].rearrange("p a q -> p (a q)"), pm_b[:, 0:256], reads=[pm])
            tiles = []
            wt = [(r, hf) for hf in range(h0, 2) for r in range(4)]
            for g in range(2):
                for idx, (r, hf) in enumerate(wt):
                    def eS(ps, g=g, r=r, hf=hf):
                        P.mm(ps[:], kws[:, g, (r * 2 + hf) * 128:(r * 2 + hf + 1) * 128], qs[:, g, :], start=True, stop=(hf == 0))
                        if hf == 1:
                            P.mm(ps[:], identb[:], smask_s[:, r, :], start=False, stop=True)

                    def eX(pb, r=r):
                        mask_mul(pb, AP_(wmask_s, r * 128, [[1024, 128], [0, 4], [1, 128]]), [wmask_s])

                    def ePV(pb, g=g, r=r, hf=hf, idx=idx):
                        P.mm(ps_acc[g][0:65, :], vws[:, r * 2 + hf, 65 * g:65 * g + 65], pb[:], start=(idx == 0), stop=(idx == len(wt) - 1))
                    tiles.append((eS, eX if hf == 0 else None, ePV))
            run_stream(tiles)
            for g in range(2):
                o_epilogue(ps_acc[g], g, 2, gs)
            nkt = 4 * m + 4
            for g in range(2):
                tiles = []
                for kt in range(nkt):
                    def eS(ps, g=g, kt=kt):
                        diag = kt >= 4 * m
                        col = ((kt % 4) * 32 + kt // 4) * 128
                        P.mm(ps[:], ksT_s[g][:, col:col + 128], qs[:, g, :], start=True, stop=not diag)
                        if diag:
                            P.mm(ps[:], identb[:], smask_s[:, kt - 4 * m, :], start=False, stop=True)

                    def eX(pb, g=g, kt=kt):
                        bank = (ps_o[0], pm)[state['x'] % 2]
                        state['x'] += 1
                        P.mm(bank[:, 0:128], Eall_s[:, kt % 64, :], nmT[g][:, kt // 64, :], start=True, stop=True)
                        mask_mul(pb, AP_(bank, 0, [[512, 128], [0, 4], [1, 128]]), [bank])

                    def ePV(pb, g=g, kt=kt):
                        P.mm(ps_acc[g][0:65, :], vsx_s[:, (kt % 4) * 32 + kt // 4, 65 * g:65 * g + 65], pb[:], start=(kt == 0), stop=(kt == nkt - 1))
                    tiles.append((eS, eX, ePV))
                run_stream(tiles)
                o_epilogue(ps_acc[g], g, 1, gs)
            accf = acc[:].rearrange("p h d -> p (h d)")
            P.act(accb[:], accf, AF.Square, accum_out=ssq[:])
            P.act(rat[:], ssq[:], AF.Sqrt, bias=EPS, scale=1.0 / 512)
            P.recip(rat2[:], rat[:])
            P.dma('gpsimd', rasc[m], rat2[:])
            P.copy('gpsimd', accb[:], accf)
            for bk in range(4):
                P.tr(pm_b[:, bk * 128:(bk + 1) * 128], accb[:, bk * 128:(bk + 1) * 128], identb[:])
            P.copy('vector', atT[:].rearrange("p b q -> p (b q)"), pm_b[:, 0:512], reads=[pm])
            P.dma('gpsimd', atsc[m], atT[:].rearrange("p b q -> p (b q)"))
            if G.get('dbg'):
                P.dma('gpsimd', G['dbg']['at'][m], atT[:].rearrange("p b q -> p (b q)"))
                P.dma('gpsimd', G['dbg']['ra'][m], rat2[:])


def emit_C(C, l, xsrc, xdst, W, G):
    P = C.P
    with C.phase("C%d" % l):
        Wo = C.sb("Wo", [128, 8, D], BF16)
        Wdn = C.sb("Wdn", [128, 32, D], BF16)
        wst = [C.sb("wst%d" % i, [128, 8, 512], F32) for i in range(2)]
        Wup = [C.sb("Wup%d" % i, [128, 8, 512], BF16) for i in range(2)]
        gout_s = C.sb("gout_s", [128, 8], F32)
        gffn_s = C.sb("gffn_s", [128, D], F32)
        cw_s = C.sb("cw_s", [128, 4, 3], F32)
        selw_s = C.sb("selw_s", [128, 4], F32)
        xt = [C.sb("xt%d" % i, [128, D], F32) for i in range(1)]
        x1 = [C.sb("x1_%d" % i, [128, D], F32) for i in range(4)]
        at_s = [C.sb("at_s%d" % i, [128, 4, 128], BF16) for i in range(2)]
        yv = [C.sb("yv%d" % i, [128, 4, 128], F32) for i in range(1)]
        bg2 = [C.sb("bg2_%d" % i, [128, 4, 2], F32) for i in range(2)]
        tl = [C.sb("tl%d" % i, [128, 4, 8], F32) for i in range(2)]
        halo = C.sb("halo", [128, 4, 2], F32)
        pt0 = C.sb("pt0", [128, 4], F32)
        pt1 = C.sb("pt1", [128, 4], F32)
        ysq = C.sb("ysq", [128, 4, 128], F32)
        ybf = C.sb("ybf", [128, 4, 128], BF16)
        rc = C.sb("rc", [128, 1], F32)
        rc2 = C.sb("rc2", [128, 1], F32)
        ra_s = [C.sb("ra_s%d" % i, [128, 1], F32) for i in range(2)]
        ss = C.sb("ss", [128, 1], F32)
        rstd = C.sb("rstd", [128, 1], F32)
        h2n = C.sb("h2n", [128, D], BF16)
        h2T = C.sb("h2T", [128, 8, 512], BF16)
        rl = [C.sb("rl%d" % i, [128, 512], F32) for i in range(1)]
        actT = C.sb("actT", [128, 32, 512], BF16)
        pa = [C.ps("pa%d" % i, [128, 512], F32) for i in range(2)]
        pcv = [C.ps("pcv%d" % i, [128, 512], F32) for i in range(2)]
        pu = [C.ps("pu%d" % i, [128, 512], F32) for i in range(2)]
        pT = C.ps("pT", [128, 8, 128], BF16)
        px = C.ps("px", [128, 512], F32)
        identb, onesf = G['identb'], G['onesf']
        gathu, ysc, bg2sc, atsc, rasc = G['gathu'], G['ysc'], G['bg2sc'], G['atsc'], G['rasc']

        with P.group('cst'):
            P.dma('sync', gout_s[:], W['gout'][l])
            P.dma('sync', gffn_s[:], W['gffn'][l])
            P.dma('sync', cw_s[:], W['cw'][l])
            P.dma('sync', selw_s[:], W['selw'])
        n = 0
        for kc in range(8):
            for hf in range(2):
                st = wst[n % 2]
                n += 1
                P.dma('sync' if hf == 0 else 'scalar', st[:, 0, :], W['w_o'][l, kc * 128:(kc + 1) * 128, hf * 512:(hf + 1) * 512], writes=[st])
                P.ts('vector' if hf == 0 else 'gpsimd', Wo[:, kc, hf * 512:(hf + 1) * 512], st[:, 0, :], gout_s[:, kc:kc + 1], ALU.mult, reads=[st, gout_s])
        for f4 in range(8):
            st = wst[n % 2]
            n += 1
            stv = AP_(st, 0, [[4096, 128], [1024, 4], [1, 1024]])
            P.dma('sync' if f4 % 2 == 0 else 'scalar', stv, W['w_dn'][l, f4 * 512:(f4 + 1) * 512, :].rearrange("(a p) n -> p a n", p=128), writes=[st])
            P.copy('vector' if f4 % 2 == 0 else 'gpsimd', Wdn[:, f4 * 4:(f4 + 1) * 4, :], stv, reads=[st])
        up_n = [n]

        for bt in range(B_NM // 4):
            for j in range(4):
                m = bt * 4 + j
                xb = xt[0]
                ab, rab, yb, bgb, tlb = at_s[m % 2], ra_s[m % 2], yv[0], bg2[m % 2], tl[m % 2]
                P.dma('sync', xb[:], xsrc[m])
                P.dma('scalar', ab[:], atsc[m].rearrange("p (b q) -> p b q", b=4))
                P.dma('sync', rab[:], rasc[m])
                P.dma('scalar', yb[:], ysc[m].rearrange("p (c t) -> p c t", c=4))
                P.dma('sync', bgb[:], bg2sc[m].rearrange("p (c e) -> p c e", e=2))
                if m == 0:
                    P.memset('gpsimd', tlb[:, 0, :], 0.0)
                with P.group("tl%d" % (m % 2)):
                    if m > 0:
                        P.dma('gpsimd', tlb[:, 0, :], gathu[3 * B_NM + m - 1].rearrange("(p e) -> p e", e=8), writes=[tlb])
                    for cd in range(1, 4):
                        P.dma('gpsimd', tlb[:, cd, :], gathu[(cd - 1) * B_NM + m].rearrange("(p e) -> p e", e=8), writes=[tlb])
                hf = halo[:].rearrange("p c e -> p (c e)")
                P.ts('vector', hf, tlb[:, 0, :], selw_s[:, 0:1], ALU.mult)
                for cd in range(1, 4):
                    P.stt(hf, tlb[:, cd, :], selw_s[:, cd:cd + 1], hf, ALU.mult, ALU.add)
                P.tt('vector', pt0[:], halo[:, :, 1], cw_s[:, :, 1], ALU.mult)
                P.tt('vector', pt1[:], halo[:, :, 0], cw_s[:, :, 0], ALU.mult)
                P.tt('vector', pt0[:], pt0[:], pt1[:], ALU.add)
                P.tt('vector', pt0[:], pt0[:], bgb[:, :, 0], ALU.mult)
                P.tt('vector', yb[:, :, 0], yb[:, :, 0], pt0[:], ALU.add)
                P.tt('vector', pt1[:], halo[:, :, 1], cw_s[:, :, 0], ALU.mult)
                P.tt('vector', pt1[:], pt1[:], bgb[:, :, 1], ALU.mult)
                P.tt('vector', yb[:, :, 1], yb[:, :, 1], pt1[:], ALU.add)
                P.act(ysq[:], yb[:], AF.Square)
                for ch in range(4):
                    P.mm(px[:, 0:1], ysq[:, ch, :], onesf[:], start=(ch == 0), stop=(ch == 3))
                P.act(rc[:], px[:, 0:1], AF.Sqrt, bias=EPS, scale=1.0 / 512)
                P.recip(rc2[:], rc[:])
                P.copy('gpsimd', ybf[:], yb[:])
                for nh in range(2):
                    for kc in range(4):
                        P.mm(pa[nh][:], ab[:, kc, :], Wo[:, kc, nh * 512:(nh + 1) * 512], start=(kc == 0), stop=(kc == 3))
                    for kc in range(4):
                        P.mm(pcv[nh][:], ybf[:, kc, :], Wo[:, 4 + kc, nh * 512:(nh + 1) * 512], start=(kc == 0), stop=(kc == 3))
                x1b = x1[j]
                for nh in range(2):
                    sl = slice(nh * 512, (nh + 1) * 512)
                    P.stt(x1b[:, sl], pa[nh][:], rab[:, 0:1], xb[:, sl], ALU.mult, ALU.add)
                    P.stt(x1b[:, sl], pcv[nh][:], rc2[:, 0:1], x1b[:, sl], ALU.mult, ALU.add)
                if G.get('dbg'):
                    P.dma('gpsimd', G['dbg']['x1'][m], x1b[:])
                    P.dma('gpsimd', G['dbg']['y'][m], ybf[:].rearrange("p c t -> p (c t)"))
                    P.dma('gpsimd', G['dbg']['rc'][m], rc2[:])
                P.act(h2n[:], x1b[:], AF.Square, accum_out=ss[:])
                P.act(rstd[:], ss[:], AF.Sqrt, bias=EPS, scale=1.0 / D)
                P.recip(rstd[:], rstd[:])
                P.stt(h2n[:], x1b[:], rstd[:, 0:1], gffn_s[:], ALU.mult, ALU.mult)
                for kc in range(8):
                    P.tr(pT[:, kc, :], h2n[:, kc * 128:(kc + 1) * 128], identb[:])
                P.copy('scalar', h2T[:, :, j * 128:(j + 1) * 128], pT[:])
            for u in range(8):
                st = wst[up_n[0] % 2]
                wb = Wup[up_n[0] % 2]
                up_n[0] += 1
                P.dma('sync' if u % 2 == 0 else 'scalar', st[:], W['w_up'][l, :, u * 512:(u + 1) * 512].rearrange("(kc p) n -> p kc n", p=128))
                P.copy('gpsimd' if u % 2 == 0 else 'vector', wb[:], st[:])
                for fl in range(4):
                    f = u * 4 + fl
                    pp = pu[f % 2]
                    for kc in range(8):
                        P.mm(pp[:], wb[:, kc, fl * 128:(fl + 1) * 128], h2T[:, kc, :], start=(kc == 0), stop=(kc == 7))
                    rb = rl[0]
                    P.act(rb[:], pp[:], AF.Relu)
                    P.tt('gpsimd' if f % 2 == 0 else 'vector', actT[:, f, :], rb[:], rb[:], ALU.mult)
            for j in range(4):
                m = bt * 4 + j
                ob = x1[j]
                for nh in range(2):
                    pp = pa[nh] if j % 2 == 0 else pcv[nh]
                    for f in range(32):
                        P.mm(pp[:], actT[:, f, j * 128:(j + 1) * 128], Wdn[:, f, nh * 512:(nh + 1) * 512], start=(f == 0), stop=(f == 31))
                    P.tt('vector', ob[:, nh * 512:(nh + 1) * 512], pp[:], x1[j][:, nh * 512:(nh + 1) * 512], ALU.add)
                P.dma('sync', xdst[m], ob[:])


def build_fused(nlayers=DEPTH, stages="AGZBC", dbg=False):
    C = Ctx()
    P = C.P
    W = {}
    W['x0'] = C.din("x0", [B_NM, 128, D], F32)
    W['w_in'] = C.din("w_in", [nlayers, D, INW], F32)
    W['gmix'] = C.din("gmix", [nlayers, 128, 8], F32)
    W['gq'] = C.din("gq", [nlayers, 128, 512], F32)
    W['gk3'] = C.din("gk3", [nlayers, 128, 3, 128], F32)
    W['cw'] = C.din("cw", [nlayers, 128, 4, 3], F32)
    W['peT'] = C.din("peT", [nlayers, 2, 64, 32], F32)
    W['w1'] = C.din("w1", [nlayers, 2, 2048, 256], F32)
    W['b1b'] = C.din("b1b", [nlayers, 2, 128, 256], F32)
    W['w2'] = C.din("w2", [nlayers, 2, 256, 64], F32)
    W['b2b'] = C.din("b2b", [nlayers, 128, 2, 64], F32)
    W['w_o'] = C.din("w_o", [nlayers, D, D], F32)
    W['gout'] = C.din("gout", [nlayers, 128, 8], F32)
    W['gffn'] = C.din("gffn", [nlayers, 128, D], F32)
    W['w_up'] = C.din("w_up", [nlayers, D, DFF], F32)
    W['w_dn'] = C.din("w_dn", [nlayers, DFF, D], F32)
    W['smask'] = C.din("smask", [128, 4, 512], BF16)
    W['cmask'] = C.din("cmask", [128, 5, 512], BF16)
    W['wmask'] = C.din("wmask", [128, 8, 128], BF16)
    W['Eall'] = C.din("Eall", [128, 64, 128], BF16)
    W['Bsel'] = C.din("Bsel", [128, 512], F32)
    W['kaug'] = C.din("kaug", [4, T], BF16)
    W['kcaug'] = C.din("kcaug", [4, 2, 1024], BF16)
    W['qaug'] = C.din("qaug", [B_NM, 4, 2, 512], BF16)
    W['vcxc'] = C.din("vcxc", [128, 8, 2, 257], BF16)
    W['selw'] = C.din("selw", [128, 4], F32)
    identb_d = C.din("identb", [128, 128], BF16)
    identf_d = C.din("identf", [128, 128], F32)
    onesf_d = C.din("onesf", [128, 1], F32)
    xo = C.dout("xo", [B_NM, 128, D], F32)

    G = {}
    COMPS = [('ks', 1024), ('kw', 1024), ('kc', 1024), ('vc', 1024), ('vs0', 520), ('vs1', 520), ('vw0', 520), ('vw1', 520)]
    G['pay'] = {cn: C.dint("pay_" + cn, [rows, 512], BF16) for cn, rows in COMPS}
    G['gath'] = {cn: C.dint("gath_" + cn, [4 * rows, 512], BF16) for cn, rows in COMPS}
    G['payu'] = C.dint("payu", [B_NM, 1024], F32).ap()
    G['gathu'] = C.dint("gathu", [4 * B_NM, 1024], F32).ap()
    xbuf = C.dint("xbuf", [B_NM, 128, D], F32).ap()
    G['qsc'] = C.dint("qsc", [B_NM, 2, 64, 512], BF16)
    G['gsc'] = C.dint("gsc", [B_NM, 128, 24], F32).ap()
    G['ysc'] = C.dint("ysc", [B_NM, 128, 512], F32).ap()
    G['bg2sc'] = C.dint("bg2sc", [B_NM, 128, 8], F32).ap()
    G['atsc'] = C.dint("atsc", [B_NM, 128, 512], BF16).ap()
    G['rasc'] = C.dint("rasc", [B_NM, 128, 1], F32).ap()
    qsc_ap = G['qsc'].ap()
    if dbg:
        G['dbg'] = {'at': C.dout("dbg_at", [B_NM, 128, 512], BF16), 'ra': C.dout("dbg_ra", [B_NM, 128, 1], F32),
                    'x1': C.dout("dbg_x1", [B_NM, 128, D], F32), 'y': C.dout("dbg_y", [B_NM, 128, 512], BF16),
                    'rc': C.dout("dbg_rc", [B_NM, 128, 1], F32),
                    'kc': C.dout("dbg_kc", [68, 2, 1024], BF16), 'vc': C.dout("dbg_vc", [128, 8, 2, 321], BF16),
                    'ks': C.dout("dbg_ks", [68, T], BF16), 'vs': C.dout("dbg_vs", [128, 128, 130], BF16)}

    G['identb'] = C.sb("identb_s", [128, 128], BF16)
    G['identf'] = C.sb("identf_s", [128, 128], F32)
    G['onesf'] = C.sb("onesf_s", [128, 1], F32)
    with P.group('cst'):
        P.dma('sync', G['identb'][:], identb_d)
        P.dma('sync', G['identf'][:], identf_d)
        P.dma('sync', G['onesf'][:], onesf_d)
    P.barrier()
    rg = [[0, 1, 2, 3], [4, 5, 6, 7]]
    def mk_cc(i_ap, o_ap):
        return lambda e: e.collective_compute("AllGather", ALU.bypass, replica_groups=rg, ins=[i_ap.opt()], outs=[o_ap.opt()])
    cc_fns = [mk_cc(G['pay'][cn].ap(), G['gath'][cn].ap()) for cn, _ in COMPS] + [mk_cc(G['payu'], G['gathu'])]
    for l in range(nlayers):
        xsrc = W['x0'] if l == 0 else xbuf
        xdst = xo if l == nlayers - 1 else xbuf
        if 'A' in stages:
            emit_A(C, l, xsrc, W, G)
        if 'G' in stages:
            P.collectives(cc_fns)
        with C.phase("ZB%d" % l):
            G['kcT_s'] = C.sb("kcT_s", [68, 2, 1024], BF16)
            G['vcx_s'] = C.sb("vcx_s", [128, 8, 2, 321], BF16)
            with P.group('cst2'):
                P.dma('sync', G['kcT_s'][64:68, :, :], W['kcaug'])
                P.dma('scalar', G['vcx_s'][:, :, :, 64:321], W['vcxc'])
            if 'Z' in stages:
                emit_Z(C, l, W, G)
            Gb = dict(G)
            Gb['qsc'] = qsc_ap
            if 'B' in stages:
                emit_B(C, l, W, Gb)
        if 'C' in stages:
            emit_C(C, l, xsrc, xdst, W, G)
    if 'C' not in stages:
        with C.phase("dbg"):
            xt = [C.sb("xt%d" % i, [128, D], F32) for i in range(2)]
            for m in range(B_NM):
                P.dma('sync', xt[m % 2][:], W['x0'][m])
                P.dma('sync', xo[m], xt[m % 2][:])
    return C.finish()


IDENTB = np.eye(128, dtype=np.float32).astype(NPBF)
ONESF = np.ones((128, 1), np.float32)


def _bf(a):
    return np.ascontiguousarray(a).astype(NPBF)


def _selmap():
    i = np.arange(1024)[:, None] * 16
    j = np.arange(256)[None, :] * 64
    sh = np.clip(np.minimum(i + 32, j + 64) - np.maximum(i, j), 0, None).astype(np.float32) / 32.0
    sh[1023] = 0.0
    return sh


def _pos_rows(pos):
    pos = np.maximum(pos, 0)
    return np.stack([np.ones_like(pos), np.ones_like(pos), pos % 128, pos // 128]).astype(np.float32)


def core_consts(s):
    ki = np.arange(128)[:, None]
    qi = np.arange(128)[None, :]
    tri_gt = np.where(ki > qi, NEGM, 0.0).astype(np.float32)
    tri_le = np.where(ki <= qi, NEGM, 0.0).astype(np.float32)
    full = np.full((128, 128), NEGM, np.float32)
    zero = np.zeros((128, 128), np.float32)
    smask = np.tile(np.stack([zero if d < s else (tri_gt if d == s else full) for d in range(4)], axis=1), (1, 1, 4))
    cm = []
    for v in range(5):
        ip = ki - 32 * v
        vis = (16 * ip + 31) <= (128 * s + qi)
        cm.append(np.where(vis, 0.0, NEGM).astype(np.float32))
    cmask = np.tile(np.stack(cm, axis=1), (1, 1, 4))
    wm = []
    for d in range(8):
        off = s + 4 - d
        if off == 0:
            wm.append((ki <= qi).astype(np.float32))
        elif off == 4:
            wm.append((ki > qi).astype(np.float32))
        elif 0 < off < 4:
            wm.append(np.ones((128, 128), np.float32))
        else:
            wm.append(zero)
    wmask = np.stack(wm, axis=1)
    E = np.zeros((128, 64, 128), np.float32)
    for kt in range(64):
        for k in range(128):
            E[2 * kt + k // 64, kt, k] = 1.0
    r = np.arange(512)[None, :] - 256 - 2 * s
    qq = np.arange(128)[:, None]
    Brel = np.zeros((128, 512), np.float32)
    Brel = np.where(r >= 2, np.float32(-1e30), Brel)
    Brel = np.where(r == 1, np.where(qq >= 64, np.float32(1e4), np.float32(-1e30)), Brel)
    Brel = np.where(r == 0, np.float32(1e4), Brel)
    Brel = np.where((r == -1) & (qq < 64), np.float32(1e4), Brel)
    rr, mm, kk = np.meshgrid(np.arange(4), np.arange(32), np.arange(128), indexing='ij')
    kpos = (128 * (4 * mm + rr) + kk).reshape(-1)
    kaug = _pos_rows(kpos)
    kc = _pos_rows(np.arange(1024) * 16 + 31)
    kcaug = np.stack([kc, kc], axis=1)
    qaug = np.zeros((B_NM, 4, 2, 512), np.float32)
    qv = np.arange(128)
    for m in range(B_NM):
        c = 4 * m + s
        for g in range(2):
            for rh in range(4):
                sl = 2.0 ** (-(4 * g + rh + 1))
                cs = slice(rh * 128, (rh + 1) * 128)
                qaug[m, 0, g, cs] = -sl * qv
                qaug[m, 1, g, cs] = -sl * 128.0 * c
                qaug[m, 2, g, cs] = sl
                qaug[m, 3, g, cs] = sl * 128.0
    sm = _selmap().reshape(8, 128, 256).transpose(1, 0, 2)
    vcxc = np.zeros((128, 8, 2, 257), np.float32)
    vcxc[:, :, :, 0] = 1.0
    vcxc[127, 7, :, 0] = 0.0
    vcxc[:, :, :, 1:257] = sm[:, :, None, :]
    selw = np.zeros((128, 4), np.float32)
    selw[:, s] = 1.0
    return {'smask': _bf(smask), 'cmask': _bf(cmask), 'wmask': _bf(wmask), 'Eall': _bf(E),
            'Bsel': np.ascontiguousarray(Brel.astype(np.float32)), 'kaug': _bf(kaug), 'kcaug': _bf(kcaug),
            'qaug': _bf(qaug), 'vcxc': _bf(vcxc), 'selw': selw,
            'identb': IDENTB, 'identf': np.eye(128, dtype=np.float32), 'onesf': ONESF}


def prep_fused(p, L=DEPTH):
    p = {k: (v if k == 'x' else v[:L]) for k, v in p.items()}
    common = {
        'w_in': np.ascontiguousarray(p['w_in'], dtype=np.float32),
        'gmix': np.ascontiguousarray(p['g_mix_norm'].reshape(L, 8, 128).transpose(0, 2, 1)),
        'gq': np.ascontiguousarray(np.tile(p['g_q'][:, None, :], (1, 128, 8))),
        'gk3': np.ascontiguousarray(np.broadcast_to(np.tile(p['g_k'], (1, 1, 2))[:, None], (L, 128, 3, 128))),
        'cw': np.ascontiguousarray(p['conv_w'].reshape(L, 3, 4, 128).transpose(0, 3, 2, 1)),
        'peT': np.ascontiguousarray(p['pe_cmp'].transpose(0, 1, 3, 2)),
        'w1': np.ascontiguousarray(p['w_cmp1'], dtype=np.float32),
        'b1b': np.ascontiguousarray(np.broadcast_to(p['b_cmp1'][:, :, None, :], (L, 2, 128, 256))),
        'w2': np.ascontiguousarray(p['w_cmp2'], dtype=np.float32),
        'b2b': np.ascontiguousarray(np.broadcast_to(p['b_cmp2'][:, None], (L, 128, 2, 64))),
        'w_o': np.ascontiguousarray(p['w_o'], dtype=np.float32),
        'gout': np.ascontiguousarray(p['g_out'].reshape(L, 8, 128).transpose(0, 2, 1)),
        'gffn': np.ascontiguousarray(np.tile(p['g_ffn_norm'][:, None, :], (1, 128, 1))),
        'w_up': np.ascontiguousarray(p['w_up'], dtype=np.float32),
        'w_dn': np.ascontiguousarray(p['w_down'], dtype=np.float32),
    }
    maps = []
    x = np.ascontiguousarray(p['x'], dtype=np.float32)
    for b in range(NB):
        xv = x[b].reshape(T // 128, 128, D)
        for s in range(4):
            mp = dict(common)
            mp.update(core_consts(s))
            mp['x0'] = np.ascontiguousarray(xv[np.arange(B_NM) * 4 + s])
            maps.append(mp)
    return maps


_PROG = {}


def run_fused(inputs, nlayers=DEPTH):
    p = {k: np.asarray(v) for k, v in inputs.items()}
    if nlayers not in _PROG:
        _PROG[nlayers] = build_fused(nlayers)
    res = run_bass_kernel_spmd(_PROG[nlayers], prep_fused(p, nlayers), core_ids=list(range(NCORES)))
    out = np.empty((NB, T, D), np.float32)
    for b in range(NB):
        ov = out[b].reshape(T // 128, 128, D)
        for s in range(4):
            ov[np.arange(B_NM) * 4 + s] = np.asarray(res.results[4 * b + s]['xo'])
    return out


def kernel(**inputs):
    return run_fused(inputs, DEPTH)
```

```python
import contextlib
import numpy as np
import ml_dtypes
import concourse.bass as bass
import concourse.mybir as mybir
from concourse.bass_utils import run_bass_kernel_spmd

F32 = mybir.dt.float32
BF16 = mybir.dt.bfloat16
AF = mybir.ActivationFunctionType
ALU = mybir.AluOpType
AX = mybir.AxisListType
NPBF = ml_dtypes.bfloat16

ENGS = ['tensor', 'vector', 'scalar', 'gpsimd', 'sync']
NCORES = 8
D = 1024
T = 16384
NB = 2
DEPTH = 4
INW = 2840
EPS = 1e-6
NEGM = -30000.0


def key_of(ap):
    t = getattr(ap, 'tensor', ap)
    return t.name.split('@')[0]


class Prog:
    def __init__(self, nc):
        self.nc = nc
        self.q = {e: [] for e in ENGS}
        self.semcount = {}
        self.res = {}
        self.seen = {e: {} for e in ENGS}

    grp = None

    @contextlib.contextmanager
    def group(self, sem):
        s = 'd_' + sem
        self.grp = {'sem': sem, 's': s, 'start': self.semcount.get(s, 0), 'keys': set(), 'pre': {}}
        try:
            yield
        finally:
            g, self.grp = self.grp, None
            v = self.semcount.get(s, 0)
            for k in g['keys']:
                self.res[k]['w'] = (s, v)

    def _need(self, eng, reads, writes):
        need = {}

        def add(tok):
            if tok is None:
                return
            s, v = tok
            if eng == 'tensor' and s == 'e_tensor':
                return
            if self.grp is not None and s == self.grp['s'] and v > self.grp['start']:
                return
            if need.get(s, 0) < v:
                need[s] = v
        for k in reads:
            st = self.res.get(k)
            if st:
                add(st['w'])
        for k in writes:
            sts = [self.res.get(k)]
            if self.grp is not None:
                if k not in self.grp['pre']:
                    st0 = self.res.get(k)
                    self.grp['pre'][k] = {'w': st0['w'], 'r': dict(st0['r'])} if st0 else None
                sts.append(self.grp['pre'][k])
            for st in sts:
                if st:
                    add(st['w'])
                    for s, v in st['r'].items():
                        add((s, v))
        for s, v in need.items():
            if self.seen[eng].get(s, 0) < v:
                self.q[eng].append(('wait', s, v))
                self.seen[eng][s] = v

    def _update(self, reads, writes, tok):
        s, v = tok
        for k in reads:
            st = self.res.setdefault(k, {'w': None, 'r': {}})
            if st['r'].get(s, 0) < v:
                st['r'][s] = v
        for k in writes:
            self.res[k] = {'w': tok, 'r': {}}

    def op(self, eng, fn, reads=(), writes=()):
        reads = [r if isinstance(r, str) else key_of(r) for r in reads]
        writes = [r if isinstance(r, str) else key_of(r) for r in writes]
        self._need(eng, reads, writes)
        s = 'e_' + eng
        v = self.semcount.get(s, 0) + 1
        self.semcount[s] = v
        self.q[eng].append(('op', fn, s, 1))
        self._update(reads, writes, (s, v))

    def dma(self, eng, out, in_, reads=None, writes=None, sem=None, **kw):
        if reads is None:
            reads = [] if in_.tensor.name in self.dram_names else [in_]
        if writes is None:
            writes = [] if out.tensor.name in self.dram_names else [out]
        assert reads or writes or sem
        reads = [r if isinstance(r, str) else key_of(r) for r in reads]
        writes = [r if isinstance(r, str) else key_of(r) for r in writes]
        if self.grp is not None:
            sem = self.grp['sem']
            self.grp['keys'].update(writes)
        if sem is None:
            sem = writes[0] if writes else 'st_' + reads[0]
        self._need(eng, reads, writes)
        s = 'd_' + sem
        v = self.semcount.get(s, 0) + 16
        self.semcount[s] = v
        self.q[eng].append(('op', lambda e: e.dma_start(out=out, in_=in_, **kw), s, 16))
        self._update(reads, writes, (s, v))

    dram_names = set()

    def mm(self, out, lhsT, rhs, start=True, stop=True, reads=None, writes=None):
        self.op('tensor', lambda e: e.matmul(out, lhsT=lhsT, rhs=rhs, start=start, stop=stop),
                reads=reads if reads is not None else [lhsT, rhs],
                writes=writes if writes is not None else [out])

    def tr(self, out, in_, ident, reads=None, writes=None):
        self.op('tensor', lambda e: e.transpose(out, in_, ident),
                reads=reads if reads is not None else [in_, ident],
                writes=writes if writes is not None else [out])

    def act(self, out, in_, func, bias=None, scale=None, accum_out=None, reads=None, writes=None, eng='scalar'):
        kw = {}
        rd = [in_]
        if bias is not None:
            kw['bias'] = bias
            if not isinstance(bias, (int, float)):
                rd.append(bias)
        if scale is not None:
            kw['scale'] = scale
            if not isinstance(scale, (int, float)):
                rd.append(scale)
        wr = [out]
        if accum_out is not None:
            kw['accum_out'] = accum_out
            wr.append(accum_out)
        self.op('scalar', lambda e: e.activation(out=out, in_=in_, func=func, **kw),
                reads=reads if reads is not None else rd,
                writes=writes if writes is not None else wr)

    def tt(self, eng, out, in0, in1, op, reads=None, writes=None):
        self.op(eng, lambda e: e.tensor_tensor(out=out, in0=in0, in1=in1, op=op),
                reads=reads if reads is not None else [in0, in1],
                writes=writes if writes is not None else [out])

    def ts(self, eng, out, in0, s1, op0, s2=None, op1=None, reads=None, writes=None):
        rd = [in0]
        if not isinstance(s1, (int, float)):
            rd.append(s1)
        if s2 is not None and not isinstance(s2, (int, float)):
            rd.append(s2)
        if op1 is None:
            fn = lambda e: e.tensor_scalar(out=out, in0=in0, scalar1=s1, scalar2=None, op0=op0)
        else:
            fn = lambda e: e.tensor_scalar(out=out, in0=in0, scalar1=s1, scalar2=s2, op0=op0, op1=op1)
        self.op(eng, fn, reads=reads if reads is not None else rd,
                writes=writes if writes is not None else [out])

    def stt(self, out, in0, scalar, in1, op0, op1, reads=None, writes=None):
        rd = [in0, in1]
        if not isinstance(scalar, (int, float)):
            rd.append(scalar)
        self.op('vector', lambda e: e.scalar_tensor_tensor(out=out, in0=in0, scalar=scalar, in1=in1, op0=op0, op1=op1),
                reads=reads if reads is not None else rd,
                writes=writes if writes is not None else [out])

    def copy(self, eng, out, in_, reads=None, writes=None):
        if eng == 'scalar':
            fn = lambda e: e.activation(out=out, in_=in_, func=AF.Copy)
        else:
            fn = lambda e: e.tensor_copy(out=out, in_=in_)
        self.op(eng, fn, reads=reads if reads is not None else [in_],
                writes=writes if writes is not None else [out])

    def recip(self, out, in_):
        self.op('vector', lambda e: e.reciprocal(out=out, in_=in_), reads=[in_], writes=[out])

    def reduce(self, out, in_, op=ALU.add, axis=AX.X):
        self.op('vector', lambda e: e.tensor_reduce(out=out, in_=in_, axis=axis, op=op), reads=[in_], writes=[out])

    def memset(self, eng, ap, val):
        self.op(eng, lambda e: e.memset(ap, val), writes=[ap])

    def barrier(self):
        for e in ENGS:
            for s, v in self.semcount.items():
                if self.seen[e].get(s, 0) < v:
                    self.q[e].append(('wait', s, v))
                    self.seen[e][s] = v
        self.res = {}

    def collectives(self, fns):
        self.barrier()
        s = 'c_cc'
        for fn in fns:
            v = self.semcount.get(s, 0) + 1
            self.semcount[s] = v
            self.q['gpsimd'].append(('op', fn, s, 1))
        self.barrier()

    def emit(self):
        nc = self.nc
        for s, v in self.semcount.items():
            if self.seen['sync'].get(s, 0) < v:
                self.q['sync'].append(('wait', s, v))
        with contextlib.ExitStack() as es:
            semh = {s: es.enter_context(nc.semaphore(s)) for s in self.semcount}
            block = es.enter_context(nc.Block())

            def mk(engname):
                def body(e):
                    for it in self.q[engname]:
                        if it[0] == 'wait':
                            e.wait_ge(semh[it[1]], it[2])
                        else:
                            ins = it[1](e)
                            ins.then_inc(semh[it[2]], it[3])
                return body
            for engname in ENGS:
                if self.q[engname]:
                    getattr(block, engname)(mk(engname))


def AP_(t, offset, dims):
    return bass.AP(t, offset, [list(d) for d in dims])


class Ctx:
    def __init__(self):
        self.nc = bass.Bass("TRN2", target_bir_lowering=False)
        self.es = contextlib.ExitStack()
        self.P = Prog(self.nc)
        self.P.dram_names = set()

    def din(self, name, shape, dt):
        self.P.dram_names.add(name)
        return self.nc.dram_tensor(name, list(shape), dt, kind="ExternalInput").ap()

    def dout(self, name, shape, dt):
        self.P.dram_names.add(name)
        return self.nc.dram_tensor(name, list(shape), dt, kind="ExternalOutput").ap()

    tag = None

    def dint(self, name, shape, dt):
        self.P.dram_names.add(name)
        return self.nc.dram_tensor(name, list(shape), dt)

    def sb(self, name, shape, dt):
        if self.tag:
            name = name + '@' + self.tag
        return self.es.enter_context(self.nc.sbuf_tensor(name, list(shape), dt))

    def ps(self, name, shape, dt):
        if self.tag:
            name = name + '@' + self.tag
        return self.es.enter_context(self.nc.psum_tensor(name, list(shape), dt))

    @contextlib.contextmanager
    def phase(self, tag):
        old, oldtag = self.es, self.tag
        self.es, self.tag = contextlib.ExitStack(), tag
        try:
            yield
        finally:
            self.P.barrier()
            self.es.close()
            self.es, self.tag = old, oldtag

    def finish(self):
        self.P.emit()
        self.es.close()
        return self.nc


B_NM = 32
DFF = 4096
PAYR = 6176
KS_R0, KW_R0, KC_R0, VC_R0, VS_R0, VW_R0 = 0, 1024, 2048, 3072, 4096, 5136
TP = T + 128


def emit_A(C, l, xsrc, W, G):
    P = C.P
    with C.phase("A%d" % l):
        Wbf = C.sb("Wbf", [128, 8, INW], BF16)
        wst = [C.sb("wst%d" % i, [128, INW // 2], F32) for i in range(2)]
        gmix_s = C.sb("gmix_s", [128, 8], F32)
        gq_s = C.sb("gq_s", [128, 512], F32)
        gk3_s = C.sb("gk3_s", [128, 3, 128], F32)
        cw_s = C.sb("cw_s", [128, 4, 3], F32)
        xt = [C.sb("xt%d" % i, [128, D], F32) for i in range(2)]
        junk = C.sb("junk", [128, D], BF16)
        ss = C.sb("ss", [128, 1], F32)
        rstd = C.sb("rstd", [128, 1], F32)
        hn = C.sb("hn", [128, D], BF16)
        hT = [C.sb("hT%d" % i, [128, 8, 128], BF16) for i in range(2)]
        zs = [C.sb("zs%d" % i, [128, 1304], F32) for i in range(2)]
        hcs = C.sb("hcs", [128, 4, 128], F32)
        ub = [C.sb("ub%d" % i, [128, 4, 130], F32) for i in range(2)]
        bgs = [C.sb("bgs%d" % i, [128, 4, 128], F32) for i in range(2)]
        sq = C.sb("sq", [128, 1280], F32)
        ssg = C.sb("ssg", [128, 20], F32)
        rg = C.sb("rg", [128, 20], F32)
        qtmp = C.sb("qtmp", [128, 512], F32)
        ktmp = C.sb("ktmp", [128, 256], F32)
        nrm = C.sb("nrm", [128, 1024], BF16)
        trs = C.sb("trs", [128, 8, 128], BF16)
        vsw = C.sb("vsw", [128, 2, 130], BF16)
        gts = C.sb("gts", [128, 24], F32)
        cv0 = C.sb("cv0", [128, 4, 128], F32)
        cv1 = C.sb("cv1", [128, 4, 128], F32)
        yv = C.sb("yv", [128, 4, 128], F32)
        ut = C.sb("ut", [128, 4, 2], F32)
        bg2 = C.sb("bg2", [128, 4, 2], F32)
        pTa = C.ps("pTa", [128, 8, 128], BF16)
        pTb = C.ps("pTb", [128, 8, 128], BF16)
        pz = [C.ps("pz%d" % i, [128, 512], F32) for i in range(3)]
        pc = [C.ps("pc%d" % i, [128, 4, 128], F32) for i in range(3)]
        identb = G['identb']

        with P.group('cst'):
            P.dma('sync', gmix_s[:], W['gmix'][l])
            P.dma('sync', gq_s[:], W['gq'][l])
            P.dma('sync', gk3_s[:], W['gk3'][l])
            P.dma('sync', cw_s[:], W['cw'][l])
        P.memset('gpsimd', vsw[:], 1.0)
        P.memset('gpsimd', ub[0][:], 0.0)
        P.memset('gpsimd', ub[1][:], 0.0)
        for kc in range(8):
            for hf in range(2):
                st = wst[hf]
                P.dma('sync' if hf == 0 else 'scalar', st[:], W['w_in'][l, kc * 128:(kc + 1) * 128, hf * 1420:(hf + 1) * 1420])
                P.ts('vector' if hf == 0 else 'gpsimd', Wbf[:, kc, hf * 1420:(hf + 1) * 1420], st[:], gmix_s[:, kc:kc + 1], ALU.mult)

        pay, payu, qsc, gsc, ysc, bg2sc = G['pay'], G['payu'], G['qsc'], G['gsc'], G['ysc'], G['bg2sc']

        def stage_F(m):
            xb, hTb, zb, ubb, bgb = xt[m % 2], hT[m % 2], zs[m % 2], ub[m % 2], bgs[m % 2]
            P.dma('sync', xb[:], xsrc[m])
            P.act(junk[:], xb[:], AF.Square, accum_out=ss[:])
            P.act(rstd[:], ss[:], AF.Sqrt, bias=EPS, scale=1.0 / D)
            P.recip(rstd[:], rstd[:])
            P.act(hn[:], xb[:], AF.Copy, scale=rstd[:, 0:1])
            for kc in range(8):
                P.tr(pTa[:, kc, :], hn[:, kc * 128:(kc + 1) * 128], identb[:])
            P.copy('vector', hTb[:], pTa[:])
            for bi, (c0, c1) in enumerate([(0, 512), (512, 1024), (1024, 1304)]):
                for kc in range(8):
                    P.mm(pz[bi][:, 0:c1 - c0], hTb[:, kc, :], Wbf[:, kc, c0:c1], start=(kc == 0), stop=(kc == 7))
                P.copy('scalar', zb[:, c0:c1], pz[bi][:, 0:c1 - c0])
            for ch in range(12):
                for kc in range(8):
                    P.mm(pc[ch // 4][:, ch % 4, :], Wbf[:, kc, 1304 + ch * 128:1304 + (ch + 1) * 128], hTb[:, kc, :],
                         start=(kc == 0), stop=(kc == 7))
                if ch == 3:
                    P.copy('scalar', hcs[:], pc[0][:])
                if ch == 7:
                    P.tt('vector', ubb[:, :, 2:130], pc[1][:], hcs[:], ALU.mult)
            P.copy('vector', bgb[:], pc[2][:])

        def stage_G(m):
            zb, ubb, bgb = zs[m % 2], ub[m % 2], bgs[m % 2]
            P.act(sq[:], zb[:, 0:1280], AF.Square)
            P.reduce(ssg[:], sq[:].rearrange("p (g d) -> p g d", d=64))
            P.act(rg[:, 0:8], ssg[:, 0:8], AF.Sqrt, bias=64 * EPS, scale=1.0)
            P.act(rg[:, 8:20], ssg[:, 8:20], AF.Sqrt, bias=EPS, scale=1.0 / 64)
            P.recip(rg[:], rg[:])
            P.tt('vector', qtmp[:].rearrange("p (g d) -> p g d", d=64), zb[:, 0:512].rearrange("p (g d) -> p g d", d=64),
                 AP_(rg, 0, [[20, 128], [1, 8], [0, 64]]), ALU.mult)
            P.tt('gpsimd', nrm[:, 0:512], qtmp[:], gq_s[:], ALU.mult)
            P.tt('vector', ktmp[:, 0:128].rearrange("p (g d) -> p g d", d=64), zb[:, 768:896].rearrange("p (g d) -> p g d", d=64),
                 AP_(rg, 12, [[20, 128], [1, 2], [0, 64]]), ALU.mult)
            P.tt('vector', ktmp[:, 128:256].rearrange("p (g d) -> p g d", d=64), zb[:, 1024:1152].rearrange("p (g d) -> p g d", d=64),
                 AP_(rg, 16, [[20, 128], [1, 2], [0, 64]]), ALU.mult)
            P.tt('gpsimd', nrm[:, 512:768], ktmp[:], gk3_s[:, 1:3, :].rearrange("p a d -> p (a d)"), ALU.mult)
            P.copy('gpsimd', nrm[:, 768:1024], zb[:, 512:768])
            P.copy('gpsimd', AP_(vsw, 1, [[260, 128], [65, 2], [1, 64]]), zb[:, 896:1024].rearrange("p (g d) -> p g d", d=64))
            P.copy('gpsimd', AP_(vsw, 131, [[260, 128], [65, 2], [1, 64]]), zb[:, 1152:1280].rearrange("p (g d) -> p g d", d=64))
            P.dma('scalar', AP_(pay['vs%d' % (m // 16)], (m % 16) * 128 * 130, [[130, 128], [1, 130]]), vsw[:, 0, :])
            P.dma('scalar', AP_(pay['vw%d' % (m // 16)], (m % 16) * 128 * 130, [[130, 128], [1, 130]]), vsw[:, 1, :])
            P.act(gts[:], zb[:, 1280:1304], AF.Sigmoid)
            P.dma('scalar', gsc[m], gts[:])
            for bk in range(8):
                P.tr(pTb[:, bk, :], nrm[:, bk * 128:(bk + 1) * 128], identb[:])
            P.copy('vector', trs[:], pTb[:])
            for e in range(2):
                for g in range(2):
                    P.dma('sync' if g == 0 else 'scalar', AP_(qsc, m * 65536 + g * 32768 + e * 128, [[512, 64], [256, 2], [1, 128]]),
                          trs[e * 64:(e + 1) * 64, 2 * g:2 * g + 2, :])
            for ci, cn in enumerate(('ks', 'kw', 'kc', 'vc')):
                P.dma('sync' if ci % 2 == 0 else 'scalar', AP_(pay[cn], m * 128, [[4096, 128], [1, 128]]), trs[:, 4 + ci, :])
            P.tt('gpsimd', cv0[:], ubb[:, :, 2:130], AP_(cw_s, 2, [[12, 128], [3, 4], [0, 128]]), ALU.mult)
            P.tt('gpsimd', cv1[:], ubb[:, :, 1:129], AP_(cw_s, 1, [[12, 128], [3, 4], [0, 128]]), ALU.mult)
            P.tt('gpsimd', cv0[:], cv0[:], cv1[:], ALU.add)
            P.tt('gpsimd', cv1[:], ubb[:, :, 0:128], AP_(cw_s, 0, [[12, 128], [3, 4], [0, 128]]), ALU.mult)
            P.tt('gpsimd', cv0[:], cv0[:], cv1[:], ALU.add)
            P.tt('vector', yv[:], bgb[:], cv0[:], ALU.mult)
            P.dma('scalar', ysc[m].rearrange("p (c t) -> p c t", c=4), yv[:])
            P.copy('gpsimd', ut[:], ubb[:, :, 128:130])
            P.dma('sync', payu[m].rearrange("(p e) -> p e", e=8), ut[:].rearrange("p c e -> p (c e)"))
            P.copy('gpsimd', bg2[:], bgb[:, :, 0:2])
            P.dma('sync', bg2sc[m], bg2[:].rearrange("p c e -> p (c e)"))

        stage_F(0)
        for m in range(B_NM):
            if m + 1 < B_NM:
                stage_F(m + 1)
            stage_G(m)


def emit_Z(C, l, W, G):
    P = C.P
    with C.phase("Z%d" % l):
        kcT_all = C.sb("kcT_all", [128, TP], BF16)
        vcT_all = C.sb("vcT_all", [128, TP], BF16)
        W1bf = C.sb("W1bf", [128, 2, 32, 256], BF16)
        w1st = [C.sb("w1st%d" % i, [128, 8, 256], F32) for i in range(2)]
        W2bf = C.sb("W2bf", [128, 2, 2, 64], BF16)
        w2st = C.sb("w2st", [128, 2, 2, 64], F32)
        peT_s = C.sb("peT_s", [64, 2, 32], F32)
        pebf = C.sb("pebf", [64, 2, 32, 128], BF16)
        b1b_s = C.sb("b1b_s", [128, 2, 256], F32)
        b2b_s = C.sb("b2b_s", [128, 2, 64], F32)
        gkc = C.sb("gkc", [128, 128], F32)
        c1b = C.sb("c1b", [128, 2, 256], F32)
        hid = C.sb("hid", [128, 256], F32)
        g_x2 = C.sb("g_x2", [128, 256], F32)
        g_in = C.sb("g_in", [128, 256], F32)
        g_sg = C.sb("g_sg", [128, 256], F32)
        hbf = C.sb("hbf", [128, 256], BF16)
        hidT = C.sb("hidT", [128, 2, 128], BF16)
        co = C.sb("co", [128, 2, 2, 64], F32)
        cosq = C.sb("cosq", [128, 128], F32)
        css = C.sb("css", [128, 2], F32)
        crs = C.sb("crs", [128, 2], F32)
        kcn = C.sb("kcn", [128, 128], BF16)
        pT = C.ps("pT", [128, 8, 128], BF16)
        px = [C.ps("px%d" % i, [128, 512], F32) for i in range(2)]
        po = C.ps("po", [128, 512], F32)
        identb = G['identb']
        gath = G['gath']
        kcT_s, vcx_s = G['kcT_s'], G['vcx_s']

        with P.group('cst'):
            P.dma('sync', b1b_s[:], W['b1b'][l].rearrange("k p n -> p k n"))
            P.dma('sync', b2b_s[:], W['b2b'][l])
            P.dma('sync', peT_s[:], W['peT'][l].rearrange("k d j -> d k j"))
            P.dma('sync', w2st[:], W['w2'][l].rearrange("k (c p) d -> p k c d", p=128))
            P.dma('sync', gkc[:], W['gk3'][l, :, 0, :])
        P.memset('gpsimd', kcT_all[:, T:TP], 0.0)
        P.memset('gpsimd', vcT_all[:, T:TP], 0.0)
        with P.group('kvcl'):
            for r in range(4):
                for kv, dst in enumerate((kcT_all, vcT_all)):
                    P.dma('sync' if kv == 0 else 'scalar', AP_(dst, r * 128, [[TP, 128], [512, 32], [1, 128]]),
                          AP_(gath['kc' if kv == 0 else 'vc'], r * 1024 * 512, [[4096, 128], [128, 32], [1, 128]]),
                          writes=[dst])
        P.copy('vector', W2bf[:], w2st[:])
        P.copy('vector', pebf[:], AP_(peT_s, 0, [[64, 64], [32, 2], [1, 32], [0, 128]]))
        n = 0
        for kv in range(2):
            for jq in range(4):
                st = w1st[n % 2]
                n += 1
                src = W['w1'][l, kv, jq * 512:(jq + 1) * 512, :].rearrange("(j d) n -> d j n", d=64)
                with P.group("w1st%d" % ((n - 1) % 2)):
                    P.dma('sync', st[0:64, :, :], src, writes=[st])
                    P.dma('scalar', st[64:128, :, :], src, writes=[st])
                P.copy('vector' if jq % 2 == 0 else 'gpsimd', W1bf[:, kv, jq * 8:(jq + 1) * 8, :], st[:])
        for kv in range(2):
            for j in range(32):
                P.mm(px[0][:, 0:256], pebf[:, kv, j, :], W1bf[0:64, kv, j, :], start=(j == 0), stop=(j == 31))
            P.tt('vector', c1b[:, kv, :], px[0][:, 0:256], b1b_s[:, kv, :], ALU.add)
        n = 0
        for ib in range(8):
            for kv in range(2):
                src_all = kcT_all if kv == 0 else vcT_all
                for g in range(2):
                    pp = px[n % 2]
                    n += 1
                    base = 16 * 128 * ib
                    for j in range(32):
                        lhs = AP_(src_all, 64 * g * TP + base + j, [[TP, 64], [16, 128]])
                        P.mm(pp[:, 0:256], lhs, W1bf[64 * g:64 * g + 64, kv, j, :], start=(j == 0), stop=(j == 31),
                             reads=[src_all, W1bf])
                    P.tt('vector', hid[:], pp[:, 0:256], c1b[:, kv, :], ALU.add)
                    P.act(g_x2[:], hid[:], AF.Square)
                    P.ts('gpsimd', g_x2[:], g_x2[:], 0.044715, ALU.mult, 1.0, ALU.add)
                    P.tt('gpsimd', g_in[:], g_x2[:], hid[:], ALU.mult)
                    P.act(g_sg[:], g_in[:], AF.Sigmoid, scale=1.5957691216057308)
                    P.tt('vector', hbf[:], hid[:], g_sg[:], ALU.mult)
                    for c in range(2):
                        P.tr(pT[:, c, :], hbf[:, c * 128:(c + 1) * 128], identb[:])
                    P.copy('vector', hidT[:], pT[:, 0:2, :])
                    for c in range(2):
                        P.mm(po[:, 0:64], hidT[:, c, :], W2bf[:, kv, c, :], start=(c == 0), stop=(c == 1))
                    P.tt('vector', co[:, kv, g, :], po[:, 0:64], b2b_s[:, kv, :], ALU.add)
            P.act(cosq[:], co[:, 0, :, :].rearrange("p g d -> p (g d)"), AF.Square)
            P.reduce(css[:], cosq[:].rearrange("p (g d) -> p g d", d=64))
            P.act(crs[:], css[:], AF.Sqrt, bias=EPS, scale=1.0 / 64)
            P.recip(crs[:], crs[:])
            P.tt('vector', cosq[:].rearrange("p (g d) -> p g d", d=64), co[:, 0, :, :], AP_(crs, 0, [[2, 128], [1, 2], [0, 64]]), ALU.mult)
            P.tt('vector', kcn[:], cosq[:], gkc[:], ALU.mult)
            for g in range(2):
                P.tr(pT[0:64, 4 + g, :], kcn[:, 64 * g:64 * g + 64], identb[:])
            P.copy('vector', kcT_s[0:64, :, ib * 128:(ib + 1) * 128], pT[0:64, 4:6, :])
            P.copy('gpsimd', vcx_s[:, ib, :, 0:64], co[:, 1, :, :])
        if G.get('dbg'):
            P.dma('sync', G['dbg']['kc'], kcT_s[:])
            P.dma('sync', G['dbg']['vc'], vcx_s[:])


def emit_B(C, l, W, G):
    P = C.P
    with C.phase("B%d" % l):
        ksT_s = [C.sb("ksT_s%d" % g, [68, T], BF16) for g in range(2)]
        vsx_s = C.sb("vsx_s", [128, 128, 130], BF16)
        smask_s = C.sb("smask_s", [128, 4, 512], BF16)
        cmask_s = C.sb("cmask_s", [128, 5, 512], BF16)
        wmask_s = C.sb("wmask_s", [128, 8, 128], BF16)
        Eall_s = C.sb("Eall_s", [128, 64, 128], BF16)
        Bsel_s = C.sb("Bsel_s", [128, 512], F32)
        qT_s = [C.sb("qT_s%d" % i, [68, 2, 512], BF16) for i in range(2)]
        kwT_s = [C.sb("kwT_s%d" % i, [68, 2, 1024], BF16) for i in range(2)]
        vwx_s = [C.sb("vwx_s%d" % i, [128, 8, 130], BF16) for i in range(2)]
        gates_s = [C.sb("gates_s%d" % i, [128, 24], F32) for i in range(2)]
        NPB = 6
        Pb = [C.sb("Pb%d" % i, [128, 512], BF16) for i in range(NPB)]
        Pc = [C.sb("Pc%d" % g, [128, 8, 512], BF16) for g in range(2)]
        oc = C.sb("oc", [128, 4, 321], F32)
        den4 = C.sb("den4", [128, 4], F32)
        rd4 = C.sb("rd4", [128, 4], F32)
        coef4 = C.sb("coef4", [128, 4], F32)
        imp = [C.sb("imp%d" % g, [128, 256], F32) for g in range(2)]
        tmpi = C.sb("tmpi", [128, 256], F32)
        m8 = C.sb("m8", [128, 16], F32)
        thr = C.sb("thr", [128, 1], F32)
        nm = [C.sb("nm%d" % g, [128, 256], BF16) for g in range(2)]
        nmT = [C.sb("nmT%d" % g, [128, 2, 128], BF16) for g in range(2)]
        osT = C.sb("osT", [65, 512], F32)
        otmp = C.sb("otmp", [128, 4, 64], F32)
        acc = C.sb("acc", [128, 8, 64], F32)
        accb = C.sb("accb", [128, 512], BF16)
        ssq = C.sb("ssq", [128, 1], F32)
        rat = C.sb("rat", [128, 1], F32)
        rat2 = C.sb("rat2", [128, 1], F32)
        atT = C.sb("atT", [128, 4, 128], BF16)
        ps_s = [C.ps("ps_s%d" % i, [128, 512], F32) for i in range(4)]
        ps_o = [C.ps("ps_o%d" % i, [128, 512], F32) for i in range(1)]
        ps_acc = [C.ps("ps_acc%d" % i, [128, 512], F32) for i in range(2)]
        pm = C.ps("pm", [128, 512], F32)
        pm_b = pm[:].bitcast(BF16)
        identb, identf = G['identb'], G['identf']
        kcT_s, vcx_s = G['kcT_s'], G['vcx_s']
        gath, qsc, gsc, atsc, rasc = G['gath'], G['qsc'], G['gsc'], G['atsc'], G['rasc']

        with P.group('cst'):
            P.dma('sync', cmask_s[:], W['cmask'])
            P.dma('sync', Bsel_s[:], W['Bsel'])
            P.dma('scalar', wmask_s[:], W['wmask'])
            P.dma('scalar', smask_s[:], W['smask'])
            P.dma('scalar', Eall_s[:], W['Eall'])
        with P.group('ksT'):
            for g in range(2):
                P.dma('gpsimd', ksT_s[g][64:68, :], W['kaug'], writes=[ksT_s[g]])
                for r in range(4):
                    P.dma('sync' if r % 2 == 0 else 'scalar', ksT_s[g][0:64, r * 4096:(r + 1) * 4096],
                          AP_(gath['ks'], r * 1024 * 512 + 64 * g * 4096, [[4096, 64], [1, 4096]]),
                          writes=[ksT_s[g]])
        with P.group('vsx'):
            for r in range(4):
                for hv in range(2):
                    P.dma('gpsimd', vsx_s[:, r * 32 + hv * 16:r * 32 + hv * 16 + 16, :],
                          AP_(gath['vs%d' % hv], r * 520 * 512, [[130, 128], [128 * 130, 16], [1, 130]]), writes=[vsx_s])

        if G.get('dbg'):
            P.dma('sync', G['dbg']['ks'], ksT_s[1][:])
            P.dma('sync', G['dbg']['vs'], vsx_s[:])
        state = {'s': 0, 'p': 0, 'x': 0}
        pending = []

        LA = 4

        def run_stream(tiles):
            n = len(tiles)
            bufs = []
            for i in range(n + LA):
                if i < n:
                    ps = ps_s[state['s'] % 4]
                    pb = Pb[state['p'] % NPB]
                    state['s'] += 1
                    state['p'] += 1
                    tiles[i][0](ps)
                    P.act(pb[:], ps[:], AF.Exp)
                    bufs.append(pb)
                j = i - (LA - 2)
                if 0 <= j < n and tiles[j][1] is not None:
                    tiles[j][1](bufs[j])
                if i - LA >= 0:
                    tiles[i - LA][2](bufs[i - LA])

        def mask_mul(pb, map_, mkeys):
            P.tt('vector', pb[:].rearrange("p (a q) -> p a q", a=4), pb[:].rearrange("p (a q) -> p a q", a=4), map_,
                 ALU.mult, reads=[pb] + mkeys, writes=[pb])

        def o_epilogue(psacc, g, br, gs):
            P.copy('vector', osT[:], psacc[0:65, :])
            pmv = pm[:, 0:260].rearrange("p (r d) -> p r d", d=65)
            for r in range(4):
                P.tr(pmv[:, r, :], osT[0:65, r * 128:(r + 1) * 128], identf[0:65, 0:65])
            P.recip(rd4[:], pmv[:, :, 0])
            P.tt('vector', coef4[:], rd4[:], AP_(gs, 12 * g + br, [[24, 128], [3, 4]]), ALU.mult)
            dst = acc[:, 4 * g:4 * g + 4, :]
            P.tt('vector', otmp[:], pmv[:, :, 1:65], AP_(coef4, 0, [[4, 128], [1, 4], [0, 64]]), ALU.mult)
            P.tt('gpsimd', dst, dst, otmp[:], ALU.add)

        for m in range(B_NM):
            qs = qT_s[m % 2]
            kws = kwT_s[m % 2]
            vws = vwx_s[m % 2]
            gs = gates_s[m % 2]
            h0 = 0 if m > 0 else 1
            with P.group("qT_s%d" % (m % 2)):
                P.dma('sync', qs[0:64, :, :], qsc[m].rearrange("g d n -> d g n"), writes=[qs])
                P.dma('scalar', qs[64:68, :, :], W['qaug'][m], writes=[qs])
            P.dma('scalar', gs[:], gsc[m])
            nh = 2 - h0
            c0 = 128 * (m - 1 + h0)
            with P.group("kws%d" % (m % 2)):
                for r in range(4):
                    P.dma('sync' if r % 2 == 0 else 'scalar',
                          AP_(kws, (r * 2 + h0) * 128, [[2048, 64], [1024, 2], [1, 128 * nh]]),
                          AP_(gath['kw'], r * 1024 * 512 + c0, [[4096, 64], [64 * 4096, 2], [1, 128 * nh]]),
                          writes=[kws])
            with P.group("vws%d" % (m % 2)):
                for r in range(4):
                    for hf in range(h0, 2):
                        mp = m - 1 + hf
                        P.dma('gpsimd', vws[:, r * 2 + hf, :],
                              AP_(gath['vw%d' % (mp // 16)], r * 520 * 512 + (mp % 16) * 128 * 130, [[130, 128], [1, 130]]),
                              writes=[vws])
            for g in range(2):
                P.copy('gpsimd', AP_(kws, 64 * 2048 + g * 1024 + h0 * 128, [[2048, 4], [256, 4], [1, 128 * (2 - h0)]]),
                       AP_(ksT_s[0], 64 * T + 128 * (m - 1 + h0), [[T, 4], [4096, 4], [1, 128 * (2 - h0)]]),
                       reads=[ksT_s[0]], writes=[kws])
            n_it = (32 * m + 30) // 128 + 1
            for g in range(2):
                for it in range(n_it):
                    ps = ps_s[state['s'] % 4]
                    state['s'] += 1
                    delta = 128 * it - 32 * m
                    masked = delta >= -128
                    P.mm(ps[:], kcT_s[:, g, it * 128:(it + 1) * 128], qs[:, g, :], start=True, stop=not masked)
                    if masked:
                        P.mm(ps[:], identb[:], cmask_s[:, (-delta) // 32, :], start=False, stop=True)
                    P.act(Pc[g][:, it, :], ps[:], AF.Exp)
                for r in range(4):
                    po = ps_o[0]
                    for it in range(n_it):
                        P.mm(po[:, 0:321], Pc[g][:, it, r * 128:(r + 1) * 128], vcx_s[:, it, g, :], start=(it == 0), stop=(it == n_it - 1))
                    P.copy('scalar', oc[:, r, :], po[:, 0:321])
                P.ts('vector', den4[:], oc[:, :, 64], 1e-30, ALU.max)
                P.recip(rd4[:], den4[:])
                P.ts('vector', imp[g][:], oc[:, 0, 65:321], rd4[:, 0:1], ALU.mult)
                for r in range(1, 4):
                    P.stt(imp[g][:], oc[:, r, 65:321], rd4[:, r:r + 1], imp[g][:], ALU.mult, ALU.add)
                P.tt('vector', coef4[:], rd4[:], AP_(gs, 12 * g + 0, [[24, 128], [3, 4]]), ALU.mult)
                P.tt('vector', acc[:, 4 * g:4 * g + 4, :], oc[:, :, 0:64], AP_(coef4, 0, [[4, 128], [1, 4], [0, 64]]), ALU.mult)
                P.tt('vector', imp[g][:], imp[g][:], Bsel_s[:, 256 - 8 * m:512 - 8 * m], ALU.add)
                P.ts('vector', imp[g][:, 0:1], imp[g][:, 0:1], 1e4, ALU.add)
                P.op('vector', lambda e, g=g: e.max(out=m8[:, 0:8], in_=imp[g][:]), reads=[imp[g]], writes=[m8])
                P.op('vector', lambda e, g=g: e.match_replace(out=tmpi[:], in_to_replace=m8[:, 0:8], in_values=imp[g][:], imm_value=-3.0e38),
                     reads=[imp[g], m8], writes=[tmpi])
                P.op('vector', lambda e: e.max(out=m8[:, 8:16], in_=tmpi[:]), reads=[tmpi], writes=[m8])
                P.ts('vector', thr[:], m8[:, 15:16], -1e29, ALU.max)
                P.ts('vector', nm[g][:], imp[g][:], thr[:, 0:1], ALU.is_ge)
            if pending:
                pending.pop()()
            tiles = []
            wt = [(r, hf) for hf in range(h0, 2) for r in range(4)]
            for g in range(2):
                for idx, (r, hf) in enumerate(wt):
                    def eS(ps, g=g, r=r, hf=hf):
                        P.mm(ps[:], kws[:, g, (r * 2 + hf) * 128:(r * 2 + hf + 1) * 128], qs[:, g, :], start=True, stop=(hf == 0))
                        if hf == 1:
                            P.mm(ps[:], identb[:], smask_s[:, r, :], start=False, stop=True)

                    def eX(pb, r=r):
                        mask_mul(pb, AP_(wmask_s, r * 128, [[1024, 128], [0, 4], [1, 128]]), [wmask_s])

                    def ePV(pb, g=g, r=r, hf=hf, idx=idx):
                        P.mm(ps_acc[g][0:65, :], vws[:, r * 2 + hf, 65 * g:65 * g + 65], pb[:], start=(idx == 0), stop=(idx == len(wt) - 1))
                    tiles.append((eS, eX if hf == 0 else None, ePV))
            run_stream(tiles)
            for g in range(2):
                for hb in range(2):
                    P.tr(pm_b[:, hb * 128:(hb + 1) * 128], nm[g][:, hb * 128:(hb + 1) * 128], identb[:])
                P.copy('vector', nmT[g][:].rearrange("p a q -> p (a q)"), pm_b[:, 0:256], reads=[pm])
            for g in range(2):
                o_epilogue(ps_acc[g], g, 2, gs)
            nkt = 4 * m + 4
            for g in range(2):
                tiles = []
                for kt in range(nkt):
                    def eS(ps, g=g, kt=kt):
                        diag = kt >= 4 * m
                        col = ((kt % 4) * 32 + kt // 4) * 128
                        P.mm(ps[:], ksT_s[g][:, col:col + 128], qs[:, g, :], start=True, stop=not diag)
                        if diag:
                            P.mm(ps[:], identb[:], smask_s[:, kt - 4 * m, :], start=False, stop=True)

                    def eX(pb, g=g, kt=kt):
                        bank = (ps_o[0], pm)[state['x'] % 2]
                        state['x'] += 1
                        P.mm(bank[:, 0:128], Eall_s[:, kt % 64, :], nmT[g][:, kt // 64, :], start=True, stop=True)
                        mask_mul(pb, AP_(bank, 0, [[512, 128], [0, 4], [1, 128]]), [bank])

                    def ePV(pb, g=g, kt=kt):
                        P.mm(ps_acc[g][0:65, :], vsx_s[:, (kt % 4) * 32 + kt // 4, 65 * g:65 * g + 65], pb[:], start=(kt == 0), stop=(kt == nkt - 1))
                    tiles.append((eS, eX, ePV))
                run_stream(tiles)
                o_epilogue(ps_acc[g], g, 1, gs)
            accf = acc[:].rearrange("p h d -> p (h d)")
            P.act(accb[:], accf, AF.Square, accum_out=ssq[:])
            P.act(rat[:], ssq[:], AF.Sqrt, bias=EPS, scale=1.0 / 512)
            P.recip(rat2[:], rat[:])
            P.dma('gpsimd', rasc[m], rat2[:])
            P.copy('gpsimd', accb[:], accf)
            if G.get('dbg'):
                P.dma('gpsimd', G['dbg']['ra'][m], rat2[:])

            def finish_out(m=m):
                for bk in range(4):
                    P.tr(pm_b[:, bk * 128:(bk + 1) * 128], accb[:, bk * 128:(bk + 1) * 128], identb[:])
                P.copy('vector', atT[:].rearrange("p b q -> p (b q)"), pm_b[:, 0:512], reads=[pm])
                P.dma('gpsimd', atsc[m], atT[:].rearrange("p b q -> p (b q)"))
                if G.get('dbg'):
                    P.dma('gpsimd', G['dbg']['at'][m], atT[:].rearrange("p b q -> p (b q)"))
            pending.append(finish_out)
        pending.pop()()


def emit_C(C, l, xsrc, xdst, W, G):
    P = C.P
    with C.phase("C%d" % l):
        Wo = C.sb("Wo", [128, 8, D], BF16)
        Wdn = C.sb("Wdn", [128, 32, D], BF16)
        wst = [C.sb("wst%d" % i, [128, 8, 512], F32) for i in range(2)]
        Wup = [C.sb("Wup%d" % i, [128, 8, 512], BF16) for i in range(2)]
        gout_s = C.sb("gout_s", [128, 8], F32)
        gffn_s = C.sb("gffn_s", [128, D], F32)
        cw_s = C.sb("cw_s", [128, 4, 3], F32)
        selw_s = C.sb("selw_s", [128, 4], F32)
        xt = [C.sb("xt%d" % i, [128, D], F32) for i in range(1)]
        x1 = [C.sb("x1_%d" % i, [128, D], F32) for i in range(4)]
        at_s = [C.sb("at_s%d" % i, [128, 4, 128], BF16) for i in range(2)]
        yv = [C.sb("yv%d" % i, [128, 4, 128], F32) for i in range(1)]
        bg2 = [C.sb("bg2_%d" % i, [128, 4, 2], F32) for i in range(2)]
        tl = [C.sb("tl%d" % i, [128, 4, 8], F32) for i in range(2)]
        halo = C.sb("halo", [128, 4, 2], F32)
        pt0 = C.sb("pt0", [128, 4], F32)
        pt1 = C.sb("pt1", [128, 4], F32)
        ysq = C.sb("ysq", [128, 4, 128], F32)
        ybf = C.sb("ybf", [128, 4, 128], BF16)
        rc = C.sb("rc", [128, 1], F32)
        rc2 = C.sb("rc2", [128, 1], F32)
        ra_s = [C.sb("ra_s%d" % i, [128, 1], F32) for i in range(2)]
        ss = C.sb("ss", [128, 1], F32)
        rstd = C.sb("rstd", [128, 1], F32)
        h2n = C.sb("h2n", [128, D], BF16)
        h2T = C.sb("h2T", [128, 8, 512], BF16)
        rl = [C.sb("rl%d" % i, [128, 512], F32) for i in range(1)]
        actT = C.sb("actT", [128, 32, 512], BF16)
        pa = [C.ps("pa%d" % i, [128, 512], F32) for i in range(2)]
        pcv = [C.ps("pcv%d" % i, [128, 512], F32) for i in range(2)]
        pu = [C.ps("pu%d" % i, [128, 512], F32) for i in range(2)]
        pT = C.ps("pT", [128, 8, 128], BF16)
        px = C.ps("px", [128, 512], F32)
        identb, onesf = G['identb'], G['onesf']
        gathu, ysc, bg2sc, atsc, rasc = G['gathu'], G['ysc'], G['bg2sc'], G['atsc'], G['rasc']

        with P.group('cst'):
            P.dma('sync', gout_s[:], W['gout'][l])
            P.dma('sync', gffn_s[:], W['gffn'][l])
            P.dma('sync', cw_s[:], W['cw'][l])
            P.dma('sync', selw_s[:], W['selw'])
        n = 0
        for kc in range(8):
            for hf in range(2):
                st = wst[n % 2]
                n += 1
                P.dma('sync' if hf == 0 else 'scalar', st[:, 0, :], W['w_o'][l, kc * 128:(kc + 1) * 128, hf * 512:(hf + 1) * 512], writes=[st])
                P.ts('vector' if hf == 0 else 'gpsimd', Wo[:, kc, hf * 512:(hf + 1) * 512], st[:, 0, :], gout_s[:, kc:kc + 1], ALU.mult, reads=[st, gout_s])
        for f4 in range(8):
            st = wst[n % 2]
            n += 1
            stv = AP_(st, 0, [[4096, 128], [1024, 4], [1, 1024]])
            P.dma('sync' if f4 % 2 == 0 else 'scalar', stv, W['w_dn'][l, f4 * 512:(f4 + 1) * 512, :].rearrange("(a p) n -> p a n", p=128), writes=[st])
            P.copy('vector' if f4 % 2 == 0 else 'gpsimd', Wdn[:, f4 * 4:(f4 + 1) * 4, :], stv, reads=[st])
        up_n = [n]

        for bt in range(B_NM // 4):
            for j in range(4):
                m = bt * 4 + j
                xb = xt[0]
                ab, rab, yb, bgb, tlb = at_s[m % 2], ra_s[m % 2], yv[0], bg2[m % 2], tl[m % 2]
                P.dma('sync', xb[:], xsrc[m])
                P.dma('scalar', ab[:], atsc[m].rearrange("p (b q) -> p b q", b=4))
                P.dma('sync', rab[:], rasc[m])
                P.dma('scalar', yb[:], ysc[m].rearrange("p (c t) -> p c t", c=4))
                P.dma('sync', bgb[:], bg2sc[m].rearrange("p (c e) -> p c e", e=2))
                if m == 0:
                    P.memset('gpsimd', tlb[:, 0, :], 0.0)
                with P.group("tl%d" % (m % 2)):
                    if m > 0:
                        P.dma('gpsimd', tlb[:, 0, :], gathu[3 * B_NM + m - 1].rearrange("(p e) -> p e", e=8), writes=[tlb])
                    for cd in range(1, 4):
                        P.dma('gpsimd', tlb[:, cd, :], gathu[(cd - 1) * B_NM + m].rearrange("(p e) -> p e", e=8), writes=[tlb])
                hf = halo[:].rearrange("p c e -> p (c e)")
                P.ts('vector', hf, tlb[:, 0, :], selw_s[:, 0:1], ALU.mult)
                for cd in range(1, 4):
                    P.stt(hf, tlb[:, cd, :], selw_s[:, cd:cd + 1], hf, ALU.mult, ALU.add)
                P.tt('vector', pt0[:], halo[:, :, 1], cw_s[:, :, 1], ALU.mult)
                P.tt('vector', pt1[:], halo[:, :, 0], cw_s[:, :, 0], ALU.mult)
                P.tt('vector', pt0[:], pt0[:], pt1[:], ALU.add)
                P.tt('vector', pt0[:], pt0[:], bgb[:, :, 0], ALU.mult)
                P.tt('vector', yb[:, :, 0], yb[:, :, 0], pt0[:], ALU.add)
                P.tt('vector', pt1[:], halo[:, :, 1], cw_s[:, :, 0], ALU.mult)
                P.tt('vector', pt1[:], pt1[:], bgb[:, :, 1], ALU.mult)
                P.tt('vector', yb[:, :, 1], yb[:, :, 1], pt1[:], ALU.add)
                P.act(ysq[:], yb[:], AF.Square)
                for ch in range(4):
                    P.mm(px[:, 0:1], ysq[:, ch, :], onesf[:], start=(ch == 0), stop=(ch == 3))
                P.act(rc[:], px[:, 0:1], AF.Sqrt, bias=EPS, scale=1.0 / 512)
                P.recip(rc2[:], rc[:])
                P.copy('gpsimd', ybf[:], yb[:])
                for nh in range(2):
                    for kc in range(4):
                        P.mm(pa[nh][:], ab[:, kc, :], Wo[:, kc, nh * 512:(nh + 1) * 512], start=(kc == 0), stop=(kc == 3))
                    for kc in range(4):
                        P.mm(pcv[nh][:], ybf[:, kc, :], Wo[:, 4 + kc, nh * 512:(nh + 1) * 512], start=(kc == 0), stop=(kc == 3))
                x1b = x1[j]
                for nh in range(2):
                    sl = slice(nh * 512, (nh + 1) * 512)
                    P.stt(x1b[:, sl], pa[nh][:], rab[:, 0:1], xb[:, sl], ALU.mult, ALU.add)
                    P.stt(x1b[:, sl], pcv[nh][:], rc2[:, 0:1], x1b[:, sl], ALU.mult, ALU.add)
                if G.get('dbg'):
                    P.dma('gpsimd', G['dbg']['x1'][m], x1b[:])
                    P.dma('gpsimd', G['dbg']['y'][m], ybf[:].rearrange("p c t -> p (c t)"))
                    P.dma('gpsimd', G['dbg']['rc'][m], rc2[:])
                P.act(h2n[:], x1b[:], AF.Square, accum_out=ss[:])
                P.act(rstd[:], ss[:], AF.Sqrt, bias=EPS, scale=1.0 / D)
                P.recip(rstd[:], rstd[:])
                P.stt(h2n[:], x1b[:], rstd[:, 0:1], gffn_s[:], ALU.mult, ALU.mult)
                for kc in range(8):
                    P.tr(pT[:, kc, :], h2n[:, kc * 128:(kc + 1) * 128], identb[:])
                P.copy('scalar', h2T[:, :, j * 128:(j + 1) * 128], pT[:])
            for u in range(8):
                st = wst[up_n[0] % 2]
                wb = Wup[up_n[0] % 2]
                up_n[0] += 1
                P.dma('sync' if u % 2 == 0 else 'scalar', st[:], W['w_up'][l, :, u * 512:(u + 1) * 512].rearrange("(kc p) n -> p kc n", p=128))
                P.copy('gpsimd' if u % 2 == 0 else 'vector', wb[:], st[:])
                for fl in range(4):
                    f = u * 4 + fl
                    pp = pu[f % 2]
                    for kc in range(8):
                        P.mm(pp[:], wb[:, kc, fl * 128:(fl + 1) * 128], h2T[:, kc, :], start=(kc == 0), stop=(kc == 7))
                    rb = rl[0]
                    P.act(rb[:], pp[:], AF.Relu)
                    P.tt('gpsimd' if f % 2 == 0 else 'vector', actT[:, f, :], rb[:], rb[:], ALU.mult)
            for j in range(4):
                m = bt * 4 + j
                ob = x1[j]
                for nh in range(2):
                    pp = pa[nh] if j % 2 == 0 else pcv[nh]
                    for f in range(32):
                        P.mm(pp[:], actT[:, f, j * 128:(j + 1) * 128], Wdn[:, f, nh * 512:(nh + 1) * 512], start=(f == 0), stop=(f == 31))
                    P.tt('vector', ob[:, nh * 512:(nh + 1) * 512], pp[:], x1[j][:, nh * 512:(nh + 1) * 512], ALU.add)
                P.dma('sync', xdst[m], ob[:])


def build_fused(nlayers=DEPTH, stages="AGZBC", dbg=False):
    C = Ctx()
    P = C.P
    W = {}
    W['x0'] = C.din("x0", [B_NM, 128, D], F32)
    W['w_in'] = C.din("w_in", [nlayers, D, INW], F32)
    W['gmix'] = C.din("gmix", [nlayers, 128, 8], F32)
    W['gq'] = C.din("gq", [nlayers, 128, 512], F32)
    W['gk3'] = C.din("gk3", [nlayers, 128, 3, 128], F32)
    W['cw'] = C.din("cw", [nlayers, 128, 4, 3], F32)
    W['peT'] = C.din("peT", [nlayers, 2, 64, 32], F32)
    W['w1'] = C.din("w1", [nlayers, 2, 2048, 256], F32)
    W['b1b'] = C.din("b1b", [nlayers, 2, 128, 256], F32)
    W['w2'] = C.din("w2", [nlayers, 2, 256, 64], F32)
    W['b2b'] = C.din("b2b", [nlayers, 128, 2, 64], F32)
    W['w_o'] = C.din("w_o", [nlayers, D, D], F32)
    W['gout'] = C.din("gout", [nlayers, 128, 8], F32)
    W['gffn'] = C.din("gffn", [nlayers, 128, D], F32)
    W['w_up'] = C.din("w_up", [nlayers, D, DFF], F32)
    W['w_dn'] = C.din("w_dn", [nlayers, DFF, D], F32)
    W['smask'] = C.din("smask", [128, 4, 512], BF16)
    W['cmask'] = C.din("cmask", [128, 5, 512], BF16)
    W['wmask'] = C.din("wmask", [128, 8, 128], BF16)
    W['Eall'] = C.din("Eall", [128, 64, 128], BF16)
    W['Bsel'] = C.din("Bsel", [128, 512], F32)
    W['kaug'] = C.din("kaug", [4, T], BF16)
    W['kcaug'] = C.din("kcaug", [4, 2, 1024], BF16)
    W['qaug'] = C.din("qaug", [B_NM, 4, 2, 512], BF16)
    W['vcxc'] = C.din("vcxc", [128, 8, 2, 257], BF16)
    W['selw'] = C.din("selw", [128, 4], F32)
    identb_d = C.din("identb", [128, 128], BF16)
    identf_d = C.din("identf", [128, 128], F32)
    onesf_d = C.din("onesf", [128, 1], F32)
    xo = C.dout("xo", [B_NM, 128, D], F32)

    G = {}
    COMPS = [('ks', 1024), ('kw', 1024), ('kc', 1024), ('vc', 1024), ('vs0', 520), ('vs1', 520), ('vw0', 520), ('vw1', 520)]
    G['pay'] = {cn: C.dint("pay_" + cn, [rows, 512], BF16) for cn, rows in COMPS}
    G['gath'] = {cn: C.dint("gath_" + cn, [4 * rows, 512], BF16) for cn, rows in COMPS}
    G['payu'] = C.dint("payu", [B_NM, 1024], F32).ap()
    G['gathu'] = C.dint("gathu", [4 * B_NM, 1024], F32).ap()
    xbuf = C.dint("xbuf", [B_NM, 128, D], F32).ap()
    G['qsc'] = C.dint("qsc", [B_NM, 2, 64, 512], BF16)
    G['gsc'] = C.dint("gsc", [B_NM, 128, 24], F32).ap()
    G['ysc'] = C.dint("ysc", [B_NM, 128, 512], F32).ap()
    G['bg2sc'] = C.dint("bg2sc", [B_NM, 128, 8], F32).ap()
    G['atsc'] = C.dint("atsc", [B_NM, 128, 512], BF16).ap()
    G['rasc'] = C.dint("rasc", [B_NM, 128, 1], F32).ap()
    qsc_ap = G['qsc'].ap()
    if dbg:
        G['dbg'] = {'at': C.dout("dbg_at", [B_NM, 128, 512], BF16), 'ra': C.dout("dbg_ra", [B_NM, 128, 1], F32),
                    'x1': C.dout("dbg_x1", [B_NM, 128, D], F32), 'y': C.dout("dbg_y", [B_NM, 128, 512], BF16),
                    'rc': C.dout("dbg_rc", [B_NM, 128, 1], F32),
                    'kc': C.dout("dbg_kc", [68, 2, 1024], BF16), 'vc': C.dout("dbg_vc", [128, 8, 2, 321], BF16),
                    'ks': C.dout("dbg_ks", [68, T], BF16), 'vs': C.dout("dbg_vs", [128, 128, 130], BF16)}

    G['identb'] = C.sb("identb_s", [128, 128], BF16)
    G['identf'] = C.sb("identf_s", [128, 128], F32)
    G['onesf'] = C.sb("onesf_s", [128, 1], F32)
    with P.group('cst'):
        P.dma('sync', G['identb'][:], identb_d)
        P.dma('sync', G['identf'][:], identf_d)
        P.dma('sync', G['onesf'][:], onesf_d)
    P.barrier()
    rg = [[0, 1, 2, 3], [4, 5, 6, 7]]
    def mk_cc(i_ap, o_ap):
        return lambda e: e.collective_compute("AllGather", ALU.bypass, replica_groups=rg, ins=[i_ap.opt()], outs=[o_ap.opt()])
    cc_fns = [mk_cc(G['pay'][cn].ap(), G['gath'][cn].ap()) for cn, _ in COMPS] + [mk_cc(G['payu'], G['gathu'])]
    for l in range(nlayers):
        xsrc = W['x0'] if l == 0 else xbuf
        xdst = xo if l == nlayers - 1 else xbuf
        if 'A' in stages:
            emit_A(C, l, xsrc, W, G)
        if 'G' in stages:
            P.collectives(cc_fns)
        with C.phase("ZB%d" % l):
            G['kcT_s'] = C.sb("kcT_s", [68, 2, 1024], BF16)
            G['vcx_s'] = C.sb("vcx_s", [128, 8, 2, 321], BF16)
            with P.group('cst2'):
                P.dma('sync', G['kcT_s'][64:68, :, :], W['kcaug'])
                P.dma('scalar', G['vcx_s'][:, :, :, 64:321], W['vcxc'])
            if 'Z' in stages:
                emit_Z(C, l, W, G)
            Gb = dict(G)
            Gb['qsc'] = qsc_ap
            if 'B' in stages:
                emit_B(C, l, W, Gb)
        if 'C' in stages:
            emit_C(C, l, xsrc, xdst, W, G)
    if 'C' not in stages:
        with C.phase("dbg"):
            xt = [C.sb("xt%d" % i, [128, D], F32) for i in range(2)]
            for m in range(B_NM):
                P.dma('sync', xt[m % 2][:], W['x0'][m])
                P.dma('sync', xo[m], xt[m % 2][:])
    return C.finish()


IDENTB = np.eye(128, dtype=np.float32).astype(NPBF)
ONESF = np.ones((128, 1), np.float32)


def _bf(a):
    return np.ascontiguousarray(a).astype(NPBF)


def _selmap():
    i = np.arange(1024)[:, None] * 16
    j = np.arange(256)[None, :] * 64
    sh = np.clip(np.minimum(i + 32, j + 64) - np.maximum(i, j), 0, None).astype(np.float32) / 32.0
    sh[1023] = 0.0
    return sh


def _pos_rows(pos):
    pos = np.maximum(pos, 0)
    return np.stack([np.ones_like(pos), np.ones_like(pos), pos % 128, pos // 128]).astype(np.float32)


def core_consts(s):
    ki = np.arange(128)[:, None]
    qi = np.arange(128)[None, :]
    tri_gt = np.where(ki > qi, NEGM, 0.0).astype(np.float32)
    tri_le = np.where(ki <= qi, NEGM, 0.0).astype(np.float32)
    full = np.full((128, 128), NEGM, np.float32)
    zero = np.zeros((128, 128), np.float32)
    smask = np.tile(np.stack([zero if d < s else (tri_gt if d == s else full) for d in range(4)], axis=1), (1, 1, 4))
    cm = []
    for v in range(5):
        ip = ki - 32 * v
        vis = (16 * ip + 31) <= (128 * s + qi)
        cm.append(np.where(vis, 0.0, NEGM).astype(np.float32))
    cmask = np.tile(np.stack(cm, axis=1), (1, 1, 4))
    wm = []
    for d in range(8):
        off = s + 4 - d
        if off == 0:
            wm.append((ki <= qi).astype(np.float32))
        elif off == 4:
            wm.append((ki > qi).astype(np.float32))
        elif 0 < off < 4:
            wm.append(np.ones((128, 128), np.float32))
        else:
            wm.append(zero)
    wmask = np.stack(wm, axis=1)
    E = np.zeros((128, 64, 128), np.float32)
    for kt in range(64):
        for k in range(128):
            E[2 * kt + k // 64, kt, k] = 1.0
    r = np.arange(512)[None, :] - 256 - 2 * s
    qq = np.arange(128)[:, None]
    Brel = np.zeros((128, 512), np.float32)
    Brel = np.where(r >= 2, np.float32(-1e30), Brel)
    Brel = np.where(r == 1, np.where(qq >= 64, np.float32(1e4), np.float32(-1e30)), Brel)
    Brel = np.where(r == 0, np.float32(1e4), Brel)
    Brel = np.where((r == -1) & (qq < 64), np.float32(1e4), Brel)
    rr, mm, kk = np.meshgrid(np.arange(4), np.arange(32), np.arange(128), indexing='ij')
    kpos = (128 * (4 * mm + rr) + kk).reshape(-1)
    kaug = _pos_rows(kpos)
    kc = _pos_rows(np.arange(1024) * 16 + 31)
    kcaug = np.stack([kc, kc], axis=1)
    qaug = np.zeros((B_NM, 4, 2, 512), np.float32)
    qv = np.arange(128)
    for m in range(B_NM):
        c = 4 * m + s
        for g in range(2):
            for rh in range(4):
                sl = 2.0 ** (-(4 * g + rh + 1))
                cs = slice(rh * 128, (rh + 1) * 128)
                qaug[m, 0, g, cs] = -sl * qv
                qaug[m, 1, g, cs] = -sl * 128.0 * c
                qaug[m, 2, g, cs] = sl
                qaug[m, 3, g, cs] = sl * 128.0
    sm = _selmap().reshape(8, 128, 256).transpose(1, 0, 2)
    vcxc = np.zeros((128, 8, 2, 257), np.float32)
    vcxc[:, :, :, 0] = 1.0
    vcxc[127, 7, :, 0] = 0.0
    vcxc[:, :, :, 1:257] = sm[:, :, None, :]
    selw = np.zeros((128, 4), np.float32)
    selw[:, s] = 1.0
    return {'smask': _bf(smask), 'cmask': _bf(cmask), 'wmask': _bf(wmask), 'Eall': _bf(E),
            'Bsel': np.ascontiguousarray(Brel.astype(np.float32)), 'kaug': _bf(kaug), 'kcaug': _bf(kcaug),
            'qaug': _bf(qaug), 'vcxc': _bf(vcxc), 'selw': selw,
            'identb': IDENTB, 'identf': np.eye(128, dtype=np.float32), 'onesf': ONESF}


def prep_fused(p, L=DEPTH):
    p = {k: (v if k == 'x' else v[:L]) for k, v in p.items()}
    common = {
        'w_in': np.ascontiguousarray(p['w_in'], dtype=np.float32),
        'gmix': np.ascontiguousarray(p['g_mix_norm'].reshape(L, 8, 128).transpose(0, 2, 1)),
        'gq': np.ascontiguousarray(np.tile(p['g_q'][:, None, :], (1, 128, 8))),
        'gk3': np.ascontiguousarray(np.broadcast_to(np.tile(p['g_k'], (1, 1, 2))[:, None], (L, 128, 3, 128))),
        'cw': np.ascontiguousarray(p['conv_w'].reshape(L, 3, 4, 128).transpose(0, 3, 2, 1)),
        'peT': np.ascontiguousarray(p['pe_cmp'].transpose(0, 1, 3, 2)),
        'w1': np.ascontiguousarray(p['w_cmp1'], dtype=np.float32),
        'b1b': np.ascontiguousarray(np.broadcast_to(p['b_cmp1'][:, :, None, :], (L, 2, 128, 256))),
        'w2': np.ascontiguousarray(p['w_cmp2'], dtype=np.float32),
        'b2b': np.ascontiguousarray(np.broadcast_to(p['b_cmp2'][:, None], (L, 128, 2, 64))),
        'w_o': np.ascontiguousarray(p['w_o'], dtype=np.float32),
        'gout': np.ascontiguousarray(p['g_out'].reshape(L, 8, 128).transpose(0, 2, 1)),
        'gffn': np.ascontiguousarray(np.tile(p['g_ffn_norm'][:, None, :], (1, 128, 1))),
        'w_up': np.ascontiguousarray(p['w_up'], dtype=np.float32),
        'w_dn': np.ascontiguousarray(p['w_down'], dtype=np.float32),
    }
    maps = []
    x = np.ascontiguousarray(p['x'], dtype=np.float32)
    for b in range(NB):
        xv = x[b].reshape(T // 128, 128, D)
        for s in range(4):
            mp = dict(common)
            mp.update(core_consts(s))
            mp['x0'] = np.ascontiguousarray(xv[np.arange(B_NM) * 4 + s])
            maps.append(mp)
    return maps


_PROG = {}


def run_fused(inputs, nlayers=DEPTH):
    p = {k: np.asarray(v) for k, v in inputs.items()}
    if nlayers not in _PROG:
        _PROG[nlayers] = build_fused(nlayers)
    res = run_bass_kernel_spmd(_PROG[nlayers], prep_fused(p, nlayers), core_ids=list(range(NCORES)))
    out = np.empty((NB, T, D), np.float32)
    for b in range(NB):
        ov = out[b].reshape(T // 128, 128, D)
        for s in range(4):
            ov[np.arange(B_NM) * 4 + s] = np.asarray(res.results[4 * b + s]['xo'])
    return out


def kernel(**inputs):
    return run_fused(inputs, DEPTH)
```

```python
import contextlib
import numpy as np
import ml_dtypes
import concourse.bass as bass
import concourse.mybir as mybir
from concourse.bass_utils import run_bass_kernel_spmd

F32 = mybir.dt.float32
BF16 = mybir.dt.bfloat16
AF = mybir.ActivationFunctionType
ALU = mybir.AluOpType
AX = mybir.AxisListType
NPBF = ml_dtypes.bfloat16

ENGS = ['tensor', 'vector', 'scalar', 'gpsimd', 'sync']
NCORES = 8
D = 1024
T = 16384
NB = 2
DEPTH = 4
INW = 2840
EPS = 1e-6
NEGM = -30000.0


def key_of(ap):
    t = getattr(ap, 'tensor', ap)
    return t.name.split('@')[0]


class Prog:
    def __init__(self, nc):
        self.nc = nc
        self.q = {e: [] for e in ENGS}
        self.semcount = {}
        self.res = {}
        self.seen = {e: {} for e in ENGS}

    grp = None

    @contextlib.contextmanager
    def group(self, sem):
        s = 'd_' + sem
        self.grp = {'sem': sem, 's': s, 'start': self.semcount.get(s, 0), 'keys': set(), 'pre': {}}
        try:
            yield
        finally:
            g, self.grp = self.grp, None
            v = self.semcount.get(s, 0)
            for k in g['keys']:
                self.res[k]['w'] = (s, v)

    def _need(self, eng, reads, writes):
        need = {}

        def add(tok):
            if tok is None:
                return
            s, v = tok
            if eng == 'tensor' and s == 'e_tensor':
                return
            if self.grp is not None and s == self.grp['s'] and v > self.grp['start']:
                return
            if need.get(s, 0) < v:
                need[s] = v
        for k in reads:
            st = self.res.get(k)
            if st:
                add(st['w'])
        for k in writes:
            sts = [self.res.get(k)]
            if self.grp is not None:
                if k not in self.grp['pre']:
                    st0 = self.res.get(k)
                    self.grp['pre'][k] = {'w': st0['w'], 'r': dict(st0['r'])} if st0 else None
                sts.append(self.grp['pre'][k])
            for st in sts:
                if st:
                    add(st['w'])
                    for s, v in st['r'].items():
                        add((s, v))
        for s, v in need.items():
            if self.seen[eng].get(s, 0) < v:
                self.q[eng].append(('wait', s, v))
                self.seen[eng][s] = v

    def _update(self, reads, writes, tok):
        s, v = tok
        for k in reads:
            st = self.res.setdefault(k, {'w': None, 'r': {}})
            if st['r'].get(s, 0) < v:
                st['r'][s] = v
        for k in writes:
            self.res[k] = {'w': tok, 'r': {}}

    def op(self, eng, fn, reads=(), writes=()):
        reads = [r if isinstance(r, str) else key_of(r) for r in reads]
        writes = [r if isinstance(r, str) else key_of(r) for r in writes]
        self._need(eng, reads, writes)
        s = 'e_' + eng
        v = self.semcount.get(s, 0) + 1
        self.semcount[s] = v
        self.q[eng].append(('op', fn, s, 1))
        self._update(reads, writes, (s, v))

    def dma(self, eng, out, in_, reads=None, writes=None, sem=None, **kw):
        if reads is None:
            reads = [] if in_.tensor.name in self.dram_names else [in_]
        if writes is None:
            writes = [] if out.tensor.name in self.dram_names else [out]
        assert reads or writes or sem
        reads = [r if isinstance(r, str) else key_of(r) for r in reads]
        writes = [r if isinstance(r, str) else key_of(r) for r in writes]
        if self.grp is not None:
            sem = self.grp['sem']
            self.grp['keys'].update(writes)
        if sem is None:
            sem = writes[0] if writes else 'st_' + reads[0]
        self._need(eng, reads, writes)
        s = 'd_' + sem
        v = self.semcount.get(s, 0) + 16
        self.semcount[s] = v
        self.q[eng].append(('op', lambda e: e.dma_start(out=out, in_=in_, **kw), s, 16))
        self._update(reads, writes, (s, v))

    dram_names = set()

    def mm(self, out, lhsT, rhs, start=True, stop=True, reads=None, writes=None):
        self.op('tensor', lambda e: e.matmul(out, lhsT=lhsT, rhs=rhs, start=start, stop=stop),
                reads=reads if reads is not None else [lhsT, rhs],
                writes=writes if writes is not None else [out])

    def tr(self, out, in_, ident, reads=None, writes=None):
        self.op('tensor', lambda e: e.transpose(out, in_, ident),
                reads=reads if reads is not None else [in_, ident],
                writes=writes if writes is not None else [out])

    def act(self, out, in_, func, bias=None, scale=None, accum_out=None, reads=None, writes=None, eng='scalar'):
        kw = {}
        rd = [in_]
        if bias is not None:
            kw['bias'] = bias
            if not isinstance(bias, (int, float)):
                rd.append(bias)
        if scale is not None:
            kw['scale'] = scale
            if not isinstance(scale, (int, float)):
                rd.append(scale)
        wr = [out]
        if accum_out is not None:
            kw['accum_out'] = accum_out
            wr.append(accum_out)
        self.op('scalar', lambda e: e.activation(out=out, in_=in_, func=func, **kw),
                reads=reads if reads is not None else rd,
                writes=writes if writes is not None else wr)

    def tt(self, eng, out, in0, in1, op, reads=None, writes=None):
        self.op(eng, lambda e: e.tensor_tensor(out=out, in0=in0, in1=in1, op=op),
                reads=reads if reads is not None else [in0, in1],
                writes=writes if writes is not None else [out])

    def ts(self, eng, out, in0, s1, op0, s2=None, op1=None, reads=None, writes=None):
        rd = [in0]
        if not isinstance(s1, (int, float)):
            rd.append(s1)
        if s2 is not None and not isinstance(s2, (int, float)):
            rd.append(s2)
        if op1 is None:
            fn = lambda e: e.tensor_scalar(out=out, in0=in0, scalar1=s1, scalar2=None, op0=op0)
        else:
            fn = lambda e: e.tensor_scalar(out=out, in0=in0, scalar1=s1, scalar2=s2, op0=op0, op1=op1)
        self.op(eng, fn, reads=reads if reads is not None else rd,
                writes=writes if writes is not None else [out])

    def stt(self, out, in0, scalar, in1, op0, op1, reads=None, writes=None):
        rd = [in0, in1]
        if not isinstance(scalar, (int, float)):
            rd.append(scalar)
        self.op('vector', lambda e: e.scalar_tensor_tensor(out=out, in0=in0, scalar=scalar, in1=in1, op0=op0, op1=op1),
                reads=reads if reads is not None else rd,
                writes=writes if writes is not None else [out])

    def copy(self, eng, out, in_, reads=None, writes=None):
        if eng == 'scalar':
            fn = lambda e: e.activation(out=out, in_=in_, func=AF.Copy)
        else:
            fn = lambda e: e.tensor_copy(out=out, in_=in_)
        self.op(eng, fn, reads=reads if reads is not None else [in_],
                writes=writes if writes is not None else [out])

    def recip(self, out, in_):
        self.op('vector', lambda e: e.reciprocal(out=out, in_=in_), reads=[in_], writes=[out])

    def reduce(self, out, in_, op=ALU.add, axis=AX.X):
        self.op('vector', lambda e: e.tensor_reduce(out=out, in_=in_, axis=axis, op=op), reads=[in_], writes=[out])

    def memset(self, eng, ap, val):
        self.op(eng, lambda e: e.memset(ap, val), writes=[ap])

    def barrier(self):
        for e in ENGS:
            for s, v in self.semcount.items():
                if self.seen[e].get(s, 0) < v:
                    self.q[e].append(('wait', s, v))
                    self.seen[e][s] = v
        self.res = {}

    def collectives(self, fns):
        self.barrier()
        s = 'c_cc'
        for fn in fns:
            v = self.semcount.get(s, 0) + 1
            self.semcount[s] = v
            self.q['gpsimd'].append(('op', fn, s, 1))
        self.barrier()

    def emit(self):
        nc = self.nc
        for s, v in self.semcount.items():
            if self.seen['sync'].get(s, 0) < v:
                self.q['sync'].append(('wait', s, v))
        with contextlib.ExitStack() as es:
            semh = {s: es.enter_context(nc.semaphore(s)) for s in self.semcount}
            block = es.enter_context(nc.Block())

            def mk(engname):
                def body(e):
                    for it in self.q[engname]:
                        if it[0] == 'wait':
                            e.wait_ge(semh[it[1]], it[2])
                        else:
                            ins = it[1](e)
                            ins.then_inc(semh[it[2]], it[3])
                return body
            for engname in ENGS:
                if self.q[engname]:
                    getattr(block, engname)(mk(engname))


def AP_(t, offset, dims):
    return bass.AP(t, offset, [list(d) for d in dims])


class Ctx:
    def __init__(self):
        self.nc = bass.Bass("TRN2", target_bir_lowering=False)
        self.es = contextlib.ExitStack()
        self.P = Prog(self.nc)
        self.P.dram_names = set()

    def din(self, name, shape, dt):
        self.P.dram_names.add(name)
        return self.nc.dram_tensor(name, list(shape), dt, kind="ExternalInput").ap()

    def dout(self, name, shape, dt):
        self.P.dram_names.add(name)
        return self.nc.dram_tensor(name, list(shape), dt, kind="ExternalOutput").ap()

    tag = None

    def dint(self, name, shape, dt):
        self.P.dram_names.add(name)
        return self.nc.dram_tensor(name, list(shape), dt)

    def sb(self, name, shape, dt):
        if self.tag:
            name = name + '@' + self.tag
        return self.es.enter_context(self.nc.sbuf_tensor(name, list(shape), dt))

    def ps(self, name, shape, dt):
        if self.tag:
            name = name + '@' + self.tag
        return self.es.enter_context(self.nc.psum_tensor(name, list(shape), dt))

    @contextlib.contextmanager
    def phase(self, tag):
        old, oldtag = self.es, self.tag
        self.es, self.tag = contextlib.ExitStack(), tag
        try:
            yield
        finally:
            self.P.barrier()
            self.es.close()
            self.es, self.tag = old, oldtag

    def finish(self):
        self.P.emit()
        self.es.close()
        return self.nc


B_NM = 32
DFF = 4096
PAYR = 6176
KS_R0, KW_R0, KC_R0, VC_R0, VS_R0, VW_R0 = 0, 1024, 2048, 3072, 4096, 5136
TP = T + 128


def emit_A(C, l, xsrc, W, G):
    P = C.P
    with C.phase("A%d" % l):
        Wbf = C.sb("Wbf", [128, 8, INW], BF16)
        wst = [C.sb("wst%d" % i, [128, INW // 2], F32) for i in range(2)]
        gmix_s = C.sb("gmix_s", [128, 8], F32)
        gq_s = C.sb("gq_s", [128, 512], F32)
        gk3_s = C.sb("gk3_s", [128, 3, 128], F32)
        cw_s = C.sb("cw_s", [128, 4, 3], F32)
        xt = [C.sb("xt%d" % i, [128, D], F32) for i in range(2)]
        junk = C.sb("junk", [128, D], BF16)
        ss = [C.sb("ss%d" % i, [128, 1], F32) for i in range(2)]
        rstd = [C.sb("rstd%d" % i, [128, 1], F32) for i in range(2)]
        hn = [C.sb("hn%d" % i, [128, D], BF16) for i in range(2)]
        hT = [C.sb("hT%d" % i, [128, 8, 128], BF16) for i in range(2)]
        zs = [C.sb("zs%d" % i, [128, 1304], F32) for i in range(2)]
        hcs = C.sb("hcs", [128, 4, 128], F32)
        ub = [C.sb("ub%d" % i, [128, 4, 130], F32) for i in range(2)]
        bgs = [C.sb("bgs%d" % i, [128, 4, 128], F32) for i in range(2)]
        sq = C.sb("sq", [128, 1280], F32)
        ssg = C.sb("ssg", [128, 20], F32)
        rg = C.sb("rg", [128, 20], F32)
        qtmp = C.sb("qtmp", [128, 512], F32)
        ktmp = C.sb("ktmp", [128, 256], F32)
        nrm = C.sb("nrm", [128, 1024], BF16)
        trs = C.sb("trs", [128, 8, 128], BF16)
        vsw = C.sb("vsw", [128, 2, 130], BF16)
        gts = C.sb("gts", [128, 24], F32)
        cv0 = C.sb("cv0", [128, 4, 128], F32)
        cv1 = C.sb("cv1", [128, 4, 128], F32)
        yv = C.sb("yv", [128, 4, 128], F32)
        ut = C.sb("ut", [128, 4, 2], F32)
        bg2 = C.sb("bg2", [128, 4, 2], F32)
        pTa = C.ps("pTa", [128, 8, 128], BF16)
        pTb = C.ps("pTb", [128, 8, 128], BF16)
        pz = [C.ps("pz%d" % i, [128, 512], F32) for i in range(3)]
        pc = [C.ps("pc%d" % i, [128, 4, 128], F32) for i in range(3)]
        identb = G['identb']

        with P.group('cst'):
            P.dma('sync', gmix_s[:], W['gmix'][l])
            P.dma('sync', gq_s[:], W['gq'][l])
            P.dma('sync', gk3_s[:], W['gk3'][l])
            P.dma('sync', cw_s[:], W['cw'][l])
        P.memset('gpsimd', vsw[:], 1.0)
        P.memset('gpsimd', ub[0][:], 0.0)
        P.memset('gpsimd', ub[1][:], 0.0)
        for kc in range(8):
            for hf in range(2):
                st = wst[hf]
                P.dma('sync' if hf == 0 else 'scalar', st[:], W['w_in'][l, kc * 128:(kc + 1) * 128, hf * 1420:(hf + 1) * 1420])
                P.ts('vector' if hf == 0 else 'gpsimd', Wbf[:, kc, hf * 1420:(hf + 1) * 1420], st[:], gmix_s[:, kc:kc + 1], ALU.mult)

        pay, payu, qsc, gsc, ysc, bg2sc = G['pay'], G['payu'], G['qsc'], G['gsc'], G['ysc'], G['bg2sc']

        def F_pre(m):
            xb = xt[m % 2]
            P.dma('sync', xb[:], xsrc[m])
            P.act(junk[:], xb[:], AF.Square, accum_out=ss[m % 2][:])
            P.act(rstd[m % 2][:], ss[m % 2][:], AF.Sqrt, bias=EPS, scale=1.0 / D)
            P.recip(rstd[m % 2][:], rstd[m % 2][:])
            P.act(hn[m % 2][:], xb[:], AF.Copy, scale=rstd[m % 2][:, 0:1])

        def F_pe_a(m):
            for kc in range(8):
                P.tr(pTa[:, kc, :], hn[m % 2][:, kc * 128:(kc + 1) * 128], identb[:])
            P.copy('vector', hT[m % 2][:], pTa[:])

        def F_pe_b(m):
            hTb, zb, ubb, bgb = hT[m % 2], zs[m % 2], ub[m % 2], bgs[m % 2]
            for bi, (c0, c1) in enumerate([(0, 512), (512, 1024), (1024, 1304)]):
                for kc in range(8):
                    P.mm(pz[bi][:, 0:c1 - c0], hTb[:, kc, :], Wbf[:, kc, c0:c1], start=(kc == 0), stop=(kc == 7))
                P.copy('scalar', zb[:, c0:c1], pz[bi][:, 0:c1 - c0])
            for ch in range(12):
                for kc in range(8):
                    P.mm(pc[ch // 4][:, ch % 4, :], Wbf[:, kc, 1304 + ch * 128:1304 + (ch + 1) * 128], hTb[:, kc, :],
                         start=(kc == 0), stop=(kc == 7))
                if ch == 3:
                    P.copy('scalar', hcs[:], pc[0][:])
                if ch == 7:
                    P.tt('vector', ubb[:, :, 2:130], pc[1][:], hcs[:], ALU.mult)
            P.copy('vector', bgb[:], pc[2][:])

        def G1(m):
            zb, ubb, bgb = zs[m % 2], ub[m % 2], bgs[m % 2]
            P.act(sq[:], zb[:, 0:1280], AF.Square)
            P.reduce(ssg[:], sq[:].rearrange("p (g d) -> p g d", d=64))
            P.act(rg[:, 0:8], ssg[:, 0:8], AF.Sqrt, bias=64 * EPS, scale=1.0)
            P.act(rg[:, 8:20], ssg[:, 8:20], AF.Sqrt, bias=EPS, scale=1.0 / 64)
            P.recip(rg[:], rg[:])
            P.tt('vector', qtmp[:].rearrange("p (g d) -> p g d", d=64), zb[:, 0:512].rearrange("p (g d) -> p g d", d=64),
                 AP_(rg, 0, [[20, 128], [1, 8], [0, 64]]), ALU.mult)
            P.tt('gpsimd', nrm[:, 0:512], qtmp[:], gq_s[:], ALU.mult)
            P.tt('vector', ktmp[:, 0:128].rearrange("p (g d) -> p g d", d=64), zb[:, 768:896].rearrange("p (g d) -> p g d", d=64),
                 AP_(rg, 12, [[20, 128], [1, 2], [0, 64]]), ALU.mult)
            P.tt('vector', ktmp[:, 128:256].rearrange("p (g d) -> p g d", d=64), zb[:, 1024:1152].rearrange("p (g d) -> p g d", d=64),
                 AP_(rg, 16, [[20, 128], [1, 2], [0, 64]]), ALU.mult)
            P.tt('gpsimd', nrm[:, 512:768], ktmp[:], gk3_s[:, 1:3, :].rearrange("p a d -> p (a d)"), ALU.mult)
            P.copy('gpsimd', nrm[:, 768:1024], zb[:, 512:768])
            P.copy('gpsimd', AP_(vsw, 1, [[260, 128], [65, 2], [1, 64]]), zb[:, 896:1024].rearrange("p (g d) -> p g d", d=64))
            P.copy('gpsimd', AP_(vsw, 131, [[260, 128], [65, 2], [1, 64]]), zb[:, 1152:1280].rearrange("p (g d) -> p g d", d=64))
            P.dma('scalar', AP_(pay['vs%d' % (m // 16)], (m % 16) * 128 * 130, [[130, 128], [1, 130]]), vsw[:, 0, :])
            P.dma('scalar', AP_(pay['vw%d' % (m // 16)], (m % 16) * 128 * 130, [[130, 128], [1, 130]]), vsw[:, 1, :])
            P.act(gts[:], zb[:, 1280:1304], AF.Sigmoid)
            P.dma('scalar', gsc[m], gts[:])
            P.tt('gpsimd', cv0[:], ubb[:, :, 2:130], AP_(cw_s, 2, [[12, 128], [3, 4], [0, 128]]), ALU.mult)
            P.tt('gpsimd', cv1[:], ubb[:, :, 1:129], AP_(cw_s, 1, [[12, 128], [3, 4], [0, 128]]), ALU.mult)
            P.tt('gpsimd', cv0[:], cv0[:], cv1[:], ALU.add)
            P.tt('gpsimd', cv1[:], ubb[:, :, 0:128], AP_(cw_s, 0, [[12, 128], [3, 4], [0, 128]]), ALU.mult)
            P.tt('gpsimd', cv0[:], cv0[:], cv1[:], ALU.add)
            P.tt('vector', yv[:], bgb[:], cv0[:], ALU.mult)
            P.dma('scalar', ysc[m].rearrange("p (c t) -> p c t", c=4), yv[:])
            P.copy('gpsimd', ut[:], ubb[:, :, 128:130])
            P.dma('sync', payu[m].rearrange("(p e) -> p e", e=8), ut[:].rearrange("p c e -> p (c e)"))
            P.copy('gpsimd', bg2[:], bgb[:, :, 0:2])
            P.dma('sync', bg2sc[m], bg2[:].rearrange("p c e -> p (c e)"))

        def G2(m):
            for bk in range(8):
                P.tr(pTb[:, bk, :], nrm[:, bk * 128:(bk + 1) * 128], identb[:])
            P.copy('vector', trs[:], pTb[:])
            for e in range(2):
                for g in range(2):
                    P.dma('sync' if g == 0 else 'scalar', AP_(qsc, m * 65536 + g * 32768 + e * 128, [[512, 64], [256, 2], [1, 128]]),
                          trs[e * 64:(e + 1) * 64, 2 * g:2 * g + 2, :])
            for ci, cn in enumerate(('ks', 'kw', 'kc', 'vc')):
                P.dma('sync' if ci % 2 == 0 else 'scalar', AP_(pay[cn], m * 128, [[4096, 128], [1, 128]]), trs[:, 4 + ci, :])

        F_pre(0)
        F_pre(1)
        F_pe_a(0)
        F_pe_b(0)
        for m in range(B_NM):
            if m + 2 < B_NM:
                F_pre(m + 2)
            if m + 1 < B_NM:
                F_pe_a(m + 1)
            G1(m)
            if m + 1 < B_NM:
                F_pe_b(m + 1)
            G2(m)


def emit_Z(C, l, W, G):
    P = C.P
    with C.phase("Z%d" % l):
        kcT_all = C.sb("kcT_all", [128, TP], BF16)
        vcT_all = C.sb("vcT_all", [128, TP], BF16)
        W1bf = C.sb("W1bf", [128, 2, 32, 256], BF16)
        w1st = [C.sb("w1st%d" % i, [128, 8, 256], F32) for i in range(2)]
        W2bf = C.sb("W2bf", [128, 2, 2, 64], BF16)
        w2st = C.sb("w2st", [128, 2, 2, 64], F32)
        peT_s = C.sb("peT_s", [64, 2, 32], F32)
        pebf = C.sb("pebf", [64, 2, 32, 128], BF16)
        b1b_s = C.sb("b1b_s", [128, 2, 256], F32)
        b2b_s = C.sb("b2b_s", [128, 2, 64], F32)
        gkc = C.sb("gkc", [128, 128], F32)
        c1b = C.sb("c1b", [128, 2, 256], F32)
        hid = C.sb("hid", [128, 256], F32)
        g_x2 = C.sb("g_x2", [128, 256], F32)
        g_in = C.sb("g_in", [128, 256], F32)
        g_sg = C.sb("g_sg", [128, 256], F32)
        hbf = C.sb("hbf", [128, 256], BF16)
        hidT = C.sb("hidT", [128, 2, 128], BF16)
        co = C.sb("co", [128, 2, 2, 64], F32)
        cosq = C.sb("cosq", [128, 128], F32)
        css = C.sb("css", [128, 2], F32)
        crs = C.sb("crs", [128, 2], F32)
        kcn = C.sb("kcn", [128, 128], BF16)
        pT = C.ps("pT", [128, 8, 128], BF16)
        px = [C.ps("px%d" % i, [128, 512], F32) for i in range(2)]
        po = C.ps("po", [128, 512], F32)
        identb = G['identb']
        gath = G['gath']
        kcT_s, vcx_s = G['kcT_s'], G['vcx_s']

        with P.group('cst'):
            P.dma('sync', b1b_s[:], W['b1b'][l].rearrange("k p n -> p k n"))
            P.dma('sync', b2b_s[:], W['b2b'][l])
            P.dma('sync', peT_s[:], W['peT'][l].rearrange("k d j -> d k j"))
            P.dma('sync', w2st[:], W['w2'][l].rearrange("k (c p) d -> p k c d", p=128))
            P.dma('sync', gkc[:], W['gk3'][l, :, 0, :])
        P.memset('gpsimd', kcT_all[:, T:TP], 0.0)
        P.memset('gpsimd', vcT_all[:, T:TP], 0.0)
        with P.group('kvcl'):
            for r in range(4):
                for kv, dst in enumerate((kcT_all, vcT_all)):
                    P.dma('sync' if kv == 0 else 'scalar', AP_(dst, r * 128, [[TP, 128], [512, 32], [1, 128]]),
                          AP_(gath['kc' if kv == 0 else 'vc'], r * 1024 * 512, [[4096, 128], [128, 32], [1, 128]]),
                          writes=[dst])
        P.copy('vector', W2bf[:], w2st[:])
        P.copy('vector', pebf[:], AP_(peT_s, 0, [[64, 64], [32, 2], [1, 32], [0, 128]]))
        n = 0
        for kv in range(2):
            for jq in range(4):
                st = w1st[n % 2]
                n += 1
                src = W['w1'][l, kv, jq * 512:(jq + 1) * 512, :].rearrange("(j d) n -> d j n", d=64)
                with P.group("w1st%d" % ((n - 1) % 2)):
                    P.dma('sync', st[0:64, :, :], src, writes=[st])
                    P.dma('scalar', st[64:128, :, :], src, writes=[st])
                P.copy('vector' if jq % 2 == 0 else 'gpsimd', W1bf[:, kv, jq * 8:(jq + 1) * 8, :], st[:])
        for kv in range(2):
            for j in range(32):
                P.mm(px[0][:, 0:256], pebf[:, kv, j, :], W1bf[0:64, kv, j, :], start=(j == 0), stop=(j == 31))
            P.tt('vector', c1b[:, kv, :], px[0][:, 0:256], b1b_s[:, kv, :], ALU.add)
        n = 0
        for ib in range(8):
            for kv in range(2):
                src_all = kcT_all if kv == 0 else vcT_all
                for g in range(2):
                    pp = px[n % 2]
                    n += 1
                    base = 16 * 128 * ib
                    for j in range(32):
                        lhs = AP_(src_all, 64 * g * TP + base + j, [[TP, 64], [16, 128]])
                        P.mm(pp[:, 0:256], lhs, W1bf[64 * g:64 * g + 64, kv, j, :], start=(j == 0), stop=(j == 31),
                             reads=[src_all, W1bf])
                    P.tt('vector', hid[:], pp[:, 0:256], c1b[:, kv, :], ALU.add)
                    P.act(g_x2[:], hid[:], AF.Square)
                    P.ts('gpsimd', g_x2[:], g_x2[:], 0.044715, ALU.mult, 1.0, ALU.add)
                    P.tt('gpsimd', g_in[:], g_x2[:], hid[:], ALU.mult)
                    P.act(g_sg[:], g_in[:], AF.Sigmoid, scale=1.5957691216057308)
                    P.tt('vector', hbf[:], hid[:], g_sg[:], ALU.mult)
                    for c in range(2):
                        P.tr(pT[:, c, :], hbf[:, c * 128:(c + 1) * 128], identb[:])
                    P.copy('vector', hidT[:], pT[:, 0:2, :])
                    for c in range(2):
                        P.mm(po[:, 0:64], hidT[:, c, :], W2bf[:, kv, c, :], start=(c == 0), stop=(c == 1))
                    P.tt('vector', co[:, kv, g, :], po[:, 0:64], b2b_s[:, kv, :], ALU.add)
            P.act(cosq[:], co[:, 0, :, :].rearrange("p g d -> p (g d)"), AF.Square)
            P.reduce(css[:], cosq[:].rearrange("p (g d) -> p g d", d=64))
            P.act(crs[:], css[:], AF.Sqrt, bias=EPS, scale=1.0 / 64)
            P.recip(crs[:], crs[:])
            P.tt('vector', cosq[:].rearrange("p (g d) -> p g d", d=64), co[:, 0, :, :], AP_(crs, 0, [[2, 128], [1, 2], [0, 64]]), ALU.mult)
            P.tt('vector', kcn[:], cosq[:], gkc[:], ALU.mult)
            for g in range(2):
                P.tr(pT[0:64, 4 + g, :], kcn[:, 64 * g:64 * g + 64], identb[:])
            P.copy('vector', kcT_s[0:64, :, ib * 128:(ib + 1) * 128], pT[0:64, 4:6, :])
            P.copy('gpsimd', vcx_s[:, ib, :, 0:64], co[:, 1, :, :])
        if G.get('dbg'):
            P.dma('sync', G['dbg']['kc'], kcT_s[:])
            P.dma('sync', G['dbg']['vc'], vcx_s[:])


def emit_B(C, l, W, G):
    P = C.P
    with C.phase("B%d" % l):
        ksT_s = [C.sb("ksT_s%d" % g, [68, T], BF16) for g in range(2)]
        vsx_s = C.sb("vsx_s", [128, 128, 130], BF16)
        smask_s = C.sb("smask_s", [128, 4, 512], BF16)
        cmask_s = C.sb("cmask_s", [128, 5, 512], BF16)
        wmask_s = C.sb("wmask_s", [128, 8, 128], BF16)
        Eall_s = C.sb("Eall_s", [128, 64, 128], BF16)
        Bsel_s = C.sb("Bsel_s", [128, 512], F32)
        qT_s = [C.sb("qT_s%d" % i, [68, 2, 512], BF16) for i in range(2)]
        kwT_s = [C.sb("kwT_s%d" % i, [68, 2, 1024], BF16) for i in range(2)]
        vwx_s = [C.sb("vwx_s%d" % i, [128, 8, 130], BF16) for i in range(2)]
        gates_s = [C.sb("gates_s%d" % i, [128, 24], F32) for i in range(2)]
        NPB = 6
        Pb = [C.sb("Pb%d" % i, [128, 512], BF16) for i in range(NPB)]
        Pc = [C.sb("Pc%d" % g, [128, 8, 512], BF16) for g in range(2)]
        oc = C.sb("oc", [128, 4, 321], F32)
        den4 = C.sb("den4", [128, 4], F32)
        rd4 = C.sb("rd4", [128, 4], F32)
        coef4 = C.sb("coef4", [128, 4], F32)
        imp = [C.sb("imp%d" % g, [128, 256], F32) for g in range(2)]
        tmpi = C.sb("tmpi", [128, 256], F32)
        m8 = C.sb("m8", [128, 16], F32)
        thr = C.sb("thr", [128, 1], F32)
        nm = [C.sb("nm%d" % g, [128, 256], BF16) for g in range(2)]
        nmT = [C.sb("nmT%d" % g, [128, 2, 128], BF16) for g in range(2)]
        osT = C.sb("osT", [65, 512], F32)
        otmp = C.sb("otmp", [128, 4, 64], F32)
        acc = C.sb("acc", [128, 8, 64], F32)
        accb = C.sb("accb", [128, 512], BF16)
        ssq = C.sb("ssq", [128, 1], F32)
        rat = C.sb("rat", [128, 1], F32)
        rat2 = C.sb("rat2", [128, 1], F32)
        atT = C.sb("atT", [128, 4, 128], BF16)
        ps_s = [C.ps("ps_s%d" % i, [128, 512], F32) for i in range(4)]
        ps_o = [C.ps("ps_o%d" % i, [128, 512], F32) for i in range(1)]
        ps_acc = [C.ps("ps_acc%d" % i, [128, 512], F32) for i in range(2)]
        pm = C.ps("pm", [128, 512], F32)
        pm_b = pm[:].bitcast(BF16)
        identb, identf = G['identb'], G['identf']
        kcT_s, vcx_s = G['kcT_s'], G['vcx_s']
        gath, qsc, gsc, atsc, rasc = G['gath'], G['qsc'], G['gsc'], G['atsc'], G['rasc']

        with P.group('cst'):
            P.dma('sync', cmask_s[:], W['cmask'])
            P.dma('sync', Bsel_s[:], W['Bsel'])
            P.dma('scalar', wmask_s[:], W['wmask'])
            P.dma('scalar', smask_s[:], W['smask'])
            P.dma('scalar', Eall_s[:], W['Eall'])
        with P.group('ksT'):
            for g in range(2):
                P.dma('gpsimd', ksT_s[g][64:68, :], W['kaug'], writes=[ksT_s[g]])
                for r in range(4):
                    P.dma('sync' if r % 2 == 0 else 'scalar', ksT_s[g][0:64, r * 4096:(r + 1) * 4096],
                          AP_(gath['ks'], r * 1024 * 512 + 64 * g * 4096, [[4096, 64], [1, 4096]]),
                          writes=[ksT_s[g]])
        with P.group('vsx'):
            for r in range(4):
                for hv in range(2):
                    P.dma('gpsimd', vsx_s[:, r * 32 + hv * 16:r * 32 + hv * 16 + 16, :],
                          AP_(gath['vs%d' % hv], r * 520 * 512, [[130, 128], [128 * 130, 16], [1, 130]]), writes=[vsx_s])

        if G.get('dbg'):
            P.dma('sync', G['dbg']['ks'], ksT_s[1][:])
            P.dma('sync', G['dbg']['vs'], vsx_s[:])
        state = {'s': 0, 'p': 0, 'x': 0}
        pending = []

        LA = 4

        def run_stream(tiles):
            n = len(tiles)
            bufs = []
            for i in range(n + LA):
                if i < n:
                    ps = ps_s[state['s'] % 4]
                    pb = Pb[state['p'] % NPB]
                    state['s'] += 1
                    state['p'] += 1
                    tiles[i][0](ps)
                    P.act(pb[:], ps[:], AF.Exp)
                    bufs.append(pb)
                j = i - (LA - 2)
                if 0 <= j < n and tiles[j][1] is not None:
                    tiles[j][1](bufs[j])
                if i - LA >= 0:
                    tiles[i - LA][2](bufs[i - LA])

        def mask_mul(pb, map_, mkeys):
            P.tt('vector', pb[:].rearrange("p (a q) -> p a q", a=4), pb[:].rearrange("p (a q) -> p a q", a=4), map_,
                 ALU.mult, reads=[pb] + mkeys, writes=[pb])

        def o_epilogue(psacc, g, br, gs):
            P.copy('vector', osT[:], psacc[0:65, :])
            pmv = pm[:, 0:260].rearrange("p (r d) -> p r d", d=65)
            for r in range(4):
                P.tr(pmv[:, r, :], osT[0:65, r * 128:(r + 1) * 128], identf[0:65, 0:65])
            P.recip(rd4[:], pmv[:, :, 0])
            P.tt('vector', coef4[:], rd4[:], AP_(gs, 12 * g + br, [[24, 128], [3, 4]]), ALU.mult)
            dst = acc[:, 4 * g:4 * g + 4, :]
            P.tt('vector', otmp[:], pmv[:, :, 1:65], AP_(coef4, 0, [[4, 128], [1, 4], [0, 64]]), ALU.mult)
            P.tt('gpsimd', dst, dst, otmp[:], ALU.add)

        for m in range(B_NM):
            qs = qT_s[m % 2]
            kws = kwT_s[m % 2]
            vws = vwx_s[m % 2]
            gs = gates_s[m % 2]
            h0 = 0 if m > 0 else 1
            with P.group("qT_s%d" % (m % 2)):
                P.dma('sync', qs[0:64, :, :], qsc[m].rearrange("g d n -> d g n"), writes=[qs])
                P.dma('scalar', qs[64:68, :, :], W['qaug'][m], writes=[qs])
            P.dma('scalar', gs[:], gsc[m])
            nh = 2 - h0
            c0 = 128 * (m - 1 + h0)
            with P.group("kws%d" % (m % 2)):
                for r in range(4):
                    P.dma('sync' if r % 2 == 0 else 'scalar',
                          AP_(kws, (r * 2 + h0) * 128, [[2048, 64], [1024, 2], [1, 128 * nh]]),
                          AP_(gath['kw'], r * 1024 * 512 + c0, [[4096, 64], [64 * 4096, 2], [1, 128 * nh]]),
                          writes=[kws])
            with P.group("vws%d" % (m % 2)):
                for r in range(4):
                    for hf in range(h0, 2):
                        mp = m - 1 + hf
                        P.dma('gpsimd', vws[:, r * 2 + hf, :],
                              AP_(gath['vw%d' % (mp // 16)], r * 520 * 512 + (mp % 16) * 128 * 130, [[130, 128], [1, 130]]),
                              writes=[vws])
            for g in range(2):
                P.copy('gpsimd', AP_(kws, 64 * 2048 + g * 1024 + h0 * 128, [[2048, 4], [256, 4], [1, 128 * (2 - h0)]]),
                       AP_(ksT_s[0], 64 * T + 128 * (m - 1 + h0), [[T, 4], [4096, 4], [1, 128 * (2 - h0)]]),
                       reads=[ksT_s[0]], writes=[kws])
            n_it = (32 * m + 30) // 128 + 1
            for g in range(2):
                for it in range(n_it):
                    ps = ps_s[state['s'] % 4]
                    state['s'] += 1
                    delta = 128 * it - 32 * m
                    masked = delta >= -128
                    P.mm(ps[:], kcT_s[:, g, it * 128:(it + 1) * 128], qs[:, g, :], start=True, stop=not masked)
                    if masked:
                        P.mm(ps[:], identb[:], cmask_s[:, (-delta) // 32, :], start=False, stop=True)
                    P.act(Pc[g][:, it, :], ps[:], AF.Exp)
                for r in range(4):
                    po = ps_o[0]
                    for it in range(n_it):
                        P.mm(po[:, 0:321], Pc[g][:, it, r * 128:(r + 1) * 128], vcx_s[:, it, g, :], start=(it == 0), stop=(it == n_it - 1))
                    P.copy('scalar', oc[:, r, :], po[:, 0:321])
                P.ts('vector', den4[:], oc[:, :, 64], 1e-30, ALU.max)
                P.recip(rd4[:], den4[:])
                P.ts('vector', imp[g][:], oc[:, 0, 65:321], rd4[:, 0:1], ALU.mult)
                for r in range(1, 4):
                    P.stt(imp[g][:], oc[:, r, 65:321], rd4[:, r:r + 1], imp[g][:], ALU.mult, ALU.add)
                P.tt('vector', coef4[:], rd4[:], AP_(gs, 12 * g + 0, [[24, 128], [3, 4]]), ALU.mult)
                P.tt('vector', acc[:, 4 * g:4 * g + 4, :], oc[:, :, 0:64], AP_(coef4, 0, [[4, 128], [1, 4], [0, 64]]), ALU.mult)
                P.tt('vector', imp[g][:], imp[g][:], Bsel_s[:, 256 - 8 * m:512 - 8 * m], ALU.add)
                P.ts('vector', imp[g][:, 0:1], imp[g][:, 0:1], 1e4, ALU.add)
                P.op('vector', lambda e, g=g: e.max(out=m8[:, 0:8], in_=imp[g][:]), reads=[imp[g]], writes=[m8])
                P.op('vector', lambda e, g=g: e.match_replace(out=tmpi[:], in_to_replace=m8[:, 0:8], in_values=imp[g][:], imm_value=-3.0e38),
                     reads=[imp[g], m8], writes=[tmpi])
                P.op('vector', lambda e: e.max(out=m8[:, 8:16], in_=tmpi[:]), reads=[tmpi], writes=[m8])
                P.ts('vector', thr[:], m8[:, 15:16], -1e29, ALU.max)
                P.ts('vector', nm[g][:], imp[g][:], thr[:, 0:1], ALU.is_ge)
            if pending:
                pending.pop()()
            tiles = []
            wt = [(r, hf) for hf in range(h0, 2) for r in range(4)]
            for g in range(2):
                for idx, (r, hf) in enumerate(wt):
                    def eS(ps, g=g, r=r, hf=hf):
                        P.mm(ps[:], kws[:, g, (r * 2 + hf) * 128:(r * 2 + hf + 1) * 128], qs[:, g, :], start=True, stop=(hf == 0))
                        if hf == 1:
                            P.mm(ps[:], identb[:], smask_s[:, r, :], start=False, stop=True)

                    def eX(pb, r=r):
                        mask_mul(pb, AP_(wmask_s, r * 128, [[1024, 128], [0, 4], [1, 128]]), [wmask_s])

                    def ePV(pb, g=g, r=r, hf=hf, idx=idx):
                        P.mm(ps_acc[g][0:65, :], vws[:, r * 2 + hf, 65 * g:65 * g + 65], pb[:], start=(idx == 0), stop=(idx == len(wt) - 1))
                    tiles.append((eS, eX if hf == 0 else None, ePV))
            run_stream(tiles)
            for g in range(2):
                for hb in range(2):
                    P.tr(pm_b[:, hb * 128:(hb + 1) * 128], nm[g][:, hb * 128:(hb + 1) * 128], identb[:])
                P.copy('vector', nmT[g][:].rearrange("p a q -> p (a q)"), pm_b[:, 0:256], reads=[pm])
            for g in range(2):
                o_epilogue(ps_acc[g], g, 2, gs)
            nkt = 4 * m + 4
            for g in range(2):
                tiles = []
                for kt in range(nkt):
                    def eS(ps, g=g, kt=kt):
                        diag = kt >= 4 * m
                        col = ((kt % 4) * 32 + kt // 4) * 128
                        P.mm(ps[:], ksT_s[g][:, col:col + 128], qs[:, g, :], start=True, stop=not diag)
                        if diag:
                            P.mm(ps[:], identb[:], smask_s[:, kt - 4 * m, :], start=False, stop=True)

                    def eX(pb, g=g, kt=kt):
                        bank = (ps_o[0], pm)[state['x'] % 2]
                        state['x'] += 1
                        P.mm(bank[:, 0:128], Eall_s[:, kt % 64, :], nmT[g][:, kt // 64, :], start=True, stop=True)
                        mask_mul(pb, AP_(bank, 0, [[512, 128], [0, 4], [1, 128]]), [bank])

                    def ePV(pb, g=g, kt=kt):
                        P.mm(ps_acc[g][0:65, :], vsx_s[:, (kt % 4) * 32 + kt // 4, 65 * g:65 * g + 65], pb[:], start=(kt == 0), stop=(kt == nkt - 1))
                    tiles.append((eS, eX, ePV))
                run_stream(tiles)
                o_epilogue(ps_acc[g], g, 1, gs)
            accf = acc[:].rearrange("p h d -> p (h d)")
            P.act(accb[:], accf, AF.Square, accum_out=ssq[:])
            P.act(rat[:], ssq[:], AF.Sqrt, bias=EPS, scale=1.0 / 512)
            P.recip(rat2[:], rat[:])
            P.dma('gpsimd', rasc[m], rat2[:])
            P.copy('gpsimd', accb[:], accf)
            if G.get('dbg'):
                P.dma('gpsimd', G['dbg']['ra'][m], rat2[:])

            def finish_out(m=m):
                for bk in range(4):
                    P.tr(pm_b[:, bk * 128:(bk + 1) * 128], accb[:, bk * 128:(bk + 1) * 128], identb[:])
                P.copy('vector', atT[:].rearrange("p b q -> p (b q)"), pm_b[:, 0:512], reads=[pm])
                P.dma('gpsimd', atsc[m], atT[:].rearrange("p b q -> p (b q)"))
                if G.get('dbg'):
                    P.dma('gpsimd', G['dbg']['at'][m], atT[:].rearrange("p b q -> p (b q)"))
            pending.append(finish_out)
        pending.pop()()


def emit_C(C, l, xsrc, xdst, W, G):
    P = C.P
    with C.phase("C%d" % l):
        Wo = C.sb("Wo", [128, 8, D], BF16)
        Wdn = C.sb("Wdn", [128, 32, D], BF16)
        wst = [C.sb("wst%d" % i, [128, 8, 512], F32) for i in range(2)]
        Wup = [C.sb("Wup%d" % i, [128, 8, 512], BF16) for i in range(2)]
        gout_s = C.sb("gout_s", [128, 8], F32)
        gffn_s = C.sb("gffn_s", [128, D], F32)
        cw_s = C.sb("cw_s", [128, 4, 3], F32)
        selw_s = C.sb("selw_s", [128, 4], F32)
        xt = [C.sb("xt%d" % i, [128, D], F32) for i in range(1)]
        x1 = [C.sb("x1_%d" % i, [128, D], F32) for i in range(4)]
        at_s = [C.sb("at_s%d" % i, [128, 4, 128], BF16) for i in range(2)]
        yv = [C.sb("yv%d" % i, [128, 4, 128], F32) for i in range(1)]
        bg2 = [C.sb("bg2_%d" % i, [128, 4, 2], F32) for i in range(2)]
        tl = [C.sb("tl%d" % i, [128, 4, 8], F32) for i in range(2)]
        halo = C.sb("halo", [128, 4, 2], F32)
        pt0 = C.sb("pt0", [128, 4], F32)
        pt1 = C.sb("pt1", [128, 4], F32)
        ysq = C.sb("ysq", [128, 4, 128], F32)
        ybf = C.sb("ybf", [128, 4, 128], BF16)
        rc = C.sb("rc", [128, 1], F32)
        rc2 = C.sb("rc2", [128, 1], F32)
        ra_s = [C.sb("ra_s%d" % i, [128, 1], F32) for i in range(2)]
        ss = C.sb("ss", [128, 1], F32)
        rstd = C.sb("rstd", [128, 1], F32)
        h2n = C.sb("h2n", [128, D], BF16)
        h2T = C.sb("h2T", [128, 8, 512], BF16)
        rl = [C.sb("rl%d" % i, [128, 512], F32) for i in range(1)]
        actT = C.sb("actT", [128, 32, 512], BF16)
        pa = [C.ps("pa%d" % i, [128, 512], F32) for i in range(2)]
        pcv = [C.ps("pcv%d" % i, [128, 512], F32) for i in range(2)]
        pu = [C.ps("pu%d" % i, [128, 512], F32) for i in range(2)]
        pT = C.ps("pT", [128, 8, 128], BF16)
        px = C.ps("px", [128, 512], F32)
        identb, onesf = G['identb'], G['onesf']
        gathu, ysc, bg2sc, atsc, rasc = G['gathu'], G['ysc'], G['bg2sc'], G['atsc'], G['rasc']

        with P.group('cst'):
            P.dma('sync', gout_s[:], W['gout'][l])
            P.dma('sync', gffn_s[:], W['gffn'][l])
            P.dma('sync', cw_s[:], W['cw'][l])
            P.dma('sync', selw_s[:], W['selw'])
        n = 0
        for kc in range(8):
            for hf in range(2):
                st = wst[n % 2]
                n += 1
                P.dma('sync' if hf == 0 else 'scalar', st[:, 0, :], W['w_o'][l, kc * 128:(kc + 1) * 128, hf * 512:(hf + 1) * 512], writes=[st])
                P.ts('vector' if hf == 0 else 'gpsimd', Wo[:, kc, hf * 512:(hf + 1) * 512], st[:, 0, :], gout_s[:, kc:kc + 1], ALU.mult, reads=[st, gout_s])
        for f4 in range(8):
            st = wst[n % 2]
            n += 1
            stv = AP_(st, 0, [[4096, 128], [1024, 4], [1, 1024]])
            P.dma('sync' if f4 % 2 == 0 else 'scalar', stv, W['w_dn'][l, f4 * 512:(f4 + 1) * 512, :].rearrange("(a p) n -> p a n", p=128), writes=[st])
            P.copy('vector' if f4 % 2 == 0 else 'gpsimd', Wdn[:, f4 * 4:(f4 + 1) * 4, :], stv, reads=[st])
        up_n = [n]

        for bt in range(B_NM // 4):
            for j in range(4):
                m = bt * 4 + j
                xb = xt[0]
                ab, rab, yb, bgb, tlb = at_s[m % 2], ra_s[m % 2], yv[0], bg2[m % 2], tl[m % 2]
                P.dma('sync', xb[:], xsrc[m])
                P.dma('scalar', ab[:], atsc[m].rearrange("p (b q) -> p b q", b=4))
                P.dma('sync', rab[:], rasc[m])
                P.dma('scalar', yb[:], ysc[m].rearrange("p (c t) -> p c t", c=4))
                P.dma('sync', bgb[:], bg2sc[m].rearrange("p (c e) -> p c e", e=2))
                if m == 0:
                    P.memset('gpsimd', tlb[:, 0, :], 0.0)
                with P.group("tl%d" % (m % 2)):
                    if m > 0:
                        P.dma('gpsimd', tlb[:, 0, :], gathu[3 * B_NM + m - 1].rearrange("(p e) -> p e", e=8), writes=[tlb])
                    for cd in range(1, 4):
                        P.dma('gpsimd', tlb[:, cd, :], gathu[(cd - 1) * B_NM + m].rearrange("(p e) -> p e", e=8), writes=[tlb])
                hf = halo[:].rearrange("p c e -> p (c e)")
                P.ts('vector', hf, tlb[:, 0, :], selw_s[:, 0:1], ALU.mult)
                for cd in range(1, 4):
                    P.stt(hf, tlb[:, cd, :], selw_s[:, cd:cd + 1], hf, ALU.mult, ALU.add)
                P.tt('vector', pt0[:], halo[:, :, 1], cw_s[:, :, 1], ALU.mult)
                P.tt('vector', pt1[:], halo[:, :, 0], cw_s[:, :, 0], ALU.mult)
                P.tt('vector', pt0[:], pt0[:], pt1[:], ALU.add)
                P.tt('vector', pt0[:], pt0[:], bgb[:, :, 0], ALU.mult)
                P.tt('vector', yb[:, :, 0], yb[:, :, 0], pt0[:], ALU.add)
                P.tt('vector', pt1[:], halo[:, :, 1], cw_s[:, :, 0], ALU.mult)
                P.tt('vector', pt1[:], pt1[:], bgb[:, :, 1], ALU.mult)
                P.tt('vector', yb[:, :, 1], yb[:, :, 1], pt1[:], ALU.add)
                P.act(ysq[:], yb[:], AF.Square)
                for ch in range(4):
                    P.mm(px[:, 0:1], ysq[:, ch, :], onesf[:], start=(ch == 0), stop=(ch == 3))
                P.act(rc[:], px[:, 0:1], AF.Sqrt, bias=EPS, scale=1.0 / 512)
                P.recip(rc2[:], rc[:])
                P.copy('gpsimd', ybf[:], yb[:])
                for nh in range(2):
                    for kc in range(4):
                        P.mm(pa[nh][:], ab[:, kc, :], Wo[:, kc, nh * 512:(nh + 1) * 512], start=(kc == 0), stop=(kc == 3))
                    for kc in range(4):
                        P.mm(pcv[nh][:], ybf[:, kc, :], Wo[:, 4 + kc, nh * 512:(nh + 1) * 512], start=(kc == 0), stop=(kc == 3))
                x1b = x1[j]
                for nh in range(2):
                    sl = slice(nh * 512, (nh + 1) * 512)
                    P.stt(x1b[:, sl], pa[nh][:], rab[:, 0:1], xb[:, sl], ALU.mult, ALU.add)
                    P.stt(x1b[:, sl], pcv[nh][:], rc2[:, 0:1], x1b[:, sl], ALU.mult, ALU.add)
                if G.get('dbg'):
                    P.dma('gpsimd', G['dbg']['x1'][m], x1b[:])
                    P.dma('gpsimd', G['dbg']['y'][m], ybf[:].rearrange("p c t -> p (c t)"))
                    P.dma('gpsimd', G['dbg']['rc'][m], rc2[:])
                P.act(h2n[:], x1b[:], AF.Square, accum_out=ss[:])
                P.act(rstd[:], ss[:], AF.Sqrt, bias=EPS, scale=1.0 / D)
                P.recip(rstd[:], rstd[:])
                P.stt(h2n[:], x1b[:], rstd[:, 0:1], gffn_s[:], ALU.mult, ALU.mult)
                for kc in range(8):
                    P.tr(pT[:, kc, :], h2n[:, kc * 128:(kc + 1) * 128], identb[:])
                P.copy('scalar', h2T[:, :, j * 128:(j + 1) * 128], pT[:])
            for u in range(8):
                st = wst[up_n[0] % 2]
                wb = Wup[up_n[0] % 2]
                up_n[0] += 1
                P.dma('sync' if u % 2 == 0 else 'scalar', st[:], W['w_up'][l, :, u * 512:(u + 1) * 512].rearrange("(kc p) n -> p kc n", p=128))
                P.copy('gpsimd' if u % 2 == 0 else 'vector', wb[:], st[:])
                for fl in range(4):
                    f = u * 4 + fl
                    pp = pu[f % 2]
                    for kc in range(8):
                        P.mm(pp[:], wb[:, kc, fl * 128:(fl + 1) * 128], h2T[:, kc, :], start=(kc == 0), stop=(kc == 7))
                    rb = rl[0]
                    P.act(rb[:], pp[:], AF.Relu)
                    P.tt('gpsimd' if f % 2 == 0 else 'vector', actT[:, f, :], rb[:], rb[:], ALU.mult)
            for j in range(4):
                m = bt * 4 + j
                ob = x1[j]
                for nh in range(2):
                    pp = pa[nh] if j % 2 == 0 else pcv[nh]
                    for f in range(32):
                        P.mm(pp[:], actT[:, f, j * 128:(j + 1) * 128], Wdn[:, f, nh * 512:(nh + 1) * 512], start=(f == 0), stop=(f == 31))
                    P.tt('vector', ob[:, nh * 512:(nh + 1) * 512], pp[:], x1[j][:, nh * 512:(nh + 1) * 512], ALU.add)
                P.dma('sync', xdst[m], ob[:])


def build_fused(nlayers=DEPTH, stages="AGZBC", dbg=False):
    C = Ctx()
    P = C.P
    W = {}
    W['x0'] = C.din("x0", [B_NM, 128, D], F32)
    W['w_in'] = C.din("w_in", [nlayers, D, INW], F32)
    W['gmix'] = C.din("gmix", [nlayers, 128, 8], F32)
    W['gq'] = C.din("gq", [nlayers, 128, 512], F32)
    W['gk3'] = C.din("gk3", [nlayers, 128, 3, 128], F32)
    W['cw'] = C.din("cw", [nlayers, 128, 4, 3], F32)
    W['peT'] = C.din("peT", [nlayers, 2, 64, 32], F32)
    W['w1'] = C.din("w1", [nlayers, 2, 2048, 256], F32)
    W['b1b'] = C.din("b1b", [nlayers, 2, 128, 256], F32)
    W['w2'] = C.din("w2", [nlayers, 2, 256, 64], F32)
    W['b2b'] = C.din("b2b", [nlayers, 128, 2, 64], F32)
    W['w_o'] = C.din("w_o", [nlayers, D, D], F32)
    W['gout'] = C.din("gout", [nlayers, 128, 8], F32)
    W['gffn'] = C.din("gffn", [nlayers, 128, D], F32)
    W['w_up'] = C.din("w_up", [nlayers, D, DFF], F32)
    W['w_dn'] = C.din("w_dn", [nlayers, DFF, D], F32)
    W['smask'] = C.din("smask", [128, 4, 512], BF16)
    W['cmask'] = C.din("cmask", [128, 5, 512], BF16)
    W['wmask'] = C.din("wmask", [128, 8, 128], BF16)
    W['Eall'] = C.din("Eall", [128, 64, 128], BF16)
    W['Bsel'] = C.din("Bsel", [128, 512], F32)
    W['kaug'] = C.din("kaug", [4, T], BF16)
    W['kcaug'] = C.din("kcaug", [4, 2, 1024], BF16)
    W['qaug'] = C.din("qaug", [B_NM, 4, 2, 512], BF16)
    W['vcxc'] = C.din("vcxc", [128, 8, 2, 257], BF16)
    W['selw'] = C.din("selw", [128, 4], F32)
    identb_d = C.din("identb", [128, 128], BF16)
    identf_d = C.din("identf", [128, 128], F32)
    onesf_d = C.din("onesf", [128, 1], F32)
    xo = C.dout("xo", [B_NM, 128, D], F32)

    G = {}
    COMPS = [('ks', 1024), ('kw', 1024), ('kc', 1024), ('vc', 1024), ('vs0', 520), ('vs1', 520), ('vw0', 520), ('vw1', 520)]
    G['pay'] = {cn: C.dint("pay_" + cn, [rows, 512], BF16) for cn, rows in COMPS}
    G['gath'] = {cn: C.dint("gath_" + cn, [4 * rows, 512], BF16) for cn, rows in COMPS}
    G['payu'] = C.dint("payu", [B_NM, 1024], F32).ap()
    G['gathu'] = C.dint("gathu", [4 * B_NM, 1024], F32).ap()
    xbuf = C.dint("xbuf", [B_NM, 128, D], F32).ap()
    G['qsc'] = C.dint("qsc", [B_NM, 2, 64, 512], BF16)
    G['gsc'] = C.dint("gsc", [B_NM, 128, 24], F32).ap()
    G['ysc'] = C.dint("ysc", [B_NM, 128, 512], F32).ap()
    G['bg2sc'] = C.dint("bg2sc", [B_NM, 128, 8], F32).ap()
    G['atsc'] = C.dint("atsc", [B_NM, 128, 512], BF16).ap()
    G['rasc'] = C.dint("rasc", [B_NM, 128, 1], F32).ap()
    qsc_ap = G['qsc'].ap()
    if dbg:
        G['dbg'] = {'at': C.dout("dbg_at", [B_NM, 128, 512], BF16), 'ra': C.dout("dbg_ra", [B_NM, 128, 1], F32),
                    'x1': C.dout("dbg_x1", [B_NM, 128, D], F32), 'y': C.dout("dbg_y", [B_NM, 128, 512], BF16),
                    'rc': C.dout("dbg_rc", [B_NM, 128, 1], F32),
                    'kc': C.dout("dbg_kc", [68, 2, 1024], BF16), 'vc': C.dout("dbg_vc", [128, 8, 2, 321], BF16),
                    'ks': C.dout("dbg_ks", [68, T], BF16), 'vs': C.dout("dbg_vs", [128, 128, 130], BF16)}

    G['identb'] = C.sb("identb_s", [128, 128], BF16)
    G['identf'] = C.sb("identf_s", [128, 128], F32)
    G['onesf'] = C.sb("onesf_s", [128, 1], F32)
    with P.group('cst'):
        P.dma('sync', G['identb'][:], identb_d)
        P.dma('sync', G['identf'][:], identf_d)
        P.dma('sync', G['onesf'][:], onesf_d)
    P.barrier()
    rg = [[0, 1, 2, 3], [4, 5, 6, 7]]
    def mk_cc(i_ap, o_ap):
        return lambda e: e.collective_compute("AllGather", ALU.bypass, replica_groups=rg, ins=[i_ap.opt()], outs=[o_ap.opt()])
    cc_fns = [mk_cc(G['pay'][cn].ap(), G['gath'][cn].ap()) for cn, _ in COMPS] + [mk_cc(G['payu'], G['gathu'])]
    for l in range(nlayers):
        xsrc = W['x0'] if l == 0 else xbuf
        xdst = xo if l == nlayers - 1 else xbuf
        if 'A' in stages:
            emit_A(C, l, xsrc, W, G)
        if 'G' in stages:
            P.collectives(cc_fns)
        with C.phase("ZB%d" % l):
            G['kcT_s'] = C.sb("kcT_s", [68, 2, 1024], BF16)
            G['vcx_s'] = C.sb("vcx_s", [128, 8, 2, 321], BF16)
            with P.group('cst2'):
                P.dma('sync', G['kcT_s'][64:68, :, :], W['kcaug'])
                P.dma('scalar', G['vcx_s'][:, :, :, 64:321], W['vcxc'])
            if 'Z' in stages:
                emit_Z(C, l, W, G)
            Gb = dict(G)
            Gb['qsc'] = qsc_ap
            if 'B' in stages:
                emit_B(C, l, W, Gb)
        if 'C' in stages:
            emit_C(C, l, xsrc, xdst, W, G)
    if 'C' not in stages:
        with C.phase("dbg"):
            xt = [C.sb("xt%d" % i, [128, D], F32) for i in range(2)]
            for m in range(B_NM):
                P.dma('sync', xt[m % 2][:], W['x0'][m])
                P.dma('sync', xo[m], xt[m % 2][:])
    return C.finish()


IDENTB = np.eye(128, dtype=np.float32).astype(NPBF)
ONESF = np.ones((128, 1), np.float32)


def _bf(a):
    return np.ascontiguousarray(a).astype(NPBF)


def _selmap():
    i = np.arange(1024)[:, None] * 16
    j = np.arange(256)[None, :] * 64
    sh = np.clip(np.minimum(i + 32, j + 64) - np.maximum(i, j), 0, None).astype(np.float32) / 32.0
    sh[1023] = 0.0
    return sh


def _pos_rows(pos):
    pos = np.maximum(pos, 0)
    return np.stack([np.ones_like(pos), np.ones_like(pos), pos % 128, pos // 128]).astype(np.float32)


def core_consts(s):
    ki = np.arange(128)[:, None]
    qi = np.arange(128)[None, :]
    tri_gt = np.where(ki > qi, NEGM, 0.0).astype(np.float32)
    tri_le = np.where(ki <= qi, NEGM, 0.0).astype(np.float32)
    full = np.full((128, 128), NEGM, np.float32)
    zero = np.zeros((128, 128), np.float32)
    smask = np.tile(np.stack([zero if d < s else (tri_gt if d == s else full) for d in range(4)], axis=1), (1, 1, 4))
    cm = []
    for v in range(5):
        ip = ki - 32 * v
        vis = (16 * ip + 31) <= (128 * s + qi)
        cm.append(np.where(vis, 0.0, NEGM).astype(np.float32))
    cmask = np.tile(np.stack(cm, axis=1), (1, 1, 4))
    wm = []
    for d in range(8):
        off = s + 4 - d
        if off == 0:
            wm.append((ki <= qi).astype(np.float32))
        elif off == 4:
            wm.append((ki > qi).astype(np.float32))
        elif 0 < off < 4:
            wm.append(np.ones((128, 128), np.float32))
        else:
            wm.append(zero)
    wmask = np.stack(wm, axis=1)
    E = np.zeros((128, 64, 128), np.float32)
    for kt in range(64):
        for k in range(128):
            E[2 * kt + k // 64, kt, k] = 1.0
    r = np.arange(512)[None, :] - 256 - 2 * s
    qq = np.arange(128)[:, None]
    Brel = np.zeros((128, 512), np.float32)
    Brel = np.where(r >= 2, np.float32(-1e30), Brel)
    Brel = np.where(r == 1, np.where(qq >= 64, np.float32(1e4), np.float32(-1e30)), Brel)
    Brel = np.where(r == 0, np.float32(1e4), Brel)
    Brel = np.where((r == -1) & (qq < 64), np.float32(1e4), Brel)
    rr, mm, kk = np.meshgrid(np.arange(4), np.arange(32), np.arange(128), indexing='ij')
    kpos = (128 * (4 * mm + rr) + kk).reshape(-1)
    kaug = _pos_rows(kpos)
    kc = _pos_rows(np.arange(1024) * 16 + 31)
    kcaug = np.stack([kc, kc], axis=1)
    qaug = np.zeros((B_NM, 4, 2, 512), np.float32)
    qv = np.arange(128)
    for m in range(B_NM):
        c = 4 * m + s
        for g in range(2):
            for rh in range(4):
                sl = 2.0 ** (-(4 * g + rh + 1))
                cs = slice(rh * 128, (rh + 1) * 128)
                qaug[m, 0, g, cs] = -sl * qv
                qaug[m, 1, g, cs] = -sl * 128.0 * c
                qaug[m, 2, g, cs] = sl
                qaug[m, 3, g, cs] = sl * 128.0
    sm = _selmap().reshape(8, 128, 256).transpose(1, 0, 2)
    vcxc = np.zeros((128, 8, 2, 257), np.float32)
    vcxc[:, :, :, 0] = 1.0
    vcxc[127, 7, :, 0] = 0.0
    vcxc[:, :, :, 1:257] = sm[:, :, None, :]
    selw = np.zeros((128, 4), np.float32)
    selw[:, s] = 1.0
    return {'smask': _bf(smask), 'cmask': _bf(cmask), 'wmask': _bf(wmask), 'Eall': _bf(E),
            'Bsel': np.ascontiguousarray(Brel.astype(np.float32)), 'kaug': _bf(kaug), 'kcaug': _bf(kcaug),
            'qaug': _bf(qaug), 'vcxc': _bf(vcxc), 'selw': selw,
            'identb': IDENTB, 'identf': np.eye(128, dtype=np.float32), 'onesf': ONESF}


def prep_fused(p, L=DEPTH):
    p = {k: (v if k == 'x' else v[:L]) for k, v in p.items()}
    common = {
        'w_in': np.ascontiguousarray(p['w_in'], dtype=np.float32),
        'gmix': np.ascontiguousarray(p['g_mix_norm'].reshape(L, 8, 128).transpose(0, 2, 1)),
        'gq': np.ascontiguousarray(np.tile(p['g_q'][:, None, :], (1, 128, 8))),
        'gk3': np.ascontiguousarray(np.broadcast_to(np.tile(p['g_k'], (1, 1, 2))[:, None], (L, 128, 3, 128))),
        'cw': np.ascontiguousarray(p['conv_w'].reshape(L, 3, 4, 128).transpose(0, 3, 2, 1)),
        'peT': np.ascontiguousarray(p['pe_cmp'].transpose(0, 1, 3, 2)),
        'w1': np.ascontiguousarray(p['w_cmp1'], dtype=np.float32),
        'b1b': np.ascontiguousarray(np.broadcast_to(p['b_cmp1'][:, :, None, :], (L, 2, 128, 256))),
        'w2': np.ascontiguousarray(p['w_cmp2'], dtype=np.float32),
        'b2b': np.ascontiguousarray(np.broadcast_to(p['b_cmp2'][:, None], (L, 128, 2, 64))),
        'w_o': np.ascontiguousarray(p['w_o'], dtype=np.float32),
        'gout': np.ascontiguousarray(p['g_out'].reshape(L, 8, 128).transpose(0, 2, 1)),
        'gffn': np.ascontiguousarray(np.tile(p['g_ffn_norm'][:, None, :], (1, 128, 1))),
        'w_up': np.ascontiguousarray(p['w_up'], dtype=np.float32),
        'w_dn': np.ascontiguousarray(p['w_down'], dtype=np.float32),
    }
    maps = []
    x = np.ascontiguousarray(p['x'], dtype=np.float32)
    for b in range(NB):
        xv = x[b].reshape(T // 128, 128, D)
        for s in range(4):
            mp = dict(common)
            mp.update(core_consts(s))
            mp['x0'] = np.ascontiguousarray(xv[np.arange(B_NM) * 4 + s])
            maps.append(mp)
    return maps


_PROG = {}


def run_fused(inputs, nlayers=DEPTH):
    p = {k: np.asarray(v) for k, v in inputs.items()}
    if nlayers not in _PROG:
        _PROG[nlayers] = build_fused(nlayers)
    res = run_bass_kernel_spmd(_PROG[nlayers], prep_fused(p, nlayers), core_ids=list(range(NCORES)))
    out = np.empty((NB, T, D), np.float32)
    for b in range(NB):
        ov = out[b].reshape(T // 128, 128, D)
        for s in range(4):
            ov[np.arange(B_NM) * 4 + s] = np.asarray(res.results[4 * b + s]['xo'])
    return out


def kernel(**inputs):
    return run_fused(inputs, DEPTH)
```

```python
import contextlib
import numpy as np
import ml_dtypes
import concourse.bass as bass
import concourse.mybir as mybir
from concourse.bass_utils import run_bass_kernel_spmd

F32 = mybir.dt.float32
BF16 = mybir.dt.bfloat16
AF = mybir.ActivationFunctionType
ALU = mybir.AluOpType
AX = mybir.AxisListType
NPBF = ml_dtypes.bfloat16

ENGS = ['tensor', 'vector', 'scalar', 'gpsimd', 'sync']
NCORES = 8
D = 1024
T = 16384
NB = 2
DEPTH = 4
INW = 2840
EPS = 1e-6
NEGM = -30000.0


def key_of(ap):
    t = getattr(ap, 'tensor', ap)
    return t.name.split('@')[0]


class Prog:
    def __init__(self, nc):
        self.nc = nc
        self.q = {e: [] for e in ENGS}
        self.semcount = {}
        self.res = {}
        self.seen = {e: {} for e in ENGS}

    grp = None

    @contextlib.contextmanager
    def group(self, sem):
        s = 'd_' + sem
        self.grp = {'sem': sem, 's': s, 'start': self.semcount.get(s, 0), 'keys': set(), 'pre': {}}
        try:
            yield
        finally:
            g, self.grp = self.grp, None
            v = self.semcount.get(s, 0)
            for k in g['keys']:
                self.res[k]['w'] = (s, v)

    def _need(self, eng, reads, writes):
        need = {}

        def add(tok):
            if tok is None:
                return
            s, v = tok
            if eng == 'tensor' and s == 'e_tensor':
                return
            if self.grp is not None and s == self.grp['s'] and v > self.grp['start']:
                return
            if need.get(s, 0) < v:
                need[s] = v
        for k in reads:
            st = self.res.get(k)
            if st:
                add(st['w'])
        for k in writes:
            sts = [self.res.get(k)]
            if self.grp is not None:
                if k not in self.grp['pre']:
                    st0 = self.res.get(k)
                    self.grp['pre'][k] = {'w': st0['w'], 'r': dict(st0['r'])} if st0 else None
                sts.append(self.grp['pre'][k])
            for st in sts:
                if st:
                    add(st['w'])
                    for s, v in st['r'].items():
                        add((s, v))
        for s, v in need.items():
            if self.seen[eng].get(s, 0) < v:
                self.q[eng].append(('wait', s, v))
                self.seen[eng][s] = v

    def _update(self, reads, writes, tok):
        s, v = tok
        for k in reads:
            st = self.res.setdefault(k, {'w': None, 'r': {}})
            if st['r'].get(s, 0) < v:
                st['r'][s] = v
        for k in writes:
            self.res[k] = {'w': tok, 'r': {}}

    def op(self, eng, fn, reads=(), writes=()):
        reads = [r if isinstance(r, str) else key_of(r) for r in reads]
        writes = [r if isinstance(r, str) else key_of(r) for r in writes]
        self._need(eng, reads, writes)
        s = 'e_' + eng
        v = self.semcount.get(s, 0) + 1
        self.semcount[s] = v
        self.q[eng].append(('op', fn, s, 1))
        self._update(reads, writes, (s, v))

    def dma(self, eng, out, in_, reads=None, writes=None, sem=None, **kw):
        if reads is None:
            reads = [] if in_.tensor.name in self.dram_names else [in_]
        if writes is None:
            writes = [] if out.tensor.name in self.dram_names else [out]
        assert reads or writes or sem
        reads = [r if isinstance(r, str) else key_of(r) for r in reads]
        writes = [r if isinstance(r, str) else key_of(r) for r in writes]
        if self.grp is not None:
            sem = self.grp['sem']
            self.grp['keys'].update(writes)
        if sem is None:
            sem = writes[0] if writes else 'st_' + reads[0]
        self._need(eng, reads, writes)
        s = 'd_' + sem
        v = self.semcount.get(s, 0) + 16
        self.semcount[s] = v
        self.q[eng].append(('op', lambda e: e.dma_start(out=out, in_=in_, **kw), s, 16))
        self._update(reads, writes, (s, v))

    dram_names = set()

    def mm(self, out, lhsT, rhs, start=True, stop=True, reads=None, writes=None):
        self.op('tensor', lambda e: e.matmul(out, lhsT=lhsT, rhs=rhs, start=start, stop=stop),
                reads=reads if reads is not None else [lhsT, rhs],
                writes=writes if writes is not None else [out])

    def tr(self, out, in_, ident, reads=None, writes=None):
        self.op('tensor', lambda e: e.transpose(out, in_, ident),
                reads=reads if reads is not None else [in_, ident],
                writes=writes if writes is not None else [out])

    def act(self, out, in_, func, bias=None, scale=None, accum_out=None, reads=None, writes=None, eng='scalar'):
        kw = {}
        rd = [in_]
        if bias is not None:
            kw['bias'] = bias
            if not isinstance(bias, (int, float)):
                rd.append(bias)
        if scale is not None:
            kw['scale'] = scale
            if not isinstance(scale, (int, float)):
                rd.append(scale)
        wr = [out]
        if accum_out is not None:
            kw['accum_out'] = accum_out
            wr.append(accum_out)
        self.op('scalar', lambda e: e.activation(out=out, in_=in_, func=func, **kw),
                reads=reads if reads is not None else rd,
                writes=writes if writes is not None else wr)

    def tt(self, eng, out, in0, in1, op, reads=None, writes=None):
        self.op(eng, lambda e: e.tensor_tensor(out=out, in0=in0, in1=in1, op=op),
                reads=reads if reads is not None else [in0, in1],
                writes=writes if writes is not None else [out])

    def ts(self, eng, out, in0, s1, op0, s2=None, op1=None, reads=None, writes=None):
        rd = [in0]
        if not isinstance(s1, (int, float)):
            rd.append(s1)
        if s2 is not None and not isinstance(s2, (int, float)):
            rd.append(s2)
        if op1 is None:
            fn = lambda e: e.tensor_scalar(out=out, in0=in0, scalar1=s1, scalar2=None, op0=op0)
        else:
            fn = lambda e: e.tensor_scalar(out=out, in0=in0, scalar1=s1, scalar2=s2, op0=op0, op1=op1)
        self.op(eng, fn, reads=reads if reads is not None else rd,
                writes=writes if writes is not None else [out])

    def stt(self, out, in0, scalar, in1, op0, op1, reads=None, writes=None):
        rd = [in0, in1]
        if not isinstance(scalar, (int, float)):
            rd.append(scalar)
        self.op('vector', lambda e: e.scalar_tensor_tensor(out=out, in0=in0, scalar=scalar, in1=in1, op0=op0, op1=op1),
                reads=reads if reads is not None else rd,
                writes=writes if writes is not None else [out])

    def copy(self, eng, out, in_, reads=None, writes=None):
        if eng == 'scalar':
            fn = lambda e: e.activation(out=out, in_=in_, func=AF.Copy)
        else:
            fn = lambda e: e.tensor_copy(out=out, in_=in_)
        self.op(eng, fn, reads=reads if reads is not None else [in_],
                writes=writes if writes is not None else [out])

    def recip(self, out, in_):
        self.op('vector', lambda e: e.reciprocal(out=out, in_=in_), reads=[in_], writes=[out])

    def reduce(self, out, in_, op=ALU.add, axis=AX.X):
        self.op('vector', lambda e: e.tensor_reduce(out=out, in_=in_, axis=axis, op=op), reads=[in_], writes=[out])

    def memset(self, eng, ap, val):
        self.op(eng, lambda e: e.memset(ap, val), writes=[ap])

    def barrier(self):
        for e in ENGS:
            for s, v in self.semcount.items():
                if self.seen[e].get(s, 0) < v:
                    self.q[e].append(('wait', s, v))
                    self.seen[e][s] = v
        self.res = {}

    def collectives(self, fns):
        self.barrier()
        s = 'c_cc'
        for fn in fns:
            v = self.semcount.get(s, 0) + 1
            self.semcount[s] = v
            self.q['gpsimd'].append(('op', fn, s, 1))
        self.barrier()

    def emit(self):
        nc = self.nc
        for s, v in self.semcount.items():
            if self.seen['sync'].get(s, 0) < v:
                self.q['sync'].append(('wait', s, v))
        with contextlib.ExitStack() as es:
            semh = {s: es.enter_context(nc.semaphore(s)) for s in self.semcount}
            block = es.enter_context(nc.Block())

            def mk(engname):
                def body(e):
                    for it in self.q[engname]:
                        if it[0] == 'wait':
                            e.wait_ge(semh[it[1]], it[2])
                        else:
                            ins = it[1](e)
                            ins.then_inc(semh[it[2]], it[3])
                return body
            for engname in ENGS:
                if self.q[engname]:
                    getattr(block, engname)(mk(engname))


def AP_(t, offset, dims):
    return bass.AP(t, offset, [list(d) for d in dims])


class Ctx:
    def __init__(self):
        self.nc = bass.Bass("TRN2", target_bir_lowering=False)
        self.es = contextlib.ExitStack()
        self.P = Prog(self.nc)
        self.P.dram_names = set()

    def din(self, name, shape, dt):
        self.P.dram_names.add(name)
        return self.nc.dram_tensor(name, list(shape), dt, kind="ExternalInput").ap()

    def dout(self, name, shape, dt):
        self.P.dram_names.add(name)
        return self.nc.dram_tensor(name, list(shape), dt, kind="ExternalOutput").ap()

    tag = None

    def dint(self, name, shape, dt):
        self.P.dram_names.add(name)
        return self.nc.dram_tensor(name, list(shape), dt)

    def sb(self, name, shape, dt):
        if self.tag:
            name = name + '@' + self.tag
        return self.es.enter_context(self.nc.sbuf_tensor(name, list(shape), dt))

    def ps(self, name, shape, dt):
        if self.tag:
            name = name + '@' + self.tag
        return self.es.enter_context(self.nc.psum_tensor(name, list(shape), dt))

    @contextlib.contextmanager
    def phase(self, tag):
        old, oldtag = self.es, self.tag
        self.es, self.tag = contextlib.ExitStack(), tag
        try:
            yield
        finally:
            self.P.barrier()
            self.es.close()
            self.es, self.tag = old, oldtag

    def finish(self):
        self.P.emit()
        self.es.close()
        return self.nc


B_NM = 32
DFF = 4096
PAYR = 6176
KS_R0, KW_R0, KC_R0, VC_R0, VS_R0, VW_R0 = 0, 1024, 2048, 3072, 4096, 5136
TP = T + 128


def emit_A(C, l, xsrc, W, G):
    P = C.P
    with C.phase("A%d" % l):
        Wbf = C.sb("Wbf", [128, 8, INW], BF16)
        wst = [C.sb("wst%d" % i, [128, INW // 2], F32) for i in range(2)]
        gmix_s = C.sb("gmix_s", [128, 8], F32)
        gq_s = C.sb("gq_s", [128, 512], F32)
        gk3_s = C.sb("gk3_s", [128, 3, 128], F32)
        cw_s = C.sb("cw_s", [128, 4, 3], F32)
        xt = [C.sb("xt%d" % i, [128, D], F32) for i in range(2)]
        junk = C.sb("junk", [128, D], BF16)
        ss = [C.sb("ss%d" % i, [128, 1], F32) for i in range(2)]
        rstd = [C.sb("rstd%d" % i, [128, 1], F32) for i in range(2)]
        hn = [C.sb("hn%d" % i, [128, D], BF16) for i in range(2)]
        hT = [C.sb("hT%d" % i, [128, 8, 128], BF16) for i in range(2)]
        zs = [C.sb("zs%d" % i, [128, 1304], F32) for i in range(2)]
        hcs = C.sb("hcs", [128, 4, 128], F32)
        ub = [C.sb("ub%d" % i, [128, 4, 130], F32) for i in range(2)]
        bgs = [C.sb("bgs%d" % i, [128, 4, 128], F32) for i in range(2)]
        sq = C.sb("sq", [128, 1280], F32)
        ssg = C.sb("ssg", [128, 20], F32)
        rg = C.sb("rg", [128, 20], F32)
        qtmp = C.sb("qtmp", [128, 512], F32)
        ktmp = C.sb("ktmp", [128, 256], F32)
        nrm = C.sb("nrm", [128, 1024], BF16)
        trs = C.sb("trs", [128, 8, 128], BF16)
        vsw = C.sb("vsw", [128, 2, 130], BF16)
        gts = C.sb("gts", [128, 24], F32)
        cv0 = C.sb("cv0", [128, 4, 128], F32)
        cv1 = C.sb("cv1", [128, 4, 128], F32)
        yv = C.sb("yv", [128, 4, 128], F32)
        ut = C.sb("ut", [128, 4, 2], F32)
        bg2 = C.sb("bg2", [128, 4, 2], F32)
        pTa = C.ps("pTa", [128, 8, 128], BF16)
        pTb = C.ps("pTb", [128, 8, 128], BF16)
        pz = [C.ps("pz%d" % i, [128, 512], F32) for i in range(3)]
        pc = [C.ps("pc%d" % i, [128, 4, 128], F32) for i in range(3)]
        identb = G['identb']

        with P.group('cst'):
            P.dma('sync', gmix_s[:], W['gmix'][l])
            P.dma('sync', gq_s[:], W['gq'][l])
            P.dma('sync', gk3_s[:], W['gk3'][l])
            P.dma('sync', cw_s[:], W['cw'][l])
        P.memset('gpsimd', vsw[:], 1.0)
        P.memset('gpsimd', ub[0][:], 0.0)
        P.memset('gpsimd', ub[1][:], 0.0)
        for kc in range(8):
            for hf in range(2):
                st = wst[hf]
                P.dma('sync' if hf == 0 else 'scalar', st[:], W['w_in'][l, kc * 128:(kc + 1) * 128, hf * 1420:(hf + 1) * 1420])
                P.ts('vector' if hf == 0 else 'gpsimd', Wbf[:, kc, hf * 1420:(hf + 1) * 1420], st[:], gmix_s[:, kc:kc + 1], ALU.mult)

        pay, payu, qsc, gsc, ysc, bg2sc = G['pay'], G['payu'], G['qsc'], G['gsc'], G['ysc'], G['bg2sc']

        def F_pre(m):
            xb = xt[m % 2]
            P.dma('sync', xb[:], xsrc[m])
            P.act(junk[:], xb[:], AF.Square, accum_out=ss[m % 2][:])
            P.act(rstd[m % 2][:], ss[m % 2][:], AF.Sqrt, bias=EPS, scale=1.0 / D)
            P.recip(rstd[m % 2][:], rstd[m % 2][:])
            P.act(hn[m % 2][:], xb[:], AF.Copy, scale=rstd[m % 2][:, 0:1])

        def F_pe_a(m):
            for kc in range(8):
                P.tr(pTa[:, kc, :], hn[m % 2][:, kc * 128:(kc + 1) * 128], identb[:])
            P.copy('vector', hT[m % 2][:], pTa[:])

        def F_pe_b(m):
            hTb, zb, ubb, bgb = hT[m % 2], zs[m % 2], ub[m % 2], bgs[m % 2]
            for bi, (c0, c1) in enumerate([(0, 512), (512, 1024), (1024, 1304)]):
                for kc in range(8):
                    P.mm(pz[bi][:, 0:c1 - c0], hTb[:, kc, :], Wbf[:, kc, c0:c1], start=(kc == 0), stop=(kc == 7))
                P.copy('scalar', zb[:, c0:c1], pz[bi][:, 0:c1 - c0])
            for ch in range(12):
                for kc in range(8):
                    P.mm(pc[ch // 4][:, ch % 4, :], Wbf[:, kc, 1304 + ch * 128:1304 + (ch + 1) * 128], hTb[:, kc, :],
                         start=(kc == 0), stop=(kc == 7))
                if ch == 3:
                    P.copy('scalar', hcs[:], pc[0][:])
                if ch == 7:
                    P.tt('vector', ubb[:, :, 2:130], pc[1][:], hcs[:], ALU.mult)
            P.copy('vector', bgb[:], pc[2][:])

        def G1(m):
            zb, ubb, bgb = zs[m % 2], ub[m % 2], bgs[m % 2]
            P.act(sq[:], zb[:, 0:1280], AF.Square)
            P.reduce(ssg[:], sq[:].rearrange("p (g d) -> p g d", d=64))
            P.act(rg[:, 0:8], ssg[:, 0:8], AF.Sqrt, bias=64 * EPS, scale=1.0)
            P.act(rg[:, 8:20], ssg[:, 8:20], AF.Sqrt, bias=EPS, scale=1.0 / 64)
            P.recip(rg[:], rg[:])
            P.tt('vector', qtmp[:].rearrange("p (g d) -> p g d", d=64), zb[:, 0:512].rearrange("p (g d) -> p g d", d=64),
                 AP_(rg, 0, [[20, 128], [1, 8], [0, 64]]), ALU.mult)
            P.tt('gpsimd', nrm[:, 0:512], qtmp[:], gq_s[:], ALU.mult)
            P.tt('vector', ktmp[:, 0:128].rearrange("p (g d) -> p g d", d=64), zb[:, 768:896].rearrange("p (g d) -> p g d", d=64),
                 AP_(rg, 12, [[20, 128], [1, 2], [0, 64]]), ALU.mult)
            P.tt('vector', ktmp[:, 128:256].rearrange("p (g d) -> p g d", d=64), zb[:, 1024:1152].rearrange("p (g d) -> p g d", d=64),
                 AP_(rg, 16, [[20, 128], [1, 2], [0, 64]]), ALU.mult)
            P.tt('gpsimd', nrm[:, 512:768], ktmp[:], gk3_s[:, 1:3, :].rearrange("p a d -> p (a d)"), ALU.mult)
            P.copy('gpsimd', nrm[:, 768:1024], zb[:, 512:768])
            P.copy('gpsimd', AP_(vsw, 1, [[260, 128], [65, 2], [1, 64]]), zb[:, 896:1024].rearrange("p (g d) -> p g d", d=64))
            P.copy('gpsimd', AP_(vsw, 131, [[260, 128], [65, 2], [1, 64]]), zb[:, 1152:1280].rearrange("p (g d) -> p g d", d=64))
            P.dma('scalar', AP_(pay['vs%d' % (m // 16)], (m % 16) * 128 * 130, [[130, 128], [1, 130]]), vsw[:, 0, :])
            P.dma('scalar', AP_(pay['vw%d' % (m // 16)], (m % 16) * 128 * 130, [[130, 128], [1, 130]]), vsw[:, 1, :])
            P.act(gts[:], zb[:, 1280:1304], AF.Sigmoid)
            P.dma('scalar', gsc[m], gts[:])
            P.tt('gpsimd', cv0[:], ubb[:, :, 2:130], AP_(cw_s, 2, [[12, 128], [3, 4], [0, 128]]), ALU.mult)
            P.tt('gpsimd', cv1[:], ubb[:, :, 1:129], AP_(cw_s, 1, [[12, 128], [3, 4], [0, 128]]), ALU.mult)
            P.tt('gpsimd', cv0[:], cv0[:], cv1[:], ALU.add)
            P.tt('gpsimd', cv1[:], ubb[:, :, 0:128], AP_(cw_s, 0, [[12, 128], [3, 4], [0, 128]]), ALU.mult)
            P.tt('gpsimd', cv0[:], cv0[:], cv1[:], ALU.add)
            P.tt('vector', yv[:], bgb[:], cv0[:], ALU.mult)
            P.dma('scalar', ysc[m].rearrange("p (c t) -> p c t", c=4), yv[:])
            P.copy('gpsimd', ut[:], ubb[:, :, 128:130])
            P.dma('sync', payu[m].rearrange("(p e) -> p e", e=8), ut[:].rearrange("p c e -> p (c e)"))
            P.copy('gpsimd', bg2[:], bgb[:, :, 0:2])
            P.dma('sync', bg2sc[m], bg2[:].rearrange("p c e -> p (c e)"))

        def G2(m):
            for bk in range(8):
                P.tr(pTb[:, bk, :], nrm[:, bk * 128:(bk + 1) * 128], identb[:])
            P.copy('vector', trs[:], pTb[:])
            for e in range(2):
                for g in range(2):
                    P.dma('sync' if g == 0 else 'scalar', AP_(qsc, m * 65536 + g * 32768 + e * 128, [[512, 64], [256, 2], [1, 128]]),
                          trs[e * 64:(e + 1) * 64, 2 * g:2 * g + 2, :])
            for ci, cn in enumerate(('ks', 'kw', 'kc', 'vc')):
                P.dma('sync' if ci % 2 == 0 else 'scalar', AP_(pay[cn], m * 128, [[4096, 128], [1, 128]]), trs[:, 4 + ci, :])

        F_pre(0)
        F_pre(1)
        F_pe_a(0)
        F_pe_b(0)
        for m in range(B_NM):
            if m + 2 < B_NM:
                F_pre(m + 2)
            if m + 1 < B_NM:
                F_pe_a(m + 1)
            G1(m)
            if m + 1 < B_NM:
                F_pe_b(m + 1)
            G2(m)


def emit_Z(C, l, W, G):
    P = C.P
    with C.phase("Z%d" % l):
        kcT_all = C.sb("kcT_all", [128, TP], BF16)
        vcT_all = C.sb("vcT_all", [128, TP], BF16)
        W1bf = C.sb("W1bf", [128, 2, 32, 256], BF16)
        w1st = [C.sb("w1st%d" % i, [128, 8, 256], F32) for i in range(2)]
        W2bf = C.sb("W2bf", [128, 2, 2, 64], BF16)
        w2st = C.sb("w2st", [128, 2, 2, 64], F32)
        peT_s = C.sb("peT_s", [64, 2, 32], F32)
        pebf = C.sb("pebf", [64, 2, 32, 128], BF16)
        b1b_s = C.sb("b1b_s", [128, 2, 256], F32)
        b2b_s = C.sb("b2b_s", [128, 2, 64], F32)
        gkc = C.sb("gkc", [128, 128], F32)
        c1b = C.sb("c1b", [128, 2, 256], F32)
        hid = C.sb("hid", [128, 256], F32)
        g_x2 = C.sb("g_x2", [128, 256], F32)
        g_in = C.sb("g_in", [128, 256], F32)
        g_sg = C.sb("g_sg", [128, 256], F32)
        hbf = C.sb("hbf", [128, 256], BF16)
        hidT = C.sb("hidT", [128, 2, 128], BF16)
        co = C.sb("co", [128, 2, 2, 64], F32)
        cosq = C.sb("cosq", [128, 128], F32)
        css = C.sb("css", [128, 2], F32)
        crs = C.sb("crs", [128, 2], F32)
        kcn = C.sb("kcn", [128, 128], BF16)
        pT = C.ps("pT", [128, 8, 128], BF16)
        px = [C.ps("px%d" % i, [128, 512], F32) for i in range(2)]
        po = C.ps("po", [128, 512], F32)
        identb = G['identb']
        gath = G['gath']
        kcT_s, vcx_s = G['kcT_s'], G['vcx_s']

        with P.group('cst'):
            P.dma('sync', b1b_s[:], W['b1b'][l].rearrange("k p n -> p k n"))
            P.dma('sync', b2b_s[:], W['b2b'][l])
            P.dma('sync', peT_s[:], W['peT'][l].rearrange("k d j -> d k j"))
            P.dma('sync', w2st[:], W['w2'][l].rearrange("k (c p) d -> p k c d", p=128))
            P.dma('sync', gkc[:], W['gk3'][l, :, 0, :])
        P.memset('gpsimd', kcT_all[:, T:TP], 0.0)
        P.memset('gpsimd', vcT_all[:, T:TP], 0.0)
        with P.group('kvcl'):
            for r in range(4):
                for kv, dst in enumerate((kcT_all, vcT_all)):
                    P.dma('sync' if kv == 0 else 'scalar', AP_(dst, r * 128, [[TP, 128], [512, 32], [1, 128]]),
                          AP_(gath['kc' if kv == 0 else 'vc'], r * 1024 * 512, [[4096, 128], [128, 32], [1, 128]]),
                          writes=[dst])
        P.copy('vector', W2bf[:], w2st[:])
        P.copy('vector', pebf[:], AP_(peT_s, 0, [[64, 64], [32, 2], [1, 32], [0, 128]]))
        n = 0
        for kv in range(2):
            for jq in range(4):
                st = w1st[n % 2]
                n += 1
                src = W['w1'][l, kv, jq * 512:(jq + 1) * 512, :].rearrange("(j d) n -> d j n", d=64)
                with P.group("w1st%d" % ((n - 1) % 2)):
                    P.dma('sync', st[0:64, :, :], src, writes=[st])
                    P.dma('scalar', st[64:128, :, :], src, writes=[st])
                P.copy('vector' if jq % 2 == 0 else 'gpsimd', W1bf[:, kv, jq * 8:(jq + 1) * 8, :], st[:])
        for kv in range(2):
            for j in range(32):
                P.mm(px[0][:, 0:256], pebf[:, kv, j, :], W1bf[0:64, kv, j, :], start=(j == 0), stop=(j == 31))
            P.tt('vector', c1b[:, kv, :], px[0][:, 0:256], b1b_s[:, kv, :], ALU.add)
        n = 0
        for ib in range(8):
            for kv in range(2):
                src_all = kcT_all if kv == 0 else vcT_all
                for g in range(2):
                    pp = px[n % 2]
                    n += 1
                    base = 16 * 128 * ib
                    for j in range(32):
                        lhs = AP_(src_all, 64 * g * TP + base + j, [[TP, 64], [16, 128]])
                        P.mm(pp[:, 0:256], lhs, W1bf[64 * g:64 * g + 64, kv, j, :], start=(j == 0), stop=(j == 31),
                             reads=[src_all, W1bf])
                    P.tt('vector', hid[:], pp[:, 0:256], c1b[:, kv, :], ALU.add)
                    P.act(g_x2[:], hid[:], AF.Square)
                    P.ts('gpsimd', g_x2[:], g_x2[:], 0.044715, ALU.mult, 1.0, ALU.add)
                    P.tt('gpsimd', g_in[:], g_x2[:], hid[:], ALU.mult)
                    P.act(g_sg[:], g_in[:], AF.Sigmoid, scale=1.5957691216057308)
                    P.tt('vector', hbf[:], hid[:], g_sg[:], ALU.mult)
                    for c in range(2):
                        P.tr(pT[:, c, :], hbf[:, c * 128:(c + 1) * 128], identb[:])
                    P.copy('vector', hidT[:], pT[:, 0:2, :])
                    for c in range(2):
                        P.mm(po[:, 0:64], hidT[:, c, :], W2bf[:, kv, c, :], start=(c == 0), stop=(c == 1))
                    P.tt('vector', co[:, kv, g, :], po[:, 0:64], b2b_s[:, kv, :], ALU.add)
            P.act(cosq[:], co[:, 0, :, :].rearrange("p g d -> p (g d)"), AF.Square)
            P.reduce(css[:], cosq[:].rearrange("p (g d) -> p g d", d=64))
            P.act(crs[:], css[:], AF.Sqrt, bias=EPS, scale=1.0 / 64)
            P.recip(crs[:], crs[:])
            P.tt('vector', cosq[:].rearrange("p (g d) -> p g d", d=64), co[:, 0, :, :], AP_(crs, 0, [[2, 128], [1, 2], [0, 64]]), ALU.mult)
            P.tt('vector', kcn[:], cosq[:], gkc[:], ALU.mult)
            for g in range(2):
                P.tr(pT[0:64, 4 + g, :], kcn[:, 64 * g:64 * g + 64], identb[:])
            P.copy('vector', kcT_s[0:64, :, ib * 128:(ib + 1) * 128], pT[0:64, 4:6, :])
            P.copy('gpsimd', vcx_s[:, ib, :, 0:64], co[:, 1, :, :])
        if G.get('dbg'):
            P.dma('sync', G['dbg']['kc'], kcT_s[:])
            P.dma('sync', G['dbg']['vc'], vcx_s[:])


def emit_B(C, l, W, G):
    P = C.P
    with C.phase("B%d" % l):
        ksT_s = [C.sb("ksT_s%d" % g, [68, T], BF16) for g in range(2)]
        vsx_s = C.sb("vsx_s", [128, 128, 130], BF16)
        smask_s = C.sb("smask_s", [128, 4, 512], BF16)
        cmask_s = C.sb("cmask_s", [128, 5, 512], BF16)
        wmask_s = C.sb("wmask_s", [128, 8, 128], BF16)
        Eall_s = C.sb("Eall_s", [128, 64, 128], BF16)
        Bsel_s = C.sb("Bsel_s", [128, 512], F32)
        qT_s = [C.sb("qT_s%d" % i, [68, 2, 512], BF16) for i in range(2)]
        kwT_s = [C.sb("kwT_s%d" % i, [68, 2, 1024], BF16) for i in range(2)]
        vwx_s = [C.sb("vwx_s%d" % i, [128, 8, 130], BF16) for i in range(2)]
        gates_s = [C.sb("gates_s%d" % i, [128, 24], F32) for i in range(2)]
        NPB = 6
        Pb = [C.sb("Pb%d" % i, [128, 512], BF16) for i in range(NPB)]
        Pc = [C.sb("Pc%d" % g, [128, 8, 512], BF16) for g in range(2)]
        oc = C.sb("oc", [128, 4, 321], F32)
        den4 = C.sb("den4", [128, 4], F32)
        rd4 = C.sb("rd4", [128, 4], F32)
        coef4 = C.sb("coef4", [128, 4], F32)
        imp = [C.sb("imp%d" % g, [128, 256], F32) for g in range(2)]
        tmpi = C.sb("tmpi", [128, 256], F32)
        m8 = C.sb("m8", [128, 16], F32)
        thr = C.sb("thr", [128, 1], F32)
        nm = [C.sb("nm%d" % g, [128, 256], BF16) for g in range(2)]
        nmT = [C.sb("nmT%d" % g, [128, 2, 128], BF16) for g in range(2)]
        osT = C.sb("osT", [65, 512], F32)
        otmp = C.sb("otmp", [128, 4, 64], F32)
        acc = C.sb("acc", [128, 8, 64], F32)
        accb = C.sb("accb", [128, 512], BF16)
        ssq = C.sb("ssq", [128, 1], F32)
        rat = C.sb("rat", [128, 1], F32)
        rat2 = C.sb("rat2", [128, 1], F32)
        atT = C.sb("atT", [128, 4, 128], BF16)
        ps_s = [C.ps("ps_s%d" % i, [128, 512], F32) for i in range(4)]
        ps_o = [C.ps("ps_o%d" % i, [128, 512], F32) for i in range(1)]
        ps_acc = [C.ps("ps_acc%d" % i, [128, 512], F32) for i in range(2)]
        pm = C.ps("pm", [128, 512], F32)
        pm_b = pm[:].bitcast(BF16)
        identb, identf = G['identb'], G['identf']
        kcT_s, vcx_s = G['kcT_s'], G['vcx_s']
        gath, qsc, gsc, atsc, rasc = G['gath'], G['qsc'], G['gsc'], G['atsc'], G['rasc']

        with P.group('cst'):
            P.dma('sync', cmask_s[:], W['cmask'])
            P.dma('sync', Bsel_s[:], W['Bsel'])
            P.dma('scalar', wmask_s[:], W['wmask'])
            P.dma('scalar', smask_s[:], W['smask'])
            P.dma('scalar', Eall_s[:], W['Eall'])
        with P.group('ksT'):
            for g in range(2):
                P.dma('gpsimd', ksT_s[g][64:68, :], W['kaug'], writes=[ksT_s[g]])
                for r in range(4):
                    P.dma('sync' if r % 2 == 0 else 'scalar', ksT_s[g][0:64, r * 4096:(r + 1) * 4096],
                          AP_(gath['ks'], r * 1024 * 512 + 64 * g * 4096, [[4096, 64], [1, 4096]]),
                          writes=[ksT_s[g]])
        with P.group('vsx'):
            for r in range(4):
                for hv in range(2):
                    P.dma('gpsimd', vsx_s[:, r * 32 + hv * 16:r * 32 + hv * 16 + 16, :],
                          AP_(gath['vs%d' % hv], r * 520 * 512, [[130, 128], [128 * 130, 16], [1, 130]]), writes=[vsx_s])

        if G.get('dbg'):
            P.dma('sync', G['dbg']['ks'], ksT_s[1][:])
            P.dma('sync', G['dbg']['vs'], vsx_s[:])
        state = {'s': 0, 'p': 0, 'x': 0}
        pending = []

        LA = 4

        def run_stream(tiles):
            n = len(tiles)
            bufs = []
            for i in range(n + LA):
                if i < n:
                    ps = ps_s[state['s'] % 4]
                    pb = Pb[state['p'] % NPB]
                    state['s'] += 1
                    state['p'] += 1
                    tiles[i][0](ps)
                    P.act(pb[:], ps[:], AF.Exp)
                    bufs.append(pb)
                j = i - (LA - 2)
                if 0 <= j < n and tiles[j][1] is not None:
                    tiles[j][1](bufs[j])
                if i - LA >= 0:
                    tiles[i - LA][2](bufs[i - LA])

        def mask_mul(pb, map_, mkeys):
            P.tt('vector', pb[:].rearrange("p (a q) -> p a q", a=4), pb[:].rearrange("p (a q) -> p a q", a=4), map_,
                 ALU.mult, reads=[pb] + mkeys, writes=[pb])

        def o_epilogue(psacc, g, br, gs):
            P.copy('vector', osT[:], psacc[0:65, :])
            pmv = pm[:, 0:260].rearrange("p (r d) -> p r d", d=65)
            for r in range(4):
                P.tr(pmv[:, r, :], osT[0:65, r * 128:(r + 1) * 128], identf[0:65, 0:65])
            P.recip(rd4[:], pmv[:, :, 0])
            P.tt('vector', coef4[:], rd4[:], AP_(gs, 12 * g + br, [[24, 128], [3, 4]]), ALU.mult)
            dst = acc[:, 4 * g:4 * g + 4, :]
            P.tt('vector', otmp[:], pmv[:, :, 1:65], AP_(coef4, 0, [[4, 128], [1, 4], [0, 64]]), ALU.mult)
            P.tt('gpsimd', dst, dst, otmp[:], ALU.add)

        for m in range(B_NM):
            qs = qT_s[m % 2]
            kws = kwT_s[m % 2]
            vws = vwx_s[m % 2]
            gs = gates_s[m % 2]
            h0 = 0 if m > 0 else 1
            with P.group("qT_s%d" % (m % 2)):
                P.dma('sync', qs[0:64, :, :], qsc[m].rearrange("g d n -> d g n"), writes=[qs])
                P.dma('scalar', qs[64:68, :, :], W['qaug'][m], writes=[qs])
            P.dma('scalar', gs[:], gsc[m])
            nh = 2 - h0
            c0 = 128 * (m - 1 + h0)
            with P.group("kws%d" % (m % 2)):
                for r in range(4):
                    P.dma('sync' if r % 2 == 0 else 'scalar',
                          AP_(kws, (r * 2 + h0) * 128, [[2048, 64], [1024, 2], [1, 128 * nh]]),
                          AP_(gath['kw'], r * 1024 * 512 + c0, [[4096, 64], [64 * 4096, 2], [1, 128 * nh]]),
                          writes=[kws])
            with P.group("vws%d" % (m % 2)):
                for r in range(4):
                    for hf in range(h0, 2):
                        mp = m - 1 + hf
                        P.dma('gpsimd', vws[:, r * 2 + hf, :],
                              AP_(gath['vw%d' % (mp // 16)], r * 520 * 512 + (mp % 16) * 128 * 130, [[130, 128], [1, 130]]),
                              writes=[vws])
            for g in range(2):
                P.copy('gpsimd', AP_(kws, 64 * 2048 + g * 1024 + h0 * 128, [[2048, 4], [256, 4], [1, 128 * (2 - h0)]]),
                       AP_(ksT_s[0], 64 * T + 128 * (m - 1 + h0), [[T, 4], [4096, 4], [1, 128 * (2 - h0)]]),
                       reads=[ksT_s[0]], writes=[kws])
            n_it = (32 * m + 30) // 128 + 1
            for g in range(2):
                for it in range(n_it):
                    ps = ps_s[state['s'] % 4]
                    state['s'] += 1
                    delta = 128 * it - 32 * m
                    masked = delta >= -128
                    P.mm(ps[:], kcT_s[:, g, it * 128:(it + 1) * 128], qs[:, g, :], start=True, stop=not masked)
                    if masked:
                        P.mm(ps[:], identb[:], cmask_s[:, (-delta) // 32, :], start=False, stop=True)
                    P.act(Pc[g][:, it, :], ps[:], AF.Exp)
                for r in range(4):
                    po = ps_o[0]
                    for it in range(n_it):
                        P.mm(po[:, 0:321], Pc[g][:, it, r * 128:(r + 1) * 128], vcx_s[:, it, g, :], start=(it == 0), stop=(it == n_it - 1))
                    P.copy('scalar', oc[:, r, :], po[:, 0:321])
                P.ts('vector', den4[:], oc[:, :, 64], 1e-30, ALU.max)
                P.recip(rd4[:], den4[:])
                P.ts('vector', imp[g][:], oc[:, 0, 65:321], rd4[:, 0:1], ALU.mult)
                for r in range(1, 4):
                    P.stt(imp[g][:], oc[:, r, 65:321], rd4[:, r:r + 1], imp[g][:], ALU.mult, ALU.add)
                P.tt('vector', coef4[:], rd4[:], AP_(gs, 12 * g + 0, [[24, 128], [3, 4]]), ALU.mult)
                P.tt('vector', acc[:, 4 * g:4 * g + 4, :], oc[:, :, 0:64], AP_(coef4, 0, [[4, 128], [1, 4], [0, 64]]), ALU.mult)
                P.tt('vector', imp[g][:], imp[g][:], Bsel_s[:, 256 - 8 * m:512 - 8 * m], ALU.add)
                P.ts('vector', imp[g][:, 0:1], imp[g][:, 0:1], 1e4, ALU.add)
                P.op('vector', lambda e, g=g: e.max(out=m8[:, 0:8], in_=imp[g][:]), reads=[imp[g]], writes=[m8])
                P.op('vector', lambda e, g=g: e.match_replace(out=tmpi[:], in_to_replace=m8[:, 0:8], in_values=imp[g][:], imm_value=-3.0e38),
                     reads=[imp[g], m8], writes=[tmpi])
                P.op('vector', lambda e: e.max(out=m8[:, 8:16], in_=tmpi[:]), reads=[tmpi], writes=[m8])
                P.ts('vector', thr[:], m8[:, 15:16], -1e29, ALU.max)
                P.ts('vector', nm[g][:], imp[g][:], thr[:, 0:1], ALU.is_ge)
            if pending:
                pending.pop()()
            tiles = []
            wt = [(r, hf) for hf in range(h0, 2) for r in range(4)]
            for g in range(2):
                for idx, (r, hf) in enumerate(wt):
                    def eS(ps, g=g, r=r, hf=hf):
                        P.mm(ps[:], kws[:, g, (r * 2 + hf) * 128:(r * 2 + hf + 1) * 128], qs[:, g, :], start=True, stop=(hf == 0))
                        if hf == 1:
                            P.mm(ps[:], identb[:], smask_s[:, r, :], start=False, stop=True)

                    def eX(pb, r=r):
                        mask_mul(pb, AP_(wmask_s, r * 128, [[1024, 128], [0, 4], [1, 128]]), [wmask_s])

                    def ePV(pb, g=g, r=r, hf=hf, idx=idx):
                        P.mm(ps_acc[g][0:65, :], vws[:, r * 2 + hf, 65 * g:65 * g + 65], pb[:], start=(idx == 0), stop=(idx == len(wt) - 1))
                    tiles.append((eS, eX if hf == 0 else None, ePV))
            run_stream(tiles)
            for g in range(2):
                for hb in range(2):
                    P.tr(pm_b[:, hb * 128:(hb + 1) * 128], nm[g][:, hb * 128:(hb + 1) * 128], identb[:])
                P.copy('vector', nmT[g][:].rearrange("p a q -> p (a q)"), pm_b[:, 0:256], reads=[pm])
            for g in range(2):
                o_epilogue(ps_acc[g], g, 2, gs)
            nkt = 4 * m + 4
            for g in range(2):
                tiles = []
                for kt in range(nkt):
                    def eS(ps, g=g, kt=kt):
                        diag = kt >= 4 * m
                        col = ((kt % 4) * 32 + kt // 4) * 128
                        P.mm(ps[:], ksT_s[g][:, col:col + 128], qs[:, g, :], start=True, stop=not diag)
                        if diag:
                            P.mm(ps[:], identb[:], smask_s[:, kt - 4 * m, :], start=False, stop=True)

                    def eX(pb, g=g, kt=kt):
                        bank = (ps_o[0], pm)[state['x'] % 2]
                        state['x'] += 1
                        P.mm(bank[:, 0:128], Eall_s[:, kt % 64, :], nmT[g][:, kt // 64, :], start=True, stop=True)
                        mask_mul(pb, AP_(bank, 0, [[512, 128], [0, 4], [1, 128]]), [bank])

                    def ePV(pb, g=g, kt=kt):
                        P.mm(ps_acc[g][0:65, :], vsx_s[:, (kt % 4) * 32 + kt // 4, 65 * g:65 * g + 65], pb[:], start=(kt == 0), stop=(kt == nkt - 1))
                    tiles.append((eS, eX, ePV))
                run_stream(tiles)
                o_epilogue(ps_acc[g], g, 1, gs)
            accf = acc[:].rearrange("p h d -> p (h d)")
            P.act(accb[:], accf, AF.Square, accum_out=ssq[:])
            P.act(rat[:], ssq[:], AF.Sqrt, bias=EPS, scale=1.0 / 512)
            P.recip(rat2[:], rat[:])
            P.dma('gpsimd', rasc[m], rat2[:])
            P.copy('gpsimd', accb[:], accf)
            if G.get('dbg'):
                P.dma('gpsimd', G['dbg']['ra'][m], rat2[:])

            def finish_out(m=m):
                for bk in range(4):
                    P.tr(pm_b[:, bk * 128:(bk + 1) * 128], accb[:, bk * 128:(bk + 1) * 128], identb[:])
                P.copy('vector', atT[:].rearrange("p b q -> p (b q)"), pm_b[:, 0:512], reads=[pm])
                P.dma('gpsimd', atsc[m], atT[:].rearrange("p b q -> p (b q)"))
                if G.get('dbg'):
                    P.dma('gpsimd', G['dbg']['at'][m], atT[:].rearrange("p b q -> p (b q)"))
            pending.append(finish_out)
        pending.pop()()


def emit_C(C, l, xsrc, xdst, W, G):
    P = C.P
    with C.phase("C%d" % l):
        Wo = C.sb("Wo", [128, 8, D], BF16)
        Wdn = C.sb("Wdn", [128, 32, D], BF16)
        wst = [C.sb("wst%d" % i, [128, 8, 256], F32) for i in range(2)]
        Wup = [C.sb("Wup%d" % i, [128, 8, 256], BF16) for i in range(2)]
        gout_s = C.sb("gout_s", [128, 8], F32)
        gffn_s = C.sb("gffn_s", [128, D], F32)
        cw_s = C.sb("cw_s", [128, 4, 3], F32)
        selw_s = C.sb("selw_s", [128, 4], F32)
        xt = [C.sb("xt%d" % i, [128, D], F32) for i in range(2)]
        x1 = [C.sb("x1_%d" % i, [128, D], F32) for i in range(4)]
        at_s = [C.sb("at_s%d" % i, [128, 4, 128], BF16) for i in range(2)]
        yv = [C.sb("yv%d" % i, [128, 4, 128], F32) for i in range(2)]
        bg2 = [C.sb("bg2_%d" % i, [128, 4, 2], F32) for i in range(2)]
        tl = [C.sb("tl%d" % i, [128, 4, 8], F32) for i in range(2)]
        halo = C.sb("halo", [128, 4, 2], F32)
        pt0 = C.sb("pt0", [128, 4], F32)
        pt1 = C.sb("pt1", [128, 4], F32)
        ysq = [C.sb("ysq%d" % i, [128, 4, 128], F32) for i in range(2)]
        ybf = [C.sb("ybf%d" % i, [128, 4, 128], BF16) for i in range(2)]
        rc = C.sb("rc", [128, 1], F32)
        rc2 = [C.sb("rc2_%d" % i, [128, 1], F32) for i in range(2)]
        ra_s = [C.sb("ra_s%d" % i, [128, 1], F32) for i in range(2)]
        ss = C.sb("ss", [128, 1], F32)
        rstd = C.sb("rstd", [128, 1], F32)
        h2n = [C.sb("h2n%d" % i, [128, D], BF16) for i in range(2)]
        h2T = C.sb("h2T", [128, 8, 512], BF16)
        rl = [C.sb("rl%d" % i, [128, 512], F32) for i in range(2)]
        actT = C.sb("actT", [128, 32, 512], BF16)
        pa = [C.ps("pa%d" % i, [128, 512], F32) for i in range(2)]
        pcv = [C.ps("pcv%d" % i, [128, 512], F32) for i in range(2)]
        pu = [C.ps("pu%d" % i, [128, 512], F32) for i in range(2)]
        pT = C.ps("pT", [128, 8, 128], BF16)
        px = C.ps("px", [128, 512], F32)
        identb, onesf = G['identb'], G['onesf']
        gathu, ysc, bg2sc, atsc, rasc = G['gathu'], G['ysc'], G['bg2sc'], G['atsc'], G['rasc']

        with P.group('cst'):
            P.dma('sync', gout_s[:], W['gout'][l])
            P.dma('sync', gffn_s[:], W['gffn'][l])
            P.dma('sync', cw_s[:], W['cw'][l])
            P.dma('sync', selw_s[:], W['selw'])
        n = 0
        for kc in range(8):
            for hf in range(2):
                st = wst[n % 2]
                n += 1
                stw = AP_(st, 0, [[2048, 128], [1, 512]])
                P.dma('sync' if hf == 0 else 'scalar', stw, W['w_o'][l, kc * 128:(kc + 1) * 128, hf * 512:(hf + 1) * 512], writes=[st])
                P.ts('vector' if hf == 0 else 'gpsimd', Wo[:, kc, hf * 512:(hf + 1) * 512], stw, gout_s[:, kc:kc + 1], ALU.mult, reads=[st, gout_s])
        for f2 in range(16):
            st = wst[n % 2]
            n += 1
            stv = AP_(st, 0, [[2048, 128], [1024, 2], [1, 1024]])
            P.dma('sync' if f2 % 2 == 0 else 'scalar', stv, W['w_dn'][l, f2 * 256:(f2 + 1) * 256, :].rearrange("(a p) n -> p a n", p=128), writes=[st])
            P.copy('vector' if f2 % 2 == 0 else 'gpsimd', Wdn[:, f2 * 2:(f2 + 1) * 2, :], stv, reads=[st])
        up_n = [n]

        def S1a(m):
            xb, ab, rab, yb, bgb, tlb = xt[m % 2], at_s[m % 2], ra_s[m % 2], yv[m % 2], bg2[m % 2], tl[m % 2]
            P.dma('sync', xb[:], xsrc[m])
            P.dma('scalar', ab[:], atsc[m].rearrange("p (b q) -> p b q", b=4))
            P.dma('sync', rab[:], rasc[m])
            P.dma('scalar', yb[:], ysc[m].rearrange("p (c t) -> p c t", c=4))
            P.dma('sync', bgb[:], bg2sc[m].rearrange("p (c e) -> p c e", e=2))
            if m == 0:
                P.memset('gpsimd', tlb[:, 0, :], 0.0)
            with P.group("tl%d" % (m % 2)):
                if m > 0:
                    P.dma('gpsimd', tlb[:, 0, :], gathu[3 * B_NM + m - 1].rearrange("(p e) -> p e", e=8), writes=[tlb])
                for cd in range(1, 4):
                    P.dma('gpsimd', tlb[:, cd, :], gathu[(cd - 1) * B_NM + m].rearrange("(p e) -> p e", e=8), writes=[tlb])
            hf = halo[:].rearrange("p c e -> p (c e)")
            P.ts('vector', hf, tlb[:, 0, :], selw_s[:, 0:1], ALU.mult)
            for cd in range(1, 4):
                P.stt(hf, tlb[:, cd, :], selw_s[:, cd:cd + 1], hf, ALU.mult, ALU.add)
            P.tt('vector', pt0[:], halo[:, :, 1], cw_s[:, :, 1], ALU.mult)
            P.tt('vector', pt1[:], halo[:, :, 0], cw_s[:, :, 0], ALU.mult)
            P.tt('vector', pt0[:], pt0[:], pt1[:], ALU.add)
            P.tt('vector', pt0[:], pt0[:], bgb[:, :, 0], ALU.mult)
            P.tt('vector', yb[:, :, 0], yb[:, :, 0], pt0[:], ALU.add)
            P.tt('vector', pt1[:], halo[:, :, 1], cw_s[:, :, 0], ALU.mult)
            P.tt('vector', pt1[:], pt1[:], bgb[:, :, 1], ALU.mult)
            P.tt('vector', yb[:, :, 1], yb[:, :, 1], pt1[:], ALU.add)
            P.act(ysq[m % 2][:], yb[:], AF.Square)
            P.copy('gpsimd', ybf[m % 2][:], yb[:])

        def S1b(m, j):
            xb, ab, rab = xt[m % 2], at_s[m % 2], ra_s[m % 2]
            for ch in range(4):
                P.mm(px[:, 0:1], ysq[m % 2][:, ch, :], onesf[:], start=(ch == 0), stop=(ch == 3))
            P.act(rc[:], px[:, 0:1], AF.Sqrt, bias=EPS, scale=1.0 / 512)
            P.recip(rc2[m % 2][:], rc[:])
            for nh in range(2):
                for kc in range(4):
                    P.mm(pa[nh][:], ab[:, kc, :], Wo[:, kc, nh * 512:(nh + 1) * 512], start=(kc == 0), stop=(kc == 3))
                for kc in range(4):
                    P.mm(pcv[nh][:], ybf[m % 2][:, kc, :], Wo[:, 4 + kc, nh * 512:(nh + 1) * 512], start=(kc == 0), stop=(kc == 3))
            x1b = x1[j]
            for nh in range(2):
                sl = slice(nh * 512, (nh + 1) * 512)
                P.stt(x1b[:, sl], pa[nh][:], rab[:, 0:1], xb[:, sl], ALU.mult, ALU.add)
                P.stt(x1b[:, sl], pcv[nh][:], rc2[m % 2][:, 0:1], x1b[:, sl], ALU.mult, ALU.add)
            if G.get('dbg'):
                P.dma('gpsimd', G['dbg']['x1'][m], x1b[:])
                P.dma('gpsimd', G['dbg']['y'][m], ybf[m % 2][:].rearrange("p c t -> p (c t)"))
                P.dma('gpsimd', G['dbg']['rc'][m], rc2[m % 2][:])
            P.act(h2n[m % 2][:], x1b[:], AF.Square, accum_out=ss[:])
            P.act(rstd[:], ss[:], AF.Sqrt, bias=EPS, scale=1.0 / D)
            P.recip(rstd[:], rstd[:])
            P.stt(h2n[m % 2][:], x1b[:], rstd[:, 0:1], gffn_s[:], ALU.mult, ALU.mult)

        def S1c(m, j):
            for kc in range(8):
                P.tr(pT[:, kc, :], h2n[m % 2][:, kc * 128:(kc + 1) * 128], identb[:])
            P.copy('scalar', h2T[:, :, j * 128:(j + 1) * 128], pT[:])

        NBT = B_NM // 4
        for j in range(4):
            S1a(j)
            S1b(j, j)
            S1c(j, j)
        for bt in range(NBT):
            for u in range(16):
                st = wst[up_n[0] % 2]
                wb = Wup[up_n[0] % 2]
                up_n[0] += 1
                P.dma('sync' if u % 2 == 0 else 'scalar', st[:], W['w_up'][l, :, u * 256:(u + 1) * 256].rearrange("(kc p) n -> p kc n", p=128))
                P.copy('gpsimd' if u % 2 == 0 else 'vector', wb[:], st[:])
                for fl in range(2):
                    f = u * 2 + fl
                    pp = pu[f % 2]
                    for kc in range(8):
                        P.mm(pp[:], wb[:, kc, fl * 128:(fl + 1) * 128], h2T[:, kc, :], start=(kc == 0), stop=(kc == 7))
                    rb = rl[f % 2]
                    P.act(rb[:], pp[:], AF.Relu)
                    P.tt('gpsimd' if f % 2 == 0 else 'vector', actT[:, f, :], rb[:], rb[:], ALU.mult)
            nxt = bt + 1 < NBT
            for j in range(4):
                m = bt * 4 + j
                mn = m + 4
                if nxt:
                    S1a(mn)
                ob = x1[j]
                for nh in range(2):
                    pp = pu[nh]
                    for f in range(32):
                        P.mm(pp[:], actT[:, f, j * 128:(j + 1) * 128], Wdn[:, f, nh * 512:(nh + 1) * 512], start=(f == 0), stop=(f == 31))
                    P.tt('vector', ob[:, nh * 512:(nh + 1) * 512], pp[:], x1[j][:, nh * 512:(nh + 1) * 512], ALU.add)
                P.dma('sync', xdst[m], ob[:])
                if nxt:
                    S1b(mn, j)
                    if j >= 1:
                        S1c(mn - 1, j - 1)
            if nxt:
                S1c(bt * 4 + 7, 3)


def build_fused(nlayers=DEPTH, stages="AGZBC", dbg=False):
    C = Ctx()
    P = C.P
    W = {}
    W['x0'] = C.din("x0", [B_NM, 128, D], F32)
    W['w_in'] = C.din("w_in", [nlayers, D, INW], F32)
    W['gmix'] = C.din("gmix", [nlayers, 128, 8], F32)
    W['gq'] = C.din("gq", [nlayers, 128, 512], F32)
    W['gk3'] = C.din("gk3", [nlayers, 128, 3, 128], F32)
    W['cw'] = C.din("cw", [nlayers, 128, 4, 3], F32)
    W['peT'] = C.din("peT", [nlayers, 2, 64, 32], F32)
    W['w1'] = C.din("w1", [nlayers, 2, 2048, 256], F32)
    W['b1b'] = C.din("b1b", [nlayers, 2, 128, 256], F32)
    W['w2'] = C.din("w2", [nlayers, 2, 256, 64], F32)
    W['b2b'] = C.din("b2b", [nlayers, 128, 2, 64], F32)
    W['w_o'] = C.din("w_o", [nlayers, D, D], F32)
    W['gout'] = C.din("gout", [nlayers, 128, 8], F32)
    W['gffn'] = C.din("gffn", [nlayers, 128, D], F32)
    W['w_up'] = C.din("w_up", [nlayers, D, DFF], F32)
    W['w_dn'] = C.din("w_dn", [nlayers, DFF, D], F32)
    W['smask'] = C.din("smask", [128, 4, 512], BF16)
    W['cmask'] = C.din("cmask", [128, 5, 512], BF16)
    W['wmask'] = C.din("wmask", [128, 8, 128], BF16)
    W['Eall'] = C.din("Eall", [128, 64, 128], BF16)
    W['Bsel'] = C.din("Bsel", [128, 512], F32)
    W['kaug'] = C.din("kaug", [4, T], BF16)
    W['kcaug'] = C.din("kcaug", [4, 2, 1024], BF16)
    W['qaug'] = C.din("qaug", [B_NM, 4, 2, 512], BF16)
    W['vcxc'] = C.din("vcxc", [128, 8, 2, 257], BF16)
    W['selw'] = C.din("selw", [128, 4], F32)
    identb_d = C.din("identb", [128, 128], BF16)
    identf_d = C.din("identf", [128, 128], F32)
    onesf_d = C.din("onesf", [128, 1], F32)
    xo = C.dout("xo", [B_NM, 128, D], F32)

    G = {}
    COMPS = [('ks', 1024), ('kw', 1024), ('kc', 1024), ('vc', 1024), ('vs0', 520), ('vs1', 520), ('vw0', 520), ('vw1', 520)]
    G['pay'] = {cn: C.dint("pay_" + cn, [rows, 512], BF16) for cn, rows in COMPS}
    G['gath'] = {cn: C.dint("gath_" + cn, [4 * rows, 512], BF16) for cn, rows in COMPS}
    G['payu'] = C.dint("payu", [B_NM, 1024], F32).ap()
    G['gathu'] = C.dint("gathu", [4 * B_NM, 1024], F32).ap()
    xbuf = C.dint("xbuf", [B_NM, 128, D], F32).ap()
    G['qsc'] = C.dint("qsc", [B_NM, 2, 64, 512], BF16)
    G['gsc'] = C.dint("gsc", [B_NM, 128, 24], F32).ap()
    G['ysc'] = C.dint("ysc", [B_NM, 128, 512], F32).ap()
    G['bg2sc'] = C.dint("bg2sc", [B_NM, 128, 8], F32).ap()
    G['atsc'] = C.dint("atsc", [B_NM, 128, 512], BF16).ap()
    G['rasc'] = C.dint("rasc", [B_NM, 128, 1], F32).ap()
    qsc_ap = G['qsc'].ap()
    if dbg:
        G['dbg'] = {'at': C.dout("dbg_at", [B_NM, 128, 512], BF16), 'ra': C.dout("dbg_ra", [B_NM, 128, 1], F32),
                    'x1': C.dout("dbg_x1", [B_NM, 128, D], F32), 'y': C.dout("dbg_y", [B_NM, 128, 512], BF16),
                    'rc': C.dout("dbg_rc", [B_NM, 128, 1], F32),
                    'kc': C.dout("dbg_kc", [68, 2, 1024], BF16), 'vc': C.dout("dbg_vc", [128, 8, 2, 321], BF16),
                    'ks': C.dout("dbg_ks", [68, T], BF16), 'vs': C.dout("dbg_vs", [128, 128, 130], BF16)}

    G['identb'] = C.sb("identb_s", [128, 128], BF16)
    G['identf'] = C.sb("identf_s", [128, 128], F32)
    G['onesf'] = C.sb("onesf_s", [128, 1], F32)
    with P.group('cst'):
        P.dma('sync', G['identb'][:], identb_d)
        P.dma('sync', G['identf'][:], identf_d)
        P.dma('sync', G['onesf'][:], onesf_d)
    P.barrier()
    rg = [[0, 1, 2, 3], [4, 5, 6, 7]]
    def mk_cc(i_ap, o_ap):
        return lambda e: e.collective_compute("AllGather", ALU.bypass, replica_groups=rg, ins=[i_ap.opt()], outs=[o_ap.opt()])
    cc_fns = [mk_cc(G['pay'][cn].ap(), G['gath'][cn].ap()) for cn, _ in COMPS] + [mk_cc(G['payu'], G['gathu'])]
    for l in range(nlayers):
        xsrc = W['x0'] if l == 0 else xbuf
        xdst = xo if l == nlayers - 1 else xbuf
        if 'A' in stages:
            emit_A(C, l, xsrc, W, G)
        if 'G' in stages:
            P.collectives(cc_fns)
        with C.phase("ZB%d" % l):
            G['kcT_s'] = C.sb("kcT_s", [68, 2, 1024], BF16)
            G['vcx_s'] = C.sb("vcx_s", [128, 8, 2, 321], BF16)
            with P.group('cst2'):
                P.dma('sync', G['kcT_s'][64:68, :, :], W['kcaug'])
                P.dma('scalar', G['vcx_s'][:, :, :, 64:321], W['vcxc'])
            if 'Z' in stages:
                emit_Z(C, l, W, G)
            Gb = dict(G)
            Gb['qsc'] = qsc_ap
            if 'B' in stages:
                emit_B(C, l, W, Gb)
        if 'C' in stages:
            emit_C(C, l, xsrc, xdst, W, G)
    if 'C' not in stages:
        with C.phase("dbg"):
            xt = [C.sb("xt%d" % i, [128, D], F32) for i in range(2)]
            for m in range(B_NM):
                P.dma('sync', xt[m % 2][:], W['x0'][m])
                P.dma('sync', xo[m], xt[m % 2][:])
    return C.finish()


IDENTB = np.eye(128, dtype=np.float32).astype(NPBF)
ONESF = np.ones((128, 1), np.float32)


def _bf(a):
    return np.ascontiguousarray(a).astype(NPBF)


def _selmap():
    i = np.arange(1024)[:, None] * 16
    j = np.arange(256)[None, :] * 64
    sh = np.clip(np.minimum(i + 32, j + 64) - np.maximum(i, j), 0, None).astype(np.float32) / 32.0
    sh[1023] = 0.0
    return sh


def _pos_rows(pos):
    pos = np.maximum(pos, 0)
    return np.stack([np.ones_like(pos), np.ones_like(pos), pos % 128, pos // 128]).astype(np.float32)


def core_consts(s):
    ki = np.arange(128)[:, None]
    qi = np.arange(128)[None, :]
    tri_gt = np.where(ki > qi, NEGM, 0.0).astype(np.float32)
    tri_le = np.where(ki <= qi, NEGM, 0.0).astype(np.float32)
    full = np.full((128, 128), NEGM, np.float32)
    zero = np.zeros((128, 128), np.float32)
    smask = np.tile(np.stack([zero if d < s else (tri_gt if d == s else full) for d in range(4)], axis=1), (1, 1, 4))
    cm = []
    for v in range(5):
        ip = ki - 32 * v
        vis = (16 * ip + 31) <= (128 * s + qi)
        cm.append(np.where(vis, 0.0, NEGM).astype(np.float32))
    cmask = np.tile(np.stack(cm, axis=1), (1, 1, 4))
    wm = []
    for d in range(8):
        off = s + 4 - d
        if off == 0:
            wm.append((ki <= qi).astype(np.float32))
        elif off == 4:
            wm.append((ki > qi).astype(np.float32))
        elif 0 < off < 4:
            wm.append(np.ones((128, 128), np.float32))
        else:
            wm.append(zero)
    wmask = np.stack(wm, axis=1)
    E = np.zeros((128, 64, 128), np.float32)
    for kt in range(64):
        for k in range(128):
            E[2 * kt + k // 64, kt, k] = 1.0
    r = np.arange(512)[None, :] - 256 - 2 * s
    qq = np.arange(128)[:, None]
    Brel = np.zeros((128, 512), np.float32)
    Brel = np.where(r >= 2, np.float32(-1e30), Brel)
    Brel = np.where(r == 1, np.where(qq >= 64, np.float32(1e4), np.float32(-1e30)), Brel)
    Brel = np.where(r == 0, np.float32(1e4), Brel)
    Brel = np.where((r == -1) & (qq < 64), np.float32(1e4), Brel)
    rr, mm, kk = np.meshgrid(np.arange(4), np.arange(32), np.arange(128), indexing='ij')
    kpos = (128 * (4 * mm + rr) + kk).reshape(-1)
    kaug = _pos_rows(kpos)
    kc = _pos_rows(np.arange(1024) * 16 + 31)
    kcaug = np.stack([kc, kc], axis=1)
    qaug = np.zeros((B_NM, 4, 2, 512), np.float32)
    qv = np.arange(128)
    for m in range(B_NM):
        c = 4 * m + s
        for g in range(2):
            for rh in range(4):
                sl = 2.0 ** (-(4 * g + rh + 1))
                cs = slice(rh * 128, (rh + 1) * 128)
                qaug[m, 0, g, cs] = -sl * qv
                qaug[m, 1, g, cs] = -sl * 128.0 * c
                qaug[m, 2, g, cs] = sl
                qaug[m, 3, g, cs] = sl * 128.0
    sm = _selmap().reshape(8, 128, 256).transpose(1, 0, 2)
    vcxc = np.zeros((128, 8, 2, 257), np.float32)
    vcxc[:, :, :, 0] = 1.0
    vcxc[127, 7, :, 0] = 0.0
    vcxc[:, :, :, 1:257] = sm[:, :, None, :]
    selw = np.zeros((128, 4), np.float32)
    selw[:, s] = 1.0
    return {'smask': _bf(smask), 'cmask': _bf(cmask), 'wmask': _bf(wmask), 'Eall': _bf(E),
            'Bsel': np.ascontiguousarray(Brel.astype(np.float32)), 'kaug': _bf(kaug), 'kcaug': _bf(kcaug),
            'qaug': _bf(qaug), 'vcxc': _bf(vcxc), 'selw': selw,
            'identb': IDENTB, 'identf': np.eye(128, dtype=np.float32), 'onesf': ONESF}


def prep_fused(p, L=DEPTH):
    p = {k: (v if k == 'x' else v[:L]) for k, v in p.items()}
    common = {
        'w_in': np.ascontiguousarray(p['w_in'], dtype=np.float32),
        'gmix': np.ascontiguousarray(p['g_mix_norm'].reshape(L, 8, 128).transpose(0, 2, 1)),
        'gq': np.ascontiguousarray(np.tile(p['g_q'][:, None, :], (1, 128, 8))),
        'gk3': np.ascontiguousarray(np.broadcast_to(np.tile(p['g_k'], (1, 1, 2))[:, None], (L, 128, 3, 128))),
        'cw': np.ascontiguousarray(p['conv_w'].reshape(L, 3, 4, 128).transpose(0, 3, 2, 1)),
        'peT': np.ascontiguousarray(p['pe_cmp'].transpose(0, 1, 3, 2)),
        'w1': np.ascontiguousarray(p['w_cmp1'], dtype=np.float32),
        'b1b': np.ascontiguousarray(np.broadcast_to(p['b_cmp1'][:, :, None, :], (L, 2, 128, 256))),
        'w2': np.ascontiguousarray(p['w_cmp2'], dtype=np.float32),
        'b2b': np.ascontiguousarray(np.broadcast_to(p['b_cmp2'][:, None], (L, 128, 2, 64))),
        'w_o': np.ascontiguousarray(p['w_o'], dtype=np.float32),
        'gout': np.ascontiguousarray(p['g_out'].reshape(L, 8, 128).transpose(0, 2, 1)),
        'gffn': np.ascontiguousarray(np.tile(p['g_ffn_norm'][:, None, :], (1, 128, 1))),
        'w_up': np.ascontiguousarray(p['w_up'], dtype=np.float32),
        'w_dn': np.ascontiguousarray(p['w_down'], dtype=np.float32),
    }
    maps = []
    x = np.ascontiguousarray(p['x'], dtype=np.float32)
    for b in range(NB):
        xv = x[b].reshape(T // 128, 128, D)
        for s in range(4):
            mp = dict(common)
            mp.update(core_consts(s))
            mp['x0'] = np.ascontiguousarray(xv[np.arange(B_NM) * 4 + s])
            maps.append(mp)
    return maps


_PROG = {}


def run_fused(inputs, nlayers=DEPTH):
    p = {k: np.asarray(v) for k, v in inputs.items()}
    if nlayers not in _PROG:
        _PROG[nlayers] = build_fused(nlayers)
    res = run_bass_kernel_spmd(_PROG[nlayers], prep_fused(p, nlayers), core_ids=list(range(NCORES)))
    out = np.empty((NB, T, D), np.float32)
    for b in range(NB):
        ov = out[b].reshape(T // 128, 128, D)
        for s in range(4):
            ov[np.arange(B_NM) * 4 + s] = np.asarray(res.results[4 * b + s]['xo'])
    return out


def kernel(**inputs):
    return run_fused(inputs, DEPTH)
```

```python
import contextlib
import numpy as np
import ml_dtypes
import concourse.bass as bass
import concourse.mybir as mybir
from concourse.bass_utils import run_bass_kernel_spmd

F32 = mybir.dt.float32
BF16 = mybir.dt.bfloat16
AF = mybir.ActivationFunctionType
ALU = mybir.AluOpType
AX = mybir.AxisListType
NPBF = ml_dtypes.bfloat16

ENGS = ['tensor', 'vector', 'scalar', 'gpsimd', 'sync']
NCORES = 8
D = 1024
T = 16384
NB = 2
DEPTH = 4
INW = 2840
EPS = 1e-6
NEGM = -30000.0


def key_of(ap):
    t = getattr(ap, 'tensor', ap)
    return t.name.split('@')[0]


class Prog:
    def __init__(self, nc):
        self.nc = nc
        self.q = {e: [] for e in ENGS}
        self.semcount = {}
        self.res = {}
        self.seen = {e: {} for e in ENGS}

    grp = None

    @contextlib.contextmanager
    def group(self, sem):
        s = 'd_' + sem
        self.grp = {'sem': sem, 's': s, 'start': self.semcount.get(s, 0), 'keys': set(), 'pre': {}}
        try:
            yield
        finally:
            g, self.grp = self.grp, None
            v = self.semcount.get(s, 0)
            for k in g['keys']:
                self.res[k]['w'] = (s, v)

    def _need(self, eng, reads, writes):
        need = {}

        def add(tok):
            if tok is None:
                return
            s, v = tok
            if eng == 'tensor' and s == 'e_tensor':
                return
            if self.grp is not None and s == self.grp['s'] and v > self.grp['start']:
                return
            if need.get(s, 0) < v:
                need[s] = v
        for k in reads:
            st = self.res.get(k)
            if st:
                add(st['w'])
        for k in writes:
            sts = [self.res.get(k)]
            if self.grp is not None:
                if k not in self.grp['pre']:
                    st0 = self.res.get(k)
                    self.grp['pre'][k] = {'w': st0['w'], 'r': dict(st0['r'])} if st0 else None
                sts.append(self.grp['pre'][k])
            for st in sts:
                if st:
                    add(st['w'])
                    for s, v in st['r'].items():
                        add((s, v))
        for s, v in need.items():
            if self.seen[eng].get(s, 0) < v:
                self.q[eng].append(('wait', s, v))
                self.seen[eng][s] = v

    def _update(self, reads, writes, tok):
        s, v = tok
        for k in reads:
            st = self.res.setdefault(k, {'w': None, 'r': {}})
            if st['r'].get(s, 0) < v:
                st['r'][s] = v
        for k in writes:
            self.res[k] = {'w': tok, 'r': {}}

    def op(self, eng, fn, reads=(), writes=()):
        reads = [r if isinstance(r, str) else key_of(r) for r in reads]
        writes = [r if isinstance(r, str) else key_of(r) for r in writes]
        self._need(eng, reads, writes)
        s = 'e_' + eng
        v = self.semcount.get(s, 0) + 1
        self.semcount[s] = v
        self.q[eng].append(('op', fn, s, 1))
        self._update(reads, writes, (s, v))

    def dma(self, eng, out, in_, reads=None, writes=None, sem=None, **kw):
        if reads is None:
            reads = [] if in_.tensor.name in self.dram_names else [in_]
        if writes is None:
            writes = [] if out.tensor.name in self.dram_names else [out]
        assert reads or writes or sem
        reads = [r if isinstance(r, str) else key_of(r) for r in reads]
        writes = [r if isinstance(r, str) else key_of(r) for r in writes]
        if self.grp is not None:
            sem = self.grp['sem']
            self.grp['keys'].update(writes)
        if sem is None:
            sem = writes[0] if writes else 'st_' + reads[0]
        self._need(eng, reads, writes)
        s = 'd_' + sem
        v = self.semcount.get(s, 0) + 16
        self.semcount[s] = v
        self.q[eng].append(('op', lambda e: e.dma_start(out=out, in_=in_, **kw), s, 16))
        self._update(reads, writes, (s, v))

    dram_names = set()

    def mm(self, out, lhsT, rhs, start=True, stop=True, reads=None, writes=None):
        self.op('tensor', lambda e: e.matmul(out, lhsT=lhsT, rhs=rhs, start=start, stop=stop),
                reads=reads if reads is not None else [lhsT, rhs],
                writes=writes if writes is not None else [out])

    def tr(self, out, in_, ident, reads=None, writes=None):
        self.op('tensor', lambda e: e.transpose(out, in_, ident),
                reads=reads if reads is not None else [in_, ident],
                writes=writes if writes is not None else [out])

    def act(self, out, in_, func, bias=None, scale=None, accum_out=None, reads=None, writes=None, eng='scalar'):
        kw = {}
        rd = [in_]
        if bias is not None:
            kw['bias'] = bias
            if not isinstance(bias, (int, float)):
                rd.append(bias)
        if scale is not None:
            kw['scale'] = scale
            if not isinstance(scale, (int, float)):
                rd.append(scale)
        wr = [out]
        if accum_out is not None:
            kw['accum_out'] = accum_out
            wr.append(accum_out)
        self.op('scalar', lambda e: e.activation(out=out, in_=in_, func=func, **kw),
                reads=reads if reads is not None else rd,
                writes=writes if writes is not None else wr)

    def tt(self, eng, out, in0, in1, op, reads=None, writes=None):
        self.op(eng, lambda e: e.tensor_tensor(out=out, in0=in0, in1=in1, op=op),
                reads=reads if reads is not None else [in0, in1],
                writes=writes if writes is not None else [out])

    def ts(self, eng, out, in0, s1, op0, s2=None, op1=None, reads=None, writes=None):
        rd = [in0]
        if not isinstance(s1, (int, float)):
            rd.append(s1)
        if s2 is not None and not isinstance(s2, (int, float)):
            rd.append(s2)
        if op1 is None:
            fn = lambda e: e.tensor_scalar(out=out, in0=in0, scalar1=s1, scalar2=None, op0=op0)
        else:
            fn = lambda e: e.tensor_scalar(out=out, in0=in0, scalar1=s1, scalar2=s2, op0=op0, op1=op1)
        self.op(eng, fn, reads=reads if reads is not None else rd,
                writes=writes if writes is not None else [out])

    def stt(self, out, in0, scalar, in1, op0, op1, reads=None, writes=None):
        rd = [in0, in1]
        if not isinstance(scalar, (int, float)):
            rd.append(scalar)
        self.op('vector', lambda e: e.scalar_tensor_tensor(out=out, in0=in0, scalar=scalar, in1=in1, op0=op0, op1=op1),
                reads=reads if reads is not None else rd,
                writes=writes if writes is not None else [out])

    def copy(self, eng, out, in_, reads=None, writes=None):
        if eng == 'scalar':
            fn = lambda e: e.activation(out=out, in_=in_, func=AF.Copy)
        else:
            fn = lambda e: e.tensor_copy(out=out, in_=in_)
        self.op(eng, fn, reads=reads if reads is not None else [in_],
                writes=writes if writes is not None else [out])

    def recip(self, out, in_):
        self.op('vector', lambda e: e.reciprocal(out=out, in_=in_), reads=[in_], writes=[out])

    def reduce(self, out, in_, op=ALU.add, axis=AX.X):
        self.op('vector', lambda e: e.tensor_reduce(out=out, in_=in_, axis=axis, op=op), reads=[in_], writes=[out])

    def memset(self, eng, ap, val):
        self.op(eng, lambda e: e.memset(ap, val), writes=[ap])

    def barrier(self):
        for e in ENGS:
            for s, v in self.semcount.items():
                if self.seen[e].get(s, 0) < v:
                    self.q[e].append(('wait', s, v))
                    self.seen[e][s] = v
        self.res = {}

    def collectives(self, fns):
        self.barrier()
        s = 'c_cc'
        for fn in fns:
            v = self.semcount.get(s, 0) + 1
            self.semcount[s] = v
            self.q['gpsimd'].append(('op', fn, s, 1))
        self.barrier()

    def emit(self):
        nc = self.nc
        for s, v in self.semcount.items():
            if self.seen['sync'].get(s, 0) < v:
                self.q['sync'].append(('wait', s, v))
        with contextlib.ExitStack() as es:
            semh = {s: es.enter_context(nc.semaphore(s)) for s in self.semcount}
            block = es.enter_context(nc.Block())

            def mk(engname):
                def body(e):
                    for it in self.q[engname]:
                        if it[0] == 'wait':
                            e.wait_ge(semh[it[1]], it[2])
                        else:
                            ins = it[1](e)
                            ins.then_inc(semh[it[2]], it[3])
                return body
            for engname in ENGS:
                if self.q[engname]:
                    getattr(block, engname)(mk(engname))


def AP_(t, offset, dims):
    return bass.AP(t, offset, [list(d) for d in dims])


class Ctx:
    def __init__(self):
        self.nc = bass.Bass("TRN2", target_bir_lowering=False)
        self.es = contextlib.ExitStack()
        self.P = Prog(self.nc)
        self.P.dram_names = set()

    def din(self, name, shape, dt):
        self.P.dram_names.add(name)
        return self.nc.dram_tensor(name, list(shape), dt, kind="ExternalInput").ap()

    def dout(self, name, shape, dt):
        self.P.dram_names.add(name)
        return self.nc.dram_tensor(name, list(shape), dt, kind="ExternalOutput").ap()

    tag = None

    def dint(self, name, shape, dt):
        self.P.dram_names.add(name)
        return self.nc.dram_tensor(name, list(shape), dt)

    def sb(self, name, shape, dt):
        if self.tag:
            name = name + '@' + self.tag
        return self.es.enter_context(self.nc.sbuf_tensor(name, list(shape), dt))

    def ps(self, name, shape, dt):
        if self.tag:
            name = name + '@' + self.tag
        return self.es.enter_context(self.nc.psum_tensor(name, list(shape), dt))

    @contextlib.contextmanager
    def phase(self, tag):
        old, oldtag = self.es, self.tag
        self.es, self.tag = contextlib.ExitStack(), tag
        try:
            yield
        finally:
            self.P.barrier()
            self.es.close()
            self.es, self.tag = old, oldtag

    def finish(self):
        self.P.emit()
        self.es.close()
        return self.nc


B_NM = 32
DFF = 4096
PAYR = 6176
KS_R0, KW_R0, KC_R0, VC_R0, VS_R0, VW_R0 = 0, 1024, 2048, 3072, 4096, 5136
TP = T + 128


def emit_A(C, l, xsrc, W, G):
    P = C.P
    with C.phase("A%d" % l):
        Wbf = C.sb("Wbf", [128, 8, INW], BF16)
        wst = [C.sb("wst%d" % i, [128, INW // 2], F32) for i in range(2)]
        gmix_s = C.sb("gmix_s", [128, 8], F32)
        gq_s = C.sb("gq_s", [128, 512], F32)
        gk3_s = C.sb("gk3_s", [128, 3, 128], F32)
        cw_s = C.sb("cw_s", [128, 4, 3], F32)
        xt = [C.sb("xt%d" % i, [128, D], F32) for i in range(3)]
        junk = C.sb("junk", [128, D], BF16)
        ss = [C.sb("ss%d" % i, [128, 1], F32) for i in range(2)]
        rstd = [C.sb("rstd%d" % i, [128, 1], F32) for i in range(2)]
        hn = [C.sb("hn%d" % i, [128, D], BF16) for i in range(2)]
        hT = [C.sb("hT%d" % i, [128, 8, 128], BF16) for i in range(2)]
        zs = [C.sb("zs%d" % i, [128, 1304], F32) for i in range(2)]
        hcs = C.sb("hcs", [128, 4, 128], F32)
        ub = [C.sb("ub%d" % i, [128, 4, 130], F32) for i in range(2)]
        bgs = [C.sb("bgs%d" % i, [128, 4, 128], F32) for i in range(2)]
        sq = C.sb("sq", [128, 1280], F32)
        ssg = C.sb("ssg", [128, 20], F32)
        rg = C.sb("rg", [128, 20], F32)
        qtmp = C.sb("qtmp", [128, 512], F32)
        ktmp = C.sb("ktmp", [128, 256], F32)
        nrm = C.sb("nrm", [128, 1024], BF16)
        trs = C.sb("trs", [128, 8, 128], BF16)
        vsw = C.sb("vsw", [128, 2, 130], BF16)
        gts = C.sb("gts", [128, 24], F32)
        cv0 = C.sb("cv0", [128, 4, 128], F32)
        cv1 = C.sb("cv1", [128, 4, 128], F32)
        yv = C.sb("yv", [128, 4, 128], F32)
        ut = C.sb("ut", [128, 4, 2], F32)
        bg2 = C.sb("bg2", [128, 4, 2], F32)
        pTa = C.ps("pTa", [128, 8, 128], BF16)
        pTb = C.ps("pTb", [128, 8, 128], BF16)
        pz = [C.ps("pz%d" % i, [128, 512], F32) for i in range(3)]
        pc = [C.ps("pc%d" % i, [128, 4, 128], F32) for i in range(3)]
        identb = G['identb']

        with P.group('cst'):
            P.dma('sync', gmix_s[:], W['gmix'][l])
            P.dma('sync', gq_s[:], W['gq'][l])
            P.dma('sync', gk3_s[:], W['gk3'][l])
            P.dma('sync', cw_s[:], W['cw'][l])
        P.memset('gpsimd', vsw[:], 1.0)
        P.memset('gpsimd', ub[0][:], 0.0)
        P.memset('gpsimd', ub[1][:], 0.0)
        for kc in range(8):
            for hf in range(2):
                st = wst[hf]
                P.dma('sync' if hf == 0 else 'scalar', st[:], W['w_in'][l, kc * 128:(kc + 1) * 128, hf * 1420:(hf + 1) * 1420])
                P.ts('vector' if hf == 0 else 'gpsimd', Wbf[:, kc, hf * 1420:(hf + 1) * 1420], st[:], gmix_s[:, kc:kc + 1], ALU.mult)

        pay, payu, qsc, gsc, ysc, bg2sc = G['pay'], G['payu'], G['qsc'], G['gsc'], G['ysc'], G['bg2sc']

        def X_load(m):
            P.dma('gpsimd', xt[m % 3][:], xsrc[m])

        def F_pre(m):
            xb = xt[m % 3]
            P.act(junk[:], xb[:], AF.Square, accum_out=ss[m % 2][:])
            P.act(rstd[m % 2][:], ss[m % 2][:], AF.Sqrt, bias=EPS, scale=1.0 / D)
            P.recip(rstd[m % 2][:], rstd[m % 2][:])
            P.act(hn[m % 2][:], xb[:], AF.Copy, scale=rstd[m % 2][:, 0:1])

        def F_pe_a(m):
            for kc in range(8):
                P.tr(pTa[:, kc, :], hn[m % 2][:, kc * 128:(kc + 1) * 128], identb[:])
            P.copy('vector', hT[m % 2][:], pTa[:])

        def F_pe_b(m):
            hTb, zb, ubb, bgb = hT[m % 2], zs[m % 2], ub[m % 2], bgs[m % 2]
            for bi, (c0, c1) in enumerate([(0, 512), (512, 1024), (1024, 1304)]):
                for kc in range(8):
                    P.mm(pz[bi][:, 0:c1 - c0], hTb[:, kc, :], Wbf[:, kc, c0:c1], start=(kc == 0), stop=(kc == 7))
                P.copy('scalar', zb[:, c0:c1], pz[bi][:, 0:c1 - c0])
            for ch in range(12):
                for kc in range(8):
                    P.mm(pc[ch // 4][:, ch % 4, :], Wbf[:, kc, 1304 + ch * 128:1304 + (ch + 1) * 128], hTb[:, kc, :],
                         start=(kc == 0), stop=(kc == 7))
                if ch == 3:
                    P.copy('scalar', hcs[:], pc[0][:])
                if ch == 7:
                    P.tt('vector', ubb[:, :, 2:130], pc[1][:], hcs[:], ALU.mult)
            P.copy('vector', bgb[:], pc[2][:])

        def G1(m):
            zb, ubb, bgb = zs[m % 2], ub[m % 2], bgs[m % 2]
            P.act(sq[:], zb[:, 0:1280], AF.Square)
            P.reduce(ssg[:], sq[:].rearrange("p (g d) -> p g d", d=64))
            P.act(rg[:, 0:8], ssg[:, 0:8], AF.Sqrt, bias=64 * EPS, scale=1.0)
            P.act(rg[:, 8:20], ssg[:, 8:20], AF.Sqrt, bias=EPS, scale=1.0 / 64)
            P.recip(rg[:], rg[:])
            P.tt('vector', qtmp[:].rearrange("p (g d) -> p g d", d=64), zb[:, 0:512].rearrange("p (g d) -> p g d", d=64),
                 AP_(rg, 0, [[20, 128], [1, 8], [0, 64]]), ALU.mult)
            P.tt('gpsimd', nrm[:, 0:512], qtmp[:], gq_s[:], ALU.mult)
            P.tt('vector', ktmp[:, 0:128].rearrange("p (g d) -> p g d", d=64), zb[:, 768:896].rearrange("p (g d) -> p g d", d=64),
                 AP_(rg, 12, [[20, 128], [1, 2], [0, 64]]), ALU.mult)
            P.tt('vector', ktmp[:, 128:256].rearrange("p (g d) -> p g d", d=64), zb[:, 1024:1152].rearrange("p (g d) -> p g d", d=64),
                 AP_(rg, 16, [[20, 128], [1, 2], [0, 64]]), ALU.mult)
            P.tt('gpsimd', nrm[:, 512:768], ktmp[:], gk3_s[:, 1:3, :].rearrange("p a d -> p (a d)"), ALU.mult)
            P.copy('gpsimd', nrm[:, 768:1024], zb[:, 512:768])
            P.copy('gpsimd', AP_(vsw, 1, [[260, 128], [65, 2], [1, 64]]), zb[:, 896:1024].rearrange("p (g d) -> p g d", d=64))
            P.copy('gpsimd', AP_(vsw, 131, [[260, 128], [65, 2], [1, 64]]), zb[:, 1152:1280].rearrange("p (g d) -> p g d", d=64))
            P.dma('scalar', AP_(pay['vs%d' % (m // 16)], (m % 16) * 128 * 130, [[130, 128], [1, 130]]), vsw[:, 0, :])
            P.dma('scalar', AP_(pay['vw%d' % (m // 16)], (m % 16) * 128 * 130, [[130, 128], [1, 130]]), vsw[:, 1, :])
            P.act(gts[:], zb[:, 1280:1304], AF.Sigmoid)
            P.dma('scalar', gsc[m], gts[:])
            P.tt('gpsimd', cv0[:], ubb[:, :, 2:130], AP_(cw_s, 2, [[12, 128], [3, 4], [0, 128]]), ALU.mult)
            P.tt('gpsimd', cv1[:], ubb[:, :, 1:129], AP_(cw_s, 1, [[12, 128], [3, 4], [0, 128]]), ALU.mult)
            P.tt('gpsimd', cv0[:], cv0[:], cv1[:], ALU.add)
            P.tt('gpsimd', cv1[:], ubb[:, :, 0:128], AP_(cw_s, 0, [[12, 128], [3, 4], [0, 128]]), ALU.mult)
            P.tt('gpsimd', cv0[:], cv0[:], cv1[:], ALU.add)
            P.tt('vector', yv[:], bgb[:], cv0[:], ALU.mult)
            P.dma('scalar', ysc[m].rearrange("p (c t) -> p c t", c=4), yv[:])
            P.copy('gpsimd', ut[:], ubb[:, :, 128:130])
            P.dma('sync', payu[m].rearrange("(p e) -> p e", e=8), ut[:].rearrange("p c e -> p (c e)"))
            P.copy('gpsimd', bg2[:], bgb[:, :, 0:2])
            P.dma('sync', bg2sc[m], bg2[:].rearrange("p c e -> p (c e)"))

        def G2(m):
            for bk in range(8):
                P.tr(pTb[:, bk, :], nrm[:, bk * 128:(bk + 1) * 128], identb[:])
            P.copy('vector', trs[:], pTb[:])
            for e in range(2):
                for g in range(2):
                    P.dma('sync' if g == 0 else 'scalar', AP_(qsc, m * 65536 + g * 32768 + e * 128, [[512, 64], [256, 2], [1, 128]]),
                          trs[e * 64:(e + 1) * 64, 2 * g:2 * g + 2, :])
            for ci, cn in enumerate(('ks', 'kw', 'kc', 'vc')):
                P.dma('sync' if ci % 2 == 0 else 'scalar', AP_(pay[cn], m * 128, [[4096, 128], [1, 128]]), trs[:, 4 + ci, :])

        X_load(0)
        X_load(1)
        X_load(2)
        F_pre(0)
        F_pre(1)
        F_pe_a(0)
        F_pe_b(0)
        for m in range(B_NM):
            if m + 3 < B_NM:
                X_load(m + 3)
            if m + 1 < B_NM:
                F_pe_a(m + 1)
            if m + 2 < B_NM:
                F_pre(m + 2)
            G1(m)
            if m + 1 < B_NM:
                F_pe_b(m + 1)
            G2(m)


def emit_Z(C, l, W, G):
    P = C.P
    with C.phase("Z%d" % l):
        kcT_all = C.sb("kcT_all", [128, TP], BF16)
        vcT_all = C.sb("vcT_all", [128, TP], BF16)
        W1bf = C.sb("W1bf", [128, 2, 32, 256], BF16)
        w1st = [C.sb("w1st%d" % i, [128, 8, 256], F32) for i in range(2)]
        W2bf = C.sb("W2bf", [128, 2, 2, 64], BF16)
        w2st = C.sb("w2st", [128, 2, 2, 64], F32)
        peT_s = C.sb("peT_s", [64, 2, 32], F32)
        pebf = C.sb("pebf", [64, 2, 32, 128], BF16)
        b1b_s = C.sb("b1b_s", [128, 2, 256], F32)
        b2b_s = C.sb("b2b_s", [128, 2, 64], F32)
        gkc = C.sb("gkc", [128, 128], F32)
        c1b = C.sb("c1b", [128, 2, 256], F32)
        hid = C.sb("hid", [128, 256], F32)
        g_x2 = C.sb("g_x2", [128, 256], F32)
        g_in = C.sb("g_in", [128, 256], F32)
        g_sg = C.sb("g_sg", [128, 256], F32)
        hbf = C.sb("hbf", [128, 256], BF16)
        hidT = C.sb("hidT", [128, 2, 128], BF16)
        co = C.sb("co", [128, 2, 2, 64], F32)
        cosq = C.sb("cosq", [128, 128], F32)
        css = C.sb("css", [128, 2], F32)
        crs = C.sb("crs", [128, 2], F32)
        kcn = C.sb("kcn", [128, 128], BF16)
        pT = C.ps("pT", [128, 8, 128], BF16)
        px = [C.ps("px%d" % i, [128, 512], F32) for i in range(2)]
        po = C.ps("po", [128, 512], F32)
        identb = G['identb']
        gath = G['gath']
        kcT_s, vcx_s = G['kcT_s'], G['vcx_s']

        with P.group('cst'):
            P.dma('sync', b1b_s[:], W['b1b'][l].rearrange("k p n -> p k n"))
            P.dma('sync', b2b_s[:], W['b2b'][l])
            P.dma('sync', peT_s[:], W['peT'][l].rearrange("k d j -> d k j"))
            P.dma('sync', w2st[:], W['w2'][l].rearrange("k (c p) d -> p k c d", p=128))
            P.dma('sync', gkc[:], W['gk3'][l, :, 0, :])
        P.memset('gpsimd', kcT_all[:, T:TP], 0.0)
        P.memset('gpsimd', vcT_all[:, T:TP], 0.0)
        with P.group('kvcl'):
            for r in range(4):
                for kv, dst in enumerate((kcT_all, vcT_all)):
                    P.dma('sync' if kv == 0 else 'scalar', AP_(dst, r * 128, [[TP, 128], [512, 32], [1, 128]]),
                          AP_(gath['kc' if kv == 0 else 'vc'], r * 1024 * 512, [[4096, 128], [128, 32], [1, 128]]),
                          writes=[dst])
        P.copy('vector', W2bf[:], w2st[:])
        P.copy('vector', pebf[:], AP_(peT_s, 0, [[64, 64], [32, 2], [1, 32], [0, 128]]))
        n = 0
        for kv in range(2):
            for jq in range(4):
                st = w1st[n % 2]
                n += 1
                src = W['w1'][l, kv, jq * 512:(jq + 1) * 512, :].rearrange("(j d) n -> d j n", d=64)
                with P.group("w1st%d" % ((n - 1) % 2)):
                    P.dma('sync', st[0:64, :, :], src, writes=[st])
                    P.dma('scalar', st[64:128, :, :], src, writes=[st])
                P.copy('vector' if jq % 2 == 0 else 'gpsimd', W1bf[:, kv, jq * 8:(jq + 1) * 8, :], st[:])
        for kv in range(2):
            for j in range(32):
                P.mm(px[0][:, 0:256], pebf[:, kv, j, :], W1bf[0:64, kv, j, :], start=(j == 0), stop=(j == 31))
            P.tt('vector', c1b[:, kv, :], px[0][:, 0:256], b1b_s[:, kv, :], ALU.add)
        n = 0
        for ib in range(8):
            for kv in range(2):
                src_all = kcT_all if kv == 0 else vcT_all
                for g in range(2):
                    pp = px[n % 2]
                    n += 1
                    base = 16 * 128 * ib
                    for j in range(32):
                        lhs = AP_(src_all, 64 * g * TP + base + j, [[TP, 64], [16, 128]])
                        P.mm(pp[:, 0:256], lhs, W1bf[64 * g:64 * g + 64, kv, j, :], start=(j == 0), stop=(j == 31),
                             reads=[src_all, W1bf])
                    P.tt('vector', hid[:], pp[:, 0:256], c1b[:, kv, :], ALU.add)
                    P.act(g_x2[:], hid[:], AF.Square)
                    P.ts('gpsimd', g_x2[:], g_x2[:], 0.044715, ALU.mult, 1.0, ALU.add)
                    P.tt('gpsimd', g_in[:], g_x2[:], hid[:], ALU.mult)
                    P.act(g_sg[:], g_in[:], AF.Sigmoid, scale=1.5957691216057308)
                    P.tt('vector', hbf[:], hid[:], g_sg[:], ALU.mult)
                    for c in range(2):
                        P.tr(pT[:, c, :], hbf[:, c * 128:(c + 1) * 128], identb[:])
                    P.copy('vector', hidT[:], pT[:, 0:2, :])
                    for c in range(2):
                        P.mm(po[:, 0:64], hidT[:, c, :], W2bf[:, kv, c, :], start=(c == 0), stop=(c == 1))
                    P.tt('vector', co[:, kv, g, :], po[:, 0:64], b2b_s[:, kv, :], ALU.add)
            P.act(cosq[:], co[:, 0, :, :].rearrange("p g d -> p (g d)"), AF.Square)
            P.reduce(css[:], cosq[:].rearrange("p (g d) -> p g d", d=64))
            P.act(crs[:], css[:], AF.Sqrt, bias=EPS, scale=1.0 / 64)
            P.recip(crs[:], crs[:])
            P.tt('vector', cosq[:].rearrange("p (g d) -> p g d", d=64), co[:, 0, :, :], AP_(crs, 0, [[2, 128], [1, 2], [0, 64]]), ALU.mult)
            P.tt('vector', kcn[:], cosq[:], gkc[:], ALU.mult)
            for g in range(2):
                P.tr(pT[0:64, 4 + g, :], kcn[:, 64 * g:64 * g + 64], identb[:])
            P.copy('vector', kcT_s[0:64, :, ib * 128:(ib + 1) * 128], pT[0:64, 4:6, :])
            P.copy('gpsimd', vcx_s[:, ib, :, 0:64], co[:, 1, :, :])
        if G.get('dbg'):
            P.dma('sync', G['dbg']['kc'], kcT_s[:])
            P.dma('sync', G['dbg']['vc'], vcx_s[:])


def emit_B(C, l, W, G):
    P = C.P
    with C.phase("B%d" % l):
        ksT_s = [C.sb("ksT_s%d" % g, [68, T], BF16) for g in range(2)]
        vsx_s = C.sb("vsx_s", [128, 128, 130], BF16)
        smask_s = C.sb("smask_s", [128, 4, 512], BF16)
        cmask_s = C.sb("cmask_s", [128, 5, 512], BF16)
        wmask_s = C.sb("wmask_s", [128, 8, 128], BF16)
        Eall_s = C.sb("Eall_s", [128, 64, 128], BF16)
        Bsel_s = C.sb("Bsel_s", [128, 512], F32)
        qT_s = [C.sb("qT_s%d" % i, [68, 2, 512], BF16) for i in range(2)]
        kwT_s = [C.sb("kwT_s%d" % i, [68, 2, 1024], BF16) for i in range(2)]
        vwx_s = [C.sb("vwx_s%d" % i, [128, 8, 130], BF16) for i in range(2)]
        gates_s = [C.sb("gates_s%d" % i, [128, 24], F32) for i in range(2)]
        NPB = 6
        Pb = [C.sb("Pb%d" % i, [128, 512], BF16) for i in range(NPB)]
        Pc = [C.sb("Pc%d" % g, [128, 8, 512], BF16) for g in range(2)]
        oc = C.sb("oc", [128, 4, 321], F32)
        den4 = C.sb("den4", [128, 4], F32)
        rd4 = C.sb("rd4", [128, 4], F32)
        coef4 = C.sb("coef4", [128, 4], F32)
        imp = [C.sb("imp%d" % g, [128, 256], F32) for g in range(2)]
        tmpi = C.sb("tmpi", [128, 256], F32)
        m8 = C.sb("m8", [128, 16], F32)
        thr = C.sb("thr", [128, 1], F32)
        nm = [C.sb("nm%d" % g, [128, 256], BF16) for g in range(2)]
        nmT = [C.sb("nmT%d" % g, [128, 2, 128], BF16) for g in range(2)]
        osT = C.sb("osT", [65, 512], F32)
        otmp = C.sb("otmp", [128, 4, 64], F32)
        acc = C.sb("acc", [128, 8, 64], F32)
        accb = C.sb("accb", [128, 512], BF16)
        ssq = C.sb("ssq", [128, 1], F32)
        rat = C.sb("rat", [128, 1], F32)
        rat2 = C.sb("rat2", [128, 1], F32)
        atT = C.sb("atT", [128, 4, 128], BF16)
        ps_s = [C.ps("ps_s%d" % i, [128, 512], F32) for i in range(4)]
        ps_o = [C.ps("ps_o%d" % i, [128, 512], F32) for i in range(1)]
        ps_acc = [C.ps("ps_acc%d" % i, [128, 512], F32) for i in range(2)]
        pm = C.ps("pm", [128, 512], F32)
        pm_b = pm[:].bitcast(BF16)
        identb, identf = G['identb'], G['identf']
        kcT_s, vcx_s = G['kcT_s'], G['vcx_s']
        gath, qsc, gsc, atsc, rasc = G['gath'], G['qsc'], G['gsc'], G['atsc'], G['rasc']

        with P.group('cst'):
            P.dma('sync', cmask_s[:], W['cmask'])
            P.dma('sync', Bsel_s[:], W['Bsel'])
            P.dma('scalar', wmask_s[:], W['wmask'])
            P.dma('scalar', smask_s[:], W['smask'])
            P.dma('scalar', Eall_s[:], W['Eall'])
        with P.group('ksT'):
            for g in range(2):
                P.dma('gpsimd', ksT_s[g][64:68, :], W['kaug'], writes=[ksT_s[g]])
                for r in range(4):
                    P.dma('sync' if r % 2 == 0 else 'scalar', ksT_s[g][0:64, r * 4096:(r + 1) * 4096],
                          AP_(gath['ks'], r * 1024 * 512 + 64 * g * 4096, [[4096, 64], [1, 4096]]),
                          writes=[ksT_s[g]])
        with P.group('vsx'):
            for r in range(4):
                for hv in range(2):
                    P.dma('gpsimd', vsx_s[:, r * 32 + hv * 16:r * 32 + hv * 16 + 16, :],
                          AP_(gath['vs%d' % hv], r * 520 * 512, [[130, 128], [128 * 130, 16], [1, 130]]), writes=[vsx_s])

        if G.get('dbg'):
            P.dma('sync', G['dbg']['ks'], ksT_s[1][:])
            P.dma('sync', G['dbg']['vs'], vsx_s[:])
        state = {'s': 0, 'p': 0, 'x': 0}
        pending = []

        LA = 4

        def run_stream(tiles):
            n = len(tiles)
            bufs = []
            for i in range(n + LA):
                if i < n:
                    ps = ps_s[state['s'] % 4]
                    pb = Pb[state['p'] % NPB]
                    state['s'] += 1
                    state['p'] += 1
                    tiles[i][0](ps)
                    P.act(pb[:], ps[:], AF.Exp)
                    bufs.append(pb)
                j = i - (LA - 2)
                if 0 <= j < n and tiles[j][1] is not None:
                    tiles[j][1](bufs[j])
                if i - LA >= 0:
                    tiles[i - LA][2](bufs[i - LA])

        def mask_mul(pb, map_, mkeys):
            P.tt('vector', pb[:].rearrange("p (a q) -> p a q", a=4), pb[:].rearrange("p (a q) -> p a q", a=4), map_,
                 ALU.mult, reads=[pb] + mkeys, writes=[pb])

        def o_epilogue(psacc, g, br, gs):
            P.copy('vector', osT[:], psacc[0:65, :])
            pmv = pm[:, 0:260].rearrange("p (r d) -> p r d", d=65)
            for r in range(4):
                P.tr(pmv[:, r, :], osT[0:65, r * 128:(r + 1) * 128], identf[0:65, 0:65])
            P.recip(rd4[:], pmv[:, :, 0])
            P.tt('vector', coef4[:], rd4[:], AP_(gs, 12 * g + br, [[24, 128], [3, 4]]), ALU.mult)
            dst = acc[:, 4 * g:4 * g + 4, :]
            P.tt('vector', otmp[:], pmv[:, :, 1:65], AP_(coef4, 0, [[4, 128], [1, 4], [0, 64]]), ALU.mult)
            P.tt('gpsimd', dst, dst, otmp[:], ALU.add)

        for m in range(B_NM):
            qs = qT_s[m % 2]
            kws = kwT_s[m % 2]
            vws = vwx_s[m % 2]
            gs = gates_s[m % 2]
            h0 = 0 if m > 0 else 1
            with P.group("qT_s%d" % (m % 2)):
                P.dma('sync', qs[0:64, :, :], qsc[m].rearrange("g d n -> d g n"), writes=[qs])
                P.dma('scalar', qs[64:68, :, :], W['qaug'][m], writes=[qs])
            P.dma('scalar', gs[:], gsc[m])
            nh = 2 - h0
            c0 = 128 * (m - 1 + h0)
            with P.group("kws%d" % (m % 2)):
                for r in range(4):
                    P.dma('sync' if r % 2 == 0 else 'scalar',
                          AP_(kws, (r * 2 + h0) * 128, [[2048, 64], [1024, 2], [1, 128 * nh]]),
                          AP_(gath['kw'], r * 1024 * 512 + c0, [[4096, 64], [64 * 4096, 2], [1, 128 * nh]]),
                          writes=[kws])
            with P.group("vws%d" % (m % 2)):
                for r in range(4):
                    for hf in range(h0, 2):
                        mp = m - 1 + hf
                        P.dma('gpsimd', vws[:, r * 2 + hf, :],
                              AP_(gath['vw%d' % (mp // 16)], r * 520 * 512 + (mp % 16) * 128 * 130, [[130, 128], [1, 130]]),
                              writes=[vws])
            for g in range(2):
                P.copy('gpsimd', AP_(kws, 64 * 2048 + g * 1024 + h0 * 128, [[2048, 4], [256, 4], [1, 128 * (2 - h0)]]),
                       AP_(ksT_s[0], 64 * T + 128 * (m - 1 + h0), [[T, 4], [4096, 4], [1, 128 * (2 - h0)]]),
                       reads=[ksT_s[0]], writes=[kws])
            n_it = (32 * m + 30) // 128 + 1
            for g in range(2):
                for it in range(n_it):
                    ps = ps_s[state['s'] % 4]
                    state['s'] += 1
                    delta = 128 * it - 32 * m
                    masked = delta >= -128
                    P.mm(ps[:], kcT_s[:, g, it * 128:(it + 1) * 128], qs[:, g, :], start=True, stop=not masked)
                    if masked:
                        P.mm(ps[:], identb[:], cmask_s[:, (-delta) // 32, :], start=False, stop=True)
                    P.act(Pc[g][:, it, :], ps[:], AF.Exp)
                for r in range(4):
                    po = ps_o[0]
                    for it in range(n_it):
                        P.mm(po[:, 0:321], Pc[g][:, it, r * 128:(r + 1) * 128], vcx_s[:, it, g, :], start=(it == 0), stop=(it == n_it - 1))
                    P.copy('scalar', oc[:, r, :], po[:, 0:321])
                P.ts('vector', den4[:], oc[:, :, 64], 1e-30, ALU.max)
                P.recip(rd4[:], den4[:])
                P.ts('vector', imp[g][:], oc[:, 0, 65:321], rd4[:, 0:1], ALU.mult)
                for r in range(1, 4):
                    P.stt(imp[g][:], oc[:, r, 65:321], rd4[:, r:r + 1], imp[g][:], ALU.mult, ALU.add)
                P.tt('vector', coef4[:], rd4[:], AP_(gs, 12 * g + 0, [[24, 128], [3, 4]]), ALU.mult)
                P.tt('vector', acc[:, 4 * g:4 * g + 4, :], oc[:, :, 0:64], AP_(coef4, 0, [[4, 128], [1, 4], [0, 64]]), ALU.mult)
                P.tt('vector', imp[g][:], imp[g][:], Bsel_s[:, 256 - 8 * m:512 - 8 * m], ALU.add)
                P.ts('vector', imp[g][:, 0:1], imp[g][:, 0:1], 1e4, ALU.add)
                P.op('vector', lambda e, g=g: e.max(out=m8[:, 0:8], in_=imp[g][:]), reads=[imp[g]], writes=[m8])
                P.op('vector', lambda e, g=g: e.match_replace(out=tmpi[:], in_to_replace=m8[:, 0:8], in_values=imp[g][:], imm_value=-3.0e38),
                     reads=[imp[g], m8], writes=[tmpi])
                P.op('vector', lambda e: e.max(out=m8[:, 8:16], in_=tmpi[:]), reads=[tmpi], writes=[m8])
                P.ts('vector', thr[:], m8[:, 15:16], -1e29, ALU.max)
                P.ts('vector', nm[g][:], imp[g][:], thr[:, 0:1], ALU.is_ge)
            if pending:
                pending.pop()()
            tiles = []
            wt = [(r, hf) for hf in range(h0, 2) for r in range(4)]
            for g in range(2):
                for idx, (r, hf) in enumerate(wt):
                    def eS(ps, g=g, r=r, hf=hf):
                        P.mm(ps[:], kws[:, g, (r * 2 + hf) * 128:(r * 2 + hf + 1) * 128], qs[:, g, :], start=True, stop=(hf == 0))
                        if hf == 1:
                            P.mm(ps[:], identb[:], smask_s[:, r, :], start=False, stop=True)

                    def eX(pb, r=r):
                        mask_mul(pb, AP_(wmask_s, r * 128, [[1024, 128], [0, 4], [1, 128]]), [wmask_s])

                    def ePV(pb, g=g, r=r, hf=hf, idx=idx):
                        P.mm(ps_acc[g][0:65, :], vws[:, r * 2 + hf, 65 * g:65 * g + 65], pb[:], start=(idx == 0), stop=(idx == len(wt) - 1))
                    tiles.append((eS, eX if hf == 0 else None, ePV))
            run_stream(tiles)
            for g in range(2):
                for hb in range(2):
                    P.tr(pm_b[:, hb * 128:(hb + 1) * 128], nm[g][:, hb * 128:(hb + 1) * 128], identb[:])
                P.copy('vector', nmT[g][:].rearrange("p a q -> p (a q)"), pm_b[:, 0:256], reads=[pm])
            for g in range(2):
                o_epilogue(ps_acc[g], g, 2, gs)
            nkt = 4 * m + 4
            for g in range(2):
                tiles = []
                for kt in range(nkt):
                    def eS(ps, g=g, kt=kt):
                        diag = kt >= 4 * m
                        col = ((kt % 4) * 32 + kt // 4) * 128
                        P.mm(ps[:], ksT_s[g][:, col:col + 128], qs[:, g, :], start=True, stop=not diag)
                        if diag:
                            P.mm(ps[:], identb[:], smask_s[:, kt - 4 * m, :], start=False, stop=True)

                    def eX(pb, g=g, kt=kt):
                        bank = (ps_o[0], pm)[state['x'] % 2]
                        state['x'] += 1
                        P.mm(bank[:, 0:128], Eall_s[:, kt % 64, :], nmT[g][:, kt // 64, :], start=True, stop=True)
                        mask_mul(pb, AP_(bank, 0, [[512, 128], [0, 4], [1, 128]]), [bank])

                    def ePV(pb, g=g, kt=kt):
                        P.mm(ps_acc[g][0:65, :], vsx_s[:, (kt % 4) * 32 + kt // 4, 65 * g:65 * g + 65], pb[:], start=(kt == 0), stop=(kt == nkt - 1))
                    tiles.append((eS, eX, ePV))
                run_stream(tiles)
                o_epilogue(ps_acc[g], g, 1, gs)
            accf = acc[:].rearrange("p h d -> p (h d)")
            P.act(accb[:], accf, AF.Square, accum_out=ssq[:])
            P.act(rat[:], ssq[:], AF.Sqrt, bias=EPS, scale=1.0 / 512)
            P.recip(rat2[:], rat[:])
            P.dma('gpsimd', rasc[m], rat2[:])
            P.copy('gpsimd', accb[:], accf)
            if G.get('dbg'):
                P.dma('gpsimd', G['dbg']['ra'][m], rat2[:])

            def finish_out(m=m):
                for bk in range(4):
                    P.tr(pm_b[:, bk * 128:(bk + 1) * 128], accb[:, bk * 128:(bk + 1) * 128], identb[:])
                P.copy('vector', atT[:].rearrange("p b q -> p (b q)"), pm_b[:, 0:512], reads=[pm])
                P.dma('gpsimd', atsc[m], atT[:].rearrange("p b q -> p (b q)"))
                if G.get('dbg'):
                    P.dma('gpsimd', G['dbg']['at'][m], atT[:].rearrange("p b q -> p (b q)"))
            pending.append(finish_out)
        pending.pop()()


def emit_C(C, l, xsrc, xdst, W, G):
    P = C.P
    with C.phase("C%d" % l):
        Wo = C.sb("Wo", [128, 8, D], BF16)
        Wdn = C.sb("Wdn", [128, 32, D], BF16)
        wst = [C.sb("wst%d" % i, [128, 8, 256], F32) for i in range(2)]
        Wup = [C.sb("Wup%d" % i, [128, 8, 256], BF16) for i in range(2)]
        gout_s = C.sb("gout_s", [128, 8], F32)
        gffn_s = C.sb("gffn_s", [128, D], F32)
        cw_s = C.sb("cw_s", [128, 4, 3], F32)
        selw_s = C.sb("selw_s", [128, 4], F32)
        xt = [C.sb("xt%d" % i, [128, D], F32) for i in range(2)]
        x1 = [C.sb("x1_%d" % i, [128, D], F32) for i in range(4)]
        at_s = [C.sb("at_s%d" % i, [128, 4, 128], BF16) for i in range(2)]
        yv = [C.sb("yv%d" % i, [128, 4, 128], F32) for i in range(2)]
        bg2 = [C.sb("bg2_%d" % i, [128, 4, 2], F32) for i in range(2)]
        tl = [C.sb("tl%d" % i, [128, 4, 8], F32) for i in range(2)]
        halo = C.sb("halo", [128, 4, 2], F32)
        pt0 = C.sb("pt0", [128, 4], F32)
        pt1 = C.sb("pt1", [128, 4], F32)
        ysq = [C.sb("ysq%d" % i, [128, 4, 128], F32) for i in range(2)]
        ybf = [C.sb("ybf%d" % i, [128, 4, 128], BF16) for i in range(2)]
        rc = C.sb("rc", [128, 1], F32)
        rc2 = [C.sb("rc2_%d" % i, [128, 1], F32) for i in range(2)]
        ra_s = [C.sb("ra_s%d" % i, [128, 1], F32) for i in range(2)]
        ss = C.sb("ss", [128, 1], F32)
        rstd = C.sb("rstd", [128, 1], F32)
        h2n = [C.sb("h2n%d" % i, [128, D], BF16) for i in range(2)]
        h2T = C.sb("h2T", [128, 8, 512], BF16)
        rl = [C.sb("rl%d" % i, [128, 512], F32) for i in range(2)]
        actT = C.sb("actT", [128, 32, 512], BF16)
        pa = [C.ps("pa%d" % i, [128, 512], F32) for i in range(2)]
        pcv = [C.ps("pcv%d" % i, [128, 512], F32) for i in range(2)]
        pu = [C.ps("pu%d" % i, [128, 512], F32) for i in range(2)]
        pT = C.ps("pT", [128, 8, 128], BF16)
        px = C.ps("px", [128, 512], F32)
        identb, onesf = G['identb'], G['onesf']
        gathu, ysc, bg2sc, atsc, rasc = G['gathu'], G['ysc'], G['bg2sc'], G['atsc'], G['rasc']

        with P.group('cst'):
            P.dma('sync', gout_s[:], W['gout'][l])
            P.dma('sync', gffn_s[:], W['gffn'][l])
            P.dma('sync', cw_s[:], W['cw'][l])
            P.dma('sync', selw_s[:], W['selw'])
        n = 0
        for kc in range(8):
            for hf in range(2):
                st = wst[n % 2]
                n += 1
                stw = AP_(st, 0, [[2048, 128], [1, 512]])
                P.dma('sync' if hf == 0 else 'scalar', stw, W['w_o'][l, kc * 128:(kc + 1) * 128, hf * 512:(hf + 1) * 512], writes=[st])
                P.ts('vector' if hf == 0 else 'gpsimd', Wo[:, kc, hf * 512:(hf + 1) * 512], stw, gout_s[:, kc:kc + 1], ALU.mult, reads=[st, gout_s])
        for f2 in range(16):
            st = wst[n % 2]
            n += 1
            stv = AP_(st, 0, [[2048, 128], [1024, 2], [1, 1024]])
            P.dma('sync' if f2 % 2 == 0 else 'scalar', stv, W['w_dn'][l, f2 * 256:(f2 + 1) * 256, :].rearrange("(a p) n -> p a n", p=128), writes=[st])
            P.copy('vector' if f2 % 2 == 0 else 'gpsimd', Wdn[:, f2 * 2:(f2 + 1) * 2, :], stv, reads=[st])
        up_n = [n]

        def S1a(m):
            xb, ab, rab, yb, bgb, tlb = xt[m % 2], at_s[m % 2], ra_s[m % 2], yv[m % 2], bg2[m % 2], tl[m % 2]
            P.dma('sync', xb[:], xsrc[m])
            P.dma('scalar', ab[:], atsc[m].rearrange("p (b q) -> p b q", b=4))
            P.dma('sync', rab[:], rasc[m])
            P.dma('scalar', yb[:], ysc[m].rearrange("p (c t) -> p c t", c=4))
            P.dma('sync', bgb[:], bg2sc[m].rearrange("p (c e) -> p c e", e=2))
            if m == 0:
                P.memset('gpsimd', tlb[:, 0, :], 0.0)
            with P.group("tl%d" % (m % 2)):
                if m > 0:
                    P.dma('gpsimd', tlb[:, 0, :], gathu[3 * B_NM + m - 1].rearrange("(p e) -> p e", e=8), writes=[tlb])
                for cd in range(1, 4):
                    P.dma('gpsimd', tlb[:, cd, :], gathu[(cd - 1) * B_NM + m].rearrange("(p e) -> p e", e=8), writes=[tlb])
            hf = halo[:].rearrange("p c e -> p (c e)")
            P.ts('vector', hf, tlb[:, 0, :], selw_s[:, 0:1], ALU.mult)
            for cd in range(1, 4):
                P.stt(hf, tlb[:, cd, :], selw_s[:, cd:cd + 1], hf, ALU.mult, ALU.add)
            P.tt('vector', pt0[:], halo[:, :, 1], cw_s[:, :, 1], ALU.mult)
            P.tt('vector', pt1[:], halo[:, :, 0], cw_s[:, :, 0], ALU.mult)
            P.tt('vector', pt0[:], pt0[:], pt1[:], ALU.add)
            P.tt('vector', pt0[:], pt0[:], bgb[:, :, 0], ALU.mult)
            P.tt('vector', yb[:, :, 0], yb[:, :, 0], pt0[:], ALU.add)
            P.tt('vector', pt1[:], halo[:, :, 1], cw_s[:, :, 0], ALU.mult)
            P.tt('vector', pt1[:], pt1[:], bgb[:, :, 1], ALU.mult)
            P.tt('vector', yb[:, :, 1], yb[:, :, 1], pt1[:], ALU.add)
            P.act(ysq[m % 2][:], yb[:], AF.Square)
            P.copy('gpsimd', ybf[m % 2][:], yb[:])

        def S1b(m, j):
            xb, ab, rab = xt[m % 2], at_s[m % 2], ra_s[m % 2]
            for ch in range(4):
                P.mm(px[:, 0:1], ysq[m % 2][:, ch, :], onesf[:], start=(ch == 0), stop=(ch == 3))
            P.act(rc[:], px[:, 0:1], AF.Sqrt, bias=EPS, scale=1.0 / 512)
            P.recip(rc2[m % 2][:], rc[:])
            for nh in range(2):
                for kc in range(4):
                    P.mm(pa[nh][:], ab[:, kc, :], Wo[:, kc, nh * 512:(nh + 1) * 512], start=(kc == 0), stop=(kc == 3))
                for kc in range(4):
                    P.mm(pcv[nh][:], ybf[m % 2][:, kc, :], Wo[:, 4 + kc, nh * 512:(nh + 1) * 512], start=(kc == 0), stop=(kc == 3))
            x1b = x1[j]
            for nh in range(2):
                sl = slice(nh * 512, (nh + 1) * 512)
                P.stt(x1b[:, sl], pa[nh][:], rab[:, 0:1], xb[:, sl], ALU.mult, ALU.add)
                P.stt(x1b[:, sl], pcv[nh][:], rc2[m % 2][:, 0:1], x1b[:, sl], ALU.mult, ALU.add)
            if G.get('dbg'):
                P.dma('gpsimd', G['dbg']['x1'][m], x1b[:])
                P.dma('gpsimd', G['dbg']['y'][m], ybf[m % 2][:].rearrange("p c t -> p (c t)"))
                P.dma('gpsimd', G['dbg']['rc'][m], rc2[m % 2][:])
            P.act(h2n[m % 2][:], x1b[:], AF.Square, accum_out=ss[:])
            P.act(rstd[:], ss[:], AF.Sqrt, bias=EPS, scale=1.0 / D)
            P.recip(rstd[:], rstd[:])
            P.stt(h2n[m % 2][:], x1b[:], rstd[:, 0:1], gffn_s[:], ALU.mult, ALU.mult)

        def S1c(m, j):
            for kc in range(8):
                P.tr(pT[:, kc, :], h2n[m % 2][:, kc * 128:(kc + 1) * 128], identb[:])
            P.copy('scalar', h2T[:, :, j * 128:(j + 1) * 128], pT[:])

        NBT = B_NM // 4
        for j in range(4):
            S1a(j)
            S1b(j, j)
            S1c(j, j)
        for bt in range(NBT):
            for u in range(16):
                st = wst[up_n[0] % 2]
                wb = Wup[up_n[0] % 2]
                up_n[0] += 1
                P.dma('sync' if u % 2 == 0 else 'scalar', st[:], W['w_up'][l, :, u * 256:(u + 1) * 256].rearrange("(kc p) n -> p kc n", p=128))
                P.copy('gpsimd' if u % 2 == 0 else 'vector', wb[:], st[:])
                for fl in range(2):
                    f = u * 2 + fl
                    pp = pu[f % 2]
                    for kc in range(8):
                        P.mm(pp[:], wb[:, kc, fl * 128:(fl + 1) * 128], h2T[:, kc, :], start=(kc == 0), stop=(kc == 7))
                    rb = rl[f % 2]
                    P.act(rb[:], pp[:], AF.Relu)
                    P.tt('gpsimd' if f % 2 == 0 else 'vector', actT[:, f, :], rb[:], rb[:], ALU.mult)
            nxt = bt + 1 < NBT
            for j in range(4):
                m = bt * 4 + j
                mn = m + 4
                if nxt:
                    S1a(mn)
                ob = x1[j]
                for nh in range(2):
                    pp = pu[nh]
                    for f in range(32):
                        P.mm(pp[:], actT[:, f, j * 128:(j + 1) * 128], Wdn[:, f, nh * 512:(nh + 1) * 512], start=(f == 0), stop=(f == 31))
                    P.tt('vector', ob[:, nh * 512:(nh + 1) * 512], pp[:], x1[j][:, nh * 512:(nh + 1) * 512], ALU.add)
                P.dma('sync', xdst[m], ob[:])
                if nxt:
                    S1b(mn, j)
                    if j >= 1:
                        S1c(mn - 1, j - 1)
            if nxt:
                S1c(bt * 4 + 7, 3)


def build_fused(nlayers=DEPTH, stages="AGZBC", dbg=False):
    C = Ctx()
    P = C.P
    W = {}
    W['x0'] = C.din("x0", [B_NM, 128, D], F32)
    W['w_in'] = C.din("w_in", [nlayers, D, INW], F32)
    W['gmix'] = C.din("gmix", [nlayers, 128, 8], F32)
    W['gq'] = C.din("gq", [nlayers, 128, 512], F32)
    W['gk3'] = C.din("gk3", [nlayers, 128, 3, 128], F32)
    W['cw'] = C.din("cw", [nlayers, 128, 4, 3], F32)
    W['peT'] = C.din("peT", [nlayers, 2, 64, 32], F32)
    W['w1'] = C.din("w1", [nlayers, 2, 2048, 256], F32)
    W['b1b'] = C.din("b1b", [nlayers, 2, 128, 256], F32)
    W['w2'] = C.din("w2", [nlayers, 2, 256, 64], F32)
    W['b2b'] = C.din("b2b", [nlayers, 128, 2, 64], F32)
    W['w_o'] = C.din("w_o", [nlayers, D, D], F32)
    W['gout'] = C.din("gout", [nlayers, 128, 8], F32)
    W['gffn'] = C.din("gffn", [nlayers, 128, D], F32)
    W['w_up'] = C.din("w_up", [nlayers, D, DFF], F32)
    W['w_dn'] = C.din("w_dn", [nlayers, DFF, D], F32)
    W['smask'] = C.din("smask", [128, 4, 512], BF16)
    W['cmask'] = C.din("cmask", [128, 5, 512], BF16)
    W['wmask'] = C.din("wmask", [128, 8, 128], BF16)
    W['Eall'] = C.din("Eall", [128, 64, 128], BF16)
    W['Bsel'] = C.din("Bsel", [128, 512], F32)
    W['kaug'] = C.din("kaug", [4, T], BF16)
    W['kcaug'] = C.din("kcaug", [4, 2, 1024], BF16)
    W['qaug'] = C.din("qaug", [B_NM, 4, 2, 512], BF16)
    W['vcxc'] = C.din("vcxc", [128, 8, 2, 257], BF16)
    W['selw'] = C.din("selw", [128, 4], F32)
    identb_d = C.din("identb", [128, 128], BF16)
    identf_d = C.din("identf", [128, 128], F32)
    onesf_d = C.din("onesf", [128, 1], F32)
    xo = C.dout("xo", [B_NM, 128, D], F32)

    G = {}
    COMPS = [('ks', 1024), ('kw', 1024), ('kc', 1024), ('vc', 1024), ('vs0', 520), ('vs1', 520), ('vw0', 520), ('vw1', 520)]
    G['pay'] = {cn: C.dint("pay_" + cn, [rows, 512], BF16) for cn, rows in COMPS}
    G['gath'] = {cn: C.dint("gath_" + cn, [4 * rows, 512], BF16) for cn, rows in COMPS}
    G['payu'] = C.dint("payu", [B_NM, 1024], F32).ap()
    G['gathu'] = C.dint("gathu", [4 * B_NM, 1024], F32).ap()
    xbuf = C.dint("xbuf", [B_NM, 128, D], F32).ap()
    G['qsc'] = C.dint("qsc", [B_NM, 2, 64, 512], BF16)
    G['gsc'] = C.dint("gsc", [B_NM, 128, 24], F32).ap()
    G['ysc'] = C.dint("ysc", [B_NM, 128, 512], F32).ap()
    G['bg2sc'] = C.dint("bg2sc", [B_NM, 128, 8], F32).ap()
    G['atsc'] = C.dint("atsc", [B_NM, 128, 512], BF16).ap()
    G['rasc'] = C.dint("rasc", [B_NM, 128, 1], F32).ap()
    qsc_ap = G['qsc'].ap()
    if dbg:
        G['dbg'] = {'at': C.dout("dbg_at", [B_NM, 128, 512], BF16), 'ra': C.dout("dbg_ra", [B_NM, 128, 1], F32),
                    'x1': C.dout("dbg_x1", [B_NM, 128, D], F32), 'y': C.dout("dbg_y", [B_NM, 128, 512], BF16),
                    'rc': C.dout("dbg_rc", [B_NM, 128, 1], F32),
                    'kc': C.dout("dbg_kc", [68, 2, 1024], BF16), 'vc': C.dout("dbg_vc", [128, 8, 2, 321], BF16),
                    'ks': C.dout("dbg_ks", [68, T], BF16), 'vs': C.dout("dbg_vs", [128, 128, 130], BF16)}

    G['identb'] = C.sb("identb_s", [128, 128], BF16)
    G['identf'] = C.sb("identf_s", [128, 128], F32)
    G['onesf'] = C.sb("onesf_s", [128, 1], F32)
    with P.group('cst'):
        P.dma('sync', G['identb'][:], identb_d)
        P.dma('sync', G['identf'][:], identf_d)
        P.dma('sync', G['onesf'][:], onesf_d)
    P.barrier()
    rg = [[0, 1, 2, 3], [4, 5, 6, 7]]
    def mk_cc(i_ap, o_ap):
        return lambda e: e.collective_compute("AllGather", ALU.bypass, replica_groups=rg, ins=[i_ap.opt()], outs=[o_ap.opt()])
    cc_fns = [mk_cc(G['pay'][cn].ap(), G['gath'][cn].ap()) for cn, _ in COMPS] + [mk_cc(G['payu'], G['gathu'])]
    for l in range(nlayers):
        xsrc = W['x0'] if l == 0 else xbuf
        xdst = xo if l == nlayers - 1 else xbuf
        if 'A' in stages:
            emit_A(C, l, xsrc, W, G)
        if 'G' in stages:
            P.collectives(cc_fns)
        with C.phase("ZB%d" % l):
            G['kcT_s'] = C.sb("kcT_s", [68, 2, 1024], BF16)
            G['vcx_s'] = C.sb("vcx_s", [128, 8, 2, 321], BF16)
            with P.group('cst2'):
                P.dma('sync', G['kcT_s'][64:68, :, :], W['kcaug'])
                P.dma('scalar', G['vcx_s'][:, :, :, 64:321], W['vcxc'])
            if 'Z' in stages:
                emit_Z(C, l, W, G)
            Gb = dict(G)
            Gb['qsc'] = qsc_ap
            if 'B' in stages:
                emit_B(C, l, W, Gb)
        if 'C' in stages:
            emit_C(C, l, xsrc, xdst, W, G)
    if 'C' not in stages:
        with C.phase("dbg"):
            xt = [C.sb("xt%d" % i, [128, D], F32) for i in range(2)]
            for m in range(B_NM):
                P.dma('sync', xt[m % 2][:], W['x0'][m])
                P.dma('sync', xo[m], xt[m % 2][:])
    return C.finish()


IDENTB = np.eye(128, dtype=np.float32).astype(NPBF)
ONESF = np.ones((128, 1), np.float32)


def _bf(a):
    return np.ascontiguousarray(a).astype(NPBF)


def _selmap():
    i = np.arange(1024)[:, None] * 16
    j = np.arange(256)[None, :] * 64
    sh = np.clip(np.minimum(i + 32, j + 64) - np.maximum(i, j), 0, None).astype(np.float32) / 32.0
    sh[1023] = 0.0
    return sh


def _pos_rows(pos):
    pos = np.maximum(pos, 0)
    return np.stack([np.ones_like(pos), np.ones_like(pos), pos % 128, pos // 128]).astype(np.float32)


def core_consts(s):
    ki = np.arange(128)[:, None]
    qi = np.arange(128)[None, :]
    tri_gt = np.where(ki > qi, NEGM, 0.0).astype(np.float32)
    tri_le = np.where(ki <= qi, NEGM, 0.0).astype(np.float32)
    full = np.full((128, 128), NEGM, np.float32)
    zero = np.zeros((128, 128), np.float32)
    smask = np.tile(np.stack([zero if d < s else (tri_gt if d == s else full) for d in range(4)], axis=1), (1, 1, 4))
    cm = []
    for v in range(5):
        ip = ki - 32 * v
        vis = (16 * ip + 31) <= (128 * s + qi)
        cm.append(np.where(vis, 0.0, NEGM).astype(np.float32))
    cmask = np.tile(np.stack(cm, axis=1), (1, 1, 4))
    wm = []
    for d in range(8):
        off = s + 4 - d
        if off == 0:
            wm.append((ki <= qi).astype(np.float32))
        elif off == 4:
            wm.append((ki > qi).astype(np.float32))
        elif 0 < off < 4:
            wm.append(np.ones((128, 128), np.float32))
        else:
            wm.append(zero)
    wmask = np.stack(wm, axis=1)
    E = np.zeros((128, 64, 128), np.float32)
    for kt in range(64):
        for k in range(128):
            E[2 * kt + k // 64, kt, k] = 1.0
    r = np.arange(512)[None, :] - 256 - 2 * s
    qq = np.arange(128)[:, None]
    Brel = np.zeros((128, 512), np.float32)
    Brel = np.where(r >= 2, np.float32(-1e30), Brel)
    Brel = np.where(r == 1, np.where(qq >= 64, np.float32(1e4), np.float32(-1e30)), Brel)
    Brel = np.where(r == 0, np.float32(1e4), Brel)
    Brel = np.where((r == -1) & (qq < 64), np.float32(1e4), Brel)
    rr, mm, kk = np.meshgrid(np.arange(4), np.arange(32), np.arange(128), indexing='ij')
    kpos = (128 * (4 * mm + rr) + kk).reshape(-1)
    kaug = _pos_rows(kpos)
    kc = _pos_rows(np.arange(1024) * 16 + 31)
    kcaug = np.stack([kc, kc], axis=1)
    qaug = np.zeros((B_NM, 4, 2, 512), np.float32)
    qv = np.arange(128)
    for m in range(B_NM):
        c = 4 * m + s
        for g in range(2):
            for rh in range(4):
                sl = 2.0 ** (-(4 * g + rh + 1))
                cs = slice(rh * 128, (rh + 1) * 128)
                qaug[m, 0, g, cs] = -sl * qv
                qaug[m, 1, g, cs] = -sl * 128.0 * c
                qaug[m, 2, g, cs] = sl
                qaug[m, 3, g, cs] = sl * 128.0
    sm = _selmap().reshape(8, 128, 256).transpose(1, 0, 2)
    vcxc = np.zeros((128, 8, 2, 257), np.float32)
    vcxc[:, :, :, 0] = 1.0
    vcxc[127, 7, :, 0] = 0.0
    vcxc[:, :, :, 1:257] = sm[:, :, None, :]
    selw = np.zeros((128, 4), np.float32)
    selw[:, s] = 1.0
    return {'smask': _bf(smask), 'cmask': _bf(cmask), 'wmask': _bf(wmask), 'Eall': _bf(E),
            'Bsel': np.ascontiguousarray(Brel.astype(np.float32)), 'kaug': _bf(kaug), 'kcaug': _bf(kcaug),
            'qaug': _bf(qaug), 'vcxc': _bf(vcxc), 'selw': selw,
            'identb': IDENTB, 'identf': np.eye(128, dtype=np.float32), 'onesf': ONESF}


def prep_fused(p, L=DEPTH):
    p = {k: (v if k == 'x' else v[:L]) for k, v in p.items()}
    common = {
        'w_in': np.ascontiguousarray(p['w_in'], dtype=np.float32),
        'gmix': np.ascontiguousarray(p['g_mix_norm'].reshape(L, 8, 128).transpose(0, 2, 1)),
        'gq': np.ascontiguousarray(np.tile(p['g_q'][:, None, :], (1, 128, 8))),
        'gk3': np.ascontiguousarray(np.broadcast_to(np.tile(p['g_k'], (1, 1, 2))[:, None], (L, 128, 3, 128))),
        'cw': np.ascontiguousarray(p['conv_w'].reshape(L, 3, 4, 128).transpose(0, 3, 2, 1)),
        'peT': np.ascontiguousarray(p['pe_cmp'].transpose(0, 1, 3, 2)),
        'w1': np.ascontiguousarray(p['w_cmp1'], dtype=np.float32),
        'b1b': np.ascontiguousarray(np.broadcast_to(p['b_cmp1'][:, :, None, :], (L, 2, 128, 256))),
        'w2': np.ascontiguousarray(p['w_cmp2'], dtype=np.float32),
        'b2b': np.ascontiguousarray(np.broadcast_to(p['b_cmp2'][:, None], (L, 128, 2, 64))),
        'w_o': np.ascontiguousarray(p['w_o'], dtype=np.float32),
        'gout': np.ascontiguousarray(p['g_out'].reshape(L, 8, 128).transpose(0, 2, 1)),
        'gffn': np.ascontiguousarray(np.tile(p['g_ffn_norm'][:, None, :], (1, 128, 1))),
        'w_up': np.ascontiguousarray(p['w_up'], dtype=np.float32),
        'w_dn': np.ascontiguousarray(p['w_down'], dtype=np.float32),
    }
    maps = []
    x = np.ascontiguousarray(p['x'], dtype=np.float32)
    for b in range(NB):
        xv = x[b].reshape(T // 128, 128, D)
        for s in range(4):
            mp = dict(common)
            mp.update(core_consts(s))
            mp['x0'] = np.ascontiguousarray(xv[np.arange(B_NM) * 4 + s])
            maps.append(mp)
    return maps


_PROG = {}


def run_fused(inputs, nlayers=DEPTH):
    p = {k: np.asarray(v) for k, v in inputs.items()}
    if nlayers not in _PROG:
        _PROG[nlayers] = build_fused(nlayers)
    res = run_bass_kernel_spmd(_PROG[nlayers], prep_fused(p, nlayers), core_ids=list(range(NCORES)))
    out = np.empty((NB, T, D), np.float32)
    for b in range(NB):
        ov = out[b].reshape(T // 128, 128, D)
        for s in range(4):
            ov[np.arange(B_NM) * 4 + s] = np.asarray(res.results[4 * b + s]['xo'])
    return out


def kernel(**inputs):
    return run_fused(inputs, DEPTH)
```

```python
import contextlib
import numpy as np
import ml_dtypes
import concourse.bass as bass
import concourse.mybir as mybir
from concourse.bass_utils import run_bass_kernel_spmd

F32 = mybir.dt.float32
BF16 = mybir.dt.bfloat16
AF = mybir.ActivationFunctionType
ALU = mybir.AluOpType
AX = mybir.AxisListType
NPBF = ml_dtypes.bfloat16

ENGS = ['tensor', 'vector', 'scalar', 'gpsimd', 'sync']
NCORES = 8
D = 1024
T = 16384
NB = 2
DEPTH = 4
INW = 2840
EPS = 1e-6
NEGM = -30000.0


def key_of(ap):
    t = getattr(ap, 'tensor', ap)
    return t.name.split('@')[0]


class Prog:
    def __init__(self, nc):
        self.nc = nc
        self.q = {e: [] for e in ENGS}
        self.semcount = {}
        self.res = {}
        self.seen = {e: {} for e in ENGS}

    grp = None

    @contextlib.contextmanager
    def group(self, sem):
        s = 'd_' + sem
        self.grp = {'sem': sem, 's': s, 'start': self.semcount.get(s, 0), 'keys': set(), 'pre': {}}
        try:
            yield
        finally:
            g, self.grp = self.grp, None
            v = self.semcount.get(s, 0)
            for k in g['keys']:
                self.res[k]['w'] = (s, v)

    def _need(self, eng, reads, writes):
        need = {}

        def add(tok):
            if tok is None:
                return
            s, v = tok
            if eng == 'tensor' and s == 'e_tensor':
                return
            if self.grp is not None and s == self.grp['s'] and v > self.grp['start']:
                return
            if need.get(s, 0) < v:
                need[s] = v
        for k in reads:
            st = self.res.get(k)
            if st:
                add(st['w'])
        for k in writes:
            sts = [self.res.get(k)]
            if self.grp is not None:
                if k not in self.grp['pre']:
                    st0 = self.res.get(k)
                    self.grp['pre'][k] = {'w': st0['w'], 'r': dict(st0['r'])} if st0 else None
                sts.append(self.grp['pre'][k])
            for st in sts:
                if st:
                    add(st['w'])
                    for s, v in st['r'].items():
                        add((s, v))
        for s, v in need.items():
            if self.seen[eng].get(s, 0) < v:
                self.q[eng].append(('wait', s, v))
                self.seen[eng][s] = v

    def _update(self, reads, writes, tok):
        s, v = tok
        for k in reads:
            st = self.res.setdefault(k, {'w': None, 'r': {}})
            if st['r'].get(s, 0) < v:
                st['r'][s] = v
        for k in writes:
            self.res[k] = {'w': tok, 'r': {}}

    def op(self, eng, fn, reads=(), writes=()):
        reads = [r if isinstance(r, str) else key_of(r) for r in reads]
        writes = [r if isinstance(r, str) else key_of(r) for r in writes]
        self._need(eng, reads, writes)
        s = 'e_' + eng
        v = self.semcount.get(s, 0) + 1
        self.semcount[s] = v
        self.q[eng].append(('op', fn, s, 1))
        self._update(reads, writes, (s, v))

    def dma(self, eng, out, in_, reads=None, writes=None, sem=None, **kw):
        if reads is None:
            reads = [] if in_.tensor.name in self.dram_names else [in_]
        if writes is None:
            writes = [] if out.tensor.name in self.dram_names else [out]
        assert reads or writes or sem
        reads = [r if isinstance(r, str) else key_of(r) for r in reads]
        writes = [r if isinstance(r, str) else key_of(r) for r in writes]
        if self.grp is not None:
            sem = self.grp['sem']
            self.grp['keys'].update(writes)
        if sem is None:
            sem = writes[0] if writes else 'st_' + reads[0]
        self._need(eng, reads, writes)
        s = 'd_' + sem
        v = self.semcount.get(s, 0) + 16
        self.semcount[s] = v
        self.q[eng].append(('op', lambda e: e.dma_start(out=out, in_=in_, **kw), s, 16))
        self._update(reads, writes, (s, v))

    dram_names = set()

    def mm(self, out, lhsT, rhs, start=True, stop=True, reads=None, writes=None):
        self.op('tensor', lambda e: e.matmul(out, lhsT=lhsT, rhs=rhs, start=start, stop=stop),
                reads=reads if reads is not None else [lhsT, rhs],
                writes=writes if writes is not None else [out])

    def tr(self, out, in_, ident, reads=None, writes=None):
        self.op('tensor', lambda e: e.transpose(out, in_, ident),
                reads=reads if reads is not None else [in_, ident],
                writes=writes if writes is not None else [out])

    def act(self, out, in_, func, bias=None, scale=None, accum_out=None, reads=None, writes=None, eng='scalar'):
        kw = {}
        rd = [in_]
        if bias is not None:
            kw['bias'] = bias
            if not isinstance(bias, (int, float)):
                rd.append(bias)
        if scale is not None:
            kw['scale'] = scale
            if not isinstance(scale, (int, float)):
                rd.append(scale)
        wr = [out]
        if accum_out is not None:
            kw['accum_out'] = accum_out
            wr.append(accum_out)
        self.op('scalar', lambda e: e.activation(out=out, in_=in_, func=func, **kw),
                reads=reads if reads is not None else rd,
                writes=writes if writes is not None else wr)

    def tt(self, eng, out, in0, in1, op, reads=None, writes=None):
        self.op(eng, lambda e: e.tensor_tensor(out=out, in0=in0, in1=in1, op=op),
                reads=reads if reads is not None else [in0, in1],
                writes=writes if writes is not None else [out])

    def ts(self, eng, out, in0, s1, op0, s2=None, op1=None, reads=None, writes=None):
        rd = [in0]
        if not isinstance(s1, (int, float)):
            rd.append(s1)
        if s2 is not None and not isinstance(s2, (int, float)):
            rd.append(s2)
        if op1 is None:
            fn = lambda e: e.tensor_scalar(out=out, in0=in0, scalar1=s1, scalar2=None, op0=op0)
        else:
            fn = lambda e: e.tensor_scalar(out=out, in0=in0, scalar1=s1, scalar2=s2, op0=op0, op1=op1)
        self.op(eng, fn, reads=reads if reads is not None else rd,
                writes=writes if writes is not None else [out])

    def stt(self, out, in0, scalar, in1, op0, op1, reads=None, writes=None):
        rd = [in0, in1]
        if not isinstance(scalar, (int, float)):
            rd.append(scalar)
        self.op('vector', lambda e: e.scalar_tensor_tensor(out=out, in0=in0, scalar=scalar, in1=in1, op0=op0, op1=op1),
                reads=reads if reads is not None else rd,
                writes=writes if writes is not None else [out])

    def copy(self, eng, out, in_, reads=None, writes=None):
        if eng == 'scalar':
            fn = lambda e: e.activation(out=out, in_=in_, func=AF.Copy)
        else:
            fn = lambda e: e.tensor_copy(out=out, in_=in_)
        self.op(eng, fn, reads=reads if reads is not None else [in_],
                writes=writes if writes is not None else [out])

    def recip(self, out, in_):
        self.op('vector', lambda e: e.reciprocal(out=out, in_=in_), reads=[in_], writes=[out])

    def reduce(self, out, in_, op=ALU.add, axis=AX.X):
        self.op('vector', lambda e: e.tensor_reduce(out=out, in_=in_, axis=axis, op=op), reads=[in_], writes=[out])

    def memset(self, eng, ap, val):
        self.op(eng, lambda e: e.memset(ap, val), writes=[ap])

    def barrier(self):
        for e in ENGS:
            for s, v in self.semcount.items():
                if self.seen[e].get(s, 0) < v:
                    self.q[e].append(('wait', s, v))
                    self.seen[e][s] = v
        self.res = {}

    def collectives(self, fns):
        self.barrier()
        s = 'c_cc'
        for fn in fns:
            v = self.semcount.get(s, 0) + 1
            self.semcount[s] = v
            self.q['gpsimd'].append(('op', fn, s, 1))
        self.barrier()

    def emit(self):
        nc = self.nc
        for s, v in self.semcount.items():
            if self.seen['sync'].get(s, 0) < v:
                self.q['sync'].append(('wait', s, v))
        with contextlib.ExitStack() as es:
            semh = {s: es.enter_context(nc.semaphore(s)) for s in self.semcount}
            block = es.enter_context(nc.Block())

            def mk(engname):
                def body(e):
                    for it in self.q[engname]:
                        if it[0] == 'wait':
                            e.wait_ge(semh[it[1]], it[2])
                        else:
                            ins = it[1](e)
                            ins.then_inc(semh[it[2]], it[3])
                return body
            for engname in ENGS:
                if self.q[engname]:
                    getattr(block, engname)(mk(engname))


def AP_(t, offset, dims):
    return bass.AP(t, offset, [list(d) for d in dims])


class Ctx:
    def __init__(self):
        self.nc = bass.Bass("TRN2", target_bir_lowering=False)
        self.es = contextlib.ExitStack()
        self.P = Prog(self.nc)
        self.P.dram_names = set()

    def din(self, name, shape, dt):
        self.P.dram_names.add(name)
        return self.nc.dram_tensor(name, list(shape), dt, kind="ExternalInput").ap()

    def dout(self, name, shape, dt):
        self.P.dram_names.add(name)
        return self.nc.dram_tensor(name, list(shape), dt, kind="ExternalOutput").ap()

    tag = None

    def dint(self, name, shape, dt):
        self.P.dram_names.add(name)
        return self.nc.dram_tensor(name, list(shape), dt)

    def sb(self, name, shape, dt):
        if self.tag:
            name = name + '@' + self.tag
        return self.es.enter_context(self.nc.sbuf_tensor(name, list(shape), dt))

    def ps(self, name, shape, dt):
        if self.tag:
            name = name + '@' + self.tag
        return self.es.enter_context(self.nc.psum_tensor(name, list(shape), dt))

    @contextlib.contextmanager
    def phase(self, tag):
        old, oldtag = self.es, self.tag
        self.es, self.tag = contextlib.ExitStack(), tag
        try:
            yield
        finally:
            self.P.barrier()
            self.es.close()
            self.es, self.tag = old, oldtag

    def finish(self):
        self.P.emit()
        self.es.close()
        return self.nc


B_NM = 32
DFF = 4096
PAYR = 6176
KS_R0, KW_R0, KC_R0, VC_R0, VS_R0, VW_R0 = 0, 1024, 2048, 3072, 4096, 5136
TP = T + 128


def emit_A(C, l, xsrc, W, G):
    P = C.P
    with C.phase("A%d" % l):
        Wbf = C.sb("Wbf", [128, 8, INW], BF16)
        wst = [C.sb("wst%d" % i, [128, INW // 2], F32) for i in range(2)]
        gmix_s = C.sb("gmix_s", [128, 8], F32)
        gq_s = C.sb("gq_s", [128, 512], F32)
        gk3_s = C.sb("gk3_s", [128, 3, 128], F32)
        cw_s = C.sb("cw_s", [128, 4, 3], F32)
        xt = [C.sb("xt%d" % i, [128, D], F32) for i in range(3)]
        junk = C.sb("junk", [128, D], BF16)
        ss = [C.sb("ss%d" % i, [128, 1], F32) for i in range(2)]
        rstd = [C.sb("rstd%d" % i, [128, 1], F32) for i in range(2)]
        hn = [C.sb("hn%d" % i, [128, D], BF16) for i in range(2)]
        hT = [C.sb("hT%d" % i, [128, 8, 128], BF16) for i in range(2)]
        zs = [C.sb("zs%d" % i, [128, 1304], F32) for i in range(2)]
        hcs = C.sb("hcs", [128, 4, 128], F32)
        ub = [C.sb("ub%d" % i, [128, 4, 130], F32) for i in range(2)]
        bgs = [C.sb("bgs%d" % i, [128, 4, 128], F32) for i in range(2)]
        sq = C.sb("sq", [128, 1280], F32)
        ssg = C.sb("ssg", [128, 20], F32)
        rg = C.sb("rg", [128, 20], F32)
        qtmp = C.sb("qtmp", [128, 512], F32)
        ktmp = C.sb("ktmp", [128, 256], F32)
        nrm = C.sb("nrm", [128, 1024], BF16)
        trs = C.sb("trs", [128, 8, 128], BF16)
        vsw = C.sb("vsw", [128, 2, 130], BF16)
        gts = C.sb("gts", [128, 24], F32)
        cv0 = C.sb("cv0", [128, 4, 128], F32)
        cv1 = C.sb("cv1", [128, 4, 128], F32)
        yv = C.sb("yv", [128, 4, 128], F32)
        ut = C.sb("ut", [128, 4, 2], F32)
        bg2 = C.sb("bg2", [128, 4, 2], F32)
        pTa = C.ps("pTa", [128, 8, 128], BF16)
        pTb = C.ps("pTb", [128, 8, 128], BF16)
        pz = [C.ps("pz%d" % i, [128, 512], F32) for i in range(3)]
        pc = [C.ps("pc%d" % i, [128, 4, 128], F32) for i in range(3)]
        identb = G['identb']

        with P.group('cst'):
            P.dma('sync', gmix_s[:], W['gmix'][l])
            P.dma('sync', gq_s[:], W['gq'][l])
            P.dma('sync', gk3_s[:], W['gk3'][l])
            P.dma('sync', cw_s[:], W['cw'][l])
        P.memset('gpsimd', vsw[:], 1.0)
        P.memset('gpsimd', ub[0][:], 0.0)
        P.memset('gpsimd', ub[1][:], 0.0)
        for kc in range(8):
            for hf in range(2):
                st = wst[hf]
                P.dma('sync' if hf == 0 else 'scalar', st[:], W['w_in'][l, kc * 128:(kc + 1) * 128, hf * 1420:(hf + 1) * 1420])
                if hf == 0:
                    P.ts('vector', Wbf[:, kc, 0:1420], st[:], gmix_s[:, kc:kc + 1], ALU.mult)
                else:
                    P.act(Wbf[:, kc, 1420:2840], st[:], AF.Copy, scale=gmix_s[:, kc:kc + 1])

        pay, payu, qsc, gsc, ysc, bg2sc = G['pay'], G['payu'], G['qsc'], G['gsc'], G['ysc'], G['bg2sc']

        def X_load(m):
            P.dma('gpsimd', xt[m % 3][:], xsrc[m])

        def F_pre(m):
            xb = xt[m % 3]
            P.act(junk[:], xb[:], AF.Square, accum_out=ss[m % 2][:])
            P.act(rstd[m % 2][:], ss[m % 2][:], AF.Sqrt, bias=EPS, scale=1.0 / D)
            P.recip(rstd[m % 2][:], rstd[m % 2][:])
            P.act(hn[m % 2][:], xb[:], AF.Copy, scale=rstd[m % 2][:, 0:1])

        def F_pe_a(m):
            for kc in range(8):
                P.tr(pTa[:, kc, :], hn[m % 2][:, kc * 128:(kc + 1) * 128], identb[:])
            P.copy('vector', hT[m % 2][:], pTa[:])

        def F_pe_b(m):
            hTb, zb, ubb, bgb = hT[m % 2], zs[m % 2], ub[m % 2], bgs[m % 2]
            for bi, (c0, c1) in enumerate([(0, 512), (512, 1024), (1024, 1304)]):
                for kc in range(8):
                    P.mm(pz[bi][:, 0:c1 - c0], hTb[:, kc, :], Wbf[:, kc, c0:c1], start=(kc == 0), stop=(kc == 7))
                P.copy('scalar', zb[:, c0:c1], pz[bi][:, 0:c1 - c0])
            for ch in range(12):
                for kc in range(8):
                    P.mm(pc[ch // 4][:, ch % 4, :], Wbf[:, kc, 1304 + ch * 128:1304 + (ch + 1) * 128], hTb[:, kc, :],
                         start=(kc == 0), stop=(kc == 7))
                if ch == 3:
                    P.copy('scalar', hcs[:], pc[0][:])
                if ch == 7:
                    P.tt('vector', ubb[:, :, 2:130], pc[1][:], hcs[:], ALU.mult)
            P.copy('vector', bgb[:], pc[2][:])

        def G1(m):
            zb, ubb, bgb = zs[m % 2], ub[m % 2], bgs[m % 2]
            P.act(sq[:], zb[:, 0:1280], AF.Square)
            P.reduce(ssg[:], sq[:].rearrange("p (g d) -> p g d", d=64))
            P.act(rg[:, 0:8], ssg[:, 0:8], AF.Sqrt, bias=64 * EPS, scale=1.0)
            P.act(rg[:, 8:20], ssg[:, 8:20], AF.Sqrt, bias=EPS, scale=1.0 / 64)
            P.recip(rg[:], rg[:])
            P.tt('vector', qtmp[:].rearrange("p (g d) -> p g d", d=64), zb[:, 0:512].rearrange("p (g d) -> p g d", d=64),
                 AP_(rg, 0, [[20, 128], [1, 8], [0, 64]]), ALU.mult)
            P.tt('gpsimd', nrm[:, 0:512], qtmp[:], gq_s[:], ALU.mult)
            P.tt('vector', ktmp[:, 0:128].rearrange("p (g d) -> p g d", d=64), zb[:, 768:896].rearrange("p (g d) -> p g d", d=64),
                 AP_(rg, 12, [[20, 128], [1, 2], [0, 64]]), ALU.mult)
            P.tt('vector', ktmp[:, 128:256].rearrange("p (g d) -> p g d", d=64), zb[:, 1024:1152].rearrange("p (g d) -> p g d", d=64),
                 AP_(rg, 16, [[20, 128], [1, 2], [0, 64]]), ALU.mult)
            P.tt('gpsimd', nrm[:, 512:768], ktmp[:], gk3_s[:, 1:3, :].rearrange("p a d -> p (a d)"), ALU.mult)
            P.copy('gpsimd', nrm[:, 768:1024], zb[:, 512:768])
            P.copy('gpsimd', AP_(vsw, 1, [[260, 128], [65, 2], [1, 64]]), zb[:, 896:1024].rearrange("p (g d) -> p g d", d=64))
            P.copy('gpsimd', AP_(vsw, 131, [[260, 128], [65, 2], [1, 64]]), zb[:, 1152:1280].rearrange("p (g d) -> p g d", d=64))
            P.dma('scalar', AP_(pay['vs%d' % (m // 16)], (m % 16) * 128 * 130, [[130, 128], [1, 130]]), vsw[:, 0, :])
            P.dma('scalar', AP_(pay['vw%d' % (m // 16)], (m % 16) * 128 * 130, [[130, 128], [1, 130]]), vsw[:, 1, :])
            P.act(gts[:], zb[:, 1280:1304], AF.Sigmoid)
            P.dma('scalar', gsc[m], gts[:])
            P.tt('gpsimd', cv0[:], ubb[:, :, 2:130], AP_(cw_s, 2, [[12, 128], [3, 4], [0, 128]]), ALU.mult)
            P.tt('gpsimd', cv1[:], ubb[:, :, 1:129], AP_(cw_s, 1, [[12, 128], [3, 4], [0, 128]]), ALU.mult)
            P.tt('gpsimd', cv0[:], cv0[:], cv1[:], ALU.add)
            P.tt('gpsimd', cv1[:], ubb[:, :, 0:128], AP_(cw_s, 0, [[12, 128], [3, 4], [0, 128]]), ALU.mult)
            P.tt('gpsimd', cv0[:], cv0[:], cv1[:], ALU.add)
            P.tt('vector', yv[:], bgb[:], cv0[:], ALU.mult)
            P.dma('scalar', ysc[m].rearrange("p (c t) -> p c t", c=4), yv[:])
            P.copy('gpsimd', ut[:], ubb[:, :, 128:130])
            P.dma('sync', payu[m].rearrange("(p e) -> p e", e=8), ut[:].rearrange("p c e -> p (c e)"))
            P.copy('gpsimd', bg2[:], bgb[:, :, 0:2])
            P.dma('sync', bg2sc[m], bg2[:].rearrange("p c e -> p (c e)"))

        def G2(m):
            for bk in range(8):
                P.tr(pTb[:, bk, :], nrm[:, bk * 128:(bk + 1) * 128], identb[:])
            P.copy('vector', trs[:], pTb[:])
            for e in range(2):
                for g in range(2):
                    P.dma('sync' if g == 0 else 'scalar', AP_(qsc, m * 65536 + g * 32768 + e * 128, [[512, 64], [256, 2], [1, 128]]),
                          trs[e * 64:(e + 1) * 64, 2 * g:2 * g + 2, :])
            for ci, cn in enumerate(('ks', 'kw', 'kc', 'vc')):
                P.dma('sync' if ci % 2 == 0 else 'scalar', AP_(pay[cn], m * 128, [[4096, 128], [1, 128]]), trs[:, 4 + ci, :])

        X_load(0)
        X_load(1)
        X_load(2)
        F_pre(0)
        F_pre(1)
        F_pe_a(0)
        F_pe_b(0)
        for m in range(B_NM):
            if m + 3 < B_NM:
                X_load(m + 3)
            if m + 1 < B_NM:
                F_pe_a(m + 1)
            if m + 2 < B_NM:
                F_pre(m + 2)
            G1(m)
            if m + 1 < B_NM:
                F_pe_b(m + 1)
            G2(m)


def emit_Z(C, l, W, G):
    P = C.P
    with C.phase("Z%d" % l):
        kcT_all = C.sb("kcT_all", [128, TP], BF16)
        vcT_all = C.sb("vcT_all", [128, TP], BF16)
        W1bf = C.sb("W1bf", [128, 2, 32, 256], BF16)
        w1st = [C.sb("w1st%d" % i, [128, 8, 256], F32) for i in range(2)]
        W2bf = C.sb("W2bf", [128, 2, 2, 64], BF16)
        w2st = C.sb("w2st", [128, 2, 2, 64], F32)
        peT_s = C.sb("peT_s", [64, 2, 32], F32)
        pebf = C.sb("pebf", [64, 2, 32, 128], BF16)
        b1b_s = C.sb("b1b_s", [128, 2, 256], F32)
        b2b_s = C.sb("b2b_s", [128, 2, 64], F32)
        gkc = C.sb("gkc", [128, 128], F32)
        c1b = C.sb("c1b", [128, 2, 256], F32)
        hid = C.sb("hid", [128, 256], F32)
        g_x2 = C.sb("g_x2", [128, 256], F32)
        g_in = C.sb("g_in", [128, 256], F32)
        g_sg = C.sb("g_sg", [128, 256], F32)
        hbf = C.sb("hbf", [128, 256], BF16)
        hidT = C.sb("hidT", [128, 2, 128], BF16)
        co = C.sb("co", [128, 2, 2, 64], F32)
        cosq = C.sb("cosq", [128, 128], F32)
        css = C.sb("css", [128, 2], F32)
        crs = C.sb("crs", [128, 2], F32)
        kcn = C.sb("kcn", [128, 128], BF16)
        pT = C.ps("pT", [128, 8, 128], BF16)
        px = [C.ps("px%d" % i, [128, 512], F32) for i in range(2)]
        po = C.ps("po", [128, 512], F32)
        identb = G['identb']
        gath = G['gath']
        kcT_s, vcx_s = G['kcT_s'], G['vcx_s']

        with P.group('cst'):
            P.dma('sync', b1b_s[:], W['b1b'][l].rearrange("k p n -> p k n"))
            P.dma('sync', b2b_s[:], W['b2b'][l])
            P.dma('sync', peT_s[:], W['peT'][l].rearrange("k d j -> d k j"))
            P.dma('sync', w2st[:], W['w2'][l].rearrange("k (c p) d -> p k c d", p=128))
            P.dma('sync', gkc[:], W['gk3'][l, :, 0, :])
        P.memset('gpsimd', kcT_all[:, T:TP], 0.0)
        P.memset('gpsimd', vcT_all[:, T:TP], 0.0)
        with P.group('kvcl'):
            for r in range(4):
                for kv, dst in enumerate((kcT_all, vcT_all)):
                    P.dma('sync' if kv == 0 else 'scalar', AP_(dst, r * 128, [[TP, 128], [512, 32], [1, 128]]),
                          AP_(gath['kc' if kv == 0 else 'vc'], r * 1024 * 512, [[4096, 128], [128, 32], [1, 128]]),
                          writes=[dst])
        P.copy('vector', W2bf[:], w2st[:])
        P.copy('vector', pebf[:], AP_(peT_s, 0, [[64, 64], [32, 2], [1, 32], [0, 128]]))
        n = 0
        for kv in range(2):
            for jq in range(4):
                st = w1st[n % 2]
                n += 1
                src = W['w1'][l, kv, jq * 512:(jq + 1) * 512, :].rearrange("(j d) n -> d j n", d=64)
                with P.group("w1st%d" % ((n - 1) % 2)):
                    P.dma('sync', st[0:64, :, :], src, writes=[st])
                    P.dma('scalar', st[64:128, :, :], src, writes=[st])
                P.copy('vector' if jq % 2 == 0 else 'gpsimd', W1bf[:, kv, jq * 8:(jq + 1) * 8, :], st[:])
        for kv in range(2):
            for j in range(32):
                P.mm(px[0][:, 0:256], pebf[:, kv, j, :], W1bf[0:64, kv, j, :], start=(j == 0), stop=(j == 31))
            P.tt('vector', c1b[:, kv, :], px[0][:, 0:256], b1b_s[:, kv, :], ALU.add)
        n = 0
        for ib in range(8):
            for kv in range(2):
                src_all = kcT_all if kv == 0 else vcT_all
                for g in range(2):
                    pp = px[n % 2]
                    n += 1
                    base = 16 * 128 * ib
                    for j in range(32):
                        lhs = AP_(src_all, 64 * g * TP + base + j, [[TP, 64], [16, 128]])
                        P.mm(pp[:, 0:256], lhs, W1bf[64 * g:64 * g + 64, kv, j, :], start=(j == 0), stop=(j == 31),
                             reads=[src_all, W1bf])
                    P.tt('vector', hid[:], pp[:, 0:256], c1b[:, kv, :], ALU.add)
                    P.act(g_x2[:], hid[:], AF.Square)
                    P.ts('gpsimd', g_x2[:], g_x2[:], 0.044715, ALU.mult, 1.0, ALU.add)
                    P.tt('gpsimd', g_in[:], g_x2[:], hid[:], ALU.mult)
                    P.act(g_sg[:], g_in[:], AF.Sigmoid, scale=1.5957691216057308)
                    P.tt('vector', hbf[:], hid[:], g_sg[:], ALU.mult)
                    for c in range(2):
                        P.tr(pT[:, c, :], hbf[:, c * 128:(c + 1) * 128], identb[:])
                    P.copy('vector', hidT[:], pT[:, 0:2, :])
                    for c in range(2):
                        P.mm(po[:, 0:64], hidT[:, c, :], W2bf[:, kv, c, :], start=(c == 0), stop=(c == 1))
                    P.tt('vector', co[:, kv, g, :], po[:, 0:64], b2b_s[:, kv, :], ALU.add)
            P.act(cosq[:], co[:, 0, :, :].rearrange("p g d -> p (g d)"), AF.Square)
            P.reduce(css[:], cosq[:].rearrange("p (g d) -> p g d", d=64))
            P.act(crs[:], css[:], AF.Sqrt, bias=EPS, scale=1.0 / 64)
            P.recip(crs[:], crs[:])
            P.tt('vector', cosq[:].rearrange("p (g d) -> p g d", d=64), co[:, 0, :, :], AP_(crs, 0, [[2, 128], [1, 2], [0, 64]]), ALU.mult)
            P.tt('vector', kcn[:], cosq[:], gkc[:], ALU.mult)
            for g in range(2):
                P.tr(pT[0:64, 4 + g, :], kcn[:, 64 * g:64 * g + 64], identb[:])
            P.copy('vector', kcT_s[0:64, :, ib * 128:(ib + 1) * 128], pT[0:64, 4:6, :])
            P.copy('gpsimd', vcx_s[:, ib, :, 0:64], co[:, 1, :, :])
        if G.get('dbg'):
            P.dma('sync', G['dbg']['kc'], kcT_s[:])
            P.dma('sync', G['dbg']['vc'], vcx_s[:])


def emit_B(C, l, W, G):
    P = C.P
    with C.phase("B%d" % l):
        ksT_s = [C.sb("ksT_s%d" % g, [68, T], BF16) for g in range(2)]
        vsx_s = C.sb("vsx_s", [128, 128, 130], BF16)
        smask_s = C.sb("smask_s", [128, 4, 512], BF16)
        cmask_s = C.sb("cmask_s", [128, 5, 512], BF16)
        wmask_s = C.sb("wmask_s", [128, 8, 128], BF16)
        Eall_s = C.sb("Eall_s", [128, 64, 128], BF16)
        Bsel_s = C.sb("Bsel_s", [128, 512], F32)
        qT_s = [C.sb("qT_s%d" % i, [68, 2, 512], BF16) for i in range(2)]
        kwT_s = [C.sb("kwT_s%d" % i, [68, 2, 1024], BF16) for i in range(2)]
        vwx_s = [C.sb("vwx_s%d" % i, [128, 8, 130], BF16) for i in range(2)]
        gates_s = [C.sb("gates_s%d" % i, [128, 24], F32) for i in range(2)]
        NPB = 6
        Pb = [C.sb("Pb%d" % i, [128, 512], BF16) for i in range(NPB)]
        Pc = [C.sb("Pc%d" % g, [128, 8, 512], BF16) for g in range(2)]
        oc = C.sb("oc", [128, 4, 321], F32)
        den4 = C.sb("den4", [128, 4], F32)
        rd4 = C.sb("rd4", [128, 4], F32)
        coef4 = C.sb("coef4", [128, 4], F32)
        imp = [C.sb("imp%d" % g, [128, 256], F32) for g in range(2)]
        tmpi = C.sb("tmpi", [128, 256], F32)
        m8 = C.sb("m8", [128, 16], F32)
        thr = C.sb("thr", [128, 1], F32)
        nm = [C.sb("nm%d" % g, [128, 256], BF16) for g in range(2)]
        nmT = [C.sb("nmT%d" % g, [128, 2, 128], BF16) for g in range(2)]
        osT = C.sb("osT", [65, 512], F32)
        otmp = C.sb("otmp", [128, 4, 64], F32)
        acc = C.sb("acc", [128, 8, 64], F32)
        accb = C.sb("accb", [128, 512], BF16)
        ssq = C.sb("ssq", [128, 1], F32)
        rat = C.sb("rat", [128, 1], F32)
        rat2 = C.sb("rat2", [128, 1], F32)
        atT = C.sb("atT", [128, 4, 128], BF16)
        ps_s = [C.ps("ps_s%d" % i, [128, 512], F32) for i in range(4)]
        ps_o = [C.ps("ps_o%d" % i, [128, 512], F32) for i in range(1)]
        ps_acc = [C.ps("ps_acc%d" % i, [128, 512], F32) for i in range(2)]
        pm = C.ps("pm", [128, 512], F32)
        pm_b = pm[:].bitcast(BF16)
        identb, identf = G['identb'], G['identf']
        kcT_s, vcx_s = G['kcT_s'], G['vcx_s']
        gath, qsc, gsc, atsc, rasc = G['gath'], G['qsc'], G['gsc'], G['atsc'], G['rasc']

        with P.group('cst'):
            P.dma('sync', cmask_s[:], W['cmask'])
            P.dma('sync', Bsel_s[:], W['Bsel'])
            P.dma('scalar', wmask_s[:], W['wmask'])
            P.dma('scalar', smask_s[:], W['smask'])
            P.dma('scalar', Eall_s[:], W['Eall'])
        with P.group('ksT'):
            for g in range(2):
                P.dma('gpsimd', ksT_s[g][64:68, :], W['kaug'], writes=[ksT_s[g]])
                for r in range(4):
                    P.dma('sync' if r % 2 == 0 else 'scalar', ksT_s[g][0:64, r * 4096:(r + 1) * 4096],
                          AP_(gath['ks'], r * 1024 * 512 + 64 * g * 4096, [[4096, 64], [1, 4096]]),
                          writes=[ksT_s[g]])
        with P.group('vsx'):
            for r in range(4):
                for hv in range(2):
                    P.dma('gpsimd', vsx_s[:, r * 32 + hv * 16:r * 32 + hv * 16 + 16, :],
                          AP_(gath['vs%d' % hv], r * 520 * 512, [[130, 128], [128 * 130, 16], [1, 130]]), writes=[vsx_s])

        if G.get('dbg'):
            P.dma('sync', G['dbg']['ks'], ksT_s[1][:])
            P.dma('sync', G['dbg']['vs'], vsx_s[:])
        state = {'s': 0, 'p': 0, 'x': 0}
        pending = []

        LA = 4

        def run_stream(tiles):
            n = len(tiles)
            bufs = []
            for i in range(n + LA):
                if i < n:
                    ps = ps_s[state['s'] % 4]
                    pb = Pb[state['p'] % NPB]
                    state['s'] += 1
                    state['p'] += 1
                    tiles[i][0](ps)
                    P.act(pb[:], ps[:], AF.Exp)
                    bufs.append(pb)
                j = i - (LA - 2)
                if 0 <= j < n and tiles[j][1] is not None:
                    tiles[j][1](bufs[j])
                if i - LA >= 0:
                    tiles[i - LA][2](bufs[i - LA])

        def mask_mul(pb, map_, mkeys):
            P.tt('vector', pb[:].rearrange("p (a q) -> p a q", a=4), pb[:].rearrange("p (a q) -> p a q", a=4), map_,
                 ALU.mult, reads=[pb] + mkeys, writes=[pb])

        def o_epilogue(psacc, g, br, gs):
            P.copy('vector', osT[:], psacc[0:65, :])
            pmv = pm[:, 0:260].rearrange("p (r d) -> p r d", d=65)
            for r in range(4):
                P.tr(pmv[:, r, :], osT[0:65, r * 128:(r + 1) * 128], identf[0:65, 0:65])
            P.recip(rd4[:], pmv[:, :, 0])
            P.tt('vector', coef4[:], rd4[:], AP_(gs, 12 * g + br, [[24, 128], [3, 4]]), ALU.mult)
            dst = acc[:, 4 * g:4 * g + 4, :]
            P.tt('vector', otmp[:], pmv[:, :, 1:65], AP_(coef4, 0, [[4, 128], [1, 4], [0, 64]]), ALU.mult)
            P.tt('gpsimd', dst, dst, otmp[:], ALU.add)

        for m in range(B_NM):
            qs = qT_s[m % 2]
            kws = kwT_s[m % 2]
            vws = vwx_s[m % 2]
            gs = gates_s[m % 2]
            h0 = 0 if m > 0 else 1
            with P.group("qT_s%d" % (m % 2)):
                P.dma('sync', qs[0:64, :, :], qsc[m].rearrange("g d n -> d g n"), writes=[qs])
                P.dma('scalar', qs[64:68, :, :], W['qaug'][m], writes=[qs])
            P.dma('scalar', gs[:], gsc[m])
            nh = 2 - h0
            c0 = 128 * (m - 1 + h0)
            with P.group("kws%d" % (m % 2)):
                for r in range(4):
                    P.dma('sync' if r % 2 == 0 else 'scalar',
                          AP_(kws, (r * 2 + h0) * 128, [[2048, 64], [1024, 2], [1, 128 * nh]]),
                          AP_(gath['kw'], r * 1024 * 512 + c0, [[4096, 64], [64 * 4096, 2], [1, 128 * nh]]),
                          writes=[kws])
            with P.group("vws%d" % (m % 2)):
                for r in range(4):
                    for hf in range(h0, 2):
                        mp = m - 1 + hf
                        P.dma('gpsimd', vws[:, r * 2 + hf, :],
                              AP_(gath['vw%d' % (mp // 16)], r * 520 * 512 + (mp % 16) * 128 * 130, [[130, 128], [1, 130]]),
                              writes=[vws])
            for g in range(2):
                P.copy('gpsimd', AP_(kws, 64 * 2048 + g * 1024 + h0 * 128, [[2048, 4], [256, 4], [1, 128 * (2 - h0)]]),
                       AP_(ksT_s[0], 64 * T + 128 * (m - 1 + h0), [[T, 4], [4096, 4], [1, 128 * (2 - h0)]]),
                       reads=[ksT_s[0]], writes=[kws])
            n_it = (32 * m + 30) // 128 + 1
            for g in range(2):
                for it in range(n_it):
                    ps = ps_s[state['s'] % 4]
                    state['s'] += 1
                    delta = 128 * it - 32 * m
                    masked = delta >= -128
                    P.mm(ps[:], kcT_s[:, g, it * 128:(it + 1) * 128], qs[:, g, :], start=True, stop=not masked)
                    if masked:
                        P.mm(ps[:], identb[:], cmask_s[:, (-delta) // 32, :], start=False, stop=True)
                    P.act(Pc[g][:, it, :], ps[:], AF.Exp)
                for r in range(4):
                    po = ps_o[0]
                    for it in range(n_it):
                        P.mm(po[:, 0:321], Pc[g][:, it, r * 128:(r + 1) * 128], vcx_s[:, it, g, :], start=(it == 0), stop=(it == n_it - 1))
                    P.copy('scalar', oc[:, r, :], po[:, 0:321])
                P.ts('vector', den4[:], oc[:, :, 64], 1e-30, ALU.max)
                P.recip(rd4[:], den4[:])
                P.ts('vector', imp[g][:], oc[:, 0, 65:321], rd4[:, 0:1], ALU.mult)
                for r in range(1, 4):
                    P.stt(imp[g][:], oc[:, r, 65:321], rd4[:, r:r + 1], imp[g][:], ALU.mult, ALU.add)
                P.tt('vector', coef4[:], rd4[:], AP_(gs, 12 * g + 0, [[24, 128], [3, 4]]), ALU.mult)
                P.tt('vector', acc[:, 4 * g:4 * g + 4, :], oc[:, :, 0:64], AP_(coef4, 0, [[4, 128], [1, 4], [0, 64]]), ALU.mult)
                P.tt('vector', imp[g][:], imp[g][:], Bsel_s[:, 256 - 8 * m:512 - 8 * m], ALU.add)
                P.ts('vector', imp[g][:, 0:1], imp[g][:, 0:1], 1e4, ALU.add)
                P.op('vector', lambda e, g=g: e.max(out=m8[:, 0:8], in_=imp[g][:]), reads=[imp[g]], writes=[m8])
                P.op('vector', lambda e, g=g: e.match_replace(out=tmpi[:], in_to_replace=m8[:, 0:8], in_values=imp[g][:], imm_value=-3.0e38),
                     reads=[imp[g], m8], writes=[tmpi])
                P.op('vector', lambda e: e.max(out=m8[:, 8:16], in_=tmpi[:]), reads=[tmpi], writes=[m8])
                P.ts('vector', thr[:], m8[:, 15:16], -1e29, ALU.max)
                P.ts('vector', nm[g][:], imp[g][:], thr[:, 0:1], ALU.is_ge)
            if pending:
                pending.pop()()
            tiles = []
            wt = [(r, hf) for hf in range(h0, 2) for r in range(4)]
            for g in range(2):
                for idx, (r, hf) in enumerate(wt):
                    def eS(ps, g=g, r=r, hf=hf):
                        P.mm(ps[:], kws[:, g, (r * 2 + hf) * 128:(r * 2 + hf + 1) * 128], qs[:, g, :], start=True, stop=(hf == 0))
                        if hf == 1:
                            P.mm(ps[:], identb[:], smask_s[:, r, :], start=False, stop=True)

                    def eX(pb, r=r):
                        mask_mul(pb, AP_(wmask_s, r * 128, [[1024, 128], [0, 4], [1, 128]]), [wmask_s])

                    def ePV(pb, g=g, r=r, hf=hf, idx=idx):
                        P.mm(ps_acc[g][0:65, :], vws[:, r * 2 + hf, 65 * g:65 * g + 65], pb[:], start=(idx == 0), stop=(idx == len(wt) - 1))
                    tiles.append((eS, eX if hf == 0 else None, ePV))
            run_stream(tiles)
            for g in range(2):
                for hb in range(2):
                    P.tr(pm_b[:, hb * 128:(hb + 1) * 128], nm[g][:, hb * 128:(hb + 1) * 128], identb[:])
                P.copy('vector', nmT[g][:].rearrange("p a q -> p (a q)"), pm_b[:, 0:256], reads=[pm])
            for g in range(2):
                o_epilogue(ps_acc[g], g, 2, gs)
            nkt = 4 * m + 4
            for g in range(2):
                tiles = []
                for kt in range(nkt):
                    def eS(ps, g=g, kt=kt):
                        diag = kt >= 4 * m
                        col = ((kt % 4) * 32 + kt // 4) * 128
                        P.mm(ps[:], ksT_s[g][:, col:col + 128], qs[:, g, :], start=True, stop=not diag)
                        if diag:
                            P.mm(ps[:], identb[:], smask_s[:, kt - 4 * m, :], start=False, stop=True)

                    def eX(pb, g=g, kt=kt):
                        bank = (ps_o[0], pm)[state['x'] % 2]
                        state['x'] += 1
                        P.mm(bank[:, 0:128], Eall_s[:, kt % 64, :], nmT[g][:, kt // 64, :], start=True, stop=True)
                        mask_mul(pb, AP_(bank, 0, [[512, 128], [0, 4], [1, 128]]), [bank])

                    def ePV(pb, g=g, kt=kt):
                        P.mm(ps_acc[g][0:65, :], vsx_s[:, (kt % 4) * 32 + kt // 4, 65 * g:65 * g + 65], pb[:], start=(kt == 0), stop=(kt == nkt - 1))
                    tiles.append((eS, eX, ePV))
                run_stream(tiles)
                o_epilogue(ps_acc[g], g, 1, gs)
            accf = acc[:].rearrange("p h d -> p (h d)")
            P.act(accb[:], accf, AF.Square, accum_out=ssq[:])
            P.act(rat[:], ssq[:], AF.Sqrt, bias=EPS, scale=1.0 / 512)
            P.recip(rat2[:], rat[:])
            P.dma('gpsimd', rasc[m], rat2[:])
            P.copy('gpsimd', accb[:], accf)
            if G.get('dbg'):
                P.dma('gpsimd', G['dbg']['ra'][m], rat2[:])

            def finish_out(m=m):
                for bk in range(4):
                    P.tr(pm_b[:, bk * 128:(bk + 1) * 128], accb[:, bk * 128:(bk + 1) * 128], identb[:])
                P.copy('vector', atT[:].rearrange("p b q -> p (b q)"), pm_b[:, 0:512], reads=[pm])
                P.dma('gpsimd', atsc[m], atT[:].rearrange("p b q -> p (b q)"))
                if G.get('dbg'):
                    P.dma('gpsimd', G['dbg']['at'][m], atT[:].rearrange("p b q -> p (b q)"))
            pending.append(finish_out)
        pending.pop()()


def emit_C(C, l, xsrc, xdst, W, G):
    P = C.P
    with C.phase("C%d" % l):
        Wo = C.sb("Wo", [128, 8, D], BF16)
        Wdn = C.sb("Wdn", [128, 32, D], BF16)
        wst = [C.sb("wst%d" % i, [128, 8, 256], F32) for i in range(2)]
        Wup = [C.sb("Wup%d" % i, [128, 8, 256], BF16) for i in range(2)]
        gout_s = C.sb("gout_s", [128, 8], F32)
        gffn_s = C.sb("gffn_s", [128, D], F32)
        cw_s = C.sb("cw_s", [128, 4, 3], F32)
        selw_s = C.sb("selw_s", [128, 4], F32)
        xt = [C.sb("xt%d" % i, [128, D], F32) for i in range(2)]
        x1 = [C.sb("x1_%d" % i, [128, D], F32) for i in range(4)]
        at_s = [C.sb("at_s%d" % i, [128, 4, 128], BF16) for i in range(2)]
        yv = [C.sb("yv%d" % i, [128, 4, 128], F32) for i in range(2)]
        bg2 = [C.sb("bg2_%d" % i, [128, 4, 2], F32) for i in range(2)]
        tl = [C.sb("tl%d" % i, [128, 4, 8], F32) for i in range(2)]
        halo = C.sb("halo", [128, 4, 2], F32)
        pt0 = C.sb("pt0", [128, 4], F32)
        pt1 = C.sb("pt1", [128, 4], F32)
        ysq = [C.sb("ysq%d" % i, [128, 4, 128], F32) for i in range(2)]
        ybf = [C.sb("ybf%d" % i, [128, 4, 128], BF16) for i in range(2)]
        rc = C.sb("rc", [128, 1], F32)
        rc2 = [C.sb("rc2_%d" % i, [128, 1], F32) for i in range(2)]
        ra_s = [C.sb("ra_s%d" % i, [128, 1], F32) for i in range(2)]
        ss = C.sb("ss", [128, 1], F32)
        rstd = C.sb("rstd", [128, 1], F32)
        h2n = [C.sb("h2n%d" % i, [128, D], BF16) for i in range(2)]
        h2T = C.sb("h2T", [128, 8, 512], BF16)
        rl = [C.sb("rl%d" % i, [128, 512], F32) for i in range(2)]
        actT = C.sb("actT", [128, 32, 512], BF16)
        pa = [C.ps("pa%d" % i, [128, 512], F32) for i in range(2)]
        pcv = [C.ps("pcv%d" % i, [128, 512], F32) for i in range(2)]
        pu = [C.ps("pu%d" % i, [128, 512], F32) for i in range(2)]
        pT = C.ps("pT", [128, 8, 128], BF16)
        px = C.ps("px", [128, 512], F32)
        identb, onesf = G['identb'], G['onesf']
        gathu, ysc, bg2sc, atsc, rasc = G['gathu'], G['ysc'], G['bg2sc'], G['atsc'], G['rasc']

        with P.group('cst'):
            P.dma('sync', gout_s[:], W['gout'][l])
            P.dma('sync', gffn_s[:], W['gffn'][l])
            P.dma('sync', cw_s[:], W['cw'][l])
            P.dma('sync', selw_s[:], W['selw'])
        n = 0
        for kc in range(8):
            for hf in range(2):
                st = wst[n % 2]
                n += 1
                stw = AP_(st, 0, [[2048, 128], [1, 512]])
                P.dma('sync' if hf == 0 else 'scalar', stw, W['w_o'][l, kc * 128:(kc + 1) * 128, hf * 512:(hf + 1) * 512], writes=[st])
                if hf == 0:
                    P.ts('vector', Wo[:, kc, 0:512], stw, gout_s[:, kc:kc + 1], ALU.mult, reads=[st, gout_s])
                else:
                    P.act(Wo[:, kc, 512:1024], stw, AF.Copy, scale=gout_s[:, kc:kc + 1], reads=[st, gout_s])
        for f2 in range(16):
            st = wst[n % 2]
            n += 1
            stv = AP_(st, 0, [[2048, 128], [1024, 2], [1, 1024]])
            P.dma('sync' if f2 % 2 == 0 else 'scalar', stv, W['w_dn'][l, f2 * 256:(f2 + 1) * 256, :].rearrange("(a p) n -> p a n", p=128), writes=[st])
            P.copy(('vector', 'scalar', 'vector', 'gpsimd')[f2 % 4], Wdn[:, f2 * 2:(f2 + 1) * 2, :], stv, reads=[st])
        up_n = [n]

        def S1a(m):
            xb, ab, rab, yb, bgb, tlb = xt[m % 2], at_s[m % 2], ra_s[m % 2], yv[m % 2], bg2[m % 2], tl[m % 2]
            P.dma('sync', xb[:], xsrc[m])
            P.dma('scalar', ab[:], atsc[m].rearrange("p (b q) -> p b q", b=4))
            P.dma('sync', rab[:], rasc[m])
            P.dma('scalar', yb[:], ysc[m].rearrange("p (c t) -> p c t", c=4))
            P.dma('sync', bgb[:], bg2sc[m].rearrange("p (c e) -> p c e", e=2))
            if m == 0:
                P.memset('gpsimd', tlb[:, 0, :], 0.0)
            with P.group("tl%d" % (m % 2)):
                if m > 0:
                    P.dma('gpsimd', tlb[:, 0, :], gathu[3 * B_NM + m - 1].rearrange("(p e) -> p e", e=8), writes=[tlb])
                for cd in range(1, 4):
                    P.dma('gpsimd', tlb[:, cd, :], gathu[(cd - 1) * B_NM + m].rearrange("(p e) -> p e", e=8), writes=[tlb])
            hf = halo[:].rearrange("p c e -> p (c e)")
            P.ts('vector', hf, tlb[:, 0, :], selw_s[:, 0:1], ALU.mult)
            for cd in range(1, 4):
                P.stt(hf, tlb[:, cd, :], selw_s[:, cd:cd + 1], hf, ALU.mult, ALU.add)
            P.tt('vector', pt0[:], halo[:, :, 1], cw_s[:, :, 1], ALU.mult)
            P.tt('vector', pt1[:], halo[:, :, 0], cw_s[:, :, 0], ALU.mult)
            P.tt('vector', pt0[:], pt0[:], pt1[:], ALU.add)
            P.tt('vector', pt0[:], pt0[:], bgb[:, :, 0], ALU.mult)
            P.tt('vector', yb[:, :, 0], yb[:, :, 0], pt0[:], ALU.add)
            P.tt('vector', pt1[:], halo[:, :, 1], cw_s[:, :, 0], ALU.mult)
            P.tt('vector', pt1[:], pt1[:], bgb[:, :, 1], ALU.mult)
            P.tt('vector', yb[:, :, 1], yb[:, :, 1], pt1[:], ALU.add)
            P.act(ysq[m % 2][:], yb[:], AF.Square)
            P.copy('gpsimd', ybf[m % 2][:], yb[:])

        def S1b(m, j):
            xb, ab, rab = xt[m % 2], at_s[m % 2], ra_s[m % 2]
            for ch in range(4):
                P.mm(px[:, 0:1], ysq[m % 2][:, ch, :], onesf[:], start=(ch == 0), stop=(ch == 3))
            P.act(rc[:], px[:, 0:1], AF.Sqrt, bias=EPS, scale=1.0 / 512)
            P.recip(rc2[m % 2][:], rc[:])
            for nh in range(2):
                for kc in range(4):
                    P.mm(pa[nh][:], ab[:, kc, :], Wo[:, kc, nh * 512:(nh + 1) * 512], start=(kc == 0), stop=(kc == 3))
                for kc in range(4):
                    P.mm(pcv[nh][:], ybf[m % 2][:, kc, :], Wo[:, 4 + kc, nh * 512:(nh + 1) * 512], start=(kc == 0), stop=(kc == 3))
            x1b = x1[j]
            for nh in range(2):
                sl = slice(nh * 512, (nh + 1) * 512)
                P.stt(x1b[:, sl], pa[nh][:], rab[:, 0:1], xb[:, sl], ALU.mult, ALU.add)
                P.stt(x1b[:, sl], pcv[nh][:], rc2[m % 2][:, 0:1], x1b[:, sl], ALU.mult, ALU.add)
            if G.get('dbg'):
                P.dma('gpsimd', G['dbg']['x1'][m], x1b[:])
                P.dma('gpsimd', G['dbg']['y'][m], ybf[m % 2][:].rearrange("p c t -> p (c t)"))
                P.dma('gpsimd', G['dbg']['rc'][m], rc2[m % 2][:])
            P.act(h2n[m % 2][:], x1b[:], AF.Square, accum_out=ss[:])
            P.act(rstd[:], ss[:], AF.Sqrt, bias=EPS, scale=1.0 / D)
            P.recip(rstd[:], rstd[:])
            P.stt(h2n[m % 2][:], x1b[:], rstd[:, 0:1], gffn_s[:], ALU.mult, ALU.mult)

        def S1c(m, j):
            for kc in range(8):
                P.tr(pT[:, kc, :], h2n[m % 2][:, kc * 128:(kc + 1) * 128], identb[:])
            P.copy('scalar', h2T[:, :, j * 128:(j + 1) * 128], pT[:])

        NBT = B_NM // 4
        for j in range(4):
            S1a(j)
            S1b(j, j)
            S1c(j, j)
        for bt in range(NBT):
            for u in range(16):
                st = wst[up_n[0] % 2]
                wb = Wup[up_n[0] % 2]
                up_n[0] += 1
                P.dma('sync' if u % 2 == 0 else 'scalar', st[:], W['w_up'][l, :, u * 256:(u + 1) * 256].rearrange("(kc p) n -> p kc n", p=128))
                P.copy('gpsimd' if u % 2 == 0 else 'vector', wb[:], st[:])
                for fl in range(2):
                    f = u * 2 + fl
                    pp = pu[f % 2]
                    for kc in range(8):
                        P.mm(pp[:], wb[:, kc, fl * 128:(fl + 1) * 128], h2T[:, kc, :], start=(kc == 0), stop=(kc == 7))
                    rb = rl[f % 2]
                    P.act(rb[:], pp[:], AF.Relu)
                    P.tt('gpsimd' if f % 2 == 0 else 'vector', actT[:, f, :], rb[:], rb[:], ALU.mult)
            nxt = bt + 1 < NBT
            for j in range(4):
                m = bt * 4 + j
                mn = m + 4
                if nxt:
                    S1a(mn)
                ob = x1[j]
                for nh in range(2):
                    pp = pu[nh]
                    for f in range(32):
                        P.mm(pp[:], actT[:, f, j * 128:(j + 1) * 128], Wdn[:, f, nh * 512:(nh + 1) * 512], start=(f == 0), stop=(f == 31))
                    P.tt('vector', ob[:, nh * 512:(nh + 1) * 512], pp[:], x1[j][:, nh * 512:(nh + 1) * 512], ALU.add)
                P.dma('sync', xdst[m], ob[:])
                if nxt:
                    S1b(mn, j)
                    if j >= 1:
                        S1c(mn - 1, j - 1)
            if nxt:
                S1c(bt * 4 + 7, 3)


def build_fused(nlayers=DEPTH, stages="AGZBC", dbg=False):
    C = Ctx()
    P = C.P
    W = {}
    W['x0'] = C.din("x0", [B_NM, 128, D], F32)
    W['w_in'] = C.din("w_in", [nlayers, D, INW], F32)
    W['gmix'] = C.din("gmix", [nlayers, 128, 8], F32)
    W['gq'] = C.din("gq", [nlayers, 128, 512], F32)
    W['gk3'] = C.din("gk3", [nlayers, 128, 3, 128], F32)
    W['cw'] = C.din("cw", [nlayers, 128, 4, 3], F32)
    W['peT'] = C.din("peT", [nlayers, 2, 64, 32], F32)
    W['w1'] = C.din("w1", [nlayers, 2, 2048, 256], F32)
    W['b1b'] = C.din("b1b", [nlayers, 2, 128, 256], F32)
    W['w2'] = C.din("w2", [nlayers, 2, 256, 64], F32)
    W['b2b'] = C.din("b2b", [nlayers, 128, 2, 64], F32)
    W['w_o'] = C.din("w_o", [nlayers, D, D], F32)
    W['gout'] = C.din("gout", [nlayers, 128, 8], F32)
    W['gffn'] = C.din("gffn", [nlayers, 128, D], F32)
    W['w_up'] = C.din("w_up", [nlayers, D, DFF], F32)
    W['w_dn'] = C.din("w_dn", [nlayers, DFF, D], F32)
    W['smask'] = C.din("smask", [128, 4, 512], BF16)
    W['cmask'] = C.din("cmask", [128, 5, 512], BF16)
    W['wmask'] = C.din("wmask", [128, 8, 128], BF16)
    W['Eall'] = C.din("Eall", [128, 64, 128], BF16)
    W['Bsel'] = C.din("Bsel", [128, 512], F32)
    W['kaug'] = C.din("kaug", [4, T], BF16)
    W['kcaug'] = C.din("kcaug", [4, 2, 1024], BF16)
    W['qaug'] = C.din("qaug", [B_NM, 4, 2, 512], BF16)
    W['vcxc'] = C.din("vcxc", [128, 8, 2, 257], BF16)
    W['selw'] = C.din("selw", [128, 4], F32)
    identb_d = C.din("identb", [128, 128], BF16)
    identf_d = C.din("identf", [128, 128], F32)
    onesf_d = C.din("onesf", [128, 1], F32)
    xo = C.dout("xo", [B_NM, 128, D], F32)

    G = {}
    COMPS = [('ks', 1024), ('kw', 1024), ('kc', 1024), ('vc', 1024), ('vs0', 520), ('vs1', 520), ('vw0', 520), ('vw1', 520)]
    G['pay'] = {cn: C.dint("pay_" + cn, [rows, 512], BF16) for cn, rows in COMPS}
    G['gath'] = {cn: C.dint("gath_" + cn, [4 * rows, 512], BF16) for cn, rows in COMPS}
    G['payu'] = C.dint("payu", [B_NM, 1024], F32).ap()
    G['gathu'] = C.dint("gathu", [4 * B_NM, 1024], F32).ap()
    xbuf = C.dint("xbuf", [B_NM, 128, D], F32).ap()
    G['qsc'] = C.dint("qsc", [B_NM, 2, 64, 512], BF16)
    G['gsc'] = C.dint("gsc", [B_NM, 128, 24], F32).ap()
    G['ysc'] = C.dint("ysc", [B_NM, 128, 512], F32).ap()
    G['bg2sc'] = C.dint("bg2sc", [B_NM, 128, 8], F32).ap()
    G['atsc'] = C.dint("atsc", [B_NM, 128, 512], BF16).ap()
    G['rasc'] = C.dint("rasc", [B_NM, 128, 1], F32).ap()
    qsc_ap = G['qsc'].ap()
    if dbg:
        G['dbg'] = {'at': C.dout("dbg_at", [B_NM, 128, 512], BF16), 'ra': C.dout("dbg_ra", [B_NM, 128, 1], F32),
                    'x1': C.dout("dbg_x1", [B_NM, 128, D], F32), 'y': C.dout("dbg_y", [B_NM, 128, 512], BF16),
                    'rc': C.dout("dbg_rc", [B_NM, 128, 1], F32),
                    'kc': C.dout("dbg_kc", [68, 2, 1024], BF16), 'vc': C.dout("dbg_vc", [128, 8, 2, 321], BF16),
                    'ks': C.dout("dbg_ks", [68, T], BF16), 'vs': C.dout("dbg_vs", [128, 128, 130], BF16)}

    G['identb'] = C.sb("identb_s", [128, 128], BF16)
    G['identf'] = C.sb("identf_s", [128, 128], F32)
    G['onesf'] = C.sb("onesf_s", [128, 1], F32)
    with P.group('cst'):
        P.dma('sync', G['identb'][:], identb_d)
        P.dma('sync', G['identf'][:], identf_d)
        P.dma('sync', G['onesf'][:], onesf_d)
    P.barrier()
    rg = [[0, 1, 2, 3], [4, 5, 6, 7]]
    def mk_cc(i_ap, o_ap):
        return lambda e: e.collective_compute("AllGather", ALU.bypass, replica_groups=rg, ins=[i_ap.opt()], outs=[o_ap.opt()])
    cc_fns = [mk_cc(G['pay'][cn].ap(), G['gath'][cn].ap()) for cn, _ in COMPS] + [mk_cc(G['payu'], G['gathu'])]
    for l in range(nlayers):
        xsrc = W['x0'] if l == 0 else xbuf
        xdst = xo if l == nlayers - 1 else xbuf
        if 'A' in stages:
            emit_A(C, l, xsrc, W, G)
        if 'G' in stages:
            P.collectives(cc_fns)
        with C.phase("ZB%d" % l):
            G['kcT_s'] = C.sb("kcT_s", [68, 2, 1024], BF16)
            G['vcx_s'] = C.sb("vcx_s", [128, 8, 2, 321], BF16)
            with P.group('cst2'):
                P.dma('sync', G['kcT_s'][64:68, :, :], W['kcaug'])
                P.dma('scalar', G['vcx_s'][:, :, :, 64:321], W['vcxc'])
            if 'Z' in stages:
                emit_Z(C, l, W, G)
            Gb = dict(G)
            Gb['qsc'] = qsc_ap
            if 'B' in stages:
                emit_B(C, l, W, Gb)
        if 'C' in stages:
            emit_C(C, l, xsrc, xdst, W, G)
    if 'C' not in stages:
        with C.phase("dbg"):
            xt = [C.sb("xt%d" % i, [128, D], F32) for i in range(2)]
            for m in range(B_NM):
                P.dma('sync', xt[m % 2][:], W['x0'][m])
                P.dma('sync', xo[m], xt[m % 2][:])
    return C.finish()


IDENTB = np.eye(128, dtype=np.float32).astype(NPBF)
ONESF = np.ones((128, 1), np.float32)


def _bf(a):
    return np.ascontiguousarray(a).astype(NPBF)


def _selmap():
    i = np.arange(1024)[:, None] * 16
    j = np.arange(256)[None, :] * 64
    sh = np.clip(np.minimum(i + 32, j + 64) - np.maximum(i, j), 0, None).astype(np.float32) / 32.0
    sh[1023] = 0.0
    return sh


def _pos_rows(pos):
    pos = np.maximum(pos, 0)
    return np.stack([np.ones_like(pos), np.ones_like(pos), pos % 128, pos // 128]).astype(np.float32)


def core_consts(s):
    ki = np.arange(128)[:, None]
    qi = np.arange(128)[None, :]
    tri_gt = np.where(ki > qi, NEGM, 0.0).astype(np.float32)
    tri_le = np.where(ki <= qi, NEGM, 0.0).astype(np.float32)
    full = np.full((128, 128), NEGM, np.float32)
    zero = np.zeros((128, 128), np.float32)
    smask = np.tile(np.stack([zero if d < s else (tri_gt if d == s else full) for d in range(4)], axis=1), (1, 1, 4))
    cm = []
    for v in range(5):
        ip = ki - 32 * v
        vis = (16 * ip + 31) <= (128 * s + qi)
        cm.append(np.where(vis, 0.0, NEGM).astype(np.float32))
    cmask = np.tile(np.stack(cm, axis=1), (1, 1, 4))
    wm = []
    for d in range(8):
        off = s + 4 - d
        if off == 0:
            wm.append((ki <= qi).astype(np.float32))
        elif off == 4:
            wm.append((ki > qi).astype(np.float32))
        elif 0 < off < 4:
            wm.append(np.ones((128, 128), np.float32))
        else:
            wm.append(zero)
    wmask = np.stack(wm, axis=1)
    E = np.zeros((128, 64, 128), np.float32)
    for kt in range(64):
        for k in range(128):
            E[2 * kt + k // 64, kt, k] = 1.0
    r = np.arange(512)[None, :] - 256 - 2 * s
    qq = np.arange(128)[:, None]
    Brel = np.zeros((128, 512), np.float32)
    Brel = np.where(r >= 2, np.float32(-1e30), Brel)
    Brel = np.where(r == 1, np.where(qq >= 64, np.float32(1e4), np.float32(-1e30)), Brel)
    Brel = np.where(r == 0, np.float32(1e4), Brel)
    Brel = np.where((r == -1) & (qq < 64), np.float32(1e4), Brel)
    rr, mm, kk = np.meshgrid(np.arange(4), np.arange(32), np.arange(128), indexing='ij')
    kpos = (128 * (4 * mm + rr) + kk).reshape(-1)
    kaug = _pos_rows(kpos)
    kc = _pos_rows(np.arange(1024) * 16 + 31)
    kcaug = np.stack([kc, kc], axis=1)
    qaug = np.zeros((B_NM, 4, 2, 512), np.float32)
    qv = np.arange(128)
    for m in range(B_NM):
        c = 4 * m + s
        for g in range(2):
            for rh in range(4):
                sl = 2.0 ** (-(4 * g + rh + 1))
                cs = slice(rh * 128, (rh + 1) * 128)
                qaug[m, 0, g, cs] = -sl * qv
                qaug[m, 1, g, cs] = -sl * 128.0 * c
                qaug[m, 2, g, cs] = sl
                qaug[m, 3, g, cs] = sl * 128.0
    sm = _selmap().reshape(8, 128, 256).transpose(1, 0, 2)
    vcxc = np.zeros((128, 8, 2, 257), np.float32)
    vcxc[:, :, :, 0] = 1.0
    vcxc[127, 7, :, 0] = 0.0
    vcxc[:, :, :, 1:257] = sm[:, :, None, :]
    selw = np.zeros((128, 4), np.float32)
    selw[:, s] = 1.0
    return {'smask': _bf(smask), 'cmask': _bf(cmask), 'wmask': _bf(wmask), 'Eall': _bf(E),
            'Bsel': np.ascontiguousarray(Brel.astype(np.float32)), 'kaug': _bf(kaug), 'kcaug': _bf(kcaug),
            'qaug': _bf(qaug), 'vcxc': _bf(vcxc), 'selw': selw,
            'identb': IDENTB, 'identf': np.eye(128, dtype=np.float32), 'onesf': ONESF}


def prep_fused(p, L=DEPTH):
    p = {k: (v if k == 'x' else v[:L]) for k, v in p.items()}
    common = {
        'w_in': np.ascontiguousarray(p['w_in'], dtype=np.float32),
        'gmix': np.ascontiguousarray(p['g_mix_norm'].reshape(L, 8, 128).transpose(0, 2, 1)),
        'gq': np.ascontiguousarray(np.tile(p['g_q'][:, None, :], (1, 128, 8))),
        'gk3': np.ascontiguousarray(np.broadcast_to(np.tile(p['g_k'], (1, 1, 2))[:, None], (L, 128, 3, 128))),
        'cw': np.ascontiguousarray(p['conv_w'].reshape(L, 3, 4, 128).transpose(0, 3, 2, 1)),
        'peT': np.ascontiguousarray(p['pe_cmp'].transpose(0, 1, 3, 2)),
        'w1': np.ascontiguousarray(p['w_cmp1'], dtype=np.float32),
        'b1b': np.ascontiguousarray(np.broadcast_to(p['b_cmp1'][:, :, None, :], (L, 2, 128, 256))),
        'w2': np.ascontiguousarray(p['w_cmp2'], dtype=np.float32),
        'b2b': np.ascontiguousarray(np.broadcast_to(p['b_cmp2'][:, None], (L, 128, 2, 64))),
        'w_o': np.ascontiguousarray(p['w_o'], dtype=np.float32),
        'gout': np.ascontiguousarray(p['g_out'].reshape(L, 8, 128).transpose(0, 2, 1)),
        'gffn': np.ascontiguousarray(np.tile(p['g_ffn_norm'][:, None, :], (1, 128, 1))),
        'w_up': np.ascontiguousarray(p['w_up'], dtype=np.float32),
        'w_dn': np.ascontiguousarray(p['w_down'], dtype=np.float32),
    }
    maps = []
    x = np.ascontiguousarray(p['x'], dtype=np.float32)
    for b in range(NB):
        xv = x[b].reshape(T // 128, 128, D)
        for s in range(4):
            mp = dict(common)
            mp.update(core_consts(s))
            mp['x0'] = np.ascontiguousarray(xv[np.arange(B_NM) * 4 + s])
            maps.append(mp)
    return maps


_PROG = {}


def run_fused(inputs, nlayers=DEPTH):
    p = {k: np.asarray(v) for k, v in inputs.items()}
    if nlayers not in _PROG:
        _PROG[nlayers] = build_fused(nlayers)
    res = run_bass_kernel_spmd(_PROG[nlayers], prep_fused(p, nlayers), core_ids=list(range(NCORES)))
    out = np.empty((NB, T, D), np.float32)
    for b in range(NB):
        ov = out[b].reshape(T // 128, 128, D)
        for s in range(4):
            ov[np.arange(B_NM) * 4 + s] = np.asarray(res.results[4 * b + s]['xo'])
    return out


def kernel(**inputs):
    return run_fused(inputs, DEPTH)
```

```python
import contextlib
import numpy as np
import ml_dtypes
import concourse.bass as bass
import concourse.mybir as mybir
from concourse.bass_utils import run_bass_kernel_spmd

F32 = mybir.dt.float32
BF16 = mybir.dt.bfloat16
AF = mybir.ActivationFunctionType
ALU = mybir.AluOpType
AX = mybir.AxisListType
NPBF = ml_dtypes.bfloat16

ENGS = ['tensor', 'vector', 'scalar', 'gpsimd', 'sync']
NCORES = 8
D = 1024
T = 16384
NB = 2
DEPTH = 4
INW = 2840
EPS = 1e-6
NEGM = -30000.0


def key_of(ap):
    t = getattr(ap, 'tensor', ap)
    return t.name.split('@')[0]


class Prog:
    def __init__(self, nc):
        self.nc = nc
        self.q = {e: [] for e in ENGS}
        self.semcount = {}
        self.res = {}
        self.seen = {e: {} for e in ENGS}

    grp = None

    @contextlib.contextmanager
    def group(self, sem):
        s = 'd_' + sem
        self.grp = {'sem': sem, 's': s, 'start': self.semcount.get(s, 0), 'keys': set(), 'pre': {}}
        try:
            yield
        finally:
            g, self.grp = self.grp, None
            v = self.semcount.get(s, 0)
            for k in g['keys']:
                self.res[k]['w'] = (s, v)

    def _need(self, eng, reads, writes):
        need = {}

        def add(tok):
            if tok is None:
                return
            s, v = tok
            if eng == 'tensor' and s == 'e_tensor':
                return
            if self.grp is not None and s == self.grp['s'] and v > self.grp['start']:
                return
            if need.get(s, 0) < v:
                need[s] = v
        for k in reads:
            st = self.res.get(k)
            if st:
                add(st['w'])
        for k in writes:
            sts = [self.res.get(k)]
            if self.grp is not None:
                if k not in self.grp['pre']:
                    st0 = self.res.get(k)
                    self.grp['pre'][k] = {'w': st0['w'], 'r': dict(st0['r'])} if st0 else None
                sts.append(self.grp['pre'][k])
            for st in sts:
                if st:
                    add(st['w'])
                    for s, v in st['r'].items():
                        add((s, v))
        for s, v in need.items():
            if self.seen[eng].get(s, 0) < v:
                self.q[eng].append(('wait', s, v))
                self.seen[eng][s] = v

    def _update(self, reads, writes, tok):
        s, v = tok
        for k in reads:
            st = self.res.setdefault(k, {'w': None, 'r': {}})
            if st['r'].get(s, 0) < v:
                st['r'][s] = v
        for k in writes:
            self.res[k] = {'w': tok, 'r': {}}

    def op(self, eng, fn, reads=(), writes=()):
        reads = [r if isinstance(r, str) else key_of(r) for r in reads]
        writes = [r if isinstance(r, str) else key_of(r) for r in writes]
        self._need(eng, reads, writes)
        s = 'e_' + eng
        v = self.semcount.get(s, 0) + 1
        self.semcount[s] = v
        self.q[eng].append(('op', fn, s, 1))
        self._update(reads, writes, (s, v))

    def dma(self, eng, out, in_, reads=None, writes=None, sem=None, **kw):
        if reads is None:
            reads = [] if in_.tensor.name in self.dram_names else [in_]
        if writes is None:
            writes = [] if out.tensor.name in self.dram_names else [out]
        assert reads or writes or sem
        reads = [r if isinstance(r, str) else key_of(r) for r in reads]
        writes = [r if isinstance(r, str) else key_of(r) for r in writes]
        if self.grp is not None:
            sem = self.grp['sem']
            self.grp['keys'].update(writes)
        if sem is None:
            sem = writes[0] if writes else 'st_' + reads[0]
        self._need(eng, reads, writes)
        s = 'd_' + sem
        v = self.semcount.get(s, 0) + 16
        self.semcount[s] = v
        self.q[eng].append(('op', lambda e: e.dma_start(out=out, in_=in_, **kw), s, 16))
        self._update(reads, writes, (s, v))

    dram_names = set()

    def mm(self, out, lhsT, rhs, start=True, stop=True, reads=None, writes=None):
        self.op('tensor', lambda e: e.matmul(out, lhsT=lhsT, rhs=rhs, start=start, stop=stop),
                reads=reads if reads is not None else [lhsT, rhs],
                writes=writes if writes is not None else [out])

    def tr(self, out, in_, ident, reads=None, writes=None):
        self.op('tensor', lambda e: e.transpose(out, in_, ident),
                reads=reads if reads is not None else [in_, ident],
                writes=writes if writes is not None else [out])

    def act(self, out, in_, func, bias=None, scale=None, accum_out=None, reads=None, writes=None, eng='scalar'):
        kw = {}
        rd = [in_]
        if bias is not None:
            kw['bias'] = bias
            if not isinstance(bias, (int, float)):
                rd.append(bias)
        if scale is not None:
            kw['scale'] = scale
            if not isinstance(scale, (int, float)):
                rd.append(scale)
        wr = [out]
        if accum_out is not None:
            kw['accum_out'] = accum_out
            wr.append(accum_out)
        self.op('scalar', lambda e: e.activation(out=out, in_=in_, func=func, **kw),
                reads=reads if reads is not None else rd,
                writes=writes if writes is not None else wr)

    def tt(self, eng, out, in0, in1, op, reads=None, writes=None):
        self.op(eng, lambda e: e.tensor_tensor(out=out, in0=in0, in1=in1, op=op),
                reads=reads if reads is not None else [in0, in1],
                writes=writes if writes is not None else [out])

    def ts(self, eng, out, in0, s1, op0, s2=None, op1=None, reads=None, writes=None):
        rd = [in0]
        if not isinstance(s1, (int, float)):
            rd.append(s1)
        if s2 is not None and not isinstance(s2, (int, float)):
            rd.append(s2)
        if op1 is None:
            fn = lambda e: e.tensor_scalar(out=out, in0=in0, scalar1=s1, scalar2=None, op0=op0)
        else:
            fn = lambda e: e.tensor_scalar(out=out, in0=in0, scalar1=s1, scalar2=s2, op0=op0, op1=op1)
        self.op(eng, fn, reads=reads if reads is not None else rd,
                writes=writes if writes is not None else [out])

    def stt(self, out, in0, scalar, in1, op0, op1, reads=None, writes=None):
        rd = [in0, in1]
        if not isinstance(scalar, (int, float)):
            rd.append(scalar)
        self.op('vector', lambda e: e.scalar_tensor_tensor(out=out, in0=in0, scalar=scalar, in1=in1, op0=op0, op1=op1),
                reads=reads if reads is not None else rd,
                writes=writes if writes is not None else [out])

    def copy(self, eng, out, in_, reads=None, writes=None):
        if eng == 'scalar':
            fn = lambda e: e.activation(out=out, in_=in_, func=AF.Copy)
        else:
            fn = lambda e: e.tensor_copy(out=out, in_=in_)
        self.op(eng, fn, reads=reads if reads is not None else [in_],
                writes=writes if writes is not None else [out])

    def recip(self, out, in_):
        self.op('vector', lambda e: e.reciprocal(out=out, in_=in_), reads=[in_], writes=[out])

    def reduce(self, out, in_, op=ALU.add, axis=AX.X):
        self.op('vector', lambda e: e.tensor_reduce(out=out, in_=in_, axis=axis, op=op), reads=[in_], writes=[out])

    def memset(self, eng, ap, val):
        self.op(eng, lambda e: e.memset(ap, val), writes=[ap])

    def barrier(self):
        for e in ENGS:
            for s, v in self.semcount.items():
                if self.seen[e].get(s, 0) < v:
                    self.q[e].append(('wait', s, v))
                    self.seen[e][s] = v
        self.res = {}

    def collectives(self, fns):
        self.barrier()
        s = 'c_cc'
        for fn in fns:
            v = self.semcount.get(s, 0) + 1
            self.semcount[s] = v
            self.q['gpsimd'].append(('op', fn, s, 1))
        self.barrier()

    def emit(self):
        nc = self.nc
        for s, v in self.semcount.items():
            if self.seen['sync'].get(s, 0) < v:
                self.q['sync'].append(('wait', s, v))
        with contextlib.ExitStack() as es:
            semh = {s: es.enter_context(nc.semaphore(s)) for s in self.semcount}
            block = es.enter_context(nc.Block())

            def mk(engname):
                def body(e):
                    for it in self.q[engname]:
                        if it[0] == 'wait':
                            e.wait_ge(semh[it[1]], it[2])
                        else:
                            ins = it[1](e)
                            ins.then_inc(semh[it[2]], it[3])
                return body
            for engname in ENGS:
                if self.q[engname]:
                    getattr(block, engname)(mk(engname))


def AP_(t, offset, dims):
    return bass.AP(t, offset, [list(d) for d in dims])


class Ctx:
    def __init__(self):
        self.nc = bass.Bass("TRN2", target_bir_lowering=False)
        self.es = contextlib.ExitStack()
        self.P = Prog(self.nc)
        self.P.dram_names = set()

    def din(self, name, shape, dt):
        self.P.dram_names.add(name)
        return self.nc.dram_tensor(name, list(shape), dt, kind="ExternalInput").ap()

    def dout(self, name, shape, dt):
        self.P.dram_names.add(name)
        return self.nc.dram_tensor(name, list(shape), dt, kind="ExternalOutput").ap()

    tag = None

    def dint(self, name, shape, dt):
        self.P.dram_names.add(name)
        return self.nc.dram_tensor(name, list(shape), dt)

    def sb(self, name, shape, dt):
        if self.tag:
            name = name + '@' + self.tag
        return self.es.enter_context(self.nc.sbuf_tensor(name, list(shape), dt))

    def ps(self, name, shape, dt):
        if self.tag:
            name = name + '@' + self.tag
        return self.es.enter_context(self.nc.psum_tensor(name, list(shape), dt))

    @contextlib.contextmanager
    def phase(self, tag):
        old, oldtag = self.es, self.tag
        self.es, self.tag = contextlib.ExitStack(), tag
        try:
            yield
        finally:
            self.P.barrier()
            self.es.close()
            self.es, self.tag = old, oldtag

    def finish(self):
        self.P.emit()
        self.es.close()
        return self.nc


B_NM = 32
DFF = 4096
PAYR = 6176
KS_R0, KW_R0, KC_R0, VC_R0, VS_R0, VW_R0 = 0, 1024, 2048, 3072, 4096, 5136
TP = T + 128


def emit_A(C, l, xsrc, W, G):
    P = C.P
    with C.phase("A%d" % l):
        Wbf = C.sb("Wbf", [128, 8, INW], BF16)
        wst = [C.sb("wst%d" % i, [128, INW // 2], F32) for i in range(4)]
        gmix_s = C.sb("gmix_s", [128, 8], F32)
        gq_s = C.sb("gq_s", [128, 512], F32)
        gk3_s = C.sb("gk3_s", [128, 3, 128], F32)
        cw_s = C.sb("cw_s", [128, 4, 3], F32)
        xt = [C.sb("xt%d" % i, [128, D], F32) for i in range(3)]
        junk = C.sb("junk", [128, D], BF16)
        ss = [C.sb("ss%d" % i, [128, 1], F32) for i in range(2)]
        rstd = [C.sb("rstd%d" % i, [128, 1], F32) for i in range(2)]
        hn = [C.sb("hn%d" % i, [128, D], BF16) for i in range(2)]
        hT = [C.sb("hT%d" % i, [128, 8, 128], BF16) for i in range(2)]
        zs = [C.sb("zs%d" % i, [128, 1304], F32) for i in range(2)]
        hcs = C.sb("hcs", [128, 4, 128], F32)
        ub = [C.sb("ub%d" % i, [128, 4, 130], F32) for i in range(2)]
        bgs = [C.sb("bgs%d" % i, [128, 4, 128], F32) for i in range(2)]
        sq = C.sb("sq", [128, 1280], F32)
        ssg = C.sb("ssg", [128, 20], F32)
        rg = C.sb("rg", [128, 20], F32)
        qtmp = C.sb("qtmp", [128, 512], F32)
        ktmp = C.sb("ktmp", [128, 256], F32)
        nrm = C.sb("nrm", [128, 1024], BF16)
        trs = C.sb("trs", [128, 8, 128], BF16)
        vsw = C.sb("vsw", [128, 2, 130], BF16)
        gts = C.sb("gts", [128, 24], F32)
        cv0 = C.sb("cv0", [128, 4, 128], F32)
        cv1 = C.sb("cv1", [128, 4, 128], F32)
        yv = C.sb("yv", [128, 4, 128], F32)
        ut = C.sb("ut", [128, 4, 2], F32)
        bg2 = C.sb("bg2", [128, 4, 2], F32)
        pTa = C.ps("pTa", [128, 8, 128], BF16)
        pTb = C.ps("pTb", [128, 8, 128], BF16)
        pz = [C.ps("pz%d" % i, [128, 512], F32) for i in range(3)]
        pc = [C.ps("pc%d" % i, [128, 4, 128], F32) for i in range(3)]
        identb = G['identb']

        with P.group('cst'):
            P.dma('sync', gmix_s[:], W['gmix'][l])
            P.dma('sync', gq_s[:], W['gq'][l])
            P.dma('sync', gk3_s[:], W['gk3'][l])
            P.dma('sync', cw_s[:], W['cw'][l])
        P.memset('gpsimd', vsw[:], 1.0)
        P.memset('gpsimd', ub[0][:], 0.0)
        P.memset('gpsimd', ub[1][:], 0.0)
        for kc in range(8):
            for hf in range(2):
                st = wst[hf + 2 * (kc % 2)]
                P.dma('sync' if hf == 0 else 'scalar', st[:], W['w_in'][l, kc * 128:(kc + 1) * 128, hf * 1420:(hf + 1) * 1420])
                if hf == 0:
                    P.ts('vector', Wbf[:, kc, 0:1420], st[:], gmix_s[:, kc:kc + 1], ALU.mult)
                else:
                    P.act(Wbf[:, kc, 1420:2840], st[:], AF.Copy, scale=gmix_s[:, kc:kc + 1])

        pay, payu, qsc, gsc, ysc, bg2sc = G['pay'], G['payu'], G['qsc'], G['gsc'], G['ysc'], G['bg2sc']

        def X_load(m):
            P.dma('gpsimd', xt[m % 3][:], xsrc[m])

        def F_pre(m):
            xb = xt[m % 3]
            P.act(junk[:], xb[:], AF.Square, accum_out=ss[m % 2][:])
            P.act(rstd[m % 2][:], ss[m % 2][:], AF.Sqrt, bias=EPS, scale=1.0 / D)
            P.recip(rstd[m % 2][:], rstd[m % 2][:])
            P.act(hn[m % 2][:], xb[:], AF.Copy, scale=rstd[m % 2][:, 0:1])

        def F_pe_a(m):
            for kc in range(8):
                P.tr(pTa[:, kc, :], hn[m % 2][:, kc * 128:(kc + 1) * 128], identb[:])
            P.copy('vector', hT[m % 2][:], pTa[:])

        def F_pe_b(m):
            hTb, zb, ubb, bgb = hT[m % 2], zs[m % 2], ub[m % 2], bgs[m % 2]
            for bi, (c0, c1) in enumerate([(0, 512), (512, 1024), (1024, 1304)]):
                for kc in range(8):
                    P.mm(pz[bi][:, 0:c1 - c0], hTb[:, kc, :], Wbf[:, kc, c0:c1], start=(kc == 0), stop=(kc == 7))
                P.copy('scalar', zb[:, c0:c1], pz[bi][:, 0:c1 - c0])
            for ch in range(12):
                for kc in range(8):
                    P.mm(pc[ch // 4][:, ch % 4, :], Wbf[:, kc, 1304 + ch * 128:1304 + (ch + 1) * 128], hTb[:, kc, :],
                         start=(kc == 0), stop=(kc == 7))
                if ch == 3:
                    P.copy('scalar', hcs[:], pc[0][:])
                if ch == 7:
                    P.tt('vector', ubb[:, :, 2:130], pc[1][:], hcs[:], ALU.mult)
            P.copy('vector', bgb[:], pc[2][:])

        def G1(m):
            zb, ubb, bgb = zs[m % 2], ub[m % 2], bgs[m % 2]
            P.act(sq[:], zb[:, 0:1280], AF.Square)
            P.reduce(ssg[:], sq[:].rearrange("p (g d) -> p g d", d=64))
            P.act(rg[:, 0:8], ssg[:, 0:8], AF.Sqrt, bias=64 * EPS, scale=1.0)
            P.act(rg[:, 8:20], ssg[:, 8:20], AF.Sqrt, bias=EPS, scale=1.0 / 64)
            P.recip(rg[:], rg[:])
            P.tt('vector', qtmp[:].rearrange("p (g d) -> p g d", d=64), zb[:, 0:512].rearrange("p (g d) -> p g d", d=64),
                 AP_(rg, 0, [[20, 128], [1, 8], [0, 64]]), ALU.mult)
            P.tt('gpsimd', nrm[:, 0:512], qtmp[:], gq_s[:], ALU.mult)
            P.tt('vector', ktmp[:, 0:128].rearrange("p (g d) -> p g d", d=64), zb[:, 768:896].rearrange("p (g d) -> p g d", d=64),
                 AP_(rg, 12, [[20, 128], [1, 2], [0, 64]]), ALU.mult)
            P.tt('vector', ktmp[:, 128:256].rearrange("p (g d) -> p g d", d=64), zb[:, 1024:1152].rearrange("p (g d) -> p g d", d=64),
                 AP_(rg, 16, [[20, 128], [1, 2], [0, 64]]), ALU.mult)
            P.tt('gpsimd', nrm[:, 512:768], ktmp[:], gk3_s[:, 1:3, :].rearrange("p a d -> p (a d)"), ALU.mult)
            P.copy('gpsimd', nrm[:, 768:1024], zb[:, 512:768])
            P.copy('gpsimd', AP_(vsw, 1, [[260, 128], [65, 2], [1, 64]]), zb[:, 896:1024].rearrange("p (g d) -> p g d", d=64))
            P.copy('gpsimd', AP_(vsw, 131, [[260, 128], [65, 2], [1, 64]]), zb[:, 1152:1280].rearrange("p (g d) -> p g d", d=64))
            P.dma('scalar', AP_(pay['vs%d' % (m // 16)], (m % 16) * 128 * 130, [[130, 128], [1, 130]]), vsw[:, 0, :])
            P.dma('scalar', AP_(pay['vw%d' % (m // 16)], (m % 16) * 128 * 130, [[130, 128], [1, 130]]), vsw[:, 1, :])
            P.act(gts[:], zb[:, 1280:1304], AF.Sigmoid)
            P.dma('scalar', gsc[m], gts[:])
            P.tt('gpsimd', cv0[:], ubb[:, :, 2:130], AP_(cw_s, 2, [[12, 128], [3, 4], [0, 128]]), ALU.mult)
            P.tt('gpsimd', cv1[:], ubb[:, :, 1:129], AP_(cw_s, 1, [[12, 128], [3, 4], [0, 128]]), ALU.mult)
            P.tt('gpsimd', cv0[:], cv0[:], cv1[:], ALU.add)
            P.tt('gpsimd', cv1[:], ubb[:, :, 0:128], AP_(cw_s, 0, [[12, 128], [3, 4], [0, 128]]), ALU.mult)
            P.tt('gpsimd', cv0[:], cv0[:], cv1[:], ALU.add)
            P.tt('vector', yv[:], bgb[:], cv0[:], ALU.mult)
            P.dma('scalar', ysc[m].rearrange("p (c t) -> p c t", c=4), yv[:])
            P.copy('gpsimd', ut[:], ubb[:, :, 128:130])
            P.dma('sync', payu[m].rearrange("(p e) -> p e", e=8), ut[:].rearrange("p c e -> p (c e)"))
            P.copy('gpsimd', bg2[:], bgb[:, :, 0:2])
            P.dma('sync', bg2sc[m], bg2[:].rearrange("p c e -> p (c e)"))

        def G2(m):
            for bk in range(8):
                P.tr(pTb[:, bk, :], nrm[:, bk * 128:(bk + 1) * 128], identb[:])
            P.copy('vector', trs[:], pTb[:])
            for e in range(2):
                for g in range(2):
                    P.dma('sync' if g == 0 else 'scalar', AP_(qsc, m * 65536 + g * 32768 + e * 128, [[512, 64], [256, 2], [1, 128]]),
                          trs[e * 64:(e + 1) * 64, 2 * g:2 * g + 2, :])
            for ci, cn in enumerate(('ks', 'kw', 'kc', 'vc')):
                P.dma('sync' if ci % 2 == 0 else 'scalar', AP_(pay[cn], m * 128, [[4096, 128], [1, 128]]), trs[:, 4 + ci, :])

        X_load(0)
        X_load(1)
        X_load(2)
        F_pre(0)
        F_pre(1)
        F_pe_a(0)
        F_pe_b(0)
        for m in range(B_NM):
            if m + 3 < B_NM:
                X_load(m + 3)
            if m + 1 < B_NM:
                F_pe_a(m + 1)
            if m + 2 < B_NM:
                F_pre(m + 2)
            G1(m)
            if m + 1 < B_NM:
                F_pe_b(m + 1)
            G2(m)


def emit_Z(C, l, W, G):
    P = C.P
    with C.phase("Z%d" % l):
        kcT_all = C.sb("kcT_all", [128, TP], BF16)
        vcT_all = C.sb("vcT_all", [128, TP], BF16)
        W1bf = C.sb("W1bf", [128, 2, 32, 256], BF16)
        w1st = [C.sb("w1st%d" % i, [128, 8, 256], F32) for i in range(2)]
        W2bf = C.sb("W2bf", [128, 2, 2, 64], BF16)
        w2st = C.sb("w2st", [128, 2, 2, 64], F32)
        peT_s = C.sb("peT_s", [64, 2, 32], F32)
        pebf = C.sb("pebf", [64, 2, 32, 128], BF16)
        b1b_s = C.sb("b1b_s", [128, 2, 256], F32)
        b2b_s = C.sb("b2b_s", [128, 2, 64], F32)
        gkc = C.sb("gkc", [128, 128], F32)
        c1b = C.sb("c1b", [128, 2, 256], F32)
        hid = C.sb("hid", [128, 256], F32)
        g_x2 = C.sb("g_x2", [128, 256], F32)
        g_in = C.sb("g_in", [128, 256], F32)
        g_sg = C.sb("g_sg", [128, 256], F32)
        hbf = C.sb("hbf", [128, 256], BF16)
        hidT = C.sb("hidT", [128, 2, 128], BF16)
        co = C.sb("co", [128, 2, 2, 64], F32)
        cosq = C.sb("cosq", [128, 128], F32)
        css = C.sb("css", [128, 2], F32)
        crs = C.sb("crs", [128, 2], F32)
        kcn = C.sb("kcn", [128, 128], BF16)
        pT = C.ps("pT", [128, 8, 128], BF16)
        px = [C.ps("px%d" % i, [128, 512], F32) for i in range(2)]
        po = C.ps("po", [128, 512], F32)
        identb = G['identb']
        gath = G['gath']
        kcT_s, vcx_s = G['kcT_s'], G['vcx_s']

        with P.group('cst'):
            P.dma('sync', b1b_s[:], W['b1b'][l].rearrange("k p n -> p k n"))
            P.dma('sync', b2b_s[:], W['b2b'][l])
            P.dma('sync', peT_s[:], W['peT'][l].rearrange("k d j -> d k j"))
            P.dma('sync', w2st[:], W['w2'][l].rearrange("k (c p) d -> p k c d", p=128))
            P.dma('sync', gkc[:], W['gk3'][l, :, 0, :])
        P.memset('gpsimd', kcT_all[:, T:TP], 0.0)
        P.memset('gpsimd', vcT_all[:, T:TP], 0.0)
        with P.group('kvcl'):
            for r in range(4):
                for kv, dst in enumerate((kcT_all, vcT_all)):
                    P.dma('sync' if kv == 0 else 'scalar', AP_(dst, r * 128, [[TP, 128], [512, 32], [1, 128]]),
                          AP_(gath['kc' if kv == 0 else 'vc'], r * 1024 * 512, [[4096, 128], [128, 32], [1, 128]]),
                          writes=[dst])
        P.copy('vector', W2bf[:], w2st[:])
        P.copy('vector', pebf[:], AP_(peT_s, 0, [[64, 64], [32, 2], [1, 32], [0, 128]]))
        n = 0
        for kv in range(2):
            for jq in range(4):
                st = w1st[n % 2]
                n += 1
                src = W['w1'][l, kv, jq * 512:(jq + 1) * 512, :].rearrange("(j d) n -> d j n", d=64)
                with P.group("w1st%d" % ((n - 1) % 2)):
                    P.dma('sync', st[0:64, :, :], src, writes=[st])
                    P.dma('scalar', st[64:128, :, :], src, writes=[st])
                P.copy('vector' if jq % 2 == 0 else 'gpsimd', W1bf[:, kv, jq * 8:(jq + 1) * 8, :], st[:])
        for kv in range(2):
            for j in range(32):
                P.mm(px[0][:, 0:256], pebf[:, kv, j, :], W1bf[0:64, kv, j, :], start=(j == 0), stop=(j == 31))
            P.tt('vector', c1b[:, kv, :], px[0][:, 0:256], b1b_s[:, kv, :], ALU.add)
        n = 0
        for ib in range(8):
            for kv in range(2):
                src_all = kcT_all if kv == 0 else vcT_all
                for g in range(2):
                    pp = px[n % 2]
                    n += 1
                    base = 16 * 128 * ib
                    for j in range(32):
                        lhs = AP_(src_all, 64 * g * TP + base + j, [[TP, 64], [16, 128]])
                        P.mm(pp[:, 0:256], lhs, W1bf[64 * g:64 * g + 64, kv, j, :], start=(j == 0), stop=(j == 31),
                             reads=[src_all, W1bf])
                    P.tt('vector', hid[:], pp[:, 0:256], c1b[:, kv, :], ALU.add)
                    P.act(g_x2[:], hid[:], AF.Square)
                    P.ts('gpsimd', g_x2[:], g_x2[:], 0.044715, ALU.mult, 1.0, ALU.add)
                    P.tt('gpsimd', g_in[:], g_x2[:], hid[:], ALU.mult)
                    P.act(g_sg[:], g_in[:], AF.Sigmoid, scale=1.5957691216057308)
                    P.tt('vector', hbf[:], hid[:], g_sg[:], ALU.mult)
                    for c in range(2):
                        P.tr(pT[:, c, :], hbf[:, c * 128:(c + 1) * 128], identb[:])
                    P.copy('vector', hidT[:], pT[:, 0:2, :])
                    for c in range(2):
                        P.mm(po[:, 0:64], hidT[:, c, :], W2bf[:, kv, c, :], start=(c == 0), stop=(c == 1))
                    P.tt('vector', co[:, kv, g, :], po[:, 0:64], b2b_s[:, kv, :], ALU.add)
            P.act(cosq[:], co[:, 0, :, :].rearrange("p g d -> p (g d)"), AF.Square)
            P.reduce(css[:], cosq[:].rearrange("p (g d) -> p g d", d=64))
            P.act(crs[:], css[:], AF.Sqrt, bias=EPS, scale=1.0 / 64)
            P.recip(crs[:], crs[:])
            P.tt('vector', cosq[:].rearrange("p (g d) -> p g d", d=64), co[:, 0, :, :], AP_(crs, 0, [[2, 128], [1, 2], [0, 64]]), ALU.mult)
            P.tt('vector', kcn[:], cosq[:], gkc[:], ALU.mult)
            for g in range(2):
                P.tr(pT[0:64, 4 + g, :], kcn[:, 64 * g:64 * g + 64], identb[:])
            P.copy('vector', kcT_s[0:64, :, ib * 128:(ib + 1) * 128], pT[0:64, 4:6, :])
            P.copy('gpsimd', vcx_s[:, ib, :, 0:64], co[:, 1, :, :])
        if G.get('dbg'):
            P.dma('sync', G['dbg']['kc'], kcT_s[:])
            P.dma('sync', G['dbg']['vc'], vcx_s[:])


def emit_B(C, l, W, G):
    P = C.P
    with C.phase("B%d" % l):
        ksT_s = [C.sb("ksT_s%d" % g, [68, T], BF16) for g in range(2)]
        vsx_s = C.sb("vsx_s", [128, 128, 130], BF16)
        smask_s = C.sb("smask_s", [128, 4, 512], BF16)
        cmask_s = C.sb("cmask_s", [128, 5, 512], BF16)
        wmask_s = C.sb("wmask_s", [128, 8, 128], BF16)
        Eall_s = C.sb("Eall_s", [128, 64, 128], BF16)
        Bsel_s = C.sb("Bsel_s", [128, 512], F32)
        qT_s = [C.sb("qT_s%d" % i, [68, 2, 512], BF16) for i in range(2)]
        kwT_s = [C.sb("kwT_s%d" % i, [68, 2, 1024], BF16) for i in range(2)]
        vwx_s = [C.sb("vwx_s%d" % i, [128, 8, 130], BF16) for i in range(2)]
        gates_s = [C.sb("gates_s%d" % i, [128, 24], F32) for i in range(2)]
        NPB = 6
        Pb = [C.sb("Pb%d" % i, [128, 512], BF16) for i in range(NPB)]
        Pc = [C.sb("Pc%d" % g, [128, 8, 512], BF16) for g in range(2)]
        oc = C.sb("oc", [128, 4, 321], F32)
        den4 = C.sb("den4", [128, 4], F32)
        rd4 = C.sb("rd4", [128, 4], F32)
        coef4 = C.sb("coef4", [128, 4], F32)
        imp = [C.sb("imp%d" % g, [128, 256], F32) for g in range(2)]
        tmpi = C.sb("tmpi", [128, 256], F32)
        m8 = C.sb("m8", [128, 16], F32)
        thr = C.sb("thr", [128, 1], F32)
        nm = [C.sb("nm%d" % g, [128, 256], BF16) for g in range(2)]
        nmT = [C.sb("nmT%d" % g, [128, 2, 128], BF16) for g in range(2)]
        osT = C.sb("osT", [65, 512], F32)
        otmp = C.sb("otmp", [128, 4, 64], F32)
        acc = C.sb("acc", [128, 8, 64], F32)
        accb = C.sb("accb", [128, 512], BF16)
        ssq = C.sb("ssq", [128, 1], F32)
        rat = C.sb("rat", [128, 1], F32)
        rat2 = C.sb("rat2", [128, 1], F32)
        atT = C.sb("atT", [128, 4, 128], BF16)
        ps_s = [C.ps("ps_s%d" % i, [128, 512], F32) for i in range(4)]
        ps_o = [C.ps("ps_o%d" % i, [128, 512], F32) for i in range(1)]
        ps_acc = [C.ps("ps_acc%d" % i, [128, 512], F32) for i in range(2)]
        pm = C.ps("pm", [128, 512], F32)
        pm_b = pm[:].bitcast(BF16)
        identb, identf = G['identb'], G['identf']
        kcT_s, vcx_s = G['kcT_s'], G['vcx_s']
        gath, qsc, gsc, atsc, rasc = G['gath'], G['qsc'], G['gsc'], G['atsc'], G['rasc']

        with P.group('cst'):
            P.dma('sync', cmask_s[:], W['cmask'])
            P.dma('sync', Bsel_s[:], W['Bsel'])
            P.dma('scalar', wmask_s[:], W['wmask'])
            P.dma('scalar', smask_s[:], W['smask'])
            P.dma('scalar', Eall_s[:], W['Eall'])
        with P.group('ksT'):
            for g in range(2):
                P.dma('gpsimd', ksT_s[g][64:68, :], W['kaug'], writes=[ksT_s[g]])
                for r in range(4):
                    P.dma('sync' if r % 2 == 0 else 'scalar', ksT_s[g][0:64, r * 4096:(r + 1) * 4096],
                          AP_(gath['ks'], r * 1024 * 512 + 64 * g * 4096, [[4096, 64], [1, 4096]]),
                          writes=[ksT_s[g]])
        with P.group('vsx'):
            for r in range(4):
                for hv in range(2):
                    P.dma('gpsimd', vsx_s[:, r * 32 + hv * 16:r * 32 + hv * 16 + 16, :],
                          AP_(gath['vs%d' % hv], r * 520 * 512, [[130, 128], [128 * 130, 16], [1, 130]]), writes=[vsx_s])

        if G.get('dbg'):
            P.dma('sync', G['dbg']['ks'], ksT_s[1][:])
            P.dma('sync', G['dbg']['vs'], vsx_s[:])
        state = {'s': 0, 'p': 0, 'x': 0}
        pending = []

        LA = 4

        def run_stream(tiles):
            n = len(tiles)
            bufs = []
            for i in range(n + LA):
                if i < n:
                    ps = ps_s[state['s'] % 4]
                    pb = Pb[state['p'] % NPB]
                    state['s'] += 1
                    state['p'] += 1
                    tiles[i][0](ps)
                    P.act(pb[:], ps[:], AF.Exp)
                    bufs.append(pb)
                j = i - (LA - 2)
                if 0 <= j < n and tiles[j][1] is not None:
                    tiles[j][1](bufs[j])
                if i - LA >= 0:
                    tiles[i - LA][2](bufs[i - LA])

        def mask_mul(pb, map_, mkeys):
            P.tt('vector', pb[:].rearrange("p (a q) -> p a q", a=4), pb[:].rearrange("p (a q) -> p a q", a=4), map_,
                 ALU.mult, reads=[pb] + mkeys, writes=[pb])

        def o_epilogue(psacc, g, br, gs):
            P.copy('vector', osT[:], psacc[0:65, :])
            pmv = pm[:, 0:260].rearrange("p (r d) -> p r d", d=65)
            for r in range(4):
                P.tr(pmv[:, r, :], osT[0:65, r * 128:(r + 1) * 128], identf[0:65, 0:65])
            P.recip(rd4[:], pmv[:, :, 0])
            P.tt('vector', coef4[:], rd4[:], AP_(gs, 12 * g + br, [[24, 128], [3, 4]]), ALU.mult)
            dst = acc[:, 4 * g:4 * g + 4, :]
            P.tt('vector', otmp[:], pmv[:, :, 1:65], AP_(coef4, 0, [[4, 128], [1, 4], [0, 64]]), ALU.mult)
            P.tt('gpsimd', dst, dst, otmp[:], ALU.add)

        for m in range(B_NM):
            qs = qT_s[m % 2]
            kws = kwT_s[m % 2]
            vws = vwx_s[m % 2]
            gs = gates_s[m % 2]
            h0 = 0 if m > 0 else 1
            with P.group("qT_s%d" % (m % 2)):
                P.dma('sync', qs[0:64, :, :], qsc[m].rearrange("g d n -> d g n"), writes=[qs])
                P.dma('scalar', qs[64:68, :, :], W['qaug'][m], writes=[qs])
            P.dma('scalar', gs[:], gsc[m])
            nh = 2 - h0
            c0 = 128 * (m - 1 + h0)
            with P.group("kws%d" % (m % 2)):
                for r in range(4):
                    P.dma('sync' if r % 2 == 0 else 'scalar',
                          AP_(kws, (r * 2 + h0) * 128, [[2048, 64], [1024, 2], [1, 128 * nh]]),
                          AP_(gath['kw'], r * 1024 * 512 + c0, [[4096, 64], [64 * 4096, 2], [1, 128 * nh]]),
                          writes=[kws])
            with P.group("vws%d" % (m % 2)):
                for r in range(4):
                    for hf in range(h0, 2):
                        mp = m - 1 + hf
                        P.dma('gpsimd', vws[:, r * 2 + hf, :],
                              AP_(gath['vw%d' % (mp // 16)], r * 520 * 512 + (mp % 16) * 128 * 130, [[130, 128], [1, 130]]),
                              writes=[vws])
            for g in range(2):
                P.copy('gpsimd', AP_(kws, 64 * 2048 + g * 1024 + h0 * 128, [[2048, 4], [256, 4], [1, 128 * (2 - h0)]]),
                       AP_(ksT_s[0], 64 * T + 128 * (m - 1 + h0), [[T, 4], [4096, 4], [1, 128 * (2 - h0)]]),
                       reads=[ksT_s[0]], writes=[kws])
            n_it = (32 * m + 30) // 128 + 1
            for g in range(2):
                for it in range(n_it):
                    ps = ps_s[state['s'] % 4]
                    state['s'] += 1
                    delta = 128 * it - 32 * m
                    masked = delta >= -128
                    P.mm(ps[:], kcT_s[:, g, it * 128:(it + 1) * 128], qs[:, g, :], start=True, stop=not masked)
                    if masked:
                        P.mm(ps[:], identb[:], cmask_s[:, (-delta) // 32, :], start=False, stop=True)
                    P.act(Pc[g][:, it, :], ps[:], AF.Exp)
                for r in range(4):
                    po = ps_o[0]
                    for it in range(n_it):
                        P.mm(po[:, 0:321], Pc[g][:, it, r * 128:(r + 1) * 128], vcx_s[:, it, g, :], start=(it == 0), stop=(it == n_it - 1))
                    P.copy('scalar', oc[:, r, :], po[:, 0:321])
                P.ts('vector', den4[:], oc[:, :, 64], 1e-30, ALU.max)
                P.recip(rd4[:], den4[:])
                P.ts('vector', imp[g][:], oc[:, 0, 65:321], rd4[:, 0:1], ALU.mult)
                for r in range(1, 4):
                    P.stt(imp[g][:], oc[:, r, 65:321], rd4[:, r:r + 1], imp[g][:], ALU.mult, ALU.add)
                P.tt('vector', coef4[:], rd4[:], AP_(gs, 12 * g + 0, [[24, 128], [3, 4]]), ALU.mult)
                P.tt('vector', acc[:, 4 * g:4 * g + 4, :], oc[:, :, 0:64], AP_(coef4, 0, [[4, 128], [1, 4], [0, 64]]), ALU.mult)
                P.tt('vector', imp[g][:], imp[g][:], Bsel_s[:, 256 - 8 * m:512 - 8 * m], ALU.add)
                P.ts('vector', imp[g][:, 0:1], imp[g][:, 0:1], 1e4, ALU.add)
                P.op('vector', lambda e, g=g: e.max(out=m8[:, 0:8], in_=imp[g][:]), reads=[imp[g]], writes=[m8])
                P.op('vector', lambda e, g=g: e.match_replace(out=tmpi[:], in_to_replace=m8[:, 0:8], in_values=imp[g][:], imm_value=-3.0e38),
                     reads=[imp[g], m8], writes=[tmpi])
                P.op('vector', lambda e: e.max(out=m8[:, 8:16], in_=tmpi[:]), reads=[tmpi], writes=[m8])
                P.ts('vector', thr[:], m8[:, 15:16], -1e29, ALU.max)
                P.ts('vector', nm[g][:], imp[g][:], thr[:, 0:1], ALU.is_ge)
            if pending:
                pending.pop()()
            tiles = []
            wt = [(r, hf) for hf in range(h0, 2) for r in range(4)]
            for g in range(2):
                for idx, (r, hf) in enumerate(wt):
                    def eS(ps, g=g, r=r, hf=hf):
                        P.mm(ps[:], kws[:, g, (r * 2 + hf) * 128:(r * 2 + hf + 1) * 128], qs[:, g, :], start=True, stop=(hf == 0))
                        if hf == 1:
                            P.mm(ps[:], identb[:], smask_s[:, r, :], start=False, stop=True)

                    def eX(pb, r=r):
                        mask_mul(pb, AP_(wmask_s, r * 128, [[1024, 128], [0, 4], [1, 128]]), [wmask_s])

                    def ePV(pb, g=g, r=r, hf=hf, idx=idx):
                        P.mm(ps_acc[g][0:65, :], vws[:, r * 2 + hf, 65 * g:65 * g + 65], pb[:], start=(idx == 0), stop=(idx == len(wt) - 1))
                    tiles.append((eS, eX if hf == 0 else None, ePV))
            run_stream(tiles)
            for g in range(2):
                for hb in range(2):
                    P.tr(pm_b[:, hb * 128:(hb + 1) * 128], nm[g][:, hb * 128:(hb + 1) * 128], identb[:])
                P.copy('vector', nmT[g][:].rearrange("p a q -> p (a q)"), pm_b[:, 0:256], reads=[pm])
            for g in range(2):
                o_epilogue(ps_acc[g], g, 2, gs)
            nkt = 4 * m + 4
            tiles = []
            for g in range(2):
                for kt in range(nkt):
                    def eS(ps, g=g, kt=kt):
                        diag = kt >= 4 * m
                        col = ((kt % 4) * 32 + kt // 4) * 128
                        P.mm(ps[:], ksT_s[g][:, col:col + 128], qs[:, g, :], start=True, stop=not diag)
                        if diag:
                            P.mm(ps[:], identb[:], smask_s[:, kt - 4 * m, :], start=False, stop=True)

                    def eX(pb, g=g, kt=kt):
                        bank = (ps_o[0], pm)[state['x'] % 2]
                        state['x'] += 1
                        P.mm(bank[:, 0:128], Eall_s[:, kt % 64, :], nmT[g][:, kt // 64, :], start=True, stop=True)
                        mask_mul(pb, AP_(bank, 0, [[512, 128], [0, 4], [1, 128]]), [bank])

                    def ePV(pb, g=g, kt=kt):
                        P.mm(ps_acc[g][0:65, :], vsx_s[:, (kt % 4) * 32 + kt // 4, 65 * g:65 * g + 65], pb[:], start=(kt == 0), stop=(kt == nkt - 1))
                    tiles.append((eS, eX, ePV))
            run_stream(tiles)
            for g in range(2):
                o_epilogue(ps_acc[g], g, 1, gs)
            accf = acc[:].rearrange("p h d -> p (h d)")
            P.act(accb[:], accf, AF.Square, accum_out=ssq[:])
            P.act(rat[:], ssq[:], AF.Sqrt, bias=EPS, scale=1.0 / 512)
            P.recip(rat2[:], rat[:])
            P.dma('gpsimd', rasc[m], rat2[:])
            P.copy('gpsimd', accb[:], accf)
            if G.get('dbg'):
                P.dma('gpsimd', G['dbg']['ra'][m], rat2[:])

            def finish_out(m=m):
                for bk in range(4):
                    P.tr(pm_b[:, bk * 128:(bk + 1) * 128], accb[:, bk * 128:(bk + 1) * 128], identb[:])
                P.copy('vector', atT[:].rearrange("p b q -> p (b q)"), pm_b[:, 0:512], reads=[pm])
                P.dma('gpsimd', atsc[m], atT[:].rearrange("p b q -> p (b q)"))
                if G.get('dbg'):
                    P.dma('gpsimd', G['dbg']['at'][m], atT[:].rearrange("p b q -> p (b q)"))
            pending.append(finish_out)
        pending.pop()()


def emit_C(C, l, xsrc, xdst, W, G):
    P = C.P
    with C.phase("C%d" % l):
        Wo = C.sb("Wo", [128, 8, D], BF16)
        Wdn = C.sb("Wdn", [128, 32, D], BF16)
        wst = [C.sb("wst%d" % i, [128, 8, 256], F32) for i in range(2)]
        Wup = [C.sb("Wup%d" % i, [128, 8, 256], BF16) for i in range(2)]
        gout_s = C.sb("gout_s", [128, 8], F32)
        gffn_s = C.sb("gffn_s", [128, D], F32)
        cw_s = C.sb("cw_s", [128, 4, 3], F32)
        selw_s = C.sb("selw_s", [128, 4], F32)
        xt = [C.sb("xt%d" % i, [128, D], F32) for i in range(2)]
        x1 = [C.sb("x1_%d" % i, [128, D], F32) for i in range(4)]
        at_s = [C.sb("at_s%d" % i, [128, 4, 128], BF16) for i in range(2)]
        yv = [C.sb("yv%d" % i, [128, 4, 128], F32) for i in range(2)]
        bg2 = [C.sb("bg2_%d" % i, [128, 4, 2], F32) for i in range(2)]
        tl = [C.sb("tl%d" % i, [128, 4, 8], F32) for i in range(2)]
        halo = C.sb("halo", [128, 4, 2], F32)
        pt0 = C.sb("pt0", [128, 4], F32)
        pt1 = C.sb("pt1", [128, 4], F32)
        ysq = [C.sb("ysq%d" % i, [128, 4, 128], F32) for i in range(2)]
        ybf = [C.sb("ybf%d" % i, [128, 4, 128], BF16) for i in range(2)]
        rc = C.sb("rc", [128, 1], F32)
        rc2 = [C.sb("rc2_%d" % i, [128, 1], F32) for i in range(2)]
        ra_s = [C.sb("ra_s%d" % i, [128, 1], F32) for i in range(2)]
        ss = C.sb("ss", [128, 1], F32)
        rstd = C.sb("rstd", [128, 1], F32)
        h2n = [C.sb("h2n%d" % i, [128, D], BF16) for i in range(2)]
        h2T = C.sb("h2T", [128, 8, 512], BF16)
        rl = [C.sb("rl%d" % i, [128, 512], F32) for i in range(2)]
        actT = C.sb("actT", [128, 32, 512], BF16)
        pa = [C.ps("pa%d" % i, [128, 512], F32) for i in range(2)]
        pcv = [C.ps("pcv%d" % i, [128, 512], F32) for i in range(2)]
        pu = [C.ps("pu%d" % i, [128, 512], F32) for i in range(2)]
        pT = C.ps("pT", [128, 8, 128], BF16)
        px = C.ps("px", [128, 512], F32)
        identb, onesf = G['identb'], G['onesf']
        gathu, ysc, bg2sc, atsc, rasc = G['gathu'], G['ysc'], G['bg2sc'], G['atsc'], G['rasc']

        with P.group('cst'):
            P.dma('sync', gout_s[:], W['gout'][l])
            P.dma('sync', gffn_s[:], W['gffn'][l])
            P.dma('sync', cw_s[:], W['cw'][l])
            P.dma('sync', selw_s[:], W['selw'])
        n = 0
        for kc in range(8):
            for hf in range(2):
                st = wst[n % 2]
                n += 1
                stw = AP_(st, 0, [[2048, 128], [1, 512]])
                P.dma('sync' if hf == 0 else 'scalar', stw, W['w_o'][l, kc * 128:(kc + 1) * 128, hf * 512:(hf + 1) * 512], writes=[st])
                if hf == 0:
                    P.ts('vector', Wo[:, kc, 0:512], stw, gout_s[:, kc:kc + 1], ALU.mult, reads=[st, gout_s])
                else:
                    P.act(Wo[:, kc, 512:1024], stw, AF.Copy, scale=gout_s[:, kc:kc + 1], reads=[st, gout_s])
        for f2 in range(16):
            st = wst[n % 2]
            n += 1
            stv = AP_(st, 0, [[2048, 128], [1024, 2], [1, 1024]])
            P.dma('sync' if f2 % 2 == 0 else 'scalar', stv, W['w_dn'][l, f2 * 256:(f2 + 1) * 256, :].rearrange("(a p) n -> p a n", p=128), writes=[st])
            P.copy(('vector', 'scalar', 'vector', 'gpsimd')[f2 % 4], Wdn[:, f2 * 2:(f2 + 1) * 2, :], stv, reads=[st])
        up_n = [n]

        def S1a(m):
            xb, ab, rab, yb, bgb, tlb = xt[m % 2], at_s[m % 2], ra_s[m % 2], yv[m % 2], bg2[m % 2], tl[m % 2]
            P.dma('sync', xb[:], xsrc[m])
            P.dma('scalar', ab[:], atsc[m].rearrange("p (b q) -> p b q", b=4))
            P.dma('sync', rab[:], rasc[m])
            P.dma('scalar', yb[:], ysc[m].rearrange("p (c t) -> p c t", c=4))
            P.dma('sync', bgb[:], bg2sc[m].rearrange("p (c e) -> p c e", e=2))
            if m == 0:
                P.memset('gpsimd', tlb[:, 0, :], 0.0)
            with P.group("tl%d" % (m % 2)):
                if m > 0:
                    P.dma('gpsimd', tlb[:, 0, :], gathu[3 * B_NM + m - 1].rearrange("(p e) -> p e", e=8), writes=[tlb])
                for cd in range(1, 4):
                    P.dma('gpsimd', tlb[:, cd, :], gathu[(cd - 1) * B_NM + m].rearrange("(p e) -> p e", e=8), writes=[tlb])
            hf = halo[:].rearrange("p c e -> p (c e)")
            P.ts('vector', hf, tlb[:, 0, :], selw_s[:, 0:1], ALU.mult)
            for cd in range(1, 4):
                P.stt(hf, tlb[:, cd, :], selw_s[:, cd:cd + 1], hf, ALU.mult, ALU.add)
            P.tt('vector', pt0[:], halo[:, :, 1], cw_s[:, :, 1], ALU.mult)
            P.tt('vector', pt1[:], halo[:, :, 0], cw_s[:, :, 0], ALU.mult)
            P.tt('vector', pt0[:], pt0[:], pt1[:], ALU.add)
            P.tt('vector', pt0[:], pt0[:], bgb[:, :, 0], ALU.mult)
            P.tt('vector', yb[:, :, 0], yb[:, :, 0], pt0[:], ALU.add)
            P.tt('vector', pt1[:], halo[:, :, 1], cw_s[:, :, 0], ALU.mult)
            P.tt('vector', pt1[:], pt1[:], bgb[:, :, 1], ALU.mult)
            P.tt('vector', yb[:, :, 1], yb[:, :, 1], pt1[:], ALU.add)
            P.act(ysq[m % 2][:], yb[:], AF.Square)
            P.copy('gpsimd', ybf[m % 2][:], yb[:])

        def S1b(m, j):
            xb, ab, rab = xt[m % 2], at_s[m % 2], ra_s[m % 2]
            for ch in range(4):
                P.mm(px[:, 0:1], ysq[m % 2][:, ch, :], onesf[:], start=(ch == 0), stop=(ch == 3))
            P.act(rc[:], px[:, 0:1], AF.Sqrt, bias=EPS, scale=1.0 / 512)
            P.recip(rc2[m % 2][:], rc[:])
            for nh in range(2):
                for kc in range(4):
                    P.mm(pa[nh][:], ab[:, kc, :], Wo[:, kc, nh * 512:(nh + 1) * 512], start=(kc == 0), stop=(kc == 3))
                for kc in range(4):
                    P.mm(pcv[nh][:], ybf[m % 2][:, kc, :], Wo[:, 4 + kc, nh * 512:(nh + 1) * 512], start=(kc == 0), stop=(kc == 3))
            x1b = x1[j]
            for nh in range(2):
                sl = slice(nh * 512, (nh + 1) * 512)
                P.stt(x1b[:, sl], pa[nh][:], rab[:, 0:1], xb[:, sl], ALU.mult, ALU.add)
                P.stt(x1b[:, sl], pcv[nh][:], rc2[m % 2][:, 0:1], x1b[:, sl], ALU.mult, ALU.add)
            if G.get('dbg'):
                P.dma('gpsimd', G['dbg']['x1'][m], x1b[:])
                P.dma('gpsimd', G['dbg']['y'][m], ybf[m % 2][:].rearrange("p c t -> p (c t)"))
                P.dma('gpsimd', G['dbg']['rc'][m], rc2[m % 2][:])
            P.act(h2n[m % 2][:], x1b[:], AF.Square, accum_out=ss[:])
            P.act(rstd[:], ss[:], AF.Sqrt, bias=EPS, scale=1.0 / D)
            P.recip(rstd[:], rstd[:])
            P.stt(h2n[m % 2][:], x1b[:], rstd[:, 0:1], gffn_s[:], ALU.mult, ALU.mult)

        def S1c(m, j):
            for kc in range(8):
                P.tr(pT[:, kc, :], h2n[m % 2][:, kc * 128:(kc + 1) * 128], identb[:])
            P.copy('scalar', h2T[:, :, j * 128:(j + 1) * 128], pT[:])

        NBT = B_NM // 4
        for j in range(4):
            S1a(j)
            S1b(j, j)
            S1c(j, j)
        for bt in range(NBT):
            for u in range(16):
                st = wst[up_n[0] % 2]
                wb = Wup[up_n[0] % 2]
                up_n[0] += 1
                P.dma('sync' if u % 2 == 0 else 'scalar', st[:], W['w_up'][l, :, u * 256:(u + 1) * 256].rearrange("(kc p) n -> p kc n", p=128))
                P.copy('gpsimd' if u % 2 == 0 else 'vector', wb[:], st[:])
                for fl in range(2):
                    f = u * 2 + fl
                    pp = pu[f % 2]
                    for kc in range(8):
                        P.mm(pp[:], wb[:, kc, fl * 128:(fl + 1) * 128], h2T[:, kc, :], start=(kc == 0), stop=(kc == 7))
                    rb = rl[f % 2]
                    P.act(rb[:], pp[:], AF.Relu)
                    P.tt('gpsimd' if f % 2 == 0 else 'vector', actT[:, f, :], rb[:], rb[:], ALU.mult)
            nxt = bt + 1 < NBT
            for j in range(4):
                m = bt * 4 + j
                mn = m + 4
                if nxt:
                    S1a(mn)
                ob = x1[j]
                for nh in range(2):
                    pp = pu[nh]
                    for f in range(32):
                        P.mm(pp[:], actT[:, f, j * 128:(j + 1) * 128], Wdn[:, f, nh * 512:(nh + 1) * 512], start=(f == 0), stop=(f == 31))
                    P.tt('vector', ob[:, nh * 512:(nh + 1) * 512], pp[:], x1[j][:, nh * 512:(nh + 1) * 512], ALU.add)
                P.dma('sync', xdst[m], ob[:])
                if nxt:
                    S1b(mn, j)
                    if j >= 1:
                        S1c(mn - 1, j - 1)
            if nxt:
                S1c(bt * 4 + 7, 3)


def build_fused(nlayers=DEPTH, stages="AGZBC", dbg=False):
    C = Ctx()
    P = C.P
    W = {}
    W['x0'] = C.din("x0", [B_NM, 128, D], F32)
    W['w_in'] = C.din("w_in", [nlayers, D, INW], F32)
    W['gmix'] = C.din("gmix", [nlayers, 128, 8], F32)
    W['gq'] = C.din("gq", [nlayers, 128, 512], F32)
    W['gk3'] = C.din("gk3", [nlayers, 128, 3, 128], F32)
    W['cw'] = C.din("cw", [nlayers, 128, 4, 3], F32)
    W['peT'] = C.din("peT", [nlayers, 2, 64, 32], F32)
    W['w1'] = C.din("w1", [nlayers, 2, 2048, 256], F32)
    W['b1b'] = C.din("b1b", [nlayers, 2, 128, 256], F32)
    W['w2'] = C.din("w2", [nlayers, 2, 256, 64], F32)
    W['b2b'] = C.din("b2b", [nlayers, 128, 2, 64], F32)
    W['w_o'] = C.din("w_o", [nlayers, D, D], F32)
    W['gout'] = C.din("gout", [nlayers, 128, 8], F32)
    W['gffn'] = C.din("gffn", [nlayers, 128, D], F32)
    W['w_up'] = C.din("w_up", [nlayers, D, DFF], F32)
    W['w_dn'] = C.din("w_dn", [nlayers, DFF, D], F32)
    W['smask'] = C.din("smask", [128, 4, 512], BF16)
    W['cmask'] = C.din("cmask", [128, 5, 512], BF16)
    W['wmask'] = C.din("wmask", [128, 8, 128], BF16)
    W['Eall'] = C.din("Eall", [128, 64, 128], BF16)
    W['Bsel'] = C.din("Bsel", [128, 512], F32)
    W['kaug'] = C.din("kaug", [4, T], BF16)
    W['kcaug'] = C.din("kcaug", [4, 2, 1024], BF16)
    W['qaug'] = C.din("qaug", [B_NM, 4, 2, 512], BF16)
    W['vcxc'] = C.din("vcxc", [128, 8, 2, 257], BF16)
    W['selw'] = C.din("selw", [128, 4], F32)
    identb_d = C.din("identb", [128, 128], BF16)
    identf_d = C.din("identf", [128, 128], F32)
    onesf_d = C.din("onesf", [128, 1], F32)
    xo = C.dout("xo", [B_NM, 128, D], F32)

    G = {}
    COMPS = [('ks', 1024), ('kw', 1024), ('kc', 1024), ('vc', 1024), ('vs0', 520), ('vs1', 520), ('vw0', 520), ('vw1', 520)]
    G['pay'] = {cn: C.dint("pay_" + cn, [rows, 512], BF16) for cn, rows in COMPS}
    G['gath'] = {cn: C.dint("gath_" + cn, [4 * rows, 512], BF16) for cn, rows in COMPS}
    G['payu'] = C.dint("payu", [B_NM, 1024], F32).ap()
    G['gathu'] = C.dint("gathu", [4 * B_NM, 1024], F32).ap()
    xbuf = C.dint("xbuf", [B_NM, 128, D], F32).ap()
    G['qsc'] = C.dint("qsc", [B_NM, 2, 64, 512], BF16)
    G['gsc'] = C.dint("gsc", [B_NM, 128, 24], F32).ap()
    G['ysc'] = C.dint("ysc", [B_NM, 128, 512], F32).ap()
    G['bg2sc'] = C.dint("bg2sc", [B_NM, 128, 8], F32).ap()
    G['atsc'] = C.dint("atsc", [B_NM, 128, 512], BF16).ap()
    G['rasc'] = C.dint("rasc", [B_NM, 128, 1], F32).ap()
    qsc_ap = G['qsc'].ap()
    if dbg:
        G['dbg'] = {'at': C.dout("dbg_at", [B_NM, 128, 512], BF16), 'ra': C.dout("dbg_ra", [B_NM, 128, 1], F32),
                    'x1': C.dout("dbg_x1", [B_NM, 128, D], F32), 'y': C.dout("dbg_y", [B_NM, 128, 512], BF16),
                    'rc': C.dout("dbg_rc", [B_NM, 128, 1], F32),
                    'kc': C.dout("dbg_kc", [68, 2, 1024], BF16), 'vc': C.dout("dbg_vc", [128, 8, 2, 321], BF16),
                    'ks': C.dout("dbg_ks", [68, T], BF16), 'vs': C.dout("dbg_vs", [128, 128, 130], BF16)}

    G['identb'] = C.sb("identb_s", [128, 128], BF16)
    G['identf'] = C.sb("identf_s", [128, 128], F32)
    G['onesf'] = C.sb("onesf_s", [128, 1], F32)
    with P.group('cst'):
        P.dma('sync', G['identb'][:], identb_d)
        P.dma('sync', G['identf'][:], identf_d)
        P.dma('sync', G['onesf'][:], onesf_d)
    P.barrier()
    rg = [[0, 1, 2, 3], [4, 5, 6, 7]]
    def mk_cc(i_ap, o_ap):
        return lambda e: e.collective_compute("AllGather", ALU.bypass, replica_groups=rg, ins=[i_ap.opt()], outs=[o_ap.opt()])
    cc_fns = [mk_cc(G['pay'][cn].ap(), G['gath'][cn].ap()) for cn, _ in COMPS] + [mk_cc(G['payu'], G['gathu'])]
    for l in range(nlayers):
        xsrc = W['x0'] if l == 0 else xbuf
        xdst = xo if l == nlayers - 1 else xbuf
        if 'A' in stages:
            emit_A(C, l, xsrc, W, G)
        if 'G' in stages:
            P.collectives(cc_fns)
        with C.phase("ZB%d" % l):
            G['kcT_s'] = C.sb("kcT_s", [68, 2, 1024], BF16)
            G['vcx_s'] = C.sb("vcx_s", [128, 8, 2, 321], BF16)
            with P.group('cst2'):
                P.dma('sync', G['kcT_s'][64:68, :, :], W['kcaug'])
                P.dma('scalar', G['vcx_s'][:, :, :, 64:321], W['vcxc'])
            if 'Z' in stages:
                emit_Z(C, l, W, G)
            Gb = dict(G)
            Gb['qsc'] = qsc_ap
            if 'B' in stages:
                emit_B(C, l, W, Gb)
        if 'C' in stages:
            emit_C(C, l, xsrc, xdst, W, G)
    if 'C' not in stages:
        with C.phase("dbg"):
            xt = [C.sb("xt%d" % i, [128, D], F32) for i in range(2)]
            for m in range(B_NM):
                P.dma('sync', xt[m % 2][:], W['x0'][m])
                P.dma('sync', xo[m], xt[m % 2][:])
    return C.finish()


IDENTB = np.eye(128, dtype=np.float32).astype(NPBF)
ONESF = np.ones((128, 1), np.float32)


def _bf(a):
    return np.ascontiguousarray(a).astype(NPBF)


def _selmap():
    i = np.arange(1024)[:, None] * 16
    j = np.arange(256)[None, :] * 64
    sh = np.clip(np.minimum(i + 32, j + 64) - np.maximum(i, j), 0, None).astype(np.float32) / 32.0
    sh[1023] = 0.0
    return sh


def _pos_rows(pos):
    pos = np.maximum(pos, 0)
    return np.stack([np.ones_like(pos), np.ones_like(pos), pos % 128, pos // 128]).astype(np.float32)


def core_consts(s):
    ki = np.arange(128)[:, None]
    qi = np.arange(128)[None, :]
    tri_gt = np.where(ki > qi, NEGM, 0.0).astype(np.float32)
    tri_le = np.where(ki <= qi, NEGM, 0.0).astype(np.float32)
    full = np.full((128, 128), NEGM, np.float32)
    zero = np.zeros((128, 128), np.float32)
    smask = np.tile(np.stack([zero if d < s else (tri_gt if d == s else full) for d in range(4)], axis=1), (1, 1, 4))
    cm = []
    for v in range(5):
        ip = ki - 32 * v
        vis = (16 * ip + 31) <= (128 * s + qi)
        cm.append(np.where(vis, 0.0, NEGM).astype(np.float32))
    cmask = np.tile(np.stack(cm, axis=1), (1, 1, 4))
    wm = []
    for d in range(8):
        off = s + 4 - d
        if off == 0:
            wm.append((ki <= qi).astype(np.float32))
        elif off == 4:
            wm.append((ki > qi).astype(np.float32))
        elif 0 < off < 4:
            wm.append(np.ones((128, 128), np.float32))
        else:
            wm.append(zero)
    wmask = np.stack(wm, axis=1)
    E = np.zeros((128, 64, 128), np.float32)
    for kt in range(64):
        for k in range(128):
            E[2 * kt + k // 64, kt, k] = 1.0
    r = np.arange(512)[None, :] - 256 - 2 * s
    qq = np.arange(128)[:, None]
    Brel = np.zeros((128, 512), np.float32)
    Brel = np.where(r >= 2, np.float32(-1e30), Brel)
    Brel = np.where(r == 1, np.where(qq >= 64, np.float32(1e4), np.float32(-1e30)), Brel)
    Brel = np.where(r == 0, np.float32(1e4), Brel)
    Brel = np.where((r == -1) & (qq < 64), np.float32(1e4), Brel)
    rr, mm, kk = np.meshgrid(np.arange(4), np.arange(32), np.arange(128), indexing='ij')
    kpos = (128 * (4 * mm + rr) + kk).reshape(-1)
    kaug = _pos_rows(kpos)
    kc = _pos_rows(np.arange(1024) * 16 + 31)
    kcaug = np.stack([kc, kc], axis=1)
    qaug = np.zeros((B_NM, 4, 2, 512), np.float32)
    qv = np.arange(128)
    for m in range(B_NM):
        c = 4 * m + s
        for g in range(2):
            for rh in range(4):
                sl = 2.0 ** (-(4 * g + rh + 1))
                cs = slice(rh * 128, (rh + 1) * 128)
                qaug[m, 0, g, cs] = -sl * qv
                qaug[m, 1, g, cs] = -sl * 128.0 * c
                qaug[m, 2, g, cs] = sl
                qaug[m, 3, g, cs] = sl * 128.0
    sm = _selmap().reshape(8, 128, 256).transpose(1, 0, 2)
    vcxc = np.zeros((128, 8, 2, 257), np.float32)
    vcxc[:, :, :, 0] = 1.0
    vcxc[127, 7, :, 0] = 0.0
    vcxc[:, :, :, 1:257] = sm[:, :, None, :]
    selw = np.zeros((128, 4), np.float32)
    selw[:, s] = 1.0
    return {'smask': _bf(smask), 'cmask': _bf(cmask), 'wmask': _bf(wmask), 'Eall': _bf(E),
            'Bsel': np.ascontiguousarray(Brel.astype(np.float32)), 'kaug': _bf(kaug), 'kcaug': _bf(kcaug),
            'qaug': _bf(qaug), 'vcxc': _bf(vcxc), 'selw': selw,
            'identb': IDENTB, 'identf': np.eye(128, dtype=np.float32), 'onesf': ONESF}


def prep_fused(p, L=DEPTH):
    p = {k: (v if k == 'x' else v[:L]) for k, v in p.items()}
    common = {
        'w_in': np.ascontiguousarray(p['w_in'], dtype=np.float32),
        'gmix': np.ascontiguousarray(p['g_mix_norm'].reshape(L, 8, 128).transpose(0, 2, 1)),
        'gq': np.ascontiguousarray(np.tile(p['g_q'][:, None, :], (1, 128, 8))),
        'gk3': np.ascontiguousarray(np.broadcast_to(np.tile(p['g_k'], (1, 1, 2))[:, None], (L, 128, 3, 128))),
        'cw': np.ascontiguousarray(p['conv_w'].reshape(L, 3, 4, 128).transpose(0, 3, 2, 1)),
        'peT': np.ascontiguousarray(p['pe_cmp'].transpose(0, 1, 3, 2)),
        'w1': np.ascontiguousarray(p['w_cmp1'], dtype=np.float32),
        'b1b': np.ascontiguousarray(np.broadcast_to(p['b_cmp1'][:, :, None, :], (L, 2, 128, 256))),
        'w2': np.ascontiguousarray(p['w_cmp2'], dtype=np.float32),
        'b2b': np.ascontiguousarray(np.broadcast_to(p['b_cmp2'][:, None], (L, 128, 2, 64))),
        'w_o': np.ascontiguousarray(p['w_o'], dtype=np.float32),
        'gout': np.ascontiguousarray(p['g_out'].reshape(L, 8, 128).transpose(0, 2, 1)),
        'gffn': np.ascontiguousarray(np.tile(p['g_ffn_norm'][:, None, :], (1, 128, 1))),
        'w_up': np.ascontiguousarray(p['w_up'], dtype=np.float32),
        'w_dn': np.ascontiguousarray(p['w_down'], dtype=np.float32),
    }
    maps = []
    x = np.ascontiguousarray(p['x'], dtype=np.float32)
    for b in range(NB):
        xv = x[b].reshape(T // 128, 128, D)
        for s in range(4):
            mp = dict(common)
            mp.update(core_consts(s))
            mp['x0'] = np.ascontiguousarray(xv[np.arange(B_NM) * 4 + s])
            maps.append(mp)
    return maps


_PROG = {}


def run_fused(inputs, nlayers=DEPTH):
    p = {k: np.asarray(v) for k, v in inputs.items()}
    if nlayers not in _PROG:
        _PROG[nlayers] = build_fused(nlayers)
    res = run_bass_kernel_spmd(_PROG[nlayers], prep_fused(p, nlayers), core_ids=list(range(NCORES)))
    out = np.empty((NB, T, D), np.float32)
    for b in range(NB):
        ov = out[b].reshape(T // 128, 128, D)
        for s in range(4):
            ov[np.arange(B_NM) * 4 + s] = np.asarray(res.results[4 * b + s]['xo'])
    return out


def kernel(**inputs):
    return run_fused(inputs, DEPTH)
```

```python
import contextlib
import numpy as np
import ml_dtypes
import concourse.bass as bass
import concourse.mybir as mybir
from concourse.bass_utils import run_bass_kernel_spmd

F32 = mybir.dt.float32
BF16 = mybir.dt.bfloat16
AF = mybir.ActivationFunctionType
ALU = mybir.AluOpType
AX = mybir.AxisListType
NPBF = ml_dtypes.bfloat16

ENGS = ['tensor', 'vector', 'scalar', 'gpsimd', 'sync']
NCORES = 8
D = 1024
T = 16384
NB = 2
DEPTH = 4
INW = 2840
EPS = 1e-6
NEGM = -30000.0


def key_of(ap):
    t = getattr(ap, 'tensor', ap)
    return t.name.split('@')[0]


class Prog:
    def __init__(self, nc):
        self.nc = nc
        self.q = {e: [] for e in ENGS}
        self.semcount = {}
        self.res = {}
        self.seen = {e: {} for e in ENGS}

    grp = None

    @contextlib.contextmanager
    def group(self, sem):
        s = 'd_' + sem
        self.grp = {'sem': sem, 's': s, 'start': self.semcount.get(s, 0), 'keys': set(), 'pre': {}}
        try:
            yield
        finally:
            g, self.grp = self.grp, None
            v = self.semcount.get(s, 0)
            for k in g['keys']:
                self.res[k]['w'] = (s, v)

    def _need(self, eng, reads, writes):
        need = {}

        def add(tok):
            if tok is None:
                return
            s, v = tok
            if eng == 'tensor' and s == 'e_tensor':
                return
            if self.grp is not None and s == self.grp['s'] and v > self.grp['start']:
                return
            if need.get(s, 0) < v:
                need[s] = v
        for k in reads:
            st = self.res.get(k)
            if st:
                add(st['w'])
        for k in writes:
            sts = [self.res.get(k)]
            if self.grp is not None:
                if k not in self.grp['pre']:
                    st0 = self.res.get(k)
                    self.grp['pre'][k] = {'w': st0['w'], 'r': dict(st0['r'])} if st0 else None
                sts.append(self.grp['pre'][k])
            for st in sts:
                if st:
                    add(st['w'])
                    for s, v in st['r'].items():
                        add((s, v))
        for s, v in need.items():
            if self.seen[eng].get(s, 0) < v:
                self.q[eng].append(('wait', s, v))
                self.seen[eng][s] = v

    def _update(self, reads, writes, tok):
        s, v = tok
        for k in reads:
            st = self.res.setdefault(k, {'w': None, 'r': {}})
            if st['r'].get(s, 0) < v:
                st['r'][s] = v
        for k in writes:
            self.res[k] = {'w': tok, 'r': {}}

    def op(self, eng, fn, reads=(), writes=()):
        reads = [r if isinstance(r, str) else key_of(r) for r in reads]
        writes = [r if isinstance(r, str) else key_of(r) for r in writes]
        self._need(eng, reads, writes)
        s = 'e_' + eng
        v = self.semcount.get(s, 0) + 1
        self.semcount[s] = v
        self.q[eng].append(('op', fn, s, 1))
        self._update(reads, writes, (s, v))

    def dma(self, eng, out, in_, reads=None, writes=None, sem=None, **kw):
        if reads is None:
            reads = [] if in_.tensor.name in self.dram_names else [in_]
        if writes is None:
            writes = [] if out.tensor.name in self.dram_names else [out]
        assert reads or writes or sem
        reads = [r if isinstance(r, str) else key_of(r) for r in reads]
        writes = [r if isinstance(r, str) else key_of(r) for r in writes]
        if self.grp is not None:
            sem = self.grp['sem']
            self.grp['keys'].update(writes)
        if sem is None:
            sem = writes[0] if writes else 'st_' + reads[0]
        self._need(eng, reads, writes)
        s = 'd_' + sem
        v = self.semcount.get(s, 0) + 16
        self.semcount[s] = v
        self.q[eng].append(('op', lambda e: e.dma_start(out=out, in_=in_, **kw), s, 16))
        self._update(reads, writes, (s, v))

    dram_names = set()

    def mm(self, out, lhsT, rhs, start=True, stop=True, reads=None, writes=None):
        self.op('tensor', lambda e: e.matmul(out, lhsT=lhsT, rhs=rhs, start=start, stop=stop),
                reads=reads if reads is not None else [lhsT, rhs],
                writes=writes if writes is not None else [out])

    def tr(self, out, in_, ident, reads=None, writes=None):
        self.op('tensor', lambda e: e.transpose(out, in_, ident),
                reads=reads if reads is not None else [in_, ident],
                writes=writes if writes is not None else [out])

    def act(self, out, in_, func, bias=None, scale=None, accum_out=None, reads=None, writes=None, eng='scalar'):
        kw = {}
        rd = [in_]
        if bias is not None:
            kw['bias'] = bias
            if not isinstance(bias, (int, float)):
                rd.append(bias)
        if scale is not None:
            kw['scale'] = scale
            if not isinstance(scale, (int, float)):
                rd.append(scale)
        wr = [out]
        if accum_out is not None:
            kw['accum_out'] = accum_out
            wr.append(accum_out)
        self.op('scalar', lambda e: e.activation(out=out, in_=in_, func=func, **kw),
                reads=reads if reads is not None else rd,
                writes=writes if writes is not None else wr)

    def tt(self, eng, out, in0, in1, op, reads=None, writes=None):
        self.op(eng, lambda e: e.tensor_tensor(out=out, in0=in0, in1=in1, op=op),
                reads=reads if reads is not None else [in0, in1],
                writes=writes if writes is not None else [out])

    def ts(self, eng, out, in0, s1, op0, s2=None, op1=None, reads=None, writes=None):
        rd = [in0]
        if not isinstance(s1, (int, float)):
            rd.append(s1)
        if s2 is not None and not isinstance(s2, (int, float)):
            rd.append(s2)
        if op1 is None:
            fn = lambda e: e.tensor_scalar(out=out, in0=in0, scalar1=s1, scalar2=None, op0=op0)
        else:
            fn = lambda e: e.tensor_scalar(out=out, in0=in0, scalar1=s1, scalar2=s2, op0=op0, op1=op1)
        self.op(eng, fn, reads=reads if reads is not None else rd,
                writes=writes if writes is not None else [out])

    def stt(self, out, in0, scalar, in1, op0, op1, reads=None, writes=None):
        rd = [in0, in1]
        if not isinstance(scalar, (int, float)):
            rd.append(scalar)
        self.op('vector', lambda e: e.scalar_tensor_tensor(out=out, in0=in0, scalar=scalar, in1=in1, op0=op0, op1=op1),
                reads=reads if reads is not None else rd,
                writes=writes if writes is not None else [out])

    def copy(self, eng, out, in_, reads=None, writes=None):
        if eng == 'scalar':
            fn = lambda e: e.activation(out=out, in_=in_, func=AF.Copy)
        else:
            fn = lambda e: e.tensor_copy(out=out, in_=in_)
        self.op(eng, fn, reads=reads if reads is not None else [in_],
                writes=writes if writes is not None else [out])

    def recip(self, out, in_):
        self.op('vector', lambda e: e.reciprocal(out=out, in_=in_), reads=[in_], writes=[out])

    def reduce(self, out, in_, op=ALU.add, axis=AX.X):
        self.op('vector', lambda e: e.tensor_reduce(out=out, in_=in_, axis=axis, op=op), reads=[in_], writes=[out])

    def memset(self, eng, ap, val):
        self.op(eng, lambda e: e.memset(ap, val), writes=[ap])

    def barrier(self):
        for e in ENGS:
            for s, v in self.semcount.items():
                if self.seen[e].get(s, 0) < v:
                    self.q[e].append(('wait', s, v))
                    self.seen[e][s] = v
        self.res = {}

    def collectives(self, fns):
        self.barrier()
        s = 'c_cc'
        for fn in fns:
            v = self.semcount.get(s, 0) + 1
            self.semcount[s] = v
            self.q['gpsimd'].append(('op', fn, s, 1))
        self.barrier()

    def emit(self):
        nc = self.nc
        for s, v in self.semcount.items():
            if self.seen['sync'].get(s, 0) < v:
                self.q['sync'].append(('wait', s, v))
        with contextlib.ExitStack() as es:
            semh = {s: es.enter_context(nc.semaphore(s)) for s in self.semcount}
            block = es.enter_context(nc.Block())

            def mk(engname):
                def body(e):
                    for it in self.q[engname]:
                        if it[0] == 'wait':
                            e.wait_ge(semh[it[1]], it[2])
                        else:
                            ins = it[1](e)
                            ins.then_inc(semh[it[2]], it[3])
                return body
            for engname in ENGS:
                if self.q[engname]:
                    getattr(block, engname)(mk(engname))


def AP_(t, offset, dims):
    return bass.AP(t, offset, [list(d) for d in dims])


class Ctx:
    def __init__(self):
        self.nc = bass.Bass("TRN2", target_bir_lowering=False)
        self.es = contextlib.ExitStack()
        self.P = Prog(self.nc)
        self.P.dram_names = set()

    def din(self, name, shape, dt):
        self.P.dram_names.add(name)
        return self.nc.dram_tensor(name, list(shape), dt, kind="ExternalInput").ap()

    def dout(self, name, shape, dt):
        self.P.dram_names.add(name)
        return self.nc.dram_tensor(name, list(shape), dt, kind="ExternalOutput").ap()

    tag = None

    def dint(self, name, shape, dt):
        self.P.dram_names.add(name)
        return self.nc.dram_tensor(name, list(shape), dt)

    def sb(self, name, shape, dt):
        if self.tag:
            name = name + '@' + self.tag
        return self.es.enter_context(self.nc.sbuf_tensor(name, list(shape), dt))

    def ps(self, name, shape, dt):
        if self.tag:
            name = name + '@' + self.tag
        return self.es.enter_context(self.nc.psum_tensor(name, list(shape), dt))

    @contextlib.contextmanager
    def phase(self, tag):
        old, oldtag = self.es, self.tag
        self.es, self.tag = contextlib.ExitStack(), tag
        try:
            yield
        finally:
            self.P.barrier()
            self.es.close()
            self.es, self.tag = old, oldtag

    def finish(self):
        self.P.emit()
        self.es.close()
        return self.nc


B_NM = 32
DFF = 4096
PAYR = 6176
KS_R0, KW_R0, KC_R0, VC_R0, VS_R0, VW_R0 = 0, 1024, 2048, 3072, 4096, 5136
TP = T + 128


def emit_A(C, l, xsrc, W, G):
    P = C.P
    with C.phase("A%d" % l):
        Wbf = C.sb("Wbf", [128, 8, INW], BF16)
        wst = [C.sb("wst%d" % i, [128, INW // 2], F32) for i in range(4)]
        gmix_s = C.sb("gmix_s", [128, 8], F32)
        gq_s = C.sb("gq_s", [128, 512], F32)
        gk3_s = C.sb("gk3_s", [128, 3, 128], F32)
        cw_s = C.sb("cw_s", [128, 4, 3], F32)
        xt = [C.sb("xt%d" % i, [128, D], F32) for i in range(3)]
        junk = C.sb("junk", [128, D], BF16)
        ss = [C.sb("ss%d" % i, [128, 1], F32) for i in range(2)]
        rstd = [C.sb("rstd%d" % i, [128, 1], F32) for i in range(2)]
        hn = [C.sb("hn%d" % i, [128, D], BF16) for i in range(2)]
        hT = [C.sb("hT%d" % i, [128, 8, 128], BF16) for i in range(2)]
        zs = [C.sb("zs%d" % i, [128, 1304], F32) for i in range(2)]
        hcs = C.sb("hcs", [128, 4, 128], F32)
        ub = [C.sb("ub%d" % i, [128, 4, 130], F32) for i in range(2)]
        bgs = [C.sb("bgs%d" % i, [128, 4, 128], F32) for i in range(2)]
        sq = C.sb("sq", [128, 1280], F32)
        ssg = C.sb("ssg", [128, 20], F32)
        rg = C.sb("rg", [128, 20], F32)
        qtmp = C.sb("qtmp", [128, 512], F32)
        ktmp = C.sb("ktmp", [128, 256], F32)
        nrm = C.sb("nrm", [128, 1024], BF16)
        trs = C.sb("trs", [128, 8, 128], BF16)
        vsw = C.sb("vsw", [128, 2, 130], BF16)
        gts = C.sb("gts", [128, 24], F32)
        cv0 = C.sb("cv0", [128, 4, 128], F32)
        cv1 = C.sb("cv1", [128, 4, 128], F32)
        yv = C.sb("yv", [128, 4, 128], F32)
        ut = C.sb("ut", [128, 4, 2], F32)
        bg2 = C.sb("bg2", [128, 4, 2], F32)
        pTa = C.ps("pTa", [128, 8, 128], BF16)
        pTb = C.ps("pTb", [128, 8, 128], BF16)
        pz = [C.ps("pz%d" % i, [128, 512], F32) for i in range(3)]
        pc = [C.ps("pc%d" % i, [128, 4, 128], F32) for i in range(3)]
        identb = G['identb']

        with P.group('cst'):
            P.dma('sync', gmix_s[:], W['gmix'][l])
            P.dma('sync', gq_s[:], W['gq'][l])
            P.dma('sync', gk3_s[:], W['gk3'][l])
            P.dma('sync', cw_s[:], W['cw'][l])
        P.memset('gpsimd', vsw[:], 1.0)
        P.memset('gpsimd', ub[0][:], 0.0)
        P.memset('gpsimd', ub[1][:], 0.0)
        for kc in range(8):
            for hf in range(2):
                st = wst[hf + 2 * (kc % 2)]
                P.dma('sync' if hf == 0 else 'scalar', st[:], W['w_in'][l, kc * 128:(kc + 1) * 128, hf * 1420:(hf + 1) * 1420])
                if hf == 0:
                    P.ts('vector', Wbf[:, kc, 0:1420], st[:], gmix_s[:, kc:kc + 1], ALU.mult)
                else:
                    P.act(Wbf[:, kc, 1420:2840], st[:], AF.Copy, scale=gmix_s[:, kc:kc + 1])

        pay, payu, qsc, gsc, ysc, bg2sc = G['pay'], G['payu'], G['qsc'], G['gsc'], G['ysc'], G['bg2sc']

        def X_load(m):
            P.dma('gpsimd', xt[m % 3][:], xsrc[m])

        def F_pre(m):
            xb = xt[m % 3]
            P.act(junk[:], xb[:], AF.Square, accum_out=ss[m % 2][:])
            P.act(rstd[m % 2][:], ss[m % 2][:], AF.Sqrt, bias=EPS, scale=1.0 / D)
            P.recip(rstd[m % 2][:], rstd[m % 2][:])
            P.act(hn[m % 2][:], xb[:], AF.Copy, scale=rstd[m % 2][:, 0:1])

        def F_pe_a(m):
            for kc in range(8):
                P.tr(pTa[:, kc, :], hn[m % 2][:, kc * 128:(kc + 1) * 128], identb[:])
            P.copy('vector', hT[m % 2][:], pTa[:])

        def F_pe_b(m):
            hTb, zb, ubb, bgb = hT[m % 2], zs[m % 2], ub[m % 2], bgs[m % 2]
            for bi, (c0, c1) in enumerate([(0, 512), (512, 1024), (1024, 1304)]):
                for kc in range(8):
                    P.mm(pz[bi][:, 0:c1 - c0], hTb[:, kc, :], Wbf[:, kc, c0:c1], start=(kc == 0), stop=(kc == 7))
                P.copy('scalar', zb[:, c0:c1], pz[bi][:, 0:c1 - c0])
            for ch in range(12):
                for kc in range(8):
                    P.mm(pc[ch // 4][:, ch % 4, :], Wbf[:, kc, 1304 + ch * 128:1304 + (ch + 1) * 128], hTb[:, kc, :],
                         start=(kc == 0), stop=(kc == 7))
                if ch == 3:
                    P.copy('scalar', hcs[:], pc[0][:])
                if ch == 7:
                    P.tt('vector', ubb[:, :, 2:130], pc[1][:], hcs[:], ALU.mult)
            P.copy('vector', bgb[:], pc[2][:])

        def G1(m):
            zb, ubb, bgb = zs[m % 2], ub[m % 2], bgs[m % 2]
            P.act(sq[:], zb[:, 0:1280], AF.Square)
            P.reduce(ssg[:], sq[:].rearrange("p (g d) -> p g d", d=64))
            P.act(rg[:, 0:8], ssg[:, 0:8], AF.Sqrt, bias=64 * EPS, scale=1.0)
            P.act(rg[:, 8:20], ssg[:, 8:20], AF.Sqrt, bias=EPS, scale=1.0 / 64)
            P.recip(rg[:], rg[:])
            P.tt('vector', qtmp[:].rearrange("p (g d) -> p g d", d=64), zb[:, 0:512].rearrange("p (g d) -> p g d", d=64),
                 AP_(rg, 0, [[20, 128], [1, 8], [0, 64]]), ALU.mult)
            P.tt('gpsimd', nrm[:, 0:512], qtmp[:], gq_s[:], ALU.mult)
            P.tt('vector', ktmp[:, 0:128].rearrange("p (g d) -> p g d", d=64), zb[:, 768:896].rearrange("p (g d) -> p g d", d=64),
                 AP_(rg, 12, [[20, 128], [1, 2], [0, 64]]), ALU.mult)
            P.tt('vector', ktmp[:, 128:256].rearrange("p (g d) -> p g d", d=64), zb[:, 1024:1152].rearrange("p (g d) -> p g d", d=64),
                 AP_(rg, 16, [[20, 128], [1, 2], [0, 64]]), ALU.mult)
            P.tt('gpsimd', nrm[:, 512:768], ktmp[:], gk3_s[:, 1:3, :].rearrange("p a d -> p (a d)"), ALU.mult)
            P.copy('gpsimd', nrm[:, 768:1024], zb[:, 512:768])
            P.copy('gpsimd', AP_(vsw, 1, [[260, 128], [65, 2], [1, 64]]), zb[:, 896:1024].rearrange("p (g d) -> p g d", d=64))
            P.copy('gpsimd', AP_(vsw, 131, [[260, 128], [65, 2], [1, 64]]), zb[:, 1152:1280].rearrange("p (g d) -> p g d", d=64))
            P.dma('scalar', AP_(pay['vs%d' % (m // 16)], (m % 16) * 128 * 130, [[130, 128], [1, 130]]), vsw[:, 0, :])
            P.dma('scalar', AP_(pay['vw%d' % (m // 16)], (m % 16) * 128 * 130, [[130, 128], [1, 130]]), vsw[:, 1, :])
            P.act(gts[:], zb[:, 1280:1304], AF.Sigmoid)
            P.dma('scalar', gsc[m], gts[:])
            P.tt('gpsimd', cv0[:], ubb[:, :, 2:130], AP_(cw_s, 2, [[12, 128], [3, 4], [0, 128]]), ALU.mult)
            P.tt('gpsimd', cv1[:], ubb[:, :, 1:129], AP_(cw_s, 1, [[12, 128], [3, 4], [0, 128]]), ALU.mult)
            P.tt('gpsimd', cv0[:], cv0[:], cv1[:], ALU.add)
            P.tt('gpsimd', cv1[:], ubb[:, :, 0:128], AP_(cw_s, 0, [[12, 128], [3, 4], [0, 128]]), ALU.mult)
            P.tt('gpsimd', cv0[:], cv0[:], cv1[:], ALU.add)
            P.tt('vector', yv[:], bgb[:], cv0[:], ALU.mult)
            P.dma('scalar', ysc[m].rearrange("p (c t) -> p c t", c=4), yv[:])
            P.copy('gpsimd', ut[:], ubb[:, :, 128:130])
            P.dma('sync', payu[m].rearrange("(p e) -> p e", e=8), ut[:].rearrange("p c e -> p (c e)"))
            P.copy('gpsimd', bg2[:], bgb[:, :, 0:2])
            P.dma('sync', bg2sc[m], bg2[:].rearrange("p c e -> p (c e)"))

        def G2(m):
            for bk in range(8):
                P.tr(pTb[:, bk, :], nrm[:, bk * 128:(bk + 1) * 128], identb[:])
            P.copy('vector', trs[:], pTb[:])
            for e in range(2):
                for g in range(2):
                    P.dma('sync' if g == 0 else 'scalar', AP_(qsc, m * 65536 + g * 32768 + e * 128, [[512, 64], [256, 2], [1, 128]]),
                          trs[e * 64:(e + 1) * 64, 2 * g:2 * g + 2, :])
            for ci, cn in enumerate(('ks', 'kw', 'kc', 'vc')):
                P.dma('sync' if ci % 2 == 0 else 'scalar', AP_(pay[cn], m * 128, [[4096, 128], [1, 128]]), trs[:, 4 + ci, :])

        X_load(0)
        X_load(1)
        X_load(2)
        F_pre(0)
        F_pre(1)
        F_pe_a(0)
        F_pe_b(0)
        for m in range(B_NM):
            if m + 3 < B_NM:
                X_load(m + 3)
            if m + 1 < B_NM:
                F_pe_a(m + 1)
            if m + 2 < B_NM:
                F_pre(m + 2)
            G1(m)
            if m + 1 < B_NM:
                F_pe_b(m + 1)
            G2(m)


def emit_Z(C, l, W, G):
    P = C.P
    with C.phase("Z%d" % l):
        kcT_all = C.sb("kcT_all", [128, TP], BF16)
        vcT_all = C.sb("vcT_all", [128, TP], BF16)
        W1bf = C.sb("W1bf", [128, 2, 32, 256], BF16)
        w1st = [C.sb("w1st%d" % i, [128, 8, 256], F32) for i in range(2)]
        W2bf = C.sb("W2bf", [128, 2, 2, 64], BF16)
        w2st = C.sb("w2st", [128, 2, 2, 64], F32)
        peT_s = C.sb("peT_s", [64, 2, 32], F32)
        pebf = C.sb("pebf", [64, 2, 32, 128], BF16)
        b1b_s = C.sb("b1b_s", [128, 2, 256], F32)
        b2b_s = C.sb("b2b_s", [128, 2, 64], F32)
        gkc = C.sb("gkc", [128, 128], F32)
        c1b = C.sb("c1b", [128, 2, 256], F32)
        hid = C.sb("hid", [128, 256], F32)
        g_x2 = C.sb("g_x2", [128, 256], F32)
        g_in = C.sb("g_in", [128, 256], F32)
        g_sg = C.sb("g_sg", [128, 256], F32)
        hbf = C.sb("hbf", [128, 256], BF16)
        hidT = C.sb("hidT", [128, 2, 128], BF16)
        co = C.sb("co", [128, 2, 2, 64], F32)
        cosq = C.sb("cosq", [128, 128], F32)
        css = C.sb("css", [128, 2], F32)
        crs = C.sb("crs", [128, 2], F32)
        kcn = C.sb("kcn", [128, 128], BF16)
        pT = C.ps("pT", [128, 8, 128], BF16)
        px = [C.ps("px%d" % i, [128, 512], F32) for i in range(2)]
        po = C.ps("po", [128, 512], F32)
        identb = G['identb']
        gath = G['gath']
        kcT_s, vcx_s = G['kcT_s'], G['vcx_s']

        with P.group('cst'):
            P.dma('sync', b1b_s[:], W['b1b'][l].rearrange("k p n -> p k n"))
            P.dma('sync', b2b_s[:], W['b2b'][l])
            P.dma('sync', peT_s[:], W['peT'][l].rearrange("k d j -> d k j"))
            P.dma('sync', w2st[:], W['w2'][l].rearrange("k (c p) d -> p k c d", p=128))
            P.dma('sync', gkc[:], W['gk3'][l, :, 0, :])
        P.memset('gpsimd', kcT_all[:, T:TP], 0.0)
        P.memset('gpsimd', vcT_all[:, T:TP], 0.0)
        with P.group('kvcl'):
            for r in range(4):
                for kv, dst in enumerate((kcT_all, vcT_all)):
                    P.dma('sync' if kv == 0 else 'scalar', AP_(dst, r * 128, [[TP, 128], [512, 32], [1, 128]]),
                          AP_(gath['kc' if kv == 0 else 'vc'], r * 1024 * 512, [[4096, 128], [128, 32], [1, 128]]),
                          writes=[dst])
        P.copy('vector', W2bf[:], w2st[:])
        P.copy('vector', pebf[:], AP_(peT_s, 0, [[64, 64], [32, 2], [1, 32], [0, 128]]))
        n = 0
        for kv in range(2):
            for jq in range(4):
                st = w1st[n % 2]
                n += 1
                src = W['w1'][l, kv, jq * 512:(jq + 1) * 512, :].rearrange("(j d) n -> d j n", d=64)
                with P.group("w1st%d" % ((n - 1) % 2)):
                    P.dma('sync', st[0:64, :, :], src, writes=[st])
                    P.dma('scalar', st[64:128, :, :], src, writes=[st])
                P.copy('vector' if jq % 2 == 0 else 'gpsimd', W1bf[:, kv, jq * 8:(jq + 1) * 8, :], st[:])
        for kv in range(2):
            for j in range(32):
                P.mm(px[0][:, 0:256], pebf[:, kv, j, :], W1bf[0:64, kv, j, :], start=(j == 0), stop=(j == 31))
            P.tt('vector', c1b[:, kv, :], px[0][:, 0:256], b1b_s[:, kv, :], ALU.add)
        n = 0
        for ib in range(8):
            for kv in range(2):
                src_all = kcT_all if kv == 0 else vcT_all
                for g in range(2):
                    pp = px[n % 2]
                    n += 1
                    base = 16 * 128 * ib
                    for j in range(32):
                        lhs = AP_(src_all, 64 * g * TP + base + j, [[TP, 64], [16, 128]])
                        P.mm(pp[:, 0:256], lhs, W1bf[64 * g:64 * g + 64, kv, j, :], start=(j == 0), stop=(j == 31),
                             reads=[src_all, W1bf])
                    P.tt('vector', hid[:], pp[:, 0:256], c1b[:, kv, :], ALU.add)
                    P.act(g_x2[:], hid[:], AF.Square)
                    P.ts('gpsimd', g_x2[:], g_x2[:], 0.044715, ALU.mult, 1.0, ALU.add)
                    P.tt('gpsimd', g_in[:], g_x2[:], hid[:], ALU.mult)
                    P.act(g_sg[:], g_in[:], AF.Sigmoid, scale=1.5957691216057308)
                    P.tt('vector', hbf[:], hid[:], g_sg[:], ALU.mult)
                    for c in range(2):
                        P.tr(pT[:, c, :], hbf[:, c * 128:(c + 1) * 128], identb[:])
                    P.copy('vector', hidT[:], pT[:, 0:2, :])
                    for c in range(2):
                        P.mm(po[:, 0:64], hidT[:, c, :], W2bf[:, kv, c, :], start=(c == 0), stop=(c == 1))
                    P.tt('vector', co[:, kv, g, :], po[:, 0:64], b2b_s[:, kv, :], ALU.add)
            P.act(cosq[:], co[:, 0, :, :].rearrange("p g d -> p (g d)"), AF.Square)
            P.reduce(css[:], cosq[:].rearrange("p (g d) -> p g d", d=64))
            P.act(crs[:], css[:], AF.Sqrt, bias=EPS, scale=1.0 / 64)
            P.recip(crs[:], crs[:])
            P.tt('vector', cosq[:].rearrange("p (g d) -> p g d", d=64), co[:, 0, :, :], AP_(crs, 0, [[2, 128], [1, 2], [0, 64]]), ALU.mult)
            P.tt('vector', kcn[:], cosq[:], gkc[:], ALU.mult)
            for g in range(2):
                P.tr(pT[0:64, 4 + g, :], kcn[:, 64 * g:64 * g + 64], identb[:])
            P.copy('vector', kcT_s[0:64, :, ib * 128:(ib + 1) * 128], pT[0:64, 4:6, :])
            P.copy('gpsimd', vcx_s[:, ib, :, 0:64], co[:, 1, :, :])
        if G.get('dbg'):
            P.dma('sync', G['dbg']['kc'], kcT_s[:])
            P.dma('sync', G['dbg']['vc'], vcx_s[:])


def emit_B(C, l, W, G):
    P = C.P
    with C.phase("B%d" % l):
        ksT_s = [C.sb("ksT_s%d" % g, [68, T], BF16) for g in range(2)]
        vsx_s = C.sb("vsx_s", [128, 128, 130], BF16)
        smask_s = C.sb("smask_s", [128, 4, 512], BF16)
        cmask_s = C.sb("cmask_s", [128, 5, 512], BF16)
        wmask_s = C.sb("wmask_s", [128, 8, 128], BF16)
        Eall_s = C.sb("Eall_s", [128, 64, 128], BF16)
        Bsel_s = C.sb("Bsel_s", [128, 512], F32)
        qT_s = [C.sb("qT_s%d" % i, [68, 2, 512], BF16) for i in range(2)]
        kwT_s = [C.sb("kwT_s%d" % i, [68, 2, 1024], BF16) for i in range(2)]
        vwx_s = [C.sb("vwx_s%d" % i, [128, 8, 130], BF16) for i in range(2)]
        gates_s = [C.sb("gates_s%d" % i, [128, 24], F32) for i in range(2)]
        NPB = 6
        Pb = [C.sb("Pb%d" % i, [128, 512], BF16) for i in range(NPB)]
        Pc = [C.sb("Pc%d" % g, [128, 8, 512], BF16) for g in range(2)]
        oc = C.sb("oc", [128, 4, 321], F32)
        den4 = C.sb("den4", [128, 4], F32)
        rd4 = C.sb("rd4", [128, 4], F32)
        coef4 = C.sb("coef4", [128, 4], F32)
        imp = [C.sb("imp%d" % g, [128, 256], F32) for g in range(2)]
        tmpi = C.sb("tmpi", [128, 256], F32)
        m8 = C.sb("m8", [128, 16], F32)
        thr = C.sb("thr", [128, 1], F32)
        nm = [C.sb("nm%d" % g, [128, 256], BF16) for g in range(2)]
        nmT = [C.sb("nmT%d" % g, [128, 2, 128], BF16) for g in range(2)]
        osT = C.sb("osT", [65, 512], F32)
        otmp = C.sb("otmp", [128, 4, 64], F32)
        acc = C.sb("acc", [128, 8, 64], F32)
        accb = C.sb("accb", [128, 512], BF16)
        ssq = C.sb("ssq", [128, 1], F32)
        rat = C.sb("rat", [128, 1], F32)
        rat2 = C.sb("rat2", [128, 1], F32)
        atT = C.sb("atT", [128, 4, 128], BF16)
        ps_s = [C.ps("ps_s%d" % i, [128, 512], F32) for i in range(4)]
        ps_o = [C.ps("ps_o%d" % i, [128, 512], F32) for i in range(1)]
        ps_acc = [C.ps("ps_acc%d" % i, [128, 512], F32) for i in range(2)]
        pm = C.ps("pm", [128, 512], F32)
        pm_b = pm[:].bitcast(BF16)
        identb, identf = G['identb'], G['identf']
        kcT_s, vcx_s = G['kcT_s'], G['vcx_s']
        gath, qsc, gsc, atsc, rasc = G['gath'], G['qsc'], G['gsc'], G['atsc'], G['rasc']

        with P.group('cst'):
            P.dma('sync', cmask_s[:], W['cmask'])
            P.dma('sync', Bsel_s[:], W['Bsel'])
            P.dma('scalar', wmask_s[:], W['wmask'])
            P.dma('scalar', smask_s[:], W['smask'])
            P.dma('scalar', Eall_s[:], W['Eall'])
        with P.group('ksT'):
            for g in range(2):
                P.dma('gpsimd', ksT_s[g][64:68, :], W['kaug'], writes=[ksT_s[g]])
                for r in range(4):
                    P.dma('sync' if r % 2 == 0 else 'scalar', ksT_s[g][0:64, r * 4096:(r + 1) * 4096],
                          AP_(gath['ks'], r * 1024 * 512 + 64 * g * 4096, [[4096, 64], [1, 4096]]),
                          writes=[ksT_s[g]])
        with P.group('vsx'):
            for r in range(4):
                for hv in range(2):
                    P.dma('gpsimd', vsx_s[:, r * 32 + hv * 16:r * 32 + hv * 16 + 16, :],
                          AP_(gath['vs%d' % hv], r * 520 * 512, [[130, 128], [128 * 130, 16], [1, 130]]), writes=[vsx_s])

        if G.get('dbg'):
            P.dma('sync', G['dbg']['ks'], ksT_s[1][:])
            P.dma('sync', G['dbg']['vs'], vsx_s[:])
        state = {'s': 0, 'p': 0, 'x': 0}
        pending = []

        LA = 4

        def run_stream(tiles):
            n = len(tiles)
            bufs = []
            for i in range(n + LA):
                if i < n:
                    ps = ps_s[state['s'] % 4]
                    pb = Pb[state['p'] % NPB]
                    state['s'] += 1
                    state['p'] += 1
                    tiles[i][0](ps)
                    P.act(pb[:], ps[:], AF.Exp)
                    bufs.append(pb)
                j = i - (LA - 2)
                if 0 <= j < n and tiles[j][1] is not None:
                    tiles[j][1](bufs[j])
                if i - LA >= 0:
                    tiles[i - LA][2](bufs[i - LA])

        def mask_mul(pb, map_, mkeys):
            P.tt('vector', pb[:].rearrange("p (a q) -> p a q", a=4), pb[:].rearrange("p (a q) -> p a q", a=4), map_,
                 ALU.mult, reads=[pb] + mkeys, writes=[pb])

        def o_epilogue(psacc, g, br, gs):
            P.copy('vector', osT[:], psacc[0:65, :])
            pmv = pm[:, 0:260].rearrange("p (r d) -> p r d", d=65)
            for r in range(4):
                P.tr(pmv[:, r, :], osT[0:65, r * 128:(r + 1) * 128], identf[0:65, 0:65])
            P.recip(rd4[:], pmv[:, :, 0])
            P.tt('vector', coef4[:], rd4[:], AP_(gs, 12 * g + br, [[24, 128], [3, 4]]), ALU.mult)
            dst = acc[:, 4 * g:4 * g + 4, :]
            P.tt('vector', otmp[:], pmv[:, :, 1:65], AP_(coef4, 0, [[4, 128], [1, 4], [0, 64]]), ALU.mult)
            P.tt('gpsimd', dst, dst, otmp[:], ALU.add)

        for m in range(B_NM):
            qs = qT_s[m % 2]
            kws = kwT_s[m % 2]
            vws = vwx_s[m % 2]
            gs = gates_s[m % 2]
            h0 = 0 if m > 0 else 1
            with P.group("qT_s%d" % (m % 2)):
                P.dma('sync', qs[0:64, :, :], qsc[m].rearrange("g d n -> d g n"), writes=[qs])
                P.dma('scalar', qs[64:68, :, :], W['qaug'][m], writes=[qs])
            P.dma('scalar', gs[:], gsc[m])
            nh = 2 - h0
            c0 = 128 * (m - 1 + h0)
            with P.group("kws%d" % (m % 2)):
                for r in range(4):
                    P.dma('sync' if r % 2 == 0 else 'scalar',
                          AP_(kws, (r * 2 + h0) * 128, [[2048, 64], [1024, 2], [1, 128 * nh]]),
                          AP_(gath['kw'], r * 1024 * 512 + c0, [[4096, 64], [64 * 4096, 2], [1, 128 * nh]]),
                          writes=[kws])
            with P.group("vws%d" % (m % 2)):
                for r in range(4):
                    for hf in range(h0, 2):
                        mp = m - 1 + hf
                        P.dma('gpsimd', vws[:, r * 2 + hf, :],
                              AP_(gath['vw%d' % (mp // 16)], r * 520 * 512 + (mp % 16) * 128 * 130, [[130, 128], [1, 130]]),
                              writes=[vws])
            for g in range(2):
                P.copy('gpsimd', AP_(kws, 64 * 2048 + g * 1024 + h0 * 128, [[2048, 4], [256, 4], [1, 128 * (2 - h0)]]),
                       AP_(ksT_s[0], 64 * T + 128 * (m - 1 + h0), [[T, 4], [4096, 4], [1, 128 * (2 - h0)]]),
                       reads=[ksT_s[0]], writes=[kws])
            n_it = (32 * m + 30) // 128 + 1
            for g in range(2):
                for it in range(n_it):
                    ps = ps_s[state['s'] % 4]
                    state['s'] += 1
                    delta = 128 * it - 32 * m
                    masked = delta >= -128
                    P.mm(ps[:], kcT_s[:, g, it * 128:(it + 1) * 128], qs[:, g, :], start=True, stop=not masked)
                    if masked:
                        P.mm(ps[:], identb[:], cmask_s[:, (-delta) // 32, :], start=False, stop=True)
                    P.act(Pc[g][:, it, :], ps[:], AF.Exp)
                for r in range(4):
                    po = ps_o[0]
                    for it in range(n_it):
                        P.mm(po[:, 0:321], Pc[g][:, it, r * 128:(r + 1) * 128], vcx_s[:, it, g, :], start=(it == 0), stop=(it == n_it - 1))
                    P.copy('scalar', oc[:, r, :], po[:, 0:321])
                P.ts('vector', den4[:], oc[:, :, 64], 1e-30, ALU.max)
                P.recip(rd4[:], den4[:])
                P.ts('vector', imp[g][:], oc[:, 0, 65:321], rd4[:, 0:1], ALU.mult)
                for r in range(1, 4):
                    P.stt(imp[g][:], oc[:, r, 65:321], rd4[:, r:r + 1], imp[g][:], ALU.mult, ALU.add)
                P.tt('vector', coef4[:], rd4[:], AP_(gs, 12 * g + 0, [[24, 128], [3, 4]]), ALU.mult)
                P.tt('vector', acc[:, 4 * g:4 * g + 4, :], oc[:, :, 0:64], AP_(coef4, 0, [[4, 128], [1, 4], [0, 64]]), ALU.mult)
                P.tt('vector', imp[g][:], imp[g][:], Bsel_s[:, 256 - 8 * m:512 - 8 * m], ALU.add)
                P.ts('vector', imp[g][:, 0:1], imp[g][:, 0:1], 1e4, ALU.add)
                P.op('vector', lambda e, g=g: e.max(out=m8[:, 0:8], in_=imp[g][:]), reads=[imp[g]], writes=[m8])
                P.op('vector', lambda e, g=g: e.match_replace(out=tmpi[:], in_to_replace=m8[:, 0:8], in_values=imp[g][:], imm_value=-3.0e38),
                     reads=[imp[g], m8], writes=[tmpi])
                P.op('vector', lambda e: e.max(out=m8[:, 8:16], in_=tmpi[:]), reads=[tmpi], writes=[m8])
                P.ts('vector', thr[:], m8[:, 15:16], -1e29, ALU.max)
                P.ts('vector', nm[g][:], imp[g][:], thr[:, 0:1], ALU.is_ge)
            if pending:
                pending.pop()()
            tiles = []
            wt = [(r, hf) for hf in range(h0, 2) for r in range(4)]
            for g in range(2):
                for idx, (r, hf) in enumerate(wt):
                    def eS(ps, g=g, r=r, hf=hf):
                        P.mm(ps[:], kws[:, g, (r * 2 + hf) * 128:(r * 2 + hf + 1) * 128], qs[:, g, :], start=True, stop=(hf == 0))
                        if hf == 1:
                            P.mm(ps[:], identb[:], smask_s[:, r, :], start=False, stop=True)

                    def eX(pb, r=r):
                        mask_mul(pb, AP_(wmask_s, r * 128, [[1024, 128], [0, 4], [1, 128]]), [wmask_s])

                    def ePV(pb, g=g, r=r, hf=hf, idx=idx):
                        P.mm(ps_acc[g][0:65, :], vws[:, r * 2 + hf, 65 * g:65 * g + 65], pb[:], start=(idx == 0), stop=(idx == len(wt) - 1))
                    tiles.append((eS, eX if hf == 0 else None, ePV))
            run_stream(tiles)
            for g in range(2):
                for hb in range(2):
                    P.tr(pm_b[:, hb * 128:(hb + 1) * 128], nm[g][:, hb * 128:(hb + 1) * 128], identb[:])
                P.copy('vector', nmT[g][:].rearrange("p a q -> p (a q)"), pm_b[:, 0:256], reads=[pm])
            for g in range(2):
                o_epilogue(ps_acc[g], g, 2, gs)
            nkt = 4 * m + 4
            tiles = []
            for g in range(2):
                for kt in range(nkt):
                    def eS(ps, g=g, kt=kt):
                        diag = kt >= 4 * m
                        col = ((kt % 4) * 32 + kt // 4) * 128
                        P.mm(ps[:], ksT_s[g][:, col:col + 128], qs[:, g, :], start=True, stop=not diag)
                        if diag:
                            P.mm(ps[:], identb[:], smask_s[:, kt - 4 * m, :], start=False, stop=True)

                    def eX(pb, g=g, kt=kt):
                        bank = (ps_o[0], pm)[state['x'] % 2]
                        state['x'] += 1
                        P.mm(bank[:, 0:128], Eall_s[:, kt % 64, :], nmT[g][:, kt // 64, :], start=True, stop=True)
                        mask_mul(pb, AP_(bank, 0, [[512, 128], [0, 4], [1, 128]]), [bank])

                    def ePV(pb, g=g, kt=kt):
                        P.mm(ps_acc[g][0:65, :], vsx_s[:, (kt % 4) * 32 + kt // 4, 65 * g:65 * g + 65], pb[:], start=(kt == 0), stop=(kt == nkt - 1))
                    tiles.append((eS, eX, ePV))
            run_stream(tiles)
            for g in range(2):
                o_epilogue(ps_acc[g], g, 1, gs)
            accf = acc[:].rearrange("p h d -> p (h d)")
            P.act(accb[:], accf, AF.Square, accum_out=ssq[:])
            P.act(rat[:], ssq[:], AF.Sqrt, bias=EPS, scale=1.0 / 512)
            P.recip(rat2[:], rat[:])
            P.dma('gpsimd', rasc[m], rat2[:])
            P.copy('gpsimd', accb[:], accf)
            if G.get('dbg'):
                P.dma('gpsimd', G['dbg']['ra'][m], rat2[:])

            def finish_out(m=m):
                for bk in range(4):
                    P.tr(pm_b[:, bk * 128:(bk + 1) * 128], accb[:, bk * 128:(bk + 1) * 128], identb[:])
                P.copy('vector', atT[:].rearrange("p b q -> p (b q)"), pm_b[:, 0:512], reads=[pm])
                P.dma('gpsimd', atsc[m], atT[:].rearrange("p b q -> p (b q)"))
                if G.get('dbg'):
                    P.dma('gpsimd', G['dbg']['at'][m], atT[:].rearrange("p b q -> p (b q)"))
            pending.append(finish_out)
        pending.pop()()


def emit_C(C, l, xsrc, xdst, W, G):
    P = C.P
    with C.phase("C%d" % l):
        Wo = C.sb("Wo", [128, 8, D], BF16)
        Wdn = C.sb("Wdn", [128, 32, D], BF16)
        wst = [C.sb("wst%d" % i, [128, 8, 256], F32) for i in range(2)]
        Wup = [C.sb("Wup%d" % i, [128, 8, 256], BF16) for i in range(4)]
        gout_s = C.sb("gout_s", [128, 8], F32)
        gffn_s = C.sb("gffn_s", [128, D], F32)
        cw_s = C.sb("cw_s", [128, 4, 3], F32)
        selw_s = C.sb("selw_s", [128, 4], F32)
        xt = [C.sb("xt%d" % i, [128, D], F32) for i in range(2)]
        x1 = [C.sb("x1_%d" % i, [128, D], F32) for i in range(4)]
        at_s = [C.sb("at_s%d" % i, [128, 4, 128], BF16) for i in range(2)]
        yv = [C.sb("yv%d" % i, [128, 4, 128], F32) for i in range(2)]
        bg2 = [C.sb("bg2_%d" % i, [128, 4, 2], F32) for i in range(2)]
        tl = [C.sb("tl%d" % i, [128, 4, 8], F32) for i in range(2)]
        halo = C.sb("halo", [128, 4, 2], F32)
        pt0 = C.sb("pt0", [128, 4], F32)
        pt1 = C.sb("pt1", [128, 4], F32)
        ysq = [C.sb("ysq%d" % i, [128, 4, 128], F32) for i in range(2)]
        ybf = [C.sb("ybf%d" % i, [128, 4, 128], BF16) for i in range(2)]
        rc = C.sb("rc", [128, 1], F32)
        rc2 = [C.sb("rc2_%d" % i, [128, 1], F32) for i in range(2)]
        ra_s = [C.sb("ra_s%d" % i, [128, 1], F32) for i in range(2)]
        ss = C.sb("ss", [128, 1], F32)
        rstd = C.sb("rstd", [128, 1], F32)
        h2n = [C.sb("h2n%d" % i, [128, D], BF16) for i in range(2)]
        h2T = C.sb("h2T", [128, 8, 512], BF16)
        rl = [C.sb("rl%d" % i, [128, 512], F32) for i in range(2)]
        actT = C.sb("actT", [128, 32, 512], BF16)
        pa = [C.ps("pa%d" % i, [128, 512], F32) for i in range(2)]
        pcv = [C.ps("pcv%d" % i, [128, 512], F32) for i in range(2)]
        pu = [C.ps("pu%d" % i, [128, 512], F32) for i in range(2)]
        pT = C.ps("pT", [128, 8, 128], BF16)
        px = C.ps("px", [128, 512], F32)
        identb, onesf = G['identb'], G['onesf']
        gathu, ysc, bg2sc, atsc, rasc = G['gathu'], G['ysc'], G['bg2sc'], G['atsc'], G['rasc']

        with P.group('cst'):
            P.dma('sync', gout_s[:], W['gout'][l])
            P.dma('sync', gffn_s[:], W['gffn'][l])
            P.dma('sync', cw_s[:], W['cw'][l])
            P.dma('sync', selw_s[:], W['selw'])
        n = 0
        for kc in range(8):
            for hf in range(2):
                st = wst[n % 2]
                n += 1
                stw = AP_(st, 0, [[2048, 128], [1, 512]])
                P.dma('sync' if hf == 0 else 'scalar', stw, W['w_o'][l, kc * 128:(kc + 1) * 128, hf * 512:(hf + 1) * 512], writes=[st])
                if hf == 0:
                    P.ts('vector', Wo[:, kc, 0:512], stw, gout_s[:, kc:kc + 1], ALU.mult, reads=[st, gout_s])
                else:
                    P.act(Wo[:, kc, 512:1024], stw, AF.Copy, scale=gout_s[:, kc:kc + 1], reads=[st, gout_s])
        for f2 in range(16):
            st = wst[n % 2]
            n += 1
            stv = AP_(st, 0, [[2048, 128], [1024, 2], [1, 1024]])
            P.dma('sync' if f2 % 2 == 0 else 'scalar', stv, W['w_dn'][l, f2 * 256:(f2 + 1) * 256, :].rearrange("(a p) n -> p a n", p=128), writes=[st])
            P.copy(('vector', 'scalar', 'vector', 'gpsimd')[f2 % 4], Wdn[:, f2 * 2:(f2 + 1) * 2, :], stv, reads=[st])
        wupb = G['wupb']
        for kc in range(8):
            for half in range(2):
                st = wst[n % 2]
                wb = Wup[n % 2]
                stf = AP_(st, 0, [[2048, 128], [1, 2048]])
                P.dma(('sync', 'scalar')[n % 2], stf, W['w_up'][l, kc * 128:(kc + 1) * 128, half * 2048:(half + 1) * 2048], writes=[st])
                P.copy(('vector', 'scalar', 'gpsimd', 'scalar')[n % 4], wb[:].rearrange("p a n -> p (a n)"), stf, reads=[st])
                P.dma(('gpsimd', 'sync', 'scalar')[n % 3], AP_(wupb, half * 8 * 128 * 2048 + kc * 256, [[2048, 128], [128 * 2048, 8], [1, 256]]),
                      wb[:], reads=[wb], writes=[], sem='wupb')
                n += 1
        P.res['wupb'] = {'w': ('d_wupb', P.semcount['d_wupb']), 'r': {}}
        up_n = [0]

        def S1a(m):
            xb, ab, rab, yb, bgb, tlb = xt[m % 2], at_s[m % 2], ra_s[m % 2], yv[m % 2], bg2[m % 2], tl[m % 2]
            P.dma('sync', xb[:], xsrc[m])
            P.dma('scalar', ab[:], atsc[m].rearrange("p (b q) -> p b q", b=4))
            P.dma('sync', rab[:], rasc[m])
            P.dma('scalar', yb[:], ysc[m].rearrange("p (c t) -> p c t", c=4))
            P.dma('sync', bgb[:], bg2sc[m].rearrange("p (c e) -> p c e", e=2))
            if m == 0:
                P.memset('gpsimd', tlb[:, 0, :], 0.0)
            with P.group("tl%d" % (m % 2)):
                if m > 0:
                    P.dma('gpsimd', tlb[:, 0, :], gathu[3 * B_NM + m - 1].rearrange("(p e) -> p e", e=8), writes=[tlb])
                for cd in range(1, 4):
                    P.dma('gpsimd', tlb[:, cd, :], gathu[(cd - 1) * B_NM + m].rearrange("(p e) -> p e", e=8), writes=[tlb])
            hf = halo[:].rearrange("p c e -> p (c e)")
            P.ts('vector', hf, tlb[:, 0, :], selw_s[:, 0:1], ALU.mult)
            for cd in range(1, 4):
                P.stt(hf, tlb[:, cd, :], selw_s[:, cd:cd + 1], hf, ALU.mult, ALU.add)
            P.tt('vector', pt0[:], halo[:, :, 1], cw_s[:, :, 1], ALU.mult)
            P.tt('vector', pt1[:], halo[:, :, 0], cw_s[:, :, 0], ALU.mult)
            P.tt('vector', pt0[:], pt0[:], pt1[:], ALU.add)
            P.tt('vector', pt0[:], pt0[:], bgb[:, :, 0], ALU.mult)
            P.tt('vector', yb[:, :, 0], yb[:, :, 0], pt0[:], ALU.add)
            P.tt('vector', pt1[:], halo[:, :, 1], cw_s[:, :, 0], ALU.mult)
            P.tt('vector', pt1[:], pt1[:], bgb[:, :, 1], ALU.mult)
            P.tt('vector', yb[:, :, 1], yb[:, :, 1], pt1[:], ALU.add)
            P.act(ysq[m % 2][:], yb[:], AF.Square)
            P.copy('gpsimd', ybf[m % 2][:], yb[:])

        def S1b(m, j):
            xb, ab, rab = xt[m % 2], at_s[m % 2], ra_s[m % 2]
            for ch in range(4):
                P.mm(px[:, 0:1], ysq[m % 2][:, ch, :], onesf[:], start=(ch == 0), stop=(ch == 3))
            P.act(rc[:], px[:, 0:1], AF.Sqrt, bias=EPS, scale=1.0 / 512)
            P.recip(rc2[m % 2][:], rc[:])
            for nh in range(2):
                for kc in range(4):
                    P.mm(pa[nh][:], ab[:, kc, :], Wo[:, kc, nh * 512:(nh + 1) * 512], start=(kc == 0), stop=(kc == 3))
                for kc in range(4):
                    P.mm(pcv[nh][:], ybf[m % 2][:, kc, :], Wo[:, 4 + kc, nh * 512:(nh + 1) * 512], start=(kc == 0), stop=(kc == 3))
            x1b = x1[j]
            for nh in range(2):
                sl = slice(nh * 512, (nh + 1) * 512)
                P.stt(x1b[:, sl], pa[nh][:], rab[:, 0:1], xb[:, sl], ALU.mult, ALU.add)
                P.stt(x1b[:, sl], pcv[nh][:], rc2[m % 2][:, 0:1], x1b[:, sl], ALU.mult, ALU.add)
            if G.get('dbg'):
                P.dma('gpsimd', G['dbg']['x1'][m], x1b[:])
                P.dma('gpsimd', G['dbg']['y'][m], ybf[m % 2][:].rearrange("p c t -> p (c t)"))
                P.dma('gpsimd', G['dbg']['rc'][m], rc2[m % 2][:])
            P.act(h2n[m % 2][:], x1b[:], AF.Square, accum_out=ss[:])
            P.act(rstd[:], ss[:], AF.Sqrt, bias=EPS, scale=1.0 / D)
            P.recip(rstd[:], rstd[:])
            P.stt(h2n[m % 2][:], x1b[:], rstd[:, 0:1], gffn_s[:], ALU.mult, ALU.mult)

        def S1c(m, j):
            for kc in range(8):
                P.tr(pT[:, kc, :], h2n[m % 2][:, kc * 128:(kc + 1) * 128], identb[:])
            P.copy('scalar', h2T[:, :, j * 128:(j + 1) * 128], pT[:])

        NBT = B_NM // 4
        for j in range(4):
            S1a(j)
            S1b(j, j)
            S1c(j, j)
        for bt in range(NBT):
            for u in range(16):
                wb = Wup[up_n[0] % 4]
                up_n[0] += 1
                P.dma('sync' if u % 2 == 0 else 'scalar', wb[:].rearrange("p a n -> p (a n)"),
                      AP_(wupb, u * 128 * 2048, [[2048, 128], [1, 2048]]), reads=['wupb'], writes=[wb])
                for fl in range(2):
                    f = u * 2 + fl
                    pp = pu[f % 2]
                    for kc in range(8):
                        P.mm(pp[:], wb[:, kc, fl * 128:(fl + 1) * 128], h2T[:, kc, :], start=(kc == 0), stop=(kc == 7))
                    rb = rl[f % 2]
                    P.act(rb[:], pp[:], AF.Relu)
                    P.tt('gpsimd' if f % 2 == 0 else 'vector', actT[:, f, :], rb[:], rb[:], ALU.mult)
            nxt = bt + 1 < NBT
            for j in range(4):
                m = bt * 4 + j
                mn = m + 4
                if nxt:
                    S1a(mn)
                ob = x1[j]
                for nh in range(2):
                    pp = pu[nh]
                    for f in range(32):
                        P.mm(pp[:], actT[:, f, j * 128:(j + 1) * 128], Wdn[:, f, nh * 512:(nh + 1) * 512], start=(f == 0), stop=(f == 31))
                    P.tt('vector', ob[:, nh * 512:(nh + 1) * 512], pp[:], x1[j][:, nh * 512:(nh + 1) * 512], ALU.add)
                P.dma('sync', xdst[m], ob[:])
                if nxt:
                    S1b(mn, j)
                    if j >= 1:
                        S1c(mn - 1, j - 1)
            if nxt:
                S1c(bt * 4 + 7, 3)


def build_fused(nlayers=DEPTH, stages="AGZBC", dbg=False):
    C = Ctx()
    P = C.P
    W = {}
    W['x0'] = C.din("x0", [B_NM, 128, D], F32)
    W['w_in'] = C.din("w_in", [nlayers, D, INW], F32)
    W['gmix'] = C.din("gmix", [nlayers, 128, 8], F32)
    W['gq'] = C.din("gq", [nlayers, 128, 512], F32)
    W['gk3'] = C.din("gk3", [nlayers, 128, 3, 128], F32)
    W['cw'] = C.din("cw", [nlayers, 128, 4, 3], F32)
    W['peT'] = C.din("peT", [nlayers, 2, 64, 32], F32)
    W['w1'] = C.din("w1", [nlayers, 2, 2048, 256], F32)
    W['b1b'] = C.din("b1b", [nlayers, 2, 128, 256], F32)
    W['w2'] = C.din("w2", [nlayers, 2, 256, 64], F32)
    W['b2b'] = C.din("b2b", [nlayers, 128, 2, 64], F32)
    W['w_o'] = C.din("w_o", [nlayers, D, D], F32)
    W['gout'] = C.din("gout", [nlayers, 128, 8], F32)
    W['gffn'] = C.din("gffn", [nlayers, 128, D], F32)
    W['w_up'] = C.din("w_up", [nlayers, D, DFF], F32)
    W['w_dn'] = C.din("w_dn", [nlayers, DFF, D], F32)
    W['smask'] = C.din("smask", [128, 4, 512], BF16)
    W['cmask'] = C.din("cmask", [128, 5, 512], BF16)
    W['wmask'] = C.din("wmask", [128, 8, 128], BF16)
    W['Eall'] = C.din("Eall", [128, 64, 128], BF16)
    W['Bsel'] = C.din("Bsel", [128, 512], F32)
    W['kaug'] = C.din("kaug", [4, T], BF16)
    W['kcaug'] = C.din("kcaug", [4, 2, 1024], BF16)
    W['qaug'] = C.din("qaug", [B_NM, 4, 2, 512], BF16)
    W['vcxc'] = C.din("vcxc", [128, 8, 2, 257], BF16)
    W['selw'] = C.din("selw", [128, 4], F32)
    identb_d = C.din("identb", [128, 128], BF16)
    identf_d = C.din("identf", [128, 128], F32)
    onesf_d = C.din("onesf", [128, 1], F32)
    xo = C.dout("xo", [B_NM, 128, D], F32)

    G = {}
    COMPS = [('ks', 1024), ('kw', 1024), ('kc', 1024), ('vc', 1024), ('vs0', 520), ('vs1', 520), ('vw0', 520), ('vw1', 520)]
    G['pay'] = {cn: C.dint("pay_" + cn, [rows, 512], BF16) for cn, rows in COMPS}
    G['gath'] = {cn: C.dint("gath_" + cn, [4 * rows, 512], BF16) for cn, rows in COMPS}
    G['payu'] = C.dint("payu", [B_NM, 1024], F32).ap()
    G['gathu'] = C.dint("gathu", [4 * B_NM, 1024], F32).ap()
    xbuf = C.dint("xbuf", [B_NM, 128, D], F32).ap()
    G['qsc'] = C.dint("qsc", [B_NM, 2, 64, 512], BF16)
    G['gsc'] = C.dint("gsc", [B_NM, 128, 24], F32).ap()
    G['ysc'] = C.dint("ysc", [B_NM, 128, 512], F32).ap()
    G['bg2sc'] = C.dint("bg2sc", [B_NM, 128, 8], F32).ap()
    G['atsc'] = C.dint("atsc", [B_NM, 128, 512], BF16).ap()
    G['rasc'] = C.dint("rasc", [B_NM, 128, 1], F32).ap()
    G['wupb'] = C.dint("wupb", [16, 128, 2048], BF16)
    qsc_ap = G['qsc'].ap()
    if dbg:
        G['dbg'] = {'at': C.dout("dbg_at", [B_NM, 128, 512], BF16), 'ra': C.dout("dbg_ra", [B_NM, 128, 1], F32),
                    'x1': C.dout("dbg_x1", [B_NM, 128, D], F32), 'y': C.dout("dbg_y", [B_NM, 128, 512], BF16),
                    'rc': C.dout("dbg_rc", [B_NM, 128, 1], F32),
                    'kc': C.dout("dbg_kc", [68, 2, 1024], BF16), 'vc': C.dout("dbg_vc", [128, 8, 2, 321], BF16),
                    'ks': C.dout("dbg_ks", [68, T], BF16), 'vs': C.dout("dbg_vs", [128, 128, 130], BF16)}

    G['identb'] = C.sb("identb_s", [128, 128], BF16)
    G['identf'] = C.sb("identf_s", [128, 128], F32)
    G['onesf'] = C.sb("onesf_s", [128, 1], F32)
    with P.group('cst'):
        P.dma('sync', G['identb'][:], identb_d)
        P.dma('sync', G['identf'][:], identf_d)
        P.dma('sync', G['onesf'][:], onesf_d)
    P.barrier()
    rg = [[0, 1, 2, 3], [4, 5, 6, 7]]
    def mk_cc(i_ap, o_ap):
        return lambda e: e.collective_compute("AllGather", ALU.bypass, replica_groups=rg, ins=[i_ap.opt()], outs=[o_ap.opt()])
    cc_fns = [mk_cc(G['pay'][cn].ap(), G['gath'][cn].ap()) for cn, _ in COMPS] + [mk_cc(G['payu'], G['gathu'])]
    for l in range(nlayers):
        xsrc = W['x0'] if l == 0 else xbuf
        xdst = xo if l == nlayers - 1 else xbuf
        if 'A' in stages:
            emit_A(C, l, xsrc, W, G)
        if 'G' in stages:
            P.collectives(cc_fns)
        with C.phase("ZB%d" % l):
            G['kcT_s'] = C.sb("kcT_s", [68, 2, 1024], BF16)
            G['vcx_s'] = C.sb("vcx_s", [128, 8, 2, 321], BF16)
            with P.group('cst2'):
                P.dma('sync', G['kcT_s'][64:68, :, :], W['kcaug'])
                P.dma('scalar', G['vcx_s'][:, :, :, 64:321], W['vcxc'])
            if 'Z' in stages:
                emit_Z(C, l, W, G)
            Gb = dict(G)
            Gb['qsc'] = qsc_ap
            if 'B' in stages:
                emit_B(C, l, W, Gb)
        if 'C' in stages:
            emit_C(C, l, xsrc, xdst, W, G)
    if 'C' not in stages:
        with C.phase("dbg"):
            xt = [C.sb("xt%d" % i, [128, D], F32) for i in range(2)]
            for m in range(B_NM):
                P.dma('sync', xt[m % 2][:], W['x0'][m])
                P.dma('sync', xo[m], xt[m % 2][:])
    return C.finish()


IDENTB = np.eye(128, dtype=np.float32).astype(NPBF)
ONESF = np.ones((128, 1), np.float32)


def _bf(a):
    return np.ascontiguousarray(a).astype(NPBF)


def _selmap():
    i = np.arange(1024)[:, None] * 16
    j = np.arange(256)[None, :] * 64
    sh = np.clip(np.minimum(i + 32, j + 64) - np.maximum(i, j), 0, None).astype(np.float32) / 32.0
    sh[1023] = 0.0
    return sh


def _pos_rows(pos):
    pos = np.maximum(pos, 0)
    return np.stack([np.ones_like(pos), np.ones_like(pos), pos % 128, pos // 128]).astype(np.float32)


def core_consts(s):
    ki = np.arange(128)[:, None]
    qi = np.arange(128)[None, :]
    tri_gt = np.where(ki > qi, NEGM, 0.0).astype(np.float32)
    tri_le = np.where(ki <= qi, NEGM, 0.0).astype(np.float32)
    full = np.full((128, 128), NEGM, np.float32)
    zero = np.zeros((128, 128), np.float32)
    smask = np.tile(np.stack([zero if d < s else (tri_gt if d == s else full) for d in range(4)], axis=1), (1, 1, 4))
    cm = []
    for v in range(5):
        ip = ki - 32 * v
        vis = (16 * ip + 31) <= (128 * s + qi)
        cm.append(np.where(vis, 0.0, NEGM).astype(np.float32))
    cmask = np.tile(np.stack(cm, axis=1), (1, 1, 4))
    wm = []
    for d in range(8):
        off = s + 4 - d
        if off == 0:
            wm.append((ki <= qi).astype(np.float32))
        elif off == 4:
            wm.append((ki > qi).astype(np.float32))
        elif 0 < off < 4:
            wm.append(np.ones((128, 128), np.float32))
        else:
            wm.append(zero)
    wmask = np.stack(wm, axis=1)
    E = np.zeros((128, 64, 128), np.float32)
    for kt in range(64):
        for k in range(128):
            E[2 * kt + k // 64, kt, k] = 1.0
    r = np.arange(512)[None, :] - 256 - 2 * s
    qq = np.arange(128)[:, None]
    Brel = np.zeros((128, 512), np.float32)
    Brel = np.where(r >= 2, np.float32(-1e30), Brel)
    Brel = np.where(r == 1, np.where(qq >= 64, np.float32(1e4), np.float32(-1e30)), Brel)
    Brel = np.where(r == 0, np.float32(1e4), Brel)
    Brel = np.where((r == -1) & (qq < 64), np.float32(1e4), Brel)
    rr, mm, kk = np.meshgrid(np.arange(4), np.arange(32), np.arange(128), indexing='ij')
    kpos = (128 * (4 * mm + rr) + kk).reshape(-1)
    kaug = _pos_rows(kpos)
    kc = _pos_rows(np.arange(1024) * 16 + 31)
    kcaug = np.stack([kc, kc], axis=1)
    qaug = np.zeros((B_NM, 4, 2, 512), np.float32)
    qv = np.arange(128)
    for m in range(B_NM):
        c = 4 * m + s
        for g in range(2):
            for rh in range(4):
                sl = 2.0 ** (-(4 * g + rh + 1))
                cs = slice(rh * 128, (rh + 1) * 128)
                qaug[m, 0, g, cs] = -sl * qv
                qaug[m, 1, g, cs] = -sl * 128.0 * c
                qaug[m, 2, g, cs] = sl
                qaug[m, 3, g, cs] = sl * 128.0
    sm = _selmap().reshape(8, 128, 256).transpose(1, 0, 2)
    vcxc = np.zeros((128, 8, 2, 257), np.float32)
    vcxc[:, :, :, 0] = 1.0
    vcxc[127, 7, :, 0] = 0.0
    vcxc[:, :, :, 1:257] = sm[:, :, None, :]
    selw = np.zeros((128, 4), np.float32)
    selw[:, s] = 1.0
    return {'smask': _bf(smask), 'cmask': _bf(cmask), 'wmask': _bf(wmask), 'Eall': _bf(E),
            'Bsel': np.ascontiguousarray(Brel.astype(np.float32)), 'kaug': _bf(kaug), 'kcaug': _bf(kcaug),
            'qaug': _bf(qaug), 'vcxc': _bf(vcxc), 'selw': selw,
            'identb': IDENTB, 'identf': np.eye(128, dtype=np.float32), 'onesf': ONESF}


def prep_fused(p, L=DEPTH):
    p = {k: (v if k == 'x' else v[:L]) for k, v in p.items()}
    common = {
        'w_in': np.ascontiguousarray(p['w_in'], dtype=np.float32),
        'gmix': np.ascontiguousarray(p['g_mix_norm'].reshape(L, 8, 128).transpose(0, 2, 1)),
        'gq': np.ascontiguousarray(np.tile(p['g_q'][:, None, :], (1, 128, 8))),
        'gk3': np.ascontiguousarray(np.broadcast_to(np.tile(p['g_k'], (1, 1, 2))[:, None], (L, 128, 3, 128))),
        'cw': np.ascontiguousarray(p['conv_w'].reshape(L, 3, 4, 128).transpose(0, 3, 2, 1)),
        'peT': np.ascontiguousarray(p['pe_cmp'].transpose(0, 1, 3, 2)),
        'w1': np.ascontiguousarray(p['w_cmp1'], dtype=np.float32),
        'b1b': np.ascontiguousarray(np.broadcast_to(p['b_cmp1'][:, :, None, :], (L, 2, 128, 256))),
        'w2': np.ascontiguousarray(p['w_cmp2'], dtype=np.float32),
        'b2b': np.ascontiguousarray(np.broadcast_to(p['b_cmp2'][:, None], (L, 128, 2, 64))),
        'w_o': np.ascontiguousarray(p['w_o'], dtype=np.float32),
        'gout': np.ascontiguousarray(p['g_out'].reshape(L, 8, 128).transpose(0, 2, 1)),
        'gffn': np.ascontiguousarray(np.tile(p['g_ffn_norm'][:, None, :], (1, 128, 1))),
        'w_up': np.ascontiguousarray(p['w_up'], dtype=np.float32),
        'w_dn': np.ascontiguousarray(p['w_down'], dtype=np.float32),
    }
    maps = []
    x = np.ascontiguousarray(p['x'], dtype=np.float32)
    for b in range(NB):
        xv = x[b].reshape(T // 128, 128, D)
        for s in range(4):
            mp = dict(common)
            mp.update(core_consts(s))
            mp['x0'] = np.ascontiguousarray(xv[np.arange(B_NM) * 4 + s])
            maps.append(mp)
    return maps


_PROG = {}


def run_fused(inputs, nlayers=DEPTH):
    p = {k: np.asarray(v) for k, v in inputs.items()}
    if nlayers not in _PROG:
        _PROG[nlayers] = build_fused(nlayers)
    res = run_bass_kernel_spmd(_PROG[nlayers], prep_fused(p, nlayers), core_ids=list(range(NCORES)))
    out = np.empty((NB, T, D), np.float32)
    for b in range(NB):
        ov = out[b].reshape(T // 128, 128, D)
        for s in range(4):
            ov[np.arange(B_NM) * 4 + s] = np.asarray(res.results[4 * b + s]['xo'])
    return out


def kernel(**inputs):
    return run_fused(inputs, DEPTH)
```

```python
import contextlib
import numpy as np
import ml_dtypes
import concourse.bass as bass
import concourse.mybir as mybir
from concourse.bass_utils import run_bass_kernel_spmd

F32 = mybir.dt.float32
BF16 = mybir.dt.bfloat16
AF = mybir.ActivationFunctionType
ALU = mybir.AluOpType
AX = mybir.AxisListType
NPBF = ml_dtypes.bfloat16

ENGS = ['tensor', 'vector', 'scalar', 'gpsimd', 'sync']
NCORES = 8
D = 1024
T = 16384
NB = 2
DEPTH = 4
INW = 2840
EPS = 1e-6
NEGM = -30000.0


def key_of(ap):
    t = getattr(ap, 'tensor', ap)
    return t.name.split('@')[0]


class Prog:
    def __init__(self, nc):
        self.nc = nc
        self.q = {e: [] for e in ENGS}
        self.semcount = {}
        self.res = {}
        self.seen = {e: {} for e in ENGS}

    grp = None

    @contextlib.contextmanager
    def group(self, sem):
        s = 'd_' + sem
        self.grp = {'sem': sem, 's': s, 'start': self.semcount.get(s, 0), 'keys': set(), 'pre': {}}
        try:
            yield
        finally:
            g, self.grp = self.grp, None
            v = self.semcount.get(s, 0)
            for k in g['keys']:
                self.res[k]['w'] = (s, v)

    def _need(self, eng, reads, writes):
        need = {}

        def add(tok):
            if tok is None:
                return
            s, v = tok
            if eng == 'tensor' and s == 'e_tensor':
                return
            if self.grp is not None and s == self.grp['s'] and v > self.grp['start']:
                return
            if need.get(s, 0) < v:
                need[s] = v
        for k in reads:
            st = self.res.get(k)
            if st:
                add(st['w'])
        for k in writes:
            sts = [self.res.get(k)]
            if self.grp is not None:
                if k not in self.grp['pre']:
                    st0 = self.res.get(k)
                    self.grp['pre'][k] = {'w': st0['w'], 'r': dict(st0['r'])} if st0 else None
                sts.append(self.grp['pre'][k])
            for st in sts:
                if st:
                    add(st['w'])
                    for s, v in st['r'].items():
                        add((s, v))
        for s, v in need.items():
            if self.seen[eng].get(s, 0) < v:
                self.q[eng].append(('wait', s, v))
                self.seen[eng][s] = v

    def _update(self, reads, writes, tok):
        s, v = tok
        for k in reads:
            st = self.res.setdefault(k, {'w': None, 'r': {}})
            if st['r'].get(s, 0) < v:
                st['r'][s] = v
        for k in writes:
            self.res[k] = {'w': tok, 'r': {}}

    def op(self, eng, fn, reads=(), writes=()):
        reads = [r if isinstance(r, str) else key_of(r) for r in reads]
        writes = [r if isinstance(r, str) else key_of(r) for r in writes]
        self._need(eng, reads, writes)
        s = 'e_' + eng
        v = self.semcount.get(s, 0) + 1
        self.semcount[s] = v
        self.q[eng].append(('op', fn, s, 1))
        self._update(reads, writes, (s, v))

    def dma(self, eng, out, in_, reads=None, writes=None, sem=None, **kw):
        if reads is None:
            reads = [] if in_.tensor.name in self.dram_names else [in_]
        if writes is None:
            writes = [] if out.tensor.name in self.dram_names else [out]
        assert reads or writes or sem
        reads = [r if isinstance(r, str) else key_of(r) for r in reads]
        writes = [r if isinstance(r, str) else key_of(r) for r in writes]
        if self.grp is not None:
            sem = self.grp['sem']
            self.grp['keys'].update(writes)
        if sem is None:
            sem = writes[0] if writes else 'st_' + reads[0]
        self._need(eng, reads, writes)
        s = 'd_' + sem
        v = self.semcount.get(s, 0) + 16
        self.semcount[s] = v
        self.q[eng].append(('op', lambda e: e.dma_start(out=out, in_=in_, **kw), s, 16))
        self._update(reads, writes, (s, v))

    dram_names = set()

    def mm(self, out, lhsT, rhs, start=True, stop=True, reads=None, writes=None):
        self.op('tensor', lambda e: e.matmul(out, lhsT=lhsT, rhs=rhs, start=start, stop=stop),
                reads=reads if reads is not None else [lhsT, rhs],
                writes=writes if writes is not None else [out])

    def tr(self, out, in_, ident, reads=None, writes=None):
        self.op('tensor', lambda e: e.transpose(out, in_, ident),
                reads=reads if reads is not None else [in_, ident],
                writes=writes if writes is not None else [out])

    def act(self, out, in_, func, bias=None, scale=None, accum_out=None, reads=None, writes=None, eng='scalar'):
        kw = {}
        rd = [in_]
        if bias is not None:
            kw['bias'] = bias
            if not isinstance(bias, (int, float)):
                rd.append(bias)
        if scale is not None:
            kw['scale'] = scale
            if not isinstance(scale, (int, float)):
                rd.append(scale)
        wr = [out]
        if accum_out is not None:
            kw['accum_out'] = accum_out
            wr.append(accum_out)
        self.op('scalar', lambda e: e.activation(out=out, in_=in_, func=func, **kw),
                reads=reads if reads is not None else rd,
                writes=writes if writes is not None else wr)

    def tt(self, eng, out, in0, in1, op, reads=None, writes=None):
        self.op(eng, lambda e: e.tensor_tensor(out=out, in0=in0, in1=in1, op=op),
                reads=reads if reads is not None else [in0, in1],
                writes=writes if writes is not None else [out])

    def ts(self, eng, out, in0, s1, op0, s2=None, op1=None, reads=None, writes=None):
        rd = [in0]
        if not isinstance(s1, (int, float)):
            rd.append(s1)
        if s2 is not None and not isinstance(s2, (int, float)):
            rd.append(s2)
        if op1 is None:
            fn = lambda e: e.tensor_scalar(out=out, in0=in0, scalar1=s1, scalar2=None, op0=op0)
        else:
            fn = lambda e: e.tensor_scalar(out=out, in0=in0, scalar1=s1, scalar2=s2, op0=op0, op1=op1)
        self.op(eng, fn, reads=reads if reads is not None else rd,
                writes=writes if writes is not None else [out])

    def stt(self, out, in0, scalar, in1, op0, op1, reads=None, writes=None):
        rd = [in0, in1]
        if not isinstance(scalar, (int, float)):
            rd.append(scalar)
        self.op('vector', lambda e: e.scalar_tensor_tensor(out=out, in0=in0, scalar=scalar, in1=in1, op0=op0, op1=op1),
                reads=reads if reads is not None else rd,
                writes=writes if writes is not None else [out])

    def copy(self, eng, out, in_, reads=None, writes=None):
        if eng == 'scalar':
            fn = lambda e: e.activation(out=out, in_=in_, func=AF.Copy)
        else:
            fn = lambda e: e.tensor_copy(out=out, in_=in_)
        self.op(eng, fn, reads=reads if reads is not None else [in_],
                writes=writes if writes is not None else [out])

    def recip(self, out, in_):
        self.op('vector', lambda e: e.reciprocal(out=out, in_=in_), reads=[in_], writes=[out])

    def reduce(self, out, in_, op=ALU.add, axis=AX.X):
        self.op('vector', lambda e: e.tensor_reduce(out=out, in_=in_, axis=axis, op=op), reads=[in_], writes=[out])

    def memset(self, eng, ap, val):
        self.op(eng, lambda e: e.memset(ap, val), writes=[ap])

    def barrier(self):
        for e in ENGS:
            for s, v in self.semcount.items():
                if self.seen[e].get(s, 0) < v:
                    self.q[e].append(('wait', s, v))
                    self.seen[e][s] = v
        self.res = {}

    def collectives(self, fns):
        self.barrier()
        s = 'c_cc'
        for fn in fns:
            v = self.semcount.get(s, 0) + 1
            self.semcount[s] = v
            self.q['gpsimd'].append(('op', fn, s, 1))
        self.barrier()

    def emit(self):
        nc = self.nc
        for s, v in self.semcount.items():
            if self.seen['sync'].get(s, 0) < v:
                self.q['sync'].append(('wait', s, v))
        with contextlib.ExitStack() as es:
            semh = {s: es.enter_context(nc.semaphore(s)) for s in self.semcount}
            block = es.enter_context(nc.Block())

            def mk(engname):
                def body(e):
                    for it in self.q[engname]:
                        if it[0] == 'wait':
                            e.wait_ge(semh[it[1]], it[2])
                        else:
                            ins = it[1](e)
                            ins.then_inc(semh[it[2]], it[3])
                return body
            for engname in ENGS:
                if self.q[engname]:
                    getattr(block, engname)(mk(engname))


def AP_(t, offset, dims):
    return bass.AP(t, offset, [list(d) for d in dims])


class Ctx:
    def __init__(self):
        self.nc = bass.Bass("TRN2", target_bir_lowering=False)
        self.es = contextlib.ExitStack()
        self.P = Prog(self.nc)
        self.P.dram_names = set()

    def din(self, name, shape, dt):
        self.P.dram_names.add(name)
        return self.nc.dram_tensor(name, list(shape), dt, kind="ExternalInput").ap()

    def dout(self, name, shape, dt):
        self.P.dram_names.add(name)
        return self.nc.dram_tensor(name, list(shape), dt, kind="ExternalOutput").ap()

    tag = None

    def dint(self, name, shape, dt):
        self.P.dram_names.add(name)
        return self.nc.dram_tensor(name, list(shape), dt)

    def sb(self, name, shape, dt):
        if self.tag:
            name = name + '@' + self.tag
        return self.es.enter_context(self.nc.sbuf_tensor(name, list(shape), dt))

    def ps(self, name, shape, dt):
        if self.tag:
            name = name + '@' + self.tag
        return self.es.enter_context(self.nc.psum_tensor(name, list(shape), dt))

    @contextlib.contextmanager
    def phase(self, tag):
        old, oldtag = self.es, self.tag
        self.es, self.tag = contextlib.ExitStack(), tag
        try:
            yield
        finally:
            self.P.barrier()
            self.es.close()
            self.es, self.tag = old, oldtag

    def finish(self):
        self.P.emit()
        self.es.close()
        return self.nc


B_NM = 32
DFF = 4096
PAYR = 6176
KS_R0, KW_R0, KC_R0, VC_R0, VS_R0, VW_R0 = 0, 1024, 2048, 3072, 4096, 5136
TP = T + 128


def emit_A(C, l, xsrc, W, G):
    P = C.P
    with C.phase("A%d" % l):
        Wbf = C.sb("Wbf", [128, 8, INW], BF16)
        wst = [C.sb("wst%d" % i, [128, INW // 2], F32) for i in range(4)]
        gmix_s = C.sb("gmix_s", [128, 8], F32)
        gq_s = C.sb("gq_s", [128, 512], F32)
        gk3_s = C.sb("gk3_s", [128, 3, 128], F32)
        cw_s = C.sb("cw_s", [128, 4, 3], F32)
        xt = [C.sb("xt%d" % i, [128, D], F32) for i in range(3)]
        junk = C.sb("junk", [128, D], BF16)
        ss = [C.sb("ss%d" % i, [128, 1], F32) for i in range(2)]
        rstd = [C.sb("rstd%d" % i, [128, 1], F32) for i in range(2)]
        hn = [C.sb("hn%d" % i, [128, D], BF16) for i in range(2)]
        hT = [C.sb("hT%d" % i, [128, 8, 128], BF16) for i in range(2)]
        zs = [C.sb("zs%d" % i, [128, 1304], F32) for i in range(2)]
        hcs = C.sb("hcs", [128, 4, 128], F32)
        ub = [C.sb("ub%d" % i, [128, 4, 130], F32) for i in range(2)]
        bgs = [C.sb("bgs%d" % i, [128, 4, 128], F32) for i in range(2)]
        sq = C.sb("sq", [128, 1280], F32)
        ssg = C.sb("ssg", [128, 20], F32)
        rg = C.sb("rg", [128, 20], F32)
        qtmp = C.sb("qtmp", [128, 512], F32)
        ktmp = C.sb("ktmp", [128, 256], F32)
        nrm = C.sb("nrm", [128, 1024], BF16)
        trs = C.sb("trs", [128, 8, 128], BF16)
        vsw = C.sb("vsw", [128, 2, 130], BF16)
        gts = C.sb("gts", [128, 24], F32)
        cv0 = C.sb("cv0", [128, 4, 128], F32)
        cv1 = C.sb("cv1", [128, 4, 128], F32)
        yv = C.sb("yv", [128, 4, 128], F32)
        ut = C.sb("ut", [128, 4, 2], F32)
        bg2 = C.sb("bg2", [128, 4, 2], F32)
        pTa = C.ps("pTa", [128, 8, 128], BF16)
        pTb = C.ps("pTb", [128, 8, 128], BF16)
        pz = [C.ps("pz%d" % i, [128, 512], F32) for i in range(3)]
        pc = [C.ps("pc%d" % i, [128, 4, 128], F32) for i in range(3)]
        identb = G['identb']

        with P.group('cst'):
            P.dma('sync', gmix_s[:], W['gmix'][l])
            P.dma('sync', gq_s[:], W['gq'][l])
            P.dma('sync', gk3_s[:], W['gk3'][l])
            P.dma('sync', cw_s[:], W['cw'][l])
        P.memset('gpsimd', vsw[:], 1.0)
        P.memset('gpsimd', ub[0][:], 0.0)
        P.memset('gpsimd', ub[1][:], 0.0)
        for kc in range(8):
            for hf in range(2):
                st = wst[hf + 2 * (kc % 2)]
                P.dma('sync' if hf == 0 else 'scalar', st[:], W['w_in'][l, kc * 128:(kc + 1) * 128, hf * 1420:(hf + 1) * 1420])
                if hf == 0:
                    P.ts('vector', Wbf[:, kc, 0:1420], st[:], gmix_s[:, kc:kc + 1], ALU.mult)
                else:
                    P.act(Wbf[:, kc, 1420:2840], st[:], AF.Copy, scale=gmix_s[:, kc:kc + 1])

        pay, payu, qsc, gsc, ysc, bg2sc = G['pay'], G['payu'], G['qsc'], G['gsc'], G['ysc'], G['bg2sc']

        def X_load(m):
            P.dma('gpsimd', xt[m % 3][:], xsrc[m])

        def F_pre(m):
            xb = xt[m % 3]
            P.act(junk[:], xb[:], AF.Square, accum_out=ss[m % 2][:])
            P.act(rstd[m % 2][:], ss[m % 2][:], AF.Sqrt, bias=EPS, scale=1.0 / D)
            P.recip(rstd[m % 2][:], rstd[m % 2][:])
            P.act(hn[m % 2][:], xb[:], AF.Copy, scale=rstd[m % 2][:, 0:1])

        def F_pe_a(m):
            for kc in range(8):
                P.tr(pTa[:, kc, :], hn[m % 2][:, kc * 128:(kc + 1) * 128], identb[:])
            P.copy('vector', hT[m % 2][:], pTa[:])

        def F_pe_b(m):
            hTb, zb, ubb, bgb = hT[m % 2], zs[m % 2], ub[m % 2], bgs[m % 2]
            for bi, (c0, c1) in enumerate([(0, 512), (512, 1024), (1024, 1304)]):
                for kc in range(8):
                    P.mm(pz[bi][:, 0:c1 - c0], hTb[:, kc, :], Wbf[:, kc, c0:c1], start=(kc == 0), stop=(kc == 7))
                P.copy('scalar', zb[:, c0:c1], pz[bi][:, 0:c1 - c0])
            for ch in range(12):
                for kc in range(8):
                    P.mm(pc[ch // 4][:, ch % 4, :], Wbf[:, kc, 1304 + ch * 128:1304 + (ch + 1) * 128], hTb[:, kc, :],
                         start=(kc == 0), stop=(kc == 7))
                if ch == 3:
                    P.copy('scalar', hcs[:], pc[0][:])
                if ch == 7:
                    P.tt('vector', ubb[:, :, 2:130], pc[1][:], hcs[:], ALU.mult)
            P.copy('vector', bgb[:], pc[2][:])

        def G1(m):
            zb, ubb, bgb = zs[m % 2], ub[m % 2], bgs[m % 2]
            P.act(sq[:], zb[:, 0:1280], AF.Square)
            P.reduce(ssg[:], sq[:].rearrange("p (g d) -> p g d", d=64))
            P.act(rg[:, 0:8], ssg[:, 0:8], AF.Sqrt, bias=64 * EPS, scale=1.0)
            P.act(rg[:, 8:20], ssg[:, 8:20], AF.Sqrt, bias=EPS, scale=1.0 / 64)
            P.recip(rg[:], rg[:])
            P.tt('vector', qtmp[:].rearrange("p (g d) -> p g d", d=64), zb[:, 0:512].rearrange("p (g d) -> p g d", d=64),
                 AP_(rg, 0, [[20, 128], [1, 8], [0, 64]]), ALU.mult)
            P.tt('gpsimd', nrm[:, 0:512], qtmp[:], gq_s[:], ALU.mult)
            P.tt('vector', ktmp[:, 0:128].rearrange("p (g d) -> p g d", d=64), zb[:, 768:896].rearrange("p (g d) -> p g d", d=64),
                 AP_(rg, 12, [[20, 128], [1, 2], [0, 64]]), ALU.mult)
            P.tt('vector', ktmp[:, 128:256].rearrange("p (g d) -> p g d", d=64), zb[:, 1024:1152].rearrange("p (g d) -> p g d", d=64),
                 AP_(rg, 16, [[20, 128], [1, 2], [0, 64]]), ALU.mult)
            P.tt('gpsimd', nrm[:, 512:768], ktmp[:], gk3_s[:, 1:3, :].rearrange("p a d -> p (a d)"), ALU.mult)
            P.copy('gpsimd', nrm[:, 768:1024], zb[:, 512:768])
            P.copy('gpsimd', AP_(vsw, 1, [[260, 128], [65, 2], [1, 64]]), zb[:, 896:1024].rearrange("p (g d) -> p g d", d=64))
            P.copy('gpsimd', AP_(vsw, 131, [[260, 128], [65, 2], [1, 64]]), zb[:, 1152:1280].rearrange("p (g d) -> p g d", d=64))
            P.dma('scalar', AP_(pay['vs%d' % (m // 16)], (m % 16) * 128 * 130, [[130, 128], [1, 130]]), vsw[:, 0, :])
            P.dma('scalar', AP_(pay['vw%d' % (m // 16)], (m % 16) * 128 * 130, [[130, 128], [1, 130]]), vsw[:, 1, :])
            P.act(gts[:], zb[:, 1280:1304], AF.Sigmoid)
            P.dma('scalar', gsc[m], gts[:])
            P.tt('gpsimd', cv0[:], ubb[:, :, 2:130], AP_(cw_s, 2, [[12, 128], [3, 4], [0, 128]]), ALU.mult)
            P.tt('gpsimd', cv1[:], ubb[:, :, 1:129], AP_(cw_s, 1, [[12, 128], [3, 4], [0, 128]]), ALU.mult)
            P.tt('gpsimd', cv0[:], cv0[:], cv1[:], ALU.add)
            P.tt('gpsimd', cv1[:], ubb[:, :, 0:128], AP_(cw_s, 0, [[12, 128], [3, 4], [0, 128]]), ALU.mult)
            P.tt('gpsimd', cv0[:], cv0[:], cv1[:], ALU.add)
            P.tt('vector', yv[:], bgb[:], cv0[:], ALU.mult)
            P.dma('scalar', ysc[m].rearrange("p (c t) -> p c t", c=4), yv[:])
            P.copy('gpsimd', ut[:], ubb[:, :, 128:130])
            P.dma('sync', payu[m].rearrange("(p e) -> p e", e=8), ut[:].rearrange("p c e -> p (c e)"))
            P.copy('gpsimd', bg2[:], bgb[:, :, 0:2])
            P.dma('sync', bg2sc[m], bg2[:].rearrange("p c e -> p (c e)"))

        def G2(m):
            for bk in range(8):
                P.tr(pTb[:, bk, :], nrm[:, bk * 128:(bk + 1) * 128], identb[:])
            P.copy('vector', trs[:], pTb[:])
            for e in range(2):
                for g in range(2):
                    P.dma('sync' if g == 0 else 'scalar', AP_(qsc, m * 65536 + g * 32768 + e * 128, [[512, 64], [256, 2], [1, 128]]),
                          trs[e * 64:(e + 1) * 64, 2 * g:2 * g + 2, :])
            for ci, cn in enumerate(('ks', 'kw', 'kc', 'vc')):
                P.dma('sync' if ci % 2 == 0 else 'scalar', AP_(pay[cn], m * 128, [[4096, 128], [1, 128]]), trs[:, 4 + ci, :])

        wupb = G['wupb']
        wstU = [C.sb("wstU%d" % i, [128, 2048], F32) for i in range(2)]
        wbU = [C.sb("wbU%d" % i, [128, 8, 256], BF16) for i in range(2)]

        def W_piece(n):
            kc, half = divmod(n, 2)
            st, wb = wstU[n % 2], wbU[n % 2]
            P.dma('gpsimd', st[:], W['w_up'][l, kc * 128:(kc + 1) * 128, half * 2048:(half + 1) * 2048])
            P.copy(('scalar', 'gpsimd')[n % 2], wb[:].rearrange("p a n -> p (a n)"), st[:])
            P.dma(('sync', 'scalar')[n % 2], AP_(wupb, half * 8 * 128 * 2048 + kc * 256, [[2048, 128], [128 * 2048, 8], [1, 256]]),
                  wb[:], reads=[wb], writes=[], sem='wupb')

        X_load(0)
        X_load(1)
        X_load(2)
        F_pre(0)
        F_pre(1)
        F_pe_a(0)
        F_pe_b(0)
        for m in range(B_NM):
            if m + 3 < B_NM:
                X_load(m + 3)
            if m % 2 == 0:
                W_piece(m // 2)
            if m + 1 < B_NM:
                F_pe_a(m + 1)
            if m + 2 < B_NM:
                F_pre(m + 2)
            G1(m)
            if m + 1 < B_NM:
                F_pe_b(m + 1)
            G2(m)


def emit_Z(C, l, W, G):
    P = C.P
    with C.phase("Z%d" % l):
        kcT_all = C.sb("kcT_all", [128, TP], BF16)
        vcT_all = C.sb("vcT_all", [128, TP], BF16)
        W1bf = C.sb("W1bf", [128, 2, 32, 256], BF16)
        w1st = [C.sb("w1st%d" % i, [128, 8, 256], F32) for i in range(2)]
        W2bf = C.sb("W2bf", [128, 2, 2, 64], BF16)
        w2st = C.sb("w2st", [128, 2, 2, 64], F32)
        peT_s = C.sb("peT_s", [64, 2, 32], F32)
        pebf = C.sb("pebf", [64, 2, 32, 128], BF16)
        b1b_s = C.sb("b1b_s", [128, 2, 256], F32)
        b2b_s = C.sb("b2b_s", [128, 2, 64], F32)
        gkc = C.sb("gkc", [128, 128], F32)
        c1b = C.sb("c1b", [128, 2, 256], F32)
        hid = C.sb("hid", [128, 256], F32)
        g_x2 = C.sb("g_x2", [128, 256], F32)
        g_in = C.sb("g_in", [128, 256], F32)
        g_sg = C.sb("g_sg", [128, 256], F32)
        hbf = C.sb("hbf", [128, 256], BF16)
        hidT = C.sb("hidT", [128, 2, 128], BF16)
        co = C.sb("co", [128, 2, 2, 64], F32)
        cosq = C.sb("cosq", [128, 128], F32)
        css = C.sb("css", [128, 2], F32)
        crs = C.sb("crs", [128, 2], F32)
        kcn = C.sb("kcn", [128, 128], BF16)
        pT = C.ps("pT", [128, 8, 128], BF16)
        px = [C.ps("px%d" % i, [128, 512], F32) for i in range(2)]
        po = C.ps("po", [128, 512], F32)
        identb = G['identb']
        gath = G['gath']
        kcT_s, vcx_s = G['kcT_s'], G['vcx_s']

        with P.group('cst'):
            P.dma('sync', b1b_s[:], W['b1b'][l].rearrange("k p n -> p k n"))
            P.dma('sync', b2b_s[:], W['b2b'][l])
            P.dma('sync', peT_s[:], W['peT'][l].rearrange("k d j -> d k j"))
            P.dma('sync', w2st[:], W['w2'][l].rearrange("k (c p) d -> p k c d", p=128))
            P.dma('sync', gkc[:], W['gk3'][l, :, 0, :])
        P.memset('gpsimd', kcT_all[:, T:TP], 0.0)
        P.memset('gpsimd', vcT_all[:, T:TP], 0.0)
        with P.group('kvcl'):
            for r in range(4):
                for kv, dst in enumerate((kcT_all, vcT_all)):
                    P.dma('sync' if kv == 0 else 'scalar', AP_(dst, r * 128, [[TP, 128], [512, 32], [1, 128]]),
                          AP_(gath['kc' if kv == 0 else 'vc'], r * 1024 * 512, [[4096, 128], [128, 32], [1, 128]]),
                          writes=[dst])
        P.copy('vector', W2bf[:], w2st[:])
        P.copy('vector', pebf[:], AP_(peT_s, 0, [[64, 64], [32, 2], [1, 32], [0, 128]]))
        n = 0
        for kv in range(2):
            for jq in range(4):
                st = w1st[n % 2]
                n += 1
                src = W['w1'][l, kv, jq * 512:(jq + 1) * 512, :].rearrange("(j d) n -> d j n", d=64)
                with P.group("w1st%d" % ((n - 1) % 2)):
                    P.dma('sync', st[0:64, :, :], src, writes=[st])
                    P.dma('scalar', st[64:128, :, :], src, writes=[st])
                P.copy('vector' if jq % 2 == 0 else 'gpsimd', W1bf[:, kv, jq * 8:(jq + 1) * 8, :], st[:])
        for kv in range(2):
            for j in range(32):
                P.mm(px[0][:, 0:256], pebf[:, kv, j, :], W1bf[0:64, kv, j, :], start=(j == 0), stop=(j == 31))
            P.tt('vector', c1b[:, kv, :], px[0][:, 0:256], b1b_s[:, kv, :], ALU.add)
        items = [(ib, kv, g) for ib in range(8) for kv in range(2) for g in range(2)]

        def z_mm(idx):
            ib, kv, g = items[idx]
            src_all = kcT_all if kv == 0 else vcT_all
            pp = px[idx % 2]
            base = 16 * 128 * ib
            for j in range(32):
                lhs = AP_(src_all, 64 * g * TP + base + j, [[TP, 64], [16, 128]])
                P.mm(pp[:, 0:256], lhs, W1bf[64 * g:64 * g + 64, kv, j, :], start=(j == 0), stop=(j == 31),
                     reads=[src_all, W1bf])

        def z_chain(idx):
            ib, kv, g = items[idx]
            pp = px[idx % 2]
            P.tt('vector', hid[:], pp[:, 0:256], c1b[:, kv, :], ALU.add)
            P.act(g_x2[:], hid[:], AF.Square)
            P.ts('gpsimd', g_x2[:], g_x2[:], 0.044715, ALU.mult, 1.0, ALU.add)
            P.tt('gpsimd', g_in[:], g_x2[:], hid[:], ALU.mult)
            P.act(g_sg[:], g_in[:], AF.Sigmoid, scale=1.5957691216057308)
            P.tt('vector', hbf[:], hid[:], g_sg[:], ALU.mult)
            for c in range(2):
                P.tr(pT[:, c, :], hbf[:, c * 128:(c + 1) * 128], identb[:])
            P.copy('vector', hidT[:], pT[:, 0:2, :])
            for c in range(2):
                P.mm(po[:, 0:64], hidT[:, c, :], W2bf[:, kv, c, :], start=(c == 0), stop=(c == 1))
            P.tt('vector', co[:, kv, g, :], po[:, 0:64], b2b_s[:, kv, :], ALU.add)
            if kv == 1 and g == 1:
                P.act(cosq[:], co[:, 0, :, :].rearrange("p g d -> p (g d)"), AF.Square)
                P.reduce(css[:], cosq[:].rearrange("p (g d) -> p g d", d=64))
                P.act(crs[:], css[:], AF.Sqrt, bias=EPS, scale=1.0 / 64)
                P.recip(crs[:], crs[:])
                P.tt('vector', cosq[:].rearrange("p (g d) -> p g d", d=64), co[:, 0, :, :], AP_(crs, 0, [[2, 128], [1, 2], [0, 64]]), ALU.mult)
                P.tt('vector', kcn[:], cosq[:], gkc[:], ALU.mult)
                for gg in range(2):
                    P.tr(pT[0:64, 4 + gg, :], kcn[:, 64 * gg:64 * gg + 64], identb[:])
                P.copy('vector', kcT_s[0:64, :, ib * 128:(ib + 1) * 128], pT[0:64, 4:6, :])
                P.copy('gpsimd', vcx_s[:, ib, :, 0:64], co[:, 1, :, :])

        z_mm(0)
        for idx in range(len(items)):
            if idx + 1 < len(items):
                z_mm(idx + 1)
            z_chain(idx)
        if G.get('dbg'):
            P.dma('sync', G['dbg']['kc'], kcT_s[:])
            P.dma('sync', G['dbg']['vc'], vcx_s[:])


def emit_B(C, l, W, G):
    P = C.P
    with C.phase("B%d" % l):
        ksT_s = [C.sb("ksT_s%d" % g, [68, T], BF16) for g in range(2)]
        vsx_s = C.sb("vsx_s", [128, 128, 130], BF16)
        smask_s = C.sb("smask_s", [128, 4, 512], BF16)
        cmask_s = C.sb("cmask_s", [128, 5, 512], BF16)
        wmask_s = C.sb("wmask_s", [128, 8, 128], BF16)
        Eall_s = C.sb("Eall_s", [128, 64, 128], BF16)
        Bsel_s = C.sb("Bsel_s", [128, 512], F32)
        qT_s = [C.sb("qT_s%d" % i, [68, 2, 512], BF16) for i in range(2)]
        kwT_s = [C.sb("kwT_s%d" % i, [68, 2, 1024], BF16) for i in range(2)]
        vwx_s = [C.sb("vwx_s%d" % i, [128, 8, 130], BF16) for i in range(2)]
        gates_s = [C.sb("gates_s%d" % i, [128, 24], F32) for i in range(2)]
        NPB = 6
        Pb = [C.sb("Pb%d" % i, [128, 512], BF16) for i in range(NPB)]
        Pc = [C.sb("Pc%d" % g, [128, 8, 512], BF16) for g in range(2)]
        oc = C.sb("oc", [128, 4, 321], F32)
        den4 = C.sb("den4", [128, 4], F32)
        rd4 = C.sb("rd4", [128, 4], F32)
        coef4 = C.sb("coef4", [128, 4], F32)
        imp = [C.sb("imp%d" % g, [128, 256], F32) for g in range(2)]
        tmpi = C.sb("tmpi", [128, 256], F32)
        m8 = C.sb("m8", [128, 16], F32)
        thr = C.sb("thr", [128, 1], F32)
        nm = [C.sb("nm%d" % g, [128, 256], BF16) for g in range(2)]
        nmT = [C.sb("nmT%d" % g, [128, 2, 128], BF16) for g in range(2)]
        osT = C.sb("osT", [65, 512], F32)
        otmp = C.sb("otmp", [128, 4, 64], F32)
        acc = C.sb("acc", [128, 8, 64], F32)
        accb = C.sb("accb", [128, 512], BF16)
        ssq = C.sb("ssq", [128, 1], F32)
        rat = C.sb("rat", [128, 1], F32)
        rat2 = C.sb("rat2", [128, 1], F32)
        atT = C.sb("atT", [128, 4, 128], BF16)
        ps_s = [C.ps("ps_s%d" % i, [128, 512], F32) for i in range(4)]
        ps_o = [C.ps("ps_o%d" % i, [128, 512], F32) for i in range(1)]
        ps_acc = [C.ps("ps_acc%d" % i, [128, 512], F32) for i in range(2)]
        pm = C.ps("pm", [128, 512], F32)
        pm_b = pm[:].bitcast(BF16)
        identb, identf = G['identb'], G['identf']
        kcT_s, vcx_s = G['kcT_s'], G['vcx_s']
        gath, qsc, gsc, atsc, rasc = G['gath'], G['qsc'], G['gsc'], G['atsc'], G['rasc']

        with P.group('cst'):
            P.dma('sync', cmask_s[:], W['cmask'])
            P.dma('sync', Bsel_s[:], W['Bsel'])
            P.dma('scalar', wmask_s[:], W['wmask'])
            P.dma('scalar', smask_s[:], W['smask'])
            P.dma('scalar', Eall_s[:], W['Eall'])
        with P.group('ksT'):
            for g in range(2):
                P.dma('gpsimd', ksT_s[g][64:68, :], W['kaug'], writes=[ksT_s[g]])
                for r in range(4):
                    P.dma('sync' if r % 2 == 0 else 'scalar', ksT_s[g][0:64, r * 4096:(r + 1) * 4096],
                          AP_(gath['ks'], r * 1024 * 512 + 64 * g * 4096, [[4096, 64], [1, 4096]]),
                          writes=[ksT_s[g]])
        with P.group('vsx'):
            for r in range(4):
                for hv in range(2):
                    P.dma('gpsimd', vsx_s[:, r * 32 + hv * 16:r * 32 + hv * 16 + 16, :],
                          AP_(gath['vs%d' % hv], r * 520 * 512, [[130, 128], [128 * 130, 16], [1, 130]]), writes=[vsx_s])

        if G.get('dbg'):
            P.dma('sync', G['dbg']['ks'], ksT_s[1][:])
            P.dma('sync', G['dbg']['vs'], vsx_s[:])
        state = {'s': 0, 'p': 0, 'x': 0}
        pending = []

        LA = 4

        def run_stream(tiles):
            n = len(tiles)
            bufs = []
            for i in range(n + LA):
                if i < n:
                    ps = ps_s[state['s'] % 4]
                    pb = Pb[state['p'] % NPB]
                    state['s'] += 1
                    state['p'] += 1
                    tiles[i][0](ps)
                    P.act(pb[:], ps[:], AF.Exp)
                    bufs.append(pb)
                j = i - (LA - 2)
                if 0 <= j < n and tiles[j][1] is not None:
                    tiles[j][1](bufs[j])
                if i - LA >= 0:
                    tiles[i - LA][2](bufs[i - LA])

        def mask_mul(pb, map_, mkeys):
            P.tt('vector', pb[:].rearrange("p (a q) -> p a q", a=4), pb[:].rearrange("p (a q) -> p a q", a=4), map_,
                 ALU.mult, reads=[pb] + mkeys, writes=[pb])

        def o_epilogue(psacc, g, br, gs):
            P.copy('vector', osT[:], psacc[0:65, :])
            pmv = pm[:, 0:260].rearrange("p (r d) -> p r d", d=65)
            for r in range(4):
                P.tr(pmv[:, r, :], osT[0:65, r * 128:(r + 1) * 128], identf[0:65, 0:65])
            P.recip(rd4[:], pmv[:, :, 0])
            P.tt('vector', coef4[:], rd4[:], AP_(gs, 12 * g + br, [[24, 128], [3, 4]]), ALU.mult)
            dst = acc[:, 4 * g:4 * g + 4, :]
            P.tt('vector', otmp[:], pmv[:, :, 1:65], AP_(coef4, 0, [[4, 128], [1, 4], [0, 64]]), ALU.mult)
            P.tt('gpsimd', dst, dst, otmp[:], ALU.add)

        for m in range(B_NM):
            qs = qT_s[m % 2]
            kws = kwT_s[m % 2]
            vws = vwx_s[m % 2]
            gs = gates_s[m % 2]
            h0 = 0 if m > 0 else 1
            with P.group("qT_s%d" % (m % 2)):
                P.dma('sync', qs[0:64, :, :], qsc[m].rearrange("g d n -> d g n"), writes=[qs])
                P.dma('scalar', qs[64:68, :, :], W['qaug'][m], writes=[qs])
            P.dma('scalar', gs[:], gsc[m])
            nh = 2 - h0
            c0 = 128 * (m - 1 + h0)
            with P.group("kws%d" % (m % 2)):
                for r in range(4):
                    P.dma('sync' if r % 2 == 0 else 'scalar',
                          AP_(kws, (r * 2 + h0) * 128, [[2048, 64], [1024, 2], [1, 128 * nh]]),
                          AP_(gath['kw'], r * 1024 * 512 + c0, [[4096, 64], [64 * 4096, 2], [1, 128 * nh]]),
                          writes=[kws])
            with P.group("vws%d" % (m % 2)):
                for r in range(4):
                    for hf in range(h0, 2):
                        mp = m - 1 + hf
                        P.dma('gpsimd', vws[:, r * 2 + hf, :],
                              AP_(gath['vw%d' % (mp // 16)], r * 520 * 512 + (mp % 16) * 128 * 130, [[130, 128], [1, 130]]),
                              writes=[vws])
            for g in range(2):
                P.copy('gpsimd', AP_(kws, 64 * 2048 + g * 1024 + h0 * 128, [[2048, 4], [256, 4], [1, 128 * (2 - h0)]]),
                       AP_(ksT_s[0], 64 * T + 128 * (m - 1 + h0), [[T, 4], [4096, 4], [1, 128 * (2 - h0)]]),
                       reads=[ksT_s[0]], writes=[kws])
            n_it = (32 * m + 30) // 128 + 1
            for g in range(2):
                for it in range(n_it):
                    ps = ps_s[state['s'] % 4]
                    state['s'] += 1
                    delta = 128 * it - 32 * m
                    masked = delta >= -128
                    P.mm(ps[:], kcT_s[:, g, it * 128:(it + 1) * 128], qs[:, g, :], start=True, stop=not masked)
                    if masked:
                        P.mm(ps[:], identb[:], cmask_s[:, (-delta) // 32, :], start=False, stop=True)
                    P.act(Pc[g][:, it, :], ps[:], AF.Exp)
                for r in range(4):
                    po = ps_o[0]
                    for it in range(n_it):
                        P.mm(po[:, 0:321], Pc[g][:, it, r * 128:(r + 1) * 128], vcx_s[:, it, g, :], start=(it == 0), stop=(it == n_it - 1))
                    P.copy('scalar', oc[:, r, :], po[:, 0:321])
                P.ts('vector', den4[:], oc[:, :, 64], 1e-30, ALU.max)
                P.recip(rd4[:], den4[:])
                P.ts('vector', imp[g][:], oc[:, 0, 65:321], rd4[:, 0:1], ALU.mult)
                for r in range(1, 4):
                    P.stt(imp[g][:], oc[:, r, 65:321], rd4[:, r:r + 1], imp[g][:], ALU.mult, ALU.add)
                P.tt('vector', coef4[:], rd4[:], AP_(gs, 12 * g + 0, [[24, 128], [3, 4]]), ALU.mult)
                P.tt('vector', acc[:, 4 * g:4 * g + 4, :], oc[:, :, 0:64], AP_(coef4, 0, [[4, 128], [1, 4], [0, 64]]), ALU.mult)
                P.tt('vector', imp[g][:], imp[g][:], Bsel_s[:, 256 - 8 * m:512 - 8 * m], ALU.add)
                P.ts('vector', imp[g][:, 0:1], imp[g][:, 0:1], 1e4, ALU.add)
                P.op('vector', lambda e, g=g: e.max(out=m8[:, 0:8], in_=imp[g][:]), reads=[imp[g]], writes=[m8])
                P.op('vector', lambda e, g=g: e.match_replace(out=tmpi[:], in_to_replace=m8[:, 0:8], in_values=imp[g][:], imm_value=-3.0e38),
                     reads=[imp[g], m8], writes=[tmpi])
                P.op('vector', lambda e: e.max(out=m8[:, 8:16], in_=tmpi[:]), reads=[tmpi], writes=[m8])
                P.ts('vector', thr[:], m8[:, 15:16], -1e29, ALU.max)
                P.ts('vector', nm[g][:], imp[g][:], thr[:, 0:1], ALU.is_ge)
            if pending:
                pending.pop()()
            tiles = []
            wt = [(r, hf) for hf in range(h0, 2) for r in range(4)]
            for g in range(2):
                for idx, (r, hf) in enumerate(wt):
                    def eS(ps, g=g, r=r, hf=hf):
                        P.mm(ps[:], kws[:, g, (r * 2 + hf) * 128:(r * 2 + hf + 1) * 128], qs[:, g, :], start=True, stop=(hf == 0))
                        if hf == 1:
                            P.mm(ps[:], identb[:], smask_s[:, r, :], start=False, stop=True)

                    def eX(pb, r=r):
                        mask_mul(pb, AP_(wmask_s, r * 128, [[1024, 128], [0, 4], [1, 128]]), [wmask_s])

                    def ePV(pb, g=g, r=r, hf=hf, idx=idx):
                        P.mm(ps_acc[g][0:65, :], vws[:, r * 2 + hf, 65 * g:65 * g + 65], pb[:], start=(idx == 0), stop=(idx == len(wt) - 1))
                    tiles.append((eS, eX if hf == 0 else None, ePV))
            run_stream(tiles)
            for g in range(2):
                for hb in range(2):
                    P.tr(pm_b[:, hb * 128:(hb + 1) * 128], nm[g][:, hb * 128:(hb + 1) * 128], identb[:])
                P.copy('vector', nmT[g][:].rearrange("p a q -> p (a q)"), pm_b[:, 0:256], reads=[pm])
            for g in range(2):
                o_epilogue(ps_acc[g], g, 2, gs)
            nkt = 4 * m + 4
            tiles = []
            for g in range(2):
                for kt in range(nkt):
                    def eS(ps, g=g, kt=kt):
                        diag = kt >= 4 * m
                        col = ((kt % 4) * 32 + kt // 4) * 128
                        P.mm(ps[:], ksT_s[g][:, col:col + 128], qs[:, g, :], start=True, stop=not diag)
                        if diag:
                            P.mm(ps[:], identb[:], smask_s[:, kt - 4 * m, :], start=False, stop=True)

                    def eX(pb, g=g, kt=kt):
                        bank = (ps_o[0], pm)[state['x'] % 2]
                        state['x'] += 1
                        P.mm(bank[:, 0:128], Eall_s[:, kt % 64, :], nmT[g][:, kt // 64, :], start=True, stop=True)
                        mask_mul(pb, AP_(bank, 0, [[512, 128], [0, 4], [1, 128]]), [bank])

                    def ePV(pb, g=g, kt=kt):
                        P.mm(ps_acc[g][0:65, :], vsx_s[:, (kt % 4) * 32 + kt // 4, 65 * g:65 * g + 65], pb[:], start=(kt == 0), stop=(kt == nkt - 1))
                    tiles.append((eS, eX, ePV))
            run_stream(tiles)
            for g in range(2):
                o_epilogue(ps_acc[g], g, 1, gs)
            accf = acc[:].rearrange("p h d -> p (h d)")
            P.act(accb[:], accf, AF.Square, accum_out=ssq[:])
            P.act(rat[:], ssq[:], AF.Sqrt, bias=EPS, scale=1.0 / 512)
            P.recip(rat2[:], rat[:])
            P.dma('gpsimd', rasc[m], rat2[:])
            P.copy('gpsimd', accb[:], accf)
            if G.get('dbg'):
                P.dma('gpsimd', G['dbg']['ra'][m], rat2[:])

            def finish_out(m=m):
                for bk in range(4):
                    P.tr(pm_b[:, bk * 128:(bk + 1) * 128], accb[:, bk * 128:(bk + 1) * 128], identb[:])
                P.copy('vector', atT[:].rearrange("p b q -> p (b q)"), pm_b[:, 0:512], reads=[pm])
                P.dma('gpsimd', atsc[m], atT[:].rearrange("p b q -> p (b q)"))
                if G.get('dbg'):
                    P.dma('gpsimd', G['dbg']['at'][m], atT[:].rearrange("p b q -> p (b q)"))
            pending.append(finish_out)
        pending.pop()()


def emit_C(C, l, xsrc, xdst, W, G):
    P = C.P
    with C.phase("C%d" % l):
        Wo = C.sb("Wo", [128, 8, D], BF16)
        Wdn = C.sb("Wdn", [128, 32, D], BF16)
        wst = [C.sb("wst%d" % i, [128, 8, 256], F32) for i in range(2)]
        Wup = [C.sb("Wup%d" % i, [128, 8, 256], BF16) for i in range(4)]
        gout_s = C.sb("gout_s", [128, 8], F32)
        gffn_s = C.sb("gffn_s", [128, D], F32)
        cw_s = C.sb("cw_s", [128, 4, 3], F32)
        selw_s = C.sb("selw_s", [128, 4], F32)
        xt = [C.sb("xt%d" % i, [128, D], F32) for i in range(2)]
        x1 = [C.sb("x1_%d" % i, [128, D], F32) for i in range(4)]
        at_s = [C.sb("at_s%d" % i, [128, 4, 128], BF16) for i in range(2)]
        yv = [C.sb("yv%d" % i, [128, 4, 128], F32) for i in range(2)]
        bg2 = [C.sb("bg2_%d" % i, [128, 4, 2], F32) for i in range(2)]
        tl = [C.sb("tl%d" % i, [128, 4, 8], F32) for i in range(2)]
        halo = C.sb("halo", [128, 4, 2], F32)
        pt0 = C.sb("pt0", [128, 4], F32)
        pt1 = C.sb("pt1", [128, 4], F32)
        ysq = [C.sb("ysq%d" % i, [128, 4, 128], F32) for i in range(2)]
        ybf = [C.sb("ybf%d" % i, [128, 4, 128], BF16) for i in range(2)]
        rc = C.sb("rc", [128, 1], F32)
        rc2 = [C.sb("rc2_%d" % i, [128, 1], F32) for i in range(2)]
        ra_s = [C.sb("ra_s%d" % i, [128, 1], F32) for i in range(2)]
        ss = C.sb("ss", [128, 1], F32)
        rstd = C.sb("rstd", [128, 1], F32)
        h2n = [C.sb("h2n%d" % i, [128, D], BF16) for i in range(2)]
        h2T = C.sb("h2T", [128, 8, 512], BF16)
        rl = [C.sb("rl%d" % i, [128, 512], F32) for i in range(2)]
        actT = C.sb("actT", [128, 32, 512], BF16)
        pa = [C.ps("pa%d" % i, [128, 512], F32) for i in range(2)]
        pcv = [C.ps("pcv%d" % i, [128, 512], F32) for i in range(2)]
        pu = [C.ps("pu%d" % i, [128, 512], F32) for i in range(2)]
        pT = C.ps("pT", [128, 8, 128], BF16)
        px = C.ps("px", [128, 512], F32)
        identb, onesf = G['identb'], G['onesf']
        gathu, ysc, bg2sc, atsc, rasc = G['gathu'], G['ysc'], G['bg2sc'], G['atsc'], G['rasc']

        with P.group('cst'):
            P.dma('sync', gout_s[:], W['gout'][l])
            P.dma('sync', gffn_s[:], W['gffn'][l])
            P.dma('sync', cw_s[:], W['cw'][l])
            P.dma('sync', selw_s[:], W['selw'])
        n = 0
        for kc in range(8):
            for hf in range(2):
                st = wst[n % 2]
                n += 1
                stw = AP_(st, 0, [[2048, 128], [1, 512]])
                P.dma('sync' if hf == 0 else 'scalar', stw, W['w_o'][l, kc * 128:(kc + 1) * 128, hf * 512:(hf + 1) * 512], writes=[st])
                if hf == 0:
                    P.ts('vector', Wo[:, kc, 0:512], stw, gout_s[:, kc:kc + 1], ALU.mult, reads=[st, gout_s])
                else:
                    P.act(Wo[:, kc, 512:1024], stw, AF.Copy, scale=gout_s[:, kc:kc + 1], reads=[st, gout_s])
        for f2 in range(16):
            st = wst[n % 2]
            n += 1
            stv = AP_(st, 0, [[2048, 128], [1024, 2], [1, 1024]])
            P.dma('sync' if f2 % 2 == 0 else 'scalar', stv, W['w_dn'][l, f2 * 256:(f2 + 1) * 256, :].rearrange("(a p) n -> p a n", p=128), writes=[st])
            P.copy(('vector', 'scalar', 'vector', 'gpsimd')[f2 % 4], Wdn[:, f2 * 2:(f2 + 1) * 2, :], stv, reads=[st])
        wupb = G['wupb']
        up_n = [0]

        def S1a(m):
            xb, ab, rab, yb, bgb, tlb = xt[m % 2], at_s[m % 2], ra_s[m % 2], yv[m % 2], bg2[m % 2], tl[m % 2]
            P.dma('sync', xb[:], xsrc[m])
            P.dma('scalar', ab[:], atsc[m].rearrange("p (b q) -> p b q", b=4))
            P.dma('sync', rab[:], rasc[m])
            P.dma('scalar', yb[:], ysc[m].rearrange("p (c t) -> p c t", c=4))
            P.dma('sync', bgb[:], bg2sc[m].rearrange("p (c e) -> p c e", e=2))
            if m == 0:
                P.memset('gpsimd', tlb[:, 0, :], 0.0)
            with P.group("tl%d" % (m % 2)):
                if m > 0:
                    P.dma('gpsimd', tlb[:, 0, :], gathu[3 * B_NM + m - 1].rearrange("(p e) -> p e", e=8), writes=[tlb])
                for cd in range(1, 4):
                    P.dma('gpsimd', tlb[:, cd, :], gathu[(cd - 1) * B_NM + m].rearrange("(p e) -> p e", e=8), writes=[tlb])
            hf = halo[:].rearrange("p c e -> p (c e)")
            P.ts('vector', hf, tlb[:, 0, :], selw_s[:, 0:1], ALU.mult)
            for cd in range(1, 4):
                P.stt(hf, tlb[:, cd, :], selw_s[:, cd:cd + 1], hf, ALU.mult, ALU.add)
            P.tt('vector', pt0[:], halo[:, :, 1], cw_s[:, :, 1], ALU.mult)
            P.tt('vector', pt1[:], halo[:, :, 0], cw_s[:, :, 0], ALU.mult)
            P.tt('vector', pt0[:], pt0[:], pt1[:], ALU.add)
            P.tt('vector', pt0[:], pt0[:], bgb[:, :, 0], ALU.mult)
            P.tt('vector', yb[:, :, 0], yb[:, :, 0], pt0[:], ALU.add)
            P.tt('vector', pt1[:], halo[:, :, 1], cw_s[:, :, 0], ALU.mult)
            P.tt('vector', pt1[:], pt1[:], bgb[:, :, 1], ALU.mult)
            P.tt('vector', yb[:, :, 1], yb[:, :, 1], pt1[:], ALU.add)
            P.act(ysq[m % 2][:], yb[:], AF.Square)
            P.copy('gpsimd', ybf[m % 2][:], yb[:])

        def S1b(m, j):
            xb, ab, rab = xt[m % 2], at_s[m % 2], ra_s[m % 2]
            for ch in range(4):
                P.mm(px[:, 0:1], ysq[m % 2][:, ch, :], onesf[:], start=(ch == 0), stop=(ch == 3))
            P.act(rc[:], px[:, 0:1], AF.Sqrt, bias=EPS, scale=1.0 / 512)
            P.recip(rc2[m % 2][:], rc[:])
            for nh in range(2):
                for kc in range(4):
                    P.mm(pa[nh][:], ab[:, kc, :], Wo[:, kc, nh * 512:(nh + 1) * 512], start=(kc == 0), stop=(kc == 3))
                for kc in range(4):
                    P.mm(pcv[nh][:], ybf[m % 2][:, kc, :], Wo[:, 4 + kc, nh * 512:(nh + 1) * 512], start=(kc == 0), stop=(kc == 3))
            x1b = x1[j]
            for nh in range(2):
                sl = slice(nh * 512, (nh + 1) * 512)
                P.stt(x1b[:, sl], pa[nh][:], rab[:, 0:1], xb[:, sl], ALU.mult, ALU.add)
                P.stt(x1b[:, sl], pcv[nh][:], rc2[m % 2][:, 0:1], x1b[:, sl], ALU.mult, ALU.add)
            if G.get('dbg'):
                P.dma('gpsimd', G['dbg']['x1'][m], x1b[:])
                P.dma('gpsimd', G['dbg']['y'][m], ybf[m % 2][:].rearrange("p c t -> p (c t)"))
                P.dma('gpsimd', G['dbg']['rc'][m], rc2[m % 2][:])
            P.act(h2n[m % 2][:], x1b[:], AF.Square, accum_out=ss[:])
            P.act(rstd[:], ss[:], AF.Sqrt, bias=EPS, scale=1.0 / D)
            P.recip(rstd[:], rstd[:])
            P.stt(h2n[m % 2][:], x1b[:], rstd[:, 0:1], gffn_s[:], ALU.mult, ALU.mult)

        def S1c(m, j):
            for kc in range(8):
                P.tr(pT[:, kc, :], h2n[m % 2][:, kc * 128:(kc + 1) * 128], identb[:])
            P.copy('scalar', h2T[:, :, j * 128:(j + 1) * 128], pT[:])

        NBT = B_NM // 4
        for j in range(4):
            S1a(j)
            S1b(j, j)
            S1c(j, j)
        for bt in range(NBT):
            for u in range(16):
                wb = Wup[up_n[0] % 4]
                up_n[0] += 1
                P.dma('sync' if u % 2 == 0 else 'scalar', wb[:].rearrange("p a n -> p (a n)"),
                      AP_(wupb, u * 128 * 2048, [[2048, 128], [1, 2048]]), reads=['wupb'], writes=[wb])
                for fl in range(2):
                    f = u * 2 + fl
                    pp = pu[f % 2]
                    for kc in range(8):
                        P.mm(pp[:], wb[:, kc, fl * 128:(fl + 1) * 128], h2T[:, kc, :], start=(kc == 0), stop=(kc == 7))
                    rb = rl[f % 2]
                    P.act(rb[:], pp[:], AF.Relu)
                    P.tt('gpsimd' if f % 2 == 0 else 'vector', actT[:, f, :], rb[:], rb[:], ALU.mult)
            nxt = bt + 1 < NBT
            for j in range(4):
                m = bt * 4 + j
                mn = m + 4
                if nxt:
                    S1a(mn)
                ob = x1[j]
                for nh in range(2):
                    pp = pu[nh]
                    for f in range(32):
                        P.mm(pp[:], actT[:, f, j * 128:(j + 1) * 128], Wdn[:, f, nh * 512:(nh + 1) * 512], start=(f == 0), stop=(f == 31))
                    P.tt('vector', ob[:, nh * 512:(nh + 1) * 512], pp[:], x1[j][:, nh * 512:(nh + 1) * 512], ALU.add)
                P.dma('sync', xdst[m], ob[:])
                if nxt:
                    S1b(mn, j)
                    if j >= 1:
                        S1c(mn - 1, j - 1)
            if nxt:
                S1c(bt * 4 + 7, 3)


def build_fused(nlayers=DEPTH, stages="AGZBC", dbg=False):
    C = Ctx()
    P = C.P
    W = {}
    W['x0'] = C.din("x0", [B_NM, 128, D], F32)
    W['w_in'] = C.din("w_in", [nlayers, D, INW], F32)
    W['gmix'] = C.din("gmix", [nlayers, 128, 8], F32)
    W['gq'] = C.din("gq", [nlayers, 128, 512], F32)
    W['gk3'] = C.din("gk3", [nlayers, 128, 3, 128], F32)
    W['cw'] = C.din("cw", [nlayers, 128, 4, 3], F32)
    W['peT'] = C.din("peT", [nlayers, 2, 64, 32], F32)
    W['w1'] = C.din("w1", [nlayers, 2, 2048, 256], F32)
    W['b1b'] = C.din("b1b", [nlayers, 2, 128, 256], F32)
    W['w2'] = C.din("w2", [nlayers, 2, 256, 64], F32)
    W['b2b'] = C.din("b2b", [nlayers, 128, 2, 64], F32)
    W['w_o'] = C.din("w_o", [nlayers, D, D], F32)
    W['gout'] = C.din("gout", [nlayers, 128, 8], F32)
    W['gffn'] = C.din("gffn", [nlayers, 128, D], F32)
    W['w_up'] = C.din("w_up", [nlayers, D, DFF], F32)
    W['w_dn'] = C.din("w_dn", [nlayers, DFF, D], F32)
    W['smask'] = C.din("smask", [128, 4, 512], BF16)
    W['cmask'] = C.din("cmask", [128, 5, 512], BF16)
    W['wmask'] = C.din("wmask", [128, 8, 128], BF16)
    W['Eall'] = C.din("Eall", [128, 64, 128], BF16)
    W['Bsel'] = C.din("Bsel", [128, 512], F32)
    W['kaug'] = C.din("kaug", [4, T], BF16)
    W['kcaug'] = C.din("kcaug", [4, 2, 1024], BF16)
    W['qaug'] = C.din("qaug", [B_NM, 4, 2, 512], BF16)
    W['vcxc'] = C.din("vcxc", [128, 8, 2, 257], BF16)
    W['selw'] = C.din("selw", [128, 4], F32)
    identb_d = C.din("identb", [128, 128], BF16)
    identf_d = C.din("identf", [128, 128], F32)
    onesf_d = C.din("onesf", [128, 1], F32)
    xo = C.dout("xo", [B_NM, 128, D], F32)

    G = {}
    COMPS = [('ks', 1024), ('kw', 1024), ('kc', 1024), ('vc', 1024), ('vs0', 520), ('vs1', 520), ('vw0', 520), ('vw1', 520)]
    G['pay'] = {cn: C.dint("pay_" + cn, [rows, 512], BF16) for cn, rows in COMPS}
    G['gath'] = {cn: C.dint("gath_" + cn, [4 * rows, 512], BF16) for cn, rows in COMPS}
    G['payu'] = C.dint("payu", [B_NM, 1024], F32).ap()
    G['gathu'] = C.dint("gathu", [4 * B_NM, 1024], F32).ap()
    xbuf = C.dint("xbuf", [B_NM, 128, D], F32).ap()
    G['qsc'] = C.dint("qsc", [B_NM, 2, 64, 512], BF16)
    G['gsc'] = C.dint("gsc", [B_NM, 128, 24], F32).ap()
    G['ysc'] = C.dint("ysc", [B_NM, 128, 512], F32).ap()
    G['bg2sc'] = C.dint("bg2sc", [B_NM, 128, 8], F32).ap()
    G['atsc'] = C.dint("atsc", [B_NM, 128, 512], BF16).ap()
    G['rasc'] = C.dint("rasc", [B_NM, 128, 1], F32).ap()
    G['wupb'] = C.dint("wupb", [16, 128, 2048], BF16)
    qsc_ap = G['qsc'].ap()
    if dbg:
        G['dbg'] = {'at': C.dout("dbg_at", [B_NM, 128, 512], BF16), 'ra': C.dout("dbg_ra", [B_NM, 128, 1], F32),
                    'x1': C.dout("dbg_x1", [B_NM, 128, D], F32), 'y': C.dout("dbg_y", [B_NM, 128, 512], BF16),
                    'rc': C.dout("dbg_rc", [B_NM, 128, 1], F32),
                    'kc': C.dout("dbg_kc", [68, 2, 1024], BF16), 'vc': C.dout("dbg_vc", [128, 8, 2, 321], BF16),
                    'ks': C.dout("dbg_ks", [68, T], BF16), 'vs': C.dout("dbg_vs", [128, 128, 130], BF16)}

    G['identb'] = C.sb("identb_s", [128, 128], BF16)
    G['identf'] = C.sb("identf_s", [128, 128], F32)
    G['onesf'] = C.sb("onesf_s", [128, 1], F32)
    with P.group('cst'):
        P.dma('sync', G['identb'][:], identb_d)
        P.dma('sync', G['identf'][:], identf_d)
        P.dma('sync', G['onesf'][:], onesf_d)
    P.barrier()
    rg = [[0, 1, 2, 3], [4, 5, 6, 7]]
    def mk_cc(i_ap, o_ap):
        return lambda e: e.collective_compute("AllGather", ALU.bypass, replica_groups=rg, ins=[i_ap.opt()], outs=[o_ap.opt()])
    cc_fns = [mk_cc(G['pay'][cn].ap(), G['gath'][cn].ap()) for cn, _ in COMPS] + [mk_cc(G['payu'], G['gathu'])]
    for l in range(nlayers):
        xsrc = W['x0'] if l == 0 else xbuf
        xdst = xo if l == nlayers - 1 else xbuf
        if 'A' in stages:
            emit_A(C, l, xsrc, W, G)
        if 'G' in stages:
            P.collectives(cc_fns)
        with C.phase("ZB%d" % l):
            G['kcT_s'] = C.sb("kcT_s", [68, 2, 1024], BF16)
            G['vcx_s'] = C.sb("vcx_s", [128, 8, 2, 321], BF16)
            with P.group('cst2'):
                P.dma('sync', G['kcT_s'][64:68, :, :], W['kcaug'])
                P.dma('scalar', G['vcx_s'][:, :, :, 64:321], W['vcxc'])
            if 'Z' in stages:
                emit_Z(C, l, W, G)
            Gb = dict(G)
            Gb['qsc'] = qsc_ap
            if 'B' in stages:
                emit_B(C, l, W, Gb)
        if 'C' in stages:
            emit_C(C, l, xsrc, xdst, W, G)
    if 'C' not in stages:
        with C.phase("dbg"):
            xt = [C.sb("xt%d" % i, [128, D], F32) for i in range(2)]
            for m in range(B_NM):
                P.dma('sync', xt[m % 2][:], W['x0'][m])
                P.dma('sync', xo[m], xt[m % 2][:])
    return C.finish()


IDENTB = np.eye(128, dtype=np.float32).astype(NPBF)
ONESF = np.ones((128, 1), np.float32)


def _bf(a):
    return np.ascontiguousarray(a).astype(NPBF)


def _selmap():
    i = np.arange(1024)[:, None] * 16
    j = np.arange(256)[None, :] * 64
    sh = np.clip(np.minimum(i + 32, j + 64) - np.maximum(i, j), 0, None).astype(np.float32) / 32.0
    sh[1023] = 0.0
    return sh


def _pos_rows(pos):
    pos = np.maximum(pos, 0)
    return np.stack([np.ones_like(pos), np.ones_like(pos), pos % 128, pos // 128]).astype(np.float32)


def core_consts(s):
    ki = np.arange(128)[:, None]
    qi = np.arange(128)[None, :]
    tri_gt = np.where(ki > qi, NEGM, 0.0).astype(np.float32)
    tri_le = np.where(ki <= qi, NEGM, 0.0).astype(np.float32)
    full = np.full((128, 128), NEGM, np.float32)
    zero = np.zeros((128, 128), np.float32)
    smask = np.tile(np.stack([zero if d < s else (tri_gt if d == s else full) for d in range(4)], axis=1), (1, 1, 4))
    cm = []
    for v in range(5):
        ip = ki - 32 * v
        vis = (16 * ip + 31) <= (128 * s + qi)
        cm.append(np.where(vis, 0.0, NEGM).astype(np.float32))
    cmask = np.tile(np.stack(cm, axis=1), (1, 1, 4))
    wm = []
    for d in range(8):
        off = s + 4 - d
        if off == 0:
            wm.append((ki <= qi).astype(np.float32))
        elif off == 4:
            wm.append((ki > qi).astype(np.float32))
        elif 0 < off < 4:
            wm.append(np.ones((128, 128), np.float32))
        else:
            wm.append(zero)
    wmask = np.stack(wm, axis=1)
    E = np.zeros((128, 64, 128), np.float32)
    for kt in range(64):
        for k in range(128):
            E[2 * kt + k // 64, kt, k] = 1.0
    r = np.arange(512)[None, :] - 256 - 2 * s
    qq = np.arange(128)[:, None]
    Brel = np.zeros((128, 512), np.float32)
    Brel = np.where(r >= 2, np.float32(-1e30), Brel)
    Brel = np.where(r == 1, np.where(qq >= 64, np.float32(1e4), np.float32(-1e30)), Brel)
    Brel = np.where(r == 0, np.float32(1e4), Brel)
    Brel = np.where((r == -1) & (qq < 64), np.float32(1e4), Brel)
    rr, mm, kk = np.meshgrid(np.arange(4), np.arange(32), np.arange(128), indexing='ij')
    kpos = (128 * (4 * mm + rr) + kk).reshape(-1)
    kaug = _pos_rows(kpos)
    kc = _pos_rows(np.arange(1024) * 16 + 31)
    kcaug = np.stack([kc, kc], axis=1)
    qaug = np.zeros((B_NM, 4, 2, 512), np.float32)
    qv = np.arange(128)
    for m in range(B_NM):
        c = 4 * m + s
        for g in range(2):
            for rh in range(4):
                sl = 2.0 ** (-(4 * g + rh + 1))
                cs = slice(rh * 128, (rh + 1) * 128)
                qaug[m, 0, g, cs] = -sl * qv
                qaug[m, 1, g, cs] = -sl * 128.0 * c
                qaug[m, 2, g, cs] = sl
                qaug[m, 3, g, cs] = sl * 128.0
    sm = _selmap().reshape(8, 128, 256).transpose(1, 0, 2)
    vcxc = np.zeros((128, 8, 2, 257), np.float32)
    vcxc[:, :, :, 0] = 1.0
    vcxc[127, 7, :, 0] = 0.0
    vcxc[:, :, :, 1:257] = sm[:, :, None, :]
    selw = np.zeros((128, 4), np.float32)
    selw[:, s] = 1.0
    return {'smask': _bf(smask), 'cmask': _bf(cmask), 'wmask': _bf(wmask), 'Eall': _bf(E),
            'Bsel': np.ascontiguousarray(Brel.astype(np.float32)), 'kaug': _bf(kaug), 'kcaug': _bf(kcaug),
            'qaug': _bf(qaug), 'vcxc': _bf(vcxc), 'selw': selw,
            'identb': IDENTB, 'identf': np.eye(128, dtype=np.float32), 'onesf': ONESF}


def prep_fused(p, L=DEPTH):
    p = {k: (v if k == 'x' else v[:L]) for k, v in p.items()}
    common = {
        'w_in': np.ascontiguousarray(p['w_in'], dtype=np.float32),
        'gmix': np.ascontiguousarray(p['g_mix_norm'].reshape(L, 8, 128).transpose(0, 2, 1)),
        'gq': np.ascontiguousarray(np.tile(p['g_q'][:, None, :], (1, 128, 8))),
        'gk3': np.ascontiguousarray(np.broadcast_to(np.tile(p['g_k'], (1, 1, 2))[:, None], (L, 128, 3, 128))),
        'cw': np.ascontiguousarray(p['conv_w'].reshape(L, 3, 4, 128).transpose(0, 3, 2, 1)),
        'peT': np.ascontiguousarray(p['pe_cmp'].transpose(0, 1, 3, 2)),
        'w1': np.ascontiguousarray(p['w_cmp1'], dtype=np.float32),
        'b1b': np.ascontiguousarray(np.broadcast_to(p['b_cmp1'][:, :, None, :], (L, 2, 128, 256))),
        'w2': np.ascontiguousarray(p['w_cmp2'], dtype=np.float32),
        'b2b': np.ascontiguousarray(np.broadcast_to(p['b_cmp2'][:, None], (L, 128, 2, 64))),
        'w_o': np.ascontiguousarray(p['w_o'], dtype=np.float32),
        'gout': np.ascontiguousarray(p['g_out'].reshape(L, 8, 128).transpose(0, 2, 1)),
        'gffn': np.ascontiguousarray(np.tile(p['g_ffn_norm'][:, None, :], (1, 128, 1))),
        'w_up': np.ascontiguousarray(p['w_up'], dtype=np.float32),
        'w_dn': np.ascontiguousarray(p['w_down'], dtype=np.float32),
    }
    maps = []
    x = np.ascontiguousarray(p['x'], dtype=np.float32)
    for b in range(NB):
        xv = x[b].reshape(T // 128, 128, D)
        for s in range(4):
            mp = dict(common)
            mp.update(core_consts(s))
            mp['x0'] = np.ascontiguousarray(xv[np.arange(B_NM) * 4 + s])
            maps.append(mp)
    return maps


_PROG = {}


def run_fused(inputs, nlayers=DEPTH):
    p = {k: np.asarray(v) for k, v in inputs.items()}
    if nlayers not in _PROG:
        _PROG[nlayers] = build_fused(nlayers)
    res = run_bass_kernel_spmd(_PROG[nlayers], prep_fused(p, nlayers), core_ids=list(range(NCORES)))
    out = np.empty((NB, T, D), np.float32)
    for b in range(NB):
        ov = out[b].reshape(T // 128, 128, D)
        for s in range(4):
            ov[np.arange(B_NM) * 4 + s] = np.asarray(res.results[4 * b + s]['xo'])
    return out


def kernel(**inputs):
    return run_fused(inputs, DEPTH)
```
